# Optimizing a Trainium2 kernel written in Bass

```python
import jax, jax.numpy as jnp
from jax import lax
import numpy as np

D_MODEL = 1024
BATCH = 8
SEQ = 8192
DEPTH = 2

PLE_DIM = 256
HEAD_DIM = 64
BRANCH_WIDTH = D_MODEL // 2
N_BRANCHES = 3
NSA_HEADS = BRANCH_WIDTH // HEAD_DIM
NSA_KV_GROUPS = 2
NSA_HPG = NSA_HEADS // NSA_KV_GROUPS
NSA_KV_WIDTH = NSA_KV_GROUPS * HEAD_DIM
NSA_GATE_WIDTH = 3 * NSA_HEADS
CMP_BLOCK = 32
CMP_STRIDE = 16
CMP_HIDDEN = 2 * HEAD_DIM
SEL_BLOCK = 64
SEL_TOPK = 16
WINDOW = 512
Q_BLOCK = 128
SG_GROUPS = 8
SG_CHUNK = 128
RWKV_HEADS = BRANCH_WIDTH // HEAD_DIM
DECAY_LORA = 64
AAA_LORA = 64
RK_SHIFT_WIDTH = 3 * BRANCH_WIDTH + DECAY_LORA + AAA_LORA
IN_SIZES = (BRANCH_WIDTH, 6 * NSA_KV_WIDTH, NSA_GATE_WIDTH, BRANCH_WIDTH,
            BRANCH_WIDTH, BRANCH_WIDTH, BRANCH_WIDTH,
            RK_SHIFT_WIDTH, BRANCH_WIDTH, N_BRANCHES * D_MODEL)
N_IN = 7 * BRANCH_WIDTH + 6 * NSA_KV_WIDTH + NSA_GATE_WIDTH + RK_SHIFT_WIDTH + N_BRANCHES * D_MODEL

NORM_EPS = 1e-6
LN_EPS = 1e-5
GN_EPS = 64e-5
MASK_NEG = -1e30
FORCE_BONUS = 1e4

kernel_name = "hybrid_nsa_sgmlp_rwkv7_block"


def _rmsnorm(x, g):
    xf = x.astype(jnp.float32)
    y = xf * lax.rsqrt(jnp.mean(xf * xf, axis=-1, keepdims=True) + NORM_EPS)
    return y.astype(x.dtype) * g


def _masked_softmax(s, mask):
    s = jnp.where(mask, s.astype(jnp.float32), MASK_NEG)
    return jnp.where(mask, jax.nn.softmax(s, axis=-1), 0.0)


def _split_cols(proj):
    out, off = [], 0
    for n in IN_SIZES:
        out.append(proj[..., off:off + n])
        off += n
    return out


def _compress(kv, w1, w2, pe):
    S = kv.shape[1]
    n_cmp = (S - CMP_BLOCK) // CMP_STRIDE + 1
    idx = jnp.arange(n_cmp)[:, None] * CMP_STRIDE + jnp.arange(CMP_BLOCK)[None, :]
    blk = kv[:, idx] + pe[:, None, :]
    hid = jax.nn.silu(jnp.einsum('bnlgd,ldh->bngh', blk, w1))
    return jnp.einsum('bngh,hd->bngd', hid, w2)


def _nsa(q, kc, vc, ks, vs, kw, vw, gate, cmp_w1, cmp_w2, cmp_pe):
    B, S, _ = q.shape
    G, HPG, Dh = NSA_KV_GROUPS, NSA_HPG, HEAD_DIM
    q = q.reshape(B, S, G, HPG, Dh) * (Dh ** -0.5)
    gate = jax.nn.sigmoid(gate).reshape(B, S, G, HPG, 3)
    k_cmp = _compress(kc.reshape(B, S, G, Dh), cmp_w1[0], cmp_w2[0], cmp_pe[0])
    v_cmp = _compress(vc.reshape(B, S, G, Dh), cmp_w1[1], cmp_w2[1], cmp_pe[1])
    n_cmp = k_cmp.shape[1]
    cmp_start = jnp.arange(n_cmp) * CMP_STRIDE
    cmp_end = cmp_start + CMP_BLOCK - 1
    n_slc = S // SEL_BLOCK
    top_k = min(SEL_TOPK, n_slc)
    slc_start = jnp.arange(n_slc) * SEL_BLOCK
    cover = ((cmp_start[:, None] <= slc_start[None, :] + SEL_BLOCK - 1)
             & (cmp_end[:, None] >= slc_start[None, :])).astype(jnp.float32)
    k_slc = ks.reshape(B, n_slc, SEL_BLOCK, G, Dh).transpose(0, 3, 1, 2, 4)
    v_slc = vs.reshape(B, n_slc, SEL_BLOCK, G, Dh).transpose(0, 3, 1, 2, 4)
    k_win = jnp.pad(kw.reshape(B, S, G, Dh), ((0, 0), (WINDOW, 0), (0, 0), (0, 0)))
    v_win = jnp.pad(vw.reshape(B, S, G, Dh), ((0, 0), (WINDOW, 0), (0, 0), (0, 0)))
    gather_blocks = jax.vmap(jax.vmap(lambda blocks, ix: blocks[ix]))
    j = jnp.arange(n_slc)

    def block(qb):
        q0 = qb * Q_BLOCK
        qblk = lax.dynamic_slice_in_dim(q, q0, Q_BLOCK, axis=1)
        gblk = lax.dynamic_slice_in_dim(gate, q0, Q_BLOCK, axis=1)
        t = q0 + jnp.arange(Q_BLOCK)
        valid_c = cmp_end[None, :] <= t[:, None]
        p_c = _masked_softmax(jnp.einsum('bqghd,bngd->bghqn', qblk, k_cmp), valid_c)
        o_c = jnp.einsum('bghqn,bngd->bqghd', p_c.astype(q.dtype), v_cmp)
        imp = jnp.einsum('bghqn,nj->bgqj', p_c, cover)
        t_blk = t // SEL_BLOCK
        forced = (j[None, :] == 0) | (j[None, :] == t_blk[:, None]) | (j[None, :] == t_blk[:, None] - 1)
        causal_blk = slc_start[None, :] <= t[:, None]
        score = jnp.where(causal_blk, jnp.where(forced, FORCE_BONUS, imp), -FORCE_BONUS)
        _, sel = lax.top_k(score, top_k)
        k_sel = gather_blocks(k_slc, sel).reshape(B, G, Q_BLOCK, top_k * SEL_BLOCK, Dh)
        v_sel = gather_blocks(v_slc, sel).reshape(B, G, Q_BLOCK, top_k * SEL_BLOCK, Dh)
        pos_sel = (sel[..., None] * SEL_BLOCK + jnp.arange(SEL_BLOCK)).reshape(B, G, Q_BLOCK, top_k * SEL_BLOCK)
        valid_s = (pos_sel <= t[None, None, :, None])[:, :, None]
        p_s = _masked_softmax(jnp.einsum('bqghd,bgqkd->bghqk', qblk, k_sel), valid_s)
        o_s = jnp.einsum('bghqk,bgqkd->bqghd', p_s.astype(q.dtype), v_sel)
        kwb = lax.dynamic_slice_in_dim(k_win, q0, Q_BLOCK + WINDOW, axis=1)
        vwb = lax.dynamic_slice_in_dim(v_win, q0, Q_BLOCK + WINDOW, axis=1)
        pos_w = q0 - WINDOW + jnp.arange(Q_BLOCK + WINDOW)
        valid_w = ((pos_w[None, :] <= t[:, None]) & (pos_w[None, :] > t[:, None] - WINDOW)
                   & (pos_w[None, :] >= 0))
        p_w = _masked_softmax(jnp.einsum('bqghd,bkgd->bghqk', qblk, kwb), valid_w)
        o_w = jnp.einsum('bghqk,bkgd->bqghd', p_w.astype(q.dtype), vwb)
        return gblk[..., 0:1] * o_c + gblk[..., 1:2] * o_s + gblk[..., 2:3] * o_w

    out = lax.map(block, jnp.arange(S // Q_BLOCK))
    return jnp.moveaxis(out, 0, 1).reshape(B, S, G * HPG * Dh)


def _spatial_gating(u, v, ln_g, ln_b, w_s, b_s):
    B, S, C = v.shape
    vf = v.astype(jnp.float32)
    mu = jnp.mean(vf, axis=-1, keepdims=True)
    var = jnp.var(vf, axis=-1, keepdims=True)
    vn = ((vf - mu) * lax.rsqrt(var + LN_EPS)).astype(v.dtype) * ln_g + ln_b
    vn = vn.reshape(B, S // SG_CHUNK, SG_CHUNK, SG_GROUPS, C // SG_GROUPS)
    causal = jnp.tril(jnp.ones((SG_CHUNK, SG_CHUNK), dtype=bool))
    w = jnp.where(causal, w_s, 0.0)
    mixed = jnp.einsum('gts,bcsgd->bctgd', w, vn) + b_s.T[:, :, None]
    return u * mixed.reshape(B, S, C)


def _rwkv7(xs, mu, w0, w2, a0, a2, k_k, k_a, r_k, lnx_g, lnx_b):
    B, S, _ = xs.shape
    C, H, N = BRANCH_WIDTH, RWKV_HEADS, HEAD_DIM
    prev = jnp.pad(xs, ((0, 0), (1, 0), (0, 0)))[:, :-1]
    xs = xs + (prev - xs) * mu
    r, k, v, wl, al = jnp.split(xs, [C, 2 * C, 3 * C, 3 * C + DECAY_LORA], axis=-1)
    w = -jax.nn.softplus(-(w0 + jnp.tanh(wl) @ w2)) - 0.5
    a = jax.nn.sigmoid(a0 + al @ a2)
    r, k, v, w, a = (z.reshape(B, S, H, N) for z in (r, k, v, w, a))
    kkf = (k * k_k).astype(jnp.float32)
    kk = kkf / jnp.maximum(jnp.sqrt(jnp.sum(kkf * kkf, axis=-1, keepdims=True)), 1e-12)
    k = k * (1 + (a - 1) * k_a)
    decay = jnp.exp(-jnp.exp(w.astype(jnp.float32)))

    def step(state, inp):
        r_t, d_t, k_t, v_t, a_t, b_t = inp
        sa = jnp.einsum('bhij,bhj->bhi', state, a_t)
        state = (state * d_t[:, :, None, :] + sa[..., None] * b_t[:, :, None, :]
                 + v_t[..., None] * k_t[:, :, None, :])
        return state, jnp.einsum('bhij,bhj->bhi', state, r_t)

    seq = lambda z: jnp.moveaxis(z.astype(jnp.float32), 1, 0)
    state0 = jnp.zeros((B, H, N, N), jnp.float32)
    _, y = lax.scan(step, state0, (seq(r), seq(decay), seq(k), seq(v), seq(-kk), seq(kk * a)))
    y = jnp.moveaxis(y, 0, 1)
    m = jnp.mean(y, axis=-1, keepdims=True)
    var = jnp.var(y, axis=-1, keepdims=True)
    y = ((y - m) * lax.rsqrt(var + GN_EPS)).reshape(B, S, C) * lnx_g + lnx_b
    bonus = (jnp.sum(r * k * r_k, axis=-1, keepdims=True) * v).reshape(B, S, C)
    return (y + bonus).astype(xs.dtype)


def setup_inputs(seed: int = 0) -> dict:
    key = jax.random.key(seed)
    ks = iter(jax.random.split(key, 32))
    nrm = lambda shape, scale: jax.random.normal(next(ks), shape, jnp.float32) * scale
    L, D, W = DEPTH, D_MODEL, BRANCH_WIDTH
    return {
        "x": nrm((BATCH, SEQ, D), 1.0),
        "p": nrm((DEPTH, BATCH, SEQ, PLE_DIM), 1.0),
        "norm_g": 1.0 + nrm((L, D), 0.02),
        "w_in": nrm((L, D, N_IN), D ** -0.5),
        "cmp_w1": nrm((L, 2, CMP_BLOCK, HEAD_DIM, CMP_HIDDEN), (CMP_BLOCK * HEAD_DIM) ** -0.5),
        "cmp_w2": nrm((L, 2, CMP_HIDDEN, HEAD_DIM), CMP_HIDDEN ** -0.5),
        "cmp_pe": nrm((L, 2, CMP_BLOCK, HEAD_DIM), 0.5),
        "sg_ln_g": 1.0 + nrm((L, W), 0.02),
        "sg_ln_b": nrm((L, W), 0.02),
        "sg_w": nrm((L, SG_GROUPS, SG_CHUNK, SG_CHUNK), SG_CHUNK ** -0.5),
        "sg_b": 1.0 + nrm((L, SG_GROUPS, SG_CHUNK), 0.02),
        "rk_mu": jax.random.uniform(next(ks), (L, RK_SHIFT_WIDTH), jnp.float32),
        "rk_w0": -1.0 + nrm((L, W), 0.5),
        "rk_w2": nrm((L, DECAY_LORA, W), 0.5 * DECAY_LORA ** -0.5),
        "rk_a0": nrm((L, W), 0.1),
        "rk_a2": nrm((L, AAA_LORA, W), 0.5 * AAA_LORA ** -0.5),
        "rk_kk": 0.85 + nrm((L, RWKV_HEADS, HEAD_DIM), 0.02),
        "rk_ka": 1.0 + nrm((L, RWKV_HEADS, HEAD_DIM), 0.02),
        "rk_rk": nrm((L, RWKV_HEADS, HEAD_DIM), 0.1),
        "rk_lnx_g": 1.0 + nrm((L, W), 0.02),
        "rk_lnx_b": nrm((L, W), 0.02),
        "w_branch": nrm((L, N_BRANCHES, W, D), W ** -0.5),
        "w_o": nrm((L, D, D), D ** -0.5),
        "ple_norm_g": 1.0 + nrm((L, D), 0.02),
        "w_ple_gate": nrm((L, D, D), D ** -0.5),
        "w_ple_proj": nrm((L, PLE_DIM, D), PLE_DIM ** -0.5),
        "final_norm_g": 1.0 + nrm((D,), 0.02),
    }


def reference(x, p, norm_g, w_in, cmp_w1, cmp_w2, cmp_pe, sg_ln_g, sg_ln_b, sg_w, sg_b,
              rk_mu, rk_w0, rk_w2, rk_a0, rk_a2, rk_kk, rk_ka, rk_rk, rk_lnx_g, rk_lnx_b,
              w_branch, w_o, ple_norm_g, w_ple_gate, w_ple_proj, final_norm_g):
    B, S, D = x.shape
    for i in range(DEPTH):
        h = _rmsnorm(x, norm_g[i])
        proj = h @ w_in[i]
        nq, nkv, ngate, nz, su, sv, sz, rs, rz, mg = _split_cols(proj)
        nkc, nvc, nks, nvs, nkw, nvw = jnp.split(nkv, 6, axis=-1)
        y_nsa = _nsa(nq, nkc, nvc, nks, nvs, nkw, nvw, ngate, cmp_w1[i], cmp_w2[i], cmp_pe[i])
        y_sg = _spatial_gating(su, sv, sg_ln_g[i], sg_ln_b[i], sg_w[i], sg_b[i])
        y_rk = _rwkv7(rs, rk_mu[i], rk_w0[i], rk_w2[i], rk_a0[i], rk_a2[i], rk_kk[i], rk_ka[i],
                      rk_rk[i], rk_lnx_g[i], rk_lnx_b[i])
        ys = jnp.stack([y_nsa * jax.nn.silu(nz), y_sg * jax.nn.silu(sz), y_rk * jax.nn.silu(rz)], axis=2)
        zs = jnp.einsum('bsnc,ncd->bsnd', ys, w_branch[i])
        merge = jax.nn.sigmoid(mg).reshape(B, S, N_BRANCHES, D)
        x = x + jnp.sum(merge * zs, axis=2) @ w_o[i]
        hp = _rmsnorm(x, ple_norm_g[i])
        x = x + jax.nn.sigmoid(hp @ w_ple_gate[i]) * (p[i] @ w_ple_proj[i])
    return _rmsnorm(x, final_norm_g)
```

```python
import contextlib
import numpy as np
import concourse.bass as bass
import concourse.mybir as mybir

F32 = mybir.dt.float32
BF16 = mybir.dt.bfloat16
AF = mybir.ActivationFunctionType
ALU = mybir.AluOpType
AX = mybir.AxisListType

ENGS = ("pe", "act", "dve", "pool", "sp")


class Buf:
    __slots__ = ("name", "w", "wfull", "r", "t")

    def __init__(self, name, t=None):
        self.name = name
        self.t = t
        self.w = []
        self.wfull = []
        self.r = []

    def __getitem__(self, k):
        return self.t[k]


class Op:
    __slots__ = ("eng", "fn", "deps", "marked", "tick", "dma", "idx")

    def __init__(self, eng, fn, dma):
        self.eng = eng
        self.fn = fn
        self.deps = []
        self.marked = False
        self.tick = None
        self.dma = dma
        self.idx = None


class DmaSem:
    def __init__(self):
        self.sem = None
        self.count = 0


class Sched:
    def __init__(self, nc, stack):
        self.nc = nc
        self.stack = stack
        self.ops = {e: [] for e in ENGS}
        self.all_ops = []
        self.dsems = {}
        self.n_sems = 0
        self.fence = []
        self.stacks = [stack]
        self.phase_keys = []
        self.free_ds = []
        self.all_ds = []
        self.keep = []

    def stack_push(self, st):
        self.stacks.append(st)
        self.phase_keys.append([])

    def stack_pop(self):
        self.stacks.pop()
        for kid in self.phase_keys.pop():
            ds = self.dsems.pop(kid, None)
            if ds is not None:
                self.free_ds.append(ds)

    def sb(self, name, shape, dt=F32):
        self.n_sems += 1
        name = "%s_u%d" % (name, self.n_sems)
        t = self.stacks[-1].enter_context(self.nc.sbuf_tensor(name, list(shape), dt))
        return Buf(name, t)

    def ps(self, name, shape, dt=F32):
        self.n_sems += 1
        name = "%s_u%d" % (name, self.n_sems)
        t = self.stacks[-1].enter_context(self.nc.psum_tensor(name, list(shape), dt))
        return Buf(name, t)

    def dram(self, name, shape, dt, kind="Internal"):
        t = self.nc.dram_tensor(name, list(shape), dt, kind=kind)
        return Buf(name, t.ap())

    def _add(self, eng, fn, reads, writes, pwrites, dma):
        op = Op(eng, fn, dma)
        deps = list(self.fence)
        for b in reads:
            deps.extend(b.w)
        for b in writes:
            deps.extend(b.w)
            deps.extend(b.r)
        for b in pwrites:
            deps.extend(b.wfull)
            deps.extend(b.r)
        seen = set()
        for d in deps:
            if id(d) in seen or d is op:
                continue
            seen.add(id(d))
            if d.eng == "pe" and eng == "pe" and d.dma is None and dma is None:
                continue
            op.deps.append(d)
            d.marked = True
        for b in reads:
            b.r.append(op)
            if len(b.r) > 24:
                b.r = self._prune(b.r)
        for b in writes:
            b.w = [op]
            b.wfull = [op]
            b.r = []
        for b in pwrites:
            b.w.append(op)
            if len(b.w) > 24:
                b.w = self._prune(b.w)
        op.idx = len(self.all_ops)
        self.all_ops.append(op)
        self.ops[eng].append(op)
        return op

    @staticmethod
    def _prune(lst):
        last = {}
        for o in lst:
            key = (o.eng, None) if o.dma is None else ("dma", id(o.dma))
            last[key] = o
        return list(last.values())

    def op(self, eng, fn, reads=(), writes=(), pwrites=()):
        return self._add(eng, fn, reads, writes, pwrites, None)

    def dma(self, eng, out_ap, in_ap, reads=(), writes=(), pwrites=(), key=None, **kw):
        if key is None:
            key = (list(writes) + list(pwrites))[0]
        ds = self.dsems.get(id(key))
        if ds is None:
            if self.free_ds:
                ds = self.free_ds.pop()
            else:
                ds = DmaSem()
                self.all_ds.append(ds)
            self.dsems[id(key)] = ds
            self.keep.append(key)
            if self.phase_keys:
                self.phase_keys[-1].append(id(key))
        fn = lambda e, o=out_ap, i=in_ap, kw=kw: e.dma_start(out=o, in_=i, **kw)
        op = self._add(eng, fn, reads, writes, pwrites, ds)
        ds.count += 16
        op.tick = ds.count
        return op

    def barrier_bufs(self, bufs):
        pass

    def emit(self):
        nc = self.nc
        stack = self.stack
        esem = {}
        for e in ENGS:
            esem[e] = stack.enter_context(nc.semaphore("s_" + e))
        for ds in self.all_ds:
            ds.sem = stack.enter_context(nc.semaphore("d%d" % self.n_sems))
            self.n_sems += 1
        for e in ENGS:
            c = 0
            for o in self.ops[e]:
                if o.dma is None:
                    if o.marked:
                        c += 1
                        o.tick = c
        self.max_ticks = {e: max([o.tick or 0 for o in self.ops[e] if o.dma is None] + [0]) for e in ENGS}

        def evkey(d):
            if d.dma is not None:
                return ("d", id(d.dma)), d.dma.sem, d.tick
            return ("e", d.eng), esem[d.eng], d.tick

        def run(eng_name, eng):
            seen = {}
            for o in self.ops[eng_name]:
                waits = {}
                for d in o.deps:
                    k, sem, val = evkey(d)
                    if seen.get(k, 0) >= val:
                        continue
                    if k not in waits or waits[k][1] < val:
                        waits[k] = (sem, val)
                for k, (sem, val) in waits.items():
                    eng.wait_ge(sem, val)
                    seen[k] = val
                inst = o.fn(eng)
                if o.dma is not None:
                    inst.then_inc(o.dma.sem, 16)
                elif o.marked:
                    inst.then_inc(esem[eng_name], 1)
            if eng_name == "sp":
                for e2 in ENGS:
                    m = self.max_ticks[e2]
                    if m > 0:
                        eng.wait_ge(esem[e2], m)
                for ds in self.all_ds:
                    if ds.count:
                        eng.wait_ge(ds.sem, ds.count)

        block = stack.enter_context(nc.Block())

        @block.tensor
        def _(e):
            run("pe", e)

        @block.scalar
        def _(e):
            run("act", e)

        @block.vector
        def _(e):
            run("dve", e)

        @block.gpsimd
        def _(e):
            run("pool", e)

        @block.sync
        def _(e):
            run("sp", e)


from concourse.bass_utils import run_bass_kernel_spmd

D = 1024
NCOL = 8600
PLE = 256
EPS = 1e-6


DEBUG = {}
_dbg_n = [0]


def dbg_dump(S, name, ap, buf, shape, cond=True):
    if not DEBUG.get("on") or not cond:
        return
    _dbg_n[0] += 1
    t = S.stacks[-1].enter_context(S.nc.sbuf_tensor("dbgsb_%d" % _dbg_n[0], list(shape), F32))
    tb = Buf("dbgsb", t)
    d = S.dram("dbg_" + name, list(shape), F32, kind="ExternalOutput")
    S.op("act", lambda e: e.copy(out=t[:], in_=ap), reads=[buf], writes=[tb])
    S.dma("sp", d.t, t[:], reads=[tb], writes=[d], key=tb)


class Ring:
    def __init__(self, bufs):
        self.bufs = bufs
        self.i = 0

    def next(self):
        b = self.bufs[self.i % len(self.bufs)]
        self.i += 1
        return b


def _barrier(S):
    fence = []
    for e in ENGS:
        comp = [o for o in S.ops[e] if o.dma is None]
        if comp:
            fence.append(comp[-1])
    lastd = {}
    for o in S.all_ops:
        if o.dma is not None:
            lastd[id(o.dma)] = o
    fence.extend(lastd.values())
    S.fence = fence


def make_ident(S, name="ident", dt=BF16):
    ident = S.sb(name, [128, 128], dt)
    S.op("pool", lambda e: e.memset(ident[:], 0.0), writes=[ident])
    S.op("pool", lambda e: e.affine_select(out=ident[:], in_=ident[:], pattern=[[-1, 128]],
                                           compare_op=ALU.not_equal, fill=1.0, base=0,
                                           channel_multiplier=1), reads=[ident], writes=[ident])
    return ident


def load_w_bf16(S, dst, k, src_ap, srcbuf):
    S.dma("pool", dst, src_ap, reads=[srcbuf], pwrites=[k], key=k, max_dma_last_dim=4096)


def rmsnorm_tile(S, xt_ap, xt_buf, g_buf, h_ap, h_buf, sq, ss, rs, eps=EPS, extra_reads=()):
    S.op("act", lambda e: e.activation(out=sq[:], in_=xt_ap, func=AF.Square, accum_out=ss[:]),
         reads=[xt_buf] + list(extra_reads), writes=[sq, ss])
    S.op("act", lambda e: e.activation(out=rs[:], in_=ss[:], func=AF.Sqrt, scale=1.0 / D, bias=eps),
         reads=[ss], writes=[rs])
    S.op("dve", lambda e: e.reciprocal(out=rs[:], in_=rs[:]), reads=[rs], writes=[rs])
    S.op("dve", lambda e: e.scalar_tensor_tensor(out=h_ap, in0=xt_ap, scalar=rs[:, 0:1], in1=g_buf[:],
                                                 op0=ALU.mult, op1=ALU.mult),
         reads=[xt_buf, rs, g_buf], pwrites=[h_buf])


def phase_A(S, nc, SEQ, lyr, x_src, Wd, scr):
    TT = 512
    nsub = TT // 128
    with contextlib.ExitStack() as st:
        S.stack_push(st)
        wt = S.sb("A_w", [128, 8, NCOL], BF16)
        gt = S.sb("A_g", [128, D])
        ident = make_ident(S, "A_ident")
        xt = S.sb("A_x", [128, nsub, D])
        sq = S.sb("A_sq", [128, D], BF16)
        ss = S.sb("A_ss", [128, 1])
        rs = S.sb("A_rs", [128, 1])
        h = S.sb("A_h", [128, nsub, D], BF16)
        hT = S.sb("A_hT", [128, 8, TT], BF16)
        stg_b = Ring([S.sb("A_sb%d" % i, [128, 512], BF16) for i in range(4)])
        stg_f = Ring([S.sb("A_sf%d" % i, [128, 512], F32) for i in range(3)])
        pT = Ring([S.ps("A_pT%d" % i, [128, 8, 128], BF16) for i in range(2)])
        pacc = Ring([S.ps("A_pa%d" % i, [128, 512], F32) for i in range(6)])

        w_in = Wd["w_in"]
        for k in range(8):
            S.dma("pool", wt[:, k, :], w_in.t[lyr, k * 128:(k + 1) * 128, 0:NCOL], reads=[w_in], pwrites=[wt],
                  key=wt, max_dma_last_dim=4096)
        S.dma("sp", gt[:], Wd["norm_g"].t[lyr:lyr + 1, :].partition_broadcast(128), reads=[Wd["norm_g"]],
              writes=[gt])

        FM = []
        for c in range(4):
            FM.append((c * 128, scr["qT"], c * 128, AF.Copy, 0.125, BF16))
        FM.append((512, scr["kcT"], 0, None, 1.0, BF16))
        FM.append((640, scr["vcT"], 0, None, 1.0, BF16))
        FM.append((768, scr["ksT"], 0, None, 1.0, BF16))
        FM.append((1024, scr["kwT"], 0, None, 1.0, BF16))
        for c in range(13):
            FM.append((3352 + c * 128, scr["rsT"], c * 128, None, 1.0, F32))
        for c in range(4):
            FM.append((5016 + c * 128, scr["rzT"], c * 128, AF.Silu, 1.0, BF16))
        for c in range(24):
            FM.append((5528 + c * 128, scr["mgT"], c * 128, AF.Sigmoid, 1.0, BF16))
        TM = [
            (896, 128, scr["vsw"], 0, None, BF16),
            (1152, 128, scr["vsw"], 128, None, BF16),
            (1280, 24, scr["gate"], 0, AF.Sigmoid, F32),
            (1304, 512, scr["nzs"], 0, AF.Silu, BF16),
            (1816, 512, scr["su"], 0, None, F32),
            (2328, 512, scr["sv"], 0, None, F32),
            (2840, 512, scr["szs"], 0, AF.Silu, BF16),
        ]
        evac_i = [0]

        def evac(out_ap, out_buf, in_ap, in_buf, func, scale):
            if func is None and scale == 1.0:
                if evac_i[0] % 2 == 0:
                    S.op("dve", lambda e: e.tensor_copy(out=out_ap, in_=in_ap), reads=[in_buf], writes=[out_buf])
                else:
                    S.op("act", lambda e: e.copy(out=out_ap, in_=in_ap), reads=[in_buf], writes=[out_buf])
                evac_i[0] += 1
            else:
                S.op("act", lambda e: e.activation(out=out_ap, in_=in_ap, func=func, scale=scale),
                     reads=[in_buf], writes=[out_buf])

        for ti in range(SEQ // TT):
            t0 = ti * TT
            S.dma("sp", xt[:], x_src.t[t0:t0 + TT, :].rearrange("(s p) d -> p s d", p=128), reads=[x_src],
                  writes=[xt])
            for s in range(nsub):
                rmsnorm_tile(S, xt[:, s, :], xt, gt, h[:, s, :], h, sq, ss, rs)
                pt = pT.next()
                for k in range(8):
                    S.op("pe", lambda e, k=k, s=s, pt=pt: e.transpose(out=pt[:, k, :], in_=h[:, s, k * 128:(k + 1) * 128],
                                                                     identity=ident[:]),
                         reads=[h, ident], writes=[pt] if k == 0 else (), pwrites=() if k == 0 else [pt])
                S.op("dve", lambda e, s=s, pt=pt: e.tensor_copy(out=hT[:, :, s * 128:(s + 1) * 128], in_=pt[:]),
                     reads=[pt], pwrites=[hT])
            for (c0, dbuf, r0, func, scale, dt) in FM:
                pa = pacc.next()
                for k in range(8):
                    S.op("pe", lambda e, k=k, pa=pa, c0=c0: e.matmul(pa[:], lhsT=wt[:, k, c0:c0 + 128], rhs=hT[:, k, :],
                                                                    start=(k == 0), stop=(k == 7)),
                         reads=[wt, hT], writes=[pa] if k == 0 else (), pwrites=() if k == 0 else [pa])
                sg = stg_b.next() if dt == BF16 else stg_f.next()
                evac(sg[:], sg, pa[:], pa, func, scale)
                S.dma("sp", dbuf.t[r0:r0 + 128, t0:t0 + TT], sg[:], reads=[sg], pwrites=[dbuf], key=sg)
            for s in range(nsub):
                for (c0, ncol, dbuf, dc0, func, dt) in TM:
                    pa = pacc.next()
                    for k in range(8):
                        S.op("pe", lambda e, k=k, pa=pa, c0=c0, ncol=ncol, s=s: e.matmul(
                            pa[:, 0:ncol], lhsT=hT[:, k, s * 128:(s + 1) * 128], rhs=wt[:, k, c0:c0 + ncol],
                            start=(k == 0), stop=(k == 7)),
                            reads=[wt, hT], writes=[pa] if k == 0 else (), pwrites=() if k == 0 else [pa])
                    sg = stg_b.next() if dt == BF16 else stg_f.next()
                    evac(sg[:, 0:ncol], sg, pa[:, 0:ncol], pa, func, 1.0)
                    S.dma("sp", dbuf.t[t0 + s * 128:t0 + (s + 1) * 128, dc0:dc0 + ncol], sg[:, 0:ncol], reads=[sg],
                          pwrites=[dbuf], key=sg)
        _barrier(S)
        S.stack_pop()


def make_scratch(S, SEQ, kind="Internal"):
    scr = {}
    def mk(name, shape, dt):
        scr[name] = S.dram(name, shape, dt, kind=kind)
    mk("qT", [512, SEQ], BF16)
    mk("kcT", [128, SEQ], BF16)
    mk("vcT", [128, SEQ], BF16)
    mk("ksT", [128, SEQ], BF16)
    mk("kwT", [128, SEQ], BF16)
    mk("vsw", [SEQ, 256], BF16)
    mk("gate", [SEQ, 24], F32)
    mk("nzs", [SEQ, 512], BF16)
    mk("su", [SEQ, 512], F32)
    mk("sv", [SEQ, 512], F32)
    mk("szs", [SEQ, 512], BF16)
    mk("rsT", [1664, SEQ], F32)
    mk("rzT", [512, SEQ], BF16)
    mk("mgT", [3072, SEQ], BF16)
    mk("ysT", [3, 512, SEQ], BF16)
    mk("xtok", [SEQ, 5, 2, 256], F32)
    return scr


def phase_C(S, nc, SEQ, lyr, Wd, scr):
    LN_EPS = 1e-5
    with contextlib.ExitStack() as st:
        S.stack_push(st)
        ident = make_ident(S, "C_ident")
        wraw = S.sb("C_wraw", [128, 8, 128])
        wbf = S.sb("C_wbf", [128, 8, 128], BF16)
        WT = S.sb("C_WT", [128, 8, 128], BF16)
        bsT = S.sb("C_bsT", [128, 8])
        lng = S.sb("C_lng", [128, 512])
        lnb = S.sb("C_lnb", [128, 512])
        pw = S.ps("C_pw", [128, 8, 128], BF16)
        S.dma("sp", wraw[:], Wd["sg_w"].t[lyr].rearrange("g t s -> t g s"), reads=[Wd["sg_w"]], writes=[wraw])
        S.dma("sp", bsT[:], Wd["sg_b"].t[lyr].rearrange("g t -> t g"), reads=[Wd["sg_b"]], writes=[bsT],
              allow_slow_non_contiguous=True)
        S.dma("sp", lng[:], Wd["sg_ln_g"].t[lyr:lyr + 1, :].partition_broadcast(128), reads=[Wd["sg_ln_g"]], writes=[lng])
        S.dma("sp", lnb[:], Wd["sg_ln_b"].t[lyr:lyr + 1, :].partition_broadcast(128), reads=[Wd["sg_ln_b"]], writes=[lnb])
        S.op("pool", lambda e: e.affine_select(out=wraw[:], in_=wraw[:], pattern=[[0, 8], [-1, 128]],
                                               compare_op=ALU.is_ge, fill=0.0, base=0, channel_multiplier=1),
             reads=[wraw], writes=[wraw])
        S.op("dve", lambda e: e.tensor_copy(out=wbf[:], in_=wraw[:]), reads=[wraw], writes=[wbf])
        for g in range(8):
            S.op("pe", lambda e, g=g: e.transpose(out=pw[:, g, :], in_=wbf[:, g, :], identity=ident[:]),
                 reads=[wbf, ident], pwrites=[pw])
        S.op("dve", lambda e: e.tensor_copy(out=WT[:], in_=pw[:]), reads=[pw], writes=[WT])

        NB = 2
        svt = Ring([S.sb("C_sv%d" % i, [128, 512]) for i in range(NB)])
        sut = Ring([S.sb("C_su%d" % i, [128, 512]) for i in range(NB)])
        szt = Ring([S.sb("C_sz%d" % i, [128, 512], BF16) for i in range(NB)])
        stats = S.sb("C_stats", [128, 6])
        mv = S.sb("C_mv", [128, 2])
        rstd = S.sb("C_rstd", [128, 1])
        vn0 = S.sb("C_vnf", [128, 512])
        vn = Ring([S.sb("C_vn%d" % i, [128, 512], BF16) for i in range(2)])
        y0 = S.sb("C_y0", [128, 512])
        yb = Ring([S.sb("C_yb%d" % i, [128, 512], BF16) for i in range(2)])
        pm = Ring([S.ps("C_pm%d" % i, [128, 512]) for i in range(2)])
        pt = Ring([S.ps("C_pt%d" % i, [128, 4, 128], BF16) for i in range(2)])
        stg = Ring([S.sb("C_stg%d" % i, [128, 4, 512], BF16) for i in range(2)])
        ys = scr["ysT"]
        sgb = None
        for c in range(SEQ // 128):
            t0 = c * 128
            v = svt.next(); u = sut.next(); z = szt.next()
            S.dma("sp", v[:], scr["sv"].t[t0:t0 + 128, :], reads=[scr["sv"]], writes=[v])
            S.dma("sp", u[:], scr["su"].t[t0:t0 + 128, :], reads=[scr["su"]], writes=[u])
            S.dma("sp", z[:], scr["szs"].t[t0:t0 + 128, :], reads=[scr["szs"]], writes=[z])
            S.op("dve", lambda e, v=v: e.bn_stats(out=stats[:], in_=v[:]), reads=[v], writes=[stats])
            S.op("dve", lambda e: e.bn_aggr(out=mv[:], in_=stats[:]), reads=[stats], writes=[mv])
            S.op("act", lambda e: e.activation(out=rstd[:], in_=mv[:, 1:2], func=AF.Sqrt, bias=LN_EPS, scale=1.0),
                 reads=[mv], writes=[rstd])
            S.op("dve", lambda e: e.reciprocal(out=rstd[:], in_=rstd[:]), reads=[rstd], writes=[rstd])
            S.op("dve", lambda e, v=v: e.tensor_scalar(out=vn0[:], in0=v[:], scalar1=mv[:, 0:1], scalar2=rstd[:, 0:1],
                                                       op0=ALU.subtract, op1=ALU.mult),
                 reads=[v, mv, rstd], writes=[vn0])
            S.op("pool", lambda e: e.tensor_tensor(out=vn0[:], in0=vn0[:], in1=lng[:], op=ALU.mult),
                 reads=[vn0, lng], writes=[vn0])
            vb = vn.next()
            S.op("pool", lambda e, vb=vb: e.tensor_tensor(out=vb[:], in0=vn0[:], in1=lnb[:], op=ALU.add),
                 reads=[vn0, lnb], writes=[vb])
            pmm = pm.next()
            for g in range(8):
                S.op("pe", lambda e, g=g, vb=vb, pmm=pmm: e.matmul(pmm[:, g * 64:(g + 1) * 64], lhsT=WT[:, g, :],
                                                                   rhs=vb[:, g * 64:(g + 1) * 64], start=True, stop=True),
                     reads=[WT, vb], writes=[pmm] if g == 0 else (), pwrites=() if g == 0 else [pmm])
            S.op("dve", lambda e, pmm=pmm: e.tensor_tensor(
                out=y0[:].rearrange("p (g d) -> p g d", g=8), in0=pmm[:].rearrange("p (g d) -> p g d", g=8),
                in1=bsT[:].unsqueeze(2).to_broadcast([128, 8, 64]), op=ALU.add), reads=[pmm, bsT], writes=[y0])
            S.op("pool", lambda e, u=u: e.tensor_tensor(out=y0[:], in0=y0[:], in1=u[:], op=ALU.mult),
                 reads=[y0, u], writes=[y0])
            y = yb.next()
            S.op("dve", lambda e, y=y, z=z: e.tensor_tensor(out=y[:], in0=y0[:], in1=z[:], op=ALU.mult),
                 reads=[y0, z], writes=[y])
            ptt = pt.next()
            for k in range(4):
                S.op("pe", lambda e, k=k, y=y, ptt=ptt: e.transpose(out=ptt[:, k, :], in_=y[:, k * 128:(k + 1) * 128],
                                                                    identity=ident[:]),
                     reads=[y, ident], writes=[ptt] if k == 0 else (), pwrites=() if k == 0 else [ptt])
            if c % 4 == 0:
                sgb = stg.next()
            cc = c % 4
            S.op("act", lambda e, ptt=ptt, sgb=sgb, cc=cc: e.copy(out=sgb[:, :, cc * 128:(cc + 1) * 128], in_=ptt[:]),
                 reads=[ptt], writes=[sgb] if cc == 0 else (), pwrites=() if cc == 0 else [sgb])
            if cc == 3 or c == SEQ // 128 - 1:
                tb = (c // 4) * 512
                n = (cc + 1) * 128
                S.dma("sp", ys.t[1, :, tb:tb + n].rearrange("(k p) t -> p k t", p=128), sgb[:, :, 0:n], reads=[sgb],
                      pwrites=[ys], key=sgb)
        _barrier(S)
        S.stack_pop()


def phase_E(S, nc, SEQ, lyr, x_src, x_dst, Wd, scr, final):
    TT = 512
    nsub = 4
    with contextlib.ExitStack() as st:
        S.stack_push(st)
        ident = make_ident(S, "E_ident")
        wb = S.sb("E_wb", [128, 3, 4, D], BF16)
        wo = S.sb("E_wo", [128, 8, D], BF16)
        wpg = S.sb("E_wpg", [128, 8, D], BF16)
        wpp = S.sb("E_wpp", [128, 2, D], BF16)
        gpl = S.sb("E_gpl", [128, D])
        gfin = S.sb("E_gfin", [128, D])
        for n in range(3):
            S.dma("pool", wb[:, n, :, :], Wd["w_branch"].t[lyr, n].rearrange("(k p) d -> p k d", p=128),
                  reads=[Wd["w_branch"]], pwrites=[wb], key=wb, max_dma_last_dim=4096)
        for k0 in range(0, 8, 4):
            S.dma("pool", wo[:, k0:k0 + 4, :], Wd["w_o"].t[lyr, k0 * 128:(k0 + 4) * 128, :].rearrange("(k p) d -> p k d", p=128),
                  reads=[Wd["w_o"]], pwrites=[wo], key=wo, max_dma_last_dim=4096)
            S.dma("pool", wpg[:, k0:k0 + 4, :], Wd["w_ple_gate"].t[lyr, k0 * 128:(k0 + 4) * 128, :].rearrange("(k p) d -> p k d", p=128),
                  reads=[Wd["w_ple_gate"]], pwrites=[wpg], key=wpg, max_dma_last_dim=4096)
        S.dma("pool", wpp[:], Wd["w_ple_proj"].t[lyr].rearrange("(k p) d -> p k d", p=128),
              reads=[Wd["w_ple_proj"]], pwrites=[wpp], key=wpp, max_dma_last_dim=4096)
        S.dma("sp", gpl[:], Wd["ple_norm_g"].t[lyr:lyr + 1, :].partition_broadcast(128), reads=[Wd["ple_norm_g"]], writes=[gpl])
        if final:
            S.dma("sp", gfin[:], Wd["final_norm_g"].t[0:1, :].partition_broadcast(128), reads=[Wd["final_norm_g"]], writes=[gfin])

        yst = S.sb("E_ys", [128, 3, 4, TT], BF16)
        mgt = S.sb("E_mg", [128, 24, TT], BF16)
        mrg = S.sb("E_mrg", [128, 8, TT])
        mrb = S.sb("E_mrb", [128, 8, TT], BF16)
        tmp = Ring([S.sb("E_tmp%d" % i, [128, TT]) for i in range(2)])
        xt = S.sb("E_x", [128, nsub, D])
        pin = S.sb("E_p", [128, nsub, PLE])
        pbf = S.sb("E_pbf", [128, PLE], BF16)
        pTs = S.sb("E_pT", [128, 2, 128], BF16)
        sq = S.sb("E_sq", [128, D], BF16)
        ss = S.sb("E_ss", [128, 1])
        rs = S.sb("E_rs", [128, 1])
        hp = S.sb("E_hp", [128, D], BF16)
        hpT = S.sb("E_hpT", [128, 8, 128], BF16)
        gate = S.sb("E_gate", [128, D])
        xo = Ring([S.sb("E_xo%d" % i, [128, D]) for i in range(2)])
        pz = Ring([S.ps("E_pz%d" % i, [128, TT]) for i in range(3)])
        po = Ring([S.ps("E_po%d" % i, [128, 512]) for i in range(2)])
        pg = Ring([S.ps("E_pg%d" % i, [128, 512]) for i in range(2)])
        ptr = S.ps("E_ptr", [128, 8, 128], BF16)

        for ti in range(SEQ // TT):
            t0 = ti * TT
            for n in range(3):
                S.dma("sp", yst[:, n, :, :], scr["ysT"].t[n, :, t0:t0 + TT].rearrange("(k p) t -> p k t", p=128),
                      reads=[scr["ysT"]], writes=[yst] if n == 0 else (), pwrites=() if n == 0 else [yst], key=yst)
            for k0 in range(0, 24, 8):
                S.dma("sp", mgt[:, k0:k0 + 8, :], scr["mgT"].t[k0 * 128:(k0 + 8) * 128, t0:t0 + TT].rearrange("(k p) t -> p k t", p=128),
                      reads=[scr["mgT"]], writes=[mgt] if k0 == 0 else (), pwrites=() if k0 == 0 else [mgt], key=mgt)
            S.dma("sp", xt[:], x_src.t[t0:t0 + TT, :].rearrange("(s p) d -> p s d", p=128), reads=[x_src], writes=[xt])
            S.dma("sp", pin[:], Wd["p"].t[lyr, t0:t0 + TT, :].rearrange("(s p) d -> p s d", p=128), reads=[Wd["p"]], writes=[pin])
            for dc in range(8):
                pzs = []
                for n in range(3):
                    pzz = pz.next()
                    pzs.append(pzz)
                    for k in range(4):
                        S.op("pe", lambda e, n=n, k=k, dc=dc, pzz=pzz: e.matmul(
                            pzz[:], lhsT=wb[:, n, k, dc * 128:(dc + 1) * 128], rhs=yst[:, n, k, :], start=(k == 0), stop=(k == 3)),
                            reads=[wb, yst], writes=[pzz] if k == 0 else (), pwrites=() if k == 0 else [pzz])
                S.op("dve", lambda e, dc=dc, p0=pzs[0]: e.tensor_tensor(out=mrg[:, dc, :], in0=p0[:], in1=mgt[:, dc, :], op=ALU.mult),
                     reads=[pzs[0], mgt], pwrites=[mrg])
                t1 = tmp.next()
                S.op("dve", lambda e, dc=dc, p1=pzs[1], t1=t1: e.tensor_tensor(out=t1[:], in0=p1[:], in1=mgt[:, 8 + dc, :], op=ALU.mult),
                     reads=[pzs[1], mgt], writes=[t1])
                t2 = tmp.next()
                S.op("dve", lambda e, dc=dc, p2=pzs[2], t2=t2: e.tensor_tensor(out=t2[:], in0=p2[:], in1=mgt[:, 16 + dc, :], op=ALU.mult),
                     reads=[pzs[2], mgt], writes=[t2])
                S.op("pool", lambda e, dc=dc, t1=t1: e.tensor_tensor(out=mrg[:, dc, :], in0=mrg[:, dc, :], in1=t1[:], op=ALU.add),
                     reads=[mrg, t1], pwrites=[mrg])
                S.op("pool", lambda e, dc=dc, t2=t2: e.tensor_tensor(out=mrb[:, dc, :], in0=mrg[:, dc, :], in1=t2[:], op=ALU.add),
                     reads=[mrg, t2], pwrites=[mrb])
            for s in range(nsub):
                for blk in range(2):
                    pp = po.next()
                    for k in range(8):
                        S.op("pe", lambda e, k=k, s=s, blk=blk, pp=pp: e.matmul(
                            pp[:], lhsT=mrb[:, k, s * 128:(s + 1) * 128], rhs=wo[:, k, blk * 512:(blk + 1) * 512],
                            start=(k == 0), stop=(k == 7)),
                            reads=[mrb, wo], writes=[pp] if k == 0 else (), pwrites=() if k == 0 else [pp])
                    S.op("dve", lambda e, s=s, blk=blk, pp=pp: e.tensor_tensor(
                        out=xt[:, s, blk * 512:(blk + 1) * 512], in0=pp[:], in1=xt[:, s, blk * 512:(blk + 1) * 512], op=ALU.add),
                        reads=[pp, xt], pwrites=[xt])
                rmsnorm_tile(S, xt[:, s, :], xt, gpl, hp[:], hp, sq, ss, rs)
                for k in range(8):
                    S.op("pe", lambda e, k=k: e.transpose(out=ptr[:, k, :], in_=hp[:, k * 128:(k + 1) * 128], identity=ident[:]),
                         reads=[hp, ident], writes=[ptr] if k == 0 else (), pwrites=() if k == 0 else [ptr])
                S.op("act", lambda e: e.copy(out=hpT[:], in_=ptr[:]), reads=[ptr], writes=[hpT])
                S.op("pool", lambda e, s=s: e.tensor_copy(out=pbf[:], in_=pin[:, s, :]), reads=[pin], writes=[pbf])
                for k in range(2):
                    S.op("pe", lambda e, k=k: e.transpose(out=ptr[:, k, :], in_=pbf[:, k * 128:(k + 1) * 128], identity=ident[:]),
                         reads=[pbf, ident, hpT], writes=[ptr] if k == 0 else (), pwrites=() if k == 0 else [ptr])
                S.op("act", lambda e: e.copy(out=pTs[:], in_=ptr[:, 0:2, :]), reads=[ptr], writes=[pTs])
                xout = xo.next()
                for blk in range(2):
                    pgg = pg.next()
                    for k in range(8):
                        S.op("pe", lambda e, k=k, blk=blk, pgg=pgg: e.matmul(
                            pgg[:], lhsT=hpT[:, k, :], rhs=wpg[:, k, blk * 512:(blk + 1) * 512], start=(k == 0), stop=(k == 7)),
                            reads=[hpT, wpg], writes=[pgg] if k == 0 else (), pwrites=() if k == 0 else [pgg])
                    S.op("act", lambda e, blk=blk, pgg=pgg: e.activation(out=gate[:, blk * 512:(blk + 1) * 512], in_=pgg[:], func=AF.Sigmoid),
                         reads=[pgg], pwrites=[gate])
                    ppp = pg.next()
                    for k in range(2):
                        S.op("pe", lambda e, k=k, blk=blk, ppp=ppp: e.matmul(
                            ppp[:], lhsT=pTs[:, k, :], rhs=wpp[:, k, blk * 512:(blk + 1) * 512], start=(k == 0), stop=(k == 1)),
                            reads=[pTs, wpp], writes=[ppp] if k == 0 else (), pwrites=() if k == 0 else [ppp])
                    S.op("dve", lambda e, blk=blk, ppp=ppp: e.tensor_tensor(
                        out=gate[:, blk * 512:(blk + 1) * 512], in0=ppp[:], in1=gate[:, blk * 512:(blk + 1) * 512], op=ALU.mult),
                        reads=[ppp, gate], pwrites=[gate])
                    S.op("pool", lambda e, blk=blk, s=s, xout=xout: e.tensor_tensor(
                        out=xout[:, blk * 512:(blk + 1) * 512], in0=gate[:, blk * 512:(blk + 1) * 512],
                        in1=xt[:, s, blk * 512:(blk + 1) * 512], op=ALU.add),
                        reads=[gate, xt], writes=[xout] if blk == 0 else (), pwrites=() if blk == 0 else [xout])
                if final:
                    S.op("act", lambda e, xout=xout: e.activation(out=sq[:], in_=xout[:], func=AF.Square, accum_out=ss[:]),
                         reads=[xout], writes=[sq, ss])
                    S.op("act", lambda e: e.activation(out=rs[:], in_=ss[:], func=AF.Sqrt, scale=1.0 / D, bias=EPS),
                         reads=[ss], writes=[rs])
                    S.op("dve", lambda e: e.reciprocal(out=rs[:], in_=rs[:]), reads=[rs], writes=[rs])
                    S.op("dve", lambda e, xout=xout: e.scalar_tensor_tensor(out=xout[:], in0=xout[:], scalar=rs[:, 0:1], in1=gfin[:],
                                                                            op0=ALU.mult, op1=ALU.mult),
                         reads=[xout, rs, gfin], writes=[xout])
                S.dma("sp", x_dst.t[t0 + s * 128:t0 + (s + 1) * 128, :], xout[:], reads=[xout], pwrites=[x_dst], key=xout)
        _barrier(S)
        S.stack_pop()


WSPEC = {
    "norm_g": [2, 1024], "w_in": [2, 1024, 9112], "cmp_w1": [2, 2, 32, 64, 128], "cmp_w2": [2, 2, 128, 64],
    "cmp_pe": [2, 2, 32, 64], "sg_ln_g": [2, 512], "sg_ln_b": [2, 512], "sg_w": [2, 8, 128, 128], "sg_b": [2, 8, 128],
    "rk_mu": [2, 1664], "rk_w0": [2, 512], "rk_w2": [2, 64, 512], "rk_a0": [2, 512], "rk_a2": [2, 64, 512],
    "rk_kk": [2, 8, 64], "rk_ka": [2, 8, 64], "rk_rk": [2, 8, 64], "rk_lnx_g": [2, 512], "rk_lnx_b": [2, 512],
    "w_branch": [2, 3, 512, 1024], "w_o": [2, 1024, 1024], "ple_norm_g": [2, 1024], "w_ple_gate": [2, 1024, 1024],
    "w_ple_proj": [2, 256, 1024], "final_norm_g": [1, 1024],
}


def build(SEQ, nlayers=2, enable=(1, 1, 1), scr_kind="Internal"):
    nc = bass.Bass("TRN2", target_bir_lowering=False)
    with contextlib.ExitStack() as stack:
        S = Sched(nc, stack)
        x = Buf("x", nc.dram_tensor("x", [SEQ, D], F32, kind="ExternalInput").ap())
        Wd = {"p": Buf("p", nc.dram_tensor("p", [2, SEQ, PLE], F32, kind="ExternalInput").ap())}
        for k, shp in WSPEC.items():
            Wd[k] = Buf(k, nc.dram_tensor(k, shp, F32, kind="ExternalInput").ap())
        out = Buf("out", nc.dram_tensor("out", [SEQ, D], F32, kind="ExternalOutput").ap())
        scr = make_scratch(S, SEQ, kind=scr_kind)
        xmid = S.dram("xmid", [SEQ, D], F32, kind=scr_kind)
        cur = x
        for lyr in range(nlayers):
            last = lyr == nlayers - 1
            dst = out if last else xmid
            phase_A(S, nc, SEQ, lyr, cur, Wd, scr)
            if enable[0]:
                phase_B(S, nc, SEQ, lyr, Wd, scr)
            if enable[1]:
                phase_C(S, nc, SEQ, lyr, Wd, scr)
            if enable[2]:
                phase_D(S, nc, SEQ, lyr, Wd, scr)
            phase_E(S, nc, SEQ, lyr, cur, dst, Wd, scr, final=(last and nlayers == 2))
            cur = dst
        S.emit()
    return nc


def phase_B(S, nc, SEQ, lyr, Wd, scr):
    NC = (SEQ - 32) // 16 + 1
    NT = (NC + 127) // 128
    NCp = NT * 128
    KT = SEQ // 128
    with contextlib.ExitStack() as st:
        S.stack_push(st)
        ident = make_ident(S, "B_ident")
        ksT = S.sb("B_ksT", [64, 2, SEQ], BF16)
        kwT = S.sb("B_kwT", [64, 2, SEQ], BF16)
        vs = S.sb("B_vs", [128, KT, 2, 65], BF16)
        vw = S.sb("B_vw", [128, KT, 2, 65], BF16)
        kcmpT = S.sb("B_kcmpT", [64, 2, NCp], BF16)
        Rc = S.sb("B_Rc", [128, NT, 2, 193], BF16)
        EXW = S.sb("B_EXW", [128, SEQ], BF16)
        S.dma("sp", ksT[:], scr["ksT"].t.rearrange("(g d) t -> d g t", g=2), reads=[scr["ksT"]], writes=[ksT])
        S.dma("sp", kwT[:], scr["kwT"].t.rearrange("(g d) t -> d g t", g=2), reads=[scr["kwT"]], writes=[kwT])
        S.op("pool", lambda e: e.memset(vs[:], 1.0), writes=[vs])
        S.op("pool", lambda e: e.memset(vw[:], 1.0), writes=[vw])
        for k0 in range(0, KT, 8):
            k1 = min(KT, k0 + 8)
            for (dst, c0) in ((vs, 0), (vw, 128)):
                for g in range(2):
                    S.dma("sp", dst[:, k0:k1, g, 0:64],
                          scr["vsw"].t[k0 * 128:k1 * 128, c0 + g * 64:c0 + (g + 1) * 64].rearrange("(k p) d -> p k d", p=128),
                          reads=[scr["vsw"]], pwrites=[dst], key=dst)
        S.op("pool", lambda e: e.memset(EXW[:], 1.0), writes=[EXW])
        S.op("pool", lambda e: e.affine_select(out=EXW[:], in_=EXW[:], pattern=[[1, SEQ]], compare_op=ALU.is_ge, fill=0.0,
                                               base=0, channel_multiplier=-64), reads=[EXW], writes=[EXW])
        S.op("pool", lambda e: e.affine_select(out=EXW[:], in_=EXW[:], pattern=[[-1, SEQ]], compare_op=ALU.is_ge, fill=0.0,
                                               base=63, channel_multiplier=64), reads=[EXW], writes=[EXW])
        S.op("pool", lambda e: e.memset(Rc[:], 1.0), writes=[Rc])
        for nt in range(NT):
            for g in range(2):
                S.op("pool", lambda e, nt=nt, g=g: e.affine_select(
                    out=Rc[:, nt, g, 65:193], in_=Rc[:, nt, g, 65:193], pattern=[[-4, 128]], compare_op=ALU.is_ge, fill=0.0,
                    base=nt * 128 + 1, channel_multiplier=1), reads=[Rc], writes=[Rc])
                S.op("pool", lambda e, nt=nt, g=g: e.affine_select(
                    out=Rc[:, nt, g, 65:193], in_=Rc[:, nt, g, 65:193], pattern=[[4, 128]], compare_op=ALU.is_ge, fill=0.0,
                    base=3 - nt * 128, channel_multiplier=-1), reads=[Rc], writes=[Rc])
        npad = NCp - NC
        if npad:
            S.op("pool", lambda e: e.affine_select(
                out=Rc[:, NT - 1, :, :], in_=Rc[:, NT - 1, :, :], pattern=[[0, 2 * 193]], compare_op=ALU.is_ge, fill=0.0,
                base=(NC - 1) - (NT - 1) * 128, channel_multiplier=-1), reads=[Rc], writes=[Rc])
        S.op("pool", lambda e: e.memset(kcmpT[:], 0.0), writes=[kcmpT])

        with contextlib.ExitStack() as st2:
            S.stack_push(st2)
            kvT = S.sb("B_kvT", [64, 2, SEQ], BF16)
            w1 = S.sb("B_w1", [64, 32, 128], BF16)
            w2 = S.sb("B_w2", [128, 64], BF16)
            peT = S.sb("B_peT", [64, 32])
            peTb = S.sb("B_peTb", [64, 32], BF16)
            cb = S.sb("B_cb", [128, 1])
            hid = S.sb("B_hid", [128, NCp], BF16)
            ph = S.ps("B_ph", [128, 512])
            pc1 = S.ps("B_pc1", [128, 512])
            pk = S.ps("B_pk", [128, 512])
            for kv in range(2):
                src = scr["kcT"] if kv == 0 else scr["vcT"]
                S.dma("sp", kvT[:], src.t.rearrange("(g d) t -> d g t", g=2), reads=[src], writes=[kvT])
                S.dma("pool", w1[:], Wd["cmp_w1"].t[lyr, kv].rearrange("l d h -> d l h"), reads=[Wd["cmp_w1"]], writes=[w1])
                S.dma("pool", w2[:], Wd["cmp_w2"].t[lyr, kv], reads=[Wd["cmp_w2"]], writes=[w2])
                S.dma("sp", peT[:], Wd["cmp_pe"].t[lyr, kv].rearrange("l d -> d l"), reads=[Wd["cmp_pe"]], writes=[peT],
                      allow_slow_non_contiguous=True)
                S.op("dve", lambda e: e.tensor_copy(out=peTb[:], in_=peT[:]), reads=[peT], writes=[peTb])
                for l in range(32):
                    S.op("pe", lambda e, l=l: e.matmul(pc1[:, 0:1], lhsT=w1[:, l, :], rhs=peTb[:, l:l + 1], start=(l == 0), stop=(l == 31)),
                         reads=[w1, peTb], writes=[pc1] if l == 0 else (), pwrites=() if l == 0 else [pc1])
                S.op("dve", lambda e: e.tensor_copy(out=cb[:], in_=pc1[:, 0:1]), reads=[pc1], writes=[cb])
                for g in range(2):
                    S.op("dve", lambda e: e.memset(hid[:], 0.0), writes=[hid])
                    for n0 in range(0, NC, 512):
                        nn = min(512, NC - n0)
                        for l in range(32):
                            S.op("pe", lambda e, l=l, g=g, n0=n0, nn=nn: e.matmul(
                                ph[:, 0:nn], lhsT=w1[:, l, :], rhs=kvT[:, g, n0 * 16 + l: n0 * 16 + l + (nn - 1) * 16 + 1: 16], start=(l == 0), stop=(l == 31)),
                                reads=[w1, kvT], writes=[ph] if l == 0 else (), pwrites=() if l == 0 else [ph])
                        S.op("act", lambda e, n0=n0, nn=nn: e.activation(out=hid[:, n0:n0 + nn], in_=ph[:, 0:nn], func=AF.Silu, bias=cb[:, 0:1]),
                             reads=[ph, cb], pwrites=[hid])
                    if kv == 0:
                        for n0 in range(0, NC, 512):
                            nn = min(512, NC - n0)
                            S.op("pe", lambda e, n0=n0, nn=nn: e.matmul(pk[0:64, 0:nn], lhsT=w2[:], rhs=hid[:, n0:n0 + nn], start=True, stop=True),
                                 reads=[w2, hid], writes=[pk])
                            S.op("dve", lambda e, g=g, n0=n0, nn=nn: e.tensor_copy(out=kcmpT[:, g, n0:n0 + nn], in_=pk[0:64, 0:nn]),
                                 reads=[pk], pwrites=[kcmpT])
                    else:
                        for nt in range(NT):
                            rows = min(128, NC - nt * 128)
                            S.op("pe", lambda e, nt=nt: e.matmul(pk[:, 0:64], lhsT=hid[:, nt * 128:(nt + 1) * 128], rhs=w2[:], start=True, stop=True),
                                 reads=[w2, hid], writes=[pk])
                            S.op("dve", lambda e, g=g, nt=nt: e.tensor_copy(out=Rc[:, nt, g, 0:64], in_=pk[:, 0:64]),
                                 reads=[pk], pwrites=[Rc])
            _barrier(S)
            S.stack_pop()

        qt = Ring([S.sb("B_q%d" % i, [64, 8, 128], BF16) for i in range(2)])
        gt = Ring([S.sb("B_g%d" % i, [128, 24]) for i in range(2)])
        nzt = Ring([S.sb("B_nz%d" % i, [128, 512], BF16) for i in range(2)])
        Et = Ring([S.sb("B_E%d" % i, [128, 512], BF16) for i in range(4)])
        psT = Ring([S.ps("B_psT%d" % i, [128, 512]) for i in range(3)])
        pcA = S.ps("B_pcA", [128, 2, 193])
        pcB = S.ps("B_pcB", [128, 2, 193])
        pos = S.ps("B_pos", [128, 4, 65])
        pow_ = S.ps("B_pow", [128, 4, 65])
        pmisc = S.ps("B_pmisc", [128, 4, 128], BF16)
        oc = S.sb("B_oc", [128, 4, 193])
        rcs = S.sb("B_rcs", [128, 4])
        rss = S.sb("B_rss", [128, 4])
        rws = S.sb("B_rws", [128, 4])
        cc = S.sb("B_cc", [128, 3, 4])
        sc = S.sb("B_sc", [128, 128])
        sc2 = S.sb("B_sc2", [128, 128])
        m1 = S.sb("B_m1", [128, 8])
        m2 = S.sb("B_m2", [128, 8])
        negq = S.sb("B_negq", [128, 128], BF16)
        negT4 = S.sb("B_negT4", [128, 4, 128], BF16)
        yg = S.sb("B_yg", [128, 4, 64])
        ytmp = S.sb("B_ytmp", [128, 4, 64])
        ynsa = S.sb("B_ynsa", [128, 512], BF16)
        stg = Ring([S.sb("B_stg%d" % i, [128, 4, 128], BF16) for i in range(2)])

        def qk_exp(kT_ap, kbuf, q_ap, qbuf, neg_lhsT=None):
            p = psT.next()
            if neg_lhsT is not None:
                S.op("pe", lambda e, p=p: e.matmul(p[:], lhsT=neg_lhsT, rhs=negT4[:].rearrange("p h q -> p (h q)"), start=True, stop=False),
                     reads=[EXW, negT4], writes=[p])
                S.op("pe", lambda e, p=p: e.matmul(p[:], lhsT=kT_ap, rhs=q_ap, start=False, stop=True), reads=[kbuf, qbuf], pwrites=[p])
            else:
                S.op("pe", lambda e, p=p: e.matmul(p[:], lhsT=kT_ap, rhs=q_ap, start=True, stop=True), reads=[kbuf, qbuf], writes=[p])
            E = Et.next()
            S.op("act", lambda e, p=p, E=E: e.activation(out=E[:], in_=p[:], func=AF.Exp), reads=[p], writes=[E])
            return E

        def mask(E, base, cm, qstep):
            S.op("pool", lambda e, E=E: e.affine_select(out=E[:], in_=E[:], pattern=[[0, 4], [qstep, 128]], compare_op=ALU.is_ge,
                                                       fill=0.0, base=base, channel_multiplier=cm), reads=[E], writes=[E])

        for qb in range(SEQ // 128):
            q0 = qb * 128
            q = qt.next(); gg = gt.next(); nz = nzt.next()
            S.dma("sp", q[:], scr["qT"].t[:, q0:q0 + 128].rearrange("(h d) t -> d h t", h=8), reads=[scr["qT"]], writes=[q])
            S.dma("sp", gg[:], scr["gate"].t[q0:q0 + 128, :], reads=[scr["gate"]], writes=[gg])
            S.dma("sp", nz[:], scr["nzs"].t[q0:q0 + 128, :], reads=[scr["nzs"]], writes=[nz])
            for g in range(2):
                q_ap = q[:, 4 * g:4 * g + 4, :].rearrange("d h q -> d (h q)")
                n_max = min(8 * qb + 6, NC - 1)
                ntl = n_max // 128 + 1
                for nt in range(ntl):
                    E = qk_exp(kcmpT[:, g, nt * 128:(nt + 1) * 128], kcmpT, q_ap, q)
                    if q0 - 16 * (128 * nt + 127) - 31 < 0:
                        mask(E, q0 - 16 * 128 * nt - 31, -16, 1)
                    for h in range(4):
                        pcx = pcA if h < 2 else pcB
                        first = (nt == 0 and h % 2 == 0)
                        S.op("pe", lambda e, E=E, h=h, pcx=pcx, nt=nt, first=first, g=g, ntl=ntl: e.matmul(
                            pcx[:, h % 2, :], lhsT=E[:, h * 128:(h + 1) * 128], rhs=Rc[:, nt, g, :], start=first,
                            stop=(nt == ntl - 1 and h % 2 == 1), skip_group_check=True),
                            reads=[E, Rc], writes=[pcx] if first else (), pwrites=() if first else [pcx])
                S.op("act", lambda e: e.copy(out=oc[:, 0:2, :], in_=pcA[:]), reads=[pcA], pwrites=[oc])
                S.op("act", lambda e: e.copy(out=oc[:, 2:4, :], in_=pcB[:]), reads=[pcB], pwrites=[oc])
                S.op("dve", lambda e: e.tensor_scalar(out=rcs[:], in0=oc[:, :, 64], scalar1=1e-30, scalar2=None, op0=ALU.max),
                     reads=[oc], writes=[rcs])
                S.op("dve", lambda e: e.reciprocal(out=rcs[:], in_=rcs[:]), reads=[rcs], writes=[rcs])
                S.op("dve", lambda e: e.tensor_scalar(out=sc[:], in0=oc[:, 0, 65:193], scalar1=rcs[:, 0:1], scalar2=None, op0=ALU.mult),
                     reads=[oc, rcs], writes=[sc])
                for h in range(1, 4):
                    S.op("dve", lambda e, h=h: e.scalar_tensor_tensor(out=sc[:], in0=oc[:, h, 65:193], scalar=rcs[:, h:h + 1], in1=sc[:],
                                                                      op0=ALU.mult, op1=ALU.add), reads=[oc, rcs, sc], writes=[sc])
                for half in range(2):
                    tb = 2 * qb + half
                    ps_ = slice(half * 64, (half + 1) * 64)
                    if tb + 1 < 128:
                        S.op("dve", lambda e, ps_=ps_, tb=tb: e.memset(sc[ps_, tb + 1:128], -1e4), reads=[sc], writes=[sc])
                    lo = max(tb - 1, 0)
                    S.op("dve", lambda e, ps_=ps_, tb=tb, lo=lo: e.memset(sc[ps_, lo:tb + 1], 1e4), reads=[sc], writes=[sc])
                S.op("dve", lambda e: e.memset(sc[:, 0:1], 1e4), reads=[sc], writes=[sc])
                S.op("dve", lambda e: e.max(out=m1[:], in_=sc[:]), reads=[sc], writes=[m1])
                S.op("dve", lambda e: e.match_replace(out=sc2[:], in_to_replace=m1[:], in_values=sc[:], imm_value=-3e4),
                     reads=[sc, m1], writes=[sc2])
                S.op("dve", lambda e: e.max(out=m2[:], in_=sc2[:]), reads=[sc2], writes=[m2])
                S.op("dve", lambda e: e.tensor_scalar(out=negq[:], in0=sc[:], scalar1=m2[:, 7:8], scalar2=-1e4, op0=ALU.is_lt, op1=ALU.mult),
                     reads=[sc, m2], writes=[negq])
                S.op("pe", lambda e: e.transpose(out=pmisc[:, 0, :], in_=negq[:], identity=ident[:]), reads=[negq, ident], writes=[pmisc])
                S.op("dve", lambda e: e.tensor_copy(out=negT4[:], in_=pmisc[:, 0:1, :].to_broadcast([128, 4, 128])),
                     reads=[pmisc], writes=[negT4])
                kts = list(range(max(0, qb - 4), qb + 1))
                for i, kt in enumerate(kts):
                    E = qk_exp(kwT[:, g, kt * 128:(kt + 1) * 128], kwT, q_ap, q)
                    if kt == qb - 4:
                        mask(E, -1, 1, -1)
                    if kt == qb:
                        mask(E, 0, -1, 1)
                    for h in range(4):
                        first = (i == 0 and h == 0)
                        S.op("pe", lambda e, E=E, h=h, kt=kt, first=first, last=(i == len(kts) - 1 and h == 3), g=g: e.matmul(
                            pow_[:, h, :], lhsT=E[:, h * 128:(h + 1) * 128], rhs=vw[:, kt, g, :], start=first, stop=last,
                            skip_group_check=True),
                            reads=[E, vw], writes=[pow_] if first else (), pwrites=() if first else [pow_])
                for kt in range(qb + 1):
                    E = qk_exp(ksT[:, g, kt * 128:(kt + 1) * 128], ksT, q_ap, q, neg_lhsT=EXW[:, kt * 128:(kt + 1) * 128])
                    if kt == qb:
                        mask(E, 0, -1, 1)
                    for h in range(4):
                        first = (kt == 0 and h == 0)
                        S.op("pe", lambda e, E=E, h=h, kt=kt, first=first, last=(kt == qb and h == 3), g=g: e.matmul(
                            pos[:, h, :], lhsT=E[:, h * 128:(h + 1) * 128], rhs=vs[:, kt, g, :], start=first, stop=last,
                            skip_group_check=True),
                            reads=[E, vs], writes=[pos] if first else (), pwrites=() if first else [pos])
                S.op("dve", lambda e: e.reciprocal(out=rss[:], in_=pos[:, :, 64]), reads=[pos], writes=[rss])
                S.op("dve", lambda e: e.reciprocal(out=rws[:], in_=pow_[:, :, 64]), reads=[pow_], writes=[rws])
                gv = gg[:, g * 12:(g + 1) * 12].rearrange("p (h b) -> p b h", b=3)
                for b, rr in enumerate((rcs, rss, rws)):
                    S.op("dve", lambda e, b=b, rr=rr, gv=gv: e.tensor_tensor(out=cc[:, b, :], in0=gv[:, b, :], in1=rr[:], op=ALU.mult),
                         reads=[gg, rr], pwrites=[cc])
                if qb == 2:
                    dbg_dump(S, "oc%d" % g, oc[:], oc, [128, 4, 193])
                    dbg_dump(S, "pos%d" % g, pos[:], pos, [128, 4, 65])
                    dbg_dump(S, "pow%d" % g, pow_[:], pow_, [128, 4, 65])
                    dbg_dump(S, "cc%d" % g, cc[:], cc, [128, 3, 4])
                    dbg_dump(S, "gg%d" % g, gg[:], gg, [128, 24])
                    dbg_dump(S, "sc%d" % g, sc[:], sc, [128, 128])
                    dbg_dump(S, "negq%d" % g, negq[:], negq, [128, 128])
                bc = lambda b: cc[:, b, :].unsqueeze(2).to_broadcast([128, 4, 64])
                S.op("dve", lambda e: e.tensor_tensor(out=yg[:], in0=oc[:, :, 0:64], in1=bc(0), op=ALU.mult), reads=[oc, cc], writes=[yg])
                S.op("dve", lambda e: e.tensor_tensor(out=ytmp[:], in0=pos[:, :, 0:64], in1=bc(1), op=ALU.mult), reads=[pos, cc], writes=[ytmp])
                S.op("pool", lambda e: e.tensor_tensor(out=yg[:], in0=yg[:], in1=ytmp[:], op=ALU.add), reads=[yg, ytmp], writes=[yg])
                S.op("dve", lambda e: e.tensor_tensor(out=ytmp[:], in0=pow_[:, :, 0:64], in1=bc(2), op=ALU.mult), reads=[pow_, cc], writes=[ytmp])
                S.op("pool", lambda e: e.tensor_tensor(out=yg[:], in0=yg[:], in1=ytmp[:], op=ALU.add), reads=[yg, ytmp], writes=[yg])
                if qb == 2:
                    dbg_dump(S, "yg%d" % g, yg[:], yg, [128, 4, 64])
                S.op("pool", lambda e, g=g, nz=nz: e.tensor_tensor(out=ynsa[:, g * 256:(g + 1) * 256], in0=yg[:].rearrange("p h d -> p (h d)"),
                                                                   in1=nz[:, g * 256:(g + 1) * 256], op=ALU.mult),
                     reads=[yg, nz], pwrites=[ynsa])
            for k in range(4):
                S.op("pe", lambda e, k=k: e.transpose(out=pmisc[:, k, :], in_=ynsa[:, k * 128:(k + 1) * 128], identity=ident[:]),
                     reads=[ynsa, ident], writes=[pmisc] if k == 0 else (), pwrites=() if k == 0 else [pmisc])
            sg = stg.next()
            S.op("act", lambda e, sg=sg: e.copy(out=sg[:], in_=pmisc[:]), reads=[pmisc], writes=[sg])
            S.dma("sp", scr["ysT"].t[0, :, q0:q0 + 128].rearrange("(k p) t -> p k t", p=128), sg[:], reads=[sg], pwrites=[scr["ysT"]], key=sg)
        _barrier(S)
        S.stack_pop()


def phase_D(S, nc, SEQ, lyr, Wd, scr):
    TP = 128
    TB = 8
    GN_EPS = 64e-5
    xtok = scr["xtok"]
    with contextlib.ExitStack() as st:
        S.stack_push(st)
        identF = make_ident(S, "D_ident", F32)
        ones = S.sb("D_ones", [128, 128])
        S.op("pool", lambda e: e.memset(ones[:], 0.0), writes=[ones])
        S.op("pool", lambda e: e.memset(ones[0:64, 0:64], 1.0), reads=[ones], writes=[ones])
        S.op("pool", lambda e: e.memset(ones[64:128, 64:128], 1.0), reads=[ones], writes=[ones])

        def cvec(name, key, n):
            t = S.sb("D_" + name, [128, n])
            S.dma("sp", t[:], Wd[key].t[lyr].rearrange("(c p) -> p c", p=128), reads=[Wd[key]], writes=[t],
                  allow_slow_non_contiguous=True)
            return t

        def cvec2(name, key):
            t = S.sb("D_" + name, [128, 4])
            S.dma("sp", t[:], Wd[key].t[lyr].rearrange("(c a) j -> (a j) c", a=2), reads=[Wd[key]], writes=[t],
                  allow_slow_non_contiguous=True)
            return t
        mu = cvec("mu", "rk_mu", 13)
        w0 = cvec("w0", "rk_w0", 4)
        a0 = cvec("a0", "rk_a0", 4)
        lg = cvec("lg", "rk_lnx_g", 4)
        lb = cvec("lb", "rk_lnx_b", 4)
        kkc = cvec2("kkc", "rk_kk")
        ka = cvec2("ka", "rk_ka")
        rkc = cvec2("rkc", "rk_rk")
        omka = S.sb("D_omka", [128, 4])
        S.op("pool", lambda e: e.tensor_scalar(out=omka[:], in0=ka[:], scalar1=-1.0, scalar2=1.0, op0=ALU.mult, op1=ALU.add),
             reads=[ka], writes=[omka])
        w2 = S.sb("D_w2", [64, 512], BF16)
        a2 = S.sb("D_a2", [128, 512], BF16)
        S.dma("pool", w2[:], Wd["rk_w2"].t[lyr], reads=[Wd["rk_w2"]], writes=[w2])
        S.dma("pool", a2[64:128, :], Wd["rk_a2"].t[lyr], reads=[Wd["rk_a2"]], writes=[a2])
        St = S.ps("D_state", [128, 4, 64])
        S.op("dve", lambda e: e.memset(St[:], 0.0), writes=[St])

        rst = S.sb("D_rst", [128, 13, TP + 1])
        xs = S.sb("D_xs", [128, 13, TP])
        th = S.sb("D_th", [128, TP], BF16)
        dd = S.sb("D_dd", [128, 4, TP])
        aa = S.sb("D_aa", [128, 4, TP])
        kkf = S.sb("D_kkf", [128, 4, TP])
        sq = S.sb("D_sq", [128, 4, TP])
        rn = S.sb("D_rn", [128, 4, TP])
        kp = S.sb("D_kp", [128, 4, TP])
        am = S.sb("D_am", [128, 4, TP])
        bm = S.sb("D_bm", [128, 4, TP])
        t1 = S.sb("D_t1", [128, 4, TP])
        bonus = S.sb("D_bonus", [128, 4, TP])
        vv = S.sb("D_vv", [128, 4, TP])
        tk = S.sb("D_tk", [128, 5, 4, 128])
        bcr = Ring([S.sb("D_bc%d" % i, [128, TB, 5, 256]) for i in range(2)])
        tmp = S.sb("D_tmp", [128, 4, 64])
        tmp2 = S.sb("D_tmp2", [128, 4, 64])
        kv = Ring([S.sb("D_kv%d" % i, [128, 4, 64]) for i in range(2)])
        sa = S.sb("D_sa", [128, 4])
        ybuf = S.sb("D_y", [128, 4, TP])
        ysq = S.sb("D_ysq", [128, 4, TP])
        mean = S.sb("D_mean", [128, 4, TP])
        var = S.sb("D_var", [128, 4, TP])
        rzt = S.sb("D_rz", [128, 4, TP], BF16)
        yo = S.sb("D_yo", [128, 4, TP], BF16)
        pa = Ring([S.ps("D_pa%d" % i, [128, 4, 128]) for i in range(4)])

        bc4 = lambda t: t[:].unsqueeze(2).to_broadcast([128, 4, TP])
        for nb in range(SEQ // TP):
            t0 = nb * TP
            S.dma("sp", rst[:, :, 1:TP + 1], scr["rsT"].t[:, t0:t0 + TP].rearrange("(c p) t -> p c t", p=128), reads=[scr["rsT"]],
                  writes=[rst])
            if nb == 0:
                S.op("pool", lambda e: e.memset(rst[:, :, 0:1], 0.0), reads=[rst], pwrites=[rst])
            else:
                S.dma("sp", rst[:, :, 0:1], scr["rsT"].t[:, t0 - 1:t0].rearrange("(c p) t -> p c t", p=128), reads=[scr["rsT"]],
                      pwrites=[rst], key=rst, allow_slow_non_contiguous=True)
            S.op("pool", lambda e: e.tensor_tensor(out=xs[:], in0=rst[:, :, 0:TP], in1=rst[:, :, 1:TP + 1], op=ALU.subtract),
                 reads=[rst], writes=[xs])
            S.op("pool", lambda e: e.tensor_tensor(out=xs[:], in0=xs[:], in1=mu[:].unsqueeze(2).to_broadcast([128, 13, TP]), op=ALU.mult),
                 reads=[xs, mu], writes=[xs])
            S.op("pool", lambda e: e.tensor_tensor(out=xs[:], in0=xs[:], in1=rst[:, :, 1:TP + 1], op=ALU.add), reads=[xs, rst], writes=[xs])
            r = xs[:, 0:4, :]; k = xs[:, 4:8, :]; v = xs[:, 8:12, :]
            S.op("act", lambda e: e.activation(out=th[0:64, :], in_=xs[0:64, 12, :], func=AF.Tanh), reads=[xs], pwrites=[th])
            S.op("act", lambda e: e.copy(out=th[64:128, :], in_=xs[64:128, 12, :]), reads=[xs], pwrites=[th])
            pw = pa.next(); pp = pa.next()
            for p in range(4):
                S.op("pe", lambda e, p=p, pw=pw: e.matmul(pw[:, p, :], lhsT=w2[0:64, p * 128:(p + 1) * 128], rhs=th[0:64, :], start=True, stop=True),
                     reads=[w2, th], writes=[pw] if p == 0 else (), pwrites=() if p == 0 else [pw])
                S.op("pe", lambda e, p=p, pp=pp: e.matmul(pp[:, p, :], lhsT=a2[64:128, p * 128:(p + 1) * 128], rhs=th[64:128, :], start=True, stop=True),
                     reads=[a2, th], writes=[pp] if p == 0 else (), pwrites=() if p == 0 else [pp])
            for p in range(4):
                S.op("act", lambda e, p=p, pw=pw: e.activation(out=dd[:, p, :], in_=pw[:, p, :], func=AF.Sigmoid, bias=w0[:, p:p + 1]),
                     reads=[pw, w0], pwrites=[dd])
                S.op("act", lambda e, p=p, pp=pp: e.activation(out=aa[:, p, :], in_=pp[:, p, :], func=AF.Sigmoid, bias=a0[:, p:p + 1]),
                     reads=[pp, a0], pwrites=[aa])
            S.op("act", lambda e: e.activation(out=dd[:], in_=dd[:], func=AF.Exp, scale=-0.6065306597126334), reads=[dd], writes=[dd])
            S.op("pool", lambda e: e.tensor_tensor(out=kkf[:], in0=k, in1=bc4(kkc), op=ALU.mult), reads=[xs, kkc], writes=[kkf])
            S.op("pool", lambda e: e.tensor_tensor(out=sq[:], in0=kkf[:], in1=kkf[:], op=ALU.mult), reads=[kkf], writes=[sq])
            pn = pa.next()
            for p in range(4):
                S.op("pe", lambda e, p=p, pn=pn: e.matmul(pn[:, p, :], lhsT=ones[:], rhs=sq[:, p, :], start=True, stop=True),
                     reads=[ones, sq], writes=[pn] if p == 0 else (), pwrites=() if p == 0 else [pn])
            S.op("act", lambda e, pn=pn: e.activation(out=rn[:], in_=pn[:], func=AF.Sqrt), reads=[pn], writes=[rn])
            S.op("pool", lambda e: e.tensor_scalar(out=rn[:], in0=rn[:], scalar1=1e-12, scalar2=None, op0=ALU.max), reads=[rn], writes=[rn])
            S.op("dve", lambda e: e.reciprocal(out=rn[:], in_=rn[:]), reads=[rn], writes=[rn])
            S.op("pool", lambda e: e.tensor_tensor(out=kkf[:], in0=kkf[:], in1=rn[:], op=ALU.mult), reads=[kkf, rn], writes=[kkf])
            S.op("pool", lambda e: e.tensor_tensor(out=t1[:], in0=aa[:], in1=bc4(ka), op=ALU.mult), reads=[aa, ka], writes=[t1])
            S.op("pool", lambda e: e.tensor_tensor(out=t1[:], in0=t1[:], in1=bc4(omka), op=ALU.add), reads=[t1, omka], writes=[t1])
            S.op("pool", lambda e: e.tensor_tensor(out=kp[:], in0=k, in1=t1[:], op=ALU.mult), reads=[xs, t1], writes=[kp])
            S.op("pool", lambda e: e.tensor_scalar(out=am[:], in0=kkf[:], scalar1=-1.0, scalar2=None, op0=ALU.mult), reads=[kkf], writes=[am])
            S.op("pool", lambda e: e.tensor_tensor(out=bm[:], in0=kkf[:], in1=aa[:], op=ALU.mult), reads=[kkf, aa], writes=[bm])
            S.op("pool", lambda e: e.tensor_tensor(out=t1[:], in0=r, in1=kp[:], op=ALU.mult), reads=[xs, kp], writes=[t1])
            S.op("pool", lambda e: e.tensor_tensor(out=sq[:], in0=t1[:], in1=bc4(rkc), op=ALU.mult), reads=[t1, rkc], writes=[sq])
            pr = pa.next()
            for p in range(4):
                S.op("pe", lambda e, p=p, pr=pr: e.matmul(pr[:, p, :], lhsT=ones[:], rhs=sq[:, p, :], start=True, stop=True),
                     reads=[ones, sq], writes=[pr] if p == 0 else (), pwrites=() if p == 0 else [pr])
            S.op("act", lambda e, pr=pr: e.copy(out=bonus[:], in_=pr[:]), reads=[pr], writes=[bonus])
            S.op("pool", lambda e: e.tensor_tensor(out=bonus[:], in0=bonus[:], in1=v, op=ALU.mult), reads=[bonus, xs], writes=[bonus])
            S.op("pool", lambda e: e.tensor_copy(out=vv[:], in_=v), reads=[xs], writes=[vv])
            S.op("pool", lambda e: e.tensor_copy(out=t1[:], in_=r), reads=[xs], writes=[t1])
            for oi, src in enumerate((am, bm, dd, kp, t1)):
                pt = pa.next()
                for p in range(4):
                    S.op("pe", lambda e, p=p, src=src, pt=pt: e.transpose(out=pt[:, p, :], in_=src[:, p, :], identity=identF[:]),
                         reads=[src, identF], writes=[pt] if p == 0 else (), pwrites=() if p == 0 else [pt])
                S.op("act", lambda e, oi=oi, pt=pt: e.copy(out=tk[:, oi, :, :], in_=pt[:]), reads=[pt], pwrites=[tk])
            for oi in range(5):
                for h2 in range(2):
                    S.dma("sp", xtok.t[t0:t0 + TP, oi, h2, :].rearrange("t (p j) -> t p j", p=4), tk[:, oi, :, h2 * 64:(h2 + 1) * 64],
                          reads=[tk], pwrites=[xtok], key=tk)
            S.dma("sp", rzt[:], scr["rzT"].t[:, t0:t0 + TP].rearrange("(c p) t -> p c t", p=128), reads=[scr["rzT"]], writes=[rzt])
            xflat = xtok.t.rearrange("t o h c -> (t o) h c")
            for tb in range(0, TP, TB):
                bc = bcr.next()
                for h2 in range(2):
                    S.dma("sp", bc[h2 * 64:(h2 + 1) * 64, :, :, :].rearrange("p t o c -> p (t o) c"),
                          xflat[(t0 + tb) * 5:(t0 + tb + TB) * 5, h2, :].partition_broadcast(64),
                          reads=[xtok], writes=[bc] if h2 == 0 else (), pwrites=() if h2 == 0 else [bc], key=bc)
                for tt in range(TB):
                    t = tb + tt
                    A = bc[:, tt, 0, :].rearrange("p (a j) -> p a j", a=4)
                    B = bc[:, tt, 1, :].rearrange("p (a j) -> p a j", a=4)
                    Dd = bc[:, tt, 2, :].rearrange("p (a j) -> p a j", a=4)
                    Kk = bc[:, tt, 3, :].rearrange("p (a j) -> p a j", a=4)
                    R = bc[:, tt, 4, :].rearrange("p (a j) -> p a j", a=4)
                    kvb = kv.next()
                    S.op("pool", lambda e, Kk=Kk, t=t, kvb=kvb: e.tensor_tensor(out=kvb[:], in0=Kk, in1=vv[:, :, t:t + 1].to_broadcast([128, 4, 64]),
                                                                              op=ALU.mult), reads=[bc, vv], writes=[kvb])
                    S.op("dve", lambda e, A=A: e.tensor_tensor(out=tmp[:], in0=St[:], in1=A, op=ALU.mult), reads=[St, bc], writes=[tmp])
                    S.op("dve", lambda e: e.tensor_reduce(out=sa[:], in_=tmp[:], axis=AX.X, op=ALU.add), reads=[tmp], writes=[sa])
                    S.op("dve", lambda e, Dd=Dd: e.tensor_tensor(out=St[:], in0=St[:], in1=Dd, op=ALU.mult), reads=[St, bc, tmp], writes=[St])
                    for p in range(4):
                        S.op("dve", lambda e, B=B, p=p: e.scalar_tensor_tensor(out=St[:, p, :], in0=B[:, p, :], scalar=sa[:, p:p + 1], in1=St[:, p, :],
                                                                             op0=ALU.mult, op1=ALU.add), reads=[bc, sa, St], writes=[St])
                    S.op("dve", lambda e, kvb=kvb: e.tensor_tensor(out=St[:], in0=St[:], in1=kvb[:], op=ALU.add), reads=[St, kvb], writes=[St])
                    S.op("dve", lambda e, R=R: e.tensor_tensor(out=tmp[:], in0=St[:], in1=R, op=ALU.mult), reads=[St, bc], writes=[tmp])
                    S.op("dve", lambda e, t=t: e.tensor_reduce(out=ybuf[:, :, t], in_=tmp[:], axis=AX.X, op=ALU.add), reads=[tmp], pwrites=[ybuf])
            S.op("pool", lambda e: e.tensor_tensor(out=ysq[:], in0=ybuf[:], in1=ybuf[:], op=ALU.mult), reads=[ybuf], writes=[ysq])
            pm = pa.next(); pq = pa.next()
            for p in range(4):
                S.op("pe", lambda e, p=p, pm=pm: e.matmul(pm[:, p, :], lhsT=ones[:], rhs=ybuf[:, p, :], start=True, stop=True),
                     reads=[ones, ybuf], writes=[pm] if p == 0 else (), pwrites=() if p == 0 else [pm])
                S.op("pe", lambda e, p=p, pq=pq: e.matmul(pq[:, p, :], lhsT=ones[:], rhs=ysq[:, p, :], start=True, stop=True),
                     reads=[ones, ysq], writes=[pq] if p == 0 else (), pwrites=() if p == 0 else [pq])
            S.op("act", lambda e, pm=pm: e.activation(out=mean[:], in_=pm[:], func=AF.Copy, scale=1.0 / 64), reads=[pm], writes=[mean])
            S.op("act", lambda e, pq=pq: e.activation(out=var[:], in_=pq[:], func=AF.Copy, scale=1.0 / 64), reads=[pq], writes=[var])
            S.op("pool", lambda e: e.tensor_tensor(out=ysq[:], in0=mean[:], in1=mean[:], op=ALU.mult), reads=[mean, ysq], writes=[ysq])
            S.op("pool", lambda e: e.tensor_tensor(out=var[:], in0=var[:], in1=ysq[:], op=ALU.subtract), reads=[var, ysq], writes=[var])
            S.op("act", lambda e: e.activation(out=var[:], in_=var[:], func=AF.Sqrt, bias=GN_EPS, scale=1.0), reads=[var], writes=[var])
            S.op("dve", lambda e: e.reciprocal(out=var[:], in_=var[:]), reads=[var], writes=[var])
            S.op("pool", lambda e: e.tensor_tensor(out=mean[:], in0=ybuf[:], in1=mean[:], op=ALU.subtract), reads=[ybuf, mean], writes=[mean])
            S.op("pool", lambda e: e.tensor_tensor(out=mean[:], in0=mean[:], in1=var[:], op=ALU.mult), reads=[mean, var], writes=[mean])
            S.op("pool", lambda e: e.tensor_tensor(out=mean[:], in0=mean[:], in1=bc4(lg), op=ALU.mult), reads=[mean, lg], writes=[mean])
            S.op("pool", lambda e: e.tensor_tensor(out=mean[:], in0=mean[:], in1=bc4(lb), op=ALU.add), reads=[mean, lb], writes=[mean])
            S.op("pool", lambda e: e.tensor_tensor(out=mean[:], in0=mean[:], in1=bonus[:], op=ALU.add), reads=[mean, bonus], writes=[mean])
            S.op("pool", lambda e: e.tensor_tensor(out=yo[:], in0=mean[:], in1=rzt[:], op=ALU.mult), reads=[mean, rzt], writes=[yo])
            S.dma("sp", scr["ysT"].t[2, :, t0:t0 + TP].rearrange("(c p) t -> p c t", p=128), yo[:], reads=[yo], pwrites=[scr["ysT"]], key=yo)
        _barrier(S)
        S.stack_pop()


_NC_CACHE = {}


def kernel(**inputs):
    SEQ = 8192
    if "nc" not in _NC_CACHE:
        _NC_CACHE["nc"] = build(SEQ, nlayers=2, enable=(1, 1, 1), scr_kind="Internal")
    nc = _NC_CACHE["nc"]
    x = np.ascontiguousarray(np.asarray(inputs["x"], dtype=np.float32))
    p = np.asarray(inputs["p"], dtype=np.float32)
    base = {}
    for k in WSPEC:
        v = np.ascontiguousarray(np.asarray(inputs[k], dtype=np.float32))
        base[k] = v.reshape(WSPEC[k])
    in_maps = []
    for b in range(8):
        m = dict(base)
        m["x"] = np.ascontiguousarray(x[b])
        m["p"] = np.ascontiguousarray(p[:, b])
        in_maps.append(m)
    res = run_bass_kernel_spmd(nc, in_maps, core_ids=list(range(8)))
    return np.stack([np.asarray(r["out"], dtype=np.float32) for r in res.results], axis=0)
```

```python
import contextlib
import numpy as np
import concourse.bass as bass
import concourse.mybir as mybir

F32 = mybir.dt.float32
BF16 = mybir.dt.bfloat16
AF = mybir.ActivationFunctionType
ALU = mybir.AluOpType
AX = mybir.AxisListType

ENGS = ("pe", "act", "dve", "pool", "sp")


class Buf:
    __slots__ = ("name", "w", "wfull", "r", "t")

    def __init__(self, name, t=None):
        self.name = name
        self.t = t
        self.w = []
        self.wfull = []
        self.r = []

    def __getitem__(self, k):
        return self.t[k]


class Op:
    __slots__ = ("eng", "fn", "deps", "marked", "tick", "dma", "idx")

    def __init__(self, eng, fn, dma):
        self.eng = eng
        self.fn = fn
        self.deps = []
        self.marked = False
        self.tick = None
        self.dma = dma
        self.idx = None


class DmaSem:
    def __init__(self):
        self.sem = None
        self.count = 0


class Sched:
    def __init__(self, nc, stack):
        self.nc = nc
        self.stack = stack
        self.ops = {e: [] for e in ENGS}
        self.all_ops = []
        self.dsems = {}
        self.n_sems = 0
        self.fence = []
        self.stacks = [stack]
        self.phase_keys = []
        self.free_ds = []
        self.all_ds = []
        self.keep = []

    def stack_push(self, st):
        self.stacks.append(st)
        self.phase_keys.append([])

    def stack_pop(self):
        self.stacks.pop()
        for kid in self.phase_keys.pop():
            ds = self.dsems.pop(kid, None)
            if ds is not None:
                self.free_ds.append(ds)

    def sb(self, name, shape, dt=F32):
        self.n_sems += 1
        name = "%s_u%d" % (name, self.n_sems)
        t = self.stacks[-1].enter_context(self.nc.sbuf_tensor(name, list(shape), dt))
        return Buf(name, t)

    def ps(self, name, shape, dt=F32):
        self.n_sems += 1
        name = "%s_u%d" % (name, self.n_sems)
        t = self.stacks[-1].enter_context(self.nc.psum_tensor(name, list(shape), dt))
        return Buf(name, t)

    def dram(self, name, shape, dt, kind="Internal"):
        t = self.nc.dram_tensor(name, list(shape), dt, kind=kind)
        return Buf(name, t.ap())

    def _add(self, eng, fn, reads, writes, pwrites, dma):
        op = Op(eng, fn, dma)
        deps = list(self.fence)
        for b in reads:
            deps.extend(b.w)
        for b in writes:
            deps.extend(b.w)
            deps.extend(b.r)
        for b in pwrites:
            deps.extend(b.wfull)
            deps.extend(b.r)
        seen = set()
        for d in deps:
            if id(d) in seen or d is op:
                continue
            seen.add(id(d))
            if d.eng == "pe" and eng == "pe" and d.dma is None and dma is None:
                continue
            op.deps.append(d)
            d.marked = True
        for b in reads:
            b.r.append(op)
            if len(b.r) > 24:
                b.r = self._prune(b.r)
        for b in writes:
            b.w = [op]
            b.wfull = [op]
            b.r = []
        for b in pwrites:
            b.w.append(op)
            if len(b.w) > 24:
                b.w = self._prune(b.w)
        op.idx = len(self.all_ops)
        self.all_ops.append(op)
        self.ops[eng].append(op)
        return op

    @staticmethod
    def _prune(lst):
        last = {}
        for o in lst:
            key = (o.eng, None) if o.dma is None else ("dma", id(o.dma))
            last[key] = o
        return list(last.values())

    def op(self, eng, fn, reads=(), writes=(), pwrites=()):
        return self._add(eng, fn, reads, writes, pwrites, None)

    def dma(self, eng, out_ap, in_ap, reads=(), writes=(), pwrites=(), key=None, **kw):
        if key is None:
            key = (list(writes) + list(pwrites))[0]
        ds = self.dsems.get(id(key))
        if ds is None:
            if self.free_ds:
                ds = self.free_ds.pop()
            else:
                ds = DmaSem()
                self.all_ds.append(ds)
            self.dsems[id(key)] = ds
            self.keep.append(key)
            if self.phase_keys:
                self.phase_keys[-1].append(id(key))
        fn = lambda e, o=out_ap, i=in_ap, kw=kw: e.dma_start(out=o, in_=i, **kw)
        op = self._add(eng, fn, reads, writes, pwrites, ds)
        ds.count += 16
        op.tick = ds.count
        return op

    def barrier_bufs(self, bufs):
        pass

    def emit(self):
        nc = self.nc
        stack = self.stack
        esem = {}
        for e in ENGS:
            esem[e] = stack.enter_context(nc.semaphore("s_" + e))
        for ds in self.all_ds:
            ds.sem = stack.enter_context(nc.semaphore("d%d" % self.n_sems))
            self.n_sems += 1
        for e in ENGS:
            c = 0
            for o in self.ops[e]:
                if o.dma is None:
                    if o.marked:
                        c += 1
                        o.tick = c
        self.max_ticks = {e: max([o.tick or 0 for o in self.ops[e] if o.dma is None] + [0]) for e in ENGS}

        def evkey(d):
            if d.dma is not None:
                return ("d", id(d.dma)), d.dma.sem, d.tick
            return ("e", d.eng), esem[d.eng], d.tick

        def run(eng_name, eng):
            seen = {}
            for o in self.ops[eng_name]:
                waits = {}
                for d in o.deps:
                    k, sem, val = evkey(d)
                    if seen.get(k, 0) >= val:
                        continue
                    if k not in waits or waits[k][1] < val:
                        waits[k] = (sem, val)
                for k, (sem, val) in waits.items():
                    eng.wait_ge(sem, val)
                    seen[k] = val
                inst = o.fn(eng)
                if o.dma is not None:
                    inst.then_inc(o.dma.sem, 16)
                elif o.marked:
                    inst.then_inc(esem[eng_name], 1)
            if eng_name == "sp":
                for e2 in ENGS:
                    m = self.max_ticks[e2]
                    if m > 0:
                        eng.wait_ge(esem[e2], m)
                for ds in self.all_ds:
                    if ds.count:
                        eng.wait_ge(ds.sem, ds.count)

        block = stack.enter_context(nc.Block())

        @block.tensor
        def _(e):
            run("pe", e)

        @block.scalar
        def _(e):
            run("act", e)

        @block.vector
        def _(e):
            run("dve", e)

        @block.gpsimd
        def _(e):
            run("pool", e)

        @block.sync
        def _(e):
            run("sp", e)


from concourse.bass_utils import run_bass_kernel_spmd

D = 1024
NCOL = 8600
PLE = 256
EPS = 1e-6


DEBUG = {}
_dbg_n = [0]


def dbg_dump(S, name, ap, buf, shape, cond=True):
    if not DEBUG.get("on") or not cond:
        return
    _dbg_n[0] += 1
    t = S.stacks[-1].enter_context(S.nc.sbuf_tensor("dbgsb_%d" % _dbg_n[0], list(shape), F32))
    tb = Buf("dbgsb", t)
    d = S.dram("dbg_" + name, list(shape), F32, kind="ExternalOutput")
    S.op("act", lambda e: e.copy(out=t[:], in_=ap), reads=[buf], writes=[tb])
    S.dma("sp", d.t, t[:], reads=[tb], writes=[d], key=tb)


class Ring:
    def __init__(self, bufs):
        self.bufs = bufs
        self.i = 0

    def next(self):
        b = self.bufs[self.i % len(self.bufs)]
        self.i += 1
        return b


def _barrier(S):
    fence = []
    for e in ENGS:
        comp = [o for o in S.ops[e] if o.dma is None]
        if comp:
            fence.append(comp[-1])
    lastd = {}
    for o in S.all_ops:
        if o.dma is not None:
            lastd[id(o.dma)] = o
    fence.extend(lastd.values())
    S.fence = fence


def make_ident(S, name="ident", dt=BF16):
    ident = S.sb(name, [128, 128], dt)
    S.op("pool", lambda e: e.memset(ident[:], 0.0), writes=[ident])
    S.op("pool", lambda e: e.affine_select(out=ident[:], in_=ident[:], pattern=[[-1, 128]],
                                           compare_op=ALU.not_equal, fill=1.0, base=0,
                                           channel_multiplier=1), reads=[ident], writes=[ident])
    return ident


def load_w_bf16(S, dst, k, src_ap, srcbuf):
    S.dma("pool", dst, src_ap, reads=[srcbuf], pwrites=[k], key=k, max_dma_last_dim=4096)


def rmsnorm_tile(S, xt_ap, xt_buf, g_buf, h_ap, h_buf, sq, ss, rs, eps=EPS, extra_reads=()):
    S.op("act", lambda e: e.activation(out=sq[:], in_=xt_ap, func=AF.Square, accum_out=ss[:]),
         reads=[xt_buf] + list(extra_reads), writes=[sq, ss])
    S.op("act", lambda e: e.activation(out=rs[:], in_=ss[:], func=AF.Sqrt, scale=1.0 / D, bias=eps),
         reads=[ss], writes=[rs])
    S.op("dve", lambda e: e.reciprocal(out=rs[:], in_=rs[:]), reads=[rs], writes=[rs])
    S.op("dve", lambda e: e.scalar_tensor_tensor(out=h_ap, in0=xt_ap, scalar=rs[:, 0:1], in1=g_buf[:],
                                                 op0=ALU.mult, op1=ALU.mult),
         reads=[xt_buf, rs, g_buf], pwrites=[h_buf])


def phase_A(S, nc, SEQ, lyr, x_src, Wd, scr):
    TT = 512
    nsub = TT // 128
    with contextlib.ExitStack() as st:
        S.stack_push(st)
        wt = S.sb("A_w", [128, 8, NCOL], BF16)
        gt = S.sb("A_g", [128, D])
        ident = make_ident(S, "A_ident")
        xt = S.sb("A_x", [128, nsub, D])
        sq = S.sb("A_sq", [128, D], BF16)
        ss = S.sb("A_ss", [128, 1])
        rs = S.sb("A_rs", [128, 1])
        h = S.sb("A_h", [128, nsub, D], BF16)
        hT = S.sb("A_hT", [128, 8, TT], BF16)
        stg_b = Ring([S.sb("A_sb%d" % i, [128, 512], BF16) for i in range(4)])
        stg_f = Ring([S.sb("A_sf%d" % i, [128, 512], F32) for i in range(3)])
        pT = Ring([S.ps("A_pT%d" % i, [128, 8, 128], BF16) for i in range(2)])
        pacc = Ring([S.ps("A_pa%d" % i, [128, 512], F32) for i in range(6)])

        w_in = Wd["w_in"]
        for k in range(8):
            S.dma("pool", wt[:, k, :], w_in.t[lyr, k * 128:(k + 1) * 128, 0:NCOL], reads=[w_in], pwrites=[wt],
                  key=wt, max_dma_last_dim=4096)
        S.dma("sp", gt[:], Wd["norm_g"].t[lyr:lyr + 1, :].partition_broadcast(128), reads=[Wd["norm_g"]],
              writes=[gt])

        FM = []
        for c in range(4):
            FM.append((c * 128, scr["qT"], c * 128, AF.Copy, 0.125, BF16))
        FM.append((512, scr["kcT"], 0, None, 1.0, BF16))
        FM.append((640, scr["vcT"], 0, None, 1.0, BF16))
        FM.append((768, scr["ksT"], 0, None, 1.0, BF16))
        FM.append((1024, scr["kwT"], 0, None, 1.0, BF16))
        for c in range(13):
            FM.append((3352 + c * 128, scr["rsT"], c * 128, None, 1.0, F32))
        for c in range(4):
            FM.append((5016 + c * 128, scr["rzT"], c * 128, AF.Silu, 1.0, BF16))
        for c in range(24):
            FM.append((5528 + c * 128, scr["mgT"], c * 128, AF.Sigmoid, 1.0, BF16))
        TM = [
            (896, 128, scr["vsw"], 0, None, BF16),
            (1152, 128, scr["vsw"], 128, None, BF16),
            (1280, 24, scr["gate"], 0, AF.Sigmoid, F32),
            (1304, 512, scr["nzs"], 0, AF.Silu, BF16),
            (1816, 512, scr["su"], 0, None, F32),
            (2328, 512, scr["sv"], 0, None, F32),
            (2840, 512, scr["szs"], 0, AF.Silu, BF16),
        ]
        evac_i = [0]

        def evac(out_ap, out_buf, in_ap, in_buf, func, scale):
            if func is None and scale == 1.0:
                if evac_i[0] % 2 == 0:
                    S.op("dve", lambda e: e.tensor_copy(out=out_ap, in_=in_ap), reads=[in_buf], writes=[out_buf])
                else:
                    S.op("act", lambda e: e.copy(out=out_ap, in_=in_ap), reads=[in_buf], writes=[out_buf])
                evac_i[0] += 1
            else:
                S.op("act", lambda e: e.activation(out=out_ap, in_=in_ap, func=func, scale=scale),
                     reads=[in_buf], writes=[out_buf])

        for ti in range(SEQ // TT):
            t0 = ti * TT
            S.dma("sp", xt[:], x_src.t[t0:t0 + TT, :].rearrange("(s p) d -> p s d", p=128), reads=[x_src],
                  writes=[xt])
            for s in range(nsub):
                rmsnorm_tile(S, xt[:, s, :], xt, gt, h[:, s, :], h, sq, ss, rs)
                pt = pT.next()
                for k in range(8):
                    S.op("pe", lambda e, k=k, s=s, pt=pt: e.transpose(out=pt[:, k, :], in_=h[:, s, k * 128:(k + 1) * 128],
                                                                     identity=ident[:]),
                         reads=[h, ident], writes=[pt] if k == 0 else (), pwrites=() if k == 0 else [pt])
                S.op("dve", lambda e, s=s, pt=pt: e.tensor_copy(out=hT[:, :, s * 128:(s + 1) * 128], in_=pt[:]),
                     reads=[pt], pwrites=[hT])
            for (c0, dbuf, r0, func, scale, dt) in FM:
                pa = pacc.next()
                for k in range(8):
                    S.op("pe", lambda e, k=k, pa=pa, c0=c0: e.matmul(pa[:], lhsT=wt[:, k, c0:c0 + 128], rhs=hT[:, k, :],
                                                                    start=(k == 0), stop=(k == 7)),
                         reads=[wt, hT], writes=[pa] if k == 0 else (), pwrites=() if k == 0 else [pa])
                sg = stg_b.next() if dt == BF16 else stg_f.next()
                evac(sg[:], sg, pa[:], pa, func, scale)
                S.dma("sp", dbuf.t[r0:r0 + 128, t0:t0 + TT], sg[:], reads=[sg], pwrites=[dbuf], key=sg)
            for s in range(nsub):
                for (c0, ncol, dbuf, dc0, func, dt) in TM:
                    pa = pacc.next()
                    for k in range(8):
                        S.op("pe", lambda e, k=k, pa=pa, c0=c0, ncol=ncol, s=s: e.matmul(
                            pa[:, 0:ncol], lhsT=hT[:, k, s * 128:(s + 1) * 128], rhs=wt[:, k, c0:c0 + ncol],
                            start=(k == 0), stop=(k == 7)),
                            reads=[wt, hT], writes=[pa] if k == 0 else (), pwrites=() if k == 0 else [pa])
                    sg = stg_b.next() if dt == BF16 else stg_f.next()
                    evac(sg[:, 0:ncol], sg, pa[:, 0:ncol], pa, func, 1.0)
                    S.dma("sp", dbuf.t[t0 + s * 128:t0 + (s + 1) * 128, dc0:dc0 + ncol], sg[:, 0:ncol], reads=[sg],
                          pwrites=[dbuf], key=sg)
        _barrier(S)
        S.stack_pop()


def make_scratch(S, SEQ, kind="Internal"):
    scr = {}
    def mk(name, shape, dt):
        scr[name] = S.dram(name, shape, dt, kind=kind)
    mk("qT", [512, SEQ], BF16)
    mk("kcT", [128, SEQ], BF16)
    mk("vcT", [128, SEQ], BF16)
    mk("ksT", [128, SEQ], BF16)
    mk("kwT", [128, SEQ], BF16)
    mk("vsw", [SEQ, 256], BF16)
    mk("gate", [SEQ, 24], F32)
    mk("nzs", [SEQ, 512], BF16)
    mk("su", [SEQ, 512], F32)
    mk("sv", [SEQ, 512], F32)
    mk("szs", [SEQ, 512], BF16)
    mk("rsT", [1664, SEQ], F32)
    mk("rzT", [512, SEQ], BF16)
    mk("mgT", [3072, SEQ], BF16)
    mk("ysT", [3, 512, SEQ], BF16)
    mk("xtok", [SEQ, 5, 2, 256], F32)
    return scr


def phase_C(S, nc, SEQ, lyr, Wd, scr):
    LN_EPS = 1e-5
    with contextlib.ExitStack() as st:
        S.stack_push(st)
        ident = make_ident(S, "C_ident")
        wraw = S.sb("C_wraw", [128, 8, 128])
        wbf = S.sb("C_wbf", [128, 8, 128], BF16)
        WT = S.sb("C_WT", [128, 8, 128], BF16)
        bsT = S.sb("C_bsT", [128, 8])
        lng = S.sb("C_lng", [128, 512])
        lnb = S.sb("C_lnb", [128, 512])
        pw = S.ps("C_pw", [128, 8, 128], BF16)
        S.dma("sp", wraw[:], Wd["sg_w"].t[lyr].rearrange("g t s -> t g s"), reads=[Wd["sg_w"]], writes=[wraw])
        S.dma("sp", bsT[:], Wd["sg_b"].t[lyr].rearrange("g t -> t g"), reads=[Wd["sg_b"]], writes=[bsT],
              allow_slow_non_contiguous=True)
        S.dma("sp", lng[:], Wd["sg_ln_g"].t[lyr:lyr + 1, :].partition_broadcast(128), reads=[Wd["sg_ln_g"]], writes=[lng])
        S.dma("sp", lnb[:], Wd["sg_ln_b"].t[lyr:lyr + 1, :].partition_broadcast(128), reads=[Wd["sg_ln_b"]], writes=[lnb])
        S.op("pool", lambda e: e.affine_select(out=wraw[:], in_=wraw[:], pattern=[[0, 8], [-1, 128]],
                                               compare_op=ALU.is_ge, fill=0.0, base=0, channel_multiplier=1),
             reads=[wraw], writes=[wraw])
        S.op("dve", lambda e: e.tensor_copy(out=wbf[:], in_=wraw[:]), reads=[wraw], writes=[wbf])
        for g in range(8):
            S.op("pe", lambda e, g=g: e.transpose(out=pw[:, g, :], in_=wbf[:, g, :], identity=ident[:]),
                 reads=[wbf, ident], pwrites=[pw])
        S.op("dve", lambda e: e.tensor_copy(out=WT[:], in_=pw[:]), reads=[pw], writes=[WT])

        NB = 2
        svt = Ring([S.sb("C_sv%d" % i, [128, 512]) for i in range(NB)])
        sut = Ring([S.sb("C_su%d" % i, [128, 512]) for i in range(NB)])
        szt = Ring([S.sb("C_sz%d" % i, [128, 512], BF16) for i in range(NB)])
        stats = S.sb("C_stats", [128, 6])
        mv = S.sb("C_mv", [128, 2])
        rstd = S.sb("C_rstd", [128, 1])
        vn0 = S.sb("C_vnf", [128, 512])
        vn = Ring([S.sb("C_vn%d" % i, [128, 512], BF16) for i in range(2)])
        y0 = S.sb("C_y0", [128, 512])
        yb = Ring([S.sb("C_yb%d" % i, [128, 512], BF16) for i in range(2)])
        pm = Ring([S.ps("C_pm%d" % i, [128, 512]) for i in range(2)])
        pt = Ring([S.ps("C_pt%d" % i, [128, 4, 128], BF16) for i in range(2)])
        stg = Ring([S.sb("C_stg%d" % i, [128, 4, 512], BF16) for i in range(2)])
        ys = scr["ysT"]
        sgb = None
        for c in range(SEQ // 128):
            t0 = c * 128
            v = svt.next(); u = sut.next(); z = szt.next()
            S.dma("sp", v[:], scr["sv"].t[t0:t0 + 128, :], reads=[scr["sv"]], writes=[v])
            S.dma("sp", u[:], scr["su"].t[t0:t0 + 128, :], reads=[scr["su"]], writes=[u])
            S.dma("sp", z[:], scr["szs"].t[t0:t0 + 128, :], reads=[scr["szs"]], writes=[z])
            S.op("dve", lambda e, v=v: e.bn_stats(out=stats[:], in_=v[:]), reads=[v], writes=[stats])
            S.op("dve", lambda e: e.bn_aggr(out=mv[:], in_=stats[:]), reads=[stats], writes=[mv])
            S.op("act", lambda e: e.activation(out=rstd[:], in_=mv[:, 1:2], func=AF.Sqrt, bias=LN_EPS, scale=1.0),
                 reads=[mv], writes=[rstd])
            S.op("dve", lambda e: e.reciprocal(out=rstd[:], in_=rstd[:]), reads=[rstd], writes=[rstd])
            S.op("dve", lambda e, v=v: e.tensor_scalar(out=vn0[:], in0=v[:], scalar1=mv[:, 0:1], scalar2=rstd[:, 0:1],
                                                       op0=ALU.subtract, op1=ALU.mult),
                 reads=[v, mv, rstd], writes=[vn0])
            S.op("pool", lambda e: e.tensor_tensor(out=vn0[:], in0=vn0[:], in1=lng[:], op=ALU.mult),
                 reads=[vn0, lng], writes=[vn0])
            vb = vn.next()
            S.op("pool", lambda e, vb=vb: e.tensor_tensor(out=vb[:], in0=vn0[:], in1=lnb[:], op=ALU.add),
                 reads=[vn0, lnb], writes=[vb])
            pmm = pm.next()
            for g in range(8):
                S.op("pe", lambda e, g=g, vb=vb, pmm=pmm: e.matmul(pmm[:, g * 64:(g + 1) * 64], lhsT=WT[:, g, :],
                                                                   rhs=vb[:, g * 64:(g + 1) * 64], start=True, stop=True),
                     reads=[WT, vb], writes=[pmm] if g == 0 else (), pwrites=() if g == 0 else [pmm])
            S.op("dve", lambda e, pmm=pmm: e.tensor_tensor(
                out=y0[:].rearrange("p (g d) -> p g d", g=8), in0=pmm[:].rearrange("p (g d) -> p g d", g=8),
                in1=bsT[:].unsqueeze(2).to_broadcast([128, 8, 64]), op=ALU.add), reads=[pmm, bsT], writes=[y0])
            S.op("pool", lambda e, u=u: e.tensor_tensor(out=y0[:], in0=y0[:], in1=u[:], op=ALU.mult),
                 reads=[y0, u], writes=[y0])
            y = yb.next()
            S.op("dve", lambda e, y=y, z=z: e.tensor_tensor(out=y[:], in0=y0[:], in1=z[:], op=ALU.mult),
                 reads=[y0, z], writes=[y])
            ptt = pt.next()
            for k in range(4):
                S.op("pe", lambda e, k=k, y=y, ptt=ptt: e.transpose(out=ptt[:, k, :], in_=y[:, k * 128:(k + 1) * 128],
                                                                    identity=ident[:]),
                     reads=[y, ident], writes=[ptt] if k == 0 else (), pwrites=() if k == 0 else [ptt])
            if c % 4 == 0:
                sgb = stg.next()
            cc = c % 4
            S.op("act", lambda e, ptt=ptt, sgb=sgb, cc=cc: e.copy(out=sgb[:, :, cc * 128:(cc + 1) * 128], in_=ptt[:]),
                 reads=[ptt], writes=[sgb] if cc == 0 else (), pwrites=() if cc == 0 else [sgb])
            if cc == 3 or c == SEQ // 128 - 1:
                tb = (c // 4) * 512
                n = (cc + 1) * 128
                S.dma("sp", ys.t[1, :, tb:tb + n].rearrange("(k p) t -> p k t", p=128), sgb[:, :, 0:n], reads=[sgb],
                      pwrites=[ys], key=sgb)
        _barrier(S)
        S.stack_pop()


def phase_E(S, nc, SEQ, lyr, x_src, x_dst, Wd, scr, final):
    TT = 512
    nsub = 4
    with contextlib.ExitStack() as st:
        S.stack_push(st)
        ident = make_ident(S, "E_ident")
        wb = S.sb("E_wb", [128, 3, 4, D], BF16)
        wo = S.sb("E_wo", [128, 8, D], BF16)
        wpg = S.sb("E_wpg", [128, 8, D], BF16)
        wpp = S.sb("E_wpp", [128, 2, D], BF16)
        gpl = S.sb("E_gpl", [128, D])
        gfin = S.sb("E_gfin", [128, D])
        for n in range(3):
            S.dma("pool", wb[:, n, :, :], Wd["w_branch"].t[lyr, n].rearrange("(k p) d -> p k d", p=128),
                  reads=[Wd["w_branch"]], pwrites=[wb], key=wb, max_dma_last_dim=4096)
        for k0 in range(0, 8, 4):
            S.dma("pool", wo[:, k0:k0 + 4, :], Wd["w_o"].t[lyr, k0 * 128:(k0 + 4) * 128, :].rearrange("(k p) d -> p k d", p=128),
                  reads=[Wd["w_o"]], pwrites=[wo], key=wo, max_dma_last_dim=4096)
            S.dma("pool", wpg[:, k0:k0 + 4, :], Wd["w_ple_gate"].t[lyr, k0 * 128:(k0 + 4) * 128, :].rearrange("(k p) d -> p k d", p=128),
                  reads=[Wd["w_ple_gate"]], pwrites=[wpg], key=wpg, max_dma_last_dim=4096)
        S.dma("pool", wpp[:], Wd["w_ple_proj"].t[lyr].rearrange("(k p) d -> p k d", p=128),
              reads=[Wd["w_ple_proj"]], pwrites=[wpp], key=wpp, max_dma_last_dim=4096)
        S.dma("sp", gpl[:], Wd["ple_norm_g"].t[lyr:lyr + 1, :].partition_broadcast(128), reads=[Wd["ple_norm_g"]], writes=[gpl])
        if final:
            S.dma("sp", gfin[:], Wd["final_norm_g"].t[0:1, :].partition_broadcast(128), reads=[Wd["final_norm_g"]], writes=[gfin])

        yst = S.sb("E_ys", [128, 3, 4, TT], BF16)
        mgt = S.sb("E_mg", [128, 24, TT], BF16)
        mrg = S.sb("E_mrg", [128, 8, TT])
        mrb = S.sb("E_mrb", [128, 8, TT], BF16)
        tmp = Ring([S.sb("E_tmp%d" % i, [128, TT]) for i in range(2)])
        xt = S.sb("E_x", [128, nsub, D])
        pin = S.sb("E_p", [128, nsub, PLE])
        pbf = S.sb("E_pbf", [128, PLE], BF16)
        pTs = S.sb("E_pT", [128, 2, 128], BF16)
        sq = S.sb("E_sq", [128, D], BF16)
        ss = S.sb("E_ss", [128, 1])
        rs = S.sb("E_rs", [128, 1])
        hp = S.sb("E_hp", [128, D], BF16)
        hpT = S.sb("E_hpT", [128, 8, 128], BF16)
        gate = S.sb("E_gate", [128, D])
        xo = Ring([S.sb("E_xo%d" % i, [128, D]) for i in range(2)])
        pz = Ring([S.ps("E_pz%d" % i, [128, TT]) for i in range(3)])
        po = Ring([S.ps("E_po%d" % i, [128, 512]) for i in range(2)])
        pg = Ring([S.ps("E_pg%d" % i, [128, 512]) for i in range(2)])
        ptr = S.ps("E_ptr", [128, 8, 128], BF16)

        for ti in range(SEQ // TT):
            t0 = ti * TT
            for n in range(3):
                S.dma("sp", yst[:, n, :, :], scr["ysT"].t[n, :, t0:t0 + TT].rearrange("(k p) t -> p k t", p=128),
                      reads=[scr["ysT"]], writes=[yst] if n == 0 else (), pwrites=() if n == 0 else [yst], key=yst)
            for k0 in range(0, 24, 8):
                S.dma("sp", mgt[:, k0:k0 + 8, :], scr["mgT"].t[k0 * 128:(k0 + 8) * 128, t0:t0 + TT].rearrange("(k p) t -> p k t", p=128),
                      reads=[scr["mgT"]], writes=[mgt] if k0 == 0 else (), pwrites=() if k0 == 0 else [mgt], key=mgt)
            S.dma("sp", xt[:], x_src.t[t0:t0 + TT, :].rearrange("(s p) d -> p s d", p=128), reads=[x_src], writes=[xt])
            S.dma("sp", pin[:], Wd["p"].t[lyr, t0:t0 + TT, :].rearrange("(s p) d -> p s d", p=128), reads=[Wd["p"]], writes=[pin])
            for dc in range(8):
                pzs = []
                for n in range(3):
                    pzz = pz.next()
                    pzs.append(pzz)
                    for k in range(4):
                        S.op("pe", lambda e, n=n, k=k, dc=dc, pzz=pzz: e.matmul(
                            pzz[:], lhsT=wb[:, n, k, dc * 128:(dc + 1) * 128], rhs=yst[:, n, k, :], start=(k == 0), stop=(k == 3)),
                            reads=[wb, yst], writes=[pzz] if k == 0 else (), pwrites=() if k == 0 else [pzz])
                S.op("dve", lambda e, dc=dc, p0=pzs[0]: e.tensor_tensor(out=mrg[:, dc, :], in0=p0[:], in1=mgt[:, dc, :], op=ALU.mult),
                     reads=[pzs[0], mgt], pwrites=[mrg])
                t1 = tmp.next()
                S.op("dve", lambda e, dc=dc, p1=pzs[1], t1=t1: e.tensor_tensor(out=t1[:], in0=p1[:], in1=mgt[:, 8 + dc, :], op=ALU.mult),
                     reads=[pzs[1], mgt], writes=[t1])
                t2 = tmp.next()
                S.op("dve", lambda e, dc=dc, p2=pzs[2], t2=t2: e.tensor_tensor(out=t2[:], in0=p2[:], in1=mgt[:, 16 + dc, :], op=ALU.mult),
                     reads=[pzs[2], mgt], writes=[t2])
                S.op("pool", lambda e, dc=dc, t1=t1: e.tensor_tensor(out=mrg[:, dc, :], in0=mrg[:, dc, :], in1=t1[:], op=ALU.add),
                     reads=[mrg, t1], pwrites=[mrg])
                S.op("pool", lambda e, dc=dc, t2=t2: e.tensor_tensor(out=mrb[:, dc, :], in0=mrg[:, dc, :], in1=t2[:], op=ALU.add),
                     reads=[mrg, t2], pwrites=[mrb])
            for s in range(nsub):
                for blk in range(2):
                    pp = po.next()
                    for k in range(8):
                        S.op("pe", lambda e, k=k, s=s, blk=blk, pp=pp: e.matmul(
                            pp[:], lhsT=mrb[:, k, s * 128:(s + 1) * 128], rhs=wo[:, k, blk * 512:(blk + 1) * 512],
                            start=(k == 0), stop=(k == 7)),
                            reads=[mrb, wo], writes=[pp] if k == 0 else (), pwrites=() if k == 0 else [pp])
                    S.op("dve", lambda e, s=s, blk=blk, pp=pp: e.tensor_tensor(
                        out=xt[:, s, blk * 512:(blk + 1) * 512], in0=pp[:], in1=xt[:, s, blk * 512:(blk + 1) * 512], op=ALU.add),
                        reads=[pp, xt], pwrites=[xt])
                rmsnorm_tile(S, xt[:, s, :], xt, gpl, hp[:], hp, sq, ss, rs)
                for k in range(8):
                    S.op("pe", lambda e, k=k: e.transpose(out=ptr[:, k, :], in_=hp[:, k * 128:(k + 1) * 128], identity=ident[:]),
                         reads=[hp, ident], writes=[ptr] if k == 0 else (), pwrites=() if k == 0 else [ptr])
                S.op("act", lambda e: e.copy(out=hpT[:], in_=ptr[:]), reads=[ptr], writes=[hpT])
                S.op("pool", lambda e, s=s: e.tensor_copy(out=pbf[:], in_=pin[:, s, :]), reads=[pin], writes=[pbf])
                for k in range(2):
                    S.op("pe", lambda e, k=k: e.transpose(out=ptr[:, k, :], in_=pbf[:, k * 128:(k + 1) * 128], identity=ident[:]),
                         reads=[pbf, ident, hpT], writes=[ptr] if k == 0 else (), pwrites=() if k == 0 else [ptr])
                S.op("act", lambda e: e.copy(out=pTs[:], in_=ptr[:, 0:2, :]), reads=[ptr], writes=[pTs])
                xout = xo.next()
                for blk in range(2):
                    pgg = pg.next()
                    for k in range(8):
                        S.op("pe", lambda e, k=k, blk=blk, pgg=pgg: e.matmul(
                            pgg[:], lhsT=hpT[:, k, :], rhs=wpg[:, k, blk * 512:(blk + 1) * 512], start=(k == 0), stop=(k == 7)),
                            reads=[hpT, wpg], writes=[pgg] if k == 0 else (), pwrites=() if k == 0 else [pgg])
                    S.op("act", lambda e, blk=blk, pgg=pgg: e.activation(out=gate[:, blk * 512:(blk + 1) * 512], in_=pgg[:], func=AF.Sigmoid),
                         reads=[pgg], pwrites=[gate])
                    ppp = pg.next()
                    for k in range(2):
                        S.op("pe", lambda e, k=k, blk=blk, ppp=ppp: e.matmul(
                            ppp[:], lhsT=pTs[:, k, :], rhs=wpp[:, k, blk * 512:(blk + 1) * 512], start=(k == 0), stop=(k == 1)),
                            reads=[pTs, wpp], writes=[ppp] if k == 0 else (), pwrites=() if k == 0 else [ppp])
                    S.op("dve", lambda e, blk=blk, ppp=ppp: e.tensor_tensor(
                        out=gate[:, blk * 512:(blk + 1) * 512], in0=ppp[:], in1=gate[:, blk * 512:(blk + 1) * 512], op=ALU.mult),
                        reads=[ppp, gate], pwrites=[gate])
                    S.op("pool", lambda e, blk=blk, s=s, xout=xout: e.tensor_tensor(
                        out=xout[:, blk * 512:(blk + 1) * 512], in0=gate[:, blk * 512:(blk + 1) * 512],
                        in1=xt[:, s, blk * 512:(blk + 1) * 512], op=ALU.add),
                        reads=[gate, xt], writes=[xout] if blk == 0 else (), pwrites=() if blk == 0 else [xout])
                if final:
                    S.op("act", lambda e, xout=xout: e.activation(out=sq[:], in_=xout[:], func=AF.Square, accum_out=ss[:]),
                         reads=[xout], writes=[sq, ss])
                    S.op("act", lambda e: e.activation(out=rs[:], in_=ss[:], func=AF.Sqrt, scale=1.0 / D, bias=EPS),
                         reads=[ss], writes=[rs])
                    S.op("dve", lambda e: e.reciprocal(out=rs[:], in_=rs[:]), reads=[rs], writes=[rs])
                    S.op("dve", lambda e, xout=xout: e.scalar_tensor_tensor(out=xout[:], in0=xout[:], scalar=rs[:, 0:1], in1=gfin[:],
                                                                            op0=ALU.mult, op1=ALU.mult),
                         reads=[xout, rs, gfin], writes=[xout])
                S.dma("sp", x_dst.t[t0 + s * 128:t0 + (s + 1) * 128, :], xout[:], reads=[xout], pwrites=[x_dst], key=xout)
        _barrier(S)
        S.stack_pop()


def phase_D(S, nc, SEQ, lyr, Wd, scr):
    TP = 128
    C = 16
    NCH = TP // C
    GN_EPS = 64e-5
    LD = 0.6065306597126334
    with contextlib.ExitStack() as st:
        S.stack_push(st)
        identB = make_ident(S, "D_identB", BF16)
        ones = S.sb("D_ones", [128, 128])
        S.op("pool", lambda e: e.memset(ones[:], 0.0), writes=[ones])
        S.op("pool", lambda e: e.memset(ones[0:64, 0:64], 1.0), reads=[ones], writes=[ones])
        S.op("pool", lambda e: e.memset(ones[64:128, 64:128], 1.0), reads=[ones], writes=[ones])
        Ff = S.sb("D_F", [128, 64], BF16)
        S.op("pool", lambda e: e.tensor_tensor(out=Ff[:], in0=identB[:, 0:64], in1=identB[:, 64:128], op=ALU.add), reads=[identB], writes=[Ff])
        Sel = S.sb("D_Sel", [128, 16], BF16)
        S.op("pool", lambda e: e.tensor_tensor(out=Sel[:], in0=identB[:, 0:16], in1=identB[:, 16:32], op=ALU.add), reads=[identB], writes=[Sel])
        for hh in range(2, 8):
            S.op("pool", lambda e, hh=hh: e.tensor_tensor(out=Sel[:], in0=Sel[:], in1=identB[:, hh * 16:(hh + 1) * 16], op=ALU.add),
                 reads=[identB, Sel], writes=[Sel])
        maskF = S.sb("D_maskF", [128, 4, 8], BF16)
        S.op("pool", lambda e: e.memset(maskF[:], 0.0), writes=[maskF])
        for p in range(4):
            for h2 in range(2):
                S.op("pool", lambda e, p=p, h2=h2: e.memset(maskF[h2 * 64:(h2 + 1) * 64, p, 2 * p + h2:2 * p + h2 + 1], 1.0), reads=[maskF], writes=[maskF])
        maskZ = S.sb("D_maskZ", [128, 4, 2], BF16)
        S.op("pool", lambda e: e.memset(maskZ[:], 1.0), writes=[maskZ])
        S.op("pool", lambda e: e.affine_select(out=maskZ[:], in_=maskZ[:], pattern=[[-32, 4], [-16, 2]], compare_op=ALU.is_ge, fill=0.0,
                                               base=0, channel_multiplier=1), reads=[maskZ], writes=[maskZ])
        S.op("pool", lambda e: e.affine_select(out=maskZ[:], in_=maskZ[:], pattern=[[32, 4], [16, 2]], compare_op=ALU.is_ge, fill=0.0,
                                               base=15, channel_multiplier=-1), reads=[maskZ], writes=[maskZ])

        def trimask(name, pat, cm, op):
            m = S.sb(name, [128, 128], BF16)
            S.op("pool", lambda e: e.memset(m[:], 1.0), writes=[m])
            S.op("pool", lambda e: e.affine_select(out=m[:], in_=m[:], pattern=pat, compare_op=op, fill=0.0, base=0, channel_multiplier=cm),
                 reads=[m], writes=[m])
            return m
        mSL = trimask("D_mSL", [[-16, 8], [-1, 16]], 1, ALU.is_gt)
        mSU = trimask("D_mSU", [[16, 8], [1, 16]], -1, ALU.is_gt)
        mUI = trimask("D_mUI", [[16, 8], [1, 16]], -1, ALU.is_ge)
        rm = S.sb("D_rm", [128, 512])
        S.op("pool", lambda e: e.memset(rm[:], 1.0), writes=[rm])
        S.op("pool", lambda e: e.memset(rm[:, 0:512:16], 0.0), reads=[rm], writes=[rm])

        def cvec(name, key, n):
            t = S.sb("D_" + name, [128, n])
            S.dma("sp", t[:], Wd[key].t[lyr].rearrange("(c p) -> p c", p=128), reads=[Wd[key]], writes=[t],
                  allow_slow_non_contiguous=True)
            return t

        def cvec2(name, key):
            t = S.sb("D_" + name, [128, 4])
            S.dma("sp", t[:], Wd[key].t[lyr].rearrange("(c a) j -> (a j) c", a=2), reads=[Wd[key]], writes=[t],
                  allow_slow_non_contiguous=True)
            return t
        mu = cvec("mu", "rk_mu", 13)
        w0 = cvec("w0", "rk_w0", 4)
        a0 = cvec("a0", "rk_a0", 4)
        lg = cvec("lg", "rk_lnx_g", 4)
        lb = cvec("lb", "rk_lnx_b", 4)
        kkc = cvec2("kkc", "rk_kk")
        ka = cvec2("ka", "rk_ka")
        rkc = cvec2("rkc", "rk_rk")
        omka = S.sb("D_omka", [128, 4])
        S.op("pool", lambda e: e.tensor_scalar(out=omka[:], in0=ka[:], scalar1=-1.0, scalar2=1.0, op0=ALU.mult, op1=ALU.add),
             reads=[ka], writes=[omka])
        w2 = S.sb("D_w2", [64, 512], BF16)
        a2 = S.sb("D_a2", [128, 512], BF16)
        S.dma("pool", w2[:], Wd["rk_w2"].t[lyr], reads=[Wd["rk_w2"]], writes=[w2])
        S.dma("pool", a2[64:128, :], Wd["rk_a2"].t[lyr], reads=[Wd["rk_a2"]], writes=[a2])

        Hm = S.sb("D_H", [128, 4, 64])
        Hn = S.sb("D_Hn", [128, 4, 64])
        Hbf = S.sb("D_Hbf", [128, 4, 64], BF16)
        S.op("pool", lambda e: e.memset(Hm[:], 0.0), writes=[Hm])
        S.op("pool", lambda e: e.memset(Hbf[:], 0.0), writes=[Hbf])

        rst = S.sb("D_rst", [128, 13, TP + 1])
        xs = S.sb("D_xs", [128, 13, TP])
        th = S.sb("D_th", [128, TP], BF16)
        sg = S.sb("D_sg", [128, 4, TP])
        cum = S.sb("D_cum", [128, 4, TP])
        E1 = S.sb("D_E1", [128, 4, TP])
        E2 = S.sb("D_E2", [128, 4, TP])
        E3 = S.sb("D_E3", [128, 4, TP])
        aa = S.sb("D_aa", [128, 4, TP])
        kkf = S.sb("D_kkf", [128, 4, TP])
        sq = S.sb("D_sq", [128, 4, TP])
        rn = S.sb("D_rn", [128, 4, TP])
        kp = S.sb("D_kp", [128, 4, TP])
        t1 = S.sb("D_t1", [128, 4, TP])
        t2 = S.sb("D_t2", [128, 4, TP])
        comp = [S.sb("D_cmp%d" % i, [128, 4, TP], BF16) for i in range(5)]
        ZXr = Ring([[S.sb("D_Z%d_%d" % (b, i), [128, NCH, 4, 128], BF16) for i in range(5)] for b in range(2)])
        DcR = Ring([S.sb("D_Dc%d" % i, [128, NCH, 4]) for i in range(2)])
        bonR = Ring([S.sb("D_bon%d" % i, [128, 4, TP]) for i in range(2)])
        rzR = Ring([S.sb("D_rz%d" % i, [128, 4, TP], BF16) for i in range(2)])
        ybR = Ring([S.sb("D_yb%d" % i, [128, 4, TP]) for i in range(2)])
        yo = S.sb("D_yo", [128, 4, TP], BF16)
        ppre = S.ps("D_ppre", [128, 4, TP])
        R3 = lambda nm, shp, dt=BF16: Ring([S.sb("D_%s%d" % (nm, i), shp, dt) for i in range(3)])
        WyZr = R3("WyZ", [128, 4, 128]); WhTr = R3("WhT", [128, 4, 128]); BtZr = R3("BtZ", [128, 4, 128]); KtZr = R3("KtZ", [128, 4, 128])
        U0r = R3("U0", [128, 64]); Vtr = R3("Vt", [128, 64]); PTr = R3("PT", [128, 128]); QTr = R3("QT", [128, 128])
        Gr = Ring([S.sb("D_G%d" % i, [128, 128], BF16) for i in range(2)])
        Nr = Ring([S.sb("D_N%d" % i, [128, 128], BF16) for i in range(2)])
        NTr = Ring([S.sb("D_NT%d" % i, [128, 128], BF16) for i in range(2)])
        MTs = S.sb("D_MTs", [128, 128], BF16)
        X1Z = S.sb("D_X1Z", [128, 4, 128], BF16)
        X1s = S.sb("D_X1s", [128, 64], BF16)
        tksR = Ring([S.sb("D_tks%d" % i, [128, 2, 64], BF16) for i in range(2)])
        ysb = S.sb("D_ysb", [128, 64])
        yn = S.sb("D_yn", [128, 64])
        YZ = S.sb("D_YZ", [128, 4, 128], BF16)
        stats = S.sb("D_stats", [128, 6])
        mv = S.sb("D_mv", [128, 2])
        rstd = S.sb("D_rstd", [128, 1])
        b0 = S.ps("D_b0", [128, 512])
        b1 = S.ps("D_b1", [128, 4, 128])
        b2 = S.ps("D_b2", [128, 512])
        b3 = S.ps("D_b3", [128, 3, 128])
        b4 = S.ps("D_b4", [128, 3, 128])
        b5 = ppre
        b6 = S.ps("D_b6", [128, 4, 128])
        b7 = S.ps("D_b7", [128, 512])

        class Reg:
            def __init__(self, bank, ap):
                self.bank = bank
                self.t = ap
        tokc = Reg(b0, b0.t[:, 0:256].rearrange("q (o j) -> q o j", o=4))
        QTp = Reg(b2, b2.t[:, 0:128])
        mvp = Reg(b2, b2.t[:, 128:192])
        Yp = Reg(b2, b2.t[:, 192:256])
        scN = Reg(b1, b1.t[:, 0, :]); scNT = Reg(b1, b1.t[:, 1, :]); scMT = Reg(b1, b1.t[:, 2, :]); scPT = Reg(b1, b1.t[:, 3, :])
        WHp = Reg(b7, b7.t[:, 0:256].rearrange("q (p i) -> q p i", p=4))
        yfp = Reg(b7, b7.t[:, 256:320].rearrange("q (p t) -> q p t", p=4))
        gbank = Ring([b3, b4])

        bc4 = lambda t: t[:].unsqueeze(2).to_broadcast([128, 4, TP])

        def mm(out_ap, obuf, lhsT, lbuf, rhs, rbuf, start, stop=True, first_write=False):
            obuf = getattr(obuf, "bank", obuf)
            S.op("pe", lambda e: e.matmul(out_ap, lhsT=lhsT, rhs=rhs, start=start, stop=stop, skip_group_check=True),
                 reads=[lbuf, rbuf], writes=[obuf] if first_write else (), pwrites=() if first_write else [obuf])

        state = {}

        def prep(nb):
            t0 = nb * TP
            S.dma("sp", rst[:, :, 1:TP + 1], scr["rsT"].t[:, t0:t0 + TP].rearrange("(c p) t -> p c t", p=128), reads=[scr["rsT"]], writes=[rst])
            if nb == 0:
                S.op("pool", lambda e: e.memset(rst[:, :, 0:1], 0.0), reads=[rst], pwrites=[rst])
            else:
                S.dma("sp", rst[:, :, 0:1], scr["rsT"].t[:, t0 - 1:t0].rearrange("(c p) t -> p c t", p=128), reads=[scr["rsT"]],
                      pwrites=[rst], key=rst, allow_slow_non_contiguous=True)
            S.op("pool", lambda e: e.tensor_tensor(out=xs[:], in0=rst[:, :, 0:TP], in1=rst[:, :, 1:TP + 1], op=ALU.subtract), reads=[rst], writes=[xs])
            S.op("pool", lambda e: e.tensor_tensor(out=xs[:], in0=xs[:], in1=mu[:].unsqueeze(2).to_broadcast([128, 13, TP]), op=ALU.mult),
                 reads=[xs, mu], writes=[xs])
            S.op("pool", lambda e: e.tensor_tensor(out=xs[:], in0=xs[:], in1=rst[:, :, 1:TP + 1], op=ALU.add), reads=[xs, rst], writes=[xs])
            r = xs[:, 0:4, :]; k = xs[:, 4:8, :]; v = xs[:, 8:12, :]
            S.op("act", lambda e: e.activation(out=th[0:64, :], in_=xs[0:64, 12, :], func=AF.Tanh), reads=[xs], pwrites=[th])
            S.op("act", lambda e: e.copy(out=th[64:128, :], in_=xs[64:128, 12, :]), reads=[xs], pwrites=[th])
            for p in range(4):
                mm(ppre[:, p, :], ppre, w2[0:64, p * 128:(p + 1) * 128], w2, th[0:64, :], th, True, first_write=(p == 0))
            for p in range(4):
                S.op("act", lambda e, p=p: e.activation(out=sg[:, p, :], in_=ppre[:, p, :], func=AF.Sigmoid, bias=w0[:, p:p + 1]),
                     reads=[w0], writes=[ppre], pwrites=[sg])
            for p in range(4):
                mm(ppre[:, p, :], ppre, a2[64:128, p * 128:(p + 1) * 128], a2, th[64:128, :], th, True, first_write=(p == 0))
            for p in range(4):
                S.op("act", lambda e, p=p: e.activation(out=aa[:, p, :], in_=ppre[:, p, :], func=AF.Sigmoid, bias=a0[:, p:p + 1]),
                     reads=[a0], writes=[ppre], pwrites=[aa])
            S.op("dve", lambda e: e.tensor_tensor_scan(out=cum[:].rearrange("q p t -> q (p t)"), data0=rm[:],
                                                       data1=sg[:].rearrange("q p t -> q (p t)"), initial=0.0, op0=ALU.mult, op1=ALU.add),
                 reads=[rm, sg], writes=[cum])
            S.op("act", lambda e: e.activation(out=E1[:], in_=cum[:], func=AF.Exp, scale=-LD), reads=[cum], writes=[E1])
            S.op("act", lambda e: e.activation(out=E2[:], in_=cum[:], func=AF.Exp, scale=LD), reads=[cum], writes=[E2])
            S.op("pool", lambda e: e.tensor_tensor(out=t2[:], in0=cum[:], in1=sg[:], op=ALU.subtract), reads=[cum, sg], writes=[t2])
            S.op("act", lambda e: e.activation(out=E3[:], in_=t2[:], func=AF.Exp, scale=-LD), reads=[t2], writes=[E3])
            Dc = DcR.next()
            S.op("pool", lambda e, Dc=Dc: e.tensor_copy(out=Dc[:].rearrange("q c p -> q p c"), in_=E1[:, :, 15:TP:16]), reads=[E1], writes=[Dc])
            S.op("pool", lambda e: e.tensor_tensor(out=kkf[:], in0=k, in1=bc4(kkc), op=ALU.mult), reads=[xs, kkc], writes=[kkf])
            S.op("pool", lambda e: e.tensor_tensor(out=sq[:], in0=kkf[:], in1=kkf[:], op=ALU.mult), reads=[kkf], writes=[sq])
            for p in range(4):
                mm(ppre[:, p, :], ppre, ones[:], ones, sq[:, p, :], sq, True, first_write=(p == 0))
            S.op("act", lambda e: e.activation(out=rn[:], in_=ppre[:], func=AF.Sqrt), writes=[rn, ppre])
            S.op("pool", lambda e: e.tensor_scalar(out=rn[:], in0=rn[:], scalar1=1e-12, scalar2=None, op0=ALU.max), reads=[rn], writes=[rn])
            S.op("dve", lambda e: e.reciprocal(out=rn[:], in_=rn[:]), reads=[rn], writes=[rn])
            S.op("pool", lambda e: e.tensor_tensor(out=kkf[:], in0=kkf[:], in1=rn[:], op=ALU.mult), reads=[kkf, rn], writes=[kkf])
            S.op("pool", lambda e: e.tensor_tensor(out=t1[:], in0=aa[:], in1=bc4(ka), op=ALU.mult), reads=[aa, ka], writes=[t1])
            S.op("pool", lambda e: e.tensor_tensor(out=t1[:], in0=t1[:], in1=bc4(omka), op=ALU.add), reads=[t1, omka], writes=[t1])
            S.op("pool", lambda e: e.tensor_tensor(out=kp[:], in0=k, in1=t1[:], op=ALU.mult), reads=[xs, t1], writes=[kp])
            At, Bt, Kt, Rt, Vb = comp
            S.op("pool", lambda e: e.scalar_tensor_tensor(out=At[:], in0=kkf[:], scalar=-1.0, in1=E3[:], op0=ALU.mult, op1=ALU.mult)
                 if False else e.tensor_tensor(out=t2[:], in0=kkf[:], in1=E3[:], op=ALU.mult), reads=[kkf, E3], writes=[t2])
            S.op("pool", lambda e: e.tensor_scalar(out=At[:], in0=t2[:], scalar1=-1.0, scalar2=None, op0=ALU.mult), reads=[t2], writes=[At])
            S.op("pool", lambda e: e.tensor_tensor(out=t2[:], in0=kkf[:], in1=aa[:], op=ALU.mult), reads=[kkf, aa], writes=[t2])
            S.op("pool", lambda e: e.tensor_tensor(out=Bt[:], in0=t2[:], in1=E2[:], op=ALU.mult), reads=[t2, E2], writes=[Bt])
            S.op("pool", lambda e: e.tensor_tensor(out=Kt[:], in0=kp[:], in1=E2[:], op=ALU.mult), reads=[kp, E2], writes=[Kt])
            S.op("pool", lambda e: e.tensor_tensor(out=Rt[:], in0=r, in1=E1[:], op=ALU.mult), reads=[xs, E1], writes=[Rt])
            S.op("pool", lambda e: e.tensor_copy(out=Vb[:], in_=v), reads=[xs], writes=[Vb])
            S.op("pool", lambda e: e.tensor_tensor(out=t1[:], in0=r, in1=kp[:], op=ALU.mult), reads=[xs, kp], writes=[t1])
            S.op("pool", lambda e: e.tensor_tensor(out=sq[:], in0=t1[:], in1=bc4(rkc), op=ALU.mult), reads=[t1, rkc], writes=[sq])
            for p in range(4):
                mm(ppre[:, p, :], ppre, ones[:], ones, sq[:, p, :], sq, True, first_write=(p == 0))
            bon = bonR.next()
            S.op("act", lambda e, bon=bon: e.copy(out=bon[:], in_=ppre[:]), writes=[bon, ppre])
            S.op("pool", lambda e, bon=bon: e.tensor_tensor(out=bon[:], in0=bon[:], in1=v, op=ALU.mult), reads=[bon, xs], writes=[bon])
            rzt = rzR.next()
            S.dma("sp", rzt[:], scr["rzT"].t[:, t0:t0 + TP].rearrange("(c p) t -> p c t", p=128), reads=[scr["rzT"]], writes=[rzt])
            ZX = ZXr.next()
            for oi in range(5):
                for p in range(4):
                    S.op("pool", lambda e, oi=oi, p=p, ZX=ZX: e.tensor_tensor(
                        out=ZX[oi][:, :, p, :].rearrange("q c (h t) -> q c h t", t=16),
                        in0=comp[oi][:, p, :].rearrange("q (c t) -> q c t", t=16).unsqueeze(2).to_broadcast([128, NCH, 8, 16]),
                        in1=maskF[:, p, :].unsqueeze(1).unsqueeze(3).to_broadcast([128, NCH, 8, 16]), op=ALU.mult),
                        reads=[comp[oi], maskF], writes=[ZX[oi]] if p == 0 else (), pwrites=() if p == 0 else [ZX[oi]])
            return dict(ZX=ZX, Dc=Dc, bon=bon, rzt=rzt, yb=ybR.next(), t0=t0)

        def pre(bt, c):
            ZA, ZB, ZK, ZR, ZV = bt["ZX"]
            BtZ = BtZr.next(); KtZ = KtZr.next(); U0 = U0r.next(); Vt = Vtr.next(); PTs = PTr.next(); QTs = QTr.next()
            WyZ = WyZr.next(); WhT = WhTr.next()
            first = True
            for oi, Z in enumerate((ZA, ZB, ZK, ZV)):
                for p in range(4):
                    mm(tokc.t[:, oi, :], tokc, Z[:, c, p, :], Z, Ff[:], Ff, first, first_write=first)
                    first = False
            if DEBUG.get('pre_sub', 9) <= 0:
                return None
            G0 = Gr.next()
            S.op("act", lambda e: e.copy(out=G0[:, 0:64], in_=tokc.t[:, 0, :]), writes=[G0, tokc.bank])
            if DEBUG.get('pre_sub', 9) <= 1:
                return None
            mz = maskZ[:].unsqueeze(3).to_broadcast([128, 4, 2, 64])
            tks = tksR.next()
            S.op("dve", lambda e, tks=tks: e.tensor_copy(out=tks[:], in_=tokc.t[:, 1:3, :]), writes=[tks, tokc.bank])
            if DEBUG.get('pre_sub', 9) <= 2:
                return None
            S.op("pool", lambda e, tks=tks: e.tensor_tensor(out=BtZ[:].rearrange("q p (a j) -> q p a j", a=2),
                                                  in0=tks[:, 0, :].unsqueeze(1).unsqueeze(1).to_broadcast([128, 4, 2, 64]), in1=mz, op=ALU.mult),
                 reads=[tks, maskZ], writes=[BtZ])
            S.op("pool", lambda e, tks=tks: e.tensor_tensor(out=KtZ[:].rearrange("q p (a j) -> q p a j", a=2),
                                                  in0=tks[:, 1, :].unsqueeze(1).unsqueeze(1).to_broadcast([128, 4, 2, 64]), in1=mz, op=ALU.mult),
                 reads=[tks, maskZ], writes=[KtZ])
            S.op("act", lambda e: e.copy(out=Vt[:], in_=tokc.t[:, 3, :]), writes=[Vt, tokc.bank])
            if DEBUG.get('pre_stop', 9) <= 1:
                return None
            N1 = Nr.next(); NT1 = NTr.next()
            for (pb, L, R_, msk, dst, eng) in ((scN, ZA, ZB, mSL, N1, "dve"), (scNT, ZB, ZA, mSU, NT1, "dve"), (scMT, ZK, ZA, mSU, MTs, "dve"),
                                              (scPT, ZB, ZR, mUI, PTs, "dve"), (QTp, ZK, ZR, mUI, QTs, "dve")):
                for p in range(4):
                    mm(pb.t, pb, L[:, c, p, :], L, R_[:, c, p, :], R_, p == 0, first_write=(p == 0))
                S.op(eng, lambda e, pb=pb, msk=msk, dst=dst: e.tensor_tensor(out=dst[:], in0=pb.t, in1=msk[:], op=ALU.mult),
                     reads=[msk], writes=[dst, pb.bank])
            if DEBUG.get('pre_stop', 9) <= 2:
                return None
            mm(mvp.t, mvp, MTs[:], MTs, Vt[:], Vt, True, first_write=True)
            S.op("act", lambda e: e.copy(out=G0[:, 64:128], in_=mvp.t), writes=[mvp.bank], pwrites=[G0])
            if DEBUG.get('pre_stop', 9) <= 3:
                return None
            G = G0; Nk = N1; NTk = NT1
            for lev in range(4):
                gb = gbank.next()
                mm(gb[:, 0, :], gb, identB[:], identB, G[:], G, True, stop=False, first_write=True)
                mm(gb[:, 0, :], gb, NTk[:], NTk, G[:], G, False)
                if lev < 3:
                    mm(gb[:, 1, :], gb, NTk[:], NTk, Nk[:], Nk, True)
                    mm(gb[:, 2, :], gb, Nk[:], Nk, NTk[:], NTk, True)
                    G2 = Gr.next(); N2 = Nr.next(); NT2 = NTr.next()
                    S.op("act", lambda e, gb=gb, G2=G2: e.copy(out=G2[:], in_=gb[:, 0, :]), writes=[G2, gb])
                    S.op("dve", lambda e, gb=gb, N2=N2: e.tensor_copy(out=N2[:], in_=gb[:, 1, :]), writes=[N2, gb])
                    S.op("act", lambda e, gb=gb, NT2=NT2: e.copy(out=NT2[:], in_=gb[:, 2, :]), writes=[NT2, gb])
                    G = G2; Nk = N2; NTk = NT2
                else:
                    S.op("dve", lambda e, gb=gb: e.tensor_copy(out=X1s[:], in_=gb[:, 0, 0:64]), writes=[X1s, gb])
                    S.op("pool", lambda e: e.tensor_tensor(out=X1Z[:].rearrange("q p (a j) -> q p a j", a=2),
                                                           in0=X1s[:].unsqueeze(1).unsqueeze(1).to_broadcast([128, 4, 2, 64]), in1=mz, op=ALU.mult),
                         reads=[X1s, maskZ], writes=[X1Z])
                    S.op("act", lambda e, gb=gb, U0=U0: e.copy(out=U0[:], in_=gb[:, 0, 64:128]), writes=[U0, gb])
            if DEBUG.get('pre_stop', 9) <= 4:
                return None
            for p in range(4):
                mm(b5[:, p, :], b5, identB[:], identB, ZR[:, c, p, :], ZR, p == 0, stop=False, first_write=(p == 0))
                mm(b5[:, p, :], b5, X1Z[:, p, :], X1Z, PTs[:], PTs, False)
            S.op("act", lambda e, WyZ=WyZ: e.copy(out=WyZ[:], in_=b5[:]), writes=[WyZ, b5])
            if DEBUG.get('pre_stop', 9) <= 5:
                return None
            for p in range(4):
                mm(b6[:, p, :], b6, X1Z[:, p, :], X1Z, BtZ[:, p, :], BtZ, p == 0, first_write=(p == 0))
            S.op("dve", lambda e, WhT=WhT: e.tensor_copy(out=WhT[:], in_=b6[:]), writes=[WhT, b6])
            return dict(BtZ=BtZ, KtZ=KtZ, U0=U0, Vt=Vt, PTs=PTs, QTs=QTs, WyZ=WyZ, WhT=WhT, c=c, bt=bt)

        def seq(pc):
            c = pc["c"]; bt = pc["bt"]
            BtZ, KtZ, U0, Vt, PTs, QTs, WyZ, WhT = (pc[k] for k in ("BtZ", "KtZ", "U0", "Vt", "PTs", "QTs", "WyZ", "WhT"))
            for p in range(4):
                mm(WHp.t[:, p, :], WHp, BtZ[:, p, :], BtZ, U0[:], U0, p == 0, stop=False, first_write=(p == 0))
            for p in range(4):
                mm(WHp.t[:, p, :], WHp, KtZ[:, p, :], KtZ, Vt[:], Vt, False, stop=False)
            mm(Yp.t, Yp, PTs[:], PTs, U0[:], U0, True, stop=False, first_write=True)
            mm(Yp.t, Yp, QTs[:], QTs, Vt[:], Vt, False, stop=False)
            for p in range(4):
                mm(Yp.t, Yp, WyZ[:, p, :], WyZ, Hbf[:, p, :], Hbf, False, stop=(p == 3))
            for p in range(4):
                mm(WHp.t[:, p, :], WHp, WhT[:, p, :], WhT, Hbf[:, p, :], Hbf, False, stop=True)
            Dc = bt["Dc"]
            S.op("dve", lambda e: e.tensor_tensor(out=Hn[:], in0=WHp.t, in1=Hm[:], op=ALU.add), reads=[Hm], writes=[Hn, WHp.bank])
            S.op("dve", lambda e, Dc=Dc, c=c: e.tensor_tensor(out=Hm[:], in0=Hn[:], in1=Dc[:, c, :].unsqueeze(2).to_broadcast([128, 4, 64]), op=ALU.mult),
                 reads=[Hn, Dc], writes=[Hm])
            S.op("act", lambda e: e.copy(out=Hbf[:], in_=Hm[:]), reads=[Hm], writes=[Hbf])
            S.op("act", lambda e: e.copy(out=ysb[:], in_=Yp.t), writes=[ysb, Yp.bank])
            S.op("dve", lambda e: e.bn_stats(out=stats[:], in_=ysb[:]), reads=[ysb], writes=[stats])
            S.op("dve", lambda e: e.bn_aggr(out=mv[:], in_=stats[:]), reads=[stats], writes=[mv])
            S.op("act", lambda e: e.activation(out=rstd[:], in_=mv[:, 1:2], func=AF.Sqrt, bias=GN_EPS, scale=1.0), reads=[mv], writes=[rstd])
            S.op("dve", lambda e: e.reciprocal(out=rstd[:], in_=rstd[:]), reads=[rstd], writes=[rstd])
            S.op("dve", lambda e: e.tensor_scalar(out=yn[:], in0=ysb[:], scalar1=mv[:, 0:1], scalar2=rstd[:, 0:1], op0=ALU.subtract, op1=ALU.mult),
                 reads=[ysb, mv, rstd], writes=[yn])
            S.op("pool", lambda e: e.tensor_tensor(out=YZ[:].rearrange("q p (a j) -> q p a j", a=2),
                                                   in0=yn[:].unsqueeze(1).unsqueeze(1).to_broadcast([128, 4, 2, 64]),
                                                   in1=maskZ[:].unsqueeze(3).to_broadcast([128, 4, 2, 64]), op=ALU.mult),
                 reads=[yn, maskZ], writes=[YZ])
            for p in range(4):
                mm(yfp.t[:, p, :], yfp, YZ[:, p, :], YZ, Sel[:], Sel, True, first_write=(p == 0))
            yb = bt["yb"]
            S.op("act", lambda e, yb=yb, c=c: e.copy(out=yb[:, :, c * 16:(c + 1) * 16], in_=yfp.t), writes=([yb] if c == 0 else []) + [yfp.bank],
                 pwrites=() if c == 0 else [yb])
            if c == NCH - 1:
                post(bt)

        def post(bt):
            yb = bt["yb"]; bon = bt["bon"]; rzt = bt["rzt"]; t0 = bt["t0"]
            S.op("pool", lambda e: e.tensor_tensor(out=yb[:], in0=yb[:], in1=bc4(lg), op=ALU.mult), reads=[yb, lg], writes=[yb])
            S.op("pool", lambda e: e.tensor_tensor(out=yb[:], in0=yb[:], in1=bc4(lb), op=ALU.add), reads=[yb, lb], writes=[yb])
            S.op("pool", lambda e: e.tensor_tensor(out=yb[:], in0=yb[:], in1=bon[:], op=ALU.add), reads=[yb, bon], writes=[yb])
            S.op("pool", lambda e: e.tensor_tensor(out=yo[:], in0=yb[:], in1=rzt[:], op=ALU.mult), reads=[yb, rzt], writes=[yo])
            S.dma("sp", scr["ysT"].t[2, :, t0:t0 + TP].rearrange("(c p) t -> p c t", p=128), yo[:], reads=[yo], pwrites=[scr["ysT"]], key=yo)

        pending = None
        stage = DEBUG.get("d_stage", 3)
        for nb in range(SEQ // TP):
            bt = prep(nb)
            if stage < 1:
                continue
            for c in range(NCH):
                pc = pre(bt, c)
                if stage < 2:
                    continue
                if pending is not None:
                    seq(pending)
                pending = pc
        if stage >= 2:
            seq(pending)
        _barrier(S)
        S.stack_pop()


WSPEC = {
    "norm_g": [2, 1024], "w_in": [2, 1024, 9112], "cmp_w1": [2, 2, 32, 64, 128], "cmp_w2": [2, 2, 128, 64],
    "cmp_pe": [2, 2, 32, 64], "sg_ln_g": [2, 512], "sg_ln_b": [2, 512], "sg_w": [2, 8, 128, 128], "sg_b": [2, 8, 128],
    "rk_mu": [2, 1664], "rk_w0": [2, 512], "rk_w2": [2, 64, 512], "rk_a0": [2, 512], "rk_a2": [2, 64, 512],
    "rk_kk": [2, 8, 64], "rk_ka": [2, 8, 64], "rk_rk": [2, 8, 64], "rk_lnx_g": [2, 512], "rk_lnx_b": [2, 512],
    "w_branch": [2, 3, 512, 1024], "w_o": [2, 1024, 1024], "ple_norm_g": [2, 1024], "w_ple_gate": [2, 1024, 1024],
    "w_ple_proj": [2, 256, 1024], "final_norm_g": [1, 1024],
}


def build(SEQ, nlayers=2, enable=(1, 1, 1), scr_kind="Internal"):
    nc = bass.Bass("TRN2", target_bir_lowering=False)
    with contextlib.ExitStack() as stack:
        S = Sched(nc, stack)
        x = Buf("x", nc.dram_tensor("x", [SEQ, D], F32, kind="ExternalInput").ap())
        Wd = {"p": Buf("p", nc.dram_tensor("p", [2, SEQ, PLE], F32, kind="ExternalInput").ap())}
        for k, shp in WSPEC.items():
            Wd[k] = Buf(k, nc.dram_tensor(k, shp, F32, kind="ExternalInput").ap())
        out = Buf("out", nc.dram_tensor("out", [SEQ, D], F32, kind="ExternalOutput").ap())
        scr = make_scratch(S, SEQ, kind=scr_kind)
        xmid = S.dram("xmid", [SEQ, D], F32, kind=scr_kind)
        cur = x
        for lyr in range(nlayers):
            last = lyr == nlayers - 1
            dst = out if last else xmid
            phase_A(S, nc, SEQ, lyr, cur, Wd, scr)
            if enable[0]:
                phase_B(S, nc, SEQ, lyr, Wd, scr)
            if enable[1]:
                phase_C(S, nc, SEQ, lyr, Wd, scr)
            if enable[2]:
                phase_D(S, nc, SEQ, lyr, Wd, scr)
            phase_E(S, nc, SEQ, lyr, cur, dst, Wd, scr, final=(last and nlayers == 2))
            cur = dst
        S.emit()
    return nc


def phase_B(S, nc, SEQ, lyr, Wd, scr):
    NC = (SEQ - 32) // 16 + 1
    NT = (NC + 127) // 128
    NCp = NT * 128
    KT = SEQ // 128
    with contextlib.ExitStack() as st:
        S.stack_push(st)
        ident = make_ident(S, "B_ident")
        ksT = S.sb("B_ksT", [64, 2, SEQ], BF16)
        kwT = S.sb("B_kwT", [64, 2, SEQ], BF16)
        vs = S.sb("B_vs", [128, KT, 2, 65], BF16)
        vw = S.sb("B_vw", [128, KT, 2, 65], BF16)
        kcmpT = S.sb("B_kcmpT", [64, 2, NCp], BF16)
        Rc = S.sb("B_Rc", [128, NT, 2, 193], BF16)
        EXW = S.sb("B_EXW", [128, SEQ], BF16)
        S.dma("sp", ksT[:], scr["ksT"].t.rearrange("(g d) t -> d g t", g=2), reads=[scr["ksT"]], writes=[ksT])
        S.dma("sp", kwT[:], scr["kwT"].t.rearrange("(g d) t -> d g t", g=2), reads=[scr["kwT"]], writes=[kwT])
        S.op("pool", lambda e: e.memset(vs[:], 1.0), writes=[vs])
        S.op("pool", lambda e: e.memset(vw[:], 1.0), writes=[vw])
        for k0 in range(0, KT, 8):
            k1 = min(KT, k0 + 8)
            for (dst, c0) in ((vs, 0), (vw, 128)):
                for g in range(2):
                    S.dma("sp", dst[:, k0:k1, g, 0:64],
                          scr["vsw"].t[k0 * 128:k1 * 128, c0 + g * 64:c0 + (g + 1) * 64].rearrange("(k p) d -> p k d", p=128),
                          reads=[scr["vsw"]], pwrites=[dst], key=dst)
        S.op("pool", lambda e: e.memset(EXW[:], 1.0), writes=[EXW])
        S.op("pool", lambda e: e.affine_select(out=EXW[:], in_=EXW[:], pattern=[[1, SEQ]], compare_op=ALU.is_ge, fill=0.0,
                                               base=0, channel_multiplier=-64), reads=[EXW], writes=[EXW])
        S.op("pool", lambda e: e.affine_select(out=EXW[:], in_=EXW[:], pattern=[[-1, SEQ]], compare_op=ALU.is_ge, fill=0.0,
                                               base=63, channel_multiplier=64), reads=[EXW], writes=[EXW])
        S.op("pool", lambda e: e.memset(Rc[:], 1.0), writes=[Rc])
        for nt in range(NT):
            for g in range(2):
                S.op("pool", lambda e, nt=nt, g=g: e.affine_select(
                    out=Rc[:, nt, g, 65:193], in_=Rc[:, nt, g, 65:193], pattern=[[-4, 128]], compare_op=ALU.is_ge, fill=0.0,
                    base=nt * 128 + 1, channel_multiplier=1), reads=[Rc], writes=[Rc])
                S.op("pool", lambda e, nt=nt, g=g: e.affine_select(
                    out=Rc[:, nt, g, 65:193], in_=Rc[:, nt, g, 65:193], pattern=[[4, 128]], compare_op=ALU.is_ge, fill=0.0,
                    base=3 - nt * 128, channel_multiplier=-1), reads=[Rc], writes=[Rc])
        npad = NCp - NC
        if npad:
            S.op("pool", lambda e: e.affine_select(
                out=Rc[:, NT - 1, :, :], in_=Rc[:, NT - 1, :, :], pattern=[[0, 2 * 193]], compare_op=ALU.is_ge, fill=0.0,
                base=(NC - 1) - (NT - 1) * 128, channel_multiplier=-1), reads=[Rc], writes=[Rc])
        S.op("pool", lambda e: e.memset(kcmpT[:], 0.0), writes=[kcmpT])

        with contextlib.ExitStack() as st2:
            S.stack_push(st2)
            kvT = S.sb("B_kvT", [64, 2, SEQ], BF16)
            w1 = S.sb("B_w1", [64, 32, 128], BF16)
            w2 = S.sb("B_w2", [128, 64], BF16)
            peT = S.sb("B_peT", [64, 32])
            peTb = S.sb("B_peTb", [64, 32], BF16)
            cb = S.sb("B_cb", [128, 1])
            hid = S.sb("B_hid", [128, NCp], BF16)
            ph = S.ps("B_ph", [128, 512])
            pc1 = S.ps("B_pc1", [128, 512])
            pk = S.ps("B_pk", [128, 512])
            for kv in range(2):
                src = scr["kcT"] if kv == 0 else scr["vcT"]
                S.dma("sp", kvT[:], src.t.rearrange("(g d) t -> d g t", g=2), reads=[src], writes=[kvT])
                S.dma("pool", w1[:], Wd["cmp_w1"].t[lyr, kv].rearrange("l d h -> d l h"), reads=[Wd["cmp_w1"]], writes=[w1])
                S.dma("pool", w2[:], Wd["cmp_w2"].t[lyr, kv], reads=[Wd["cmp_w2"]], writes=[w2])
                S.dma("sp", peT[:], Wd["cmp_pe"].t[lyr, kv].rearrange("l d -> d l"), reads=[Wd["cmp_pe"]], writes=[peT],
                      allow_slow_non_contiguous=True)
                S.op("dve", lambda e: e.tensor_copy(out=peTb[:], in_=peT[:]), reads=[peT], writes=[peTb])
                for l in range(32):
                    S.op("pe", lambda e, l=l: e.matmul(pc1[:, 0:1], lhsT=w1[:, l, :], rhs=peTb[:, l:l + 1], start=(l == 0), stop=(l == 31)),
                         reads=[w1, peTb], writes=[pc1] if l == 0 else (), pwrites=() if l == 0 else [pc1])
                S.op("dve", lambda e: e.tensor_copy(out=cb[:], in_=pc1[:, 0:1]), reads=[pc1], writes=[cb])
                for g in range(2):
                    S.op("dve", lambda e: e.memset(hid[:], 0.0), writes=[hid])
                    for n0 in range(0, NC, 512):
                        nn = min(512, NC - n0)
                        for l in range(32):
                            S.op("pe", lambda e, l=l, g=g, n0=n0, nn=nn: e.matmul(
                                ph[:, 0:nn], lhsT=w1[:, l, :], rhs=kvT[:, g, n0 * 16 + l: n0 * 16 + l + (nn - 1) * 16 + 1: 16], start=(l == 0), stop=(l == 31)),
                                reads=[w1, kvT], writes=[ph] if l == 0 else (), pwrites=() if l == 0 else [ph])
                        S.op("act", lambda e, n0=n0, nn=nn: e.activation(out=hid[:, n0:n0 + nn], in_=ph[:, 0:nn], func=AF.Silu, bias=cb[:, 0:1]),
                             reads=[ph, cb], pwrites=[hid])
                    if kv == 0:
                        for n0 in range(0, NC, 512):
                            nn = min(512, NC - n0)
                            S.op("pe", lambda e, n0=n0, nn=nn: e.matmul(pk[0:64, 0:nn], lhsT=w2[:], rhs=hid[:, n0:n0 + nn], start=True, stop=True),
                                 reads=[w2, hid], writes=[pk])
                            S.op("dve", lambda e, g=g, n0=n0, nn=nn: e.tensor_copy(out=kcmpT[:, g, n0:n0 + nn], in_=pk[0:64, 0:nn]),
                                 reads=[pk], pwrites=[kcmpT])
                    else:
                        for nt in range(NT):
                            rows = min(128, NC - nt * 128)
                            S.op("pe", lambda e, nt=nt: e.matmul(pk[:, 0:64], lhsT=hid[:, nt * 128:(nt + 1) * 128], rhs=w2[:], start=True, stop=True),
                                 reads=[w2, hid], writes=[pk])
                            S.op("dve", lambda e, g=g, nt=nt: e.tensor_copy(out=Rc[:, nt, g, 0:64], in_=pk[:, 0:64]),
                                 reads=[pk], pwrites=[Rc])
            _barrier(S)
            S.stack_pop()

        qt = Ring([S.sb("B_q%d" % i, [64, 8, 128], BF16) for i in range(2)])
        gt = Ring([S.sb("B_g%d" % i, [128, 24]) for i in range(2)])
        nzt = Ring([S.sb("B_nz%d" % i, [128, 512], BF16) for i in range(2)])
        Et = Ring([S.sb("B_E%d" % i, [128, 512], BF16) for i in range(4)])
        psT = Ring([S.ps("B_psT%d" % i, [128, 512]) for i in range(3)])
        pcA = S.ps("B_pcA", [128, 2, 193])
        pcB = S.ps("B_pcB", [128, 2, 193])
        pos = S.ps("B_pos", [128, 4, 65])
        pow_ = S.ps("B_pow", [128, 4, 65])
        pmisc = S.ps("B_pmisc", [128, 4, 128], BF16)
        oc = S.sb("B_oc", [128, 4, 193])
        rcs = S.sb("B_rcs", [128, 4])
        rss = S.sb("B_rss", [128, 4])
        rws = S.sb("B_rws", [128, 4])
        cc = S.sb("B_cc", [128, 3, 4])
        sc = S.sb("B_sc", [128, 128])
        sc2 = S.sb("B_sc2", [128, 128])
        m1 = S.sb("B_m1", [128, 8])
        m2 = S.sb("B_m2", [128, 8])
        negq = S.sb("B_negq", [128, 128], BF16)
        negT4 = S.sb("B_negT4", [128, 4, 128], BF16)
        yg = S.sb("B_yg", [128, 4, 64])
        ytmp = S.sb("B_ytmp", [128, 4, 64])
        ynsa = S.sb("B_ynsa", [128, 512], BF16)
        stg = Ring([S.sb("B_stg%d" % i, [128, 4, 128], BF16) for i in range(2)])

        def qk_exp(kT_ap, kbuf, q_ap, qbuf, neg_lhsT=None):
            p = psT.next()
            if neg_lhsT is not None:
                S.op("pe", lambda e, p=p: e.matmul(p[:], lhsT=neg_lhsT, rhs=negT4[:].rearrange("p h q -> p (h q)"), start=True, stop=False),
                     reads=[EXW, negT4], writes=[p])
                S.op("pe", lambda e, p=p: e.matmul(p[:], lhsT=kT_ap, rhs=q_ap, start=False, stop=True), reads=[kbuf, qbuf], pwrites=[p])
            else:
                S.op("pe", lambda e, p=p: e.matmul(p[:], lhsT=kT_ap, rhs=q_ap, start=True, stop=True), reads=[kbuf, qbuf], writes=[p])
            E = Et.next()
            S.op("act", lambda e, p=p, E=E: e.activation(out=E[:], in_=p[:], func=AF.Exp), reads=[p], writes=[E])
            return E

        def mask(E, base, cm, qstep):
            S.op("pool", lambda e, E=E: e.affine_select(out=E[:], in_=E[:], pattern=[[0, 4], [qstep, 128]], compare_op=ALU.is_ge,
                                                       fill=0.0, base=base, channel_multiplier=cm), reads=[E], writes=[E])

        for qb in range(SEQ // 128):
            q0 = qb * 128
            q = qt.next(); gg = gt.next(); nz = nzt.next()
            S.dma("sp", q[:], scr["qT"].t[:, q0:q0 + 128].rearrange("(h d) t -> d h t", h=8), reads=[scr["qT"]], writes=[q])
            S.dma("sp", gg[:], scr["gate"].t[q0:q0 + 128, :], reads=[scr["gate"]], writes=[gg])
            S.dma("sp", nz[:], scr["nzs"].t[q0:q0 + 128, :], reads=[scr["nzs"]], writes=[nz])
            for g in range(2):
                q_ap = q[:, 4 * g:4 * g + 4, :].rearrange("d h q -> d (h q)")
                n_max = min(8 * qb + 6, NC - 1)
                ntl = n_max // 128 + 1
                for nt in range(ntl):
                    E = qk_exp(kcmpT[:, g, nt * 128:(nt + 1) * 128], kcmpT, q_ap, q)
                    if q0 - 16 * (128 * nt + 127) - 31 < 0:
                        mask(E, q0 - 16 * 128 * nt - 31, -16, 1)
                    for h in range(4):
                        pcx = pcA if h < 2 else pcB
                        first = (nt == 0 and h % 2 == 0)
                        S.op("pe", lambda e, E=E, h=h, pcx=pcx, nt=nt, first=first, g=g, ntl=ntl: e.matmul(
                            pcx[:, h % 2, :], lhsT=E[:, h * 128:(h + 1) * 128], rhs=Rc[:, nt, g, :], start=first,
                            stop=(nt == ntl - 1 and h % 2 == 1), skip_group_check=True),
                            reads=[E, Rc], writes=[pcx] if first else (), pwrites=() if first else [pcx])
                S.op("act", lambda e: e.copy(out=oc[:, 0:2, :], in_=pcA[:]), reads=[pcA], pwrites=[oc])
                S.op("act", lambda e: e.copy(out=oc[:, 2:4, :], in_=pcB[:]), reads=[pcB], pwrites=[oc])
                S.op("dve", lambda e: e.tensor_scalar(out=rcs[:], in0=oc[:, :, 64], scalar1=1e-30, scalar2=None, op0=ALU.max),
                     reads=[oc], writes=[rcs])
                S.op("dve", lambda e: e.reciprocal(out=rcs[:], in_=rcs[:]), reads=[rcs], writes=[rcs])
                S.op("dve", lambda e: e.tensor_scalar(out=sc[:], in0=oc[:, 0, 65:193], scalar1=rcs[:, 0:1], scalar2=None, op0=ALU.mult),
                     reads=[oc, rcs], writes=[sc])
                for h in range(1, 4):
                    S.op("dve", lambda e, h=h: e.scalar_tensor_tensor(out=sc[:], in0=oc[:, h, 65:193], scalar=rcs[:, h:h + 1], in1=sc[:],
                                                                      op0=ALU.mult, op1=ALU.add), reads=[oc, rcs, sc], writes=[sc])
                for half in range(2):
                    tb = 2 * qb + half
                    ps_ = slice(half * 64, (half + 1) * 64)
                    if tb + 1 < 128:
                        S.op("dve", lambda e, ps_=ps_, tb=tb: e.memset(sc[ps_, tb + 1:128], -1e4), reads=[sc], writes=[sc])
                    lo = max(tb - 1, 0)
                    S.op("dve", lambda e, ps_=ps_, tb=tb, lo=lo: e.memset(sc[ps_, lo:tb + 1], 1e4), reads=[sc], writes=[sc])
                S.op("dve", lambda e: e.memset(sc[:, 0:1], 1e4), reads=[sc], writes=[sc])
                S.op("dve", lambda e: e.max(out=m1[:], in_=sc[:]), reads=[sc], writes=[m1])
                S.op("dve", lambda e: e.match_replace(out=sc2[:], in_to_replace=m1[:], in_values=sc[:], imm_value=-3e4),
                     reads=[sc, m1], writes=[sc2])
                S.op("dve", lambda e: e.max(out=m2[:], in_=sc2[:]), reads=[sc2], writes=[m2])
                S.op("dve", lambda e: e.tensor_scalar(out=negq[:], in0=sc[:], scalar1=m2[:, 7:8], scalar2=-1e4, op0=ALU.is_lt, op1=ALU.mult),
                     reads=[sc, m2], writes=[negq])
                S.op("pe", lambda e: e.transpose(out=pmisc[:, 0, :], in_=negq[:], identity=ident[:]), reads=[negq, ident], writes=[pmisc])
                S.op("dve", lambda e: e.tensor_copy(out=negT4[:], in_=pmisc[:, 0:1, :].to_broadcast([128, 4, 128])),
                     reads=[pmisc], writes=[negT4])
                kts = list(range(max(0, qb - 4), qb + 1))
                for i, kt in enumerate(kts):
                    E = qk_exp(kwT[:, g, kt * 128:(kt + 1) * 128], kwT, q_ap, q)
                    if kt == qb - 4:
                        mask(E, -1, 1, -1)
                    if kt == qb:
                        mask(E, 0, -1, 1)
                    for h in range(4):
                        first = (i == 0 and h == 0)
                        S.op("pe", lambda e, E=E, h=h, kt=kt, first=first, last=(i == len(kts) - 1 and h == 3), g=g: e.matmul(
                            pow_[:, h, :], lhsT=E[:, h * 128:(h + 1) * 128], rhs=vw[:, kt, g, :], start=first, stop=last,
                            skip_group_check=True),
                            reads=[E, vw], writes=[pow_] if first else (), pwrites=() if first else [pow_])
                for kt in range(qb + 1):
                    E = qk_exp(ksT[:, g, kt * 128:(kt + 1) * 128], ksT, q_ap, q, neg_lhsT=EXW[:, kt * 128:(kt + 1) * 128])
                    if kt == qb:
                        mask(E, 0, -1, 1)
                    for h in range(4):
                        first = (kt == 0 and h == 0)
                        S.op("pe", lambda e, E=E, h=h, kt=kt, first=first, last=(kt == qb and h == 3), g=g: e.matmul(
                            pos[:, h, :], lhsT=E[:, h * 128:(h + 1) * 128], rhs=vs[:, kt, g, :], start=first, stop=last,
                            skip_group_check=True),
                            reads=[E, vs], writes=[pos] if first else (), pwrites=() if first else [pos])
                S.op("dve", lambda e: e.reciprocal(out=rss[:], in_=pos[:, :, 64]), reads=[pos], writes=[rss])
                S.op("dve", lambda e: e.reciprocal(out=rws[:], in_=pow_[:, :, 64]), reads=[pow_], writes=[rws])
                gv = gg[:, g * 12:(g + 1) * 12].rearrange("p (h b) -> p b h", b=3)
                for b, rr in enumerate((rcs, rss, rws)):
                    S.op("dve", lambda e, b=b, rr=rr, gv=gv: e.tensor_tensor(out=cc[:, b, :], in0=gv[:, b, :], in1=rr[:], op=ALU.mult),
                         reads=[gg, rr], pwrites=[cc])
                if qb == 2:
                    dbg_dump(S, "oc%d" % g, oc[:], oc, [128, 4, 193])
                    dbg_dump(S, "pos%d" % g, pos[:], pos, [128, 4, 65])
                    dbg_dump(S, "pow%d" % g, pow_[:], pow_, [128, 4, 65])
                    dbg_dump(S, "cc%d" % g, cc[:], cc, [128, 3, 4])
                    dbg_dump(S, "gg%d" % g, gg[:], gg, [128, 24])
                    dbg_dump(S, "sc%d" % g, sc[:], sc, [128, 128])
                    dbg_dump(S, "negq%d" % g, negq[:], negq, [128, 128])
                bc = lambda b: cc[:, b, :].unsqueeze(2).to_broadcast([128, 4, 64])
                S.op("dve", lambda e: e.tensor_tensor(out=yg[:], in0=oc[:, :, 0:64], in1=bc(0), op=ALU.mult), reads=[oc, cc], writes=[yg])
                S.op("dve", lambda e: e.tensor_tensor(out=ytmp[:], in0=pos[:, :, 0:64], in1=bc(1), op=ALU.mult), reads=[pos, cc], writes=[ytmp])
                S.op("pool", lambda e: e.tensor_tensor(out=yg[:], in0=yg[:], in1=ytmp[:], op=ALU.add), reads=[yg, ytmp], writes=[yg])
                S.op("dve", lambda e: e.tensor_tensor(out=ytmp[:], in0=pow_[:, :, 0:64], in1=bc(2), op=ALU.mult), reads=[pow_, cc], writes=[ytmp])
                S.op("pool", lambda e: e.tensor_tensor(out=yg[:], in0=yg[:], in1=ytmp[:], op=ALU.add), reads=[yg, ytmp], writes=[yg])
                if qb == 2:
                    dbg_dump(S, "yg%d" % g, yg[:], yg, [128, 4, 64])
                S.op("pool", lambda e, g=g, nz=nz: e.tensor_tensor(out=ynsa[:, g * 256:(g + 1) * 256], in0=yg[:].rearrange("p h d -> p (h d)"),
                                                                   in1=nz[:, g * 256:(g + 1) * 256], op=ALU.mult),
                     reads=[yg, nz], pwrites=[ynsa])
            for k in range(4):
                S.op("pe", lambda e, k=k: e.transpose(out=pmisc[:, k, :], in_=ynsa[:, k * 128:(k + 1) * 128], identity=ident[:]),
                     reads=[ynsa, ident], writes=[pmisc] if k == 0 else (), pwrites=() if k == 0 else [pmisc])
            sg = stg.next()
            S.op("act", lambda e, sg=sg: e.copy(out=sg[:], in_=pmisc[:]), reads=[pmisc], writes=[sg])
            S.dma("sp", scr["ysT"].t[0, :, q0:q0 + 128].rearrange("(k p) t -> p k t", p=128), sg[:], reads=[sg], pwrites=[scr["ysT"]], key=sg)
        _barrier(S)
        S.stack_pop()


def phase_D_seq(S, nc, SEQ, lyr, Wd, scr):
    TP = 128
    TB = 8
    GN_EPS = 64e-5
    xtok = scr["xtok"]
    with contextlib.ExitStack() as st:
        S.stack_push(st)
        identF = make_ident(S, "D_ident", F32)
        ones = S.sb("D_ones", [128, 128])
        S.op("pool", lambda e: e.memset(ones[:], 0.0), writes=[ones])
        S.op("pool", lambda e: e.memset(ones[0:64, 0:64], 1.0), reads=[ones], writes=[ones])
        S.op("pool", lambda e: e.memset(ones[64:128, 64:128], 1.0), reads=[ones], writes=[ones])

        def cvec(name, key, n):
            t = S.sb("D_" + name, [128, n])
            S.dma("sp", t[:], Wd[key].t[lyr].rearrange("(c p) -> p c", p=128), reads=[Wd[key]], writes=[t],
                  allow_slow_non_contiguous=True)
            return t

        def cvec2(name, key):
            t = S.sb("D_" + name, [128, 4])
            S.dma("sp", t[:], Wd[key].t[lyr].rearrange("(c a) j -> (a j) c", a=2), reads=[Wd[key]], writes=[t],
                  allow_slow_non_contiguous=True)
            return t
        mu = cvec("mu", "rk_mu", 13)
        w0 = cvec("w0", "rk_w0", 4)
        a0 = cvec("a0", "rk_a0", 4)
        lg = cvec("lg", "rk_lnx_g", 4)
        lb = cvec("lb", "rk_lnx_b", 4)
        kkc = cvec2("kkc", "rk_kk")
        ka = cvec2("ka", "rk_ka")
        rkc = cvec2("rkc", "rk_rk")
        omka = S.sb("D_omka", [128, 4])
        S.op("pool", lambda e: e.tensor_scalar(out=omka[:], in0=ka[:], scalar1=-1.0, scalar2=1.0, op0=ALU.mult, op1=ALU.add),
             reads=[ka], writes=[omka])
        w2 = S.sb("D_w2", [64, 512], BF16)
        a2 = S.sb("D_a2", [128, 512], BF16)
        S.dma("pool", w2[:], Wd["rk_w2"].t[lyr], reads=[Wd["rk_w2"]], writes=[w2])
        S.dma("pool", a2[64:128, :], Wd["rk_a2"].t[lyr], reads=[Wd["rk_a2"]], writes=[a2])
        St = S.sb("D_state", [128, 4, 64])
        S.op("dve", lambda e: e.memset(St[:], 0.0), writes=[St])

        rst = S.sb("D_rst", [128, 13, TP + 1])
        xs = S.sb("D_xs", [128, 13, TP])
        th = S.sb("D_th", [128, TP], BF16)
        dd = S.sb("D_dd", [128, 4, TP])
        aa = S.sb("D_aa", [128, 4, TP])
        kkf = S.sb("D_kkf", [128, 4, TP])
        sq = S.sb("D_sq", [128, 4, TP])
        rn = S.sb("D_rn", [128, 4, TP])
        kp = S.sb("D_kp", [128, 4, TP])
        am = S.sb("D_am", [128, 4, TP])
        bm = S.sb("D_bm", [128, 4, TP])
        t1 = S.sb("D_t1", [128, 4, TP])
        bonus = S.sb("D_bonus", [128, 4, TP])
        vv = S.sb("D_vv", [128, 4, TP])
        tk = S.sb("D_tk", [128, 5, 4, 128])
        bcr = Ring([S.sb("D_bc%d" % i, [128, TB, 5, 256]) for i in range(2)])
        tmp = S.sb("D_tmp", [128, 4, 64])
        tmp2 = S.sb("D_tmp2", [128, 4, 64])
        kv = Ring([S.sb("D_kv%d" % i, [128, 4, 64]) for i in range(2)])
        sa = S.sb("D_sa", [128, 4])
        ybuf = S.sb("D_y", [128, 4, TP])
        ysq = S.sb("D_ysq", [128, 4, TP])
        mean = S.sb("D_mean", [128, 4, TP])
        var = S.sb("D_var", [128, 4, TP])
        rzt = S.sb("D_rz", [128, 4, TP], BF16)
        yo = S.sb("D_yo", [128, 4, TP], BF16)
        pa = Ring([S.ps("D_pa%d" % i, [128, 4, 128]) for i in range(4)])

        bc4 = lambda t: t[:].unsqueeze(2).to_broadcast([128, 4, TP])
        for nb in range(SEQ // TP):
            t0 = nb * TP
            S.dma("sp", rst[:, :, 1:TP + 1], scr["rsT"].t[:, t0:t0 + TP].rearrange("(c p) t -> p c t", p=128), reads=[scr["rsT"]],
                  writes=[rst])
            if nb == 0:
                S.op("pool", lambda e: e.memset(rst[:, :, 0:1], 0.0), reads=[rst], pwrites=[rst])
            else:
                S.dma("sp", rst[:, :, 0:1], scr["rsT"].t[:, t0 - 1:t0].rearrange("(c p) t -> p c t", p=128), reads=[scr["rsT"]],
                      pwrites=[rst], key=rst, allow_slow_non_contiguous=True)
            S.op("pool", lambda e: e.tensor_tensor(out=xs[:], in0=rst[:, :, 0:TP], in1=rst[:, :, 1:TP + 1], op=ALU.subtract),
                 reads=[rst], writes=[xs])
            S.op("pool", lambda e: e.tensor_tensor(out=xs[:], in0=xs[:], in1=mu[:].unsqueeze(2).to_broadcast([128, 13, TP]), op=ALU.mult),
                 reads=[xs, mu], writes=[xs])
            S.op("pool", lambda e: e.tensor_tensor(out=xs[:], in0=xs[:], in1=rst[:, :, 1:TP + 1], op=ALU.add), reads=[xs, rst], writes=[xs])
            r = xs[:, 0:4, :]; k = xs[:, 4:8, :]; v = xs[:, 8:12, :]
            S.op("act", lambda e: e.activation(out=th[0:64, :], in_=xs[0:64, 12, :], func=AF.Tanh), reads=[xs], pwrites=[th])
            S.op("act", lambda e: e.copy(out=th[64:128, :], in_=xs[64:128, 12, :]), reads=[xs], pwrites=[th])
            pw = pa.next(); pp = pa.next()
            for p in range(4):
                S.op("pe", lambda e, p=p, pw=pw: e.matmul(pw[:, p, :], lhsT=w2[0:64, p * 128:(p + 1) * 128], rhs=th[0:64, :], start=True, stop=True),
                     reads=[w2, th], writes=[pw] if p == 0 else (), pwrites=() if p == 0 else [pw])
                S.op("pe", lambda e, p=p, pp=pp: e.matmul(pp[:, p, :], lhsT=a2[64:128, p * 128:(p + 1) * 128], rhs=th[64:128, :], start=True, stop=True),
                     reads=[a2, th], writes=[pp] if p == 0 else (), pwrites=() if p == 0 else [pp])
            for p in range(4):
                S.op("act", lambda e, p=p, pw=pw: e.activation(out=dd[:, p, :], in_=pw[:, p, :], func=AF.Sigmoid, bias=w0[:, p:p + 1]),
                     reads=[pw, w0], pwrites=[dd])
                S.op("act", lambda e, p=p, pp=pp: e.activation(out=aa[:, p, :], in_=pp[:, p, :], func=AF.Sigmoid, bias=a0[:, p:p + 1]),
                     reads=[pp, a0], pwrites=[aa])
            S.op("act", lambda e: e.activation(out=dd[:], in_=dd[:], func=AF.Exp, scale=-0.6065306597126334), reads=[dd], writes=[dd])
            S.op("pool", lambda e: e.tensor_tensor(out=kkf[:], in0=k, in1=bc4(kkc), op=ALU.mult), reads=[xs, kkc], writes=[kkf])
            S.op("pool", lambda e: e.tensor_tensor(out=sq[:], in0=kkf[:], in1=kkf[:], op=ALU.mult), reads=[kkf], writes=[sq])
            pn = pa.next()
            for p in range(4):
                S.op("pe", lambda e, p=p, pn=pn: e.matmul(pn[:, p, :], lhsT=ones[:], rhs=sq[:, p, :], start=True, stop=True),
                     reads=[ones, sq], writes=[pn] if p == 0 else (), pwrites=() if p == 0 else [pn])
            S.op("act", lambda e, pn=pn: e.activation(out=rn[:], in_=pn[:], func=AF.Sqrt), reads=[pn], writes=[rn])
            S.op("pool", lambda e: e.tensor_scalar(out=rn[:], in0=rn[:], scalar1=1e-12, scalar2=None, op0=ALU.max), reads=[rn], writes=[rn])
            S.op("dve", lambda e: e.reciprocal(out=rn[:], in_=rn[:]), reads=[rn], writes=[rn])
            S.op("pool", lambda e: e.tensor_tensor(out=kkf[:], in0=kkf[:], in1=rn[:], op=ALU.mult), reads=[kkf, rn], writes=[kkf])
            S.op("pool", lambda e: e.tensor_tensor(out=t1[:], in0=aa[:], in1=bc4(ka), op=ALU.mult), reads=[aa, ka], writes=[t1])
            S.op("pool", lambda e: e.tensor_tensor(out=t1[:], in0=t1[:], in1=bc4(omka), op=ALU.add), reads=[t1, omka], writes=[t1])
            S.op("pool", lambda e: e.tensor_tensor(out=kp[:], in0=k, in1=t1[:], op=ALU.mult), reads=[xs, t1], writes=[kp])
            S.op("pool", lambda e: e.tensor_scalar(out=am[:], in0=kkf[:], scalar1=-1.0, scalar2=None, op0=ALU.mult), reads=[kkf], writes=[am])
            S.op("pool", lambda e: e.tensor_tensor(out=bm[:], in0=kkf[:], in1=aa[:], op=ALU.mult), reads=[kkf, aa], writes=[bm])
            S.op("pool", lambda e: e.tensor_tensor(out=t1[:], in0=r, in1=kp[:], op=ALU.mult), reads=[xs, kp], writes=[t1])
            S.op("pool", lambda e: e.tensor_tensor(out=sq[:], in0=t1[:], in1=bc4(rkc), op=ALU.mult), reads=[t1, rkc], writes=[sq])
            pr = pa.next()
            for p in range(4):
                S.op("pe", lambda e, p=p, pr=pr: e.matmul(pr[:, p, :], lhsT=ones[:], rhs=sq[:, p, :], start=True, stop=True),
                     reads=[ones, sq], writes=[pr] if p == 0 else (), pwrites=() if p == 0 else [pr])
            S.op("act", lambda e, pr=pr: e.copy(out=bonus[:], in_=pr[:]), reads=[pr], writes=[bonus])
            S.op("pool", lambda e: e.tensor_tensor(out=bonus[:], in0=bonus[:], in1=v, op=ALU.mult), reads=[bonus, xs], writes=[bonus])
            S.op("pool", lambda e: e.tensor_copy(out=vv[:], in_=v), reads=[xs], writes=[vv])
            S.op("pool", lambda e: e.tensor_copy(out=t1[:], in_=r), reads=[xs], writes=[t1])
            for oi, src in enumerate((am, bm, dd, kp, t1)):
                pt = pa.next()
                for p in range(4):
                    S.op("pe", lambda e, p=p, src=src, pt=pt: e.transpose(out=pt[:, p, :], in_=src[:, p, :], identity=identF[:]),
                         reads=[src, identF], writes=[pt] if p == 0 else (), pwrites=() if p == 0 else [pt])
                S.op("act", lambda e, oi=oi, pt=pt: e.copy(out=tk[:, oi, :, :], in_=pt[:]), reads=[pt], pwrites=[tk])
            for oi in range(5):
                for h2 in range(2):
                    S.dma("sp", xtok.t[t0:t0 + TP, oi, h2, :].rearrange("t (p j) -> t p j", p=4), tk[:, oi, :, h2 * 64:(h2 + 1) * 64],
                          reads=[tk], pwrites=[xtok], key=tk)
            S.dma("sp", rzt[:], scr["rzT"].t[:, t0:t0 + TP].rearrange("(c p) t -> p c t", p=128), reads=[scr["rzT"]], writes=[rzt])
            xflat = xtok.t.rearrange("t o h c -> (t o) h c")
            for tb in range(0, TP, TB):
                bc = bcr.next()
                for h2 in range(2):
                    S.dma("sp", bc[h2 * 64:(h2 + 1) * 64, :, :, :].rearrange("p t o c -> p (t o) c"),
                          xflat[(t0 + tb) * 5:(t0 + tb + TB) * 5, h2, :].partition_broadcast(64),
                          reads=[xtok], writes=[bc] if h2 == 0 else (), pwrites=() if h2 == 0 else [bc], key=bc)
                for tt in range(TB):
                    t = tb + tt
                    A = bc[:, tt, 0, :].rearrange("p (a j) -> p a j", a=4)
                    B = bc[:, tt, 1, :].rearrange("p (a j) -> p a j", a=4)
                    Dd = bc[:, tt, 2, :].rearrange("p (a j) -> p a j", a=4)
                    Kk = bc[:, tt, 3, :].rearrange("p (a j) -> p a j", a=4)
                    R = bc[:, tt, 4, :].rearrange("p (a j) -> p a j", a=4)
                    kvb = kv.next()
                    S.op("pool", lambda e, Kk=Kk, t=t, kvb=kvb: e.tensor_tensor(out=kvb[:], in0=Kk, in1=vv[:, :, t:t + 1].to_broadcast([128, 4, 64]),
                                                                              op=ALU.mult), reads=[bc, vv], writes=[kvb])
                    S.op("dve", lambda e, A=A: e.tensor_tensor(out=tmp[:], in0=St[:], in1=A, op=ALU.mult), reads=[St, bc], writes=[tmp])
                    S.op("dve", lambda e: e.tensor_reduce(out=sa[:], in_=tmp[:], axis=AX.X, op=ALU.add), reads=[tmp], writes=[sa])
                    S.op("dve", lambda e, Dd=Dd: e.tensor_tensor(out=St[:], in0=St[:], in1=Dd, op=ALU.mult), reads=[St, bc, tmp], writes=[St])
                    S.op("dve", lambda e, B=B: e.tensor_tensor(out=tmp2[:], in0=B, in1=sa[:].unsqueeze(2).to_broadcast([128, 4, 64]), op=ALU.mult),
                         reads=[bc, sa], writes=[tmp2])
                    S.op("dve", lambda e: e.tensor_tensor(out=St[:], in0=St[:], in1=tmp2[:], op=ALU.add), reads=[St, tmp2], writes=[St])
                    S.op("dve", lambda e, kvb=kvb: e.tensor_tensor(out=St[:], in0=St[:], in1=kvb[:], op=ALU.add), reads=[St, kvb], writes=[St])
                    S.op("dve", lambda e, R=R: e.tensor_tensor(out=tmp[:], in0=St[:], in1=R, op=ALU.mult), reads=[St, bc], writes=[tmp])
                    S.op("dve", lambda e, t=t: e.tensor_reduce(out=ybuf[:, :, t], in_=tmp[:], axis=AX.X, op=ALU.add), reads=[tmp], pwrites=[ybuf])
            S.op("pool", lambda e: e.tensor_tensor(out=ysq[:], in0=ybuf[:], in1=ybuf[:], op=ALU.mult), reads=[ybuf], writes=[ysq])
            pm = pa.next(); pq = pa.next()
            for p in range(4):
                S.op("pe", lambda e, p=p, pm=pm: e.matmul(pm[:, p, :], lhsT=ones[:], rhs=ybuf[:, p, :], start=True, stop=True),
                     reads=[ones, ybuf], writes=[pm] if p == 0 else (), pwrites=() if p == 0 else [pm])
                S.op("pe", lambda e, p=p, pq=pq: e.matmul(pq[:, p, :], lhsT=ones[:], rhs=ysq[:, p, :], start=True, stop=True),
                     reads=[ones, ysq], writes=[pq] if p == 0 else (), pwrites=() if p == 0 else [pq])
            S.op("act", lambda e, pm=pm: e.activation(out=mean[:], in_=pm[:], func=AF.Copy, scale=1.0 / 64), reads=[pm], writes=[mean])
            S.op("act", lambda e, pq=pq: e.activation(out=var[:], in_=pq[:], func=AF.Copy, scale=1.0 / 64), reads=[pq], writes=[var])
            S.op("pool", lambda e: e.tensor_tensor(out=ysq[:], in0=mean[:], in1=mean[:], op=ALU.mult), reads=[mean, ysq], writes=[ysq])
            S.op("pool", lambda e: e.tensor_tensor(out=var[:], in0=var[:], in1=ysq[:], op=ALU.subtract), reads=[var, ysq], writes=[var])
            S.op("act", lambda e: e.activation(out=var[:], in_=var[:], func=AF.Sqrt, bias=GN_EPS, scale=1.0), reads=[var], writes=[var])
            S.op("dve", lambda e: e.reciprocal(out=var[:], in_=var[:]), reads=[var], writes=[var])
            S.op("pool", lambda e: e.tensor_tensor(out=mean[:], in0=ybuf[:], in1=mean[:], op=ALU.subtract), reads=[ybuf, mean], writes=[mean])
            S.op("pool", lambda e: e.tensor_tensor(out=mean[:], in0=mean[:], in1=var[:], op=ALU.mult), reads=[mean, var], writes=[mean])
            S.op("pool", lambda e: e.tensor_tensor(out=mean[:], in0=mean[:], in1=bc4(lg), op=ALU.mult), reads=[mean, lg], writes=[mean])
            S.op("pool", lambda e: e.tensor_tensor(out=mean[:], in0=mean[:], in1=bc4(lb), op=ALU.add), reads=[mean, lb], writes=[mean])
            S.op("pool", lambda e: e.tensor_tensor(out=mean[:], in0=mean[:], in1=bonus[:], op=ALU.add), reads=[mean, bonus], writes=[mean])
            S.op("pool", lambda e: e.tensor_tensor(out=yo[:], in0=mean[:], in1=rzt[:], op=ALU.mult), reads=[mean, rzt], writes=[yo])
            S.dma("sp", scr["ysT"].t[2, :, t0:t0 + TP].rearrange("(c p) t -> p c t", p=128), yo[:], reads=[yo], pwrites=[scr["ysT"]], key=yo)
        _barrier(S)
        S.stack_pop()


_NC_CACHE = {}


def kernel(**inputs):
    SEQ = 8192
    if "nc" not in _NC_CACHE:
        _NC_CACHE["nc"] = build(SEQ, nlayers=2, enable=(1, 1, 1), scr_kind="Internal")
    nc = _NC_CACHE["nc"]
    x = np.ascontiguousarray(np.asarray(inputs["x"], dtype=np.float32))
    p = np.asarray(inputs["p"], dtype=np.float32)
    base = {}
    for k in WSPEC:
        v = np.ascontiguousarray(np.asarray(inputs[k], dtype=np.float32))
        base[k] = v.reshape(WSPEC[k])
    in_maps = []
    for b in range(8):
        m = dict(base)
        m["x"] = np.ascontiguousarray(x[b])
        m["p"] = np.ascontiguousarray(p[:, b])
        in_maps.append(m)
    res = run_bass_kernel_spmd(nc, in_maps, core_ids=list(range(8)))
    return np.stack([np.asarray(r["out"], dtype=np.float32) for r in res.results], axis=0)
```

```python
import contextlib
import numpy as np
import concourse.bass as bass
import concourse.mybir as mybir

F32 = mybir.dt.float32
BF16 = mybir.dt.bfloat16
AF = mybir.ActivationFunctionType
ALU = mybir.AluOpType
AX = mybir.AxisListType

ENGS = ("pe", "act", "dve", "pool", "sp")


class Buf:
    __slots__ = ("name", "w", "wfull", "r", "t")

    def __init__(self, name, t=None):
        self.name = name
        self.t = t
        self.w = []
        self.wfull = []
        self.r = []

    def __getitem__(self, k):
        return self.t[k]


class Op:
    __slots__ = ("eng", "fn", "deps", "marked", "tick", "dma", "idx")

    def __init__(self, eng, fn, dma):
        self.eng = eng
        self.fn = fn
        self.deps = []
        self.marked = False
        self.tick = None
        self.dma = dma
        self.idx = None


class DmaSem:
    def __init__(self):
        self.sem = None
        self.count = 0


class Sched:
    def __init__(self, nc, stack):
        self.nc = nc
        self.stack = stack
        self.ops = {e: [] for e in ENGS}
        self.all_ops = []
        self.dsems = {}
        self.n_sems = 0
        self.fence = []
        self.stacks = [stack]
        self.phase_keys = []
        self.free_ds = []
        self.all_ds = []
        self.keep = []

    def stack_push(self, st):
        self.stacks.append(st)
        self.phase_keys.append([])

    def stack_pop(self):
        self.stacks.pop()
        for kid in self.phase_keys.pop():
            ds = self.dsems.pop(kid, None)
            if ds is not None:
                self.free_ds.append(ds)

    def sb(self, name, shape, dt=F32):
        self.n_sems += 1
        name = "%s_u%d" % (name, self.n_sems)
        t = self.stacks[-1].enter_context(self.nc.sbuf_tensor(name, list(shape), dt))
        return Buf(name, t)

    def ps(self, name, shape, dt=F32):
        self.n_sems += 1
        name = "%s_u%d" % (name, self.n_sems)
        t = self.stacks[-1].enter_context(self.nc.psum_tensor(name, list(shape), dt))
        return Buf(name, t)

    def dram(self, name, shape, dt, kind="Internal"):
        t = self.nc.dram_tensor(name, list(shape), dt, kind=kind)
        return Buf(name, t.ap())

    def _add(self, eng, fn, reads, writes, pwrites, dma):
        op = Op(eng, fn, dma)
        deps = list(self.fence)
        for b in reads:
            deps.extend(b.w)
        for b in writes:
            deps.extend(b.w)
            deps.extend(b.r)
        for b in pwrites:
            deps.extend(b.wfull)
            deps.extend(b.r)
        seen = set()
        for d in deps:
            if id(d) in seen or d is op:
                continue
            seen.add(id(d))
            if d.eng == "pe" and eng == "pe" and d.dma is None and dma is None:
                continue
            op.deps.append(d)
            d.marked = True
        for b in reads:
            b.r.append(op)
            if len(b.r) > 24:
                b.r = self._prune(b.r)
        for b in writes:
            b.w = [op]
            b.wfull = [op]
            b.r = []
        for b in pwrites:
            b.w.append(op)
            if len(b.w) > 24:
                b.w = self._prune(b.w)
        op.idx = len(self.all_ops)
        self.all_ops.append(op)
        self.ops[eng].append(op)
        return op

    @staticmethod
    def _prune(lst):
        last = {}
        for o in lst:
            key = (o.eng, None) if o.dma is None else ("dma", id(o.dma))
            last[key] = o
        return list(last.values())

    def op(self, eng, fn, reads=(), writes=(), pwrites=()):
        return self._add(eng, fn, reads, writes, pwrites, None)

    def dma(self, eng, out_ap, in_ap, reads=(), writes=(), pwrites=(), key=None, **kw):
        if key is None:
            key = (list(writes) + list(pwrites))[0]
        ds = self.dsems.get(id(key))
        if ds is None:
            if self.free_ds:
                ds = self.free_ds.pop()
            else:
                ds = DmaSem()
                self.all_ds.append(ds)
            self.dsems[id(key)] = ds
            self.keep.append(key)
            if self.phase_keys:
                self.phase_keys[-1].append(id(key))
        fn = lambda e, o=out_ap, i=in_ap, kw=kw: e.dma_start(out=o, in_=i, **kw)
        op = self._add(eng, fn, reads, writes, pwrites, ds)
        ds.count += 16
        op.tick = ds.count
        return op

    def barrier_bufs(self, bufs):
        pass

    def emit(self):
        nc = self.nc
        stack = self.stack
        esem = {}
        for e in ENGS:
            esem[e] = stack.enter_context(nc.semaphore("s_" + e))
        for ds in self.all_ds:
            ds.sem = stack.enter_context(nc.semaphore("d%d" % self.n_sems))
            self.n_sems += 1
        for e in ENGS:
            c = 0
            for o in self.ops[e]:
                if o.dma is None:
                    if o.marked:
                        c += 1
                        o.tick = c
        self.max_ticks = {e: max([o.tick or 0 for o in self.ops[e] if o.dma is None] + [0]) for e in ENGS}

        def evkey(d):
            if d.dma is not None:
                return ("d", id(d.dma)), d.dma.sem, d.tick
            return ("e", d.eng), esem[d.eng], d.tick

        def run(eng_name, eng):
            seen = {}
            for o in self.ops[eng_name]:
                waits = {}
                for d in o.deps:
                    k, sem, val = evkey(d)
                    if seen.get(k, 0) >= val:
                        continue
                    if k not in waits or waits[k][1] < val:
                        waits[k] = (sem, val)
                for k, (sem, val) in waits.items():
                    eng.wait_ge(sem, val)
                    seen[k] = val
                inst = o.fn(eng)
                if o.dma is not None:
                    inst.then_inc(o.dma.sem, 16)
                elif o.marked:
                    inst.then_inc(esem[eng_name], 1)
            if eng_name == "sp":
                for e2 in ENGS:
                    m = self.max_ticks[e2]
                    if m > 0:
                        eng.wait_ge(esem[e2], m)
                for ds in self.all_ds:
                    if ds.count:
                        eng.wait_ge(ds.sem, ds.count)

        block = stack.enter_context(nc.Block())

        @block.tensor
        def _(e):
            run("pe", e)

        @block.scalar
        def _(e):
            run("act", e)

        @block.vector
        def _(e):
            run("dve", e)

        @block.gpsimd
        def _(e):
            run("pool", e)

        @block.sync
        def _(e):
            run("sp", e)


from concourse.bass_utils import run_bass_kernel_spmd

D = 1024
NCOL = 8600
PLE = 256
EPS = 1e-6


DEBUG = {}
_dbg_n = [0]


def dbg_dump(S, name, ap, buf, shape, cond=True):
    if not DEBUG.get("on") or not cond:
        return
    _dbg_n[0] += 1
    t = S.stacks[-1].enter_context(S.nc.sbuf_tensor("dbgsb_%d" % _dbg_n[0], list(shape), F32))
    tb = Buf("dbgsb", t)
    d = S.dram("dbg_" + name, list(shape), F32, kind="ExternalOutput")
    S.op("act", lambda e: e.copy(out=t[:], in_=ap), reads=[buf], writes=[tb])
    S.dma("sp", d.t, t[:], reads=[tb], writes=[d], key=tb)


class Ring:
    def __init__(self, bufs):
        self.bufs = bufs
        self.i = 0

    def next(self):
        b = self.bufs[self.i % len(self.bufs)]
        self.i += 1
        return b


def _barrier(S):
    fence = []
    for e in ENGS:
        comp = [o for o in S.ops[e] if o.dma is None]
        if comp:
            fence.append(comp[-1])
    lastd = {}
    for o in S.all_ops:
        if o.dma is not None:
            lastd[id(o.dma)] = o
    fence.extend(lastd.values())
    S.fence = fence


def make_ident(S, name="ident", dt=BF16):
    ident = S.sb(name, [128, 128], dt)
    S.op("pool", lambda e: e.memset(ident[:], 0.0), writes=[ident])
    S.op("pool", lambda e: e.affine_select(out=ident[:], in_=ident[:], pattern=[[-1, 128]],
                                           compare_op=ALU.not_equal, fill=1.0, base=0,
                                           channel_multiplier=1), reads=[ident], writes=[ident])
    return ident


def load_w_bf16(S, dst, k, src_ap, srcbuf):
    S.dma("pool", dst, src_ap, reads=[srcbuf], pwrites=[k], key=k, max_dma_last_dim=4096)


def rmsnorm_tile(S, xt_ap, xt_buf, g_buf, h_ap, h_buf, sq, ss, rs, eps=EPS, extra_reads=()):
    S.op("act", lambda e: e.activation(out=sq[:], in_=xt_ap, func=AF.Square, accum_out=ss[:]),
         reads=[xt_buf] + list(extra_reads), writes=[sq, ss])
    S.op("act", lambda e: e.activation(out=rs[:], in_=ss[:], func=AF.Sqrt, scale=1.0 / D, bias=eps),
         reads=[ss], writes=[rs])
    S.op("dve", lambda e: e.reciprocal(out=rs[:], in_=rs[:]), reads=[rs], writes=[rs])
    S.op("dve", lambda e: e.scalar_tensor_tensor(out=h_ap, in0=xt_ap, scalar=rs[:, 0:1], in1=g_buf[:],
                                                 op0=ALU.mult, op1=ALU.mult),
         reads=[xt_buf, rs, g_buf], pwrites=[h_buf])


def phase_A(S, nc, SEQ, lyr, x_src, Wd, scr):
    TT = 512
    nsub = TT // 128
    with contextlib.ExitStack() as st:
        S.stack_push(st)
        wt = S.sb("A_w", [128, 8, NCOL], BF16)
        gt = S.sb("A_g", [128, D])
        ident = make_ident(S, "A_ident")
        xt = S.sb("A_x", [128, nsub, D])
        sq = S.sb("A_sq", [128, D], BF16)
        ss = S.sb("A_ss", [128, 1])
        rs = S.sb("A_rs", [128, 1])
        h = S.sb("A_h", [128, nsub, D], BF16)
        hT = S.sb("A_hT", [128, 8, TT], BF16)
        stg_b = Ring([S.sb("A_sb%d" % i, [128, 512], BF16) for i in range(4)])
        stg_f = Ring([S.sb("A_sf%d" % i, [128, 512], F32) for i in range(3)])
        pT = Ring([S.ps("A_pT%d" % i, [128, 8, 128], BF16) for i in range(2)])
        pacc = Ring([S.ps("A_pa%d" % i, [128, 512], F32) for i in range(6)])

        w_in = Wd["w_in"]
        for k in range(8):
            S.dma("pool", wt[:, k, :], w_in.t[lyr, k * 128:(k + 1) * 128, 0:NCOL], reads=[w_in], pwrites=[wt],
                  key=wt, max_dma_last_dim=4096)
        S.dma("sp", gt[:], Wd["norm_g"].t[lyr:lyr + 1, :].partition_broadcast(128), reads=[Wd["norm_g"]],
              writes=[gt])

        FM = []
        for c in range(4):
            FM.append((c * 128, scr["qT"], c * 128, AF.Copy, 0.125, BF16))
        FM.append((512, scr["kcT"], 0, None, 1.0, BF16))
        FM.append((640, scr["vcT"], 0, None, 1.0, BF16))
        FM.append((768, scr["ksT"], 0, None, 1.0, BF16))
        FM.append((1024, scr["kwT"], 0, None, 1.0, BF16))
        for c in range(13):
            FM.append((3352 + c * 128, scr["rsT"], c * 128, None, 1.0, F32))
        for c in range(4):
            FM.append((5016 + c * 128, scr["rzT"], c * 128, AF.Silu, 1.0, BF16))
        for c in range(24):
            FM.append((5528 + c * 128, scr["mgT"], c * 128, AF.Sigmoid, 1.0, BF16))
        TM = [
            (896, 128, scr["vsw"], 0, None, BF16),
            (1152, 128, scr["vsw"], 128, None, BF16),
            (1280, 24, scr["gate"], 0, AF.Sigmoid, F32),
            (1304, 512, scr["nzs"], 0, AF.Silu, BF16),
            (1816, 512, scr["su"], 0, None, F32),
            (2328, 512, scr["sv"], 0, None, F32),
            (2840, 512, scr["szs"], 0, AF.Silu, BF16),
        ]
        evac_i = [0]

        def evac(out_ap, out_buf, in_ap, in_buf, func, scale):
            if func is None and scale == 1.0:
                if evac_i[0] % 2 == 0:
                    S.op("dve", lambda e: e.tensor_copy(out=out_ap, in_=in_ap), reads=[in_buf], writes=[out_buf])
                else:
                    S.op("act", lambda e: e.copy(out=out_ap, in_=in_ap), reads=[in_buf], writes=[out_buf])
                evac_i[0] += 1
            else:
                S.op("act", lambda e: e.activation(out=out_ap, in_=in_ap, func=func, scale=scale),
                     reads=[in_buf], writes=[out_buf])

        for ti in range(SEQ // TT):
            t0 = ti * TT
            S.dma("sp", xt[:], x_src.t[t0:t0 + TT, :].rearrange("(s p) d -> p s d", p=128), reads=[x_src],
                  writes=[xt])
            for s in range(nsub):
                rmsnorm_tile(S, xt[:, s, :], xt, gt, h[:, s, :], h, sq, ss, rs)
                pt = pT.next()
                for k in range(8):
                    S.op("pe", lambda e, k=k, s=s, pt=pt: e.transpose(out=pt[:, k, :], in_=h[:, s, k * 128:(k + 1) * 128],
                                                                     identity=ident[:]),
                         reads=[h, ident], writes=[pt] if k == 0 else (), pwrites=() if k == 0 else [pt])
                S.op("dve", lambda e, s=s, pt=pt: e.tensor_copy(out=hT[:, :, s * 128:(s + 1) * 128], in_=pt[:]),
                     reads=[pt], pwrites=[hT])
            for (c0, dbuf, r0, func, scale, dt) in FM:
                pa = pacc.next()
                for k in range(8):
                    S.op("pe", lambda e, k=k, pa=pa, c0=c0: e.matmul(pa[:], lhsT=wt[:, k, c0:c0 + 128], rhs=hT[:, k, :],
                                                                    start=(k == 0), stop=(k == 7)),
                         reads=[wt, hT], writes=[pa] if k == 0 else (), pwrites=() if k == 0 else [pa])
                sg = stg_b.next() if dt == BF16 else stg_f.next()
                evac(sg[:], sg, pa[:], pa, func, scale)
                S.dma("sp", dbuf.t[r0:r0 + 128, t0:t0 + TT], sg[:], reads=[sg], pwrites=[dbuf], key=sg)
            for s in range(nsub):
                for (c0, ncol, dbuf, dc0, func, dt) in TM:
                    pa = pacc.next()
                    for k in range(8):
                        S.op("pe", lambda e, k=k, pa=pa, c0=c0, ncol=ncol, s=s: e.matmul(
                            pa[:, 0:ncol], lhsT=hT[:, k, s * 128:(s + 1) * 128], rhs=wt[:, k, c0:c0 + ncol],
                            start=(k == 0), stop=(k == 7)),
                            reads=[wt, hT], writes=[pa] if k == 0 else (), pwrites=() if k == 0 else [pa])
                    sg = stg_b.next() if dt == BF16 else stg_f.next()
                    evac(sg[:, 0:ncol], sg, pa[:, 0:ncol], pa, func, 1.0)
                    S.dma("sp", dbuf.t[t0 + s * 128:t0 + (s + 1) * 128, dc0:dc0 + ncol], sg[:, 0:ncol], reads=[sg],
                          pwrites=[dbuf], key=sg)
        _barrier(S)
        S.stack_pop()


def make_scratch(S, SEQ, kind="Internal"):
    scr = {}
    def mk(name, shape, dt):
        scr[name] = S.dram(name, shape, dt, kind=kind)
    mk("qT", [512, SEQ], BF16)
    mk("kcT", [128, SEQ], BF16)
    mk("vcT", [128, SEQ], BF16)
    mk("ksT", [128, SEQ], BF16)
    mk("kwT", [128, SEQ], BF16)
    mk("vsw", [SEQ, 256], BF16)
    mk("gate", [SEQ, 24], F32)
    mk("nzs", [SEQ, 512], BF16)
    mk("su", [SEQ, 512], F32)
    mk("sv", [SEQ, 512], F32)
    mk("szs", [SEQ, 512], BF16)
    mk("rsT", [1664, SEQ], F32)
    mk("rzT", [512, SEQ], BF16)
    mk("mgT", [3072, SEQ], BF16)
    mk("ysT", [3, 512, SEQ], BF16)
    mk("xtok", [SEQ, 5, 2, 256], F32)
    return scr


def phase_C(S, nc, SEQ, lyr, Wd, scr):
    LN_EPS = 1e-5
    with contextlib.ExitStack() as st:
        S.stack_push(st)
        ident = make_ident(S, "C_ident")
        wraw = S.sb("C_wraw", [128, 8, 128])
        wbf = S.sb("C_wbf", [128, 8, 128], BF16)
        WT = S.sb("C_WT", [128, 8, 128], BF16)
        bsT = S.sb("C_bsT", [128, 8])
        lng = S.sb("C_lng", [128, 512])
        lnb = S.sb("C_lnb", [128, 512])
        pw = S.ps("C_pw", [128, 8, 128], BF16)
        S.dma("sp", wraw[:], Wd["sg_w"].t[lyr].rearrange("g t s -> t g s"), reads=[Wd["sg_w"]], writes=[wraw])
        S.dma("sp", bsT[:], Wd["sg_b"].t[lyr].rearrange("g t -> t g"), reads=[Wd["sg_b"]], writes=[bsT],
              allow_slow_non_contiguous=True)
        S.dma("sp", lng[:], Wd["sg_ln_g"].t[lyr:lyr + 1, :].partition_broadcast(128), reads=[Wd["sg_ln_g"]], writes=[lng])
        S.dma("sp", lnb[:], Wd["sg_ln_b"].t[lyr:lyr + 1, :].partition_broadcast(128), reads=[Wd["sg_ln_b"]], writes=[lnb])
        S.op("pool", lambda e: e.affine_select(out=wraw[:], in_=wraw[:], pattern=[[0, 8], [-1, 128]],
                                               compare_op=ALU.is_ge, fill=0.0, base=0, channel_multiplier=1),
             reads=[wraw], writes=[wraw])
        S.op("dve", lambda e: e.tensor_copy(out=wbf[:], in_=wraw[:]), reads=[wraw], writes=[wbf])
        for g in range(8):
            S.op("pe", lambda e, g=g: e.transpose(out=pw[:, g, :], in_=wbf[:, g, :], identity=ident[:]),
                 reads=[wbf, ident], pwrites=[pw])
        S.op("dve", lambda e: e.tensor_copy(out=WT[:], in_=pw[:]), reads=[pw], writes=[WT])

        NB = 2
        svt = Ring([S.sb("C_sv%d" % i, [128, 512]) for i in range(NB)])
        sut = Ring([S.sb("C_su%d" % i, [128, 512]) for i in range(NB)])
        szt = Ring([S.sb("C_sz%d" % i, [128, 512], BF16) for i in range(NB)])
        stats = S.sb("C_stats", [128, 6])
        mv = S.sb("C_mv", [128, 2])
        rstd = S.sb("C_rstd", [128, 1])
        vn0 = S.sb("C_vnf", [128, 512])
        vn = Ring([S.sb("C_vn%d" % i, [128, 512], BF16) for i in range(2)])
        y0 = S.sb("C_y0", [128, 512])
        yb = Ring([S.sb("C_yb%d" % i, [128, 512], BF16) for i in range(2)])
        pm = Ring([S.ps("C_pm%d" % i, [128, 512]) for i in range(2)])
        pt = Ring([S.ps("C_pt%d" % i, [128, 4, 128], BF16) for i in range(2)])
        stg = Ring([S.sb("C_stg%d" % i, [128, 4, 512], BF16) for i in range(2)])
        ys = scr["ysT"]
        sgb = None
        for c in range(SEQ // 128):
            t0 = c * 128
            v = svt.next(); u = sut.next(); z = szt.next()
            S.dma("sp", v[:], scr["sv"].t[t0:t0 + 128, :], reads=[scr["sv"]], writes=[v])
            S.dma("sp", u[:], scr["su"].t[t0:t0 + 128, :], reads=[scr["su"]], writes=[u])
            S.dma("sp", z[:], scr["szs"].t[t0:t0 + 128, :], reads=[scr["szs"]], writes=[z])
            S.op("dve", lambda e, v=v: e.bn_stats(out=stats[:], in_=v[:]), reads=[v], writes=[stats])
            S.op("dve", lambda e: e.bn_aggr(out=mv[:], in_=stats[:]), reads=[stats], writes=[mv])
            S.op("act", lambda e: e.activation(out=rstd[:], in_=mv[:, 1:2], func=AF.Sqrt, bias=LN_EPS, scale=1.0),
                 reads=[mv], writes=[rstd])
            S.op("dve", lambda e: e.reciprocal(out=rstd[:], in_=rstd[:]), reads=[rstd], writes=[rstd])
            S.op("dve", lambda e, v=v: e.tensor_scalar(out=vn0[:], in0=v[:], scalar1=mv[:, 0:1], scalar2=rstd[:, 0:1],
                                                       op0=ALU.subtract, op1=ALU.mult),
                 reads=[v, mv, rstd], writes=[vn0])
            S.op("pool", lambda e: e.tensor_tensor(out=vn0[:], in0=vn0[:], in1=lng[:], op=ALU.mult),
                 reads=[vn0, lng], writes=[vn0])
            vb = vn.next()
            S.op("pool", lambda e, vb=vb: e.tensor_tensor(out=vb[:], in0=vn0[:], in1=lnb[:], op=ALU.add),
                 reads=[vn0, lnb], writes=[vb])
            pmm = pm.next()
            for g in range(8):
                S.op("pe", lambda e, g=g, vb=vb, pmm=pmm: e.matmul(pmm[:, g * 64:(g + 1) * 64], lhsT=WT[:, g, :],
                                                                   rhs=vb[:, g * 64:(g + 1) * 64], start=True, stop=True),
                     reads=[WT, vb], writes=[pmm] if g == 0 else (), pwrites=() if g == 0 else [pmm])
            S.op("dve", lambda e, pmm=pmm: e.tensor_tensor(
                out=y0[:].rearrange("p (g d) -> p g d", g=8), in0=pmm[:].rearrange("p (g d) -> p g d", g=8),
                in1=bsT[:].unsqueeze(2).to_broadcast([128, 8, 64]), op=ALU.add), reads=[pmm, bsT], writes=[y0])
            S.op("pool", lambda e, u=u: e.tensor_tensor(out=y0[:], in0=y0[:], in1=u[:], op=ALU.mult),
                 reads=[y0, u], writes=[y0])
            y = yb.next()
            S.op("dve", lambda e, y=y, z=z: e.tensor_tensor(out=y[:], in0=y0[:], in1=z[:], op=ALU.mult),
                 reads=[y0, z], writes=[y])
            ptt = pt.next()
            for k in range(4):
                S.op("pe", lambda e, k=k, y=y, ptt=ptt: e.transpose(out=ptt[:, k, :], in_=y[:, k * 128:(k + 1) * 128],
                                                                    identity=ident[:]),
                     reads=[y, ident], writes=[ptt] if k == 0 else (), pwrites=() if k == 0 else [ptt])
            if c % 4 == 0:
                sgb = stg.next()
            cc = c % 4
            S.op("act", lambda e, ptt=ptt, sgb=sgb, cc=cc: e.copy(out=sgb[:, :, cc * 128:(cc + 1) * 128], in_=ptt[:]),
                 reads=[ptt], writes=[sgb] if cc == 0 else (), pwrites=() if cc == 0 else [sgb])
            if cc == 3 or c == SEQ // 128 - 1:
                tb = (c // 4) * 512
                n = (cc + 1) * 128
                S.dma("sp", ys.t[1, :, tb:tb + n].rearrange("(k p) t -> p k t", p=128), sgb[:, :, 0:n], reads=[sgb],
                      pwrites=[ys], key=sgb)
        _barrier(S)
        S.stack_pop()


def phase_E(S, nc, SEQ, lyr, x_src, x_dst, Wd, scr, final):
    TT = 512
    nsub = 4
    with contextlib.ExitStack() as st:
        S.stack_push(st)
        ident = make_ident(S, "E_ident")
        wb = S.sb("E_wb", [128, 3, 4, D], BF16)
        wo = S.sb("E_wo", [128, 8, D], BF16)
        wpg = S.sb("E_wpg", [128, 8, D], BF16)
        wpp = S.sb("E_wpp", [128, 2, D], BF16)
        gpl = S.sb("E_gpl", [128, D])
        gfin = S.sb("E_gfin", [128, D])
        for n in range(3):
            S.dma("pool", wb[:, n, :, :], Wd["w_branch"].t[lyr, n].rearrange("(k p) d -> p k d", p=128),
                  reads=[Wd["w_branch"]], pwrites=[wb], key=wb, max_dma_last_dim=4096)
        for k0 in range(0, 8, 4):
            S.dma("pool", wo[:, k0:k0 + 4, :], Wd["w_o"].t[lyr, k0 * 128:(k0 + 4) * 128, :].rearrange("(k p) d -> p k d", p=128),
                  reads=[Wd["w_o"]], pwrites=[wo], key=wo, max_dma_last_dim=4096)
            S.dma("pool", wpg[:, k0:k0 + 4, :], Wd["w_ple_gate"].t[lyr, k0 * 128:(k0 + 4) * 128, :].rearrange("(k p) d -> p k d", p=128),
                  reads=[Wd["w_ple_gate"]], pwrites=[wpg], key=wpg, max_dma_last_dim=4096)
        S.dma("pool", wpp[:], Wd["w_ple_proj"].t[lyr].rearrange("(k p) d -> p k d", p=128),
              reads=[Wd["w_ple_proj"]], pwrites=[wpp], key=wpp, max_dma_last_dim=4096)
        S.dma("sp", gpl[:], Wd["ple_norm_g"].t[lyr:lyr + 1, :].partition_broadcast(128), reads=[Wd["ple_norm_g"]], writes=[gpl])
        if final:
            S.dma("sp", gfin[:], Wd["final_norm_g"].t[0:1, :].partition_broadcast(128), reads=[Wd["final_norm_g"]], writes=[gfin])

        yst = S.sb("E_ys", [128, 3, 4, TT], BF16)
        mgt = S.sb("E_mg", [128, 24, TT], BF16)
        mrg = S.sb("E_mrg", [128, 8, TT])
        mrb = S.sb("E_mrb", [128, 8, TT], BF16)
        tmp = Ring([S.sb("E_tmp%d" % i, [128, TT]) for i in range(2)])
        xt = S.sb("E_x", [128, nsub, D])
        pin = S.sb("E_p", [128, nsub, PLE])
        pbf = S.sb("E_pbf", [128, PLE], BF16)
        pTs = S.sb("E_pT", [128, 2, 128], BF16)
        sq = S.sb("E_sq", [128, D], BF16)
        ss = S.sb("E_ss", [128, 1])
        rs = S.sb("E_rs", [128, 1])
        hp = S.sb("E_hp", [128, D], BF16)
        hpT = S.sb("E_hpT", [128, 8, 128], BF16)
        gate = S.sb("E_gate", [128, D])
        xo = Ring([S.sb("E_xo%d" % i, [128, D]) for i in range(2)])
        pz = Ring([S.ps("E_pz%d" % i, [128, TT]) for i in range(3)])
        po = Ring([S.ps("E_po%d" % i, [128, 512]) for i in range(2)])
        pg = Ring([S.ps("E_pg%d" % i, [128, 512]) for i in range(2)])
        ptr = S.ps("E_ptr", [128, 8, 128], BF16)

        for ti in range(SEQ // TT):
            t0 = ti * TT
            for n in range(3):
                S.dma("sp", yst[:, n, :, :], scr["ysT"].t[n, :, t0:t0 + TT].rearrange("(k p) t -> p k t", p=128),
                      reads=[scr["ysT"]], writes=[yst] if n == 0 else (), pwrites=() if n == 0 else [yst], key=yst)
            for k0 in range(0, 24, 8):
                S.dma("sp", mgt[:, k0:k0 + 8, :], scr["mgT"].t[k0 * 128:(k0 + 8) * 128, t0:t0 + TT].rearrange("(k p) t -> p k t", p=128),
                      reads=[scr["mgT"]], writes=[mgt] if k0 == 0 else (), pwrites=() if k0 == 0 else [mgt], key=mgt)
            S.dma("sp", xt[:], x_src.t[t0:t0 + TT, :].rearrange("(s p) d -> p s d", p=128), reads=[x_src], writes=[xt])
            S.dma("sp", pin[:], Wd["p"].t[lyr, t0:t0 + TT, :].rearrange("(s p) d -> p s d", p=128), reads=[Wd["p"]], writes=[pin])
            for dc in range(8):
                pzs = []
                for n in range(3):
                    pzz = pz.next()
                    pzs.append(pzz)
                    for k in range(4):
                        S.op("pe", lambda e, n=n, k=k, dc=dc, pzz=pzz: e.matmul(
                            pzz[:], lhsT=wb[:, n, k, dc * 128:(dc + 1) * 128], rhs=yst[:, n, k, :], start=(k == 0), stop=(k == 3)),
                            reads=[wb, yst], writes=[pzz] if k == 0 else (), pwrites=() if k == 0 else [pzz])
                S.op("dve", lambda e, dc=dc, p0=pzs[0]: e.tensor_tensor(out=mrg[:, dc, :], in0=p0[:], in1=mgt[:, dc, :], op=ALU.mult),
                     reads=[pzs[0], mgt], pwrites=[mrg])
                t1 = tmp.next()
                S.op("dve", lambda e, dc=dc, p1=pzs[1], t1=t1: e.tensor_tensor(out=t1[:], in0=p1[:], in1=mgt[:, 8 + dc, :], op=ALU.mult),
                     reads=[pzs[1], mgt], writes=[t1])
                t2 = tmp.next()
                S.op("dve", lambda e, dc=dc, p2=pzs[2], t2=t2: e.tensor_tensor(out=t2[:], in0=p2[:], in1=mgt[:, 16 + dc, :], op=ALU.mult),
                     reads=[pzs[2], mgt], writes=[t2])
                S.op("pool", lambda e, dc=dc, t1=t1: e.tensor_tensor(out=mrg[:, dc, :], in0=mrg[:, dc, :], in1=t1[:], op=ALU.add),
                     reads=[mrg, t1], pwrites=[mrg])
                S.op("pool", lambda e, dc=dc, t2=t2: e.tensor_tensor(out=mrb[:, dc, :], in0=mrg[:, dc, :], in1=t2[:], op=ALU.add),
                     reads=[mrg, t2], pwrites=[mrb])
            for s in range(nsub):
                for blk in range(2):
                    pp = po.next()
                    for k in range(8):
                        S.op("pe", lambda e, k=k, s=s, blk=blk, pp=pp: e.matmul(
                            pp[:], lhsT=mrb[:, k, s * 128:(s + 1) * 128], rhs=wo[:, k, blk * 512:(blk + 1) * 512],
                            start=(k == 0), stop=(k == 7)),
                            reads=[mrb, wo], writes=[pp] if k == 0 else (), pwrites=() if k == 0 else [pp])
                    S.op("dve", lambda e, s=s, blk=blk, pp=pp: e.tensor_tensor(
                        out=xt[:, s, blk * 512:(blk + 1) * 512], in0=pp[:], in1=xt[:, s, blk * 512:(blk + 1) * 512], op=ALU.add),
                        reads=[pp, xt], pwrites=[xt])
                rmsnorm_tile(S, xt[:, s, :], xt, gpl, hp[:], hp, sq, ss, rs)
                for k in range(8):
                    S.op("pe", lambda e, k=k: e.transpose(out=ptr[:, k, :], in_=hp[:, k * 128:(k + 1) * 128], identity=ident[:]),
                         reads=[hp, ident], writes=[ptr] if k == 0 else (), pwrites=() if k == 0 else [ptr])
                S.op("act", lambda e: e.copy(out=hpT[:], in_=ptr[:]), reads=[ptr], writes=[hpT])
                S.op("pool", lambda e, s=s: e.tensor_copy(out=pbf[:], in_=pin[:, s, :]), reads=[pin], writes=[pbf])
                for k in range(2):
                    S.op("pe", lambda e, k=k: e.transpose(out=ptr[:, k, :], in_=pbf[:, k * 128:(k + 1) * 128], identity=ident[:]),
                         reads=[pbf, ident, hpT], writes=[ptr] if k == 0 else (), pwrites=() if k == 0 else [ptr])
                S.op("act", lambda e: e.copy(out=pTs[:], in_=ptr[:, 0:2, :]), reads=[ptr], writes=[pTs])
                xout = xo.next()
                for blk in range(2):
                    pgg = pg.next()
                    for k in range(8):
                        S.op("pe", lambda e, k=k, blk=blk, pgg=pgg: e.matmul(
                            pgg[:], lhsT=hpT[:, k, :], rhs=wpg[:, k, blk * 512:(blk + 1) * 512], start=(k == 0), stop=(k == 7)),
                            reads=[hpT, wpg], writes=[pgg] if k == 0 else (), pwrites=() if k == 0 else [pgg])
                    S.op("act", lambda e, blk=blk, pgg=pgg: e.activation(out=gate[:, blk * 512:(blk + 1) * 512], in_=pgg[:], func=AF.Sigmoid),
                         reads=[pgg], pwrites=[gate])
                    ppp = pg.next()
                    for k in range(2):
                        S.op("pe", lambda e, k=k, blk=blk, ppp=ppp: e.matmul(
                            ppp[:], lhsT=pTs[:, k, :], rhs=wpp[:, k, blk * 512:(blk + 1) * 512], start=(k == 0), stop=(k == 1)),
                            reads=[pTs, wpp], writes=[ppp] if k == 0 else (), pwrites=() if k == 0 else [ppp])
                    S.op("dve", lambda e, blk=blk, ppp=ppp: e.tensor_tensor(
                        out=gate[:, blk * 512:(blk + 1) * 512], in0=ppp[:], in1=gate[:, blk * 512:(blk + 1) * 512], op=ALU.mult),
                        reads=[ppp, gate], pwrites=[gate])
                    S.op("pool", lambda e, blk=blk, s=s, xout=xout: e.tensor_tensor(
                        out=xout[:, blk * 512:(blk + 1) * 512], in0=gate[:, blk * 512:(blk + 1) * 512],
                        in1=xt[:, s, blk * 512:(blk + 1) * 512], op=ALU.add),
                        reads=[gate, xt], writes=[xout] if blk == 0 else (), pwrites=() if blk == 0 else [xout])
                if final:
                    S.op("act", lambda e, xout=xout: e.activation(out=sq[:], in_=xout[:], func=AF.Square, accum_out=ss[:]),
                         reads=[xout], writes=[sq, ss])
                    S.op("act", lambda e: e.activation(out=rs[:], in_=ss[:], func=AF.Sqrt, scale=1.0 / D, bias=EPS),
                         reads=[ss], writes=[rs])
                    S.op("dve", lambda e: e.reciprocal(out=rs[:], in_=rs[:]), reads=[rs], writes=[rs])
                    S.op("dve", lambda e, xout=xout: e.scalar_tensor_tensor(out=xout[:], in0=xout[:], scalar=rs[:, 0:1], in1=gfin[:],
                                                                            op0=ALU.mult, op1=ALU.mult),
                         reads=[xout, rs, gfin], writes=[xout])
                S.dma("sp", x_dst.t[t0 + s * 128:t0 + (s + 1) * 128, :], xout[:], reads=[xout], pwrites=[x_dst], key=xout)
        _barrier(S)
        S.stack_pop()


def phase_D(S, nc, SEQ, lyr, Wd, scr):
    TP = 128
    C = 16
    NCH = TP // C
    GN_EPS = 64e-5
    LD = 0.6065306597126334
    with contextlib.ExitStack() as st:
        S.stack_push(st)
        identB = make_ident(S, "D_identB", BF16)
        ones = S.sb("D_ones", [128, 128])
        S.op("pool", lambda e: e.memset(ones[:], 0.0), writes=[ones])
        S.op("pool", lambda e: e.memset(ones[0:64, 0:64], 1.0), reads=[ones], writes=[ones])
        S.op("pool", lambda e: e.memset(ones[64:128, 64:128], 1.0), reads=[ones], writes=[ones])
        Ff = S.sb("D_F", [128, 64], BF16)
        S.op("pool", lambda e: e.tensor_tensor(out=Ff[:], in0=identB[:, 0:64], in1=identB[:, 64:128], op=ALU.add), reads=[identB], writes=[Ff])
        Sel = S.sb("D_Sel", [128, 16], BF16)
        S.op("pool", lambda e: e.tensor_tensor(out=Sel[:], in0=identB[:, 0:16], in1=identB[:, 16:32], op=ALU.add), reads=[identB], writes=[Sel])
        for hh in range(2, 8):
            S.op("pool", lambda e, hh=hh: e.tensor_tensor(out=Sel[:], in0=Sel[:], in1=identB[:, hh * 16:(hh + 1) * 16], op=ALU.add),
                 reads=[identB, Sel], writes=[Sel])
        maskF = S.sb("D_maskF", [128, 4, 8], BF16)
        S.op("pool", lambda e: e.memset(maskF[:], 0.0), writes=[maskF])
        for p in range(4):
            for h2 in range(2):
                S.op("pool", lambda e, p=p, h2=h2: e.memset(maskF[h2 * 64:(h2 + 1) * 64, p, 2 * p + h2:2 * p + h2 + 1], 1.0), reads=[maskF], writes=[maskF])
        maskZ = S.sb("D_maskZ", [128, 4, 2], BF16)
        S.op("pool", lambda e: e.memset(maskZ[:], 1.0), writes=[maskZ])
        S.op("pool", lambda e: e.affine_select(out=maskZ[:], in_=maskZ[:], pattern=[[-32, 4], [-16, 2]], compare_op=ALU.is_ge, fill=0.0,
                                               base=0, channel_multiplier=1), reads=[maskZ], writes=[maskZ])
        S.op("pool", lambda e: e.affine_select(out=maskZ[:], in_=maskZ[:], pattern=[[32, 4], [16, 2]], compare_op=ALU.is_ge, fill=0.0,
                                               base=15, channel_multiplier=-1), reads=[maskZ], writes=[maskZ])

        def trimask(name, pat, cm, op):
            m = S.sb(name, [128, 128], BF16)
            S.op("pool", lambda e: e.memset(m[:], 1.0), writes=[m])
            S.op("pool", lambda e: e.affine_select(out=m[:], in_=m[:], pattern=pat, compare_op=op, fill=0.0, base=0, channel_multiplier=cm),
                 reads=[m], writes=[m])
            return m
        mSL = trimask("D_mSL", [[-16, 8], [-1, 16]], 1, ALU.is_gt)
        mSU = trimask("D_mSU", [[16, 8], [1, 16]], -1, ALU.is_gt)
        mUI = trimask("D_mUI", [[16, 8], [1, 16]], -1, ALU.is_ge)
        rm = S.sb("D_rm", [128, 512])
        S.op("pool", lambda e: e.memset(rm[:], 1.0), writes=[rm])
        S.op("pool", lambda e: e.memset(rm[:, 0:512:16], 0.0), reads=[rm], writes=[rm])

        def cvec(name, key, n):
            t = S.sb("D_" + name, [128, n])
            S.dma("sp", t[:], Wd[key].t[lyr].rearrange("(c p) -> p c", p=128), reads=[Wd[key]], writes=[t],
                  allow_slow_non_contiguous=True)
            return t

        def cvec2(name, key):
            t = S.sb("D_" + name, [128, 4])
            S.dma("sp", t[:], Wd[key].t[lyr].rearrange("(c a) j -> (a j) c", a=2), reads=[Wd[key]], writes=[t],
                  allow_slow_non_contiguous=True)
            return t
        mu = cvec("mu", "rk_mu", 13)
        w0 = cvec("w0", "rk_w0", 4)
        a0 = cvec("a0", "rk_a0", 4)
        lg = cvec("lg", "rk_lnx_g", 4)
        lb = cvec("lb", "rk_lnx_b", 4)
        kkc = cvec2("kkc", "rk_kk")
        ka = cvec2("ka", "rk_ka")
        rkc = cvec2("rkc", "rk_rk")
        omka = S.sb("D_omka", [128, 4])
        S.op("pool", lambda e: e.tensor_scalar(out=omka[:], in0=ka[:], scalar1=-1.0, scalar2=1.0, op0=ALU.mult, op1=ALU.add),
             reads=[ka], writes=[omka])
        w2 = S.sb("D_w2", [64, 512], BF16)
        a2 = S.sb("D_a2", [128, 512], BF16)
        S.dma("pool", w2[:], Wd["rk_w2"].t[lyr], reads=[Wd["rk_w2"]], writes=[w2])
        S.dma("pool", a2[64:128, :], Wd["rk_a2"].t[lyr], reads=[Wd["rk_a2"]], writes=[a2])

        Hm = S.sb("D_H", [128, 4, 64])
        Hn = S.sb("D_Hn", [128, 4, 64])
        Hbf = S.sb("D_Hbf", [128, 4, 64], BF16)
        S.op("pool", lambda e: e.memset(Hm[:], 0.0), writes=[Hm])
        S.op("pool", lambda e: e.memset(Hbf[:], 0.0), writes=[Hbf])

        rst = S.sb("D_rst", [128, 13, TP + 1])
        xs = S.sb("D_xs", [128, 13, TP])
        th = S.sb("D_th", [128, TP], BF16)
        sg = S.sb("D_sg", [128, 4, TP])
        cum = S.sb("D_cum", [128, 4, TP])
        E1 = S.sb("D_E1", [128, 4, TP])
        E2 = S.sb("D_E2", [128, 4, TP])
        E3 = S.sb("D_E3", [128, 4, TP])
        aa = S.sb("D_aa", [128, 4, TP])
        kkf = S.sb("D_kkf", [128, 4, TP])
        sq = S.sb("D_sq", [128, 4, TP])
        rn = S.sb("D_rn", [128, 4, TP])
        kp = S.sb("D_kp", [128, 4, TP])
        t1 = S.sb("D_t1", [128, 4, TP])
        t2 = S.sb("D_t2", [128, 4, TP])
        comp = [S.sb("D_cmp%d" % i, [128, 4, TP], BF16) for i in range(5)]
        ZXr = Ring([[S.sb("D_Z%d_%d" % (b, i), [128, NCH, 4, 128], BF16) for i in range(5)] for b in range(2)])
        DcR = Ring([S.sb("D_Dc%d" % i, [128, NCH, 4]) for i in range(2)])
        bonR = Ring([S.sb("D_bon%d" % i, [128, 4, TP]) for i in range(2)])
        rzR = Ring([S.sb("D_rz%d" % i, [128, 4, TP], BF16) for i in range(2)])
        ybR = Ring([S.sb("D_yb%d" % i, [128, 4, TP]) for i in range(2)])
        yo = S.sb("D_yo", [128, 4, TP], BF16)
        ppre = S.ps("D_ppre", [128, 4, TP])
        R4 = lambda nm, shp, dt=BF16: Ring([S.sb("D_%s%d" % (nm, i), shp, dt) for i in range(4)])
        WyZr = R4("WyZ", [128, 4, 128]); WhTr = R4("WhT", [128, 4, 128]); BtZr = R4("BtZ", [128, 4, 128]); KtZr = R4("KtZ", [128, 4, 128])
        U0r = R4("U0", [128, 64]); Vtr = R4("Vt", [128, 64]); PTr = R4("PT", [128, 128]); QTr = R4("QT", [128, 128])
        ysbR = Ring([S.sb("D_ysb%d" % i, [128, 64]) for i in range(3)])

        class Reg:
            def __init__(self, bank, ap):
                self.bank = bank
                self.t = ap

        class Lane:
            pass
        lanes = []
        for li in range(2):
            L = Lane()
            L.Gr = Ring([S.sb("D_G%d_%d" % (li, i), [128, 128], BF16) for i in range(2)])
            L.Nr = Ring([S.sb("D_N%d_%d" % (li, i), [128, 128], BF16) for i in range(2)])
            L.NTr = Ring([S.sb("D_NT%d_%d" % (li, i), [128, 128], BF16) for i in range(2)])
            L.MTs = S.sb("D_MTs%d" % li, [128, 128], BF16)
            L.X1Z = S.sb("D_X1Z%d" % li, [128, 4, 128], BF16)
            L.X1s = S.sb("D_X1s%d" % li, [128, 64], BF16)
            L.tks = S.sb("D_tks%d" % li, [128, 2, 64], BF16)
            ba = S.ps("D_ba%d" % li, [128, 512])
            bb = ppre if li == 0 else S.ps("D_bb%d" % li, [128, 4, 128])
            bg = S.ps("D_bg%d" % li, [128, 3, 128])
            L.tokc = Reg(ba, ba.t[:, 0:256].rearrange("q (o j) -> q o j", o=4))
            L.QTp = Reg(ba, ba.t[:, 256:384])
            L.mvp = Reg(ba, ba.t[:, 384:448])
            L.sc = [Reg(bb, bb.t[:, i, :]) for i in range(4)]
            L.bb = bb
            L.bg = bg
            lanes.append(L)
        bs = S.ps("D_bs", [128, 512])
        bt_ = S.ps("D_bt", [128, 512])
        WHp = Reg(bs, bs.t[:, 0:256].rearrange("q (p i) -> q p i", p=4))
        Yp = Reg(bs, bs.t[:, 256:320])
        yfp = Reg(bt_, bt_.t[:, 0:64].rearrange("q (p t) -> q p t", p=4))
        yn = S.sb("D_yn", [128, 64])
        YZ = S.sb("D_YZ", [128, 4, 128], BF16)
        stats = S.sb("D_stats", [128, 6])
        mv = S.sb("D_mv", [128, 2])
        rstd = S.sb("D_rstd", [128, 1])

        bc4 = lambda t: t[:].unsqueeze(2).to_broadcast([128, 4, TP])

        def mm(out_ap, obuf, lhsT, lbuf, rhs, rbuf, start, stop=True, first_write=False):
            obuf = getattr(obuf, "bank", obuf)
            S.op("pe", lambda e: e.matmul(out_ap, lhsT=lhsT, rhs=rhs, start=start, stop=stop, skip_group_check=True),
                 reads=[lbuf, rbuf], writes=[obuf] if first_write else (), pwrites=() if first_write else [obuf])

        def prep(nb):
            t0 = nb * TP
            S.dma("sp", rst[:, :, 1:TP + 1], scr["rsT"].t[:, t0:t0 + TP].rearrange("(c p) t -> p c t", p=128), reads=[scr["rsT"]], writes=[rst])
            if nb == 0:
                S.op("pool", lambda e: e.memset(rst[:, :, 0:1], 0.0), reads=[rst], pwrites=[rst])
            else:
                S.dma("sp", rst[:, :, 0:1], scr["rsT"].t[:, t0 - 1:t0].rearrange("(c p) t -> p c t", p=128), reads=[scr["rsT"]],
                      pwrites=[rst], key=rst, allow_slow_non_contiguous=True)
            S.op("pool", lambda e: e.tensor_tensor(out=xs[:], in0=rst[:, :, 0:TP], in1=rst[:, :, 1:TP + 1], op=ALU.subtract), reads=[rst], writes=[xs])
            S.op("pool", lambda e: e.tensor_tensor(out=xs[:], in0=xs[:], in1=mu[:].unsqueeze(2).to_broadcast([128, 13, TP]), op=ALU.mult),
                 reads=[xs, mu], writes=[xs])
            S.op("pool", lambda e: e.tensor_tensor(out=xs[:], in0=xs[:], in1=rst[:, :, 1:TP + 1], op=ALU.add), reads=[xs, rst], writes=[xs])
            r = xs[:, 0:4, :]; k = xs[:, 4:8, :]; v = xs[:, 8:12, :]
            S.op("act", lambda e: e.activation(out=th[0:64, :], in_=xs[0:64, 12, :], func=AF.Tanh), reads=[xs], pwrites=[th])
            S.op("act", lambda e: e.copy(out=th[64:128, :], in_=xs[64:128, 12, :]), reads=[xs], pwrites=[th])
            for p in range(4):
                mm(ppre[:, p, :], ppre, w2[0:64, p * 128:(p + 1) * 128], w2, th[0:64, :], th, True, first_write=(p == 0))
            for p in range(4):
                S.op("act", lambda e, p=p: e.activation(out=sg[:, p, :], in_=ppre[:, p, :], func=AF.Sigmoid, bias=w0[:, p:p + 1]),
                     reads=[w0], writes=[ppre], pwrites=[sg])
            for p in range(4):
                mm(ppre[:, p, :], ppre, a2[64:128, p * 128:(p + 1) * 128], a2, th[64:128, :], th, True, first_write=(p == 0))
            for p in range(4):
                S.op("act", lambda e, p=p: e.activation(out=aa[:, p, :], in_=ppre[:, p, :], func=AF.Sigmoid, bias=a0[:, p:p + 1]),
                     reads=[a0], writes=[ppre], pwrites=[aa])
            S.op("dve", lambda e: e.tensor_tensor_scan(out=cum[:].rearrange("q p t -> q (p t)"), data0=rm[:],
                                                       data1=sg[:].rearrange("q p t -> q (p t)"), initial=0.0, op0=ALU.mult, op1=ALU.add),
                 reads=[rm, sg], writes=[cum])
            S.op("act", lambda e: e.activation(out=E1[:], in_=cum[:], func=AF.Exp, scale=-LD), reads=[cum], writes=[E1])
            S.op("act", lambda e: e.activation(out=E2[:], in_=cum[:], func=AF.Exp, scale=LD), reads=[cum], writes=[E2])
            S.op("pool", lambda e: e.tensor_tensor(out=t2[:], in0=cum[:], in1=sg[:], op=ALU.subtract), reads=[cum, sg], writes=[t2])
            S.op("act", lambda e: e.activation(out=E3[:], in_=t2[:], func=AF.Exp, scale=-LD), reads=[t2], writes=[E3])
            Dc = DcR.next()
            S.op("pool", lambda e, Dc=Dc: e.tensor_copy(out=Dc[:].rearrange("q c p -> q p c"), in_=E1[:, :, 15:TP:16]), reads=[E1], writes=[Dc])
            S.op("pool", lambda e: e.tensor_tensor(out=kkf[:], in0=k, in1=bc4(kkc), op=ALU.mult), reads=[xs, kkc], writes=[kkf])
            S.op("pool", lambda e: e.tensor_tensor(out=sq[:], in0=kkf[:], in1=kkf[:], op=ALU.mult), reads=[kkf], writes=[sq])
            for p in range(4):
                mm(ppre[:, p, :], ppre, ones[:], ones, sq[:, p, :], sq, True, first_write=(p == 0))
            S.op("act", lambda e: e.activation(out=rn[:], in_=ppre[:], func=AF.Sqrt), writes=[rn, ppre])
            S.op("pool", lambda e: e.tensor_scalar(out=rn[:], in0=rn[:], scalar1=1e-12, scalar2=None, op0=ALU.max), reads=[rn], writes=[rn])
            S.op("dve", lambda e: e.reciprocal(out=rn[:], in_=rn[:]), reads=[rn], writes=[rn])
            S.op("pool", lambda e: e.tensor_tensor(out=kkf[:], in0=kkf[:], in1=rn[:], op=ALU.mult), reads=[kkf, rn], writes=[kkf])
            S.op("pool", lambda e: e.tensor_tensor(out=t1[:], in0=aa[:], in1=bc4(ka), op=ALU.mult), reads=[aa, ka], writes=[t1])
            S.op("pool", lambda e: e.tensor_tensor(out=t1[:], in0=t1[:], in1=bc4(omka), op=ALU.add), reads=[t1, omka], writes=[t1])
            S.op("pool", lambda e: e.tensor_tensor(out=kp[:], in0=k, in1=t1[:], op=ALU.mult), reads=[xs, t1], writes=[kp])
            At, Bt, Kt, Rt, Vb = comp
            S.op("pool", lambda e: e.scalar_tensor_tensor(out=At[:], in0=kkf[:], scalar=-1.0, in1=E3[:], op0=ALU.mult, op1=ALU.mult)
                 if False else e.tensor_tensor(out=t2[:], in0=kkf[:], in1=E3[:], op=ALU.mult), reads=[kkf, E3], writes=[t2])
            S.op("pool", lambda e: e.tensor_scalar(out=At[:], in0=t2[:], scalar1=-1.0, scalar2=None, op0=ALU.mult), reads=[t2], writes=[At])
            S.op("pool", lambda e: e.tensor_tensor(out=t2[:], in0=kkf[:], in1=aa[:], op=ALU.mult), reads=[kkf, aa], writes=[t2])
            S.op("pool", lambda e: e.tensor_tensor(out=Bt[:], in0=t2[:], in1=E2[:], op=ALU.mult), reads=[t2, E2], writes=[Bt])
            S.op("pool", lambda e: e.tensor_tensor(out=Kt[:], in0=kp[:], in1=E2[:], op=ALU.mult), reads=[kp, E2], writes=[Kt])
            S.op("pool", lambda e: e.tensor_tensor(out=Rt[:], in0=r, in1=E1[:], op=ALU.mult), reads=[xs, E1], writes=[Rt])
            S.op("pool", lambda e: e.tensor_copy(out=Vb[:], in_=v), reads=[xs], writes=[Vb])
            S.op("pool", lambda e: e.tensor_tensor(out=t1[:], in0=r, in1=kp[:], op=ALU.mult), reads=[xs, kp], writes=[t1])
            S.op("pool", lambda e: e.tensor_tensor(out=sq[:], in0=t1[:], in1=bc4(rkc), op=ALU.mult), reads=[t1, rkc], writes=[sq])
            for p in range(4):
                mm(ppre[:, p, :], ppre, ones[:], ones, sq[:, p, :], sq, True, first_write=(p == 0))
            bon = bonR.next()
            S.op("act", lambda e, bon=bon: e.copy(out=bon[:], in_=ppre[:]), writes=[bon, ppre])
            S.op("pool", lambda e, bon=bon: e.tensor_tensor(out=bon[:], in0=bon[:], in1=v, op=ALU.mult), reads=[bon, xs], writes=[bon])
            rzt = rzR.next()
            S.dma("sp", rzt[:], scr["rzT"].t[:, t0:t0 + TP].rearrange("(c p) t -> p c t", p=128), reads=[scr["rzT"]], writes=[rzt])
            ZX = ZXr.next()
            for oi in range(5):
                for p in range(4):
                    S.op("pool", lambda e, oi=oi, p=p, ZX=ZX: e.tensor_tensor(
                        out=ZX[oi][:, :, p, :].rearrange("q c (h t) -> q c h t", t=16),
                        in0=comp[oi][:, p, :].rearrange("q (c t) -> q c t", t=16).unsqueeze(2).to_broadcast([128, NCH, 8, 16]),
                        in1=maskF[:, p, :].unsqueeze(1).unsqueeze(3).to_broadcast([128, NCH, 8, 16]), op=ALU.mult),
                        reads=[comp[oi], maskF], writes=[ZX[oi]] if p == 0 else (), pwrites=() if p == 0 else [ZX[oi]])
            return dict(ZX=ZX, Dc=Dc, bon=bon, rzt=rzt, yb=ybR.next(), t0=t0)


        def pre(bt, c, L, pc):
            ZA, ZB, ZK, ZR, ZV = bt["ZX"]
            BtZ = BtZr.next(); KtZ = KtZr.next(); U0 = U0r.next(); Vt = Vtr.next(); PTs = PTr.next(); QTs = QTr.next()
            WyZ = WyZr.next(); WhT = WhTr.next()
            pc.update(BtZ=BtZ, KtZ=KtZ, U0=U0, Vt=Vt, PTs=PTs, QTs=QTs, WyZ=WyZ, WhT=WhT, c=c, bt=bt)
            tokc = L.tokc
            first = True
            for oi, Z in enumerate((ZA, ZB, ZK, ZV)):
                for p in range(4):
                    mm(tokc.t[:, oi, :], tokc, Z[:, c, p, :], Z, Ff[:], Ff, first, first_write=first)
                    first = False
            yield
            G0 = L.Gr.next()
            tks = L.tks
            S.op("act", lambda e: e.copy(out=G0[:, 0:64], in_=tokc.t[:, 0, :]), writes=[G0, tokc.bank])
            mz = maskZ[:].unsqueeze(3).to_broadcast([128, 4, 2, 64])
            S.op("dve", lambda e: e.tensor_copy(out=tks[:], in_=tokc.t[:, 1:3, :]), writes=[tks, tokc.bank])
            S.op("act", lambda e: e.copy(out=Vt[:], in_=tokc.t[:, 3, :]), writes=[Vt, tokc.bank])
            S.op("pool", lambda e: e.tensor_tensor(out=BtZ[:].rearrange("q p (a j) -> q p a j", a=2),
                                                   in0=tks[:, 0, :].unsqueeze(1).unsqueeze(1).to_broadcast([128, 4, 2, 64]), in1=mz, op=ALU.mult),
                 reads=[tks, maskZ], writes=[BtZ])
            S.op("pool", lambda e: e.tensor_tensor(out=KtZ[:].rearrange("q p (a j) -> q p a j", a=2),
                                                   in0=tks[:, 1, :].unsqueeze(1).unsqueeze(1).to_broadcast([128, 4, 2, 64]), in1=mz, op=ALU.mult),
                 reads=[tks, maskZ], writes=[KtZ])
            yield
            N1 = L.Nr.next(); NT1 = L.NTr.next()
            specs = ((L.sc[0], ZA, ZB, mSL, N1), (L.sc[1], ZB, ZA, mSU, NT1), (L.sc[2], ZK, ZA, mSU, L.MTs), (L.sc[3], ZB, ZR, mUI, PTs))
            for gi, (pb, Lh, R_, msk, dst) in enumerate(specs):
                for p in range(4):
                    mm(pb.t, pb, Lh[:, c, p, :], Lh, R_[:, c, p, :], R_, p == 0, first_write=(gi == 0 and p == 0))
            for p in range(4):
                mm(L.QTp.t, L.QTp, ZK[:, c, p, :], ZK, ZR[:, c, p, :], ZR, p == 0, first_write=(p == 0))
            yield
            for gi, (pb, Lh, R_, msk, dst) in enumerate(specs):
                S.op("dve", lambda e, pb=pb, msk=msk, dst=dst: e.tensor_tensor(out=dst[:], in0=pb.t, in1=msk[:], op=ALU.mult),
                     reads=[msk], writes=[dst, pb.bank])
            S.op("dve", lambda e: e.tensor_tensor(out=QTs[:], in0=L.QTp.t, in1=mUI[:], op=ALU.mult), reads=[mUI], writes=[QTs, L.QTp.bank])
            yield
            mm(L.mvp.t, L.mvp, L.MTs[:], L.MTs, Vt[:], Vt, True, first_write=True)
            yield
            S.op("act", lambda e: e.copy(out=G0[:, 64:128], in_=L.mvp.t), writes=[L.mvp.bank], pwrites=[G0])
            yield
            G = G0; Nk = N1; NTk = NT1
            gb = L.bg
            for lev in range(4):
                mm(gb[:, 0, :], gb, identB[:], identB, G[:], G, True, stop=False, first_write=True)
                mm(gb[:, 0, :], gb, NTk[:], NTk, G[:], G, False)
                if lev < 3:
                    mm(gb[:, 1, :], gb, NTk[:], NTk, Nk[:], Nk, True)
                    mm(gb[:, 2, :], gb, Nk[:], Nk, NTk[:], NTk, True)
                    yield
                    G2 = L.Gr.next(); N2 = L.Nr.next(); NT2 = L.NTr.next()
                    S.op("act", lambda e, G2=G2: e.copy(out=G2[:], in_=gb[:, 0, :]), writes=[G2, gb])
                    S.op("dve", lambda e, N2=N2: e.tensor_copy(out=N2[:], in_=gb[:, 1, :]), writes=[N2, gb])
                    S.op("act", lambda e, NT2=NT2: e.copy(out=NT2[:], in_=gb[:, 2, :]), writes=[NT2, gb])
                    G = G2; Nk = N2; NTk = NT2
                    yield
                else:
                    yield
                    S.op("dve", lambda e: e.tensor_copy(out=L.X1s[:], in_=gb[:, 0, 0:64]), writes=[L.X1s, gb])
                    S.op("act", lambda e: e.copy(out=U0[:], in_=gb[:, 0, 64:128]), writes=[U0, gb])
                    S.op("pool", lambda e: e.tensor_tensor(out=L.X1Z[:].rearrange("q p (a j) -> q p a j", a=2),
                                                           in0=L.X1s[:].unsqueeze(1).unsqueeze(1).to_broadcast([128, 4, 2, 64]), in1=mz, op=ALU.mult),
                         reads=[L.X1s, maskZ], writes=[L.X1Z])
                    yield
            bb = L.bb
            for p in range(4):
                mm(bb[:, p, :], bb, identB[:], identB, ZR[:, c, p, :], ZR, p == 0, stop=False, first_write=(p == 0))
                mm(bb[:, p, :], bb, L.X1Z[:, p, :], L.X1Z, PTs[:], PTs, False)
            yield
            S.op("act", lambda e: e.copy(out=WyZ[:], in_=bb[:]), writes=[WyZ, bb])
            yield
            for p in range(4):
                mm(bb[:, p, :], bb, L.X1Z[:, p, :], L.X1Z, BtZ[:, p, :], BtZ, p == 0, first_write=(p == 0))
            yield
            S.op("dve", lambda e: e.tensor_copy(out=WhT[:], in_=bb[:]), writes=[WhT, bb])
            yield

        def state_stream(pc):
            c = pc["c"]; bt = pc["bt"]
            BtZ, KtZ, U0, Vt, PTs, QTs, WyZ, WhT = (pc[k] for k in ("BtZ", "KtZ", "U0", "Vt", "PTs", "QTs", "WyZ", "WhT"))
            for p in range(4):
                mm(WHp.t[:, p, :], WHp, BtZ[:, p, :], BtZ, U0[:], U0, p == 0, stop=False, first_write=(p == 0))
            for p in range(4):
                mm(WHp.t[:, p, :], WHp, KtZ[:, p, :], KtZ, Vt[:], Vt, False, stop=False)
            mm(Yp.t, Yp, PTs[:], PTs, U0[:], U0, False, stop=False)
            mm(Yp.t, Yp, QTs[:], QTs, Vt[:], Vt, False, stop=False)
            yield
            for p in range(4):
                mm(Yp.t, Yp, WyZ[:, p, :], WyZ, Hbf[:, p, :], Hbf, False, stop=(p == 3))
            for p in range(4):
                mm(WHp.t[:, p, :], WHp, WhT[:, p, :], WhT, Hbf[:, p, :], Hbf, False, stop=True)
            yield
            Dc = bt["Dc"]
            S.op("dve", lambda e: e.tensor_tensor(out=Hn[:], in0=WHp.t, in1=Hm[:], op=ALU.add), reads=[Hm], writes=[Hn, WHp.bank])
            S.op("dve", lambda e: e.tensor_tensor(out=Hm[:], in0=Hn[:], in1=Dc[:, c, :].unsqueeze(2).to_broadcast([128, 4, 64]), op=ALU.mult),
                 reads=[Hn, Dc], writes=[Hm])
            ysb = ysbR.next()
            pc["ysb"] = ysb
            S.op("act", lambda e: e.copy(out=ysb[:], in_=Yp.t), writes=[ysb, Yp.bank])
            S.op("act", lambda e: e.copy(out=Hbf[:], in_=Hm[:]), reads=[Hm], writes=[Hbf])
            yield

        def out_stream(pc):
            c = pc["c"]; bt = pc["bt"]; ysb = pc["ysb"]
            S.op("dve", lambda e: e.bn_stats(out=stats[:], in_=ysb[:]), reads=[ysb], writes=[stats])
            S.op("dve", lambda e: e.bn_aggr(out=mv[:], in_=stats[:]), reads=[stats], writes=[mv])
            yield
            S.op("act", lambda e: e.activation(out=rstd[:], in_=mv[:, 1:2], func=AF.Sqrt, bias=GN_EPS, scale=1.0), reads=[mv], writes=[rstd])
            yield
            S.op("dve", lambda e: e.reciprocal(out=rstd[:], in_=rstd[:]), reads=[rstd], writes=[rstd])
            S.op("dve", lambda e: e.tensor_scalar(out=yn[:], in0=ysb[:], scalar1=mv[:, 0:1], scalar2=rstd[:, 0:1], op0=ALU.subtract, op1=ALU.mult),
                 reads=[ysb, mv, rstd], writes=[yn])
            yield
            S.op("pool", lambda e: e.tensor_tensor(out=YZ[:].rearrange("q p (a j) -> q p a j", a=2),
                                                   in0=yn[:].unsqueeze(1).unsqueeze(1).to_broadcast([128, 4, 2, 64]),
                                                   in1=maskZ[:].unsqueeze(3).to_broadcast([128, 4, 2, 64]), op=ALU.mult),
                 reads=[yn, maskZ], writes=[YZ])
            yield
            for p in range(4):
                mm(yfp.t[:, p, :], yfp, YZ[:, p, :], YZ, Sel[:], Sel, True, first_write=(p == 0))
            yield
            yb = bt["yb"]
            S.op("act", lambda e: e.copy(out=yb[:, :, c * 16:(c + 1) * 16], in_=yfp.t), writes=([yb] if c == 0 else []) + [yfp.bank],
                 pwrites=() if c == 0 else [yb])
            if c == NCH - 1:
                post(bt)
            yield

        def post(bt):
            yb = bt["yb"]; bon = bt["bon"]; rzt = bt["rzt"]; t0 = bt["t0"]
            S.op("pool", lambda e: e.tensor_tensor(out=yb[:], in0=yb[:], in1=bc4(lg), op=ALU.mult), reads=[yb, lg], writes=[yb])
            S.op("pool", lambda e: e.tensor_tensor(out=yb[:], in0=yb[:], in1=bc4(lb), op=ALU.add), reads=[yb, lb], writes=[yb])
            S.op("pool", lambda e: e.tensor_tensor(out=yb[:], in0=yb[:], in1=bon[:], op=ALU.add), reads=[yb, bon], writes=[yb])
            S.op("pool", lambda e: e.tensor_tensor(out=yo[:], in0=yb[:], in1=rzt[:], op=ALU.mult), reads=[yb, rzt], writes=[yo])
            S.dma("sp", scr["ysT"].t[2, :, t0:t0 + TP].rearrange("(c p) t -> p c t", p=128), yo[:], reads=[yo], pwrites=[scr["ysT"]], key=yo)

        chunks = []
        for nb in range(SEQ // TP):
            for c in range(NCH):
                chunks.append((nb, c))
        bts = {}
        nxt = 0
        lane_gen = [None, None]
        lane_pc = [None, None]
        done_order = {}
        next_state = 0
        state_gen = None; state_pc = None
        out_q = []; out_gen = None
        n_total = len(chunks)
        finished_out = 0
        pcs = {}
        while finished_out < n_total:
            for li in range(2):
                if lane_gen[li] is None and nxt < n_total and nxt - next_state < 3:
                    nb, c = chunks[nxt]
                    if nb not in bts:
                        bts[nb] = prep(nb)
                    pc = {"idx": nxt}
                    pcs[nxt] = pc
                    lane_gen[li] = pre(bts[nb], c, lanes[li], pc)
                    lane_pc[li] = pc
                    nxt += 1
                if lane_gen[li] is not None:
                    try:
                        next(lane_gen[li])
                    except StopIteration:
                        done_order[lane_pc[li]["idx"]] = True
                        lane_gen[li] = None
            if state_gen is None and done_order.get(next_state):
                state_pc = pcs[next_state]
                state_gen = state_stream(state_pc)
            if state_gen is not None:
                try:
                    next(state_gen)
                except StopIteration:
                    out_q.append(state_pc)
                    state_gen = None
                    next_state += 1
            if out_gen is None and out_q:
                out_gen = out_stream(out_q.pop(0))
            if out_gen is not None:
                try:
                    next(out_gen)
                except StopIteration:
                    out_gen = None
                    finished_out += 1
        _barrier(S)
        S.stack_pop()


WSPEC = {
    "norm_g": [2, 1024], "w_in": [2, 1024, 9112], "cmp_w1": [2, 2, 32, 64, 128], "cmp_w2": [2, 2, 128, 64],
    "cmp_pe": [2, 2, 32, 64], "sg_ln_g": [2, 512], "sg_ln_b": [2, 512], "sg_w": [2, 8, 128, 128], "sg_b": [2, 8, 128],
    "rk_mu": [2, 1664], "rk_w0": [2, 512], "rk_w2": [2, 64, 512], "rk_a0": [2, 512], "rk_a2": [2, 64, 512],
    "rk_kk": [2, 8, 64], "rk_ka": [2, 8, 64], "rk_rk": [2, 8, 64], "rk_lnx_g": [2, 512], "rk_lnx_b": [2, 512],
    "w_branch": [2, 3, 512, 1024], "w_o": [2, 1024, 1024], "ple_norm_g": [2, 1024], "w_ple_gate": [2, 1024, 1024],
    "w_ple_proj": [2, 256, 1024], "final_norm_g": [1, 1024],
}


def build(SEQ, nlayers=2, enable=(1, 1, 1), scr_kind="Internal"):
    nc = bass.Bass("TRN2", target_bir_lowering=False)
    with contextlib.ExitStack() as stack:
        S = Sched(nc, stack)
        x = Buf("x", nc.dram_tensor("x", [SEQ, D], F32, kind="ExternalInput").ap())
        Wd = {"p": Buf("p", nc.dram_tensor("p", [2, SEQ, PLE], F32, kind="ExternalInput").ap())}
        for k, shp in WSPEC.items():
            Wd[k] = Buf(k, nc.dram_tensor(k, shp, F32, kind="ExternalInput").ap())
        out = Buf("out", nc.dram_tensor("out", [SEQ, D], F32, kind="ExternalOutput").ap())
        scr = make_scratch(S, SEQ, kind=scr_kind)
        xmid = S.dram("xmid", [SEQ, D], F32, kind=scr_kind)
        cur = x
        for lyr in range(nlayers):
            last = lyr == nlayers - 1
            dst = out if last else xmid
            phase_A(S, nc, SEQ, lyr, cur, Wd, scr)
            if enable[0]:
                phase_B(S, nc, SEQ, lyr, Wd, scr)
            if enable[1]:
                phase_C(S, nc, SEQ, lyr, Wd, scr)
            if enable[2]:
                phase_D(S, nc, SEQ, lyr, Wd, scr)
            phase_E(S, nc, SEQ, lyr, cur, dst, Wd, scr, final=(last and nlayers == 2))
            cur = dst
        S.emit()
    return nc


def phase_B(S, nc, SEQ, lyr, Wd, scr):
    NC = (SEQ - 32) // 16 + 1
    NT = (NC + 127) // 128
    NCp = NT * 128
    KT = SEQ // 128
    with contextlib.ExitStack() as st:
        S.stack_push(st)
        ident = make_ident(S, "B_ident")
        ksT = S.sb("B_ksT", [64, 2, SEQ], BF16)
        kwT = S.sb("B_kwT", [64, 2, SEQ], BF16)
        vs = S.sb("B_vs", [128, KT, 2, 65], BF16)
        vw = S.sb("B_vw", [128, KT, 2, 65], BF16)
        kcmpT = S.sb("B_kcmpT", [64, 2, NCp], BF16)
        Rc = S.sb("B_Rc", [128, NT, 2, 193], BF16)
        EXW = S.sb("B_EXW", [128, SEQ], BF16)
        S.dma("sp", ksT[:], scr["ksT"].t.rearrange("(g d) t -> d g t", g=2), reads=[scr["ksT"]], writes=[ksT])
        S.dma("sp", kwT[:], scr["kwT"].t.rearrange("(g d) t -> d g t", g=2), reads=[scr["kwT"]], writes=[kwT])
        S.op("pool", lambda e: e.memset(vs[:], 1.0), writes=[vs])
        S.op("pool", lambda e: e.memset(vw[:], 1.0), writes=[vw])
        for k0 in range(0, KT, 8):
            k1 = min(KT, k0 + 8)
            for (dst, c0) in ((vs, 0), (vw, 128)):
                for g in range(2):
                    S.dma("sp", dst[:, k0:k1, g, 0:64],
                          scr["vsw"].t[k0 * 128:k1 * 128, c0 + g * 64:c0 + (g + 1) * 64].rearrange("(k p) d -> p k d", p=128),
                          reads=[scr["vsw"]], pwrites=[dst], key=dst)
        S.op("pool", lambda e: e.memset(EXW[:], 1.0), writes=[EXW])
        S.op("pool", lambda e: e.affine_select(out=EXW[:], in_=EXW[:], pattern=[[1, SEQ]], compare_op=ALU.is_ge, fill=0.0,
                                               base=0, channel_multiplier=-64), reads=[EXW], writes=[EXW])
        S.op("pool", lambda e: e.affine_select(out=EXW[:], in_=EXW[:], pattern=[[-1, SEQ]], compare_op=ALU.is_ge, fill=0.0,
                                               base=63, channel_multiplier=64), reads=[EXW], writes=[EXW])
        S.op("pool", lambda e: e.memset(Rc[:], 1.0), writes=[Rc])
        for nt in range(NT):
            for g in range(2):
                S.op("pool", lambda e, nt=nt, g=g: e.affine_select(
                    out=Rc[:, nt, g, 65:193], in_=Rc[:, nt, g, 65:193], pattern=[[-4, 128]], compare_op=ALU.is_ge, fill=0.0,
                    base=nt * 128 + 1, channel_multiplier=1), reads=[Rc], writes=[Rc])
                S.op("pool", lambda e, nt=nt, g=g: e.affine_select(
                    out=Rc[:, nt, g, 65:193], in_=Rc[:, nt, g, 65:193], pattern=[[4, 128]], compare_op=ALU.is_ge, fill=0.0,
                    base=3 - nt * 128, channel_multiplier=-1), reads=[Rc], writes=[Rc])
        npad = NCp - NC
        if npad:
            S.op("pool", lambda e: e.affine_select(
                out=Rc[:, NT - 1, :, :], in_=Rc[:, NT - 1, :, :], pattern=[[0, 2 * 193]], compare_op=ALU.is_ge, fill=0.0,
                base=(NC - 1) - (NT - 1) * 128, channel_multiplier=-1), reads=[Rc], writes=[Rc])
        S.op("pool", lambda e: e.memset(kcmpT[:], 0.0), writes=[kcmpT])

        with contextlib.ExitStack() as st2:
            S.stack_push(st2)
            kvT = S.sb("B_kvT", [64, 2, SEQ], BF16)
            w1 = S.sb("B_w1", [64, 32, 128], BF16)
            w2 = S.sb("B_w2", [128, 64], BF16)
            peT = S.sb("B_peT", [64, 32])
            peTb = S.sb("B_peTb", [64, 32], BF16)
            cb = S.sb("B_cb", [128, 1])
            hid = S.sb("B_hid", [128, NCp], BF16)
            ph = S.ps("B_ph", [128, 512])
            pc1 = S.ps("B_pc1", [128, 512])
            pk = S.ps("B_pk", [128, 512])
            for kv in range(2):
                src = scr["kcT"] if kv == 0 else scr["vcT"]
                S.dma("sp", kvT[:], src.t.rearrange("(g d) t -> d g t", g=2), reads=[src], writes=[kvT])
                S.dma("pool", w1[:], Wd["cmp_w1"].t[lyr, kv].rearrange("l d h -> d l h"), reads=[Wd["cmp_w1"]], writes=[w1])
                S.dma("pool", w2[:], Wd["cmp_w2"].t[lyr, kv], reads=[Wd["cmp_w2"]], writes=[w2])
                S.dma("sp", peT[:], Wd["cmp_pe"].t[lyr, kv].rearrange("l d -> d l"), reads=[Wd["cmp_pe"]], writes=[peT],
                      allow_slow_non_contiguous=True)
                S.op("dve", lambda e: e.tensor_copy(out=peTb[:], in_=peT[:]), reads=[peT], writes=[peTb])
                for l in range(32):
                    S.op("pe", lambda e, l=l: e.matmul(pc1[:, 0:1], lhsT=w1[:, l, :], rhs=peTb[:, l:l + 1], start=(l == 0), stop=(l == 31)),
                         reads=[w1, peTb], writes=[pc1] if l == 0 else (), pwrites=() if l == 0 else [pc1])
                S.op("dve", lambda e: e.tensor_copy(out=cb[:], in_=pc1[:, 0:1]), reads=[pc1], writes=[cb])
                for g in range(2):
                    S.op("dve", lambda e: e.memset(hid[:], 0.0), writes=[hid])
                    for n0 in range(0, NC, 512):
                        nn = min(512, NC - n0)
                        for l in range(32):
                            S.op("pe", lambda e, l=l, g=g, n0=n0, nn=nn: e.matmul(
                                ph[:, 0:nn], lhsT=w1[:, l, :], rhs=kvT[:, g, n0 * 16 + l: n0 * 16 + l + (nn - 1) * 16 + 1: 16], start=(l == 0), stop=(l == 31)),
                                reads=[w1, kvT], writes=[ph] if l == 0 else (), pwrites=() if l == 0 else [ph])
                        S.op("act", lambda e, n0=n0, nn=nn: e.activation(out=hid[:, n0:n0 + nn], in_=ph[:, 0:nn], func=AF.Silu, bias=cb[:, 0:1]),
                             reads=[ph, cb], pwrites=[hid])
                    if kv == 0:
                        for n0 in range(0, NC, 512):
                            nn = min(512, NC - n0)
                            S.op("pe", lambda e, n0=n0, nn=nn: e.matmul(pk[0:64, 0:nn], lhsT=w2[:], rhs=hid[:, n0:n0 + nn], start=True, stop=True),
                                 reads=[w2, hid], writes=[pk])
                            S.op("dve", lambda e, g=g, n0=n0, nn=nn: e.tensor_copy(out=kcmpT[:, g, n0:n0 + nn], in_=pk[0:64, 0:nn]),
                                 reads=[pk], pwrites=[kcmpT])
                    else:
                        for nt in range(NT):
                            rows = min(128, NC - nt * 128)
                            S.op("pe", lambda e, nt=nt: e.matmul(pk[:, 0:64], lhsT=hid[:, nt * 128:(nt + 1) * 128], rhs=w2[:], start=True, stop=True),
                                 reads=[w2, hid], writes=[pk])
                            S.op("dve", lambda e, g=g, nt=nt: e.tensor_copy(out=Rc[:, nt, g, 0:64], in_=pk[:, 0:64]),
                                 reads=[pk], pwrites=[Rc])
            _barrier(S)
            S.stack_pop()

        qt = Ring([S.sb("B_q%d" % i, [64, 8, 128], BF16) for i in range(2)])
        gt = Ring([S.sb("B_g%d" % i, [128, 24]) for i in range(2)])
        nzt = Ring([S.sb("B_nz%d" % i, [128, 512], BF16) for i in range(2)])
        Et = Ring([S.sb("B_E%d" % i, [128, 512], BF16) for i in range(4)])
        psT = Ring([S.ps("B_psT%d" % i, [128, 512]) for i in range(3)])
        pcA = S.ps("B_pcA", [128, 2, 193])
        pcB = S.ps("B_pcB", [128, 2, 193])
        pos = S.ps("B_pos", [128, 4, 65])
        pow_ = S.ps("B_pow", [128, 4, 65])
        pmisc = S.ps("B_pmisc", [128, 4, 128], BF16)
        oc = S.sb("B_oc", [128, 4, 193])
        rcs = S.sb("B_rcs", [128, 4])
        rss = S.sb("B_rss", [128, 4])
        rws = S.sb("B_rws", [128, 4])
        cc = S.sb("B_cc", [128, 3, 4])
        sc = S.sb("B_sc", [128, 128])
        sc2 = S.sb("B_sc2", [128, 128])
        m1 = S.sb("B_m1", [128, 8])
        m2 = S.sb("B_m2", [128, 8])
        negq = S.sb("B_negq", [128, 128], BF16)
        negT4 = S.sb("B_negT4", [128, 4, 128], BF16)
        yg = S.sb("B_yg", [128, 4, 64])
        ytmp = S.sb("B_ytmp", [128, 4, 64])
        ynsa = S.sb("B_ynsa", [128, 512], BF16)
        stg = Ring([S.sb("B_stg%d" % i, [128, 4, 128], BF16) for i in range(2)])

        def qk_exp(kT_ap, kbuf, q_ap, qbuf, neg_lhsT=None):
            p = psT.next()
            if neg_lhsT is not None:
                S.op("pe", lambda e, p=p: e.matmul(p[:], lhsT=neg_lhsT, rhs=negT4[:].rearrange("p h q -> p (h q)"), start=True, stop=False),
                     reads=[EXW, negT4], writes=[p])
                S.op("pe", lambda e, p=p: e.matmul(p[:], lhsT=kT_ap, rhs=q_ap, start=False, stop=True), reads=[kbuf, qbuf], pwrites=[p])
            else:
                S.op("pe", lambda e, p=p: e.matmul(p[:], lhsT=kT_ap, rhs=q_ap, start=True, stop=True), reads=[kbuf, qbuf], writes=[p])
            E = Et.next()
            S.op("act", lambda e, p=p, E=E: e.activation(out=E[:], in_=p[:], func=AF.Exp), reads=[p], writes=[E])
            return E

        def pipeline(tiles, L=2):
            Es = {}
            n = len(tiles)
            for i in range(n + L):
                if i < n:
                    Es[i] = tiles[i][0]()
                if i - L >= 0:
                    tiles[i - L][1](Es.pop(i - L))

        def mask(E, base, cm, qstep):
            S.op("pool", lambda e, E=E: e.affine_select(out=E[:], in_=E[:], pattern=[[0, 4], [qstep, 128]], compare_op=ALU.is_ge,
                                                       fill=0.0, base=base, channel_multiplier=cm), reads=[E], writes=[E])

        for qb in range(SEQ // 128):
            q0 = qb * 128
            q = qt.next(); gg = gt.next(); nz = nzt.next()
            S.dma("sp", q[:], scr["qT"].t[:, q0:q0 + 128].rearrange("(h d) t -> d h t", h=8), reads=[scr["qT"]], writes=[q])
            S.dma("sp", gg[:], scr["gate"].t[q0:q0 + 128, :], reads=[scr["gate"]], writes=[gg])
            S.dma("sp", nz[:], scr["nzs"].t[q0:q0 + 128, :], reads=[scr["nzs"]], writes=[nz])
            for g in range(2):
                q_ap = q[:, 4 * g:4 * g + 4, :].rearrange("d h q -> d (h q)")
                n_max = min(8 * qb + 6, NC - 1)
                ntl = n_max // 128 + 1
                def c_qk(nt, g=g, q_ap=q_ap, q=q):
                    E = qk_exp(kcmpT[:, g, nt * 128:(nt + 1) * 128], kcmpT, q_ap, q)
                    if q0 - 16 * (128 * nt + 127) - 31 < 0:
                        mask(E, q0 - 16 * 128 * nt - 31, -16, 1)
                    return E

                def c_pv(nt, E, g=g, ntl=ntl):
                    for h in range(4):
                        pcx = pcA if h < 2 else pcB
                        first = (nt == 0 and h % 2 == 0)
                        S.op("pe", lambda e, E=E, h=h, pcx=pcx, nt=nt, first=first, g=g, ntl=ntl: e.matmul(
                            pcx[:, h % 2, :], lhsT=E[:, h * 128:(h + 1) * 128], rhs=Rc[:, nt, g, :], start=first,
                            stop=(nt == ntl - 1 and h % 2 == 1), skip_group_check=True),
                            reads=[E, Rc], writes=[pcx] if first else (), pwrites=() if first else [pcx])
                pipeline([(lambda nt=nt: c_qk(nt), lambda E, nt=nt: c_pv(nt, E)) for nt in range(ntl)])
                S.op("act", lambda e: e.copy(out=oc[:, 0:2, :], in_=pcA[:]), reads=[pcA], pwrites=[oc])
                S.op("act", lambda e: e.copy(out=oc[:, 2:4, :], in_=pcB[:]), reads=[pcB], pwrites=[oc])
                S.op("dve", lambda e: e.tensor_scalar(out=rcs[:], in0=oc[:, :, 64], scalar1=1e-30, scalar2=None, op0=ALU.max),
                     reads=[oc], writes=[rcs])
                S.op("dve", lambda e: e.reciprocal(out=rcs[:], in_=rcs[:]), reads=[rcs], writes=[rcs])
                S.op("dve", lambda e: e.tensor_scalar(out=sc[:], in0=oc[:, 0, 65:193], scalar1=rcs[:, 0:1], scalar2=None, op0=ALU.mult),
                     reads=[oc, rcs], writes=[sc])
                for h in range(1, 4):
                    S.op("dve", lambda e, h=h: e.scalar_tensor_tensor(out=sc[:], in0=oc[:, h, 65:193], scalar=rcs[:, h:h + 1], in1=sc[:],
                                                                      op0=ALU.mult, op1=ALU.add), reads=[oc, rcs, sc], writes=[sc])
                for half in range(2):
                    tb = 2 * qb + half
                    ps_ = slice(half * 64, (half + 1) * 64)
                    if tb + 1 < 128:
                        S.op("dve", lambda e, ps_=ps_, tb=tb: e.memset(sc[ps_, tb + 1:128], -1e4), reads=[sc], writes=[sc])
                    lo = max(tb - 1, 0)
                    S.op("dve", lambda e, ps_=ps_, tb=tb, lo=lo: e.memset(sc[ps_, lo:tb + 1], 1e4), reads=[sc], writes=[sc])
                S.op("dve", lambda e: e.memset(sc[:, 0:1], 1e4), reads=[sc], writes=[sc])
                S.op("dve", lambda e: e.max(out=m1[:], in_=sc[:]), reads=[sc], writes=[m1])
                S.op("dve", lambda e: e.match_replace(out=sc2[:], in_to_replace=m1[:], in_values=sc[:], imm_value=-3e4),
                     reads=[sc, m1], writes=[sc2])
                S.op("dve", lambda e: e.max(out=m2[:], in_=sc2[:]), reads=[sc2], writes=[m2])
                S.op("dve", lambda e: e.tensor_scalar(out=negq[:], in0=sc[:], scalar1=m2[:, 7:8], scalar2=-1e4, op0=ALU.is_lt, op1=ALU.mult),
                     reads=[sc, m2], writes=[negq])
                S.op("pe", lambda e: e.transpose(out=pmisc[:, 0, :], in_=negq[:], identity=ident[:]), reads=[negq, ident], writes=[pmisc])
                S.op("dve", lambda e: e.tensor_copy(out=negT4[:], in_=pmisc[:, 0:1, :].to_broadcast([128, 4, 128])),
                     reads=[pmisc], writes=[negT4])
                kts = list(range(max(0, qb - 4), qb + 1))

                def w_qk(i, kt, g=g, q_ap=q_ap, q=q, qb=qb):
                    E = qk_exp(kwT[:, g, kt * 128:(kt + 1) * 128], kwT, q_ap, q)
                    if kt == qb - 4:
                        mask(E, -1, 1, -1)
                    if kt == qb:
                        mask(E, 0, -1, 1)
                    return E

                def w_pv(i, kt, E, g=g, kts=kts):
                    for h in range(4):
                        first = (i == 0 and h == 0)
                        S.op("pe", lambda e, E=E, h=h, kt=kt, first=first, last=(i == len(kts) - 1 and h == 3), g=g: e.matmul(
                            pow_[:, h, :], lhsT=E[:, h * 128:(h + 1) * 128], rhs=vw[:, kt, g, :], start=first, stop=last,
                            skip_group_check=True),
                            reads=[E, vw], writes=[pow_] if first else (), pwrites=() if first else [pow_])
                def s_qk(kt, g=g, q_ap=q_ap, q=q, qb=qb):
                    E = qk_exp(ksT[:, g, kt * 128:(kt + 1) * 128], ksT, q_ap, q, neg_lhsT=EXW[:, kt * 128:(kt + 1) * 128])
                    if kt == qb:
                        mask(E, 0, -1, 1)
                    return E

                def s_pv(kt, E, g=g, qb=qb):
                    for h in range(4):
                        first = (kt == 0 and h == 0)
                        S.op("pe", lambda e, E=E, h=h, kt=kt, first=first, last=(kt == qb and h == 3), g=g: e.matmul(
                            pos[:, h, :], lhsT=E[:, h * 128:(h + 1) * 128], rhs=vs[:, kt, g, :], start=first, stop=last,
                            skip_group_check=True),
                            reads=[E, vs], writes=[pos] if first else (), pwrites=() if first else [pos])
                pipeline([(lambda i=i, kt=kt: w_qk(i, kt), lambda E, i=i, kt=kt: w_pv(i, kt, E)) for i, kt in enumerate(kts)] +
                         [(lambda kt=kt: s_qk(kt), lambda E, kt=kt: s_pv(kt, E)) for kt in range(qb + 1)])
                S.op("dve", lambda e: e.reciprocal(out=rss[:], in_=pos[:, :, 64]), reads=[pos], writes=[rss])
                S.op("dve", lambda e: e.reciprocal(out=rws[:], in_=pow_[:, :, 64]), reads=[pow_], writes=[rws])
                gv = gg[:, g * 12:(g + 1) * 12].rearrange("p (h b) -> p b h", b=3)
                for b, rr in enumerate((rcs, rss, rws)):
                    S.op("dve", lambda e, b=b, rr=rr, gv=gv: e.tensor_tensor(out=cc[:, b, :], in0=gv[:, b, :], in1=rr[:], op=ALU.mult),
                         reads=[gg, rr], pwrites=[cc])
                if qb == 2:
                    dbg_dump(S, "oc%d" % g, oc[:], oc, [128, 4, 193])
                    dbg_dump(S, "pos%d" % g, pos[:], pos, [128, 4, 65])
                    dbg_dump(S, "pow%d" % g, pow_[:], pow_, [128, 4, 65])
                    dbg_dump(S, "cc%d" % g, cc[:], cc, [128, 3, 4])
                    dbg_dump(S, "gg%d" % g, gg[:], gg, [128, 24])
                    dbg_dump(S, "sc%d" % g, sc[:], sc, [128, 128])
                    dbg_dump(S, "negq%d" % g, negq[:], negq, [128, 128])
                bc = lambda b: cc[:, b, :].unsqueeze(2).to_broadcast([128, 4, 64])
                S.op("dve", lambda e: e.tensor_tensor(out=yg[:], in0=oc[:, :, 0:64], in1=bc(0), op=ALU.mult), reads=[oc, cc], writes=[yg])
                S.op("dve", lambda e: e.tensor_tensor(out=ytmp[:], in0=pos[:, :, 0:64], in1=bc(1), op=ALU.mult), reads=[pos, cc], writes=[ytmp])
                S.op("pool", lambda e: e.tensor_tensor(out=yg[:], in0=yg[:], in1=ytmp[:], op=ALU.add), reads=[yg, ytmp], writes=[yg])
                S.op("dve", lambda e: e.tensor_tensor(out=ytmp[:], in0=pow_[:, :, 0:64], in1=bc(2), op=ALU.mult), reads=[pow_, cc], writes=[ytmp])
                S.op("pool", lambda e: e.tensor_tensor(out=yg[:], in0=yg[:], in1=ytmp[:], op=ALU.add), reads=[yg, ytmp], writes=[yg])
                if qb == 2:
                    dbg_dump(S, "yg%d" % g, yg[:], yg, [128, 4, 64])
                S.op("pool", lambda e, g=g, nz=nz: e.tensor_tensor(out=ynsa[:, g * 256:(g + 1) * 256], in0=yg[:].rearrange("p h d -> p (h d)"),
                                                                   in1=nz[:, g * 256:(g + 1) * 256], op=ALU.mult),
                     reads=[yg, nz], pwrites=[ynsa])
            for k in range(4):
                S.op("pe", lambda e, k=k: e.transpose(out=pmisc[:, k, :], in_=ynsa[:, k * 128:(k + 1) * 128], identity=ident[:]),
                     reads=[ynsa, ident], writes=[pmisc] if k == 0 else (), pwrites=() if k == 0 else [pmisc])
            sg = stg.next()
            S.op("act", lambda e, sg=sg: e.copy(out=sg[:], in_=pmisc[:]), reads=[pmisc], writes=[sg])
            S.dma("sp", scr["ysT"].t[0, :, q0:q0 + 128].rearrange("(k p) t -> p k t", p=128), sg[:], reads=[sg], pwrites=[scr["ysT"]], key=sg)
        _barrier(S)
        S.stack_pop()


def phase_D_seq(S, nc, SEQ, lyr, Wd, scr):
    TP = 128
    TB = 8
    GN_EPS = 64e-5
    xtok = scr["xtok"]
    with contextlib.ExitStack() as st:
        S.stack_push(st)
        identF = make_ident(S, "D_ident", F32)
        ones = S.sb("D_ones", [128, 128])
        S.op("pool", lambda e: e.memset(ones[:], 0.0), writes=[ones])
        S.op("pool", lambda e: e.memset(ones[0:64, 0:64], 1.0), reads=[ones], writes=[ones])
        S.op("pool", lambda e: e.memset(ones[64:128, 64:128], 1.0), reads=[ones], writes=[ones])

        def cvec(name, key, n):
            t = S.sb("D_" + name, [128, n])
            S.dma("sp", t[:], Wd[key].t[lyr].rearrange("(c p) -> p c", p=128), reads=[Wd[key]], writes=[t],
                  allow_slow_non_contiguous=True)
            return t

        def cvec2(name, key):
            t = S.sb("D_" + name, [128, 4])
            S.dma("sp", t[:], Wd[key].t[lyr].rearrange("(c a) j -> (a j) c", a=2), reads=[Wd[key]], writes=[t],
                  allow_slow_non_contiguous=True)
            return t
        mu = cvec("mu", "rk_mu", 13)
        w0 = cvec("w0", "rk_w0", 4)
        a0 = cvec("a0", "rk_a0", 4)
        lg = cvec("lg", "rk_lnx_g", 4)
        lb = cvec("lb", "rk_lnx_b", 4)
        kkc = cvec2("kkc", "rk_kk")
        ka = cvec2("ka", "rk_ka")
        rkc = cvec2("rkc", "rk_rk")
        omka = S.sb("D_omka", [128, 4])
        S.op("pool", lambda e: e.tensor_scalar(out=omka[:], in0=ka[:], scalar1=-1.0, scalar2=1.0, op0=ALU.mult, op1=ALU.add),
             reads=[ka], writes=[omka])
        w2 = S.sb("D_w2", [64, 512], BF16)
        a2 = S.sb("D_a2", [128, 512], BF16)
        S.dma("pool", w2[:], Wd["rk_w2"].t[lyr], reads=[Wd["rk_w2"]], writes=[w2])
        S.dma("pool", a2[64:128, :], Wd["rk_a2"].t[lyr], reads=[Wd["rk_a2"]], writes=[a2])
        St = S.sb("D_state", [128, 4, 64])
        S.op("dve", lambda e: e.memset(St[:], 0.0), writes=[St])

        rst = S.sb("D_rst", [128, 13, TP + 1])
        xs = S.sb("D_xs", [128, 13, TP])
        th = S.sb("D_th", [128, TP], BF16)
        dd = S.sb("D_dd", [128, 4, TP])
        aa = S.sb("D_aa", [128, 4, TP])
        kkf = S.sb("D_kkf", [128, 4, TP])
        sq = S.sb("D_sq", [128, 4, TP])
        rn = S.sb("D_rn", [128, 4, TP])
        kp = S.sb("D_kp", [128, 4, TP])
        am = S.sb("D_am", [128, 4, TP])
        bm = S.sb("D_bm", [128, 4, TP])
        t1 = S.sb("D_t1", [128, 4, TP])
        bonus = S.sb("D_bonus", [128, 4, TP])
        vv = S.sb("D_vv", [128, 4, TP])
        tk = S.sb("D_tk", [128, 5, 4, 128])
        bcr = Ring([S.sb("D_bc%d" % i, [128, TB, 5, 256]) for i in range(2)])
        tmp = S.sb("D_tmp", [128, 4, 64])
        tmp2 = S.sb("D_tmp2", [128, 4, 64])
        kv = Ring([S.sb("D_kv%d" % i, [128, 4, 64]) for i in range(2)])
        sa = S.sb("D_sa", [128, 4])
        ybuf = S.sb("D_y", [128, 4, TP])
        ysq = S.sb("D_ysq", [128, 4, TP])
        mean = S.sb("D_mean", [128, 4, TP])
        var = S.sb("D_var", [128, 4, TP])
        rzt = S.sb("D_rz", [128, 4, TP], BF16)
        yo = S.sb("D_yo", [128, 4, TP], BF16)
        pa = Ring([S.ps("D_pa%d" % i, [128, 4, 128]) for i in range(4)])

        bc4 = lambda t: t[:].unsqueeze(2).to_broadcast([128, 4, TP])
        for nb in range(SEQ // TP):
            t0 = nb * TP
            S.dma("sp", rst[:, :, 1:TP + 1], scr["rsT"].t[:, t0:t0 + TP].rearrange("(c p) t -> p c t", p=128), reads=[scr["rsT"]],
                  writes=[rst])
            if nb == 0:
                S.op("pool", lambda e: e.memset(rst[:, :, 0:1], 0.0), reads=[rst], pwrites=[rst])
            else:
                S.dma("sp", rst[:, :, 0:1], scr["rsT"].t[:, t0 - 1:t0].rearrange("(c p) t -> p c t", p=128), reads=[scr["rsT"]],
                      pwrites=[rst], key=rst, allow_slow_non_contiguous=True)
            S.op("pool", lambda e: e.tensor_tensor(out=xs[:], in0=rst[:, :, 0:TP], in1=rst[:, :, 1:TP + 1], op=ALU.subtract),
                 reads=[rst], writes=[xs])
            S.op("pool", lambda e: e.tensor_tensor(out=xs[:], in0=xs[:], in1=mu[:].unsqueeze(2).to_broadcast([128, 13, TP]), op=ALU.mult),
                 reads=[xs, mu], writes=[xs])
            S.op("pool", lambda e: e.tensor_tensor(out=xs[:], in0=xs[:], in1=rst[:, :, 1:TP + 1], op=ALU.add), reads=[xs, rst], writes=[xs])
            r = xs[:, 0:4, :]; k = xs[:, 4:8, :]; v = xs[:, 8:12, :]
            S.op("act", lambda e: e.activation(out=th[0:64, :], in_=xs[0:64, 12, :], func=AF.Tanh), reads=[xs], pwrites=[th])
            S.op("act", lambda e: e.copy(out=th[64:128, :], in_=xs[64:128, 12, :]), reads=[xs], pwrites=[th])
            pw = pa.next(); pp = pa.next()
            for p in range(4):
                S.op("pe", lambda e, p=p, pw=pw: e.matmul(pw[:, p, :], lhsT=w2[0:64, p * 128:(p + 1) * 128], rhs=th[0:64, :], start=True, stop=True),
                     reads=[w2, th], writes=[pw] if p == 0 else (), pwrites=() if p == 0 else [pw])
                S.op("pe", lambda e, p=p, pp=pp: e.matmul(pp[:, p, :], lhsT=a2[64:128, p * 128:(p + 1) * 128], rhs=th[64:128, :], start=True, stop=True),
                     reads=[a2, th], writes=[pp] if p == 0 else (), pwrites=() if p == 0 else [pp])
            for p in range(4):
                S.op("act", lambda e, p=p, pw=pw: e.activation(out=dd[:, p, :], in_=pw[:, p, :], func=AF.Sigmoid, bias=w0[:, p:p + 1]),
                     reads=[pw, w0], pwrites=[dd])
                S.op("act", lambda e, p=p, pp=pp: e.activation(out=aa[:, p, :], in_=pp[:, p, :], func=AF.Sigmoid, bias=a0[:, p:p + 1]),
                     reads=[pp, a0], pwrites=[aa])
            S.op("act", lambda e: e.activation(out=dd[:], in_=dd[:], func=AF.Exp, scale=-0.6065306597126334), reads=[dd], writes=[dd])
            S.op("pool", lambda e: e.tensor_tensor(out=kkf[:], in0=k, in1=bc4(kkc), op=ALU.mult), reads=[xs, kkc], writes=[kkf])
            S.op("pool", lambda e: e.tensor_tensor(out=sq[:], in0=kkf[:], in1=kkf[:], op=ALU.mult), reads=[kkf], writes=[sq])
            pn = pa.next()
            for p in range(4):
                S.op("pe", lambda e, p=p, pn=pn: e.matmul(pn[:, p, :], lhsT=ones[:], rhs=sq[:, p, :], start=True, stop=True),
                     reads=[ones, sq], writes=[pn] if p == 0 else (), pwrites=() if p == 0 else [pn])
            S.op("act", lambda e, pn=pn: e.activation(out=rn[:], in_=pn[:], func=AF.Sqrt), reads=[pn], writes=[rn])
            S.op("pool", lambda e: e.tensor_scalar(out=rn[:], in0=rn[:], scalar1=1e-12, scalar2=None, op0=ALU.max), reads=[rn], writes=[rn])
            S.op("dve", lambda e: e.reciprocal(out=rn[:], in_=rn[:]), reads=[rn], writes=[rn])
            S.op("pool", lambda e: e.tensor_tensor(out=kkf[:], in0=kkf[:], in1=rn[:], op=ALU.mult), reads=[kkf, rn], writes=[kkf])
            S.op("pool", lambda e: e.tensor_tensor(out=t1[:], in0=aa[:], in1=bc4(ka), op=ALU.mult), reads=[aa, ka], writes=[t1])
            S.op("pool", lambda e: e.tensor_tensor(out=t1[:], in0=t1[:], in1=bc4(omka), op=ALU.add), reads=[t1, omka], writes=[t1])
            S.op("pool", lambda e: e.tensor_tensor(out=kp[:], in0=k, in1=t1[:], op=ALU.mult), reads=[xs, t1], writes=[kp])
            S.op("pool", lambda e: e.tensor_scalar(out=am[:], in0=kkf[:], scalar1=-1.0, scalar2=None, op0=ALU.mult), reads=[kkf], writes=[am])
            S.op("pool", lambda e: e.tensor_tensor(out=bm[:], in0=kkf[:], in1=aa[:], op=ALU.mult), reads=[kkf, aa], writes=[bm])
            S.op("pool", lambda e: e.tensor_tensor(out=t1[:], in0=r, in1=kp[:], op=ALU.mult), reads=[xs, kp], writes=[t1])
            S.op("pool", lambda e: e.tensor_tensor(out=sq[:], in0=t1[:], in1=bc4(rkc), op=ALU.mult), reads=[t1, rkc], writes=[sq])
            pr = pa.next()
            for p in range(4):
                S.op("pe", lambda e, p=p, pr=pr: e.matmul(pr[:, p, :], lhsT=ones[:], rhs=sq[:, p, :], start=True, stop=True),
                     reads=[ones, sq], writes=[pr] if p == 0 else (), pwrites=() if p == 0 else [pr])
            S.op("act", lambda e, pr=pr: e.copy(out=bonus[:], in_=pr[:]), reads=[pr], writes=[bonus])
            S.op("pool", lambda e: e.tensor_tensor(out=bonus[:], in0=bonus[:], in1=v, op=ALU.mult), reads=[bonus, xs], writes=[bonus])
            S.op("pool", lambda e: e.tensor_copy(out=vv[:], in_=v), reads=[xs], writes=[vv])
            S.op("pool", lambda e: e.tensor_copy(out=t1[:], in_=r), reads=[xs], writes=[t1])
            for oi, src in enumerate((am, bm, dd, kp, t1)):
                pt = pa.next()
                for p in range(4):
                    S.op("pe", lambda e, p=p, src=src, pt=pt: e.transpose(out=pt[:, p, :], in_=src[:, p, :], identity=identF[:]),
                         reads=[src, identF], writes=[pt] if p == 0 else (), pwrites=() if p == 0 else [pt])
                S.op("act", lambda e, oi=oi, pt=pt: e.copy(out=tk[:, oi, :, :], in_=pt[:]), reads=[pt], pwrites=[tk])
            for oi in range(5):
                for h2 in range(2):
                    S.dma("sp", xtok.t[t0:t0 + TP, oi, h2, :].rearrange("t (p j) -> t p j", p=4), tk[:, oi, :, h2 * 64:(h2 + 1) * 64],
                          reads=[tk], pwrites=[xtok], key=tk)
            S.dma("sp", rzt[:], scr["rzT"].t[:, t0:t0 + TP].rearrange("(c p) t -> p c t", p=128), reads=[scr["rzT"]], writes=[rzt])
            xflat = xtok.t.rearrange("t o h c -> (t o) h c")
            for tb in range(0, TP, TB):
                bc = bcr.next()
                for h2 in range(2):
                    S.dma("sp", bc[h2 * 64:(h2 + 1) * 64, :, :, :].rearrange("p t o c -> p (t o) c"),
                          xflat[(t0 + tb) * 5:(t0 + tb + TB) * 5, h2, :].partition_broadcast(64),
                          reads=[xtok], writes=[bc] if h2 == 0 else (), pwrites=() if h2 == 0 else [bc], key=bc)
                for tt in range(TB):
                    t = tb + tt
                    A = bc[:, tt, 0, :].rearrange("p (a j) -> p a j", a=4)
                    B = bc[:, tt, 1, :].rearrange("p (a j) -> p a j", a=4)
                    Dd = bc[:, tt, 2, :].rearrange("p (a j) -> p a j", a=4)
                    Kk = bc[:, tt, 3, :].rearrange("p (a j) -> p a j", a=4)
                    R = bc[:, tt, 4, :].rearrange("p (a j) -> p a j", a=4)
                    kvb = kv.next()
                    S.op("pool", lambda e, Kk=Kk, t=t, kvb=kvb: e.tensor_tensor(out=kvb[:], in0=Kk, in1=vv[:, :, t:t + 1].to_broadcast([128, 4, 64]),
                                                                              op=ALU.mult), reads=[bc, vv], writes=[kvb])
                    S.op("dve", lambda e, A=A: e.tensor_tensor(out=tmp[:], in0=St[:], in1=A, op=ALU.mult), reads=[St, bc], writes=[tmp])
                    S.op("dve", lambda e: e.tensor_reduce(out=sa[:], in_=tmp[:], axis=AX.X, op=ALU.add), reads=[tmp], writes=[sa])
                    S.op("dve", lambda e, Dd=Dd: e.tensor_tensor(out=St[:], in0=St[:], in1=Dd, op=ALU.mult), reads=[St, bc, tmp], writes=[St])
                    S.op("dve", lambda e, B=B: e.tensor_tensor(out=tmp2[:], in0=B, in1=sa[:].unsqueeze(2).to_broadcast([128, 4, 64]), op=ALU.mult),
                         reads=[bc, sa], writes=[tmp2])
                    S.op("dve", lambda e: e.tensor_tensor(out=St[:], in0=St[:], in1=tmp2[:], op=ALU.add), reads=[St, tmp2], writes=[St])
                    S.op("dve", lambda e, kvb=kvb: e.tensor_tensor(out=St[:], in0=St[:], in1=kvb[:], op=ALU.add), reads=[St, kvb], writes=[St])
                    S.op("dve", lambda e, R=R: e.tensor_tensor(out=tmp[:], in0=St[:], in1=R, op=ALU.mult), reads=[St, bc], writes=[tmp])
                    S.op("dve", lambda e, t=t: e.tensor_reduce(out=ybuf[:, :, t], in_=tmp[:], axis=AX.X, op=ALU.add), reads=[tmp], pwrites=[ybuf])
            S.op("pool", lambda e: e.tensor_tensor(out=ysq[:], in0=ybuf[:], in1=ybuf[:], op=ALU.mult), reads=[ybuf], writes=[ysq])
            pm = pa.next(); pq = pa.next()
            for p in range(4):
                S.op("pe", lambda e, p=p, pm=pm: e.matmul(pm[:, p, :], lhsT=ones[:], rhs=ybuf[:, p, :], start=True, stop=True),
                     reads=[ones, ybuf], writes=[pm] if p == 0 else (), pwrites=() if p == 0 else [pm])
                S.op("pe", lambda e, p=p, pq=pq: e.matmul(pq[:, p, :], lhsT=ones[:], rhs=ysq[:, p, :], start=True, stop=True),
                     reads=[ones, ysq], writes=[pq] if p == 0 else (), pwrites=() if p == 0 else [pq])
            S.op("act", lambda e, pm=pm: e.activation(out=mean[:], in_=pm[:], func=AF.Copy, scale=1.0 / 64), reads=[pm], writes=[mean])
            S.op("act", lambda e, pq=pq: e.activation(out=var[:], in_=pq[:], func=AF.Copy, scale=1.0 / 64), reads=[pq], writes=[var])
            S.op("pool", lambda e: e.tensor_tensor(out=ysq[:], in0=mean[:], in1=mean[:], op=ALU.mult), reads=[mean, ysq], writes=[ysq])
            S.op("pool", lambda e: e.tensor_tensor(out=var[:], in0=var[:], in1=ysq[:], op=ALU.subtract), reads=[var, ysq], writes=[var])
            S.op("act", lambda e: e.activation(out=var[:], in_=var[:], func=AF.Sqrt, bias=GN_EPS, scale=1.0), reads=[var], writes=[var])
            S.op("dve", lambda e: e.reciprocal(out=var[:], in_=var[:]), reads=[var], writes=[var])
            S.op("pool", lambda e: e.tensor_tensor(out=mean[:], in0=ybuf[:], in1=mean[:], op=ALU.subtract), reads=[ybuf, mean], writes=[mean])
            S.op("pool", lambda e: e.tensor_tensor(out=mean[:], in0=mean[:], in1=var[:], op=ALU.mult), reads=[mean, var], writes=[mean])
            S.op("pool", lambda e: e.tensor_tensor(out=mean[:], in0=mean[:], in1=bc4(lg), op=ALU.mult), reads=[mean, lg], writes=[mean])
            S.op("pool", lambda e: e.tensor_tensor(out=mean[:], in0=mean[:], in1=bc4(lb), op=ALU.add), reads=[mean, lb], writes=[mean])
            S.op("pool", lambda e: e.tensor_tensor(out=mean[:], in0=mean[:], in1=bonus[:], op=ALU.add), reads=[mean, bonus], writes=[mean])
            S.op("pool", lambda e: e.tensor_tensor(out=yo[:], in0=mean[:], in1=rzt[:], op=ALU.mult), reads=[mean, rzt], writes=[yo])
            S.dma("sp", scr["ysT"].t[2, :, t0:t0 + TP].rearrange("(c p) t -> p c t", p=128), yo[:], reads=[yo], pwrites=[scr["ysT"]], key=yo)
        _barrier(S)
        S.stack_pop()


_NC_CACHE = {}


def kernel(**inputs):
    SEQ = 8192
    if "nc" not in _NC_CACHE:
        _NC_CACHE["nc"] = build(SEQ, nlayers=2, enable=(1, 1, 1), scr_kind="Internal")
    nc = _NC_CACHE["nc"]
    x = np.ascontiguousarray(np.asarray(inputs["x"], dtype=np.float32))
    p = np.asarray(inputs["p"], dtype=np.float32)
    base = {}
    for k in WSPEC:
        v = np.ascontiguousarray(np.asarray(inputs[k], dtype=np.float32))
        base[k] = v.reshape(WSPEC[k])
    in_maps = []
    for b in range(8):
        m = dict(base)
        m["x"] = np.ascontiguousarray(x[b])
        m["p"] = np.ascontiguousarray(p[:, b])
        in_maps.append(m)
    res = run_bass_kernel_spmd(nc, in_maps, core_ids=list(range(8)))
    return np.stack([np.asarray(r["out"], dtype=np.float32) for r in res.results], axis=0)
```

```python
import contextlib
import numpy as np
import concourse.bass as bass
import concourse.mybir as mybir

F32 = mybir.dt.float32
BF16 = mybir.dt.bfloat16
AF = mybir.ActivationFunctionType
ALU = mybir.AluOpType
AX = mybir.AxisListType

ENGS = ("pe", "act", "dve", "pool", "sp")


class Buf:
    __slots__ = ("name", "w", "wfull", "r", "t")

    def __init__(self, name, t=None):
        self.name = name
        self.t = t
        self.w = []
        self.wfull = []
        self.r = []

    def __getitem__(self, k):
        return self.t[k]


class Op:
    __slots__ = ("eng", "fn", "deps", "marked", "tick", "dma", "idx")

    def __init__(self, eng, fn, dma):
        self.eng = eng
        self.fn = fn
        self.deps = []
        self.marked = False
        self.tick = None
        self.dma = dma
        self.idx = None


class DmaSem:
    def __init__(self):
        self.sem = None
        self.count = 0


class Sched:
    def __init__(self, nc, stack):
        self.nc = nc
        self.stack = stack
        self.ops = {e: [] for e in ENGS}
        self.all_ops = []
        self.dsems = {}
        self.n_sems = 0
        self.fence = []
        self.stacks = [stack]
        self.phase_keys = []
        self.free_ds = []
        self.all_ds = []
        self.keep = []

    def stack_push(self, st):
        self.stacks.append(st)
        self.phase_keys.append([])

    def stack_pop(self):
        self.stacks.pop()
        for kid in self.phase_keys.pop():
            ds = self.dsems.pop(kid, None)
            if ds is not None:
                self.free_ds.append(ds)

    def sb(self, name, shape, dt=F32):
        self.n_sems += 1
        name = "%s_u%d" % (name, self.n_sems)
        t = self.stacks[-1].enter_context(self.nc.sbuf_tensor(name, list(shape), dt))
        return Buf(name, t)

    def ps(self, name, shape, dt=F32):
        self.n_sems += 1
        name = "%s_u%d" % (name, self.n_sems)
        t = self.stacks[-1].enter_context(self.nc.psum_tensor(name, list(shape), dt))
        return Buf(name, t)

    def dram(self, name, shape, dt, kind="Internal"):
        t = self.nc.dram_tensor(name, list(shape), dt, kind=kind)
        return Buf(name, t.ap())

    def _add(self, eng, fn, reads, writes, pwrites, dma):
        op = Op(eng, fn, dma)
        deps = list(self.fence)
        for b in reads:
            deps.extend(b.w)
        for b in writes:
            deps.extend(b.w)
            deps.extend(b.r)
        for b in pwrites:
            deps.extend(b.wfull)
            deps.extend(b.r)
        seen = set()
        for d in deps:
            if id(d) in seen or d is op:
                continue
            seen.add(id(d))
            if d.eng == "pe" and eng == "pe" and d.dma is None and dma is None:
                continue
            op.deps.append(d)
            d.marked = True
        for b in reads:
            b.r.append(op)
            if len(b.r) > 24:
                b.r = self._prune(b.r)
        for b in writes:
            b.w = [op]
            b.wfull = [op]
            b.r = []
        for b in pwrites:
            b.w.append(op)
            if len(b.w) > 24:
                b.w = self._prune(b.w)
        op.idx = len(self.all_ops)
        self.all_ops.append(op)
        self.ops[eng].append(op)
        return op

    @staticmethod
    def _prune(lst):
        last = {}
        for o in lst:
            key = (o.eng, None) if o.dma is None else ("dma", id(o.dma))
            last[key] = o
        return list(last.values())

    def op(self, eng, fn, reads=(), writes=(), pwrites=()):
        return self._add(eng, fn, reads, writes, pwrites, None)

    def dma(self, eng, out_ap, in_ap, reads=(), writes=(), pwrites=(), key=None, **kw):
        if key is None:
            key = (list(writes) + list(pwrites))[0]
        ds = self.dsems.get(id(key))
        if ds is None:
            if self.free_ds:
                ds = self.free_ds.pop()
            else:
                ds = DmaSem()
                self.all_ds.append(ds)
            self.dsems[id(key)] = ds
            self.keep.append(key)
            if self.phase_keys:
                self.phase_keys[-1].append(id(key))
        fn = lambda e, o=out_ap, i=in_ap, kw=kw: e.dma_start(out=o, in_=i, **kw)
        op = self._add(eng, fn, reads, writes, pwrites, ds)
        ds.count += 16
        op.tick = ds.count
        return op

    def barrier_bufs(self, bufs):
        pass

    def emit(self):
        nc = self.nc
        stack = self.stack
        esem = {}
        for e in ENGS:
            esem[e] = stack.enter_context(nc.semaphore("s_" + e))
        for ds in self.all_ds:
            ds.sem = stack.enter_context(nc.semaphore("d%d" % self.n_sems))
            self.n_sems += 1
        for e in ENGS:
            c = 0
            for o in self.ops[e]:
                if o.dma is None:
                    if o.marked:
                        c += 1
                        o.tick = c
        self.max_ticks = {e: max([o.tick or 0 for o in self.ops[e] if o.dma is None] + [0]) for e in ENGS}

        def evkey(d):
            if d.dma is not None:
                return ("d", id(d.dma)), d.dma.sem, d.tick
            return ("e", d.eng), esem[d.eng], d.tick

        def run(eng_name, eng):
            seen = {}
            for o in self.ops[eng_name]:
                waits = {}
                for d in o.deps:
                    k, sem, val = evkey(d)
                    if seen.get(k, 0) >= val:
                        continue
                    if k not in waits or waits[k][1] < val:
                        waits[k] = (sem, val)
                for k, (sem, val) in waits.items():
                    eng.wait_ge(sem, val)
                    seen[k] = val
                inst = o.fn(eng)
                if o.dma is not None:
                    inst.then_inc(o.dma.sem, 16)
                elif o.marked:
                    inst.then_inc(esem[eng_name], 1)
            if eng_name == "sp":
                for e2 in ENGS:
                    m = self.max_ticks[e2]
                    if m > 0:
                        eng.wait_ge(esem[e2], m)
                for ds in self.all_ds:
                    if ds.count:
                        eng.wait_ge(ds.sem, ds.count)

        block = stack.enter_context(nc.Block())

        @block.tensor
        def _(e):
            run("pe", e)

        @block.scalar
        def _(e):
            run("act", e)

        @block.vector
        def _(e):
            run("dve", e)

        @block.gpsimd
        def _(e):
            run("pool", e)

        @block.sync
        def _(e):
            run("sp", e)


from concourse.bass_utils import run_bass_kernel_spmd

D = 1024
NCOL = 8600
PLE = 256
EPS = 1e-6


DEBUG = {}
_dbg_n = [0]


def dbg_dump(S, name, ap, buf, shape, cond=True):
    if not DEBUG.get("on") or not cond:
        return
    _dbg_n[0] += 1
    t = S.stacks[-1].enter_context(S.nc.sbuf_tensor("dbgsb_%d" % _dbg_n[0], list(shape), F32))
    tb = Buf("dbgsb", t)
    d = S.dram("dbg_" + name, list(shape), F32, kind="ExternalOutput")
    S.op("act", lambda e: e.copy(out=t[:], in_=ap), reads=[buf], writes=[tb])
    S.dma("sp", d.t, t[:], reads=[tb], writes=[d], key=tb)


class Ring:
    def __init__(self, bufs):
        self.bufs = bufs
        self.i = 0

    def next(self):
        b = self.bufs[self.i % len(self.bufs)]
        self.i += 1
        return b


def _barrier(S):
    fence = []
    for e in ENGS:
        comp = [o for o in S.ops[e] if o.dma is None]
        if comp:
            fence.append(comp[-1])
    lastd = {}
    for o in S.all_ops:
        if o.dma is not None:
            lastd[id(o.dma)] = o
    fence.extend(lastd.values())
    S.fence = fence


def make_ident(S, name="ident", dt=BF16):
    ident = S.sb(name, [128, 128], dt)
    S.op("pool", lambda e: e.memset(ident[:], 0.0), writes=[ident])
    S.op("pool", lambda e: e.affine_select(out=ident[:], in_=ident[:], pattern=[[-1, 128]],
                                           compare_op=ALU.not_equal, fill=1.0, base=0,
                                           channel_multiplier=1), reads=[ident], writes=[ident])
    return ident


def load_w_bf16(S, dst, k, src_ap, srcbuf):
    S.dma("pool", dst, src_ap, reads=[srcbuf], pwrites=[k], key=k, max_dma_last_dim=4096)


def rmsnorm_tile(S, xt_ap, xt_buf, g_buf, h_ap, h_buf, sq, ss, rs, eps=EPS, extra_reads=()):
    S.op("act", lambda e: e.activation(out=sq[:], in_=xt_ap, func=AF.Square, accum_out=ss[:]),
         reads=[xt_buf] + list(extra_reads), writes=[sq, ss])
    S.op("act", lambda e: e.activation(out=rs[:], in_=ss[:], func=AF.Sqrt, scale=1.0 / D, bias=eps),
         reads=[ss], writes=[rs])
    S.op("dve", lambda e: e.reciprocal(out=rs[:], in_=rs[:]), reads=[rs], writes=[rs])
    S.op("dve", lambda e: e.scalar_tensor_tensor(out=h_ap, in0=xt_ap, scalar=rs[:, 0:1], in1=g_buf[:],
                                                 op0=ALU.mult, op1=ALU.mult),
         reads=[xt_buf, rs, g_buf], pwrites=[h_buf])


def phase_A(S, nc, SEQ, lyr, x_src, Wd, scr):
    TT = 512
    nsub = TT // 128
    with contextlib.ExitStack() as st:
        S.stack_push(st)
        wt = S.sb("A_w", [128, 8, NCOL], BF16)
        gt = S.sb("A_g", [128, D])
        ident = make_ident(S, "A_ident")
        xt = S.sb("A_x", [128, nsub, D])
        sq = S.sb("A_sq", [128, D], BF16)
        ss = S.sb("A_ss", [128, 1])
        rs = S.sb("A_rs", [128, 1])
        h = S.sb("A_h", [128, nsub, D], BF16)
        hT = S.sb("A_hT", [128, 8, TT], BF16)
        stg_b = Ring([S.sb("A_sb%d" % i, [128, 512], BF16) for i in range(4)])
        stg_f = Ring([S.sb("A_sf%d" % i, [128, 512], F32) for i in range(3)])
        pT = Ring([S.ps("A_pT%d" % i, [128, 8, 128], BF16) for i in range(2)])
        pacc = Ring([S.ps("A_pa%d" % i, [128, 512], F32) for i in range(6)])

        w_in = Wd["w_in"]
        for k in range(8):
            S.dma("pool", wt[:, k, :], w_in.t[lyr, k * 128:(k + 1) * 128, 0:NCOL], reads=[w_in], pwrites=[wt],
                  key=wt, max_dma_last_dim=4096)
        S.dma("sp", gt[:], Wd["norm_g"].t[lyr:lyr + 1, :].partition_broadcast(128), reads=[Wd["norm_g"]],
              writes=[gt])

        FM = []
        for c in range(4):
            FM.append((c * 128, scr["qT"], c * 128, AF.Copy, 0.125, BF16))
        FM.append((512, scr["kcT"], 0, None, 1.0, BF16))
        FM.append((640, scr["vcT"], 0, None, 1.0, BF16))
        FM.append((768, scr["ksT"], 0, None, 1.0, BF16))
        FM.append((1024, scr["kwT"], 0, None, 1.0, BF16))
        for c in range(13):
            FM.append((3352 + c * 128, scr["rsT"], c * 128, None, 1.0, F32))
        for c in range(4):
            FM.append((5016 + c * 128, scr["rzT"], c * 128, AF.Silu, 1.0, BF16))
        for c in range(24):
            FM.append((5528 + c * 128, scr["mgT"], c * 128, AF.Sigmoid, 1.0, BF16))
        TM = [
            (896, 128, scr["vsw"], 0, None, BF16),
            (1152, 128, scr["vsw"], 128, None, BF16),
            (1280, 24, scr["gate"], 0, AF.Sigmoid, F32),
            (1304, 512, scr["nzs"], 0, AF.Silu, BF16),
            (1816, 512, scr["su"], 0, None, F32),
            (2328, 512, scr["sv"], 0, None, F32),
            (2840, 512, scr["szs"], 0, AF.Silu, BF16),
        ]
        evac_i = [0]

        def evac(out_ap, out_buf, in_ap, in_buf, func, scale):
            if func is None and scale == 1.0:
                if evac_i[0] % 2 == 0:
                    S.op("dve", lambda e: e.tensor_copy(out=out_ap, in_=in_ap), reads=[in_buf], writes=[out_buf])
                else:
                    S.op("act", lambda e: e.copy(out=out_ap, in_=in_ap), reads=[in_buf], writes=[out_buf])
                evac_i[0] += 1
            else:
                S.op("act", lambda e: e.activation(out=out_ap, in_=in_ap, func=func, scale=scale),
                     reads=[in_buf], writes=[out_buf])

        for ti in range(SEQ // TT):
            t0 = ti * TT
            S.dma("sp", xt[:], x_src.t[t0:t0 + TT, :].rearrange("(s p) d -> p s d", p=128), reads=[x_src],
                  writes=[xt])
            for s in range(nsub):
                rmsnorm_tile(S, xt[:, s, :], xt, gt, h[:, s, :], h, sq, ss, rs)
                pt = pT.next()
                for k in range(8):
                    S.op("pe", lambda e, k=k, s=s, pt=pt: e.transpose(out=pt[:, k, :], in_=h[:, s, k * 128:(k + 1) * 128],
                                                                     identity=ident[:]),
                         reads=[h, ident], writes=[pt] if k == 0 else (), pwrites=() if k == 0 else [pt])
                S.op("dve", lambda e, s=s, pt=pt: e.tensor_copy(out=hT[:, :, s * 128:(s + 1) * 128], in_=pt[:]),
                     reads=[pt], pwrites=[hT])
            for (c0, dbuf, r0, func, scale, dt) in FM:
                pa = pacc.next()
                for k in range(8):
                    S.op("pe", lambda e, k=k, pa=pa, c0=c0: e.matmul(pa[:], lhsT=wt[:, k, c0:c0 + 128], rhs=hT[:, k, :],
                                                                    start=(k == 0), stop=(k == 7)),
                         reads=[wt, hT], writes=[pa] if k == 0 else (), pwrites=() if k == 0 else [pa])
                sg = stg_b.next() if dt == BF16 else stg_f.next()
                evac(sg[:], sg, pa[:], pa, func, scale)
                S.dma("sp", dbuf.t[r0:r0 + 128, t0:t0 + TT], sg[:], reads=[sg], pwrites=[dbuf], key=sg)
            for s in range(nsub):
                for (c0, ncol, dbuf, dc0, func, dt) in TM:
                    pa = pacc.next()
                    for k in range(8):
                        S.op("pe", lambda e, k=k, pa=pa, c0=c0, ncol=ncol, s=s: e.matmul(
                            pa[:, 0:ncol], lhsT=hT[:, k, s * 128:(s + 1) * 128], rhs=wt[:, k, c0:c0 + ncol],
                            start=(k == 0), stop=(k == 7)),
                            reads=[wt, hT], writes=[pa] if k == 0 else (), pwrites=() if k == 0 else [pa])
                    sg = stg_b.next() if dt == BF16 else stg_f.next()
                    evac(sg[:, 0:ncol], sg, pa[:, 0:ncol], pa, func, 1.0)
                    S.dma("sp", dbuf.t[t0 + s * 128:t0 + (s + 1) * 128, dc0:dc0 + ncol], sg[:, 0:ncol], reads=[sg],
                          pwrites=[dbuf], key=sg)
        _barrier(S)
        S.stack_pop()


def make_scratch(S, SEQ, kind="Internal"):
    scr = {}
    def mk(name, shape, dt):
        scr[name] = S.dram(name, shape, dt, kind=kind)
    mk("qT", [512, SEQ], BF16)
    mk("kcT", [128, SEQ], BF16)
    mk("vcT", [128, SEQ], BF16)
    mk("ksT", [128, SEQ], BF16)
    mk("kwT", [128, SEQ], BF16)
    mk("vsw", [SEQ, 256], BF16)
    mk("gate", [SEQ, 24], F32)
    mk("nzs", [SEQ, 512], BF16)
    mk("su", [SEQ, 512], F32)
    mk("sv", [SEQ, 512], F32)
    mk("szs", [SEQ, 512], BF16)
    mk("rsT", [1664, SEQ], F32)
    mk("rzT", [512, SEQ], BF16)
    mk("mgT", [3072, SEQ], BF16)
    mk("ysT", [3, 512, SEQ], BF16)
    mk("xtok", [SEQ, 5, 2, 256], F32)
    return scr


def phase_C(S, nc, SEQ, lyr, Wd, scr):
    LN_EPS = 1e-5
    with contextlib.ExitStack() as st:
        S.stack_push(st)
        ident = make_ident(S, "C_ident")
        wraw = S.sb("C_wraw", [128, 8, 128])
        wbf = S.sb("C_wbf", [128, 8, 128], BF16)
        WT = S.sb("C_WT", [128, 8, 128], BF16)
        bsT = S.sb("C_bsT", [128, 8])
        lng = S.sb("C_lng", [128, 512])
        lnb = S.sb("C_lnb", [128, 512])
        pw = S.ps("C_pw", [128, 8, 128], BF16)
        S.dma("sp", wraw[:], Wd["sg_w"].t[lyr].rearrange("g t s -> t g s"), reads=[Wd["sg_w"]], writes=[wraw])
        S.dma("sp", bsT[:], Wd["sg_b"].t[lyr].rearrange("g t -> t g"), reads=[Wd["sg_b"]], writes=[bsT],
              allow_slow_non_contiguous=True)
        S.dma("sp", lng[:], Wd["sg_ln_g"].t[lyr:lyr + 1, :].partition_broadcast(128), reads=[Wd["sg_ln_g"]], writes=[lng])
        S.dma("sp", lnb[:], Wd["sg_ln_b"].t[lyr:lyr + 1, :].partition_broadcast(128), reads=[Wd["sg_ln_b"]], writes=[lnb])
        S.op("pool", lambda e: e.affine_select(out=wraw[:], in_=wraw[:], pattern=[[0, 8], [-1, 128]],
                                               compare_op=ALU.is_ge, fill=0.0, base=0, channel_multiplier=1),
             reads=[wraw], writes=[wraw])
        S.op("dve", lambda e: e.tensor_copy(out=wbf[:], in_=wraw[:]), reads=[wraw], writes=[wbf])
        for g in range(8):
            S.op("pe", lambda e, g=g: e.transpose(out=pw[:, g, :], in_=wbf[:, g, :], identity=ident[:]),
                 reads=[wbf, ident], pwrites=[pw])
        S.op("dve", lambda e: e.tensor_copy(out=WT[:], in_=pw[:]), reads=[pw], writes=[WT])

        NB = 2
        svt = Ring([S.sb("C_sv%d" % i, [128, 512]) for i in range(NB)])
        sut = Ring([S.sb("C_su%d" % i, [128, 512]) for i in range(NB)])
        szt = Ring([S.sb("C_sz%d" % i, [128, 512], BF16) for i in range(NB)])
        stats = S.sb("C_stats", [128, 6])
        mv = S.sb("C_mv", [128, 2])
        rstd = S.sb("C_rstd", [128, 1])
        vn0 = S.sb("C_vnf", [128, 512])
        vn = Ring([S.sb("C_vn%d" % i, [128, 512], BF16) for i in range(2)])
        y0 = S.sb("C_y0", [128, 512])
        yb = Ring([S.sb("C_yb%d" % i, [128, 512], BF16) for i in range(2)])
        pm = Ring([S.ps("C_pm%d" % i, [128, 512]) for i in range(2)])
        pt = Ring([S.ps("C_pt%d" % i, [128, 4, 128], BF16) for i in range(2)])
        stg = Ring([S.sb("C_stg%d" % i, [128, 4, 512], BF16) for i in range(2)])
        ys = scr["ysT"]
        sgb = None
        for c in range(SEQ // 128):
            t0 = c * 128
            v = svt.next(); u = sut.next(); z = szt.next()
            S.dma("sp", v[:], scr["sv"].t[t0:t0 + 128, :], reads=[scr["sv"]], writes=[v])
            S.dma("sp", u[:], scr["su"].t[t0:t0 + 128, :], reads=[scr["su"]], writes=[u])
            S.dma("sp", z[:], scr["szs"].t[t0:t0 + 128, :], reads=[scr["szs"]], writes=[z])
            S.op("dve", lambda e, v=v: e.bn_stats(out=stats[:], in_=v[:]), reads=[v], writes=[stats])
            S.op("dve", lambda e: e.bn_aggr(out=mv[:], in_=stats[:]), reads=[stats], writes=[mv])
            S.op("act", lambda e: e.activation(out=rstd[:], in_=mv[:, 1:2], func=AF.Sqrt, bias=LN_EPS, scale=1.0),
                 reads=[mv], writes=[rstd])
            S.op("dve", lambda e: e.reciprocal(out=rstd[:], in_=rstd[:]), reads=[rstd], writes=[rstd])
            S.op("dve", lambda e, v=v: e.tensor_scalar(out=vn0[:], in0=v[:], scalar1=mv[:, 0:1], scalar2=rstd[:, 0:1],
                                                       op0=ALU.subtract, op1=ALU.mult),
                 reads=[v, mv, rstd], writes=[vn0])
            S.op("pool", lambda e: e.tensor_tensor(out=vn0[:], in0=vn0[:], in1=lng[:], op=ALU.mult),
                 reads=[vn0, lng], writes=[vn0])
            vb = vn.next()
            S.op("pool", lambda e, vb=vb: e.tensor_tensor(out=vb[:], in0=vn0[:], in1=lnb[:], op=ALU.add),
                 reads=[vn0, lnb], writes=[vb])
            pmm = pm.next()
            for g in range(8):
                S.op("pe", lambda e, g=g, vb=vb, pmm=pmm: e.matmul(pmm[:, g * 64:(g + 1) * 64], lhsT=WT[:, g, :],
                                                                   rhs=vb[:, g * 64:(g + 1) * 64], start=True, stop=True),
                     reads=[WT, vb], writes=[pmm] if g == 0 else (), pwrites=() if g == 0 else [pmm])
            S.op("dve", lambda e, pmm=pmm: e.tensor_tensor(
                out=y0[:].rearrange("p (g d) -> p g d", g=8), in0=pmm[:].rearrange("p (g d) -> p g d", g=8),
                in1=bsT[:].unsqueeze(2).to_broadcast([128, 8, 64]), op=ALU.add), reads=[pmm, bsT], writes=[y0])
            S.op("pool", lambda e, u=u: e.tensor_tensor(out=y0[:], in0=y0[:], in1=u[:], op=ALU.mult),
                 reads=[y0, u], writes=[y0])
            y = yb.next()
            S.op("dve", lambda e, y=y, z=z: e.tensor_tensor(out=y[:], in0=y0[:], in1=z[:], op=ALU.mult),
                 reads=[y0, z], writes=[y])
            ptt = pt.next()
            for k in range(4):
                S.op("pe", lambda e, k=k, y=y, ptt=ptt: e.transpose(out=ptt[:, k, :], in_=y[:, k * 128:(k + 1) * 128],
                                                                    identity=ident[:]),
                     reads=[y, ident], writes=[ptt] if k == 0 else (), pwrites=() if k == 0 else [ptt])
            if c % 4 == 0:
                sgb = stg.next()
            cc = c % 4
            S.op("act", lambda e, ptt=ptt, sgb=sgb, cc=cc: e.copy(out=sgb[:, :, cc * 128:(cc + 1) * 128], in_=ptt[:]),
                 reads=[ptt], writes=[sgb] if cc == 0 else (), pwrites=() if cc == 0 else [sgb])
            if cc == 3 or c == SEQ // 128 - 1:
                tb = (c // 4) * 512
                n = (cc + 1) * 128
                S.dma("sp", ys.t[1, :, tb:tb + n].rearrange("(k p) t -> p k t", p=128), sgb[:, :, 0:n], reads=[sgb],
                      pwrites=[ys], key=sgb)
        _barrier(S)
        S.stack_pop()


def phase_E(S, nc, SEQ, lyr, x_src, x_dst, Wd, scr, final):
    TT = 512
    nsub = 4
    with contextlib.ExitStack() as st:
        S.stack_push(st)
        ident = make_ident(S, "E_ident")
        wb = S.sb("E_wb", [128, 3, 4, D], BF16)
        wo = S.sb("E_wo", [128, 8, D], BF16)
        wpg = S.sb("E_wpg", [128, 8, D], BF16)
        wpp = S.sb("E_wpp", [128, 2, D], BF16)
        gpl = S.sb("E_gpl", [128, D])
        gfin = S.sb("E_gfin", [128, D])
        for n in range(3):
            S.dma("pool", wb[:, n, :, :], Wd["w_branch"].t[lyr, n].rearrange("(k p) d -> p k d", p=128),
                  reads=[Wd["w_branch"]], pwrites=[wb], key=wb, max_dma_last_dim=4096)
        for k0 in range(0, 8, 4):
            S.dma("pool", wo[:, k0:k0 + 4, :], Wd["w_o"].t[lyr, k0 * 128:(k0 + 4) * 128, :].rearrange("(k p) d -> p k d", p=128),
                  reads=[Wd["w_o"]], pwrites=[wo], key=wo, max_dma_last_dim=4096)
            S.dma("pool", wpg[:, k0:k0 + 4, :], Wd["w_ple_gate"].t[lyr, k0 * 128:(k0 + 4) * 128, :].rearrange("(k p) d -> p k d", p=128),
                  reads=[Wd["w_ple_gate"]], pwrites=[wpg], key=wpg, max_dma_last_dim=4096)
        S.dma("pool", wpp[:], Wd["w_ple_proj"].t[lyr].rearrange("(k p) d -> p k d", p=128),
              reads=[Wd["w_ple_proj"]], pwrites=[wpp], key=wpp, max_dma_last_dim=4096)
        S.dma("sp", gpl[:], Wd["ple_norm_g"].t[lyr:lyr + 1, :].partition_broadcast(128), reads=[Wd["ple_norm_g"]], writes=[gpl])
        if final:
            S.dma("sp", gfin[:], Wd["final_norm_g"].t[0:1, :].partition_broadcast(128), reads=[Wd["final_norm_g"]], writes=[gfin])

        yst = S.sb("E_ys", [128, 3, 4, TT], BF16)
        mgt = S.sb("E_mg", [128, 24, TT], BF16)
        mrg = S.sb("E_mrg", [128, 8, TT])
        mrb = S.sb("E_mrb", [128, 8, TT], BF16)
        tmp = Ring([S.sb("E_tmp%d" % i, [128, TT]) for i in range(2)])
        xt = S.sb("E_x", [128, nsub, D])
        pin = S.sb("E_p", [128, nsub, PLE])
        pbf = S.sb("E_pbf", [128, PLE], BF16)
        pTs = S.sb("E_pT", [128, 2, 128], BF16)
        sq = S.sb("E_sq", [128, D], BF16)
        ss = S.sb("E_ss", [128, 1])
        rs = S.sb("E_rs", [128, 1])
        hp = S.sb("E_hp", [128, D], BF16)
        hpT = S.sb("E_hpT", [128, 8, 128], BF16)
        gate = S.sb("E_gate", [128, D])
        xo = Ring([S.sb("E_xo%d" % i, [128, D]) for i in range(2)])
        pz = Ring([S.ps("E_pz%d" % i, [128, TT]) for i in range(3)])
        po = Ring([S.ps("E_po%d" % i, [128, 512]) for i in range(2)])
        pg = Ring([S.ps("E_pg%d" % i, [128, 512]) for i in range(2)])
        ptr = S.ps("E_ptr", [128, 8, 128], BF16)

        for ti in range(SEQ // TT):
            t0 = ti * TT
            for n in range(3):
                S.dma("sp", yst[:, n, :, :], scr["ysT"].t[n, :, t0:t0 + TT].rearrange("(k p) t -> p k t", p=128),
                      reads=[scr["ysT"]], writes=[yst] if n == 0 else (), pwrites=() if n == 0 else [yst], key=yst)
            for k0 in range(0, 24, 8):
                S.dma("sp", mgt[:, k0:k0 + 8, :], scr["mgT"].t[k0 * 128:(k0 + 8) * 128, t0:t0 + TT].rearrange("(k p) t -> p k t", p=128),
                      reads=[scr["mgT"]], writes=[mgt] if k0 == 0 else (), pwrites=() if k0 == 0 else [mgt], key=mgt)
            S.dma("sp", xt[:], x_src.t[t0:t0 + TT, :].rearrange("(s p) d -> p s d", p=128), reads=[x_src], writes=[xt])
            S.dma("sp", pin[:], Wd["p"].t[lyr, t0:t0 + TT, :].rearrange("(s p) d -> p s d", p=128), reads=[Wd["p"]], writes=[pin])
            for dc in range(8):
                pzs = []
                for n in range(3):
                    pzz = pz.next()
                    pzs.append(pzz)
                    for k in range(4):
                        S.op("pe", lambda e, n=n, k=k, dc=dc, pzz=pzz: e.matmul(
                            pzz[:], lhsT=wb[:, n, k, dc * 128:(dc + 1) * 128], rhs=yst[:, n, k, :], start=(k == 0), stop=(k == 3)),
                            reads=[wb, yst], writes=[pzz] if k == 0 else (), pwrites=() if k == 0 else [pzz])
                S.op("dve", lambda e, dc=dc, p0=pzs[0]: e.tensor_tensor(out=mrg[:, dc, :], in0=p0[:], in1=mgt[:, dc, :], op=ALU.mult),
                     reads=[pzs[0], mgt], pwrites=[mrg])
                t1 = tmp.next()
                S.op("dve", lambda e, dc=dc, p1=pzs[1], t1=t1: e.tensor_tensor(out=t1[:], in0=p1[:], in1=mgt[:, 8 + dc, :], op=ALU.mult),
                     reads=[pzs[1], mgt], writes=[t1])
                t2 = tmp.next()
                S.op("dve", lambda e, dc=dc, p2=pzs[2], t2=t2: e.tensor_tensor(out=t2[:], in0=p2[:], in1=mgt[:, 16 + dc, :], op=ALU.mult),
                     reads=[pzs[2], mgt], writes=[t2])
                S.op("pool", lambda e, dc=dc, t1=t1: e.tensor_tensor(out=mrg[:, dc, :], in0=mrg[:, dc, :], in1=t1[:], op=ALU.add),
                     reads=[mrg, t1], pwrites=[mrg])
                S.op("pool", lambda e, dc=dc, t2=t2: e.tensor_tensor(out=mrb[:, dc, :], in0=mrg[:, dc, :], in1=t2[:], op=ALU.add),
                     reads=[mrg, t2], pwrites=[mrb])
            for s in range(nsub):
                for blk in range(2):
                    pp = po.next()
                    for k in range(8):
                        S.op("pe", lambda e, k=k, s=s, blk=blk, pp=pp: e.matmul(
                            pp[:], lhsT=mrb[:, k, s * 128:(s + 1) * 128], rhs=wo[:, k, blk * 512:(blk + 1) * 512],
                            start=(k == 0), stop=(k == 7)),
                            reads=[mrb, wo], writes=[pp] if k == 0 else (), pwrites=() if k == 0 else [pp])
                    S.op("dve", lambda e, s=s, blk=blk, pp=pp: e.tensor_tensor(
                        out=xt[:, s, blk * 512:(blk + 1) * 512], in0=pp[:], in1=xt[:, s, blk * 512:(blk + 1) * 512], op=ALU.add),
                        reads=[pp, xt], pwrites=[xt])
                rmsnorm_tile(S, xt[:, s, :], xt, gpl, hp[:], hp, sq, ss, rs)
                for k in range(8):
                    S.op("pe", lambda e, k=k: e.transpose(out=ptr[:, k, :], in_=hp[:, k * 128:(k + 1) * 128], identity=ident[:]),
                         reads=[hp, ident], writes=[ptr] if k == 0 else (), pwrites=() if k == 0 else [ptr])
                S.op("act", lambda e: e.copy(out=hpT[:], in_=ptr[:]), reads=[ptr], writes=[hpT])
                S.op("pool", lambda e, s=s: e.tensor_copy(out=pbf[:], in_=pin[:, s, :]), reads=[pin], writes=[pbf])
                for k in range(2):
                    S.op("pe", lambda e, k=k: e.transpose(out=ptr[:, k, :], in_=pbf[:, k * 128:(k + 1) * 128], identity=ident[:]),
                         reads=[pbf, ident, hpT], writes=[ptr] if k == 0 else (), pwrites=() if k == 0 else [ptr])
                S.op("act", lambda e: e.copy(out=pTs[:], in_=ptr[:, 0:2, :]), reads=[ptr], writes=[pTs])
                xout = xo.next()
                for blk in range(2):
                    pgg = pg.next()
                    for k in range(8):
                        S.op("pe", lambda e, k=k, blk=blk, pgg=pgg: e.matmul(
                            pgg[:], lhsT=hpT[:, k, :], rhs=wpg[:, k, blk * 512:(blk + 1) * 512], start=(k == 0), stop=(k == 7)),
                            reads=[hpT, wpg], writes=[pgg] if k == 0 else (), pwrites=() if k == 0 else [pgg])
                    S.op("act", lambda e, blk=blk, pgg=pgg: e.activation(out=gate[:, blk * 512:(blk + 1) * 512], in_=pgg[:], func=AF.Sigmoid),
                         reads=[pgg], pwrites=[gate])
                    ppp = pg.next()
                    for k in range(2):
                        S.op("pe", lambda e, k=k, blk=blk, ppp=ppp: e.matmul(
                            ppp[:], lhsT=pTs[:, k, :], rhs=wpp[:, k, blk * 512:(blk + 1) * 512], start=(k == 0), stop=(k == 1)),
                            reads=[pTs, wpp], writes=[ppp] if k == 0 else (), pwrites=() if k == 0 else [ppp])
                    S.op("dve", lambda e, blk=blk, ppp=ppp: e.tensor_tensor(
                        out=gate[:, blk * 512:(blk + 1) * 512], in0=ppp[:], in1=gate[:, blk * 512:(blk + 1) * 512], op=ALU.mult),
                        reads=[ppp, gate], pwrites=[gate])
                    S.op("pool", lambda e, blk=blk, s=s, xout=xout: e.tensor_tensor(
                        out=xout[:, blk * 512:(blk + 1) * 512], in0=gate[:, blk * 512:(blk + 1) * 512],
                        in1=xt[:, s, blk * 512:(blk + 1) * 512], op=ALU.add),
                        reads=[gate, xt], writes=[xout] if blk == 0 else (), pwrites=() if blk == 0 else [xout])
                if final:
                    S.op("act", lambda e, xout=xout: e.activation(out=sq[:], in_=xout[:], func=AF.Square, accum_out=ss[:]),
                         reads=[xout], writes=[sq, ss])
                    S.op("act", lambda e: e.activation(out=rs[:], in_=ss[:], func=AF.Sqrt, scale=1.0 / D, bias=EPS),
                         reads=[ss], writes=[rs])
                    S.op("dve", lambda e: e.reciprocal(out=rs[:], in_=rs[:]), reads=[rs], writes=[rs])
                    S.op("dve", lambda e, xout=xout: e.scalar_tensor_tensor(out=xout[:], in0=xout[:], scalar=rs[:, 0:1], in1=gfin[:],
                                                                            op0=ALU.mult, op1=ALU.mult),
                         reads=[xout, rs, gfin], writes=[xout])
                S.dma("sp", x_dst.t[t0 + s * 128:t0 + (s + 1) * 128, :], xout[:], reads=[xout], pwrites=[x_dst], key=xout)
        _barrier(S)
        S.stack_pop()


def phase_D(S, nc, SEQ, lyr, Wd, scr):
    TP = 128
    C = 16
    NCH = TP // C
    GN_EPS = 64e-5
    LD = 0.6065306597126334
    with contextlib.ExitStack() as st:
        S.stack_push(st)
        identB = make_ident(S, "D_identB", BF16)
        ones = S.sb("D_ones", [128, 128])
        S.op("pool", lambda e: e.memset(ones[:], 0.0), writes=[ones])
        S.op("pool", lambda e: e.memset(ones[0:64, 0:64], 1.0), reads=[ones], writes=[ones])
        S.op("pool", lambda e: e.memset(ones[64:128, 64:128], 1.0), reads=[ones], writes=[ones])
        Ff = S.sb("D_F", [128, 64], BF16)
        S.op("pool", lambda e: e.tensor_tensor(out=Ff[:], in0=identB[:, 0:64], in1=identB[:, 64:128], op=ALU.add), reads=[identB], writes=[Ff])
        Sel = S.sb("D_Sel", [128, 16], BF16)
        S.op("pool", lambda e: e.tensor_tensor(out=Sel[:], in0=identB[:, 0:16], in1=identB[:, 16:32], op=ALU.add), reads=[identB], writes=[Sel])
        for hh in range(2, 8):
            S.op("pool", lambda e, hh=hh: e.tensor_tensor(out=Sel[:], in0=Sel[:], in1=identB[:, hh * 16:(hh + 1) * 16], op=ALU.add),
                 reads=[identB, Sel], writes=[Sel])
        maskF = S.sb("D_maskF", [128, 4, 8], BF16)
        S.op("pool", lambda e: e.memset(maskF[:], 0.0), writes=[maskF])
        for p in range(4):
            for h2 in range(2):
                S.op("pool", lambda e, p=p, h2=h2: e.memset(maskF[h2 * 64:(h2 + 1) * 64, p, 2 * p + h2:2 * p + h2 + 1], 1.0), reads=[maskF], writes=[maskF])
        maskZ = S.sb("D_maskZ", [128, 4, 2], BF16)
        S.op("pool", lambda e: e.memset(maskZ[:], 1.0), writes=[maskZ])
        S.op("pool", lambda e: e.affine_select(out=maskZ[:], in_=maskZ[:], pattern=[[-32, 4], [-16, 2]], compare_op=ALU.is_ge, fill=0.0,
                                               base=0, channel_multiplier=1), reads=[maskZ], writes=[maskZ])
        S.op("pool", lambda e: e.affine_select(out=maskZ[:], in_=maskZ[:], pattern=[[32, 4], [16, 2]], compare_op=ALU.is_ge, fill=0.0,
                                               base=15, channel_multiplier=-1), reads=[maskZ], writes=[maskZ])

        def trimask(name, pat, cm, op):
            m = S.sb(name, [128, 128], BF16)
            S.op("pool", lambda e: e.memset(m[:], 1.0), writes=[m])
            S.op("pool", lambda e: e.affine_select(out=m[:], in_=m[:], pattern=pat, compare_op=op, fill=0.0, base=0, channel_multiplier=cm),
                 reads=[m], writes=[m])
            return m
        mSL = trimask("D_mSL", [[-16, 8], [-1, 16]], 1, ALU.is_gt)
        mSU = trimask("D_mSU", [[16, 8], [1, 16]], -1, ALU.is_gt)
        mUI = trimask("D_mUI", [[16, 8], [1, 16]], -1, ALU.is_ge)
        rm = S.sb("D_rm", [128, 512])
        S.op("pool", lambda e: e.memset(rm[:], 1.0), writes=[rm])
        S.op("pool", lambda e: e.memset(rm[:, 0:512:16], 0.0), reads=[rm], writes=[rm])

        def cvec(name, key, n):
            t = S.sb("D_" + name, [128, n])
            S.dma("sp", t[:], Wd[key].t[lyr].rearrange("(c p) -> p c", p=128), reads=[Wd[key]], writes=[t],
                  allow_slow_non_contiguous=True)
            return t

        def cvec2(name, key):
            t = S.sb("D_" + name, [128, 4])
            S.dma("sp", t[:], Wd[key].t[lyr].rearrange("(c a) j -> (a j) c", a=2), reads=[Wd[key]], writes=[t],
                  allow_slow_non_contiguous=True)
            return t
        mu = cvec("mu", "rk_mu", 13)
        w0 = cvec("w0", "rk_w0", 4)
        a0 = cvec("a0", "rk_a0", 4)
        lg = cvec("lg", "rk_lnx_g", 4)
        lb = cvec("lb", "rk_lnx_b", 4)
        kkc = cvec2("kkc", "rk_kk")
        ka = cvec2("ka", "rk_ka")
        rkc = cvec2("rkc", "rk_rk")
        omka = S.sb("D_omka", [128, 4])
        S.op("pool", lambda e: e.tensor_scalar(out=omka[:], in0=ka[:], scalar1=-1.0, scalar2=1.0, op0=ALU.mult, op1=ALU.add),
             reads=[ka], writes=[omka])
        w2 = S.sb("D_w2", [64, 512], BF16)
        a2 = S.sb("D_a2", [128, 512], BF16)
        S.dma("pool", w2[:], Wd["rk_w2"].t[lyr], reads=[Wd["rk_w2"]], writes=[w2])
        S.dma("pool", a2[64:128, :], Wd["rk_a2"].t[lyr], reads=[Wd["rk_a2"]], writes=[a2])

        Hm = S.sb("D_H", [128, 4, 64])
        Hn = S.sb("D_Hn", [128, 4, 64])
        Hbf = S.sb("D_Hbf", [128, 4, 64], BF16)
        S.op("pool", lambda e: e.memset(Hm[:], 0.0), writes=[Hm])
        S.op("pool", lambda e: e.memset(Hbf[:], 0.0), writes=[Hbf])

        rst = S.sb("D_rst", [128, 13, TP + 1])
        xs = S.sb("D_xs", [128, 13, TP])
        th = S.sb("D_th", [128, TP], BF16)
        sg = S.sb("D_sg", [128, 4, TP])
        cum = S.sb("D_cum", [128, 4, TP])
        E1 = S.sb("D_E1", [128, 4, TP])
        E2 = S.sb("D_E2", [128, 4, TP])
        E3 = S.sb("D_E3", [128, 4, TP])
        aa = S.sb("D_aa", [128, 4, TP])
        kkf = S.sb("D_kkf", [128, 4, TP])
        sq = S.sb("D_sq", [128, 4, TP])
        rn = S.sb("D_rn", [128, 4, TP])
        kp = S.sb("D_kp", [128, 4, TP])
        t1 = S.sb("D_t1", [128, 4, TP])
        t2 = S.sb("D_t2", [128, 4, TP])
        comp = [S.sb("D_cmp%d" % i, [128, 4, TP], BF16) for i in range(5)]
        ZXr = Ring([[S.sb("D_Z%d_%d" % (b, i), [128, NCH, 4, 128], BF16) for i in range(5)] for b in range(2)])
        DcR = Ring([S.sb("D_Dc%d" % i, [128, NCH, 4]) for i in range(2)])
        bonR = Ring([S.sb("D_bon%d" % i, [128, 4, TP]) for i in range(2)])
        rzR = Ring([S.sb("D_rz%d" % i, [128, 4, TP], BF16) for i in range(2)])
        ybR = Ring([S.sb("D_yb%d" % i, [128, 4, TP]) for i in range(2)])
        yo = S.sb("D_yo", [128, 4, TP], BF16)
        ppre = S.ps("D_ppre", [128, 4, TP])
        R4 = lambda nm, shp, dt=BF16: Ring([S.sb("D_%s%d" % (nm, i), shp, dt) for i in range(4)])
        WyZr = R4("WyZ", [128, 4, 128]); WhTr = R4("WhT", [128, 4, 128]); BtZr = R4("BtZ", [128, 4, 128]); KtZr = R4("KtZ", [128, 4, 128])
        U0r = R4("U0", [128, 64]); Vtr = R4("Vt", [128, 64]); PTr = R4("PT", [128, 128]); QTr = R4("QT", [128, 128])
        ysbR = Ring([S.sb("D_ysb%d" % i, [128, 64]) for i in range(3)])

        class Reg:
            def __init__(self, bank, ap):
                self.bank = bank
                self.t = ap

        class Lane:
            pass
        lanes = []
        for li in range(2):
            L = Lane()
            L.Gr = Ring([S.sb("D_G%d_%d" % (li, i), [128, 128], BF16) for i in range(2)])
            L.Nr = Ring([S.sb("D_N%d_%d" % (li, i), [128, 128], BF16) for i in range(2)])
            L.NTr = Ring([S.sb("D_NT%d_%d" % (li, i), [128, 128], BF16) for i in range(2)])
            L.MTs = S.sb("D_MTs%d" % li, [128, 128], BF16)
            L.X1Z = S.sb("D_X1Z%d" % li, [128, 4, 128], BF16)
            L.X1s = S.sb("D_X1s%d" % li, [128, 64], BF16)
            L.tks = S.sb("D_tks%d" % li, [128, 2, 64], BF16)
            ba = S.ps("D_ba%d" % li, [128, 512])
            bb = ppre if li == 0 else S.ps("D_bb%d" % li, [128, 4, 128])
            bg = S.ps("D_bg%d" % li, [128, 3, 128])
            L.tokc = Reg(ba, ba.t[:, 0:256].rearrange("q (o j) -> q o j", o=4))
            L.QTp = Reg(ba, ba.t[:, 256:384])
            L.mvp = Reg(ba, ba.t[:, 384:448])
            L.sc = [Reg(bb, bb.t[:, i, :]) for i in range(4)]
            L.bb = bb
            L.bg = bg
            lanes.append(L)
        bs = S.ps("D_bs", [128, 512])
        bt_ = S.ps("D_bt", [128, 512])
        WHp = Reg(bs, bs.t[:, 0:256].rearrange("q (p i) -> q p i", p=4))
        Yp = Reg(bs, bs.t[:, 256:320])
        yfp = Reg(bt_, bt_.t[:, 0:64].rearrange("q (p t) -> q p t", p=4))
        yn = S.sb("D_yn", [128, 64])
        YZ = S.sb("D_YZ", [128, 4, 128], BF16)
        stats = S.sb("D_stats", [128, 6])
        mv = S.sb("D_mv", [128, 2])
        rstd = S.sb("D_rstd", [128, 1])

        bc4 = lambda t: t[:].unsqueeze(2).to_broadcast([128, 4, TP])

        def mm(out_ap, obuf, lhsT, lbuf, rhs, rbuf, start, stop=True, first_write=False):
            obuf = getattr(obuf, "bank", obuf)
            S.op("pe", lambda e: e.matmul(out_ap, lhsT=lhsT, rhs=rhs, start=start, stop=stop, skip_group_check=True),
                 reads=[lbuf, rbuf], writes=[obuf] if first_write else (), pwrites=() if first_write else [obuf])

        def prep(nb):
            t0 = nb * TP
            S.dma("sp", rst[:, :, 1:TP + 1], scr["rsT"].t[:, t0:t0 + TP].rearrange("(c p) t -> p c t", p=128), reads=[scr["rsT"]], writes=[rst])
            if nb == 0:
                S.op("pool", lambda e: e.memset(rst[:, :, 0:1], 0.0), reads=[rst], pwrites=[rst])
            else:
                S.dma("sp", rst[:, :, 0:1], scr["rsT"].t[:, t0 - 1:t0].rearrange("(c p) t -> p c t", p=128), reads=[scr["rsT"]],
                      pwrites=[rst], key=rst, allow_slow_non_contiguous=True)
            S.op("pool", lambda e: e.tensor_tensor(out=xs[:], in0=rst[:, :, 0:TP], in1=rst[:, :, 1:TP + 1], op=ALU.subtract), reads=[rst], writes=[xs])
            S.op("pool", lambda e: e.tensor_tensor(out=xs[:], in0=xs[:], in1=mu[:].unsqueeze(2).to_broadcast([128, 13, TP]), op=ALU.mult),
                 reads=[xs, mu], writes=[xs])
            S.op("pool", lambda e: e.tensor_tensor(out=xs[:], in0=xs[:], in1=rst[:, :, 1:TP + 1], op=ALU.add), reads=[xs, rst], writes=[xs])
            r = xs[:, 0:4, :]; k = xs[:, 4:8, :]; v = xs[:, 8:12, :]
            S.op("act", lambda e: e.activation(out=th[0:64, :], in_=xs[0:64, 12, :], func=AF.Tanh), reads=[xs], pwrites=[th])
            S.op("act", lambda e: e.copy(out=th[64:128, :], in_=xs[64:128, 12, :]), reads=[xs], pwrites=[th])
            for p in range(4):
                mm(ppre[:, p, :], ppre, w2[0:64, p * 128:(p + 1) * 128], w2, th[0:64, :], th, True, first_write=(p == 0))
            for p in range(4):
                S.op("act", lambda e, p=p: e.activation(out=sg[:, p, :], in_=ppre[:, p, :], func=AF.Sigmoid, bias=w0[:, p:p + 1]),
                     reads=[w0], writes=[ppre], pwrites=[sg])
            for p in range(4):
                mm(ppre[:, p, :], ppre, a2[64:128, p * 128:(p + 1) * 128], a2, th[64:128, :], th, True, first_write=(p == 0))
            for p in range(4):
                S.op("act", lambda e, p=p: e.activation(out=aa[:, p, :], in_=ppre[:, p, :], func=AF.Sigmoid, bias=a0[:, p:p + 1]),
                     reads=[a0], writes=[ppre], pwrites=[aa])
            S.op("dve", lambda e: e.tensor_tensor_scan(out=cum[:].rearrange("q p t -> q (p t)"), data0=rm[:],
                                                       data1=sg[:].rearrange("q p t -> q (p t)"), initial=0.0, op0=ALU.mult, op1=ALU.add),
                 reads=[rm, sg], writes=[cum])
            S.op("act", lambda e: e.activation(out=E1[:], in_=cum[:], func=AF.Exp, scale=-LD), reads=[cum], writes=[E1])
            S.op("act", lambda e: e.activation(out=E2[:], in_=cum[:], func=AF.Exp, scale=LD), reads=[cum], writes=[E2])
            S.op("pool", lambda e: e.tensor_tensor(out=t2[:], in0=cum[:], in1=sg[:], op=ALU.subtract), reads=[cum, sg], writes=[t2])
            S.op("act", lambda e: e.activation(out=E3[:], in_=t2[:], func=AF.Exp, scale=-LD), reads=[t2], writes=[E3])
            Dc = DcR.next()
            S.op("pool", lambda e, Dc=Dc: e.tensor_copy(out=Dc[:].rearrange("q c p -> q p c"), in_=E1[:, :, 15:TP:16]), reads=[E1], writes=[Dc])
            S.op("pool", lambda e: e.tensor_tensor(out=kkf[:], in0=k, in1=bc4(kkc), op=ALU.mult), reads=[xs, kkc], writes=[kkf])
            S.op("pool", lambda e: e.tensor_tensor(out=sq[:], in0=kkf[:], in1=kkf[:], op=ALU.mult), reads=[kkf], writes=[sq])
            for p in range(4):
                mm(ppre[:, p, :], ppre, ones[:], ones, sq[:, p, :], sq, True, first_write=(p == 0))
            S.op("act", lambda e: e.activation(out=rn[:], in_=ppre[:], func=AF.Sqrt), writes=[rn, ppre])
            S.op("dve", lambda e: e.tensor_scalar(out=rn[:], in0=rn[:], scalar1=1e-12, scalar2=None, op0=ALU.max), reads=[rn], writes=[rn])
            S.op("dve", lambda e: e.reciprocal(out=rn[:], in_=rn[:]), reads=[rn], writes=[rn])
            S.op("pool", lambda e: e.tensor_tensor(out=kkf[:], in0=kkf[:], in1=rn[:], op=ALU.mult), reads=[kkf, rn], writes=[kkf])
            S.op("pool", lambda e: e.tensor_tensor(out=t1[:], in0=aa[:], in1=bc4(ka), op=ALU.mult), reads=[aa, ka], writes=[t1])
            S.op("pool", lambda e: e.tensor_tensor(out=t1[:], in0=t1[:], in1=bc4(omka), op=ALU.add), reads=[t1, omka], writes=[t1])
            S.op("pool", lambda e: e.tensor_tensor(out=kp[:], in0=k, in1=t1[:], op=ALU.mult), reads=[xs, t1], writes=[kp])
            At, Bt, Kt, Rt, Vb = comp
            S.op("pool", lambda e: e.scalar_tensor_tensor(out=At[:], in0=kkf[:], scalar=-1.0, in1=E3[:], op0=ALU.mult, op1=ALU.mult)
                 if False else e.tensor_tensor(out=t2[:], in0=kkf[:], in1=E3[:], op=ALU.mult), reads=[kkf, E3], writes=[t2])
            S.op("dve", lambda e: e.tensor_scalar(out=At[:], in0=t2[:], scalar1=-1.0, scalar2=None, op0=ALU.mult), reads=[t2], writes=[At])
            S.op("pool", lambda e: e.tensor_tensor(out=t2[:], in0=kkf[:], in1=aa[:], op=ALU.mult), reads=[kkf, aa], writes=[t2])
            S.op("pool", lambda e: e.tensor_tensor(out=Bt[:], in0=t2[:], in1=E2[:], op=ALU.mult), reads=[t2, E2], writes=[Bt])
            S.op("pool", lambda e: e.tensor_tensor(out=Kt[:], in0=kp[:], in1=E2[:], op=ALU.mult), reads=[kp, E2], writes=[Kt])
            S.op("pool", lambda e: e.tensor_tensor(out=Rt[:], in0=r, in1=E1[:], op=ALU.mult), reads=[xs, E1], writes=[Rt])
            S.op("pool", lambda e: e.tensor_copy(out=Vb[:], in_=v), reads=[xs], writes=[Vb])
            S.op("pool", lambda e: e.tensor_tensor(out=t1[:], in0=r, in1=kp[:], op=ALU.mult), reads=[xs, kp], writes=[t1])
            S.op("pool", lambda e: e.tensor_tensor(out=sq[:], in0=t1[:], in1=bc4(rkc), op=ALU.mult), reads=[t1, rkc], writes=[sq])
            for p in range(4):
                mm(ppre[:, p, :], ppre, ones[:], ones, sq[:, p, :], sq, True, first_write=(p == 0))
            bon = bonR.next()
            S.op("act", lambda e, bon=bon: e.copy(out=bon[:], in_=ppre[:]), writes=[bon, ppre])
            S.op("pool", lambda e, bon=bon: e.tensor_tensor(out=bon[:], in0=bon[:], in1=v, op=ALU.mult), reads=[bon, xs], writes=[bon])
            rzt = rzR.next()
            S.dma("sp", rzt[:], scr["rzT"].t[:, t0:t0 + TP].rearrange("(c p) t -> p c t", p=128), reads=[scr["rzT"]], writes=[rzt])
            ZX = ZXr.next()
            for oi in range(5):
                for p in range(4):
                    S.op("dve" if (oi * 4 + p) % 4 != 3 else "pool", lambda e, oi=oi, p=p, ZX=ZX: e.tensor_tensor(
                        out=ZX[oi][:, :, p, :].rearrange("q c (h t) -> q c h t", t=16),
                        in0=comp[oi][:, p, :].rearrange("q (c t) -> q c t", t=16).unsqueeze(2).to_broadcast([128, NCH, 8, 16]),
                        in1=maskF[:, p, :].unsqueeze(1).unsqueeze(3).to_broadcast([128, NCH, 8, 16]), op=ALU.mult),
                        reads=[comp[oi], maskF], writes=[ZX[oi]] if p == 0 else (), pwrites=() if p == 0 else [ZX[oi]])
            return dict(ZX=ZX, Dc=Dc, bon=bon, rzt=rzt, yb=ybR.next(), t0=t0)


        def pre(bt, c, L, pc):
            ZA, ZB, ZK, ZR, ZV = bt["ZX"]
            BtZ = BtZr.next(); KtZ = KtZr.next(); U0 = U0r.next(); Vt = Vtr.next(); PTs = PTr.next(); QTs = QTr.next()
            WyZ = WyZr.next(); WhT = WhTr.next()
            pc.update(BtZ=BtZ, KtZ=KtZ, U0=U0, Vt=Vt, PTs=PTs, QTs=QTs, WyZ=WyZ, WhT=WhT, c=c, bt=bt)
            tokc = L.tokc
            first = True
            for oi, Z in enumerate((ZA, ZB, ZK, ZV)):
                for p in range(4):
                    mm(tokc.t[:, oi, :], tokc, Z[:, c, p, :], Z, Ff[:], Ff, first, first_write=first)
                    first = False
            N1 = L.Nr.next(); NT1 = L.NTr.next()
            specs = ((L.sc[0], ZA, ZB, mSL, N1), (L.sc[1], ZB, ZA, mSU, NT1), (L.sc[2], ZK, ZA, mSU, L.MTs), (L.sc[3], ZB, ZR, mUI, PTs))
            for gi, (pb, Lh, R_, msk, dst) in enumerate(specs):
                for p in range(4):
                    mm(pb.t, pb, Lh[:, c, p, :], Lh, R_[:, c, p, :], R_, p == 0, first_write=(gi == 0 and p == 0))
            for p in range(4):
                mm(L.QTp.t, L.QTp, ZK[:, c, p, :], ZK, ZR[:, c, p, :], ZR, False, first_write=False)
            yield
            G0 = L.Gr.next()
            tks = L.tks
            S.op("act", lambda e: e.copy(out=G0[:, 0:64], in_=tokc.t[:, 0, :]), writes=[G0, tokc.bank])
            mz = maskZ[:].unsqueeze(3).to_broadcast([128, 4, 2, 64])
            S.op("dve", lambda e: e.tensor_copy(out=tks[:], in_=tokc.t[:, 1:3, :]), writes=[tks, tokc.bank])
            S.op("act", lambda e: e.copy(out=Vt[:], in_=tokc.t[:, 3, :]), writes=[Vt, tokc.bank])
            S.op("dve", lambda e: e.tensor_tensor(out=BtZ[:].rearrange("q p (a j) -> q p a j", a=2),
                                                   in0=tks[:, 0, :].unsqueeze(1).unsqueeze(1).to_broadcast([128, 4, 2, 64]), in1=mz, op=ALU.mult),
                 reads=[tks, maskZ], writes=[BtZ])
            S.op("dve", lambda e: e.tensor_tensor(out=KtZ[:].rearrange("q p (a j) -> q p a j", a=2),
                                                   in0=tks[:, 1, :].unsqueeze(1).unsqueeze(1).to_broadcast([128, 4, 2, 64]), in1=mz, op=ALU.mult),
                 reads=[tks, maskZ], writes=[KtZ])
            for gi, (pb, Lh, R_, msk, dst) in enumerate(specs):
                S.op("dve", lambda e, pb=pb, msk=msk, dst=dst: e.tensor_tensor(out=dst[:], in0=pb.t, in1=msk[:], op=ALU.mult),
                     reads=[msk], writes=[dst, pb.bank])
            S.op("dve", lambda e: e.tensor_tensor(out=QTs[:], in0=L.QTp.t, in1=mUI[:], op=ALU.mult), reads=[mUI], writes=[QTs, L.QTp.bank])
            yield
            mm(L.mvp.t, L.mvp, L.MTs[:], L.MTs, Vt[:], Vt, True, first_write=True)
            yield
            S.op("act", lambda e: e.copy(out=G0[:, 64:128], in_=L.mvp.t), writes=[L.mvp.bank], pwrites=[G0])
            yield
            G = G0; Nk = N1; NTk = NT1
            gb = L.bg
            for lev in range(4):
                mm(gb[:, 0, :], gb, identB[:], identB, G[:], G, True, stop=False, first_write=True)
                mm(gb[:, 0, :], gb, NTk[:], NTk, G[:], G, False)
                if lev < 3:
                    mm(gb[:, 1, :], gb, NTk[:], NTk, Nk[:], Nk, True)
                    mm(gb[:, 2, :], gb, Nk[:], Nk, NTk[:], NTk, True)
                    yield
                    G2 = L.Gr.next(); N2 = L.Nr.next(); NT2 = L.NTr.next()
                    S.op("act", lambda e, G2=G2: e.copy(out=G2[:], in_=gb[:, 0, :]), writes=[G2, gb])
                    S.op("dve", lambda e, N2=N2: e.tensor_copy(out=N2[:], in_=gb[:, 1, :]), writes=[N2, gb])
                    S.op("act", lambda e, NT2=NT2: e.copy(out=NT2[:], in_=gb[:, 2, :]), writes=[NT2, gb])
                    G = G2; Nk = N2; NTk = NT2
                    yield
                else:
                    yield
                    S.op("dve", lambda e: e.tensor_copy(out=L.X1s[:], in_=gb[:, 0, 0:64]), writes=[L.X1s, gb])
                    S.op("act", lambda e: e.copy(out=U0[:], in_=gb[:, 0, 64:128]), writes=[U0, gb])
                    S.op("dve", lambda e: e.tensor_tensor(out=L.X1Z[:].rearrange("q p (a j) -> q p a j", a=2),
                                                           in0=L.X1s[:].unsqueeze(1).unsqueeze(1).to_broadcast([128, 4, 2, 64]), in1=mz, op=ALU.mult),
                         reads=[L.X1s, maskZ], writes=[L.X1Z])
                    yield
            bb = L.bb
            for p in range(4):
                mm(bb[:, p, :], bb, identB[:], identB, ZR[:, c, p, :], ZR, p == 0, stop=False, first_write=(p == 0))
                mm(bb[:, p, :], bb, L.X1Z[:, p, :], L.X1Z, PTs[:], PTs, False)
            ba = L.tokc.bank
            for p in range(4):
                mm(ba[:, p * 128:(p + 1) * 128], ba, L.X1Z[:, p, :], L.X1Z, BtZ[:, p, :], BtZ, p == 0, first_write=(p == 0))
            yield
            S.op("act", lambda e: e.copy(out=WyZ[:], in_=bb[:]), writes=[WyZ, bb])
            S.op("dve", lambda e: e.tensor_copy(out=WhT[:].rearrange("q p m -> q (p m)"), in_=ba[:]), writes=[WhT, ba])
            yield

        def state_stream(pc):
            c = pc["c"]; bt = pc["bt"]
            BtZ, KtZ, U0, Vt, PTs, QTs, WyZ, WhT = (pc[k] for k in ("BtZ", "KtZ", "U0", "Vt", "PTs", "QTs", "WyZ", "WhT"))
            for p in range(4):
                mm(WHp.t[:, p, :], WHp, BtZ[:, p, :], BtZ, U0[:], U0, p == 0, stop=False, first_write=(p == 0))
            for p in range(4):
                mm(WHp.t[:, p, :], WHp, KtZ[:, p, :], KtZ, Vt[:], Vt, False, stop=False)
            mm(Yp.t, Yp, PTs[:], PTs, U0[:], U0, False, stop=False)
            mm(Yp.t, Yp, QTs[:], QTs, Vt[:], Vt, False, stop=False)
            yield
            for p in range(4):
                mm(Yp.t, Yp, WyZ[:, p, :], WyZ, Hbf[:, p, :], Hbf, False, stop=(p == 3))
            for p in range(4):
                mm(WHp.t[:, p, :], WHp, WhT[:, p, :], WhT, Hbf[:, p, :], Hbf, False, stop=True)
            yield
            Dc = bt["Dc"]
            S.op("dve", lambda e: e.tensor_tensor(out=Hn[:], in0=WHp.t, in1=Hm[:], op=ALU.add), reads=[Hm], writes=[Hn, WHp.bank])
            S.op("dve", lambda e: e.tensor_tensor(out=Hm[:], in0=Hn[:], in1=Dc[:, c, :].unsqueeze(2).to_broadcast([128, 4, 64]), op=ALU.mult),
                 reads=[Hn, Dc], writes=[Hm])
            ysb = ysbR.next()
            pc["ysb"] = ysb
            S.op("act", lambda e: e.copy(out=ysb[:], in_=Yp.t), writes=[ysb, Yp.bank])
            S.op("act", lambda e: e.copy(out=Hbf[:], in_=Hm[:]), reads=[Hm], writes=[Hbf])
            yield

        def out_stream(pc):
            c = pc["c"]; bt = pc["bt"]; ysb = pc["ysb"]
            S.op("dve", lambda e: e.bn_stats(out=stats[:], in_=ysb[:]), reads=[ysb], writes=[stats])
            S.op("dve", lambda e: e.bn_aggr(out=mv[:], in_=stats[:]), reads=[stats], writes=[mv])
            yield
            S.op("act", lambda e: e.activation(out=rstd[:], in_=mv[:, 1:2], func=AF.Sqrt, bias=GN_EPS, scale=1.0), reads=[mv], writes=[rstd])
            yield
            S.op("dve", lambda e: e.reciprocal(out=rstd[:], in_=rstd[:]), reads=[rstd], writes=[rstd])
            S.op("dve", lambda e: e.tensor_scalar(out=yn[:], in0=ysb[:], scalar1=mv[:, 0:1], scalar2=rstd[:, 0:1], op0=ALU.subtract, op1=ALU.mult),
                 reads=[ysb, mv, rstd], writes=[yn])
            yield
            S.op("dve", lambda e: e.tensor_tensor(out=YZ[:].rearrange("q p (a j) -> q p a j", a=2),
                                                   in0=yn[:].unsqueeze(1).unsqueeze(1).to_broadcast([128, 4, 2, 64]),
                                                   in1=maskZ[:].unsqueeze(3).to_broadcast([128, 4, 2, 64]), op=ALU.mult),
                 reads=[yn, maskZ], writes=[YZ])
            yield
            for p in range(4):
                mm(yfp.t[:, p, :], yfp, YZ[:, p, :], YZ, Sel[:], Sel, True, first_write=(p == 0))
            yield
            yb = bt["yb"]
            S.op("act", lambda e: e.copy(out=yb[:, :, c * 16:(c + 1) * 16], in_=yfp.t), writes=([yb] if c == 0 else []) + [yfp.bank],
                 pwrites=() if c == 0 else [yb])
            if c == NCH - 1:
                post(bt)
            yield

        def post(bt):
            yb = bt["yb"]; bon = bt["bon"]; rzt = bt["rzt"]; t0 = bt["t0"]
            S.op("pool", lambda e: e.tensor_tensor(out=yb[:], in0=yb[:], in1=bc4(lg), op=ALU.mult), reads=[yb, lg], writes=[yb])
            S.op("pool", lambda e: e.tensor_tensor(out=yb[:], in0=yb[:], in1=bc4(lb), op=ALU.add), reads=[yb, lb], writes=[yb])
            S.op("pool", lambda e: e.tensor_tensor(out=yb[:], in0=yb[:], in1=bon[:], op=ALU.add), reads=[yb, bon], writes=[yb])
            S.op("pool", lambda e: e.tensor_tensor(out=yo[:], in0=yb[:], in1=rzt[:], op=ALU.mult), reads=[yb, rzt], writes=[yo])
            S.dma("sp", scr["ysT"].t[2, :, t0:t0 + TP].rearrange("(c p) t -> p c t", p=128), yo[:], reads=[yo], pwrites=[scr["ysT"]], key=yo)

        chunks = []
        for nb in range(SEQ // TP):
            for c in range(NCH):
                chunks.append((nb, c))
        bts = {}
        nxt = 0
        lane_gen = [None, None]
        lane_pc = [None, None]
        done_order = {}
        next_state = 0
        state_gen = None; state_pc = None
        out_q = []; out_gen = None
        n_total = len(chunks)
        finished_out = 0
        pcs = {}
        while finished_out < n_total:
            for li in range(2):
                if lane_gen[li] is None and nxt < n_total and nxt - next_state < 3:
                    nb, c = chunks[nxt]
                    if nb not in bts:
                        bts[nb] = prep(nb)
                    pc = {"idx": nxt}
                    pcs[nxt] = pc
                    lane_gen[li] = pre(bts[nb], c, lanes[li], pc)
                    lane_pc[li] = pc
                    nxt += 1
                if lane_gen[li] is not None:
                    try:
                        next(lane_gen[li])
                    except StopIteration:
                        done_order[lane_pc[li]["idx"]] = True
                        lane_gen[li] = None
            if state_gen is None and done_order.get(next_state):
                state_pc = pcs[next_state]
                state_gen = state_stream(state_pc)
            if state_gen is not None:
                try:
                    next(state_gen)
                except StopIteration:
                    out_q.append(state_pc)
                    state_gen = None
                    next_state += 1
            if out_gen is None and out_q:
                out_gen = out_stream(out_q.pop(0))
            if out_gen is not None:
                try:
                    next(out_gen)
                except StopIteration:
                    out_gen = None
                    finished_out += 1
        _barrier(S)
        S.stack_pop()


WSPEC = {
    "norm_g": [2, 1024], "w_in": [2, 1024, 9112], "cmp_w1": [2, 2, 32, 64, 128], "cmp_w2": [2, 2, 128, 64],
    "cmp_pe": [2, 2, 32, 64], "sg_ln_g": [2, 512], "sg_ln_b": [2, 512], "sg_w": [2, 8, 128, 128], "sg_b": [2, 8, 128],
    "rk_mu": [2, 1664], "rk_w0": [2, 512], "rk_w2": [2, 64, 512], "rk_a0": [2, 512], "rk_a2": [2, 64, 512],
    "rk_kk": [2, 8, 64], "rk_ka": [2, 8, 64], "rk_rk": [2, 8, 64], "rk_lnx_g": [2, 512], "rk_lnx_b": [2, 512],
    "w_branch": [2, 3, 512, 1024], "w_o": [2, 1024, 1024], "ple_norm_g": [2, 1024], "w_ple_gate": [2, 1024, 1024],
    "w_ple_proj": [2, 256, 1024], "final_norm_g": [1, 1024],
}


def build(SEQ, nlayers=2, enable=(1, 1, 1), scr_kind="Internal"):
    nc = bass.Bass("TRN2", target_bir_lowering=False)
    with contextlib.ExitStack() as stack:
        S = Sched(nc, stack)
        x = Buf("x", nc.dram_tensor("x", [SEQ, D], F32, kind="ExternalInput").ap())
        Wd = {"p": Buf("p", nc.dram_tensor("p", [2, SEQ, PLE], F32, kind="ExternalInput").ap())}
        for k, shp in WSPEC.items():
            Wd[k] = Buf(k, nc.dram_tensor(k, shp, F32, kind="ExternalInput").ap())
        out = Buf("out", nc.dram_tensor("out", [SEQ, D], F32, kind="ExternalOutput").ap())
        scr = make_scratch(S, SEQ, kind=scr_kind)
        xmid = S.dram("xmid", [SEQ, D], F32, kind=scr_kind)
        cur = x
        for lyr in range(nlayers):
            last = lyr == nlayers - 1
            dst = out if last else xmid
            phase_A(S, nc, SEQ, lyr, cur, Wd, scr)
            if enable[0]:
                phase_B(S, nc, SEQ, lyr, Wd, scr)
            if enable[1]:
                phase_C(S, nc, SEQ, lyr, Wd, scr)
            if enable[2]:
                phase_D(S, nc, SEQ, lyr, Wd, scr)
            phase_E(S, nc, SEQ, lyr, cur, dst, Wd, scr, final=(last and nlayers == 2))
            cur = dst
        S.emit()
    return nc


def phase_B(S, nc, SEQ, lyr, Wd, scr):
    NC = (SEQ - 32) // 16 + 1
    NT = (NC + 127) // 128
    NCp = NT * 128
    KT = SEQ // 128
    with contextlib.ExitStack() as st:
        S.stack_push(st)
        ident = make_ident(S, "B_ident")
        ksT = S.sb("B_ksT", [64, 2, SEQ], BF16)
        kwT = S.sb("B_kwT", [64, 2, SEQ], BF16)
        vs = S.sb("B_vs", [128, KT, 2, 65], BF16)
        vw = S.sb("B_vw", [128, KT, 2, 65], BF16)
        kcmpT = S.sb("B_kcmpT", [64, 2, NCp], BF16)
        Rc = S.sb("B_Rc", [128, NT, 2, 193], BF16)
        EXW = S.sb("B_EXW", [128, SEQ], BF16)
        S.dma("sp", ksT[:], scr["ksT"].t.rearrange("(g d) t -> d g t", g=2), reads=[scr["ksT"]], writes=[ksT])
        S.dma("sp", kwT[:], scr["kwT"].t.rearrange("(g d) t -> d g t", g=2), reads=[scr["kwT"]], writes=[kwT])
        S.op("pool", lambda e: e.memset(vs[:], 1.0), writes=[vs])
        S.op("pool", lambda e: e.memset(vw[:], 1.0), writes=[vw])
        for k0 in range(0, KT, 8):
            k1 = min(KT, k0 + 8)
            for (dst, c0) in ((vs, 0), (vw, 128)):
                for g in range(2):
                    S.dma("sp", dst[:, k0:k1, g, 0:64],
                          scr["vsw"].t[k0 * 128:k1 * 128, c0 + g * 64:c0 + (g + 1) * 64].rearrange("(k p) d -> p k d", p=128),
                          reads=[scr["vsw"]], pwrites=[dst], key=dst)
        S.op("pool", lambda e: e.memset(EXW[:], 1.0), writes=[EXW])
        S.op("pool", lambda e: e.affine_select(out=EXW[:], in_=EXW[:], pattern=[[1, SEQ]], compare_op=ALU.is_ge, fill=0.0,
                                               base=0, channel_multiplier=-64), reads=[EXW], writes=[EXW])
        S.op("pool", lambda e: e.affine_select(out=EXW[:], in_=EXW[:], pattern=[[-1, SEQ]], compare_op=ALU.is_ge, fill=0.0,
                                               base=63, channel_multiplier=64), reads=[EXW], writes=[EXW])
        S.op("pool", lambda e: e.memset(Rc[:], 1.0), writes=[Rc])
        for nt in range(NT):
            for g in range(2):
                S.op("pool", lambda e, nt=nt, g=g: e.affine_select(
                    out=Rc[:, nt, g, 65:193], in_=Rc[:, nt, g, 65:193], pattern=[[-4, 128]], compare_op=ALU.is_ge, fill=0.0,
                    base=nt * 128 + 1, channel_multiplier=1), reads=[Rc], writes=[Rc])
                S.op("pool", lambda e, nt=nt, g=g: e.affine_select(
                    out=Rc[:, nt, g, 65:193], in_=Rc[:, nt, g, 65:193], pattern=[[4, 128]], compare_op=ALU.is_ge, fill=0.0,
                    base=3 - nt * 128, channel_multiplier=-1), reads=[Rc], writes=[Rc])
        npad = NCp - NC
        if npad:
            S.op("pool", lambda e: e.affine_select(
                out=Rc[:, NT - 1, :, :], in_=Rc[:, NT - 1, :, :], pattern=[[0, 2 * 193]], compare_op=ALU.is_ge, fill=0.0,
                base=(NC - 1) - (NT - 1) * 128, channel_multiplier=-1), reads=[Rc], writes=[Rc])
        S.op("pool", lambda e: e.memset(kcmpT[:], 0.0), writes=[kcmpT])

        with contextlib.ExitStack() as st2:
            S.stack_push(st2)
            kvT = S.sb("B_kvT", [64, 2, SEQ], BF16)
            w1 = S.sb("B_w1", [64, 32, 128], BF16)
            w2 = S.sb("B_w2", [128, 64], BF16)
            peT = S.sb("B_peT", [64, 32])
            peTb = S.sb("B_peTb", [64, 32], BF16)
            cb = S.sb("B_cb", [128, 1])
            hid = S.sb("B_hid", [128, NCp], BF16)
            ph = S.ps("B_ph", [128, 512])
            pc1 = S.ps("B_pc1", [128, 512])
            pk = S.ps("B_pk", [128, 512])
            for kv in range(2):
                src = scr["kcT"] if kv == 0 else scr["vcT"]
                S.dma("sp", kvT[:], src.t.rearrange("(g d) t -> d g t", g=2), reads=[src], writes=[kvT])
                S.dma("pool", w1[:], Wd["cmp_w1"].t[lyr, kv].rearrange("l d h -> d l h"), reads=[Wd["cmp_w1"]], writes=[w1])
                S.dma("pool", w2[:], Wd["cmp_w2"].t[lyr, kv], reads=[Wd["cmp_w2"]], writes=[w2])
                S.dma("sp", peT[:], Wd["cmp_pe"].t[lyr, kv].rearrange("l d -> d l"), reads=[Wd["cmp_pe"]], writes=[peT],
                      allow_slow_non_contiguous=True)
                S.op("dve", lambda e: e.tensor_copy(out=peTb[:], in_=peT[:]), reads=[peT], writes=[peTb])
                for l in range(32):
                    S.op("pe", lambda e, l=l: e.matmul(pc1[:, 0:1], lhsT=w1[:, l, :], rhs=peTb[:, l:l + 1], start=(l == 0), stop=(l == 31)),
                         reads=[w1, peTb], writes=[pc1] if l == 0 else (), pwrites=() if l == 0 else [pc1])
                S.op("dve", lambda e: e.tensor_copy(out=cb[:], in_=pc1[:, 0:1]), reads=[pc1], writes=[cb])
                for g in range(2):
                    S.op("dve", lambda e: e.memset(hid[:], 0.0), writes=[hid])
                    for n0 in range(0, NC, 512):
                        nn = min(512, NC - n0)
                        for l in range(32):
                            S.op("pe", lambda e, l=l, g=g, n0=n0, nn=nn: e.matmul(
                                ph[:, 0:nn], lhsT=w1[:, l, :], rhs=kvT[:, g, n0 * 16 + l: n0 * 16 + l + (nn - 1) * 16 + 1: 16], start=(l == 0), stop=(l == 31)),
                                reads=[w1, kvT], writes=[ph] if l == 0 else (), pwrites=() if l == 0 else [ph])
                        S.op("act", lambda e, n0=n0, nn=nn: e.activation(out=hid[:, n0:n0 + nn], in_=ph[:, 0:nn], func=AF.Silu, bias=cb[:, 0:1]),
                             reads=[ph, cb], pwrites=[hid])
                    if kv == 0:
                        for n0 in range(0, NC, 512):
                            nn = min(512, NC - n0)
                            S.op("pe", lambda e, n0=n0, nn=nn: e.matmul(pk[0:64, 0:nn], lhsT=w2[:], rhs=hid[:, n0:n0 + nn], start=True, stop=True),
                                 reads=[w2, hid], writes=[pk])
                            S.op("dve", lambda e, g=g, n0=n0, nn=nn: e.tensor_copy(out=kcmpT[:, g, n0:n0 + nn], in_=pk[0:64, 0:nn]),
                                 reads=[pk], pwrites=[kcmpT])
                    else:
                        for nt in range(NT):
                            rows = min(128, NC - nt * 128)
                            S.op("pe", lambda e, nt=nt: e.matmul(pk[:, 0:64], lhsT=hid[:, nt * 128:(nt + 1) * 128], rhs=w2[:], start=True, stop=True),
                                 reads=[w2, hid], writes=[pk])
                            S.op("dve", lambda e, g=g, nt=nt: e.tensor_copy(out=Rc[:, nt, g, 0:64], in_=pk[:, 0:64]),
                                 reads=[pk], pwrites=[Rc])
            _barrier(S)
            S.stack_pop()

        qt = Ring([S.sb("B_q%d" % i, [64, 8, 128], BF16) for i in range(2)])
        gt = Ring([S.sb("B_g%d" % i, [128, 24]) for i in range(2)])
        nzt = Ring([S.sb("B_nz%d" % i, [128, 512], BF16) for i in range(2)])
        Et = Ring([S.sb("B_E%d" % i, [128, 512], BF16) for i in range(4)])
        psT = Ring([S.ps("B_psT%d" % i, [128, 512]) for i in range(3)])
        pcA = S.ps("B_pcA", [128, 2, 193])
        pcB = S.ps("B_pcB", [128, 2, 193])
        pos = S.ps("B_pos", [128, 4, 65])
        pow_ = S.ps("B_pow", [128, 4, 65])
        pmisc = S.ps("B_pmisc", [128, 4, 128], BF16)
        oc = S.sb("B_oc", [128, 4, 193])
        rcs = S.sb("B_rcs", [128, 4])
        rss = S.sb("B_rss", [128, 4])
        rws = S.sb("B_rws", [128, 4])
        cc = S.sb("B_cc", [128, 3, 4])
        sc = S.sb("B_sc", [128, 128])
        sc2 = S.sb("B_sc2", [128, 128])
        m1 = S.sb("B_m1", [128, 8])
        m2 = S.sb("B_m2", [128, 8])
        negq = S.sb("B_negq", [128, 128], BF16)
        negT4 = S.sb("B_negT4", [128, 4, 128], BF16)
        yg = S.sb("B_yg", [128, 4, 64])
        ytmp = S.sb("B_ytmp", [128, 4, 64])
        ynsa = S.sb("B_ynsa", [128, 512], BF16)
        stg = Ring([S.sb("B_stg%d" % i, [128, 4, 128], BF16) for i in range(2)])

        def qk_exp(kT_ap, kbuf, q_ap, qbuf, neg_lhsT=None):
            p = psT.next()
            if neg_lhsT is not None:
                S.op("pe", lambda e, p=p: e.matmul(p[:], lhsT=neg_lhsT, rhs=negT4[:].rearrange("p h q -> p (h q)"), start=True, stop=False),
                     reads=[EXW, negT4], writes=[p])
                S.op("pe", lambda e, p=p: e.matmul(p[:], lhsT=kT_ap, rhs=q_ap, start=False, stop=True), reads=[kbuf, qbuf], pwrites=[p])
            else:
                S.op("pe", lambda e, p=p: e.matmul(p[:], lhsT=kT_ap, rhs=q_ap, start=True, stop=True), reads=[kbuf, qbuf], writes=[p])
            E = Et.next()
            S.op("act", lambda e, p=p, E=E: e.activation(out=E[:], in_=p[:], func=AF.Exp), reads=[p], writes=[E])
            return E

        def pipeline(tiles, L=2):
            Es = {}
            n = len(tiles)
            for i in range(n + L):
                if i < n:
                    Es[i] = tiles[i][0]()
                if i - L >= 0:
                    tiles[i - L][1](Es.pop(i - L))

        def mask(E, base, cm, qstep):
            S.op("pool", lambda e, E=E: e.affine_select(out=E[:], in_=E[:], pattern=[[0, 4], [qstep, 128]], compare_op=ALU.is_ge,
                                                       fill=0.0, base=base, channel_multiplier=cm), reads=[E], writes=[E])

        for qb in range(SEQ // 128):
            q0 = qb * 128
            q = qt.next(); gg = gt.next(); nz = nzt.next()
            S.dma("sp", q[:], scr["qT"].t[:, q0:q0 + 128].rearrange("(h d) t -> d h t", h=8), reads=[scr["qT"]], writes=[q])
            S.dma("sp", gg[:], scr["gate"].t[q0:q0 + 128, :], reads=[scr["gate"]], writes=[gg])
            S.dma("sp", nz[:], scr["nzs"].t[q0:q0 + 128, :], reads=[scr["nzs"]], writes=[nz])
            for g in range(2):
                q_ap = q[:, 4 * g:4 * g + 4, :].rearrange("d h q -> d (h q)")
                n_max = min(8 * qb + 6, NC - 1)
                ntl = n_max // 128 + 1
                def c_qk(nt, g=g, q_ap=q_ap, q=q):
                    E = qk_exp(kcmpT[:, g, nt * 128:(nt + 1) * 128], kcmpT, q_ap, q)
                    if q0 - 16 * (128 * nt + 127) - 31 < 0:
                        mask(E, q0 - 16 * 128 * nt - 31, -16, 1)
                    return E

                def c_pv(nt, E, g=g, ntl=ntl):
                    for h in range(4):
                        pcx = pcA if h < 2 else pcB
                        first = (nt == 0 and h % 2 == 0)
                        S.op("pe", lambda e, E=E, h=h, pcx=pcx, nt=nt, first=first, g=g, ntl=ntl: e.matmul(
                            pcx[:, h % 2, :], lhsT=E[:, h * 128:(h + 1) * 128], rhs=Rc[:, nt, g, :], start=first,
                            stop=(nt == ntl - 1 and h % 2 == 1), skip_group_check=True),
                            reads=[E, Rc], writes=[pcx] if first else (), pwrites=() if first else [pcx])
                pipeline([(lambda nt=nt: c_qk(nt), lambda E, nt=nt: c_pv(nt, E)) for nt in range(ntl)])
                S.op("act", lambda e: e.copy(out=oc[:, 0:2, :], in_=pcA[:]), reads=[pcA], pwrites=[oc])
                S.op("act", lambda e: e.copy(out=oc[:, 2:4, :], in_=pcB[:]), reads=[pcB], pwrites=[oc])
                S.op("dve", lambda e: e.tensor_scalar(out=rcs[:], in0=oc[:, :, 64], scalar1=1e-30, scalar2=None, op0=ALU.max),
                     reads=[oc], writes=[rcs])
                S.op("dve", lambda e: e.reciprocal(out=rcs[:], in_=rcs[:]), reads=[rcs], writes=[rcs])
                S.op("dve", lambda e: e.tensor_scalar(out=sc[:], in0=oc[:, 0, 65:193], scalar1=rcs[:, 0:1], scalar2=None, op0=ALU.mult),
                     reads=[oc, rcs], writes=[sc])
                for h in range(1, 4):
                    S.op("dve", lambda e, h=h: e.scalar_tensor_tensor(out=sc[:], in0=oc[:, h, 65:193], scalar=rcs[:, h:h + 1], in1=sc[:],
                                                                      op0=ALU.mult, op1=ALU.add), reads=[oc, rcs, sc], writes=[sc])
                for half in range(2):
                    tb = 2 * qb + half
                    ps_ = slice(half * 64, (half + 1) * 64)
                    if tb + 1 < 128:
                        S.op("dve", lambda e, ps_=ps_, tb=tb: e.memset(sc[ps_, tb + 1:128], -1e4), reads=[sc], writes=[sc])
                    lo = max(tb - 1, 0)
                    S.op("dve", lambda e, ps_=ps_, tb=tb, lo=lo: e.memset(sc[ps_, lo:tb + 1], 1e4), reads=[sc], writes=[sc])
                S.op("dve", lambda e: e.memset(sc[:, 0:1], 1e4), reads=[sc], writes=[sc])
                S.op("dve", lambda e: e.max(out=m1[:], in_=sc[:]), reads=[sc], writes=[m1])
                S.op("dve", lambda e: e.match_replace(out=sc2[:], in_to_replace=m1[:], in_values=sc[:], imm_value=-3e4),
                     reads=[sc, m1], writes=[sc2])
                S.op("dve", lambda e: e.max(out=m2[:], in_=sc2[:]), reads=[sc2], writes=[m2])
                S.op("dve", lambda e: e.tensor_scalar(out=negq[:], in0=sc[:], scalar1=m2[:, 7:8], scalar2=-1e4, op0=ALU.is_lt, op1=ALU.mult),
                     reads=[sc, m2], writes=[negq])
                S.op("pe", lambda e: e.transpose(out=pmisc[:, 0, :], in_=negq[:], identity=ident[:]), reads=[negq, ident], writes=[pmisc])
                S.op("dve", lambda e: e.tensor_copy(out=negT4[:], in_=pmisc[:, 0:1, :].to_broadcast([128, 4, 128])),
                     reads=[pmisc], writes=[negT4])
                kts = list(range(max(0, qb - 4), qb + 1))

                def w_qk(i, kt, g=g, q_ap=q_ap, q=q, qb=qb):
                    E = qk_exp(kwT[:, g, kt * 128:(kt + 1) * 128], kwT, q_ap, q)
                    if kt == qb - 4:
                        mask(E, -1, 1, -1)
                    if kt == qb:
                        mask(E, 0, -1, 1)
                    return E

                def w_pv(i, kt, E, g=g, kts=kts):
                    for h in range(4):
                        first = (i == 0 and h == 0)
                        S.op("pe", lambda e, E=E, h=h, kt=kt, first=first, last=(i == len(kts) - 1 and h == 3), g=g: e.matmul(
                            pow_[:, h, :], lhsT=E[:, h * 128:(h + 1) * 128], rhs=vw[:, kt, g, :], start=first, stop=last,
                            skip_group_check=True),
                            reads=[E, vw], writes=[pow_] if first else (), pwrites=() if first else [pow_])
                def s_qk(kt, g=g, q_ap=q_ap, q=q, qb=qb):
                    E = qk_exp(ksT[:, g, kt * 128:(kt + 1) * 128], ksT, q_ap, q, neg_lhsT=EXW[:, kt * 128:(kt + 1) * 128])
                    if kt == qb:
                        mask(E, 0, -1, 1)
                    return E

                def s_pv(kt, E, g=g, qb=qb):
                    for h in range(4):
                        first = (kt == 0 and h == 0)
                        S.op("pe", lambda e, E=E, h=h, kt=kt, first=first, last=(kt == qb and h == 3), g=g: e.matmul(
                            pos[:, h, :], lhsT=E[:, h * 128:(h + 1) * 128], rhs=vs[:, kt, g, :], start=first, stop=last,
                            skip_group_check=True),
                            reads=[E, vs], writes=[pos] if first else (), pwrites=() if first else [pos])
                pipeline([(lambda i=i, kt=kt: w_qk(i, kt), lambda E, i=i, kt=kt: w_pv(i, kt, E)) for i, kt in enumerate(kts)] +
                         [(lambda kt=kt: s_qk(kt), lambda E, kt=kt: s_pv(kt, E)) for kt in range(qb + 1)])
                S.op("dve", lambda e: e.reciprocal(out=rss[:], in_=pos[:, :, 64]), reads=[pos], writes=[rss])
                S.op("dve", lambda e: e.reciprocal(out=rws[:], in_=pow_[:, :, 64]), reads=[pow_], writes=[rws])
                gv = gg[:, g * 12:(g + 1) * 12].rearrange("p (h b) -> p b h", b=3)
                for b, rr in enumerate((rcs, rss, rws)):
                    S.op("dve", lambda e, b=b, rr=rr, gv=gv: e.tensor_tensor(out=cc[:, b, :], in0=gv[:, b, :], in1=rr[:], op=ALU.mult),
                         reads=[gg, rr], pwrites=[cc])
                if qb == 2:
                    dbg_dump(S, "oc%d" % g, oc[:], oc, [128, 4, 193])
                    dbg_dump(S, "pos%d" % g, pos[:], pos, [128, 4, 65])
                    dbg_dump(S, "pow%d" % g, pow_[:], pow_, [128, 4, 65])
                    dbg_dump(S, "cc%d" % g, cc[:], cc, [128, 3, 4])
                    dbg_dump(S, "gg%d" % g, gg[:], gg, [128, 24])
                    dbg_dump(S, "sc%d" % g, sc[:], sc, [128, 128])
                    dbg_dump(S, "negq%d" % g, negq[:], negq, [128, 128])
                bc = lambda b: cc[:, b, :].unsqueeze(2).to_broadcast([128, 4, 64])
                S.op("dve", lambda e: e.tensor_tensor(out=yg[:], in0=oc[:, :, 0:64], in1=bc(0), op=ALU.mult), reads=[oc, cc], writes=[yg])
                S.op("dve", lambda e: e.tensor_tensor(out=ytmp[:], in0=pos[:, :, 0:64], in1=bc(1), op=ALU.mult), reads=[pos, cc], writes=[ytmp])
                S.op("pool", lambda e: e.tensor_tensor(out=yg[:], in0=yg[:], in1=ytmp[:], op=ALU.add), reads=[yg, ytmp], writes=[yg])
                S.op("dve", lambda e: e.tensor_tensor(out=ytmp[:], in0=pow_[:, :, 0:64], in1=bc(2), op=ALU.mult), reads=[pow_, cc], writes=[ytmp])
                S.op("pool", lambda e: e.tensor_tensor(out=yg[:], in0=yg[:], in1=ytmp[:], op=ALU.add), reads=[yg, ytmp], writes=[yg])
                if qb == 2:
                    dbg_dump(S, "yg%d" % g, yg[:], yg, [128, 4, 64])
                S.op("pool", lambda e, g=g, nz=nz: e.tensor_tensor(out=ynsa[:, g * 256:(g + 1) * 256], in0=yg[:].rearrange("p h d -> p (h d)"),
                                                                   in1=nz[:, g * 256:(g + 1) * 256], op=ALU.mult),
                     reads=[yg, nz], pwrites=[ynsa])
            for k in range(4):
                S.op("pe", lambda e, k=k: e.transpose(out=pmisc[:, k, :], in_=ynsa[:, k * 128:(k + 1) * 128], identity=ident[:]),
                     reads=[ynsa, ident], writes=[pmisc] if k == 0 else (), pwrites=() if k == 0 else [pmisc])
            sg = stg.next()
            S.op("act", lambda e, sg=sg: e.copy(out=sg[:], in_=pmisc[:]), reads=[pmisc], writes=[sg])
            S.dma("sp", scr["ysT"].t[0, :, q0:q0 + 128].rearrange("(k p) t -> p k t", p=128), sg[:], reads=[sg], pwrites=[scr["ysT"]], key=sg)
        _barrier(S)
        S.stack_pop()


def phase_D_seq(S, nc, SEQ, lyr, Wd, scr):
    TP = 128
    TB = 8
    GN_EPS = 64e-5
    xtok = scr["xtok"]
    with contextlib.ExitStack() as st:
        S.stack_push(st)
        identF = make_ident(S, "D_ident", F32)
        ones = S.sb("D_ones", [128, 128])
        S.op("pool", lambda e: e.memset(ones[:], 0.0), writes=[ones])
        S.op("pool", lambda e: e.memset(ones[0:64, 0:64], 1.0), reads=[ones], writes=[ones])
        S.op("pool", lambda e: e.memset(ones[64:128, 64:128], 1.0), reads=[ones], writes=[ones])

        def cvec(name, key, n):
            t = S.sb("D_" + name, [128, n])
            S.dma("sp", t[:], Wd[key].t[lyr].rearrange("(c p) -> p c", p=128), reads=[Wd[key]], writes=[t],
                  allow_slow_non_contiguous=True)
            return t

        def cvec2(name, key):
            t = S.sb("D_" + name, [128, 4])
            S.dma("sp", t[:], Wd[key].t[lyr].rearrange("(c a) j -> (a j) c", a=2), reads=[Wd[key]], writes=[t],
                  allow_slow_non_contiguous=True)
            return t
        mu = cvec("mu", "rk_mu", 13)
        w0 = cvec("w0", "rk_w0", 4)
        a0 = cvec("a0", "rk_a0", 4)
        lg = cvec("lg", "rk_lnx_g", 4)
        lb = cvec("lb", "rk_lnx_b", 4)
        kkc = cvec2("kkc", "rk_kk")
        ka = cvec2("ka", "rk_ka")
        rkc = cvec2("rkc", "rk_rk")
        omka = S.sb("D_omka", [128, 4])
        S.op("pool", lambda e: e.tensor_scalar(out=omka[:], in0=ka[:], scalar1=-1.0, scalar2=1.0, op0=ALU.mult, op1=ALU.add),
             reads=[ka], writes=[omka])
        w2 = S.sb("D_w2", [64, 512], BF16)
        a2 = S.sb("D_a2", [128, 512], BF16)
        S.dma("pool", w2[:], Wd["rk_w2"].t[lyr], reads=[Wd["rk_w2"]], writes=[w2])
        S.dma("pool", a2[64:128, :], Wd["rk_a2"].t[lyr], reads=[Wd["rk_a2"]], writes=[a2])
        St = S.sb("D_state", [128, 4, 64])
        S.op("dve", lambda e: e.memset(St[:], 0.0), writes=[St])

        rst = S.sb("D_rst", [128, 13, TP + 1])
        xs = S.sb("D_xs", [128, 13, TP])
        th = S.sb("D_th", [128, TP], BF16)
        dd = S.sb("D_dd", [128, 4, TP])
        aa = S.sb("D_aa", [128, 4, TP])
        kkf = S.sb("D_kkf", [128, 4, TP])
        sq = S.sb("D_sq", [128, 4, TP])
        rn = S.sb("D_rn", [128, 4, TP])
        kp = S.sb("D_kp", [128, 4, TP])
        am = S.sb("D_am", [128, 4, TP])
        bm = S.sb("D_bm", [128, 4, TP])
        t1 = S.sb("D_t1", [128, 4, TP])
        bonus = S.sb("D_bonus", [128, 4, TP])
        vv = S.sb("D_vv", [128, 4, TP])
        tk = S.sb("D_tk", [128, 5, 4, 128])
        bcr = Ring([S.sb("D_bc%d" % i, [128, TB, 5, 256]) for i in range(2)])
        tmp = S.sb("D_tmp", [128, 4, 64])
        tmp2 = S.sb("D_tmp2", [128, 4, 64])
        kv = Ring([S.sb("D_kv%d" % i, [128, 4, 64]) for i in range(2)])
        sa = S.sb("D_sa", [128, 4])
        ybuf = S.sb("D_y", [128, 4, TP])
        ysq = S.sb("D_ysq", [128, 4, TP])
        mean = S.sb("D_mean", [128, 4, TP])
        var = S.sb("D_var", [128, 4, TP])
        rzt = S.sb("D_rz", [128, 4, TP], BF16)
        yo = S.sb("D_yo", [128, 4, TP], BF16)
        pa = Ring([S.ps("D_pa%d" % i, [128, 4, 128]) for i in range(4)])

        bc4 = lambda t: t[:].unsqueeze(2).to_broadcast([128, 4, TP])
        for nb in range(SEQ // TP):
            t0 = nb * TP
            S.dma("sp", rst[:, :, 1:TP + 1], scr["rsT"].t[:, t0:t0 + TP].rearrange("(c p) t -> p c t", p=128), reads=[scr["rsT"]],
                  writes=[rst])
            if nb == 0:
                S.op("pool", lambda e: e.memset(rst[:, :, 0:1], 0.0), reads=[rst], pwrites=[rst])
            else:
                S.dma("sp", rst[:, :, 0:1], scr["rsT"].t[:, t0 - 1:t0].rearrange("(c p) t -> p c t", p=128), reads=[scr["rsT"]],
                      pwrites=[rst], key=rst, allow_slow_non_contiguous=True)
            S.op("pool", lambda e: e.tensor_tensor(out=xs[:], in0=rst[:, :, 0:TP], in1=rst[:, :, 1:TP + 1], op=ALU.subtract),
                 reads=[rst], writes=[xs])
            S.op("pool", lambda e: e.tensor_tensor(out=xs[:], in0=xs[:], in1=mu[:].unsqueeze(2).to_broadcast([128, 13, TP]), op=ALU.mult),
                 reads=[xs, mu], writes=[xs])
            S.op("pool", lambda e: e.tensor_tensor(out=xs[:], in0=xs[:], in1=rst[:, :, 1:TP + 1], op=ALU.add), reads=[xs, rst], writes=[xs])
            r = xs[:, 0:4, :]; k = xs[:, 4:8, :]; v = xs[:, 8:12, :]
            S.op("act", lambda e: e.activation(out=th[0:64, :], in_=xs[0:64, 12, :], func=AF.Tanh), reads=[xs], pwrites=[th])
            S.op("act", lambda e: e.copy(out=th[64:128, :], in_=xs[64:128, 12, :]), reads=[xs], pwrites=[th])
            pw = pa.next(); pp = pa.next()
            for p in range(4):
                S.op("pe", lambda e, p=p, pw=pw: e.matmul(pw[:, p, :], lhsT=w2[0:64, p * 128:(p + 1) * 128], rhs=th[0:64, :], start=True, stop=True),
                     reads=[w2, th], writes=[pw] if p == 0 else (), pwrites=() if p == 0 else [pw])
                S.op("pe", lambda e, p=p, pp=pp: e.matmul(pp[:, p, :], lhsT=a2[64:128, p * 128:(p + 1) * 128], rhs=th[64:128, :], start=True, stop=True),
                     reads=[a2, th], writes=[pp] if p == 0 else (), pwrites=() if p == 0 else [pp])
            for p in range(4):
                S.op("act", lambda e, p=p, pw=pw: e.activation(out=dd[:, p, :], in_=pw[:, p, :], func=AF.Sigmoid, bias=w0[:, p:p + 1]),
                     reads=[pw, w0], pwrites=[dd])
                S.op("act", lambda e, p=p, pp=pp: e.activation(out=aa[:, p, :], in_=pp[:, p, :], func=AF.Sigmoid, bias=a0[:, p:p + 1]),
                     reads=[pp, a0], pwrites=[aa])
            S.op("act", lambda e: e.activation(out=dd[:], in_=dd[:], func=AF.Exp, scale=-0.6065306597126334), reads=[dd], writes=[dd])
            S.op("pool", lambda e: e.tensor_tensor(out=kkf[:], in0=k, in1=bc4(kkc), op=ALU.mult), reads=[xs, kkc], writes=[kkf])
            S.op("pool", lambda e: e.tensor_tensor(out=sq[:], in0=kkf[:], in1=kkf[:], op=ALU.mult), reads=[kkf], writes=[sq])
            pn = pa.next()
            for p in range(4):
                S.op("pe", lambda e, p=p, pn=pn: e.matmul(pn[:, p, :], lhsT=ones[:], rhs=sq[:, p, :], start=True, stop=True),
                     reads=[ones, sq], writes=[pn] if p == 0 else (), pwrites=() if p == 0 else [pn])
            S.op("act", lambda e, pn=pn: e.activation(out=rn[:], in_=pn[:], func=AF.Sqrt), reads=[pn], writes=[rn])
            S.op("pool", lambda e: e.tensor_scalar(out=rn[:], in0=rn[:], scalar1=1e-12, scalar2=None, op0=ALU.max), reads=[rn], writes=[rn])
            S.op("dve", lambda e: e.reciprocal(out=rn[:], in_=rn[:]), reads=[rn], writes=[rn])
            S.op("pool", lambda e: e.tensor_tensor(out=kkf[:], in0=kkf[:], in1=rn[:], op=ALU.mult), reads=[kkf, rn], writes=[kkf])
            S.op("pool", lambda e: e.tensor_tensor(out=t1[:], in0=aa[:], in1=bc4(ka), op=ALU.mult), reads=[aa, ka], writes=[t1])
            S.op("pool", lambda e: e.tensor_tensor(out=t1[:], in0=t1[:], in1=bc4(omka), op=ALU.add), reads=[t1, omka], writes=[t1])
            S.op("pool", lambda e: e.tensor_tensor(out=kp[:], in0=k, in1=t1[:], op=ALU.mult), reads=[xs, t1], writes=[kp])
            S.op("pool", lambda e: e.tensor_scalar(out=am[:], in0=kkf[:], scalar1=-1.0, scalar2=None, op0=ALU.mult), reads=[kkf], writes=[am])
            S.op("pool", lambda e: e.tensor_tensor(out=bm[:], in0=kkf[:], in1=aa[:], op=ALU.mult), reads=[kkf, aa], writes=[bm])
            S.op("pool", lambda e: e.tensor_tensor(out=t1[:], in0=r, in1=kp[:], op=ALU.mult), reads=[xs, kp], writes=[t1])
            S.op("pool", lambda e: e.tensor_tensor(out=sq[:], in0=t1[:], in1=bc4(rkc), op=ALU.mult), reads=[t1, rkc], writes=[sq])
            pr = pa.next()
            for p in range(4):
                S.op("pe", lambda e, p=p, pr=pr: e.matmul(pr[:, p, :], lhsT=ones[:], rhs=sq[:, p, :], start=True, stop=True),
                     reads=[ones, sq], writes=[pr] if p == 0 else (), pwrites=() if p == 0 else [pr])
            S.op("act", lambda e, pr=pr: e.copy(out=bonus[:], in_=pr[:]), reads=[pr], writes=[bonus])
            S.op("pool", lambda e: e.tensor_tensor(out=bonus[:], in0=bonus[:], in1=v, op=ALU.mult), reads=[bonus, xs], writes=[bonus])
            S.op("pool", lambda e: e.tensor_copy(out=vv[:], in_=v), reads=[xs], writes=[vv])
            S.op("pool", lambda e: e.tensor_copy(out=t1[:], in_=r), reads=[xs], writes=[t1])
            for oi, src in enumerate((am, bm, dd, kp, t1)):
                pt = pa.next()
                for p in range(4):
                    S.op("pe", lambda e, p=p, src=src, pt=pt: e.transpose(out=pt[:, p, :], in_=src[:, p, :], identity=identF[:]),
                         reads=[src, identF], writes=[pt] if p == 0 else (), pwrites=() if p == 0 else [pt])
                S.op("act", lambda e, oi=oi, pt=pt: e.copy(out=tk[:, oi, :, :], in_=pt[:]), reads=[pt], pwrites=[tk])
            for oi in range(5):
                for h2 in range(2):
                    S.dma("sp", xtok.t[t0:t0 + TP, oi, h2, :].rearrange("t (p j) -> t p j", p=4), tk[:, oi, :, h2 * 64:(h2 + 1) * 64],
                          reads=[tk], pwrites=[xtok], key=tk)
            S.dma("sp", rzt[:], scr["rzT"].t[:, t0:t0 + TP].rearrange("(c p) t -> p c t", p=128), reads=[scr["rzT"]], writes=[rzt])
            xflat = xtok.t.rearrange("t o h c -> (t o) h c")
            for tb in range(0, TP, TB):
                bc = bcr.next()
                for h2 in range(2):
                    S.dma("sp", bc[h2 * 64:(h2 + 1) * 64, :, :, :].rearrange("p t o c -> p (t o) c"),
                          xflat[(t0 + tb) * 5:(t0 + tb + TB) * 5, h2, :].partition_broadcast(64),
                          reads=[xtok], writes=[bc] if h2 == 0 else (), pwrites=() if h2 == 0 else [bc], key=bc)
                for tt in range(TB):
                    t = tb + tt
                    A = bc[:, tt, 0, :].rearrange("p (a j) -> p a j", a=4)
                    B = bc[:, tt, 1, :].rearrange("p (a j) -> p a j", a=4)
                    Dd = bc[:, tt, 2, :].rearrange("p (a j) -> p a j", a=4)
                    Kk = bc[:, tt, 3, :].rearrange("p (a j) -> p a j", a=4)
                    R = bc[:, tt, 4, :].rearrange("p (a j) -> p a j", a=4)
                    kvb = kv.next()
                    S.op("pool", lambda e, Kk=Kk, t=t, kvb=kvb: e.tensor_tensor(out=kvb[:], in0=Kk, in1=vv[:, :, t:t + 1].to_broadcast([128, 4, 64]),
                                                                              op=ALU.mult), reads=[bc, vv], writes=[kvb])
                    S.op("dve", lambda e, A=A: e.tensor_tensor(out=tmp[:], in0=St[:], in1=A, op=ALU.mult), reads=[St, bc], writes=[tmp])
                    S.op("dve", lambda e: e.tensor_reduce(out=sa[:], in_=tmp[:], axis=AX.X, op=ALU.add), reads=[tmp], writes=[sa])
                    S.op("dve", lambda e, Dd=Dd: e.tensor_tensor(out=St[:], in0=St[:], in1=Dd, op=ALU.mult), reads=[St, bc, tmp], writes=[St])
                    S.op("dve", lambda e, B=B: e.tensor_tensor(out=tmp2[:], in0=B, in1=sa[:].unsqueeze(2).to_broadcast([128, 4, 64]), op=ALU.mult),
                         reads=[bc, sa], writes=[tmp2])
                    S.op("dve", lambda e: e.tensor_tensor(out=St[:], in0=St[:], in1=tmp2[:], op=ALU.add), reads=[St, tmp2], writes=[St])
                    S.op("dve", lambda e, kvb=kvb: e.tensor_tensor(out=St[:], in0=St[:], in1=kvb[:], op=ALU.add), reads=[St, kvb], writes=[St])
                    S.op("dve", lambda e, R=R: e.tensor_tensor(out=tmp[:], in0=St[:], in1=R, op=ALU.mult), reads=[St, bc], writes=[tmp])
                    S.op("dve", lambda e, t=t: e.tensor_reduce(out=ybuf[:, :, t], in_=tmp[:], axis=AX.X, op=ALU.add), reads=[tmp], pwrites=[ybuf])
            S.op("pool", lambda e: e.tensor_tensor(out=ysq[:], in0=ybuf[:], in1=ybuf[:], op=ALU.mult), reads=[ybuf], writes=[ysq])
            pm = pa.next(); pq = pa.next()
            for p in range(4):
                S.op("pe", lambda e, p=p, pm=pm: e.matmul(pm[:, p, :], lhsT=ones[:], rhs=ybuf[:, p, :], start=True, stop=True),
                     reads=[ones, ybuf], writes=[pm] if p == 0 else (), pwrites=() if p == 0 else [pm])
                S.op("pe", lambda e, p=p, pq=pq: e.matmul(pq[:, p, :], lhsT=ones[:], rhs=ysq[:, p, :], start=True, stop=True),
                     reads=[ones, ysq], writes=[pq] if p == 0 else (), pwrites=() if p == 0 else [pq])
            S.op("act", lambda e, pm=pm: e.activation(out=mean[:], in_=pm[:], func=AF.Copy, scale=1.0 / 64), reads=[pm], writes=[mean])
            S.op("act", lambda e, pq=pq: e.activation(out=var[:], in_=pq[:], func=AF.Copy, scale=1.0 / 64), reads=[pq], writes=[var])
            S.op("pool", lambda e: e.tensor_tensor(out=ysq[:], in0=mean[:], in1=mean[:], op=ALU.mult), reads=[mean, ysq], writes=[ysq])
            S.op("pool", lambda e: e.tensor_tensor(out=var[:], in0=var[:], in1=ysq[:], op=ALU.subtract), reads=[var, ysq], writes=[var])
            S.op("act", lambda e: e.activation(out=var[:], in_=var[:], func=AF.Sqrt, bias=GN_EPS, scale=1.0), reads=[var], writes=[var])
            S.op("dve", lambda e: e.reciprocal(out=var[:], in_=var[:]), reads=[var], writes=[var])
            S.op("pool", lambda e: e.tensor_tensor(out=mean[:], in0=ybuf[:], in1=mean[:], op=ALU.subtract), reads=[ybuf, mean], writes=[mean])
            S.op("pool", lambda e: e.tensor_tensor(out=mean[:], in0=mean[:], in1=var[:], op=ALU.mult), reads=[mean, var], writes=[mean])
            S.op("pool", lambda e: e.tensor_tensor(out=mean[:], in0=mean[:], in1=bc4(lg), op=ALU.mult), reads=[mean, lg], writes=[mean])
            S.op("pool", lambda e: e.tensor_tensor(out=mean[:], in0=mean[:], in1=bc4(lb), op=ALU.add), reads=[mean, lb], writes=[mean])
            S.op("pool", lambda e: e.tensor_tensor(out=mean[:], in0=mean[:], in1=bonus[:], op=ALU.add), reads=[mean, bonus], writes=[mean])
            S.op("pool", lambda e: e.tensor_tensor(out=yo[:], in0=mean[:], in1=rzt[:], op=ALU.mult), reads=[mean, rzt], writes=[yo])
            S.dma("sp", scr["ysT"].t[2, :, t0:t0 + TP].rearrange("(c p) t -> p c t", p=128), yo[:], reads=[yo], pwrites=[scr["ysT"]], key=yo)
        _barrier(S)
        S.stack_pop()


_NC_CACHE = {}


def kernel(**inputs):
    SEQ = 8192
    if "nc" not in _NC_CACHE:
        _NC_CACHE["nc"] = build(SEQ, nlayers=2, enable=(1, 1, 1), scr_kind="Internal")
    nc = _NC_CACHE["nc"]
    x = np.ascontiguousarray(np.asarray(inputs["x"], dtype=np.float32))
    p = np.asarray(inputs["p"], dtype=np.float32)
    base = {}
    for k in WSPEC:
        v = np.ascontiguousarray(np.asarray(inputs[k], dtype=np.float32))
        base[k] = v.reshape(WSPEC[k])
    in_maps = []
    for b in range(8):
        m = dict(base)
        m["x"] = np.ascontiguousarray(x[b])
        m["p"] = np.ascontiguousarray(p[:, b])
        in_maps.append(m)
    res = run_bass_kernel_spmd(nc, in_maps, core_ids=list(range(8)))
    return np.stack([np.asarray(r["out"], dtype=np.float32) for r in res.results], axis=0)
```

```python
import contextlib
import numpy as np
import concourse.bass as bass
import concourse.mybir as mybir

F32 = mybir.dt.float32
BF16 = mybir.dt.bfloat16
AF = mybir.ActivationFunctionType
ALU = mybir.AluOpType
AX = mybir.AxisListType

ENGS = ("pe", "act", "dve", "pool", "sp")


class Buf:
    __slots__ = ("name", "w", "wfull", "r", "t")

    def __init__(self, name, t=None):
        self.name = name
        self.t = t
        self.w = []
        self.wfull = []
        self.r = []

    def __getitem__(self, k):
        return self.t[k]


class Op:
    __slots__ = ("eng", "fn", "deps", "marked", "tick", "dma", "idx")

    def __init__(self, eng, fn, dma):
        self.eng = eng
        self.fn = fn
        self.deps = []
        self.marked = False
        self.tick = None
        self.dma = dma
        self.idx = None


class DmaSem:
    def __init__(self):
        self.sem = None
        self.count = 0


class Sched:
    def __init__(self, nc, stack):
        self.nc = nc
        self.stack = stack
        self.ops = {e: [] for e in ENGS}
        self.all_ops = []
        self.dsems = {}
        self.n_sems = 0
        self.fence = []
        self.stacks = [stack]
        self.phase_keys = []
        self.free_ds = []
        self.all_ds = []
        self.keep = []

    def stack_push(self, st):
        self.stacks.append(st)
        self.phase_keys.append([])

    def stack_pop(self):
        self.stacks.pop()
        for kid in self.phase_keys.pop():
            ds = self.dsems.pop(kid, None)
            if ds is not None:
                self.free_ds.append(ds)

    def sb(self, name, shape, dt=F32):
        self.n_sems += 1
        name = "%s_u%d" % (name, self.n_sems)
        t = self.stacks[-1].enter_context(self.nc.sbuf_tensor(name, list(shape), dt))
        return Buf(name, t)

    def ps(self, name, shape, dt=F32):
        self.n_sems += 1
        name = "%s_u%d" % (name, self.n_sems)
        t = self.stacks[-1].enter_context(self.nc.psum_tensor(name, list(shape), dt))
        return Buf(name, t)

    def dram(self, name, shape, dt, kind="Internal"):
        t = self.nc.dram_tensor(name, list(shape), dt, kind=kind)
        return Buf(name, t.ap())

    def _add(self, eng, fn, reads, writes, pwrites, dma):
        op = Op(eng, fn, dma)
        deps = list(self.fence)
        for b in reads:
            deps.extend(b.w)
        for b in writes:
            deps.extend(b.w)
            deps.extend(b.r)
        for b in pwrites:
            deps.extend(b.wfull)
            deps.extend(b.r)
        seen = set()
        for d in deps:
            if id(d) in seen or d is op:
                continue
            seen.add(id(d))
            if d.eng == "pe" and eng == "pe" and d.dma is None and dma is None:
                continue
            op.deps.append(d)
            d.marked = True
        for b in reads:
            b.r.append(op)
            if len(b.r) > 24:
                b.r = self._prune(b.r)
        for b in writes:
            b.w = [op]
            b.wfull = [op]
            b.r = []
        for b in pwrites:
            b.w.append(op)
            if len(b.w) > 24:
                b.w = self._prune(b.w)
        op.idx = len(self.all_ops)
        self.all_ops.append(op)
        self.ops[eng].append(op)
        return op

    @staticmethod
    def _prune(lst):
        last = {}
        for o in lst:
            key = (o.eng, None) if o.dma is None else ("dma", id(o.dma))
            last[key] = o
        return list(last.values())

    def op(self, eng, fn, reads=(), writes=(), pwrites=()):
        return self._add(eng, fn, reads, writes, pwrites, None)

    def dma(self, eng, out_ap, in_ap, reads=(), writes=(), pwrites=(), key=None, **kw):
        if key is None:
            key = (list(writes) + list(pwrites))[0]
        ds = self.dsems.get(id(key))
        if ds is None:
            if self.free_ds:
                ds = self.free_ds.pop()
            else:
                ds = DmaSem()
                self.all_ds.append(ds)
            self.dsems[id(key)] = ds
            self.keep.append(key)
            if self.phase_keys:
                self.phase_keys[-1].append(id(key))
        fn = lambda e, o=out_ap, i=in_ap, kw=kw: e.dma_start(out=o, in_=i, **kw)
        op = self._add(eng, fn, reads, writes, pwrites, ds)
        ds.count += 16
        op.tick = ds.count
        return op

    def barrier_bufs(self, bufs):
        pass

    def emit(self):
        nc = self.nc
        stack = self.stack
        esem = {}
        for e in ENGS:
            esem[e] = stack.enter_context(nc.semaphore("s_" + e))
        for ds in self.all_ds:
            ds.sem = stack.enter_context(nc.semaphore("d%d" % self.n_sems))
            self.n_sems += 1
        for e in ENGS:
            c = 0
            for o in self.ops[e]:
                if o.dma is None:
                    if o.marked:
                        c += 1
                        o.tick = c
        self.max_ticks = {e: max([o.tick or 0 for o in self.ops[e] if o.dma is None] + [0]) for e in ENGS}

        def evkey(d):
            if d.dma is not None:
                return ("d", id(d.dma)), d.dma.sem, d.tick
            return ("e", d.eng), esem[d.eng], d.tick

        def run(eng_name, eng):
            seen = {}
            for o in self.ops[eng_name]:
                waits = {}
                for d in o.deps:
                    k, sem, val = evkey(d)
                    if seen.get(k, 0) >= val:
                        continue
                    if k not in waits or waits[k][1] < val:
                        waits[k] = (sem, val)
                for k, (sem, val) in waits.items():
                    eng.wait_ge(sem, val)
                    seen[k] = val
                inst = o.fn(eng)
                if o.dma is not None:
                    inst.then_inc(o.dma.sem, 16)
                elif o.marked:
                    inst.then_inc(esem[eng_name], 1)
            if eng_name == "sp":
                for e2 in ENGS:
                    m = self.max_ticks[e2]
                    if m > 0:
                        eng.wait_ge(esem[e2], m)
                for ds in self.all_ds:
                    if ds.count:
                        eng.wait_ge(ds.sem, ds.count)

        block = stack.enter_context(nc.Block())

        @block.tensor
        def _(e):
            run("pe", e)

        @block.scalar
        def _(e):
            run("act", e)

        @block.vector
        def _(e):
            run("dve", e)

        @block.gpsimd
        def _(e):
            run("pool", e)

        @block.sync
        def _(e):
            run("sp", e)


from concourse.bass_utils import run_bass_kernel_spmd

D = 1024
NCOL = 8600
PLE = 256
EPS = 1e-6


DEBUG = {}
_dbg_n = [0]


def dbg_dump(S, name, ap, buf, shape, cond=True):
    if not DEBUG.get("on") or not cond:
        return
    _dbg_n[0] += 1
    t = S.stacks[-1].enter_context(S.nc.sbuf_tensor("dbgsb_%d" % _dbg_n[0], list(shape), F32))
    tb = Buf("dbgsb", t)
    d = S.dram("dbg_" + name, list(shape), F32, kind="ExternalOutput")
    S.op("act", lambda e: e.copy(out=t[:], in_=ap), reads=[buf], writes=[tb])
    S.dma("sp", d.t, t[:], reads=[tb], writes=[d], key=tb)


class Ring:
    def __init__(self, bufs):
        self.bufs = bufs
        self.i = 0

    def next(self):
        b = self.bufs[self.i % len(self.bufs)]
        self.i += 1
        return b


def _barrier(S):
    fence = []
    for e in ENGS:
        comp = [o for o in S.ops[e] if o.dma is None]
        if comp:
            fence.append(comp[-1])
    lastd = {}
    for o in S.all_ops:
        if o.dma is not None:
            lastd[id(o.dma)] = o
    fence.extend(lastd.values())
    S.fence = fence


def make_ident(S, name="ident", dt=BF16):
    ident = S.sb(name, [128, 128], dt)
    S.op("pool", lambda e: e.memset(ident[:], 0.0), writes=[ident])
    S.op("pool", lambda e: e.affine_select(out=ident[:], in_=ident[:], pattern=[[-1, 128]],
                                           compare_op=ALU.not_equal, fill=1.0, base=0,
                                           channel_multiplier=1), reads=[ident], writes=[ident])
    return ident


def load_w_bf16(S, dst, k, src_ap, srcbuf):
    S.dma("pool", dst, src_ap, reads=[srcbuf], pwrites=[k], key=k, max_dma_last_dim=4096)


def rmsnorm_tile(S, xt_ap, xt_buf, g_buf, h_ap, h_buf, sq, ss, rs, eps=EPS, extra_reads=()):
    S.op("act", lambda e: e.activation(out=sq[:], in_=xt_ap, func=AF.Square, accum_out=ss[:]),
         reads=[xt_buf] + list(extra_reads), writes=[sq, ss])
    S.op("act", lambda e: e.activation(out=rs[:], in_=ss[:], func=AF.Sqrt, scale=1.0 / D, bias=eps),
         reads=[ss], writes=[rs])
    S.op("dve", lambda e: e.reciprocal(out=rs[:], in_=rs[:]), reads=[rs], writes=[rs])
    S.op("dve", lambda e: e.scalar_tensor_tensor(out=h_ap, in0=xt_ap, scalar=rs[:, 0:1], in1=g_buf[:],
                                                 op0=ALU.mult, op1=ALU.mult),
         reads=[xt_buf, rs, g_buf], pwrites=[h_buf])


def phase_A(S, nc, SEQ, lyr, x_src, Wd, scr):
    TT = 512
    nsub = TT // 128
    with contextlib.ExitStack() as st:
        S.stack_push(st)
        wt = S.sb("A_w", [128, 8, NCOL], BF16)
        gt = S.sb("A_g", [128, D])
        ident = make_ident(S, "A_ident")
        xt = S.sb("A_x", [128, nsub, D])
        sq = S.sb("A_sq", [128, D], BF16)
        ss = S.sb("A_ss", [128, 1])
        rs = S.sb("A_rs", [128, 1])
        h = S.sb("A_h", [128, nsub, D], BF16)
        hT = S.sb("A_hT", [128, 8, TT], BF16)
        stg_b = Ring([S.sb("A_sb%d" % i, [128, 512], BF16) for i in range(4)])
        stg_f = Ring([S.sb("A_sf%d" % i, [128, 512], F32) for i in range(3)])
        pT = Ring([S.ps("A_pT%d" % i, [128, 8, 128], BF16) for i in range(2)])
        pacc = Ring([S.ps("A_pa%d" % i, [128, 512], F32) for i in range(6)])

        w_in = Wd["w_in"]
        for k in range(8):
            S.dma("pool", wt[:, k, :], w_in.t[lyr, k * 128:(k + 1) * 128, 0:NCOL], reads=[w_in], pwrites=[wt],
                  key=wt, max_dma_last_dim=4096)
        S.dma("sp", gt[:], Wd["norm_g"].t[lyr:lyr + 1, :].partition_broadcast(128), reads=[Wd["norm_g"]],
              writes=[gt])

        FM = []
        for c in range(4):
            FM.append((c * 128, scr["qT"], c * 128, AF.Copy, 0.125, BF16))
        FM.append((512, scr["kcT"], 0, None, 1.0, BF16))
        FM.append((640, scr["vcT"], 0, None, 1.0, BF16))
        FM.append((768, scr["ksT"], 0, None, 1.0, BF16))
        FM.append((1024, scr["kwT"], 0, None, 1.0, BF16))
        for c in range(13):
            FM.append((3352 + c * 128, scr["rsT"], c * 128, None, 1.0, F32))
        for c in range(4):
            FM.append((5016 + c * 128, scr["rzT"], c * 128, AF.Silu, 1.0, BF16))
        for c in range(24):
            FM.append((5528 + c * 128, scr["mgT"], c * 128, AF.Sigmoid, 1.0, BF16))
        TM = [
            (896, 128, scr["vsw"], 0, None, BF16),
            (1152, 128, scr["vsw"], 128, None, BF16),
            (1280, 24, scr["gate"], 0, AF.Sigmoid, F32),
            (1304, 512, scr["nzs"], 0, AF.Silu, BF16),
            (1816, 512, scr["su"], 0, None, F32),
            (2328, 512, scr["sv"], 0, None, F32),
            (2840, 512, scr["szs"], 0, AF.Silu, BF16),
        ]
        evac_i = [0]

        def evac(out_ap, out_buf, in_ap, in_buf, func, scale):
            if func is None and scale == 1.0:
                if evac_i[0] % 2 == 0:
                    S.op("dve", lambda e: e.tensor_copy(out=out_ap, in_=in_ap), reads=[in_buf], writes=[out_buf])
                else:
                    S.op("act", lambda e: e.copy(out=out_ap, in_=in_ap), reads=[in_buf], writes=[out_buf])
                evac_i[0] += 1
            else:
                S.op("act", lambda e: e.activation(out=out_ap, in_=in_ap, func=func, scale=scale),
                     reads=[in_buf], writes=[out_buf])

        for ti in range(SEQ // TT):
            t0 = ti * TT
            S.dma("sp", xt[:], x_src.t[t0:t0 + TT, :].rearrange("(s p) d -> p s d", p=128), reads=[x_src],
                  writes=[xt])
            for s in range(nsub):
                rmsnorm_tile(S, xt[:, s, :], xt, gt, h[:, s, :], h, sq, ss, rs)
                pt = pT.next()
                for k in range(8):
                    S.op("pe", lambda e, k=k, s=s, pt=pt: e.transpose(out=pt[:, k, :], in_=h[:, s, k * 128:(k + 1) * 128],
                                                                     identity=ident[:]),
                         reads=[h, ident], writes=[pt] if k == 0 else (), pwrites=() if k == 0 else [pt])
                S.op("dve", lambda e, s=s, pt=pt: e.tensor_copy(out=hT[:, :, s * 128:(s + 1) * 128], in_=pt[:]),
                     reads=[pt], pwrites=[hT])
            for (c0, dbuf, r0, func, scale, dt) in FM:
                pa = pacc.next()
                for k in range(8):
                    S.op("pe", lambda e, k=k, pa=pa, c0=c0: e.matmul(pa[:], lhsT=wt[:, k, c0:c0 + 128], rhs=hT[:, k, :],
                                                                    start=(k == 0), stop=(k == 7)),
                         reads=[wt, hT], writes=[pa] if k == 0 else (), pwrites=() if k == 0 else [pa])
                sg = stg_b.next() if dt == BF16 else stg_f.next()
                evac(sg[:], sg, pa[:], pa, func, scale)
                S.dma("sp", dbuf.t[r0:r0 + 128, t0:t0 + TT], sg[:], reads=[sg], pwrites=[dbuf], key=sg)
            for s in range(nsub):
                for (c0, ncol, dbuf, dc0, func, dt) in TM:
                    pa = pacc.next()
                    for k in range(8):
                        S.op("pe", lambda e, k=k, pa=pa, c0=c0, ncol=ncol, s=s: e.matmul(
                            pa[:, 0:ncol], lhsT=hT[:, k, s * 128:(s + 1) * 128], rhs=wt[:, k, c0:c0 + ncol],
                            start=(k == 0), stop=(k == 7)),
                            reads=[wt, hT], writes=[pa] if k == 0 else (), pwrites=() if k == 0 else [pa])
                    sg = stg_b.next() if dt == BF16 else stg_f.next()
                    evac(sg[:, 0:ncol], sg, pa[:, 0:ncol], pa, func, 1.0)
                    S.dma("sp", dbuf.t[t0 + s * 128:t0 + (s + 1) * 128, dc0:dc0 + ncol], sg[:, 0:ncol], reads=[sg],
                          pwrites=[dbuf], key=sg)
        _barrier(S)
        S.stack_pop()


def make_scratch(S, SEQ, kind="Internal"):
    scr = {}
    def mk(name, shape, dt):
        scr[name] = S.dram(name, shape, dt, kind=kind)
    mk("qT", [512, SEQ], BF16)
    mk("kcT", [128, SEQ], BF16)
    mk("vcT", [128, SEQ], BF16)
    mk("ksT", [128, SEQ], BF16)
    mk("kwT", [128, SEQ], BF16)
    mk("vsw", [SEQ, 256], BF16)
    mk("gate", [SEQ, 24], F32)
    mk("nzs", [SEQ, 512], BF16)
    mk("su", [SEQ, 512], F32)
    mk("sv", [SEQ, 512], F32)
    mk("szs", [SEQ, 512], BF16)
    mk("rsT", [1664, SEQ], F32)
    mk("rzT", [512, SEQ], BF16)
    mk("mgT", [3072, SEQ], BF16)
    mk("ysT", [3, 512, SEQ], BF16)
    mk("xtok", [SEQ, 5, 2, 256], F32)
    return scr


def phase_C(S, nc, SEQ, lyr, Wd, scr):
    LN_EPS = 1e-5
    with contextlib.ExitStack() as st:
        S.stack_push(st)
        ident = make_ident(S, "C_ident")
        wraw = S.sb("C_wraw", [128, 8, 128])
        wbf = S.sb("C_wbf", [128, 8, 128], BF16)
        WT = S.sb("C_WT", [128, 8, 128], BF16)
        bsT = S.sb("C_bsT", [128, 8])
        lng = S.sb("C_lng", [128, 512])
        lnb = S.sb("C_lnb", [128, 512])
        pw = S.ps("C_pw", [128, 8, 128], BF16)
        S.dma("sp", wraw[:], Wd["sg_w"].t[lyr].rearrange("g t s -> t g s"), reads=[Wd["sg_w"]], writes=[wraw])
        S.dma("sp", bsT[:], Wd["sg_b"].t[lyr].rearrange("g t -> t g"), reads=[Wd["sg_b"]], writes=[bsT],
              allow_slow_non_contiguous=True)
        S.dma("sp", lng[:], Wd["sg_ln_g"].t[lyr:lyr + 1, :].partition_broadcast(128), reads=[Wd["sg_ln_g"]], writes=[lng])
        S.dma("sp", lnb[:], Wd["sg_ln_b"].t[lyr:lyr + 1, :].partition_broadcast(128), reads=[Wd["sg_ln_b"]], writes=[lnb])
        S.op("pool", lambda e: e.affine_select(out=wraw[:], in_=wraw[:], pattern=[[0, 8], [-1, 128]],
                                               compare_op=ALU.is_ge, fill=0.0, base=0, channel_multiplier=1),
             reads=[wraw], writes=[wraw])
        S.op("dve", lambda e: e.tensor_copy(out=wbf[:], in_=wraw[:]), reads=[wraw], writes=[wbf])
        for g in range(8):
            S.op("pe", lambda e, g=g: e.transpose(out=pw[:, g, :], in_=wbf[:, g, :], identity=ident[:]),
                 reads=[wbf, ident], pwrites=[pw])
        S.op("dve", lambda e: e.tensor_copy(out=WT[:], in_=pw[:]), reads=[pw], writes=[WT])

        NB = 2
        svt = Ring([S.sb("C_sv%d" % i, [128, 512]) for i in range(NB)])
        sut = Ring([S.sb("C_su%d" % i, [128, 512]) for i in range(NB)])
        szt = Ring([S.sb("C_sz%d" % i, [128, 512], BF16) for i in range(NB)])
        stats = S.sb("C_stats", [128, 6])
        mv = S.sb("C_mv", [128, 2])
        rstd = S.sb("C_rstd", [128, 1])
        vn0 = S.sb("C_vnf", [128, 512])
        vn = Ring([S.sb("C_vn%d" % i, [128, 512], BF16) for i in range(2)])
        y0 = S.sb("C_y0", [128, 512])
        yb = Ring([S.sb("C_yb%d" % i, [128, 512], BF16) for i in range(2)])
        pm = Ring([S.ps("C_pm%d" % i, [128, 512]) for i in range(2)])
        pt = Ring([S.ps("C_pt%d" % i, [128, 4, 128], BF16) for i in range(2)])
        stg = Ring([S.sb("C_stg%d" % i, [128, 4, 512], BF16) for i in range(2)])
        ys = scr["ysT"]
        sgb = None
        for c in range(SEQ // 128):
            t0 = c * 128
            v = svt.next(); u = sut.next(); z = szt.next()
            S.dma("sp", v[:], scr["sv"].t[t0:t0 + 128, :], reads=[scr["sv"]], writes=[v])
            S.dma("sp", u[:], scr["su"].t[t0:t0 + 128, :], reads=[scr["su"]], writes=[u])
            S.dma("sp", z[:], scr["szs"].t[t0:t0 + 128, :], reads=[scr["szs"]], writes=[z])
            S.op("dve", lambda e, v=v: e.bn_stats(out=stats[:], in_=v[:]), reads=[v], writes=[stats])
            S.op("dve", lambda e: e.bn_aggr(out=mv[:], in_=stats[:]), reads=[stats], writes=[mv])
            S.op("act", lambda e: e.activation(out=rstd[:], in_=mv[:, 1:2], func=AF.Sqrt, bias=LN_EPS, scale=1.0),
                 reads=[mv], writes=[rstd])
            S.op("dve", lambda e: e.reciprocal(out=rstd[:], in_=rstd[:]), reads=[rstd], writes=[rstd])
            S.op("dve", lambda e, v=v: e.tensor_scalar(out=vn0[:], in0=v[:], scalar1=mv[:, 0:1], scalar2=rstd[:, 0:1],
                                                       op0=ALU.subtract, op1=ALU.mult),
                 reads=[v, mv, rstd], writes=[vn0])
            S.op("pool", lambda e: e.tensor_tensor(out=vn0[:], in0=vn0[:], in1=lng[:], op=ALU.mult),
                 reads=[vn0, lng], writes=[vn0])
            vb = vn.next()
            S.op("pool", lambda e, vb=vb: e.tensor_tensor(out=vb[:], in0=vn0[:], in1=lnb[:], op=ALU.add),
                 reads=[vn0, lnb], writes=[vb])
            pmm = pm.next()
            for g in range(8):
                S.op("pe", lambda e, g=g, vb=vb, pmm=pmm: e.matmul(pmm[:, g * 64:(g + 1) * 64], lhsT=WT[:, g, :],
                                                                   rhs=vb[:, g * 64:(g + 1) * 64], start=True, stop=True),
                     reads=[WT, vb], writes=[pmm] if g == 0 else (), pwrites=() if g == 0 else [pmm])
            S.op("dve", lambda e, pmm=pmm: e.tensor_tensor(
                out=y0[:].rearrange("p (g d) -> p g d", g=8), in0=pmm[:].rearrange("p (g d) -> p g d", g=8),
                in1=bsT[:].unsqueeze(2).to_broadcast([128, 8, 64]), op=ALU.add), reads=[pmm, bsT], writes=[y0])
            S.op("pool", lambda e, u=u: e.tensor_tensor(out=y0[:], in0=y0[:], in1=u[:], op=ALU.mult),
                 reads=[y0, u], writes=[y0])
            y = yb.next()
            S.op("dve", lambda e, y=y, z=z: e.tensor_tensor(out=y[:], in0=y0[:], in1=z[:], op=ALU.mult),
                 reads=[y0, z], writes=[y])
            ptt = pt.next()
            for k in range(4):
                S.op("pe", lambda e, k=k, y=y, ptt=ptt: e.transpose(out=ptt[:, k, :], in_=y[:, k * 128:(k + 1) * 128],
                                                                    identity=ident[:]),
                     reads=[y, ident], writes=[ptt] if k == 0 else (), pwrites=() if k == 0 else [ptt])
            if c % 4 == 0:
                sgb = stg.next()
            cc = c % 4
            S.op("act", lambda e, ptt=ptt, sgb=sgb, cc=cc: e.copy(out=sgb[:, :, cc * 128:(cc + 1) * 128], in_=ptt[:]),
                 reads=[ptt], writes=[sgb] if cc == 0 else (), pwrites=() if cc == 0 else [sgb])
            if cc == 3 or c == SEQ // 128 - 1:
                tb = (c // 4) * 512
                n = (cc + 1) * 128
                S.dma("sp", ys.t[1, :, tb:tb + n].rearrange("(k p) t -> p k t", p=128), sgb[:, :, 0:n], reads=[sgb],
                      pwrites=[ys], key=sgb)
        _barrier(S)
        S.stack_pop()


def phase_E(S, nc, SEQ, lyr, x_src, x_dst, Wd, scr, final):
    TT = 512
    nsub = 4
    with contextlib.ExitStack() as st:
        S.stack_push(st)
        ident = make_ident(S, "E_ident")
        wb = S.sb("E_wb", [128, 3, 4, D], BF16)
        wo = S.sb("E_wo", [128, 8, D], BF16)
        wpg = S.sb("E_wpg", [128, 8, D], BF16)
        wpp = S.sb("E_wpp", [128, 2, D], BF16)
        gpl = S.sb("E_gpl", [128, D])
        gfin = S.sb("E_gfin", [128, D])
        for n in range(3):
            S.dma("pool", wb[:, n, :, :], Wd["w_branch"].t[lyr, n].rearrange("(k p) d -> p k d", p=128),
                  reads=[Wd["w_branch"]], pwrites=[wb], key=wb, max_dma_last_dim=4096)
        for k0 in range(0, 8, 4):
            S.dma("pool", wo[:, k0:k0 + 4, :], Wd["w_o"].t[lyr, k0 * 128:(k0 + 4) * 128, :].rearrange("(k p) d -> p k d", p=128),
                  reads=[Wd["w_o"]], pwrites=[wo], key=wo, max_dma_last_dim=4096)
            S.dma("pool", wpg[:, k0:k0 + 4, :], Wd["w_ple_gate"].t[lyr, k0 * 128:(k0 + 4) * 128, :].rearrange("(k p) d -> p k d", p=128),
                  reads=[Wd["w_ple_gate"]], pwrites=[wpg], key=wpg, max_dma_last_dim=4096)
        S.dma("pool", wpp[:], Wd["w_ple_proj"].t[lyr].rearrange("(k p) d -> p k d", p=128),
              reads=[Wd["w_ple_proj"]], pwrites=[wpp], key=wpp, max_dma_last_dim=4096)
        S.dma("sp", gpl[:], Wd["ple_norm_g"].t[lyr:lyr + 1, :].partition_broadcast(128), reads=[Wd["ple_norm_g"]], writes=[gpl])
        if final:
            S.dma("sp", gfin[:], Wd["final_norm_g"].t[0:1, :].partition_broadcast(128), reads=[Wd["final_norm_g"]], writes=[gfin])

        yst = S.sb("E_ys", [128, 3, 4, TT], BF16)
        mgt = S.sb("E_mg", [128, 24, TT], BF16)
        mrg = S.sb("E_mrg", [128, 8, TT])
        mrb = S.sb("E_mrb", [128, 8, TT], BF16)
        tmp = Ring([S.sb("E_tmp%d" % i, [128, TT]) for i in range(2)])
        xt = S.sb("E_x", [128, nsub, D])
        pin = S.sb("E_p", [128, nsub, PLE])
        pbf = S.sb("E_pbf", [128, PLE], BF16)
        pTs = S.sb("E_pT", [128, 2, 128], BF16)
        sq = S.sb("E_sq", [128, D], BF16)
        ss = S.sb("E_ss", [128, 1])
        rs = S.sb("E_rs", [128, 1])
        hp = S.sb("E_hp", [128, D], BF16)
        hpT = S.sb("E_hpT", [128, 8, 128], BF16)
        gate = S.sb("E_gate", [128, D])
        xo = Ring([S.sb("E_xo%d" % i, [128, D]) for i in range(2)])
        pz = Ring([S.ps("E_pz%d" % i, [128, TT]) for i in range(3)])
        po = Ring([S.ps("E_po%d" % i, [128, 512]) for i in range(2)])
        pg = Ring([S.ps("E_pg%d" % i, [128, 512]) for i in range(2)])
        ptr = S.ps("E_ptr", [128, 8, 128], BF16)

        for ti in range(SEQ // TT):
            t0 = ti * TT
            for n in range(3):
                S.dma("sp", yst[:, n, :, :], scr["ysT"].t[n, :, t0:t0 + TT].rearrange("(k p) t -> p k t", p=128),
                      reads=[scr["ysT"]], writes=[yst] if n == 0 else (), pwrites=() if n == 0 else [yst], key=yst)
            for k0 in range(0, 24, 8):
                S.dma("sp", mgt[:, k0:k0 + 8, :], scr["mgT"].t[k0 * 128:(k0 + 8) * 128, t0:t0 + TT].rearrange("(k p) t -> p k t", p=128),
                      reads=[scr["mgT"]], writes=[mgt] if k0 == 0 else (), pwrites=() if k0 == 0 else [mgt], key=mgt)
            S.dma("sp", xt[:], x_src.t[t0:t0 + TT, :].rearrange("(s p) d -> p s d", p=128), reads=[x_src], writes=[xt])
            S.dma("sp", pin[:], Wd["p"].t[lyr, t0:t0 + TT, :].rearrange("(s p) d -> p s d", p=128), reads=[Wd["p"]], writes=[pin])
            for dc in range(8):
                pzs = []
                for n in range(3):
                    pzz = pz.next()
                    pzs.append(pzz)
                    for k in range(4):
                        S.op("pe", lambda e, n=n, k=k, dc=dc, pzz=pzz: e.matmul(
                            pzz[:], lhsT=wb[:, n, k, dc * 128:(dc + 1) * 128], rhs=yst[:, n, k, :], start=(k == 0), stop=(k == 3)),
                            reads=[wb, yst], writes=[pzz] if k == 0 else (), pwrites=() if k == 0 else [pzz])
                S.op("dve", lambda e, dc=dc, p0=pzs[0]: e.tensor_tensor(out=mrg[:, dc, :], in0=p0[:], in1=mgt[:, dc, :], op=ALU.mult),
                     reads=[pzs[0], mgt], pwrites=[mrg])
                t1 = tmp.next()
                S.op("dve", lambda e, dc=dc, p1=pzs[1], t1=t1: e.tensor_tensor(out=t1[:], in0=p1[:], in1=mgt[:, 8 + dc, :], op=ALU.mult),
                     reads=[pzs[1], mgt], writes=[t1])
                t2 = tmp.next()
                S.op("dve", lambda e, dc=dc, p2=pzs[2], t2=t2: e.tensor_tensor(out=t2[:], in0=p2[:], in1=mgt[:, 16 + dc, :], op=ALU.mult),
                     reads=[pzs[2], mgt], writes=[t2])
                S.op("pool", lambda e, dc=dc, t1=t1: e.tensor_tensor(out=mrg[:, dc, :], in0=mrg[:, dc, :], in1=t1[:], op=ALU.add),
                     reads=[mrg, t1], pwrites=[mrg])
                S.op("pool", lambda e, dc=dc, t2=t2: e.tensor_tensor(out=mrb[:, dc, :], in0=mrg[:, dc, :], in1=t2[:], op=ALU.add),
                     reads=[mrg, t2], pwrites=[mrb])
            for s in range(nsub):
                for blk in range(2):
                    pp = po.next()
                    for k in range(8):
                        S.op("pe", lambda e, k=k, s=s, blk=blk, pp=pp: e.matmul(
                            pp[:], lhsT=mrb[:, k, s * 128:(s + 1) * 128], rhs=wo[:, k, blk * 512:(blk + 1) * 512],
                            start=(k == 0), stop=(k == 7)),
                            reads=[mrb, wo], writes=[pp] if k == 0 else (), pwrites=() if k == 0 else [pp])
                    S.op("dve", lambda e, s=s, blk=blk, pp=pp: e.tensor_tensor(
                        out=xt[:, s, blk * 512:(blk + 1) * 512], in0=pp[:], in1=xt[:, s, blk * 512:(blk + 1) * 512], op=ALU.add),
                        reads=[pp, xt], pwrites=[xt])
                rmsnorm_tile(S, xt[:, s, :], xt, gpl, hp[:], hp, sq, ss, rs)
                for k in range(8):
                    S.op("pe", lambda e, k=k: e.transpose(out=ptr[:, k, :], in_=hp[:, k * 128:(k + 1) * 128], identity=ident[:]),
                         reads=[hp, ident], writes=[ptr] if k == 0 else (), pwrites=() if k == 0 else [ptr])
                S.op("act", lambda e: e.copy(out=hpT[:], in_=ptr[:]), reads=[ptr], writes=[hpT])
                S.op("pool", lambda e, s=s: e.tensor_copy(out=pbf[:], in_=pin[:, s, :]), reads=[pin], writes=[pbf])
                for k in range(2):
                    S.op("pe", lambda e, k=k: e.transpose(out=ptr[:, k, :], in_=pbf[:, k * 128:(k + 1) * 128], identity=ident[:]),
                         reads=[pbf, ident, hpT], writes=[ptr] if k == 0 else (), pwrites=() if k == 0 else [ptr])
                S.op("act", lambda e: e.copy(out=pTs[:], in_=ptr[:, 0:2, :]), reads=[ptr], writes=[pTs])
                xout = xo.next()
                for blk in range(2):
                    pgg = pg.next()
                    for k in range(8):
                        S.op("pe", lambda e, k=k, blk=blk, pgg=pgg: e.matmul(
                            pgg[:], lhsT=hpT[:, k, :], rhs=wpg[:, k, blk * 512:(blk + 1) * 512], start=(k == 0), stop=(k == 7)),
                            reads=[hpT, wpg], writes=[pgg] if k == 0 else (), pwrites=() if k == 0 else [pgg])
                    S.op("act", lambda e, blk=blk, pgg=pgg: e.activation(out=gate[:, blk * 512:(blk + 1) * 512], in_=pgg[:], func=AF.Sigmoid),
                         reads=[pgg], pwrites=[gate])
                    ppp = pg.next()
                    for k in range(2):
                        S.op("pe", lambda e, k=k, blk=blk, ppp=ppp: e.matmul(
                            ppp[:], lhsT=pTs[:, k, :], rhs=wpp[:, k, blk * 512:(blk + 1) * 512], start=(k == 0), stop=(k == 1)),
                            reads=[pTs, wpp], writes=[ppp] if k == 0 else (), pwrites=() if k == 0 else [ppp])
                    S.op("dve", lambda e, blk=blk, ppp=ppp: e.tensor_tensor(
                        out=gate[:, blk * 512:(blk + 1) * 512], in0=ppp[:], in1=gate[:, blk * 512:(blk + 1) * 512], op=ALU.mult),
                        reads=[ppp, gate], pwrites=[gate])
                    S.op("pool", lambda e, blk=blk, s=s, xout=xout: e.tensor_tensor(
                        out=xout[:, blk * 512:(blk + 1) * 512], in0=gate[:, blk * 512:(blk + 1) * 512],
                        in1=xt[:, s, blk * 512:(blk + 1) * 512], op=ALU.add),
                        reads=[gate, xt], writes=[xout] if blk == 0 else (), pwrites=() if blk == 0 else [xout])
                if final:
                    S.op("act", lambda e, xout=xout: e.activation(out=sq[:], in_=xout[:], func=AF.Square, accum_out=ss[:]),
                         reads=[xout], writes=[sq, ss])
                    S.op("act", lambda e: e.activation(out=rs[:], in_=ss[:], func=AF.Sqrt, scale=1.0 / D, bias=EPS),
                         reads=[ss], writes=[rs])
                    S.op("dve", lambda e: e.reciprocal(out=rs[:], in_=rs[:]), reads=[rs], writes=[rs])
                    S.op("dve", lambda e, xout=xout: e.scalar_tensor_tensor(out=xout[:], in0=xout[:], scalar=rs[:, 0:1], in1=gfin[:],
                                                                            op0=ALU.mult, op1=ALU.mult),
                         reads=[xout, rs, gfin], writes=[xout])
                S.dma("sp", x_dst.t[t0 + s * 128:t0 + (s + 1) * 128, :], xout[:], reads=[xout], pwrites=[x_dst], key=xout)
        _barrier(S)
        S.stack_pop()


def phase_D(S, nc, SEQ, lyr, Wd, scr):
    TP = 128
    C = 16
    NCH = TP // C
    GN_EPS = 64e-5
    LD = 0.6065306597126334
    with contextlib.ExitStack() as st:
        S.stack_push(st)
        identB = make_ident(S, "D_identB", BF16)
        ones = S.sb("D_ones", [128, 128])
        S.op("pool", lambda e: e.memset(ones[:], 0.0), writes=[ones])
        S.op("pool", lambda e: e.memset(ones[0:64, 0:64], 1.0), reads=[ones], writes=[ones])
        S.op("pool", lambda e: e.memset(ones[64:128, 64:128], 1.0), reads=[ones], writes=[ones])
        Ff = S.sb("D_F", [128, 64], BF16)
        S.op("pool", lambda e: e.tensor_tensor(out=Ff[:], in0=identB[:, 0:64], in1=identB[:, 64:128], op=ALU.add), reads=[identB], writes=[Ff])
        Sel = S.sb("D_Sel", [128, 16], BF16)
        S.op("pool", lambda e: e.tensor_tensor(out=Sel[:], in0=identB[:, 0:16], in1=identB[:, 16:32], op=ALU.add), reads=[identB], writes=[Sel])
        for hh in range(2, 8):
            S.op("pool", lambda e, hh=hh: e.tensor_tensor(out=Sel[:], in0=Sel[:], in1=identB[:, hh * 16:(hh + 1) * 16], op=ALU.add),
                 reads=[identB, Sel], writes=[Sel])
        maskF = S.sb("D_maskF", [128, 4, 8], BF16)
        S.op("pool", lambda e: e.memset(maskF[:], 0.0), writes=[maskF])
        for p in range(4):
            for h2 in range(2):
                S.op("pool", lambda e, p=p, h2=h2: e.memset(maskF[h2 * 64:(h2 + 1) * 64, p, 2 * p + h2:2 * p + h2 + 1], 1.0), reads=[maskF], writes=[maskF])
        maskZ = S.sb("D_maskZ", [128, 4, 2], BF16)
        S.op("pool", lambda e: e.memset(maskZ[:], 1.0), writes=[maskZ])
        S.op("pool", lambda e: e.affine_select(out=maskZ[:], in_=maskZ[:], pattern=[[-32, 4], [-16, 2]], compare_op=ALU.is_ge, fill=0.0,
                                               base=0, channel_multiplier=1), reads=[maskZ], writes=[maskZ])
        S.op("pool", lambda e: e.affine_select(out=maskZ[:], in_=maskZ[:], pattern=[[32, 4], [16, 2]], compare_op=ALU.is_ge, fill=0.0,
                                               base=15, channel_multiplier=-1), reads=[maskZ], writes=[maskZ])

        def trimask(name, pat, cm, op):
            m = S.sb(name, [128, 128], BF16)
            S.op("pool", lambda e: e.memset(m[:], 1.0), writes=[m])
            S.op("pool", lambda e: e.affine_select(out=m[:], in_=m[:], pattern=pat, compare_op=op, fill=0.0, base=0, channel_multiplier=cm),
                 reads=[m], writes=[m])
            return m
        mSL = trimask("D_mSL", [[-16, 8], [-1, 16]], 1, ALU.is_gt)
        mSU = trimask("D_mSU", [[16, 8], [1, 16]], -1, ALU.is_gt)
        mUI = trimask("D_mUI", [[16, 8], [1, 16]], -1, ALU.is_ge)
        rm = S.sb("D_rm", [128, 512])
        S.op("pool", lambda e: e.memset(rm[:], 1.0), writes=[rm])
        S.op("pool", lambda e: e.memset(rm[:, 0:512:16], 0.0), reads=[rm], writes=[rm])

        def cvec(name, key, n):
            t = S.sb("D_" + name, [128, n])
            S.dma("sp", t[:], Wd[key].t[lyr].rearrange("(c p) -> p c", p=128), reads=[Wd[key]], writes=[t],
                  allow_slow_non_contiguous=True)
            return t

        def cvec2(name, key):
            t = S.sb("D_" + name, [128, 4])
            S.dma("sp", t[:], Wd[key].t[lyr].rearrange("(c a) j -> (a j) c", a=2), reads=[Wd[key]], writes=[t],
                  allow_slow_non_contiguous=True)
            return t
        mu = cvec("mu", "rk_mu", 13)
        w0 = cvec("w0", "rk_w0", 4)
        a0 = cvec("a0", "rk_a0", 4)
        lg = cvec("lg", "rk_lnx_g", 4)
        lb = cvec("lb", "rk_lnx_b", 4)
        kkc = cvec2("kkc", "rk_kk")
        ka = cvec2("ka", "rk_ka")
        rkc = cvec2("rkc", "rk_rk")
        omka = S.sb("D_omka", [128, 4])
        S.op("pool", lambda e: e.tensor_scalar(out=omka[:], in0=ka[:], scalar1=-1.0, scalar2=1.0, op0=ALU.mult, op1=ALU.add),
             reads=[ka], writes=[omka])
        w2 = S.sb("D_w2", [64, 512], BF16)
        a2 = S.sb("D_a2", [128, 512], BF16)
        S.dma("pool", w2[:], Wd["rk_w2"].t[lyr], reads=[Wd["rk_w2"]], writes=[w2])
        S.dma("pool", a2[64:128, :], Wd["rk_a2"].t[lyr], reads=[Wd["rk_a2"]], writes=[a2])

        Hm = S.sb("D_H", [128, 4, 64])
        Hn = S.sb("D_Hn", [128, 4, 64])
        Hbf = S.sb("D_Hbf", [128, 4, 64], BF16)
        S.op("pool", lambda e: e.memset(Hm[:], 0.0), writes=[Hm])
        S.op("pool", lambda e: e.memset(Hbf[:], 0.0), writes=[Hbf])

        rst = S.sb("D_rst", [128, 13, TP + 1])
        xs = S.sb("D_xs", [128, 13, TP])
        th = S.sb("D_th", [128, TP], BF16)
        sg = S.sb("D_sg", [128, 4, TP])
        cum = S.sb("D_cum", [128, 4, TP])
        E1 = S.sb("D_E1", [128, 4, TP])
        E2 = S.sb("D_E2", [128, 4, TP])
        E3 = S.sb("D_E3", [128, 4, TP])
        aa = S.sb("D_aa", [128, 4, TP])
        kkf = S.sb("D_kkf", [128, 4, TP])
        sq = S.sb("D_sq", [128, 4, TP])
        rn = S.sb("D_rn", [128, 4, TP])
        kp = S.sb("D_kp", [128, 4, TP])
        t1 = S.sb("D_t1", [128, 4, TP])
        t2 = S.sb("D_t2", [128, 4, TP])
        comp = [S.sb("D_cmp%d" % i, [128, 4, TP], BF16) for i in range(5)]
        ZXr = Ring([[S.sb("D_Z%d_%d" % (b, i), [128, NCH, 4, 128], BF16) for i in range(5)] for b in range(2)])
        DcR = Ring([S.sb("D_Dc%d" % i, [128, NCH, 4]) for i in range(2)])
        bonR = Ring([S.sb("D_bon%d" % i, [128, 4, TP]) for i in range(2)])
        rzR = Ring([S.sb("D_rz%d" % i, [128, 4, TP], BF16) for i in range(2)])
        ybR = Ring([S.sb("D_yb%d" % i, [128, 4, TP]) for i in range(2)])
        yo = S.sb("D_yo", [128, 4, TP], BF16)
        ppre = S.ps("D_ppre", [128, 4, TP])
        R4 = lambda nm, shp, dt=BF16: Ring([S.sb("D_%s%d" % (nm, i), shp, dt) for i in range(4)])
        WyZr = R4("WyZ", [128, 4, 128]); WhTr = R4("WhT", [128, 4, 128]); BtZr = R4("BtZ", [128, 4, 128]); KtZr = R4("KtZ", [128, 4, 128])
        U0r = R4("U0", [128, 64]); Vtr = R4("Vt", [128, 64]); PTr = R4("PT", [128, 128]); QTr = R4("QT", [128, 128])
        ysbR = Ring([S.sb("D_ysb%d" % i, [128, 64]) for i in range(3)])

        class Reg:
            def __init__(self, bank, ap):
                self.bank = bank
                self.t = ap

        class Lane:
            pass
        lanes = []
        for li in range(2):
            L = Lane()
            L.Gr = Ring([S.sb("D_G%d_%d" % (li, i), [128, 128], BF16) for i in range(2)])
            L.Nr = Ring([S.sb("D_N%d_%d" % (li, i), [128, 128], BF16) for i in range(2)])
            L.NTr = Ring([S.sb("D_NT%d_%d" % (li, i), [128, 128], BF16) for i in range(2)])
            L.MTs = S.sb("D_MTs%d" % li, [128, 128], BF16)
            L.X1Z = S.sb("D_X1Z%d" % li, [128, 4, 128], BF16)
            L.X1s = S.sb("D_X1s%d" % li, [128, 64], BF16)
            L.tks = S.sb("D_tks%d" % li, [128, 2, 64], BF16)
            ba = S.ps("D_ba%d" % li, [128, 512])
            bb = ppre if li == 0 else S.ps("D_bb%d" % li, [128, 4, 128])
            bg = S.ps("D_bg%d" % li, [128, 3, 128])
            L.tokc = Reg(ba, ba.t[:, 0:256].rearrange("q (o j) -> q o j", o=4))
            L.QTp = Reg(ba, ba.t[:, 256:384])
            L.mvp = Reg(ba, ba.t[:, 384:448])
            L.sc = [Reg(bb, bb.t[:, i, :]) for i in range(4)]
            L.bb = bb
            L.bg = bg
            lanes.append(L)
        bs = S.ps("D_bs", [128, 512])
        bt_ = S.ps("D_bt", [128, 512])
        WHp = Reg(bs, bs.t[:, 0:256].rearrange("q (p i) -> q p i", p=4))
        Yp = Reg(bs, bs.t[:, 256:320])
        yfp = Reg(bt_, bt_.t[:, 0:64].rearrange("q (p t) -> q p t", p=4))
        yn = S.sb("D_yn", [128, 64])
        YZ = S.sb("D_YZ", [128, 4, 128], BF16)
        stats = S.sb("D_stats", [128, 6])
        mv = S.sb("D_mv", [128, 2])
        rstd = S.sb("D_rstd", [128, 1])

        bc4 = lambda t: t[:].unsqueeze(2).to_broadcast([128, 4, TP])

        def mm(out_ap, obuf, lhsT, lbuf, rhs, rbuf, start, stop=True, first_write=False):
            obuf = getattr(obuf, "bank", obuf)
            S.op("pe", lambda e: e.matmul(out_ap, lhsT=lhsT, rhs=rhs, start=start, stop=stop, skip_group_check=True),
                 reads=[lbuf, rbuf], writes=[obuf] if first_write else (), pwrites=() if first_write else [obuf])

        def prep(nb):
            t0 = nb * TP
            S.dma("sp", rst[:, :, 1:TP + 1], scr["rsT"].t[:, t0:t0 + TP].rearrange("(c p) t -> p c t", p=128), reads=[scr["rsT"]], writes=[rst])
            if nb == 0:
                S.op("pool", lambda e: e.memset(rst[:, :, 0:1], 0.0), reads=[rst], pwrites=[rst])
            else:
                S.dma("sp", rst[:, :, 0:1], scr["rsT"].t[:, t0 - 1:t0].rearrange("(c p) t -> p c t", p=128), reads=[scr["rsT"]],
                      pwrites=[rst], key=rst, allow_slow_non_contiguous=True)
            S.op("pool", lambda e: e.tensor_tensor(out=xs[:], in0=rst[:, :, 0:TP], in1=rst[:, :, 1:TP + 1], op=ALU.subtract), reads=[rst], writes=[xs])
            S.op("pool", lambda e: e.tensor_tensor(out=xs[:], in0=xs[:], in1=mu[:].unsqueeze(2).to_broadcast([128, 13, TP]), op=ALU.mult),
                 reads=[xs, mu], writes=[xs])
            S.op("pool", lambda e: e.tensor_tensor(out=xs[:], in0=xs[:], in1=rst[:, :, 1:TP + 1], op=ALU.add), reads=[xs, rst], writes=[xs])
            r = xs[:, 0:4, :]; k = xs[:, 4:8, :]; v = xs[:, 8:12, :]
            S.op("act", lambda e: e.activation(out=th[0:64, :], in_=xs[0:64, 12, :], func=AF.Tanh), reads=[xs], pwrites=[th])
            S.op("act", lambda e: e.copy(out=th[64:128, :], in_=xs[64:128, 12, :]), reads=[xs], pwrites=[th])
            for p in range(4):
                mm(ppre[:, p, :], ppre, w2[0:64, p * 128:(p + 1) * 128], w2, th[0:64, :], th, True, first_write=(p == 0))
            for p in range(4):
                S.op("act", lambda e, p=p: e.activation(out=sg[:, p, :], in_=ppre[:, p, :], func=AF.Sigmoid, bias=w0[:, p:p + 1]),
                     reads=[w0], writes=[ppre], pwrites=[sg])
            for p in range(4):
                mm(ppre[:, p, :], ppre, a2[64:128, p * 128:(p + 1) * 128], a2, th[64:128, :], th, True, first_write=(p == 0))
            for p in range(4):
                S.op("act", lambda e, p=p: e.activation(out=aa[:, p, :], in_=ppre[:, p, :], func=AF.Sigmoid, bias=a0[:, p:p + 1]),
                     reads=[a0], writes=[ppre], pwrites=[aa])
            S.op("dve", lambda e: e.tensor_tensor_scan(out=cum[:].rearrange("q p t -> q (p t)"), data0=rm[:],
                                                       data1=sg[:].rearrange("q p t -> q (p t)"), initial=0.0, op0=ALU.mult, op1=ALU.add),
                 reads=[rm, sg], writes=[cum])
            S.op("act", lambda e: e.activation(out=E1[:], in_=cum[:], func=AF.Exp, scale=-LD), reads=[cum], writes=[E1])
            S.op("act", lambda e: e.activation(out=E2[:], in_=cum[:], func=AF.Exp, scale=LD), reads=[cum], writes=[E2])
            S.op("pool", lambda e: e.tensor_tensor(out=t2[:], in0=cum[:], in1=sg[:], op=ALU.subtract), reads=[cum, sg], writes=[t2])
            S.op("act", lambda e: e.activation(out=E3[:], in_=t2[:], func=AF.Exp, scale=-LD), reads=[t2], writes=[E3])
            Dc = DcR.next()
            S.op("pool", lambda e, Dc=Dc: e.tensor_copy(out=Dc[:].rearrange("q c p -> q p c"), in_=E1[:, :, 15:TP:16]), reads=[E1], writes=[Dc])
            S.op("pool", lambda e: e.tensor_tensor(out=kkf[:], in0=k, in1=bc4(kkc), op=ALU.mult), reads=[xs, kkc], writes=[kkf])
            S.op("pool", lambda e: e.tensor_tensor(out=sq[:], in0=kkf[:], in1=kkf[:], op=ALU.mult), reads=[kkf], writes=[sq])
            for p in range(4):
                mm(ppre[:, p, :], ppre, ones[:], ones, sq[:, p, :], sq, True, first_write=(p == 0))
            S.op("act", lambda e: e.activation(out=rn[:], in_=ppre[:], func=AF.Sqrt), writes=[rn, ppre])
            S.op("dve", lambda e: e.tensor_scalar(out=rn[:], in0=rn[:], scalar1=1e-12, scalar2=None, op0=ALU.max), reads=[rn], writes=[rn])
            S.op("dve", lambda e: e.reciprocal(out=rn[:], in_=rn[:]), reads=[rn], writes=[rn])
            S.op("pool", lambda e: e.tensor_tensor(out=kkf[:], in0=kkf[:], in1=rn[:], op=ALU.mult), reads=[kkf, rn], writes=[kkf])
            S.op("pool", lambda e: e.tensor_tensor(out=t1[:], in0=aa[:], in1=bc4(ka), op=ALU.mult), reads=[aa, ka], writes=[t1])
            S.op("pool", lambda e: e.tensor_tensor(out=t1[:], in0=t1[:], in1=bc4(omka), op=ALU.add), reads=[t1, omka], writes=[t1])
            S.op("pool", lambda e: e.tensor_tensor(out=kp[:], in0=k, in1=t1[:], op=ALU.mult), reads=[xs, t1], writes=[kp])
            At, Bt, Kt, Rt, Vb = comp
            S.op("pool", lambda e: e.scalar_tensor_tensor(out=At[:], in0=kkf[:], scalar=-1.0, in1=E3[:], op0=ALU.mult, op1=ALU.mult)
                 if False else e.tensor_tensor(out=t2[:], in0=kkf[:], in1=E3[:], op=ALU.mult), reads=[kkf, E3], writes=[t2])
            S.op("dve", lambda e: e.tensor_scalar(out=At[:], in0=t2[:], scalar1=-1.0, scalar2=None, op0=ALU.mult), reads=[t2], writes=[At])
            S.op("pool", lambda e: e.tensor_tensor(out=t2[:], in0=kkf[:], in1=aa[:], op=ALU.mult), reads=[kkf, aa], writes=[t2])
            S.op("pool", lambda e: e.tensor_tensor(out=Bt[:], in0=t2[:], in1=E2[:], op=ALU.mult), reads=[t2, E2], writes=[Bt])
            S.op("pool", lambda e: e.tensor_tensor(out=Kt[:], in0=kp[:], in1=E2[:], op=ALU.mult), reads=[kp, E2], writes=[Kt])
            S.op("pool", lambda e: e.tensor_tensor(out=Rt[:], in0=r, in1=E1[:], op=ALU.mult), reads=[xs, E1], writes=[Rt])
            S.op("pool", lambda e: e.tensor_copy(out=Vb[:], in_=v), reads=[xs], writes=[Vb])
            S.op("pool", lambda e: e.tensor_tensor(out=t1[:], in0=r, in1=kp[:], op=ALU.mult), reads=[xs, kp], writes=[t1])
            S.op("pool", lambda e: e.tensor_tensor(out=sq[:], in0=t1[:], in1=bc4(rkc), op=ALU.mult), reads=[t1, rkc], writes=[sq])
            for p in range(4):
                mm(ppre[:, p, :], ppre, ones[:], ones, sq[:, p, :], sq, True, first_write=(p == 0))
            bon = bonR.next()
            S.op("act", lambda e, bon=bon: e.copy(out=bon[:], in_=ppre[:]), writes=[bon, ppre])
            S.op("pool", lambda e, bon=bon: e.tensor_tensor(out=bon[:], in0=bon[:], in1=v, op=ALU.mult), reads=[bon, xs], writes=[bon])
            rzt = rzR.next()
            S.dma("sp", rzt[:], scr["rzT"].t[:, t0:t0 + TP].rearrange("(c p) t -> p c t", p=128), reads=[scr["rzT"]], writes=[rzt])
            ZX = ZXr.next()
            for oi in range(5):
                for p in range(4):
                    S.op("dve" if (oi * 4 + p) % 4 != 3 else "pool", lambda e, oi=oi, p=p, ZX=ZX: e.tensor_tensor(
                        out=ZX[oi][:, :, p, :].rearrange("q c (h t) -> q c h t", t=16),
                        in0=comp[oi][:, p, :].rearrange("q (c t) -> q c t", t=16).unsqueeze(2).to_broadcast([128, NCH, 8, 16]),
                        in1=maskF[:, p, :].unsqueeze(1).unsqueeze(3).to_broadcast([128, NCH, 8, 16]), op=ALU.mult),
                        reads=[comp[oi], maskF], writes=[ZX[oi]] if p == 0 else (), pwrites=() if p == 0 else [ZX[oi]])
            return dict(ZX=ZX, Dc=Dc, bon=bon, rzt=rzt, yb=ybR.next(), t0=t0)


        def pre(bt, c, L, pc):
            ZA, ZB, ZK, ZR, ZV = bt["ZX"]
            BtZ = BtZr.next(); KtZ = KtZr.next(); U0 = U0r.next(); Vt = Vtr.next(); PTs = PTr.next(); QTs = QTr.next()
            WyZ = WyZr.next(); WhT = WhTr.next()
            pc.update(BtZ=BtZ, KtZ=KtZ, U0=U0, Vt=Vt, PTs=PTs, QTs=QTs, WyZ=WyZ, WhT=WhT, c=c, bt=bt)
            tokc = L.tokc
            first = True
            for oi, Z in enumerate((ZA, ZB, ZK, ZV)):
                for p in range(4):
                    mm(tokc.t[:, oi, :], tokc, Z[:, c, p, :], Z, Ff[:], Ff, first, first_write=first)
                    first = False
            N1 = L.Nr.next(); NT1 = L.NTr.next()
            specs = ((L.sc[0], ZA, ZB, mSL, N1), (L.sc[1], ZB, ZA, mSU, NT1), (L.sc[2], ZK, ZA, mSU, L.MTs), (L.sc[3], ZB, ZR, mUI, PTs))
            for gi, (pb, Lh, R_, msk, dst) in enumerate(specs):
                for p in range(4):
                    mm(pb.t, pb, Lh[:, c, p, :], Lh, R_[:, c, p, :], R_, p == 0, first_write=(gi == 0 and p == 0))
            for p in range(4):
                mm(L.QTp.t, L.QTp, ZK[:, c, p, :], ZK, ZR[:, c, p, :], ZR, False, first_write=False)
            yield
            G0 = L.Gr.next()
            tks = L.tks
            S.op("act", lambda e: e.copy(out=G0[:, 0:64], in_=tokc.t[:, 0, :]), writes=[G0, tokc.bank])
            mz = maskZ[:].unsqueeze(3).to_broadcast([128, 4, 2, 64])
            S.op("dve", lambda e: e.tensor_copy(out=tks[:], in_=tokc.t[:, 1:3, :]), writes=[tks, tokc.bank])
            S.op("act", lambda e: e.copy(out=Vt[:], in_=tokc.t[:, 3, :]), writes=[Vt, tokc.bank])
            S.op("dve", lambda e: e.tensor_tensor(out=BtZ[:].rearrange("q p (a j) -> q p a j", a=2),
                                                   in0=tks[:, 0, :].unsqueeze(1).unsqueeze(1).to_broadcast([128, 4, 2, 64]), in1=mz, op=ALU.mult),
                 reads=[tks, maskZ], writes=[BtZ])
            S.op("dve", lambda e: e.tensor_tensor(out=KtZ[:].rearrange("q p (a j) -> q p a j", a=2),
                                                   in0=tks[:, 1, :].unsqueeze(1).unsqueeze(1).to_broadcast([128, 4, 2, 64]), in1=mz, op=ALU.mult),
                 reads=[tks, maskZ], writes=[KtZ])
            for gi, (pb, Lh, R_, msk, dst) in enumerate(specs):
                S.op("dve", lambda e, pb=pb, msk=msk, dst=dst: e.tensor_tensor(out=dst[:], in0=pb.t, in1=msk[:], op=ALU.mult),
                     reads=[msk], writes=[dst, pb.bank])
            S.op("dve", lambda e: e.tensor_tensor(out=QTs[:], in0=L.QTp.t, in1=mUI[:], op=ALU.mult), reads=[mUI], writes=[QTs, L.QTp.bank])
            yield
            mm(L.mvp.t, L.mvp, L.MTs[:], L.MTs, Vt[:], Vt, True, first_write=True)
            yield
            S.op("act", lambda e: e.copy(out=G0[:, 64:128], in_=L.mvp.t), writes=[L.mvp.bank], pwrites=[G0])
            yield
            G = G0; Nk = N1; NTk = NT1
            gb = L.bg
            for lev in range(4):
                mm(gb[:, 0, :], gb, identB[:], identB, G[:], G, True, stop=False, first_write=True)
                mm(gb[:, 0, :], gb, NTk[:], NTk, G[:], G, False)
                if lev < 3:
                    mm(gb[:, 1, :], gb, NTk[:], NTk, Nk[:], Nk, True)
                    mm(gb[:, 2, :], gb, Nk[:], Nk, NTk[:], NTk, True)
                    yield
                    G2 = L.Gr.next(); N2 = L.Nr.next(); NT2 = L.NTr.next()
                    S.op("act", lambda e, G2=G2: e.copy(out=G2[:], in_=gb[:, 0, :]), writes=[G2, gb])
                    S.op("dve", lambda e, N2=N2: e.tensor_copy(out=N2[:], in_=gb[:, 1, :]), writes=[N2, gb])
                    S.op("act", lambda e, NT2=NT2: e.copy(out=NT2[:], in_=gb[:, 2, :]), writes=[NT2, gb])
                    G = G2; Nk = N2; NTk = NT2
                    yield
                else:
                    yield
                    S.op("dve", lambda e: e.tensor_copy(out=L.X1s[:], in_=gb[:, 0, 0:64]), writes=[L.X1s, gb])
                    S.op("act", lambda e: e.copy(out=U0[:], in_=gb[:, 0, 64:128]), writes=[U0, gb])
                    S.op("dve", lambda e: e.tensor_tensor(out=L.X1Z[:].rearrange("q p (a j) -> q p a j", a=2),
                                                           in0=L.X1s[:].unsqueeze(1).unsqueeze(1).to_broadcast([128, 4, 2, 64]), in1=mz, op=ALU.mult),
                         reads=[L.X1s, maskZ], writes=[L.X1Z])
                    yield
            bb = L.bb
            for p in range(4):
                mm(bb[:, p, :], bb, identB[:], identB, ZR[:, c, p, :], ZR, p == 0, stop=False, first_write=(p == 0))
                mm(bb[:, p, :], bb, L.X1Z[:, p, :], L.X1Z, PTs[:], PTs, False)
            ba = L.tokc.bank
            for p in range(4):
                mm(ba[:, p * 128:(p + 1) * 128], ba, L.X1Z[:, p, :], L.X1Z, BtZ[:, p, :], BtZ, p == 0, first_write=(p == 0))
            yield
            S.op("act", lambda e: e.copy(out=WyZ[:], in_=bb[:]), writes=[WyZ, bb])
            S.op("dve", lambda e: e.tensor_copy(out=WhT[:].rearrange("q p m -> q (p m)"), in_=ba[:]), writes=[WhT, ba])
            yield

        def state_stream(pc):
            c = pc["c"]; bt = pc["bt"]
            BtZ, KtZ, U0, Vt, PTs, QTs, WyZ, WhT = (pc[k] for k in ("BtZ", "KtZ", "U0", "Vt", "PTs", "QTs", "WyZ", "WhT"))
            for p in range(4):
                mm(WHp.t[:, p, :], WHp, BtZ[:, p, :], BtZ, U0[:], U0, p == 0, stop=False, first_write=(p == 0))
            for p in range(4):
                mm(WHp.t[:, p, :], WHp, KtZ[:, p, :], KtZ, Vt[:], Vt, False, stop=False)
            mm(Yp.t, Yp, PTs[:], PTs, U0[:], U0, False, stop=False)
            mm(Yp.t, Yp, QTs[:], QTs, Vt[:], Vt, False, stop=False)
            yield
            for p in range(4):
                mm(Yp.t, Yp, WyZ[:, p, :], WyZ, Hbf[:, p, :], Hbf, False, stop=(p == 3))
            for p in range(4):
                mm(WHp.t[:, p, :], WHp, WhT[:, p, :], WhT, Hbf[:, p, :], Hbf, False, stop=True)
            yield
            Dc = bt["Dc"]
            S.op("dve", lambda e: e.tensor_tensor(out=Hn[:], in0=WHp.t, in1=Hm[:], op=ALU.add), reads=[Hm], writes=[Hn, WHp.bank])
            S.op("dve", lambda e: e.tensor_tensor(out=Hm[:], in0=Hn[:], in1=Dc[:, c, :].unsqueeze(2).to_broadcast([128, 4, 64]), op=ALU.mult),
                 reads=[Hn, Dc], writes=[Hm])
            ysb = ysbR.next()
            pc["ysb"] = ysb
            S.op("act", lambda e: e.copy(out=ysb[:], in_=Yp.t), writes=[ysb, Yp.bank])
            S.op("act", lambda e: e.copy(out=Hbf[:], in_=Hm[:]), reads=[Hm], writes=[Hbf])
            yield

        def out_stream(pc):
            c = pc["c"]; bt = pc["bt"]; ysb = pc["ysb"]
            S.op("dve", lambda e: e.bn_stats(out=stats[:], in_=ysb[:]), reads=[ysb], writes=[stats])
            S.op("dve", lambda e: e.bn_aggr(out=mv[:], in_=stats[:]), reads=[stats], writes=[mv])
            yield
            S.op("act", lambda e: e.activation(out=rstd[:], in_=mv[:, 1:2], func=AF.Sqrt, bias=GN_EPS, scale=1.0), reads=[mv], writes=[rstd])
            yield
            S.op("dve", lambda e: e.reciprocal(out=rstd[:], in_=rstd[:]), reads=[rstd], writes=[rstd])
            S.op("dve", lambda e: e.tensor_scalar(out=yn[:], in0=ysb[:], scalar1=mv[:, 0:1], scalar2=rstd[:, 0:1], op0=ALU.subtract, op1=ALU.mult),
                 reads=[ysb, mv, rstd], writes=[yn])
            yield
            S.op("dve", lambda e: e.tensor_tensor(out=YZ[:].rearrange("q p (a j) -> q p a j", a=2),
                                                   in0=yn[:].unsqueeze(1).unsqueeze(1).to_broadcast([128, 4, 2, 64]),
                                                   in1=maskZ[:].unsqueeze(3).to_broadcast([128, 4, 2, 64]), op=ALU.mult),
                 reads=[yn, maskZ], writes=[YZ])
            yield
            for p in range(4):
                mm(yfp.t[:, p, :], yfp, YZ[:, p, :], YZ, Sel[:], Sel, True, first_write=(p == 0))
            yield
            yb = bt["yb"]
            S.op("act", lambda e: e.copy(out=yb[:, :, c * 16:(c + 1) * 16], in_=yfp.t), writes=([yb] if c == 0 else []) + [yfp.bank],
                 pwrites=() if c == 0 else [yb])
            if c == NCH - 1:
                post(bt)
            yield

        def post(bt):
            yb = bt["yb"]; bon = bt["bon"]; rzt = bt["rzt"]; t0 = bt["t0"]
            S.op("pool", lambda e: e.tensor_tensor(out=yb[:], in0=yb[:], in1=bc4(lg), op=ALU.mult), reads=[yb, lg], writes=[yb])
            S.op("pool", lambda e: e.tensor_tensor(out=yb[:], in0=yb[:], in1=bc4(lb), op=ALU.add), reads=[yb, lb], writes=[yb])
            S.op("pool", lambda e: e.tensor_tensor(out=yb[:], in0=yb[:], in1=bon[:], op=ALU.add), reads=[yb, bon], writes=[yb])
            S.op("pool", lambda e: e.tensor_tensor(out=yo[:], in0=yb[:], in1=rzt[:], op=ALU.mult), reads=[yb, rzt], writes=[yo])
            S.dma("sp", scr["ysT"].t[2, :, t0:t0 + TP].rearrange("(c p) t -> p c t", p=128), yo[:], reads=[yo], pwrites=[scr["ysT"]], key=yo)

        chunks = []
        for nb in range(SEQ // TP):
            for c in range(NCH):
                chunks.append((nb, c))
        bts = {}
        nxt = 0
        lane_gen = [None, None]
        lane_pc = [None, None]
        done_order = {}
        next_state = 0
        state_gen = None; state_pc = None
        out_q = []; out_gen = None
        n_total = len(chunks)
        finished_out = 0
        pcs = {}
        while finished_out < n_total:
            for li in range(2):
                if lane_gen[li] is None and nxt < n_total and nxt - next_state < 3:
                    nb, c = chunks[nxt]
                    if nb not in bts:
                        bts[nb] = prep(nb)
                    pc = {"idx": nxt}
                    pcs[nxt] = pc
                    lane_gen[li] = pre(bts[nb], c, lanes[li], pc)
                    lane_pc[li] = pc
                    nxt += 1
                if lane_gen[li] is not None:
                    try:
                        next(lane_gen[li])
                    except StopIteration:
                        done_order[lane_pc[li]["idx"]] = True
                        lane_gen[li] = None
            if state_gen is None and done_order.get(next_state):
                state_pc = pcs[next_state]
                state_gen = state_stream(state_pc)
            if state_gen is not None:
                try:
                    next(state_gen)
                except StopIteration:
                    out_q.append(state_pc)
                    state_gen = None
                    next_state += 1
            if out_gen is None and out_q:
                out_gen = out_stream(out_q.pop(0))
            if out_gen is not None:
                try:
                    next(out_gen)
                except StopIteration:
                    out_gen = None
                    finished_out += 1
        _barrier(S)
        S.stack_pop()


WSPEC = {
    "norm_g": [2, 1024], "w_in": [2, 1024, 9112], "cmp_w1": [2, 2, 32, 64, 128], "cmp_w2": [2, 2, 128, 64],
    "cmp_pe": [2, 2, 32, 64], "sg_ln_g": [2, 512], "sg_ln_b": [2, 512], "sg_w": [2, 8, 128, 128], "sg_b": [2, 8, 128],
    "rk_mu": [2, 1664], "rk_w0": [2, 512], "rk_w2": [2, 64, 512], "rk_a0": [2, 512], "rk_a2": [2, 64, 512],
    "rk_kk": [2, 8, 64], "rk_ka": [2, 8, 64], "rk_rk": [2, 8, 64], "rk_lnx_g": [2, 512], "rk_lnx_b": [2, 512],
    "w_branch": [2, 3, 512, 1024], "w_o": [2, 1024, 1024], "ple_norm_g": [2, 1024], "w_ple_gate": [2, 1024, 1024],
    "w_ple_proj": [2, 256, 1024], "final_norm_g": [1, 1024],
}


def build(SEQ, nlayers=2, enable=(1, 1, 1), scr_kind="Internal"):
    nc = bass.Bass("TRN2", target_bir_lowering=False)
    with contextlib.ExitStack() as stack:
        S = Sched(nc, stack)
        x = Buf("x", nc.dram_tensor("x", [SEQ, D], F32, kind="ExternalInput").ap())
        Wd = {"p": Buf("p", nc.dram_tensor("p", [2, SEQ, PLE], F32, kind="ExternalInput").ap())}
        for k, shp in WSPEC.items():
            Wd[k] = Buf(k, nc.dram_tensor(k, shp, F32, kind="ExternalInput").ap())
        out = Buf("out", nc.dram_tensor("out", [SEQ, D], F32, kind="ExternalOutput").ap())
        scr = make_scratch(S, SEQ, kind=scr_kind)
        xmid = S.dram("xmid", [SEQ, D], F32, kind=scr_kind)
        cur = x
        for lyr in range(nlayers):
            last = lyr == nlayers - 1
            dst = out if last else xmid
            phase_A(S, nc, SEQ, lyr, cur, Wd, scr)
            if enable[0]:
                phase_B(S, nc, SEQ, lyr, Wd, scr)
            if enable[1]:
                phase_C(S, nc, SEQ, lyr, Wd, scr)
            if enable[2]:
                phase_D(S, nc, SEQ, lyr, Wd, scr)
            phase_E(S, nc, SEQ, lyr, cur, dst, Wd, scr, final=(last and nlayers == 2))
            cur = dst
        S.emit()
    return nc


def phase_B(S, nc, SEQ, lyr, Wd, scr):
    NC = (SEQ - 32) // 16 + 1
    NT = (NC + 127) // 128
    NCp = NT * 128
    KT = SEQ // 128
    with contextlib.ExitStack() as st:
        S.stack_push(st)
        ident = make_ident(S, "B_ident")
        ksT = S.sb("B_ksT", [128, 2, SEQ], BF16)
        HALF = min(4096, SEQ)
        NA = SEQ // HALF
        kwT = S.sb("B_kwT", [64, 2, SEQ], BF16)
        vs = S.sb("B_vs", [128, KT, 2, 65], BF16)
        vw = S.sb("B_vw", [128, KT, 2, 65], BF16)
        kcmpT = S.sb("B_kcmpT", [64, 2, NCp], BF16)
        Rc = S.sb("B_Rc", [128, NT, 2, 193], BF16)
        S.op("pool", lambda e: e.memset(ksT[64:128, :, :], 1.0), writes=[ksT])
        for g_ in range(2):
            for a_ in range(NA):
                S.op("pool", lambda e, g_=g_, a_=a_: e.affine_select(
                    out=ksT[64:128, g_, a_ * HALF:(a_ + 1) * HALF], in_=ksT[64:128, g_, a_ * HALF:(a_ + 1) * HALF], pattern=[[1, HALF]],
                    compare_op=ALU.is_ge, fill=0.0, base=0, channel_multiplier=-64), reads=[ksT], pwrites=[ksT])
                S.op("pool", lambda e, g_=g_, a_=a_: e.affine_select(
                    out=ksT[64:128, g_, a_ * HALF:(a_ + 1) * HALF], in_=ksT[64:128, g_, a_ * HALF:(a_ + 1) * HALF], pattern=[[-1, HALF]],
                    compare_op=ALU.is_ge, fill=0.0, base=63, channel_multiplier=64), reads=[ksT], pwrites=[ksT])
        S.dma("sp", ksT[0:64, :, :], scr["ksT"].t.rearrange("(g d) t -> d g t", g=2), reads=[scr["ksT"]], pwrites=[ksT], key=ksT)
        S.dma("sp", kwT[:], scr["kwT"].t.rearrange("(g d) t -> d g t", g=2), reads=[scr["kwT"]], writes=[kwT])
        S.op("pool", lambda e: e.memset(vs[:], 1.0), writes=[vs])
        S.op("pool", lambda e: e.memset(vw[:], 1.0), writes=[vw])
        for k0 in range(0, KT, 8):
            k1 = min(KT, k0 + 8)
            for (dst, c0) in ((vs, 0), (vw, 128)):
                for g in range(2):
                    S.dma("sp", dst[:, k0:k1, g, 0:64],
                          scr["vsw"].t[k0 * 128:k1 * 128, c0 + g * 64:c0 + (g + 1) * 64].rearrange("(k p) d -> p k d", p=128),
                          reads=[scr["vsw"]], pwrites=[dst], key=dst)
        S.op("pool", lambda e: e.memset(Rc[:], 1.0), writes=[Rc])
        for nt in range(NT):
            for g in range(2):
                S.op("pool", lambda e, nt=nt, g=g: e.affine_select(
                    out=Rc[:, nt, g, 65:193], in_=Rc[:, nt, g, 65:193], pattern=[[-4, 128]], compare_op=ALU.is_ge, fill=0.0,
                    base=nt * 128 + 1, channel_multiplier=1), reads=[Rc], writes=[Rc])
                S.op("pool", lambda e, nt=nt, g=g: e.affine_select(
                    out=Rc[:, nt, g, 65:193], in_=Rc[:, nt, g, 65:193], pattern=[[4, 128]], compare_op=ALU.is_ge, fill=0.0,
                    base=3 - nt * 128, channel_multiplier=-1), reads=[Rc], writes=[Rc])
        npad = NCp - NC
        if npad:
            S.op("pool", lambda e: e.affine_select(
                out=Rc[:, NT - 1, :, :], in_=Rc[:, NT - 1, :, :], pattern=[[0, 2 * 193]], compare_op=ALU.is_ge, fill=0.0,
                base=(NC - 1) - (NT - 1) * 128, channel_multiplier=-1), reads=[Rc], writes=[Rc])
        S.op("pool", lambda e: e.memset(kcmpT[:], 0.0), writes=[kcmpT])

        with contextlib.ExitStack() as st2:
            S.stack_push(st2)
            kvT = S.sb("B_kvT", [64, 2, SEQ], BF16)
            w1 = S.sb("B_w1", [64, 32, 128], BF16)
            w2 = S.sb("B_w2", [128, 64], BF16)
            peT = S.sb("B_peT", [64, 32])
            peTb = S.sb("B_peTb", [64, 32], BF16)
            cb = S.sb("B_cb", [128, 1])
            hid = S.sb("B_hid", [128, NCp], BF16)
            ph = S.ps("B_ph", [128, 512])
            pc1 = S.ps("B_pc1", [128, 512])
            pk = S.ps("B_pk", [128, 512])
            for kv in range(2):
                src = scr["kcT"] if kv == 0 else scr["vcT"]
                S.dma("sp", kvT[:], src.t.rearrange("(g d) t -> d g t", g=2), reads=[src], writes=[kvT])
                S.dma("pool", w1[:], Wd["cmp_w1"].t[lyr, kv].rearrange("l d h -> d l h"), reads=[Wd["cmp_w1"]], writes=[w1])
                S.dma("pool", w2[:], Wd["cmp_w2"].t[lyr, kv], reads=[Wd["cmp_w2"]], writes=[w2])
                S.dma("sp", peT[:], Wd["cmp_pe"].t[lyr, kv].rearrange("l d -> d l"), reads=[Wd["cmp_pe"]], writes=[peT],
                      allow_slow_non_contiguous=True)
                S.op("dve", lambda e: e.tensor_copy(out=peTb[:], in_=peT[:]), reads=[peT], writes=[peTb])
                for l in range(32):
                    S.op("pe", lambda e, l=l: e.matmul(pc1[:, 0:1], lhsT=w1[:, l, :], rhs=peTb[:, l:l + 1], start=(l == 0), stop=(l == 31)),
                         reads=[w1, peTb], writes=[pc1] if l == 0 else (), pwrites=() if l == 0 else [pc1])
                S.op("dve", lambda e: e.tensor_copy(out=cb[:], in_=pc1[:, 0:1]), reads=[pc1], writes=[cb])
                for g in range(2):
                    S.op("dve", lambda e: e.memset(hid[:], 0.0), writes=[hid])
                    for n0 in range(0, NC, 512):
                        nn = min(512, NC - n0)
                        for l in range(32):
                            S.op("pe", lambda e, l=l, g=g, n0=n0, nn=nn: e.matmul(
                                ph[:, 0:nn], lhsT=w1[:, l, :], rhs=kvT[:, g, n0 * 16 + l: n0 * 16 + l + (nn - 1) * 16 + 1: 16], start=(l == 0), stop=(l == 31)),
                                reads=[w1, kvT], writes=[ph] if l == 0 else (), pwrites=() if l == 0 else [ph])
                        S.op("act", lambda e, n0=n0, nn=nn: e.activation(out=hid[:, n0:n0 + nn], in_=ph[:, 0:nn], func=AF.Silu, bias=cb[:, 0:1]),
                             reads=[ph, cb], pwrites=[hid])
                    if kv == 0:
                        for n0 in range(0, NC, 512):
                            nn = min(512, NC - n0)
                            S.op("pe", lambda e, n0=n0, nn=nn: e.matmul(pk[0:64, 0:nn], lhsT=w2[:], rhs=hid[:, n0:n0 + nn], start=True, stop=True),
                                 reads=[w2, hid], writes=[pk])
                            S.op("dve", lambda e, g=g, n0=n0, nn=nn: e.tensor_copy(out=kcmpT[:, g, n0:n0 + nn], in_=pk[0:64, 0:nn]),
                                 reads=[pk], pwrites=[kcmpT])
                    else:
                        for nt in range(NT):
                            rows = min(128, NC - nt * 128)
                            S.op("pe", lambda e, nt=nt: e.matmul(pk[:, 0:64], lhsT=hid[:, nt * 128:(nt + 1) * 128], rhs=w2[:], start=True, stop=True),
                                 reads=[w2, hid], writes=[pk])
                            S.op("dve", lambda e, g=g, nt=nt: e.tensor_copy(out=Rc[:, nt, g, 0:64], in_=pk[:, 0:64]),
                                 reads=[pk], pwrites=[Rc])
            _barrier(S)
            S.stack_pop()

        qt = Ring([S.sb("B_q%d" % i, [64, 8, 128], BF16) for i in range(2)])
        gt = Ring([S.sb("B_g%d" % i, [128, 24]) for i in range(2)])
        nzt = Ring([S.sb("B_nz%d" % i, [128, 512], BF16) for i in range(2)])
        Et = Ring([S.sb("B_E%d" % i, [128, 512], BF16) for i in range(4)])
        psT = Ring([S.ps("B_psT%d" % i, [128, 512]) for i in range(3)])
        pcA = S.ps("B_pcA", [128, 2, 193])
        pcB = S.ps("B_pcB", [128, 2, 193])
        pos = S.ps("B_pos", [128, 4, 65])
        pow_ = S.ps("B_pow", [128, 4, 65])
        pmisc = S.ps("B_pmisc", [128, 4, 128], BF16)
        oc = S.sb("B_oc", [128, 4, 193])
        rcs = S.sb("B_rcs", [128, 4])
        rss = S.sb("B_rss", [128, 4])
        rws = S.sb("B_rws", [128, 4])
        cc = S.sb("B_cc", [128, 3, 4])
        sc = S.sb("B_sc", [128, 128])
        sc2 = S.sb("B_sc2", [128, 128])
        m1 = S.sb("B_m1", [128, 8])
        m2 = S.sb("B_m2", [128, 8])
        negq = S.sb("B_negq", [128, 2, 128], BF16)
        S.op("pool", lambda e: e.memset(negq[:], 0.0), writes=[negq])
        qAr = {(g_, a_): Ring([S.sb("B_qA%d%d_%d" % (g_, a_, i), [128, 4, 128], BF16) for i in range(2)]) for g_ in range(2) for a_ in range(NA)}
        yg = S.sb("B_yg", [128, 4, 64])
        ytmp = S.sb("B_ytmp", [128, 4, 64])
        ynsa = S.sb("B_ynsa", [128, 512], BF16)
        stg = Ring([S.sb("B_stg%d" % i, [128, 4, 128], BF16) for i in range(2)])

        def qk_exp(kT_ap, kbuf, q_ap, qbuf, neg_lhsT=None):
            p = psT.next()
            if False:
                pass
            else:
                S.op("pe", lambda e, p=p: e.matmul(p[:], lhsT=kT_ap, rhs=q_ap, start=True, stop=True), reads=[kbuf, qbuf], writes=[p])
            E = Et.next()
            S.op("act", lambda e, p=p, E=E: e.activation(out=E[:], in_=p[:], func=AF.Exp), reads=[p], writes=[E])
            return E

        def pipeline(tiles, L=2):
            Es = {}
            n = len(tiles)
            for i in range(n + L):
                if i < n:
                    Es[i] = tiles[i][0]()
                if i - L >= 0:
                    tiles[i - L][1](Es.pop(i - L))

        def mask(E, base, cm, qstep):
            S.op("pool", lambda e, E=E: e.affine_select(out=E[:], in_=E[:], pattern=[[0, 4], [qstep, 128]], compare_op=ALU.is_ge,
                                                       fill=0.0, base=base, channel_multiplier=cm), reads=[E], writes=[E])

        for qb in range(SEQ // 128):
            q0 = qb * 128
            q = qt.next(); gg = gt.next(); nz = nzt.next()
            S.dma("sp", q[:], scr["qT"].t[:, q0:q0 + 128].rearrange("(h d) t -> d h t", h=8), reads=[scr["qT"]], writes=[q])
            qAs = {}
            for g_ in range(2):
                for a_ in range(min(NA, qb * 128 // HALF + 1)):
                    qa = qAr[(g_, a_)].next()
                    qAs[(g_, a_)] = qa
                    S.dma("sp", qa[0:64, :, :], scr["qT"].t[g_ * 256:(g_ + 1) * 256, q0:q0 + 128].rearrange("(h d) t -> d h t", h=4),
                          reads=[scr["qT"]], writes=[qa])
            S.dma("sp", gg[:], scr["gate"].t[q0:q0 + 128, :], reads=[scr["gate"]], writes=[gg])
            S.dma("sp", nz[:], scr["nzs"].t[q0:q0 + 128, :], reads=[scr["nzs"]], writes=[nz])
            for g in range(2):
                q_ap = q[:, 4 * g:4 * g + 4, :].rearrange("d h q -> d (h q)")
                n_max = min(8 * qb + 6, NC - 1)
                ntl = n_max // 128 + 1
                def c_qk(nt, g=g, q_ap=q_ap, q=q):
                    E = qk_exp(kcmpT[:, g, nt * 128:(nt + 1) * 128], kcmpT, q_ap, q)
                    if q0 - 16 * (128 * nt + 127) - 31 < 0:
                        mask(E, q0 - 16 * 128 * nt - 31, -16, 1)
                    return E

                def c_pv(nt, E, g=g, ntl=ntl):
                    for h in range(4):
                        pcx = pcA if h < 2 else pcB
                        first = (nt == 0 and h % 2 == 0)
                        S.op("pe", lambda e, E=E, h=h, pcx=pcx, nt=nt, first=first, g=g, ntl=ntl: e.matmul(
                            pcx[:, h % 2, :], lhsT=E[:, h * 128:(h + 1) * 128], rhs=Rc[:, nt, g, :], start=first,
                            stop=(nt == ntl - 1 and h % 2 == 1), skip_group_check=True),
                            reads=[E, Rc], writes=[pcx] if first else (), pwrites=() if first else [pcx])
                pipeline([(lambda nt=nt: c_qk(nt), lambda E, nt=nt: c_pv(nt, E)) for nt in range(ntl)])
                S.op("act", lambda e: e.copy(out=oc[:, 0:2, :], in_=pcA[:]), reads=[pcA], pwrites=[oc])
                S.op("act", lambda e: e.copy(out=oc[:, 2:4, :], in_=pcB[:]), reads=[pcB], pwrites=[oc])
                S.op("dve", lambda e: e.tensor_scalar(out=rcs[:], in0=oc[:, :, 64], scalar1=1e-30, scalar2=None, op0=ALU.max),
                     reads=[oc], writes=[rcs])
                S.op("dve", lambda e: e.reciprocal(out=rcs[:], in_=rcs[:]), reads=[rcs], writes=[rcs])
                S.op("dve", lambda e: e.tensor_scalar(out=sc[:], in0=oc[:, 0, 65:193], scalar1=rcs[:, 0:1], scalar2=None, op0=ALU.mult),
                     reads=[oc, rcs], writes=[sc])
                for h in range(1, 4):
                    S.op("dve", lambda e, h=h: e.scalar_tensor_tensor(out=sc[:], in0=oc[:, h, 65:193], scalar=rcs[:, h:h + 1], in1=sc[:],
                                                                      op0=ALU.mult, op1=ALU.add), reads=[oc, rcs, sc], writes=[sc])
                for half in range(2):
                    tb = 2 * qb + half
                    ps_ = slice(half * 64, (half + 1) * 64)
                    if tb + 1 < 128:
                        S.op("dve", lambda e, ps_=ps_, tb=tb: e.memset(sc[ps_, tb + 1:128], -1e4), reads=[sc], writes=[sc])
                    lo = max(tb - 1, 0)
                    S.op("dve", lambda e, ps_=ps_, tb=tb, lo=lo: e.memset(sc[ps_, lo:tb + 1], 1e4), reads=[sc], writes=[sc])
                S.op("dve", lambda e: e.memset(sc[:, 0:1], 1e4), reads=[sc], writes=[sc])
                S.op("dve", lambda e: e.max(out=m1[:], in_=sc[:]), reads=[sc], writes=[m1])
                S.op("dve", lambda e: e.match_replace(out=sc2[:], in_to_replace=m1[:], in_values=sc[:], imm_value=-3e4),
                     reads=[sc, m1], writes=[sc2])
                S.op("dve", lambda e: e.max(out=m2[:], in_=sc2[:]), reads=[sc2], writes=[m2])
                S.op("dve", lambda e: e.tensor_scalar(out=negq[:, 0, :], in0=sc[:], scalar1=m2[:, 7:8], scalar2=-1e4, op0=ALU.is_lt, op1=ALU.mult),
                     reads=[sc, m2], pwrites=[negq])
                S.op("dve", lambda e: e.tensor_scalar(out=negq[:, 1, 64:128], in0=sc[:, 0:64], scalar1=m2[:, 7:8], scalar2=-1e4, op0=ALU.is_lt, op1=ALU.mult),
                     reads=[sc, m2], pwrites=[negq])
                na_here = min(NA, qb * 128 // HALF + 1)
                S.op("pe", lambda e: e.transpose(out=pmisc[:, 1, :], in_=negq[:, 1, :], identity=ident[:]), reads=[negq, ident], writes=[pmisc])
                if na_here > 1:
                    S.op("pe", lambda e: e.transpose(out=pmisc[:, 0, :], in_=negq[:, 0, :], identity=ident[:]), reads=[negq, ident], pwrites=[pmisc])
                for a_ in range(na_here):
                    qa = qAs[(g, a_)]
                    S.op("dve", lambda e, qa=qa, a_=a_: e.tensor_copy(out=qa[64:128, :, :],
                                                                   in_=pmisc[64:128, (1 - a_):(2 - a_), :].to_broadcast([64, 4, 128])),
                         reads=[pmisc], pwrites=[qa])
                kts = list(range(max(0, qb - 4), qb + 1))

                def w_qk(i, kt, g=g, q_ap=q_ap, q=q, qb=qb):
                    E = qk_exp(kwT[:, g, kt * 128:(kt + 1) * 128], kwT, q_ap, q)
                    if kt == qb - 4:
                        mask(E, -1, 1, -1)
                    if kt == qb:
                        mask(E, 0, -1, 1)
                    return E

                def w_pv(i, kt, E, g=g, kts=kts):
                    for h in range(4):
                        first = (i == 0 and h == 0)
                        S.op("pe", lambda e, E=E, h=h, kt=kt, first=first, last=(i == len(kts) - 1 and h == 3), g=g: e.matmul(
                            pow_[:, h, :], lhsT=E[:, h * 128:(h + 1) * 128], rhs=vw[:, kt, g, :], start=first, stop=last,
                            skip_group_check=True),
                            reads=[E, vw], writes=[pow_] if first else (), pwrites=() if first else [pow_])
                def s_qk(kt, g=g, q_ap=q_ap, q=q, qb=qb, qAs=qAs):
                    qa = qAs[(g, kt * 128 // HALF)]
                    E = qk_exp(ksT[:, g, kt * 128:(kt + 1) * 128], ksT, qa[:].rearrange("p h q -> p (h q)"), qa)
                    if kt == qb:
                        mask(E, 0, -1, 1)
                    return E

                def s_pv(kt, E, g=g, qb=qb):
                    for h in range(4):
                        first = (kt == 0 and h == 0)
                        S.op("pe", lambda e, E=E, h=h, kt=kt, first=first, last=(kt == qb and h == 3), g=g: e.matmul(
                            pos[:, h, :], lhsT=E[:, h * 128:(h + 1) * 128], rhs=vs[:, kt, g, :], start=first, stop=last,
                            skip_group_check=True),
                            reads=[E, vs], writes=[pos] if first else (), pwrites=() if first else [pos])
                pipeline([(lambda i=i, kt=kt: w_qk(i, kt), lambda E, i=i, kt=kt: w_pv(i, kt, E)) for i, kt in enumerate(kts)] +
                         [(lambda kt=kt: s_qk(kt), lambda E, kt=kt: s_pv(kt, E)) for kt in range(qb + 1)])
                S.op("dve", lambda e: e.reciprocal(out=rss[:], in_=pos[:, :, 64]), reads=[pos], writes=[rss])
                S.op("dve", lambda e: e.reciprocal(out=rws[:], in_=pow_[:, :, 64]), reads=[pow_], writes=[rws])
                gv = gg[:, g * 12:(g + 1) * 12].rearrange("p (h b) -> p b h", b=3)
                for b, rr in enumerate((rcs, rss, rws)):
                    S.op("dve", lambda e, b=b, rr=rr, gv=gv: e.tensor_tensor(out=cc[:, b, :], in0=gv[:, b, :], in1=rr[:], op=ALU.mult),
                         reads=[gg, rr], pwrites=[cc])
                if qb == 2:
                    dbg_dump(S, "oc%d" % g, oc[:], oc, [128, 4, 193])
                    dbg_dump(S, "pos%d" % g, pos[:], pos, [128, 4, 65])
                    dbg_dump(S, "pow%d" % g, pow_[:], pow_, [128, 4, 65])
                    dbg_dump(S, "cc%d" % g, cc[:], cc, [128, 3, 4])
                    dbg_dump(S, "gg%d" % g, gg[:], gg, [128, 24])
                    dbg_dump(S, "sc%d" % g, sc[:], sc, [128, 128])
                    dbg_dump(S, "negq%d" % g, negq[:], negq, [128, 128])
                bc = lambda b: cc[:, b, :].unsqueeze(2).to_broadcast([128, 4, 64])
                S.op("dve", lambda e: e.tensor_tensor(out=yg[:], in0=oc[:, :, 0:64], in1=bc(0), op=ALU.mult), reads=[oc, cc], writes=[yg])
                S.op("dve", lambda e: e.tensor_tensor(out=ytmp[:], in0=pos[:, :, 0:64], in1=bc(1), op=ALU.mult), reads=[pos, cc], writes=[ytmp])
                S.op("pool", lambda e: e.tensor_tensor(out=yg[:], in0=yg[:], in1=ytmp[:], op=ALU.add), reads=[yg, ytmp], writes=[yg])
                S.op("dve", lambda e: e.tensor_tensor(out=ytmp[:], in0=pow_[:, :, 0:64], in1=bc(2), op=ALU.mult), reads=[pow_, cc], writes=[ytmp])
                S.op("pool", lambda e: e.tensor_tensor(out=yg[:], in0=yg[:], in1=ytmp[:], op=ALU.add), reads=[yg, ytmp], writes=[yg])
                if qb == 2:
                    dbg_dump(S, "yg%d" % g, yg[:], yg, [128, 4, 64])
                S.op("pool", lambda e, g=g, nz=nz: e.tensor_tensor(out=ynsa[:, g * 256:(g + 1) * 256], in0=yg[:].rearrange("p h d -> p (h d)"),
                                                                   in1=nz[:, g * 256:(g + 1) * 256], op=ALU.mult),
                     reads=[yg, nz], pwrites=[ynsa])
            for k in range(4):
                S.op("pe", lambda e, k=k: e.transpose(out=pmisc[:, k, :], in_=ynsa[:, k * 128:(k + 1) * 128], identity=ident[:]),
                     reads=[ynsa, ident], writes=[pmisc] if k == 0 else (), pwrites=() if k == 0 else [pmisc])
            sg = stg.next()
            S.op("act", lambda e, sg=sg: e.copy(out=sg[:], in_=pmisc[:]), reads=[pmisc], writes=[sg])
            S.dma("sp", scr["ysT"].t[0, :, q0:q0 + 128].rearrange("(k p) t -> p k t", p=128), sg[:], reads=[sg], pwrites=[scr["ysT"]], key=sg)
        _barrier(S)
        S.stack_pop()


def phase_D_seq(S, nc, SEQ, lyr, Wd, scr):
    TP = 128
    TB = 8
    GN_EPS = 64e-5
    xtok = scr["xtok"]
    with contextlib.ExitStack() as st:
        S.stack_push(st)
        identF = make_ident(S, "D_ident", F32)
        ones = S.sb("D_ones", [128, 128])
        S.op("pool", lambda e: e.memset(ones[:], 0.0), writes=[ones])
        S.op("pool", lambda e: e.memset(ones[0:64, 0:64], 1.0), reads=[ones], writes=[ones])
        S.op("pool", lambda e: e.memset(ones[64:128, 64:128], 1.0), reads=[ones], writes=[ones])

        def cvec(name, key, n):
            t = S.sb("D_" + name, [128, n])
            S.dma("sp", t[:], Wd[key].t[lyr].rearrange("(c p) -> p c", p=128), reads=[Wd[key]], writes=[t],
                  allow_slow_non_contiguous=True)
            return t

        def cvec2(name, key):
            t = S.sb("D_" + name, [128, 4])
            S.dma("sp", t[:], Wd[key].t[lyr].rearrange("(c a) j -> (a j) c", a=2), reads=[Wd[key]], writes=[t],
                  allow_slow_non_contiguous=True)
            return t
        mu = cvec("mu", "rk_mu", 13)
        w0 = cvec("w0", "rk_w0", 4)
        a0 = cvec("a0", "rk_a0", 4)
        lg = cvec("lg", "rk_lnx_g", 4)
        lb = cvec("lb", "rk_lnx_b", 4)
        kkc = cvec2("kkc", "rk_kk")
        ka = cvec2("ka", "rk_ka")
        rkc = cvec2("rkc", "rk_rk")
        omka = S.sb("D_omka", [128, 4])
        S.op("pool", lambda e: e.tensor_scalar(out=omka[:], in0=ka[:], scalar1=-1.0, scalar2=1.0, op0=ALU.mult, op1=ALU.add),
             reads=[ka], writes=[omka])
        w2 = S.sb("D_w2", [64, 512], BF16)
        a2 = S.sb("D_a2", [128, 512], BF16)
        S.dma("pool", w2[:], Wd["rk_w2"].t[lyr], reads=[Wd["rk_w2"]], writes=[w2])
        S.dma("pool", a2[64:128, :], Wd["rk_a2"].t[lyr], reads=[Wd["rk_a2"]], writes=[a2])
        St = S.sb("D_state", [128, 4, 64])
        S.op("dve", lambda e: e.memset(St[:], 0.0), writes=[St])

        rst = S.sb("D_rst", [128, 13, TP + 1])
        xs = S.sb("D_xs", [128, 13, TP])
        th = S.sb("D_th", [128, TP], BF16)
        dd = S.sb("D_dd", [128, 4, TP])
        aa = S.sb("D_aa", [128, 4, TP])
        kkf = S.sb("D_kkf", [128, 4, TP])
        sq = S.sb("D_sq", [128, 4, TP])
        rn = S.sb("D_rn", [128, 4, TP])
        kp = S.sb("D_kp", [128, 4, TP])
        am = S.sb("D_am", [128, 4, TP])
        bm = S.sb("D_bm", [128, 4, TP])
        t1 = S.sb("D_t1", [128, 4, TP])
        bonus = S.sb("D_bonus", [128, 4, TP])
        vv = S.sb("D_vv", [128, 4, TP])
        tk = S.sb("D_tk", [128, 5, 4, 128])
        bcr = Ring([S.sb("D_bc%d" % i, [128, TB, 5, 256]) for i in range(2)])
        tmp = S.sb("D_tmp", [128, 4, 64])
        tmp2 = S.sb("D_tmp2", [128, 4, 64])
        kv = Ring([S.sb("D_kv%d" % i, [128, 4, 64]) for i in range(2)])
        sa = S.sb("D_sa", [128, 4])
        ybuf = S.sb("D_y", [128, 4, TP])
        ysq = S.sb("D_ysq", [128, 4, TP])
        mean = S.sb("D_mean", [128, 4, TP])
        var = S.sb("D_var", [128, 4, TP])
        rzt = S.sb("D_rz", [128, 4, TP], BF16)
        yo = S.sb("D_yo", [128, 4, TP], BF16)
        pa = Ring([S.ps("D_pa%d" % i, [128, 4, 128]) for i in range(4)])

        bc4 = lambda t: t[:].unsqueeze(2).to_broadcast([128, 4, TP])
        for nb in range(SEQ // TP):
            t0 = nb * TP
            S.dma("sp", rst[:, :, 1:TP + 1], scr["rsT"].t[:, t0:t0 + TP].rearrange("(c p) t -> p c t", p=128), reads=[scr["rsT"]],
                  writes=[rst])
            if nb == 0:
                S.op("pool", lambda e: e.memset(rst[:, :, 0:1], 0.0), reads=[rst], pwrites=[rst])
            else:
                S.dma("sp", rst[:, :, 0:1], scr["rsT"].t[:, t0 - 1:t0].rearrange("(c p) t -> p c t", p=128), reads=[scr["rsT"]],
                      pwrites=[rst], key=rst, allow_slow_non_contiguous=True)
            S.op("pool", lambda e: e.tensor_tensor(out=xs[:], in0=rst[:, :, 0:TP], in1=rst[:, :, 1:TP + 1], op=ALU.subtract),
                 reads=[rst], writes=[xs])
            S.op("pool", lambda e: e.tensor_tensor(out=xs[:], in0=xs[:], in1=mu[:].unsqueeze(2).to_broadcast([128, 13, TP]), op=ALU.mult),
                 reads=[xs, mu], writes=[xs])
            S.op("pool", lambda e: e.tensor_tensor(out=xs[:], in0=xs[:], in1=rst[:, :, 1:TP + 1], op=ALU.add), reads=[xs, rst], writes=[xs])
            r = xs[:, 0:4, :]; k = xs[:, 4:8, :]; v = xs[:, 8:12, :]
            S.op("act", lambda e: e.activation(out=th[0:64, :], in_=xs[0:64, 12, :], func=AF.Tanh), reads=[xs], pwrites=[th])
            S.op("act", lambda e: e.copy(out=th[64:128, :], in_=xs[64:128, 12, :]), reads=[xs], pwrites=[th])
            pw = pa.next(); pp = pa.next()
            for p in range(4):
                S.op("pe", lambda e, p=p, pw=pw: e.matmul(pw[:, p, :], lhsT=w2[0:64, p * 128:(p + 1) * 128], rhs=th[0:64, :], start=True, stop=True),
                     reads=[w2, th], writes=[pw] if p == 0 else (), pwrites=() if p == 0 else [pw])
                S.op("pe", lambda e, p=p, pp=pp: e.matmul(pp[:, p, :], lhsT=a2[64:128, p * 128:(p + 1) * 128], rhs=th[64:128, :], start=True, stop=True),
                     reads=[a2, th], writes=[pp] if p == 0 else (), pwrites=() if p == 0 else [pp])
            for p in range(4):
                S.op("act", lambda e, p=p, pw=pw: e.activation(out=dd[:, p, :], in_=pw[:, p, :], func=AF.Sigmoid, bias=w0[:, p:p + 1]),
                     reads=[pw, w0], pwrites=[dd])
                S.op("act", lambda e, p=p, pp=pp: e.activation(out=aa[:, p, :], in_=pp[:, p, :], func=AF.Sigmoid, bias=a0[:, p:p + 1]),
                     reads=[pp, a0], pwrites=[aa])
            S.op("act", lambda e: e.activation(out=dd[:], in_=dd[:], func=AF.Exp, scale=-0.6065306597126334), reads=[dd], writes=[dd])
            S.op("pool", lambda e: e.tensor_tensor(out=kkf[:], in0=k, in1=bc4(kkc), op=ALU.mult), reads=[xs, kkc], writes=[kkf])
            S.op("pool", lambda e: e.tensor_tensor(out=sq[:], in0=kkf[:], in1=kkf[:], op=ALU.mult), reads=[kkf], writes=[sq])
            pn = pa.next()
            for p in range(4):
                S.op("pe", lambda e, p=p, pn=pn: e.matmul(pn[:, p, :], lhsT=ones[:], rhs=sq[:, p, :], start=True, stop=True),
                     reads=[ones, sq], writes=[pn] if p == 0 else (), pwrites=() if p == 0 else [pn])
            S.op("act", lambda e, pn=pn: e.activation(out=rn[:], in_=pn[:], func=AF.Sqrt), reads=[pn], writes=[rn])
            S.op("pool", lambda e: e.tensor_scalar(out=rn[:], in0=rn[:], scalar1=1e-12, scalar2=None, op0=ALU.max), reads=[rn], writes=[rn])
            S.op("dve", lambda e: e.reciprocal(out=rn[:], in_=rn[:]), reads=[rn], writes=[rn])
            S.op("pool", lambda e: e.tensor_tensor(out=kkf[:], in0=kkf[:], in1=rn[:], op=ALU.mult), reads=[kkf, rn], writes=[kkf])
            S.op("pool", lambda e: e.tensor_tensor(out=t1[:], in0=aa[:], in1=bc4(ka), op=ALU.mult), reads=[aa, ka], writes=[t1])
            S.op("pool", lambda e: e.tensor_tensor(out=t1[:], in0=t1[:], in1=bc4(omka), op=ALU.add), reads=[t1, omka], writes=[t1])
            S.op("pool", lambda e: e.tensor_tensor(out=kp[:], in0=k, in1=t1[:], op=ALU.mult), reads=[xs, t1], writes=[kp])
            S.op("pool", lambda e: e.tensor_scalar(out=am[:], in0=kkf[:], scalar1=-1.0, scalar2=None, op0=ALU.mult), reads=[kkf], writes=[am])
            S.op("pool", lambda e: e.tensor_tensor(out=bm[:], in0=kkf[:], in1=aa[:], op=ALU.mult), reads=[kkf, aa], writes=[bm])
            S.op("pool", lambda e: e.tensor_tensor(out=t1[:], in0=r, in1=kp[:], op=ALU.mult), reads=[xs, kp], writes=[t1])
            S.op("pool", lambda e: e.tensor_tensor(out=sq[:], in0=t1[:], in1=bc4(rkc), op=ALU.mult), reads=[t1, rkc], writes=[sq])
            pr = pa.next()
            for p in range(4):
                S.op("pe", lambda e, p=p, pr=pr: e.matmul(pr[:, p, :], lhsT=ones[:], rhs=sq[:, p, :], start=True, stop=True),
                     reads=[ones, sq], writes=[pr] if p == 0 else (), pwrites=() if p == 0 else [pr])
            S.op("act", lambda e, pr=pr: e.copy(out=bonus[:], in_=pr[:]), reads=[pr], writes=[bonus])
            S.op("pool", lambda e: e.tensor_tensor(out=bonus[:], in0=bonus[:], in1=v, op=ALU.mult), reads=[bonus, xs], writes=[bonus])
            S.op("pool", lambda e: e.tensor_copy(out=vv[:], in_=v), reads=[xs], writes=[vv])
            S.op("pool", lambda e: e.tensor_copy(out=t1[:], in_=r), reads=[xs], writes=[t1])
            for oi, src in enumerate((am, bm, dd, kp, t1)):
                pt = pa.next()
                for p in range(4):
                    S.op("pe", lambda e, p=p, src=src, pt=pt: e.transpose(out=pt[:, p, :], in_=src[:, p, :], identity=identF[:]),
                         reads=[src, identF], writes=[pt] if p == 0 else (), pwrites=() if p == 0 else [pt])
                S.op("act", lambda e, oi=oi, pt=pt: e.copy(out=tk[:, oi, :, :], in_=pt[:]), reads=[pt], pwrites=[tk])
            for oi in range(5):
                for h2 in range(2):
                    S.dma("sp", xtok.t[t0:t0 + TP, oi, h2, :].rearrange("t (p j) -> t p j", p=4), tk[:, oi, :, h2 * 64:(h2 + 1) * 64],
                          reads=[tk], pwrites=[xtok], key=tk)
            S.dma("sp", rzt[:], scr["rzT"].t[:, t0:t0 + TP].rearrange("(c p) t -> p c t", p=128), reads=[scr["rzT"]], writes=[rzt])
            xflat = xtok.t.rearrange("t o h c -> (t o) h c")
            for tb in range(0, TP, TB):
                bc = bcr.next()
                for h2 in range(2):
                    S.dma("sp", bc[h2 * 64:(h2 + 1) * 64, :, :, :].rearrange("p t o c -> p (t o) c"),
                          xflat[(t0 + tb) * 5:(t0 + tb + TB) * 5, h2, :].partition_broadcast(64),
                          reads=[xtok], writes=[bc] if h2 == 0 else (), pwrites=() if h2 == 0 else [bc], key=bc)
                for tt in range(TB):
                    t = tb + tt
                    A = bc[:, tt, 0, :].rearrange("p (a j) -> p a j", a=4)
                    B = bc[:, tt, 1, :].rearrange("p (a j) -> p a j", a=4)
                    Dd = bc[:, tt, 2, :].rearrange("p (a j) -> p a j", a=4)
                    Kk = bc[:, tt, 3, :].rearrange("p (a j) -> p a j", a=4)
                    R = bc[:, tt, 4, :].rearrange("p (a j) -> p a j", a=4)
                    kvb = kv.next()
                    S.op("pool", lambda e, Kk=Kk, t=t, kvb=kvb: e.tensor_tensor(out=kvb[:], in0=Kk, in1=vv[:, :, t:t + 1].to_broadcast([128, 4, 64]),
                                                                              op=ALU.mult), reads=[bc, vv], writes=[kvb])
                    S.op("dve", lambda e, A=A: e.tensor_tensor(out=tmp[:], in0=St[:], in1=A, op=ALU.mult), reads=[St, bc], writes=[tmp])
                    S.op("dve", lambda e: e.tensor_reduce(out=sa[:], in_=tmp[:], axis=AX.X, op=ALU.add), reads=[tmp], writes=[sa])
                    S.op("dve", lambda e, Dd=Dd: e.tensor_tensor(out=St[:], in0=St[:], in1=Dd, op=ALU.mult), reads=[St, bc, tmp], writes=[St])
                    S.op("dve", lambda e, B=B: e.tensor_tensor(out=tmp2[:], in0=B, in1=sa[:].unsqueeze(2).to_broadcast([128, 4, 64]), op=ALU.mult),
                         reads=[bc, sa], writes=[tmp2])
                    S.op("dve", lambda e: e.tensor_tensor(out=St[:], in0=St[:], in1=tmp2[:], op=ALU.add), reads=[St, tmp2], writes=[St])
                    S.op("dve", lambda e, kvb=kvb: e.tensor_tensor(out=St[:], in0=St[:], in1=kvb[:], op=ALU.add), reads=[St, kvb], writes=[St])
                    S.op("dve", lambda e, R=R: e.tensor_tensor(out=tmp[:], in0=St[:], in1=R, op=ALU.mult), reads=[St, bc], writes=[tmp])
                    S.op("dve", lambda e, t=t: e.tensor_reduce(out=ybuf[:, :, t], in_=tmp[:], axis=AX.X, op=ALU.add), reads=[tmp], pwrites=[ybuf])
            S.op("pool", lambda e: e.tensor_tensor(out=ysq[:], in0=ybuf[:], in1=ybuf[:], op=ALU.mult), reads=[ybuf], writes=[ysq])
            pm = pa.next(); pq = pa.next()
            for p in range(4):
                S.op("pe", lambda e, p=p, pm=pm: e.matmul(pm[:, p, :], lhsT=ones[:], rhs=ybuf[:, p, :], start=True, stop=True),
                     reads=[ones, ybuf], writes=[pm] if p == 0 else (), pwrites=() if p == 0 else [pm])
                S.op("pe", lambda e, p=p, pq=pq: e.matmul(pq[:, p, :], lhsT=ones[:], rhs=ysq[:, p, :], start=True, stop=True),
                     reads=[ones, ysq], writes=[pq] if p == 0 else (), pwrites=() if p == 0 else [pq])
            S.op("act", lambda e, pm=pm: e.activation(out=mean[:], in_=pm[:], func=AF.Copy, scale=1.0 / 64), reads=[pm], writes=[mean])
            S.op("act", lambda e, pq=pq: e.activation(out=var[:], in_=pq[:], func=AF.Copy, scale=1.0 / 64), reads=[pq], writes=[var])
            S.op("pool", lambda e: e.tensor_tensor(out=ysq[:], in0=mean[:], in1=mean[:], op=ALU.mult), reads=[mean, ysq], writes=[ysq])
            S.op("pool", lambda e: e.tensor_tensor(out=var[:], in0=var[:], in1=ysq[:], op=ALU.subtract), reads=[var, ysq], writes=[var])
            S.op("act", lambda e: e.activation(out=var[:], in_=var[:], func=AF.Sqrt, bias=GN_EPS, scale=1.0), reads=[var], writes=[var])
            S.op("dve", lambda e: e.reciprocal(out=var[:], in_=var[:]), reads=[var], writes=[var])
            S.op("pool", lambda e: e.tensor_tensor(out=mean[:], in0=ybuf[:], in1=mean[:], op=ALU.subtract), reads=[ybuf, mean], writes=[mean])
            S.op("pool", lambda e: e.tensor_tensor(out=mean[:], in0=mean[:], in1=var[:], op=ALU.mult), reads=[mean, var], writes=[mean])
            S.op("pool", lambda e: e.tensor_tensor(out=mean[:], in0=mean[:], in1=bc4(lg), op=ALU.mult), reads=[mean, lg], writes=[mean])
            S.op("pool", lambda e: e.tensor_tensor(out=mean[:], in0=mean[:], in1=bc4(lb), op=ALU.add), reads=[mean, lb], writes=[mean])
            S.op("pool", lambda e: e.tensor_tensor(out=mean[:], in0=mean[:], in1=bonus[:], op=ALU.add), reads=[mean, bonus], writes=[mean])
            S.op("pool", lambda e: e.tensor_tensor(out=yo[:], in0=mean[:], in1=rzt[:], op=ALU.mult), reads=[mean, rzt], writes=[yo])
            S.dma("sp", scr["ysT"].t[2, :, t0:t0 + TP].rearrange("(c p) t -> p c t", p=128), yo[:], reads=[yo], pwrites=[scr["ysT"]], key=yo)
        _barrier(S)
        S.stack_pop()


_NC_CACHE = {}


def kernel(**inputs):
    SEQ = 8192
    if "nc" not in _NC_CACHE:
        _NC_CACHE["nc"] = build(SEQ, nlayers=2, enable=(1, 1, 1), scr_kind="Internal")
    nc = _NC_CACHE["nc"]
    x = np.ascontiguousarray(np.asarray(inputs["x"], dtype=np.float32))
    p = np.asarray(inputs["p"], dtype=np.float32)
    base = {}
    for k in WSPEC:
        v = np.ascontiguousarray(np.asarray(inputs[k], dtype=np.float32))
        base[k] = v.reshape(WSPEC[k])
    in_maps = []
    for b in range(8):
        m = dict(base)
        m["x"] = np.ascontiguousarray(x[b])
        m["p"] = np.ascontiguousarray(p[:, b])
        in_maps.append(m)
    res = run_bass_kernel_spmd(nc, in_maps, core_ids=list(range(8)))
    return np.stack([np.asarray(r["out"], dtype=np.float32) for r in res.results], axis=0)
```

```python
import contextlib
import numpy as np
import concourse.bass as bass
import concourse.mybir as mybir

F32 = mybir.dt.float32
BF16 = mybir.dt.bfloat16
AF = mybir.ActivationFunctionType
ALU = mybir.AluOpType
AX = mybir.AxisListType

ENGS = ("pe", "act", "dve", "pool", "sp")


class Buf:
    __slots__ = ("name", "w", "wfull", "r", "t")

    def __init__(self, name, t=None):
        self.name = name
        self.t = t
        self.w = []
        self.wfull = []
        self.r = []

    def __getitem__(self, k):
        return self.t[k]


class Op:
    __slots__ = ("eng", "fn", "deps", "marked", "tick", "dma", "idx")

    def __init__(self, eng, fn, dma):
        self.eng = eng
        self.fn = fn
        self.deps = []
        self.marked = False
        self.tick = None
        self.dma = dma
        self.idx = None


class DmaSem:
    def __init__(self):
        self.sem = None
        self.count = 0


class Sched:
    def __init__(self, nc, stack):
        self.nc = nc
        self.stack = stack
        self.ops = {e: [] for e in ENGS}
        self.all_ops = []
        self.dsems = {}
        self.n_sems = 0
        self.fence = []
        self.stacks = [stack]
        self.phase_keys = []
        self.free_ds = []
        self.all_ds = []
        self.keep = []

    def stack_push(self, st):
        self.stacks.append(st)
        self.phase_keys.append([])

    def stack_pop(self):
        self.stacks.pop()
        for kid in self.phase_keys.pop():
            ds = self.dsems.pop(kid, None)
            if ds is not None:
                self.free_ds.append(ds)

    def sb(self, name, shape, dt=F32):
        self.n_sems += 1
        name = "%s_u%d" % (name, self.n_sems)
        t = self.stacks[-1].enter_context(self.nc.sbuf_tensor(name, list(shape), dt))
        return Buf(name, t)

    def ps(self, name, shape, dt=F32):
        self.n_sems += 1
        name = "%s_u%d" % (name, self.n_sems)
        t = self.stacks[-1].enter_context(self.nc.psum_tensor(name, list(shape), dt))
        return Buf(name, t)

    def dram(self, name, shape, dt, kind="Internal"):
        t = self.nc.dram_tensor(name, list(shape), dt, kind=kind)
        return Buf(name, t.ap())

    def _add(self, eng, fn, reads, writes, pwrites, dma):
        op = Op(eng, fn, dma)
        deps = list(self.fence)
        for b in reads:
            deps.extend(b.w)
        for b in writes:
            deps.extend(b.w)
            deps.extend(b.r)
        for b in pwrites:
            deps.extend(b.wfull)
            deps.extend(b.r)
        seen = set()
        for d in deps:
            if id(d) in seen or d is op:
                continue
            seen.add(id(d))
            if d.eng == "pe" and eng == "pe" and d.dma is None and dma is None:
                continue
            op.deps.append(d)
            d.marked = True
        for b in reads:
            b.r.append(op)
            if len(b.r) > 24:
                b.r = self._prune(b.r)
        for b in writes:
            b.w = [op]
            b.wfull = [op]
            b.r = []
        for b in pwrites:
            b.w.append(op)
            if len(b.w) > 24:
                b.w = self._prune(b.w)
        op.idx = len(self.all_ops)
        self.all_ops.append(op)
        self.ops[eng].append(op)
        return op

    @staticmethod
    def _prune(lst):
        last = {}
        for o in lst:
            key = (o.eng, None) if o.dma is None else ("dma", id(o.dma))
            last[key] = o
        return list(last.values())

    def op(self, eng, fn, reads=(), writes=(), pwrites=()):
        return self._add(eng, fn, reads, writes, pwrites, None)

    def dma(self, eng, out_ap, in_ap, reads=(), writes=(), pwrites=(), key=None, **kw):
        if key is None:
            key = (list(writes) + list(pwrites))[0]
        ds = self.dsems.get(id(key))
        if ds is None:
            if self.free_ds:
                ds = self.free_ds.pop()
            else:
                ds = DmaSem()
                self.all_ds.append(ds)
            self.dsems[id(key)] = ds
            self.keep.append(key)
            if self.phase_keys:
                self.phase_keys[-1].append(id(key))
        fn = lambda e, o=out_ap, i=in_ap, kw=kw: e.dma_start(out=o, in_=i, **kw)
        op = self._add(eng, fn, reads, writes, pwrites, ds)
        ds.count += 16
        op.tick = ds.count
        return op

    def barrier_bufs(self, bufs):
        pass

    def emit(self):
        nc = self.nc
        stack = self.stack
        esem = {}
        for e in ENGS:
            esem[e] = stack.enter_context(nc.semaphore("s_" + e))
        for ds in self.all_ds:
            ds.sem = stack.enter_context(nc.semaphore("d%d" % self.n_sems))
            self.n_sems += 1
        for e in ENGS:
            c = 0
            for o in self.ops[e]:
                if o.dma is None:
                    if o.marked:
                        c += 1
                        o.tick = c
        self.max_ticks = {e: max([o.tick or 0 for o in self.ops[e] if o.dma is None] + [0]) for e in ENGS}

        def evkey(d):
            if d.dma is not None:
                return ("d", id(d.dma)), d.dma.sem, d.tick
            return ("e", d.eng), esem[d.eng], d.tick

        def run(eng_name, eng):
            seen = {}
            for o in self.ops[eng_name]:
                waits = {}
                for d in o.deps:
                    k, sem, val = evkey(d)
                    if seen.get(k, 0) >= val:
                        continue
                    if k not in waits or waits[k][1] < val:
                        waits[k] = (sem, val)
                for k, (sem, val) in waits.items():
                    eng.wait_ge(sem, val)
                    seen[k] = val
                inst = o.fn(eng)
                if o.dma is not None:
                    inst.then_inc(o.dma.sem, 16)
                elif o.marked:
                    inst.then_inc(esem[eng_name], 1)
            if eng_name == "sp":
                for e2 in ENGS:
                    m = self.max_ticks[e2]
                    if m > 0:
                        eng.wait_ge(esem[e2], m)
                for ds in self.all_ds:
                    if ds.count:
                        eng.wait_ge(ds.sem, ds.count)

        block = stack.enter_context(nc.Block())

        @block.tensor
        def _(e):
            run("pe", e)

        @block.scalar
        def _(e):
            run("act", e)

        @block.vector
        def _(e):
            run("dve", e)

        @block.gpsimd
        def _(e):
            run("pool", e)

        @block.sync
        def _(e):
            run("sp", e)


from concourse.bass_utils import run_bass_kernel_spmd

D = 1024
NCOL = 8600
PLE = 256
EPS = 1e-6


DEBUG = {}
_dbg_n = [0]


def dbg_dump(S, name, ap, buf, shape, cond=True):
    if not DEBUG.get("on") or not cond:
        return
    _dbg_n[0] += 1
    t = S.stacks[-1].enter_context(S.nc.sbuf_tensor("dbgsb_%d" % _dbg_n[0], list(shape), F32))
    tb = Buf("dbgsb", t)
    d = S.dram("dbg_" + name, list(shape), F32, kind="ExternalOutput")
    S.op("act", lambda e: e.copy(out=t[:], in_=ap), reads=[buf], writes=[tb])
    S.dma("sp", d.t, t[:], reads=[tb], writes=[d], key=tb)


class Ring:
    def __init__(self, bufs):
        self.bufs = bufs
        self.i = 0

    def next(self):
        b = self.bufs[self.i % len(self.bufs)]
        self.i += 1
        return b


def _barrier(S):
    fence = []
    for e in ENGS:
        comp = [o for o in S.ops[e] if o.dma is None]
        if comp:
            fence.append(comp[-1])
    lastd = {}
    for o in S.all_ops:
        if o.dma is not None:
            lastd[id(o.dma)] = o
    fence.extend(lastd.values())
    S.fence = fence


def make_ident(S, name="ident", dt=BF16):
    ident = S.sb(name, [128, 128], dt)
    S.op("pool", lambda e: e.memset(ident[:], 0.0), writes=[ident])
    S.op("pool", lambda e: e.affine_select(out=ident[:], in_=ident[:], pattern=[[-1, 128]],
                                           compare_op=ALU.not_equal, fill=1.0, base=0,
                                           channel_multiplier=1), reads=[ident], writes=[ident])
    return ident


def load_w_bf16(S, dst, k, src_ap, srcbuf):
    S.dma("pool", dst, src_ap, reads=[srcbuf], pwrites=[k], key=k, max_dma_last_dim=4096)


def rmsnorm_tile(S, xt_ap, xt_buf, g_buf, h_ap, h_buf, sq, ss, rs, eps=EPS, extra_reads=()):
    S.op("act", lambda e: e.activation(out=sq[:], in_=xt_ap, func=AF.Square, accum_out=ss[:]),
         reads=[xt_buf] + list(extra_reads), writes=[sq, ss])
    S.op("act", lambda e: e.activation(out=rs[:], in_=ss[:], func=AF.Sqrt, scale=1.0 / D, bias=eps),
         reads=[ss], writes=[rs])
    S.op("dve", lambda e: e.reciprocal(out=rs[:], in_=rs[:]), reads=[rs], writes=[rs])
    S.op("dve", lambda e: e.scalar_tensor_tensor(out=h_ap, in0=xt_ap, scalar=rs[:, 0:1], in1=g_buf[:],
                                                 op0=ALU.mult, op1=ALU.mult),
         reads=[xt_buf, rs, g_buf], pwrites=[h_buf])


def phase_A(S, nc, SEQ, lyr, x_src, Wd, scr):
    TT = 512
    nsub = TT // 128
    with contextlib.ExitStack() as st:
        S.stack_push(st)
        wt = S.sb("A_w", [128, 8, NCOL], BF16)
        gt = S.sb("A_g", [128, D])
        ident = make_ident(S, "A_ident")
        xt = S.sb("A_x", [128, nsub, D])
        sq = S.sb("A_sq", [128, D], BF16)
        ss = S.sb("A_ss", [128, 1])
        rs = S.sb("A_rs", [128, 1])
        h = S.sb("A_h", [128, nsub, D], BF16)
        hT = S.sb("A_hT", [128, 8, TT], BF16)
        stg_b = Ring([S.sb("A_sb%d" % i, [128, 512], BF16) for i in range(4)])
        stg_f = Ring([S.sb("A_sf%d" % i, [128, 512], F32) for i in range(3)])
        pT = Ring([S.ps("A_pT%d" % i, [128, 8, 128], BF16) for i in range(2)])
        pacc = Ring([S.ps("A_pa%d" % i, [128, 512], F32) for i in range(6)])

        w_in = Wd["w_in"]
        for k in range(8):
            S.dma("pool", wt[:, k, :], w_in.t[lyr, k * 128:(k + 1) * 128, 0:NCOL], reads=[w_in], pwrites=[wt],
                  key=wt, max_dma_last_dim=4096)
        S.dma("sp", gt[:], Wd["norm_g"].t[lyr:lyr + 1, :].partition_broadcast(128), reads=[Wd["norm_g"]],
              writes=[gt])

        FM = []
        for c in range(4):
            FM.append((c * 128, scr["qT"], c * 128, AF.Copy, 0.125, BF16))
        FM.append((512, scr["kcT"], 0, None, 1.0, BF16))
        FM.append((640, scr["vcT"], 0, None, 1.0, BF16))
        FM.append((768, scr["ksT"], 0, None, 1.0, BF16))
        FM.append((1024, scr["kwT"], 0, None, 1.0, BF16))
        for c in range(13):
            FM.append((3352 + c * 128, scr["rsT"], c * 128, None, 1.0, F32))
        for c in range(4):
            FM.append((5016 + c * 128, scr["rzT"], c * 128, AF.Silu, 1.0, BF16))
        for c in range(24):
            FM.append((5528 + c * 128, scr["mgT"], c * 128, AF.Sigmoid, 1.0, BF16))
        TM = [
            (896, 128, scr["vsw"], 0, None, BF16),
            (1152, 128, scr["vsw"], 128, None, BF16),
            (1280, 24, scr["gate"], 0, AF.Sigmoid, F32),
            (1304, 512, scr["nzs"], 0, AF.Silu, BF16),
            (1816, 512, scr["su"], 0, None, F32),
            (2328, 512, scr["sv"], 0, None, F32),
            (2840, 512, scr["szs"], 0, AF.Silu, BF16),
        ]
        evac_i = [0]

        def evac(out_ap, out_buf, in_ap, in_buf, func, scale):
            if func is None and scale == 1.0:
                if evac_i[0] % 2 == 0:
                    S.op("dve", lambda e: e.tensor_copy(out=out_ap, in_=in_ap), reads=[in_buf], writes=[out_buf])
                else:
                    S.op("act", lambda e: e.copy(out=out_ap, in_=in_ap), reads=[in_buf], writes=[out_buf])
                evac_i[0] += 1
            else:
                S.op("act", lambda e: e.activation(out=out_ap, in_=in_ap, func=func, scale=scale),
                     reads=[in_buf], writes=[out_buf])

        for ti in range(SEQ // TT):
            t0 = ti * TT
            S.dma("sp", xt[:], x_src.t[t0:t0 + TT, :].rearrange("(s p) d -> p s d", p=128), reads=[x_src],
                  writes=[xt])
            for s in range(nsub):
                rmsnorm_tile(S, xt[:, s, :], xt, gt, h[:, s, :], h, sq, ss, rs)
                pt = pT.next()
                for k in range(8):
                    S.op("pe", lambda e, k=k, s=s, pt=pt: e.transpose(out=pt[:, k, :], in_=h[:, s, k * 128:(k + 1) * 128],
                                                                     identity=ident[:]),
                         reads=[h, ident], writes=[pt] if k == 0 else (), pwrites=() if k == 0 else [pt])
                S.op("dve", lambda e, s=s, pt=pt: e.tensor_copy(out=hT[:, :, s * 128:(s + 1) * 128], in_=pt[:]),
                     reads=[pt], pwrites=[hT])
            for (c0, dbuf, r0, func, scale, dt) in FM:
                pa = pacc.next()
                for k in range(8):
                    S.op("pe", lambda e, k=k, pa=pa, c0=c0: e.matmul(pa[:], lhsT=wt[:, k, c0:c0 + 128], rhs=hT[:, k, :],
                                                                    start=(k == 0), stop=(k == 7)),
                         reads=[wt, hT], writes=[pa] if k == 0 else (), pwrites=() if k == 0 else [pa])
                sg = stg_b.next() if dt == BF16 else stg_f.next()
                evac(sg[:], sg, pa[:], pa, func, scale)
                S.dma("sp", dbuf.t[r0:r0 + 128, t0:t0 + TT], sg[:], reads=[sg], pwrites=[dbuf], key=sg)
            for s in range(nsub):
                for (c0, ncol, dbuf, dc0, func, dt) in TM:
                    pa = pacc.next()
                    for k in range(8):
                        S.op("pe", lambda e, k=k, pa=pa, c0=c0, ncol=ncol, s=s: e.matmul(
                            pa[:, 0:ncol], lhsT=hT[:, k, s * 128:(s + 1) * 128], rhs=wt[:, k, c0:c0 + ncol],
                            start=(k == 0), stop=(k == 7)),
                            reads=[wt, hT], writes=[pa] if k == 0 else (), pwrites=() if k == 0 else [pa])
                    sg = stg_b.next() if dt == BF16 else stg_f.next()
                    evac(sg[:, 0:ncol], sg, pa[:, 0:ncol], pa, func, 1.0)
                    S.dma("sp", dbuf.t[t0 + s * 128:t0 + (s + 1) * 128, dc0:dc0 + ncol], sg[:, 0:ncol], reads=[sg],
                          pwrites=[dbuf], key=sg)
        _barrier(S)
        S.stack_pop()


def make_scratch(S, SEQ, kind="Internal"):
    scr = {}
    def mk(name, shape, dt):
        scr[name] = S.dram(name, shape, dt, kind=kind)
    mk("qT", [512, SEQ], BF16)
    mk("kcT", [128, SEQ], BF16)
    mk("vcT", [128, SEQ], BF16)
    mk("ksT", [128, SEQ], BF16)
    mk("kwT", [128, SEQ], BF16)
    mk("vsw", [SEQ, 256], BF16)
    mk("gate", [SEQ, 24], F32)
    mk("nzs", [SEQ, 512], BF16)
    mk("su", [SEQ, 512], F32)
    mk("sv", [SEQ, 512], F32)
    mk("szs", [SEQ, 512], BF16)
    mk("rsT", [1664, SEQ], F32)
    mk("rzT", [512, SEQ], BF16)
    mk("mgT", [3072, SEQ], BF16)
    mk("ysT", [3, 512, SEQ], BF16)
    mk("xtok", [SEQ, 5, 2, 256], F32)
    return scr


def phase_C(S, nc, SEQ, lyr, Wd, scr):
    LN_EPS = 1e-5
    with contextlib.ExitStack() as st:
        S.stack_push(st)
        ident = make_ident(S, "C_ident")
        wraw = S.sb("C_wraw", [128, 8, 128])
        wbf = S.sb("C_wbf", [128, 8, 128], BF16)
        WT = S.sb("C_WT", [128, 8, 128], BF16)
        bsT = S.sb("C_bsT", [128, 8])
        lng = S.sb("C_lng", [128, 512])
        lnb = S.sb("C_lnb", [128, 512])
        pw = S.ps("C_pw", [128, 8, 128], BF16)
        S.dma("sp", wraw[:], Wd["sg_w"].t[lyr].rearrange("g t s -> t g s"), reads=[Wd["sg_w"]], writes=[wraw])
        S.dma("sp", bsT[:], Wd["sg_b"].t[lyr].rearrange("g t -> t g"), reads=[Wd["sg_b"]], writes=[bsT],
              allow_slow_non_contiguous=True)
        S.dma("sp", lng[:], Wd["sg_ln_g"].t[lyr:lyr + 1, :].partition_broadcast(128), reads=[Wd["sg_ln_g"]], writes=[lng])
        S.dma("sp", lnb[:], Wd["sg_ln_b"].t[lyr:lyr + 1, :].partition_broadcast(128), reads=[Wd["sg_ln_b"]], writes=[lnb])
        S.op("pool", lambda e: e.affine_select(out=wraw[:], in_=wraw[:], pattern=[[0, 8], [-1, 128]],
                                               compare_op=ALU.is_ge, fill=0.0, base=0, channel_multiplier=1),
             reads=[wraw], writes=[wraw])
        S.op("dve", lambda e: e.tensor_copy(out=wbf[:], in_=wraw[:]), reads=[wraw], writes=[wbf])
        for g in range(8):
            S.op("pe", lambda e, g=g: e.transpose(out=pw[:, g, :], in_=wbf[:, g, :], identity=ident[:]),
                 reads=[wbf, ident], pwrites=[pw])
        S.op("dve", lambda e: e.tensor_copy(out=WT[:], in_=pw[:]), reads=[pw], writes=[WT])

        NB = 2
        svt = Ring([S.sb("C_sv%d" % i, [128, 512]) for i in range(NB)])
        sut = Ring([S.sb("C_su%d" % i, [128, 512]) for i in range(NB)])
        szt = Ring([S.sb("C_sz%d" % i, [128, 512], BF16) for i in range(NB)])
        stats = S.sb("C_stats", [128, 6])
        mv = S.sb("C_mv", [128, 2])
        rstd = S.sb("C_rstd", [128, 1])
        vn0 = S.sb("C_vnf", [128, 512])
        vn = Ring([S.sb("C_vn%d" % i, [128, 512], BF16) for i in range(2)])
        y0 = S.sb("C_y0", [128, 512])
        yb = Ring([S.sb("C_yb%d" % i, [128, 512], BF16) for i in range(2)])
        pm = Ring([S.ps("C_pm%d" % i, [128, 512]) for i in range(2)])
        pt = Ring([S.ps("C_pt%d" % i, [128, 4, 128], BF16) for i in range(2)])
        stg = Ring([S.sb("C_stg%d" % i, [128, 4, 512], BF16) for i in range(2)])
        ys = scr["ysT"]
        sgb = None
        for c in range(SEQ // 128):
            t0 = c * 128
            v = svt.next(); u = sut.next(); z = szt.next()
            S.dma("sp", v[:], scr["sv"].t[t0:t0 + 128, :], reads=[scr["sv"]], writes=[v])
            S.dma("sp", u[:], scr["su"].t[t0:t0 + 128, :], reads=[scr["su"]], writes=[u])
            S.dma("sp", z[:], scr["szs"].t[t0:t0 + 128, :], reads=[scr["szs"]], writes=[z])
            S.op("dve", lambda e, v=v: e.bn_stats(out=stats[:], in_=v[:]), reads=[v], writes=[stats])
            S.op("dve", lambda e: e.bn_aggr(out=mv[:], in_=stats[:]), reads=[stats], writes=[mv])
            S.op("act", lambda e: e.activation(out=rstd[:], in_=mv[:, 1:2], func=AF.Sqrt, bias=LN_EPS, scale=1.0),
                 reads=[mv], writes=[rstd])
            S.op("dve", lambda e: e.reciprocal(out=rstd[:], in_=rstd[:]), reads=[rstd], writes=[rstd])
            S.op("dve", lambda e, v=v: e.tensor_scalar(out=vn0[:], in0=v[:], scalar1=mv[:, 0:1], scalar2=rstd[:, 0:1],
                                                       op0=ALU.subtract, op1=ALU.mult),
                 reads=[v, mv, rstd], writes=[vn0])
            S.op("pool", lambda e: e.tensor_tensor(out=vn0[:], in0=vn0[:], in1=lng[:], op=ALU.mult),
                 reads=[vn0, lng], writes=[vn0])
            vb = vn.next()
            S.op("pool", lambda e, vb=vb: e.tensor_tensor(out=vb[:], in0=vn0[:], in1=lnb[:], op=ALU.add),
                 reads=[vn0, lnb], writes=[vb])
            pmm = pm.next()
            for g in range(8):
                S.op("pe", lambda e, g=g, vb=vb, pmm=pmm: e.matmul(pmm[:, g * 64:(g + 1) * 64], lhsT=WT[:, g, :],
                                                                   rhs=vb[:, g * 64:(g + 1) * 64], start=True, stop=True),
                     reads=[WT, vb], writes=[pmm] if g == 0 else (), pwrites=() if g == 0 else [pmm])
            S.op("dve", lambda e, pmm=pmm: e.tensor_tensor(
                out=y0[:].rearrange("p (g d) -> p g d", g=8), in0=pmm[:].rearrange("p (g d) -> p g d", g=8),
                in1=bsT[:].unsqueeze(2).to_broadcast([128, 8, 64]), op=ALU.add), reads=[pmm, bsT], writes=[y0])
            S.op("pool", lambda e, u=u: e.tensor_tensor(out=y0[:], in0=y0[:], in1=u[:], op=ALU.mult),
                 reads=[y0, u], writes=[y0])
            y = yb.next()
            S.op("dve", lambda e, y=y, z=z: e.tensor_tensor(out=y[:], in0=y0[:], in1=z[:], op=ALU.mult),
                 reads=[y0, z], writes=[y])
            ptt = pt.next()
            for k in range(4):
                S.op("pe", lambda e, k=k, y=y, ptt=ptt: e.transpose(out=ptt[:, k, :], in_=y[:, k * 128:(k + 1) * 128],
                                                                    identity=ident[:]),
                     reads=[y, ident], writes=[ptt] if k == 0 else (), pwrites=() if k == 0 else [ptt])
            if c % 4 == 0:
                sgb = stg.next()
            cc = c % 4
            S.op("act", lambda e, ptt=ptt, sgb=sgb, cc=cc: e.copy(out=sgb[:, :, cc * 128:(cc + 1) * 128], in_=ptt[:]),
                 reads=[ptt], writes=[sgb] if cc == 0 else (), pwrites=() if cc == 0 else [sgb])
            if cc == 3 or c == SEQ // 128 - 1:
                tb = (c // 4) * 512
                n = (cc + 1) * 128
                S.dma("sp", ys.t[1, :, tb:tb + n].rearrange("(k p) t -> p k t", p=128), sgb[:, :, 0:n], reads=[sgb],
                      pwrites=[ys], key=sgb)
        _barrier(S)
        S.stack_pop()


def phase_E(S, nc, SEQ, lyr, x_src, x_dst, Wd, scr, final):
    TT = 512
    nsub = 4
    with contextlib.ExitStack() as st:
        S.stack_push(st)
        ident = make_ident(S, "E_ident")
        wb = S.sb("E_wb", [128, 3, 4, D], BF16)
        wo = S.sb("E_wo", [128, 8, D], BF16)
        wpg = S.sb("E_wpg", [128, 8, D], BF16)
        wpp = S.sb("E_wpp", [128, 2, D], BF16)
        gpl = S.sb("E_gpl", [128, D])
        gfin = S.sb("E_gfin", [128, D])
        for n in range(3):
            S.dma("pool", wb[:, n, :, :], Wd["w_branch"].t[lyr, n].rearrange("(k p) d -> p k d", p=128),
                  reads=[Wd["w_branch"]], pwrites=[wb], key=wb, max_dma_last_dim=4096)
        for k0 in range(0, 8, 4):
            S.dma("pool", wo[:, k0:k0 + 4, :], Wd["w_o"].t[lyr, k0 * 128:(k0 + 4) * 128, :].rearrange("(k p) d -> p k d", p=128),
                  reads=[Wd["w_o"]], pwrites=[wo], key=wo, max_dma_last_dim=4096)
            S.dma("pool", wpg[:, k0:k0 + 4, :], Wd["w_ple_gate"].t[lyr, k0 * 128:(k0 + 4) * 128, :].rearrange("(k p) d -> p k d", p=128),
                  reads=[Wd["w_ple_gate"]], pwrites=[wpg], key=wpg, max_dma_last_dim=4096)
        S.dma("pool", wpp[:], Wd["w_ple_proj"].t[lyr].rearrange("(k p) d -> p k d", p=128),
              reads=[Wd["w_ple_proj"]], pwrites=[wpp], key=wpp, max_dma_last_dim=4096)
        S.dma("sp", gpl[:], Wd["ple_norm_g"].t[lyr:lyr + 1, :].partition_broadcast(128), reads=[Wd["ple_norm_g"]], writes=[gpl])
        if final:
            S.dma("sp", gfin[:], Wd["final_norm_g"].t[0:1, :].partition_broadcast(128), reads=[Wd["final_norm_g"]], writes=[gfin])

        yst = S.sb("E_ys", [128, 3, 4, TT], BF16)
        mgt = S.sb("E_mg", [128, 24, TT], BF16)
        mrg = S.sb("E_mrg", [128, 8, TT])
        mrb = S.sb("E_mrb", [128, 8, TT], BF16)
        tmp = Ring([S.sb("E_tmp%d" % i, [128, TT]) for i in range(2)])
        xt = S.sb("E_x", [128, nsub, D])
        pin = S.sb("E_p", [128, nsub, PLE])
        pbf = S.sb("E_pbf", [128, PLE], BF16)
        pTs = S.sb("E_pT", [128, 2, 128], BF16)
        sq = S.sb("E_sq", [128, D], BF16)
        ss = S.sb("E_ss", [128, 1])
        rs = S.sb("E_rs", [128, 1])
        hp = S.sb("E_hp", [128, D], BF16)
        hpT = S.sb("E_hpT", [128, 8, 128], BF16)
        gate = S.sb("E_gate", [128, D])
        xo = Ring([S.sb("E_xo%d" % i, [128, D]) for i in range(2)])
        pz = Ring([S.ps("E_pz%d" % i, [128, TT]) for i in range(3)])
        po = Ring([S.ps("E_po%d" % i, [128, 512]) for i in range(2)])
        pg = Ring([S.ps("E_pg%d" % i, [128, 512]) for i in range(2)])
        ptr = S.ps("E_ptr", [128, 8, 128], BF16)

        for ti in range(SEQ // TT):
            t0 = ti * TT
            for n in range(3):
                S.dma("sp", yst[:, n, :, :], scr["ysT"].t[n, :, t0:t0 + TT].rearrange("(k p) t -> p k t", p=128),
                      reads=[scr["ysT"]], writes=[yst] if n == 0 else (), pwrites=() if n == 0 else [yst], key=yst)
            for k0 in range(0, 24, 8):
                S.dma("sp", mgt[:, k0:k0 + 8, :], scr["mgT"].t[k0 * 128:(k0 + 8) * 128, t0:t0 + TT].rearrange("(k p) t -> p k t", p=128),
                      reads=[scr["mgT"]], writes=[mgt] if k0 == 0 else (), pwrites=() if k0 == 0 else [mgt], key=mgt)
            S.dma("sp", xt[:], x_src.t[t0:t0 + TT, :].rearrange("(s p) d -> p s d", p=128), reads=[x_src], writes=[xt])
            S.dma("sp", pin[:], Wd["p"].t[lyr, t0:t0 + TT, :].rearrange("(s p) d -> p s d", p=128), reads=[Wd["p"]], writes=[pin])
            for dc in range(8):
                pzs = []
                for n in range(3):
                    pzz = pz.next()
                    pzs.append(pzz)
                    for k in range(4):
                        S.op("pe", lambda e, n=n, k=k, dc=dc, pzz=pzz: e.matmul(
                            pzz[:], lhsT=wb[:, n, k, dc * 128:(dc + 1) * 128], rhs=yst[:, n, k, :], start=(k == 0), stop=(k == 3)),
                            reads=[wb, yst], writes=[pzz] if k == 0 else (), pwrites=() if k == 0 else [pzz])
                S.op("dve", lambda e, dc=dc, p0=pzs[0]: e.tensor_tensor(out=mrg[:, dc, :], in0=p0[:], in1=mgt[:, dc, :], op=ALU.mult),
                     reads=[pzs[0], mgt], pwrites=[mrg])
                t1 = tmp.next()
                S.op("dve", lambda e, dc=dc, p1=pzs[1], t1=t1: e.tensor_tensor(out=t1[:], in0=p1[:], in1=mgt[:, 8 + dc, :], op=ALU.mult),
                     reads=[pzs[1], mgt], writes=[t1])
                t2 = tmp.next()
                S.op("dve", lambda e, dc=dc, p2=pzs[2], t2=t2: e.tensor_tensor(out=t2[:], in0=p2[:], in1=mgt[:, 16 + dc, :], op=ALU.mult),
                     reads=[pzs[2], mgt], writes=[t2])
                S.op("pool", lambda e, dc=dc, t1=t1: e.tensor_tensor(out=mrg[:, dc, :], in0=mrg[:, dc, :], in1=t1[:], op=ALU.add),
                     reads=[mrg, t1], pwrites=[mrg])
                S.op("pool", lambda e, dc=dc, t2=t2: e.tensor_tensor(out=mrb[:, dc, :], in0=mrg[:, dc, :], in1=t2[:], op=ALU.add),
                     reads=[mrg, t2], pwrites=[mrb])
            for s in range(nsub):
                for blk in range(2):
                    pp = po.next()
                    for k in range(8):
                        S.op("pe", lambda e, k=k, s=s, blk=blk, pp=pp: e.matmul(
                            pp[:], lhsT=mrb[:, k, s * 128:(s + 1) * 128], rhs=wo[:, k, blk * 512:(blk + 1) * 512],
                            start=(k == 0), stop=(k == 7)),
                            reads=[mrb, wo], writes=[pp] if k == 0 else (), pwrites=() if k == 0 else [pp])
                    S.op("dve", lambda e, s=s, blk=blk, pp=pp: e.tensor_tensor(
                        out=xt[:, s, blk * 512:(blk + 1) * 512], in0=pp[:], in1=xt[:, s, blk * 512:(blk + 1) * 512], op=ALU.add),
                        reads=[pp, xt], pwrites=[xt])
                rmsnorm_tile(S, xt[:, s, :], xt, gpl, hp[:], hp, sq, ss, rs)
                for k in range(8):
                    S.op("pe", lambda e, k=k: e.transpose(out=ptr[:, k, :], in_=hp[:, k * 128:(k + 1) * 128], identity=ident[:]),
                         reads=[hp, ident], writes=[ptr] if k == 0 else (), pwrites=() if k == 0 else [ptr])
                S.op("act", lambda e: e.copy(out=hpT[:], in_=ptr[:]), reads=[ptr], writes=[hpT])
                S.op("pool", lambda e, s=s: e.tensor_copy(out=pbf[:], in_=pin[:, s, :]), reads=[pin], writes=[pbf])
                for k in range(2):
                    S.op("pe", lambda e, k=k: e.transpose(out=ptr[:, k, :], in_=pbf[:, k * 128:(k + 1) * 128], identity=ident[:]),
                         reads=[pbf, ident, hpT], writes=[ptr] if k == 0 else (), pwrites=() if k == 0 else [ptr])
                S.op("act", lambda e: e.copy(out=pTs[:], in_=ptr[:, 0:2, :]), reads=[ptr], writes=[pTs])
                xout = xo.next()
                for blk in range(2):
                    pgg = pg.next()
                    for k in range(8):
                        S.op("pe", lambda e, k=k, blk=blk, pgg=pgg: e.matmul(
                            pgg[:], lhsT=hpT[:, k, :], rhs=wpg[:, k, blk * 512:(blk + 1) * 512], start=(k == 0), stop=(k == 7)),
                            reads=[hpT, wpg], writes=[pgg] if k == 0 else (), pwrites=() if k == 0 else [pgg])
                    S.op("act", lambda e, blk=blk, pgg=pgg: e.activation(out=gate[:, blk * 512:(blk + 1) * 512], in_=pgg[:], func=AF.Sigmoid),
                         reads=[pgg], pwrites=[gate])
                    ppp = pg.next()
                    for k in range(2):
                        S.op("pe", lambda e, k=k, blk=blk, ppp=ppp: e.matmul(
                            ppp[:], lhsT=pTs[:, k, :], rhs=wpp[:, k, blk * 512:(blk + 1) * 512], start=(k == 0), stop=(k == 1)),
                            reads=[pTs, wpp], writes=[ppp] if k == 0 else (), pwrites=() if k == 0 else [ppp])
                    S.op("dve", lambda e, blk=blk, ppp=ppp: e.tensor_tensor(
                        out=gate[:, blk * 512:(blk + 1) * 512], in0=ppp[:], in1=gate[:, blk * 512:(blk + 1) * 512], op=ALU.mult),
                        reads=[ppp, gate], pwrites=[gate])
                    S.op("pool", lambda e, blk=blk, s=s, xout=xout: e.tensor_tensor(
                        out=xout[:, blk * 512:(blk + 1) * 512], in0=gate[:, blk * 512:(blk + 1) * 512],
                        in1=xt[:, s, blk * 512:(blk + 1) * 512], op=ALU.add),
                        reads=[gate, xt], writes=[xout] if blk == 0 else (), pwrites=() if blk == 0 else [xout])
                if final:
                    S.op("act", lambda e, xout=xout: e.activation(out=sq[:], in_=xout[:], func=AF.Square, accum_out=ss[:]),
                         reads=[xout], writes=[sq, ss])
                    S.op("act", lambda e: e.activation(out=rs[:], in_=ss[:], func=AF.Sqrt, scale=1.0 / D, bias=EPS),
                         reads=[ss], writes=[rs])
                    S.op("dve", lambda e: e.reciprocal(out=rs[:], in_=rs[:]), reads=[rs], writes=[rs])
                    S.op("dve", lambda e, xout=xout: e.scalar_tensor_tensor(out=xout[:], in0=xout[:], scalar=rs[:, 0:1], in1=gfin[:],
                                                                            op0=ALU.mult, op1=ALU.mult),
                         reads=[xout, rs, gfin], writes=[xout])
                S.dma("sp", x_dst.t[t0 + s * 128:t0 + (s + 1) * 128, :], xout[:], reads=[xout], pwrites=[x_dst], key=xout)
        _barrier(S)
        S.stack_pop()


def phase_D(S, nc, SEQ, lyr, Wd, scr):
    TP = 128
    C = 16
    NCH = TP // C
    GN_EPS = 64e-5
    LD = 0.6065306597126334
    with contextlib.ExitStack() as st:
        S.stack_push(st)
        identB = make_ident(S, "D_identB", BF16)
        ones = S.sb("D_ones", [128, 128])
        S.op("pool", lambda e: e.memset(ones[:], 0.0), writes=[ones])
        S.op("pool", lambda e: e.memset(ones[0:64, 0:64], 1.0), reads=[ones], writes=[ones])
        S.op("pool", lambda e: e.memset(ones[64:128, 64:128], 1.0), reads=[ones], writes=[ones])
        Ff = S.sb("D_F", [128, 64], BF16)
        S.op("pool", lambda e: e.tensor_tensor(out=Ff[:], in0=identB[:, 0:64], in1=identB[:, 64:128], op=ALU.add), reads=[identB], writes=[Ff])
        Sel = S.sb("D_Sel", [128, 16], BF16)
        S.op("pool", lambda e: e.tensor_tensor(out=Sel[:], in0=identB[:, 0:16], in1=identB[:, 16:32], op=ALU.add), reads=[identB], writes=[Sel])
        for hh in range(2, 8):
            S.op("pool", lambda e, hh=hh: e.tensor_tensor(out=Sel[:], in0=Sel[:], in1=identB[:, hh * 16:(hh + 1) * 16], op=ALU.add),
                 reads=[identB, Sel], writes=[Sel])
        maskF = S.sb("D_maskF", [128, 4, 8], BF16)
        S.op("pool", lambda e: e.memset(maskF[:], 0.0), writes=[maskF])
        for p in range(4):
            for h2 in range(2):
                S.op("pool", lambda e, p=p, h2=h2: e.memset(maskF[h2 * 64:(h2 + 1) * 64, p, 2 * p + h2:2 * p + h2 + 1], 1.0), reads=[maskF], writes=[maskF])
        maskZ = S.sb("D_maskZ", [128, 4, 2], BF16)
        S.op("pool", lambda e: e.memset(maskZ[:], 1.0), writes=[maskZ])
        S.op("pool", lambda e: e.affine_select(out=maskZ[:], in_=maskZ[:], pattern=[[-32, 4], [-16, 2]], compare_op=ALU.is_ge, fill=0.0,
                                               base=0, channel_multiplier=1), reads=[maskZ], writes=[maskZ])
        S.op("pool", lambda e: e.affine_select(out=maskZ[:], in_=maskZ[:], pattern=[[32, 4], [16, 2]], compare_op=ALU.is_ge, fill=0.0,
                                               base=15, channel_multiplier=-1), reads=[maskZ], writes=[maskZ])

        def trimask(name, pat, cm, op):
            m = S.sb(name, [128, 128], BF16)
            S.op("pool", lambda e: e.memset(m[:], 1.0), writes=[m])
            S.op("pool", lambda e: e.affine_select(out=m[:], in_=m[:], pattern=pat, compare_op=op, fill=0.0, base=0, channel_multiplier=cm),
                 reads=[m], writes=[m])
            return m
        mSL = trimask("D_mSL", [[-16, 8], [-1, 16]], 1, ALU.is_gt)
        mSU = trimask("D_mSU", [[16, 8], [1, 16]], -1, ALU.is_gt)
        mUI = trimask("D_mUI", [[16, 8], [1, 16]], -1, ALU.is_ge)
        rm = S.sb("D_rm", [128, 512])
        S.op("pool", lambda e: e.memset(rm[:], 1.0), writes=[rm])
        S.op("pool", lambda e: e.memset(rm[:, 0:512:16], 0.0), reads=[rm], writes=[rm])

        def cvec(name, key, n):
            t = S.sb("D_" + name, [128, n])
            S.dma("sp", t[:], Wd[key].t[lyr].rearrange("(c p) -> p c", p=128), reads=[Wd[key]], writes=[t],
                  allow_slow_non_contiguous=True)
            return t

        def cvec2(name, key):
            t = S.sb("D_" + name, [128, 4])
            S.dma("sp", t[:], Wd[key].t[lyr].rearrange("(c a) j -> (a j) c", a=2), reads=[Wd[key]], writes=[t],
                  allow_slow_non_contiguous=True)
            return t
        mu = cvec("mu", "rk_mu", 13)
        w0 = cvec("w0", "rk_w0", 4)
        a0 = cvec("a0", "rk_a0", 4)
        lg = cvec("lg", "rk_lnx_g", 4)
        lb = cvec("lb", "rk_lnx_b", 4)
        kkc = cvec2("kkc", "rk_kk")
        ka = cvec2("ka", "rk_ka")
        rkc = cvec2("rkc", "rk_rk")
        omka = S.sb("D_omka", [128, 4])
        S.op("pool", lambda e: e.tensor_scalar(out=omka[:], in0=ka[:], scalar1=-1.0, scalar2=1.0, op0=ALU.mult, op1=ALU.add),
             reads=[ka], writes=[omka])
        w2 = S.sb("D_w2", [64, 512], BF16)
        a2 = S.sb("D_a2", [128, 512], BF16)
        S.dma("pool", w2[:], Wd["rk_w2"].t[lyr], reads=[Wd["rk_w2"]], writes=[w2])
        S.dma("pool", a2[64:128, :], Wd["rk_a2"].t[lyr], reads=[Wd["rk_a2"]], writes=[a2])

        Hm = S.sb("D_H", [128, 4, 64])
        Hn = S.sb("D_Hn", [128, 4, 64])
        Hbf = S.sb("D_Hbf", [128, 4, 64], BF16)
        S.op("pool", lambda e: e.memset(Hm[:], 0.0), writes=[Hm])
        S.op("pool", lambda e: e.memset(Hbf[:], 0.0), writes=[Hbf])

        rst = S.sb("D_rst", [128, 13, TP + 1])
        xs = S.sb("D_xs", [128, 13, TP])
        th = S.sb("D_th", [128, TP], BF16)
        sg = S.sb("D_sg", [128, 4, TP])
        cum = S.sb("D_cum", [128, 4, TP])
        E1 = S.sb("D_E1", [128, 4, TP])
        E2 = S.sb("D_E2", [128, 4, TP])
        E3 = S.sb("D_E3", [128, 4, TP])
        aa = S.sb("D_aa", [128, 4, TP])
        kkf = S.sb("D_kkf", [128, 4, TP])
        sq = S.sb("D_sq", [128, 4, TP])
        rn = S.sb("D_rn", [128, 4, TP])
        kp = S.sb("D_kp", [128, 4, TP])
        t1 = S.sb("D_t1", [128, 4, TP])
        t2 = S.sb("D_t2", [128, 4, TP])
        comp = [S.sb("D_cmp%d" % i, [128, 4, TP], BF16) for i in range(5)]
        ZXr = Ring([[S.sb("D_Z%d_%d" % (b, i), [128, NCH, 4, 128], BF16) for i in range(5)] for b in range(2)])
        DcR = Ring([S.sb("D_Dc%d" % i, [128, NCH, 4]) for i in range(2)])
        bonR = Ring([S.sb("D_bon%d" % i, [128, 4, TP]) for i in range(2)])
        rzR = Ring([S.sb("D_rz%d" % i, [128, 4, TP], BF16) for i in range(2)])
        ybR = Ring([S.sb("D_yb%d" % i, [128, 4, TP]) for i in range(2)])
        yo = S.sb("D_yo", [128, 4, TP], BF16)
        ppre = S.ps("D_ppre", [128, 4, TP])
        R4 = lambda nm, shp, dt=BF16: Ring([S.sb("D_%s%d" % (nm, i), shp, dt) for i in range(4)])
        WyZr = R4("WyZ", [128, 4, 128]); WhTr = R4("WhT", [128, 4, 128]); BtZr = R4("BtZ", [128, 4, 128]); KtZr = R4("KtZ", [128, 4, 128])
        U0r = R4("U0", [128, 64]); Vtr = R4("Vt", [128, 64]); PTr = R4("PT", [128, 128]); QTr = R4("QT", [128, 128])
        ysbR = Ring([S.sb("D_ysb%d" % i, [128, 64]) for i in range(5)])

        class Reg:
            def __init__(self, bank, ap):
                self.bank = bank
                self.t = ap

        class Lane:
            pass
        lanes = []
        for li in range(2):
            L = Lane()
            L.Gr = Ring([S.sb("D_G%d_%d" % (li, i), [128, 128], BF16) for i in range(2)])
            L.Nr = Ring([S.sb("D_N%d_%d" % (li, i), [128, 128], BF16) for i in range(2)])
            L.NTr = Ring([S.sb("D_NT%d_%d" % (li, i), [128, 128], BF16) for i in range(2)])
            L.MTs = S.sb("D_MTs%d" % li, [128, 128], BF16)
            L.X1Z = S.sb("D_X1Z%d" % li, [128, 4, 128], BF16)
            L.X1s = S.sb("D_X1s%d" % li, [128, 64], BF16)
            L.tks = S.sb("D_tks%d" % li, [128, 2, 64], BF16)
            ba = S.ps("D_ba%d" % li, [128, 512])
            bb = ppre if li == 0 else S.ps("D_bb%d" % li, [128, 4, 128])
            bg = S.ps("D_bg%d" % li, [128, 3, 128])
            L.tokc = Reg(ba, ba.t[:, 0:256].rearrange("q (o j) -> q o j", o=4))
            L.QTp = Reg(ba, ba.t[:, 256:384])
            L.mvp = Reg(ba, ba.t[:, 384:448])
            L.sc = [Reg(bb, bb.t[:, i, :]) for i in range(4)]
            L.bb = bb
            L.bg = bg
            lanes.append(L)
        bs = S.ps("D_bs", [128, 512])
        bt_ = S.ps("D_bt", [128, 512])
        WHp = Reg(bs, bs.t[:, 0:256].rearrange("q (p i) -> q p i", p=4))
        Yp = Reg(bs, bs.t[:, 256:320])
        yfp = Reg(bt_, bt_.t[:, 0:64].rearrange("q (p t) -> q p t", p=4))
        yn = S.sb("D_yn", [128, 64])
        YZ = S.sb("D_YZ", [128, 4, 128], BF16)
        stats = S.sb("D_stats", [128, 6])
        mv = S.sb("D_mv", [128, 2])
        rstd = S.sb("D_rstd", [128, 1])

        bc4 = lambda t: t[:].unsqueeze(2).to_broadcast([128, 4, TP])

        def mm(out_ap, obuf, lhsT, lbuf, rhs, rbuf, start, stop=True, first_write=False):
            obuf = getattr(obuf, "bank", obuf)
            S.op("pe", lambda e: e.matmul(out_ap, lhsT=lhsT, rhs=rhs, start=start, stop=stop, skip_group_check=True),
                 reads=[lbuf, rbuf], writes=[obuf] if first_write else (), pwrites=() if first_write else [obuf])

        def prep(nb):
            t0 = nb * TP
            S.dma("sp", rst[:, :, 1:TP + 1], scr["rsT"].t[:, t0:t0 + TP].rearrange("(c p) t -> p c t", p=128), reads=[scr["rsT"]], writes=[rst])
            if nb == 0:
                S.op("pool", lambda e: e.memset(rst[:, :, 0:1], 0.0), reads=[rst], pwrites=[rst])
            else:
                S.dma("sp", rst[:, :, 0:1], scr["rsT"].t[:, t0 - 1:t0].rearrange("(c p) t -> p c t", p=128), reads=[scr["rsT"]],
                      pwrites=[rst], key=rst, allow_slow_non_contiguous=True)
            S.op("pool", lambda e: e.tensor_tensor(out=xs[:], in0=rst[:, :, 0:TP], in1=rst[:, :, 1:TP + 1], op=ALU.subtract), reads=[rst], writes=[xs])
            S.op("pool", lambda e: e.tensor_tensor(out=xs[:], in0=xs[:], in1=mu[:].unsqueeze(2).to_broadcast([128, 13, TP]), op=ALU.mult),
                 reads=[xs, mu], writes=[xs])
            S.op("pool", lambda e: e.tensor_tensor(out=xs[:], in0=xs[:], in1=rst[:, :, 1:TP + 1], op=ALU.add), reads=[xs, rst], writes=[xs])
            r = xs[:, 0:4, :]; k = xs[:, 4:8, :]; v = xs[:, 8:12, :]
            S.op("act", lambda e: e.activation(out=th[0:64, :], in_=xs[0:64, 12, :], func=AF.Tanh), reads=[xs], pwrites=[th])
            S.op("act", lambda e: e.copy(out=th[64:128, :], in_=xs[64:128, 12, :]), reads=[xs], pwrites=[th])
            for p in range(4):
                mm(ppre[:, p, :], ppre, w2[0:64, p * 128:(p + 1) * 128], w2, th[0:64, :], th, True, first_write=(p == 0))
            for p in range(4):
                S.op("act", lambda e, p=p: e.activation(out=sg[:, p, :], in_=ppre[:, p, :], func=AF.Sigmoid, bias=w0[:, p:p + 1]),
                     reads=[w0], writes=[ppre], pwrites=[sg])
            for p in range(4):
                mm(ppre[:, p, :], ppre, a2[64:128, p * 128:(p + 1) * 128], a2, th[64:128, :], th, True, first_write=(p == 0))
            for p in range(4):
                S.op("act", lambda e, p=p: e.activation(out=aa[:, p, :], in_=ppre[:, p, :], func=AF.Sigmoid, bias=a0[:, p:p + 1]),
                     reads=[a0], writes=[ppre], pwrites=[aa])
            S.op("dve", lambda e: e.tensor_tensor_scan(out=cum[:].rearrange("q p t -> q (p t)"), data0=rm[:],
                                                       data1=sg[:].rearrange("q p t -> q (p t)"), initial=0.0, op0=ALU.mult, op1=ALU.add),
                 reads=[rm, sg], writes=[cum])
            S.op("act", lambda e: e.activation(out=E1[:], in_=cum[:], func=AF.Exp, scale=-LD), reads=[cum], writes=[E1])
            S.op("act", lambda e: e.activation(out=E2[:], in_=cum[:], func=AF.Exp, scale=LD), reads=[cum], writes=[E2])
            S.op("pool", lambda e: e.tensor_tensor(out=t2[:], in0=cum[:], in1=sg[:], op=ALU.subtract), reads=[cum, sg], writes=[t2])
            S.op("act", lambda e: e.activation(out=E3[:], in_=t2[:], func=AF.Exp, scale=-LD), reads=[t2], writes=[E3])
            Dc = DcR.next()
            S.op("pool", lambda e, Dc=Dc: e.tensor_copy(out=Dc[:].rearrange("q c p -> q p c"), in_=E1[:, :, 15:TP:16]), reads=[E1], writes=[Dc])
            S.op("pool", lambda e: e.tensor_tensor(out=kkf[:], in0=k, in1=bc4(kkc), op=ALU.mult), reads=[xs, kkc], writes=[kkf])
            S.op("pool", lambda e: e.tensor_tensor(out=sq[:], in0=kkf[:], in1=kkf[:], op=ALU.mult), reads=[kkf], writes=[sq])
            for p in range(4):
                mm(ppre[:, p, :], ppre, ones[:], ones, sq[:, p, :], sq, True, first_write=(p == 0))
            S.op("act", lambda e: e.activation(out=rn[:], in_=ppre[:], func=AF.Sqrt), writes=[rn, ppre])
            S.op("dve", lambda e: e.tensor_scalar(out=rn[:], in0=rn[:], scalar1=1e-12, scalar2=None, op0=ALU.max), reads=[rn], writes=[rn])
            S.op("dve", lambda e: e.reciprocal(out=rn[:], in_=rn[:]), reads=[rn], writes=[rn])
            S.op("pool", lambda e: e.tensor_tensor(out=kkf[:], in0=kkf[:], in1=rn[:], op=ALU.mult), reads=[kkf, rn], writes=[kkf])
            S.op("pool", lambda e: e.tensor_tensor(out=t1[:], in0=aa[:], in1=bc4(ka), op=ALU.mult), reads=[aa, ka], writes=[t1])
            S.op("pool", lambda e: e.tensor_tensor(out=t1[:], in0=t1[:], in1=bc4(omka), op=ALU.add), reads=[t1, omka], writes=[t1])
            S.op("pool", lambda e: e.tensor_tensor(out=kp[:], in0=k, in1=t1[:], op=ALU.mult), reads=[xs, t1], writes=[kp])
            At, Bt, Kt, Rt, Vb = comp
            S.op("pool", lambda e: e.scalar_tensor_tensor(out=At[:], in0=kkf[:], scalar=-1.0, in1=E3[:], op0=ALU.mult, op1=ALU.mult)
                 if False else e.tensor_tensor(out=t2[:], in0=kkf[:], in1=E3[:], op=ALU.mult), reads=[kkf, E3], writes=[t2])
            S.op("dve", lambda e: e.tensor_scalar(out=At[:], in0=t2[:], scalar1=-1.0, scalar2=None, op0=ALU.mult), reads=[t2], writes=[At])
            S.op("pool", lambda e: e.tensor_tensor(out=t2[:], in0=kkf[:], in1=aa[:], op=ALU.mult), reads=[kkf, aa], writes=[t2])
            S.op("pool", lambda e: e.tensor_tensor(out=Bt[:], in0=t2[:], in1=E2[:], op=ALU.mult), reads=[t2, E2], writes=[Bt])
            S.op("pool", lambda e: e.tensor_tensor(out=Kt[:], in0=kp[:], in1=E2[:], op=ALU.mult), reads=[kp, E2], writes=[Kt])
            S.op("pool", lambda e: e.tensor_tensor(out=Rt[:], in0=r, in1=E1[:], op=ALU.mult), reads=[xs, E1], writes=[Rt])
            S.op("pool", lambda e: e.tensor_copy(out=Vb[:], in_=v), reads=[xs], writes=[Vb])
            S.op("pool", lambda e: e.tensor_tensor(out=t1[:], in0=r, in1=kp[:], op=ALU.mult), reads=[xs, kp], writes=[t1])
            S.op("pool", lambda e: e.tensor_tensor(out=sq[:], in0=t1[:], in1=bc4(rkc), op=ALU.mult), reads=[t1, rkc], writes=[sq])
            for p in range(4):
                mm(ppre[:, p, :], ppre, ones[:], ones, sq[:, p, :], sq, True, first_write=(p == 0))
            bon = bonR.next()
            S.op("act", lambda e, bon=bon: e.copy(out=bon[:], in_=ppre[:]), writes=[bon, ppre])
            S.op("pool", lambda e, bon=bon: e.tensor_tensor(out=bon[:], in0=bon[:], in1=v, op=ALU.mult), reads=[bon, xs], writes=[bon])
            rzt = rzR.next()
            S.dma("sp", rzt[:], scr["rzT"].t[:, t0:t0 + TP].rearrange("(c p) t -> p c t", p=128), reads=[scr["rzT"]], writes=[rzt])
            ZX = ZXr.next()
            for oi in range(5):
                for p in range(4):
                    S.op("dve" if (oi * 4 + p) % 2 == 0 else "pool", lambda e, oi=oi, p=p, ZX=ZX: e.tensor_tensor(
                        out=ZX[oi][:, :, p, :].rearrange("q c (h t) -> q c h t", t=16),
                        in0=comp[oi][:, p, :].rearrange("q (c t) -> q c t", t=16).unsqueeze(2).to_broadcast([128, NCH, 8, 16]),
                        in1=maskF[:, p, :].unsqueeze(1).unsqueeze(3).to_broadcast([128, NCH, 8, 16]), op=ALU.mult),
                        reads=[comp[oi], maskF], writes=[ZX[oi]] if p == 0 else (), pwrites=() if p == 0 else [ZX[oi]])
            return dict(ZX=ZX, Dc=Dc, bon=bon, rzt=rzt, yb=ybR.next(), t0=t0)


        def pre(bt, c, L, pc):
            ZA, ZB, ZK, ZR, ZV = bt["ZX"]
            BtZ = BtZr.next(); KtZ = KtZr.next(); U0 = U0r.next(); Vt = Vtr.next(); PTs = PTr.next(); QTs = QTr.next()
            WyZ = WyZr.next(); WhT = WhTr.next()
            pc.update(BtZ=BtZ, KtZ=KtZ, U0=U0, Vt=Vt, PTs=PTs, QTs=QTs, WyZ=WyZ, WhT=WhT, c=c, bt=bt)
            tokc = L.tokc
            first = True
            for oi, Z in enumerate((ZA, ZB, ZK, ZV)):
                for p in range(4):
                    mm(tokc.t[:, oi, :], tokc, Z[:, c, p, :], Z, Ff[:], Ff, first, first_write=first)
                    first = False
            N1 = L.Nr.next(); NT1 = L.NTr.next()
            specs = ((L.sc[0], ZA, ZB, mSL, N1), (L.sc[1], ZB, ZA, mSU, NT1), (L.sc[2], ZK, ZA, mSU, L.MTs), (L.sc[3], ZB, ZR, mUI, PTs))
            for gi, (pb, Lh, R_, msk, dst) in enumerate(specs):
                for p in range(4):
                    mm(pb.t, pb, Lh[:, c, p, :], Lh, R_[:, c, p, :], R_, p == 0, first_write=(gi == 0 and p == 0))
            for p in range(4):
                mm(L.QTp.t, L.QTp, ZK[:, c, p, :], ZK, ZR[:, c, p, :], ZR, False, first_write=False)
            yield
            G0 = L.Gr.next()
            tks = L.tks
            S.op("act", lambda e: e.copy(out=G0[:, 0:64], in_=tokc.t[:, 0, :]), writes=[G0, tokc.bank])
            mz = maskZ[:].unsqueeze(3).to_broadcast([128, 4, 2, 64])
            S.op("dve", lambda e: e.tensor_copy(out=tks[:], in_=tokc.t[:, 1:3, :]), writes=[tks, tokc.bank])
            S.op("act", lambda e: e.copy(out=Vt[:], in_=tokc.t[:, 3, :]), writes=[Vt, tokc.bank])
            S.op("pool", lambda e: e.tensor_tensor(out=BtZ[:].rearrange("q p (a j) -> q p a j", a=2),
                                                   in0=tks[:, 0, :].unsqueeze(1).unsqueeze(1).to_broadcast([128, 4, 2, 64]), in1=mz, op=ALU.mult),
                 reads=[tks, maskZ], writes=[BtZ])
            S.op("pool", lambda e: e.tensor_tensor(out=KtZ[:].rearrange("q p (a j) -> q p a j", a=2),
                                                   in0=tks[:, 1, :].unsqueeze(1).unsqueeze(1).to_broadcast([128, 4, 2, 64]), in1=mz, op=ALU.mult),
                 reads=[tks, maskZ], writes=[KtZ])
            for gi, (pb, Lh, R_, msk, dst) in enumerate(specs):
                S.op("dve", lambda e, pb=pb, msk=msk, dst=dst: e.tensor_tensor(out=dst[:], in0=pb.t, in1=msk[:], op=ALU.mult),
                     reads=[msk], writes=[dst, pb.bank])
            S.op("dve", lambda e: e.tensor_tensor(out=QTs[:], in0=L.QTp.t, in1=mUI[:], op=ALU.mult), reads=[mUI], writes=[QTs, L.QTp.bank])
            yield
            mm(L.mvp.t, L.mvp, L.MTs[:], L.MTs, Vt[:], Vt, True, first_write=True)
            yield
            S.op("act", lambda e: e.copy(out=G0[:, 64:128], in_=L.mvp.t), writes=[L.mvp.bank], pwrites=[G0])
            yield
            G = G0; Nk = N1; NTk = NT1
            gb = L.bg
            for lev in range(4):
                mm(gb[:, 0, :], gb, identB[:], identB, G[:], G, True, stop=False, first_write=True)
                mm(gb[:, 0, :], gb, NTk[:], NTk, G[:], G, False)
                if lev < 3:
                    mm(gb[:, 1, :], gb, NTk[:], NTk, Nk[:], Nk, True)
                    mm(gb[:, 2, :], gb, Nk[:], Nk, NTk[:], NTk, True)
                    yield
                    G2 = L.Gr.next(); N2 = L.Nr.next(); NT2 = L.NTr.next()
                    S.op("act", lambda e, G2=G2: e.copy(out=G2[:], in_=gb[:, 0, :]), writes=[G2, gb])
                    S.op("act", lambda e, N2=N2: e.copy(out=N2[:], in_=gb[:, 1, :]), writes=[N2, gb])
                    S.op("act", lambda e, NT2=NT2: e.copy(out=NT2[:], in_=gb[:, 2, :]), writes=[NT2, gb])
                    G = G2; Nk = N2; NTk = NT2
                    yield
                else:
                    yield
                    S.op("act", lambda e: e.copy(out=L.X1s[:], in_=gb[:, 0, 0:64]), writes=[L.X1s, gb])
                    S.op("act", lambda e: e.copy(out=U0[:], in_=gb[:, 0, 64:128]), writes=[U0, gb])
                    S.op("dve", lambda e: e.tensor_tensor(out=L.X1Z[:].rearrange("q p (a j) -> q p a j", a=2),
                                                           in0=L.X1s[:].unsqueeze(1).unsqueeze(1).to_broadcast([128, 4, 2, 64]), in1=mz, op=ALU.mult),
                         reads=[L.X1s, maskZ], writes=[L.X1Z])
                    yield
            bb = L.bb
            for p in range(4):
                mm(bb[:, p, :], bb, identB[:], identB, ZR[:, c, p, :], ZR, p == 0, stop=False, first_write=(p == 0))
                mm(bb[:, p, :], bb, L.X1Z[:, p, :], L.X1Z, PTs[:], PTs, False)
            ba = L.tokc.bank
            for p in range(4):
                mm(ba[:, p * 128:(p + 1) * 128], ba, L.X1Z[:, p, :], L.X1Z, BtZ[:, p, :], BtZ, p == 0, first_write=(p == 0))
            yield
            S.op("act", lambda e: e.copy(out=WyZ[:], in_=bb[:]), writes=[WyZ, bb])
            S.op("dve", lambda e: e.tensor_copy(out=WhT[:].rearrange("q p m -> q (p m)"), in_=ba[:]), writes=[WhT, ba])
            yield

        def state_stream(pc):
            c = pc["c"]; bt = pc["bt"]
            BtZ, KtZ, U0, Vt, PTs, QTs, WyZ, WhT = (pc[k] for k in ("BtZ", "KtZ", "U0", "Vt", "PTs", "QTs", "WyZ", "WhT"))
            for p in range(4):
                mm(WHp.t[:, p, :], WHp, BtZ[:, p, :], BtZ, U0[:], U0, p == 0, stop=False, first_write=(p == 0))
            for p in range(4):
                mm(WHp.t[:, p, :], WHp, KtZ[:, p, :], KtZ, Vt[:], Vt, False, stop=False)
            mm(Yp.t, Yp, PTs[:], PTs, U0[:], U0, False, stop=False)
            mm(Yp.t, Yp, QTs[:], QTs, Vt[:], Vt, False, stop=False)
            yield
            for p in range(4):
                mm(Yp.t, Yp, WyZ[:, p, :], WyZ, Hbf[:, p, :], Hbf, False, stop=(p == 3))
            for p in range(4):
                mm(WHp.t[:, p, :], WHp, WhT[:, p, :], WhT, Hbf[:, p, :], Hbf, False, stop=True)
            yield
            Dc = bt["Dc"]
            S.op("dve", lambda e: e.tensor_tensor(out=Hn[:], in0=WHp.t, in1=Hm[:], op=ALU.add), reads=[Hm], writes=[Hn, WHp.bank])
            S.op("dve", lambda e: e.tensor_tensor(out=Hm[:], in0=Hn[:], in1=Dc[:, c, :].unsqueeze(2).to_broadcast([128, 4, 64]), op=ALU.mult),
                 reads=[Hn, Dc], writes=[Hm])
            ysb = ysbR.next()
            pc["ysb"] = ysb
            S.op("act", lambda e: e.copy(out=ysb[:], in_=Yp.t), writes=[ysb, Yp.bank])
            S.op("act", lambda e: e.copy(out=Hbf[:], in_=Hm[:]), reads=[Hm], writes=[Hbf])
            yield

        def out_stream(pc):
            c = pc["c"]; bt = pc["bt"]; ysb = pc["ysb"]
            S.op("dve", lambda e: e.bn_stats(out=stats[:], in_=ysb[:]), reads=[ysb], writes=[stats])
            S.op("dve", lambda e: e.bn_aggr(out=mv[:], in_=stats[:]), reads=[stats], writes=[mv])
            yield
            S.op("act", lambda e: e.activation(out=rstd[:], in_=mv[:, 1:2], func=AF.Sqrt, bias=GN_EPS, scale=1.0), reads=[mv], writes=[rstd])
            yield
            S.op("dve", lambda e: e.reciprocal(out=rstd[:], in_=rstd[:]), reads=[rstd], writes=[rstd])
            S.op("dve", lambda e: e.tensor_scalar(out=yn[:], in0=ysb[:], scalar1=mv[:, 0:1], scalar2=rstd[:, 0:1], op0=ALU.subtract, op1=ALU.mult),
                 reads=[ysb, mv, rstd], writes=[yn])
            yield
            S.op("dve", lambda e: e.tensor_tensor(out=YZ[:].rearrange("q p (a j) -> q p a j", a=2),
                                                   in0=yn[:].unsqueeze(1).unsqueeze(1).to_broadcast([128, 4, 2, 64]),
                                                   in1=maskZ[:].unsqueeze(3).to_broadcast([128, 4, 2, 64]), op=ALU.mult),
                 reads=[yn, maskZ], writes=[YZ])
            yield
            for p in range(4):
                mm(yfp.t[:, p, :], yfp, YZ[:, p, :], YZ, Sel[:], Sel, True, first_write=(p == 0))
            yield
            yb = bt["yb"]
            S.op("act", lambda e: e.copy(out=yb[:, :, c * 16:(c + 1) * 16], in_=yfp.t), writes=([yb] if c == 0 else []) + [yfp.bank],
                 pwrites=() if c == 0 else [yb])
            if c == NCH - 1:
                post(bt)
            yield

        def post(bt):
            yb = bt["yb"]; bon = bt["bon"]; rzt = bt["rzt"]; t0 = bt["t0"]
            S.op("pool", lambda e: e.tensor_tensor(out=yb[:], in0=yb[:], in1=bc4(lg), op=ALU.mult), reads=[yb, lg], writes=[yb])
            S.op("pool", lambda e: e.tensor_tensor(out=yb[:], in0=yb[:], in1=bc4(lb), op=ALU.add), reads=[yb, lb], writes=[yb])
            S.op("pool", lambda e: e.tensor_tensor(out=yb[:], in0=yb[:], in1=bon[:], op=ALU.add), reads=[yb, bon], writes=[yb])
            S.op("pool", lambda e: e.tensor_tensor(out=yo[:], in0=yb[:], in1=rzt[:], op=ALU.mult), reads=[yb, rzt], writes=[yo])
            S.dma("sp", scr["ysT"].t[2, :, t0:t0 + TP].rearrange("(c p) t -> p c t", p=128), yo[:], reads=[yo], pwrites=[scr["ysT"]], key=yo)

        chunks = []
        for nb in range(SEQ // TP):
            for c in range(NCH):
                chunks.append((nb, c))
        bts = {}
        nxt = 0
        lane_gen = [None, None]
        lane_pc = [None, None]
        done_order = {}
        next_state = 0
        state_gen = None; state_pc = None
        out_q = []; out_gen = None
        n_total = len(chunks)
        finished_out = 0
        pcs = {}
        while finished_out < n_total:
            for li in range(2):
                if lane_gen[li] is None and nxt < n_total and nxt - next_state < 3:
                    nb, c = chunks[nxt]
                    if nb not in bts:
                        bts[nb] = prep(nb)
                    pc = {"idx": nxt}
                    pcs[nxt] = pc
                    lane_gen[li] = pre(bts[nb], c, lanes[li], pc)
                    lane_pc[li] = pc
                    nxt += 1
                if lane_gen[li] is not None:
                    try:
                        next(lane_gen[li])
                    except StopIteration:
                        done_order[lane_pc[li]["idx"]] = True
                        lane_gen[li] = None
            for _rep in range(DEBUG.get("state_rep", 2)):
                if state_gen is None and done_order.get(next_state) and next_state - finished_out < 3:
                    state_pc = pcs[next_state]
                    state_gen = state_stream(state_pc)
                if state_gen is not None:
                    try:
                        next(state_gen)
                    except StopIteration:
                        out_q.append(state_pc)
                        state_gen = None
                        next_state += 1
            for _rep in range(DEBUG.get("out_rep", 1)):
                if out_gen is None and out_q:
                    out_gen = out_stream(out_q.pop(0))
                if out_gen is not None:
                    try:
                        next(out_gen)
                    except StopIteration:
                        out_gen = None
                        finished_out += 1
        _barrier(S)
        S.stack_pop()


WSPEC = {
    "norm_g": [2, 1024], "w_in": [2, 1024, 9112], "cmp_w1": [2, 2, 32, 64, 128], "cmp_w2": [2, 2, 128, 64],
    "cmp_pe": [2, 2, 32, 64], "sg_ln_g": [2, 512], "sg_ln_b": [2, 512], "sg_w": [2, 8, 128, 128], "sg_b": [2, 8, 128],
    "rk_mu": [2, 1664], "rk_w0": [2, 512], "rk_w2": [2, 64, 512], "rk_a0": [2, 512], "rk_a2": [2, 64, 512],
    "rk_kk": [2, 8, 64], "rk_ka": [2, 8, 64], "rk_rk": [2, 8, 64], "rk_lnx_g": [2, 512], "rk_lnx_b": [2, 512],
    "w_branch": [2, 3, 512, 1024], "w_o": [2, 1024, 1024], "ple_norm_g": [2, 1024], "w_ple_gate": [2, 1024, 1024],
    "w_ple_proj": [2, 256, 1024], "final_norm_g": [1, 1024],
}


def build(SEQ, nlayers=2, enable=(1, 1, 1), scr_kind="Internal"):
    nc = bass.Bass("TRN2", target_bir_lowering=False)
    with contextlib.ExitStack() as stack:
        S = Sched(nc, stack)
        x = Buf("x", nc.dram_tensor("x", [SEQ, D], F32, kind="ExternalInput").ap())
        Wd = {"p": Buf("p", nc.dram_tensor("p", [2, SEQ, PLE], F32, kind="ExternalInput").ap())}
        for k, shp in WSPEC.items():
            Wd[k] = Buf(k, nc.dram_tensor(k, shp, F32, kind="ExternalInput").ap())
        out = Buf("out", nc.dram_tensor("out", [SEQ, D], F32, kind="ExternalOutput").ap())
        scr = make_scratch(S, SEQ, kind=scr_kind)
        xmid = S.dram("xmid", [SEQ, D], F32, kind=scr_kind)
        cur = x
        for lyr in range(nlayers):
            last = lyr == nlayers - 1
            dst = out if last else xmid
            phase_A(S, nc, SEQ, lyr, cur, Wd, scr)
            if enable[0]:
                phase_B(S, nc, SEQ, lyr, Wd, scr)
            if enable[1]:
                phase_C(S, nc, SEQ, lyr, Wd, scr)
            if enable[2]:
                phase_D(S, nc, SEQ, lyr, Wd, scr)
            phase_E(S, nc, SEQ, lyr, cur, dst, Wd, scr, final=(last and nlayers == 2))
            cur = dst
        S.emit()
    return nc


def phase_B(S, nc, SEQ, lyr, Wd, scr):
    NC = (SEQ - 32) // 16 + 1
    NT = (NC + 127) // 128
    NCp = NT * 128
    KT = SEQ // 128
    with contextlib.ExitStack() as st:
        S.stack_push(st)
        ident = make_ident(S, "B_ident")
        ksT = S.sb("B_ksT", [128, 2, SEQ], BF16)
        HALF = min(4096, SEQ)
        NA = SEQ // HALF
        kwT = S.sb("B_kwT", [64, 2, SEQ], BF16)
        vs = S.sb("B_vs", [128, KT, 2, 65], BF16)
        vw = S.sb("B_vw", [128, KT, 2, 65], BF16)
        kcmpT = S.sb("B_kcmpT", [64, 2, NCp], BF16)
        Rc = S.sb("B_Rc", [128, NT, 2, 193], BF16)
        S.op("pool", lambda e: e.memset(ksT[64:128, :, :], 1.0), writes=[ksT])
        for g_ in range(2):
            for a_ in range(NA):
                S.op("pool", lambda e, g_=g_, a_=a_: e.affine_select(
                    out=ksT[64:128, g_, a_ * HALF:(a_ + 1) * HALF], in_=ksT[64:128, g_, a_ * HALF:(a_ + 1) * HALF], pattern=[[1, HALF]],
                    compare_op=ALU.is_ge, fill=0.0, base=0, channel_multiplier=-64), reads=[ksT], pwrites=[ksT])
                S.op("pool", lambda e, g_=g_, a_=a_: e.affine_select(
                    out=ksT[64:128, g_, a_ * HALF:(a_ + 1) * HALF], in_=ksT[64:128, g_, a_ * HALF:(a_ + 1) * HALF], pattern=[[-1, HALF]],
                    compare_op=ALU.is_ge, fill=0.0, base=63, channel_multiplier=64), reads=[ksT], pwrites=[ksT])
        S.dma("sp", ksT[0:64, :, :], scr["ksT"].t.rearrange("(g d) t -> d g t", g=2), reads=[scr["ksT"]], pwrites=[ksT], key=ksT)
        S.dma("sp", kwT[:], scr["kwT"].t.rearrange("(g d) t -> d g t", g=2), reads=[scr["kwT"]], writes=[kwT])
        S.op("pool", lambda e: e.memset(vs[:], 1.0), writes=[vs])
        S.op("pool", lambda e: e.memset(vw[:], 1.0), writes=[vw])
        for k0 in range(0, KT, 8):
            k1 = min(KT, k0 + 8)
            for (dst, c0) in ((vs, 0), (vw, 128)):
                for g in range(2):
                    S.dma("sp", dst[:, k0:k1, g, 0:64],
                          scr["vsw"].t[k0 * 128:k1 * 128, c0 + g * 64:c0 + (g + 1) * 64].rearrange("(k p) d -> p k d", p=128),
                          reads=[scr["vsw"]], pwrites=[dst], key=dst)
        S.op("pool", lambda e: e.memset(Rc[:], 1.0), writes=[Rc])
        for nt in range(NT):
            for g in range(2):
                S.op("pool", lambda e, nt=nt, g=g: e.affine_select(
                    out=Rc[:, nt, g, 65:193], in_=Rc[:, nt, g, 65:193], pattern=[[-4, 128]], compare_op=ALU.is_ge, fill=0.0,
                    base=nt * 128 + 1, channel_multiplier=1), reads=[Rc], writes=[Rc])
                S.op("pool", lambda e, nt=nt, g=g: e.affine_select(
                    out=Rc[:, nt, g, 65:193], in_=Rc[:, nt, g, 65:193], pattern=[[4, 128]], compare_op=ALU.is_ge, fill=0.0,
                    base=3 - nt * 128, channel_multiplier=-1), reads=[Rc], writes=[Rc])
        npad = NCp - NC
        if npad:
            S.op("pool", lambda e: e.affine_select(
                out=Rc[:, NT - 1, :, :], in_=Rc[:, NT - 1, :, :], pattern=[[0, 2 * 193]], compare_op=ALU.is_ge, fill=0.0,
                base=(NC - 1) - (NT - 1) * 128, channel_multiplier=-1), reads=[Rc], writes=[Rc])
        S.op("pool", lambda e: e.memset(kcmpT[:], 0.0), writes=[kcmpT])

        with contextlib.ExitStack() as st2:
            S.stack_push(st2)
            kvT = S.sb("B_kvT", [64, 2, SEQ], BF16)
            w1 = S.sb("B_w1", [64, 32, 128], BF16)
            w2 = S.sb("B_w2", [128, 64], BF16)
            peT = S.sb("B_peT", [64, 32])
            peTb = S.sb("B_peTb", [64, 32], BF16)
            cb = S.sb("B_cb", [128, 1])
            hid = S.sb("B_hid", [128, NCp], BF16)
            ph = S.ps("B_ph", [128, 512])
            pc1 = S.ps("B_pc1", [128, 512])
            pk = S.ps("B_pk", [128, 512])
            for kv in range(2):
                src = scr["kcT"] if kv == 0 else scr["vcT"]
                S.dma("sp", kvT[:], src.t.rearrange("(g d) t -> d g t", g=2), reads=[src], writes=[kvT])
                S.dma("pool", w1[:], Wd["cmp_w1"].t[lyr, kv].rearrange("l d h -> d l h"), reads=[Wd["cmp_w1"]], writes=[w1])
                S.dma("pool", w2[:], Wd["cmp_w2"].t[lyr, kv], reads=[Wd["cmp_w2"]], writes=[w2])
                S.dma("sp", peT[:], Wd["cmp_pe"].t[lyr, kv].rearrange("l d -> d l"), reads=[Wd["cmp_pe"]], writes=[peT],
                      allow_slow_non_contiguous=True)
                S.op("dve", lambda e: e.tensor_copy(out=peTb[:], in_=peT[:]), reads=[peT], writes=[peTb])
                for l in range(32):
                    S.op("pe", lambda e, l=l: e.matmul(pc1[:, 0:1], lhsT=w1[:, l, :], rhs=peTb[:, l:l + 1], start=(l == 0), stop=(l == 31)),
                         reads=[w1, peTb], writes=[pc1] if l == 0 else (), pwrites=() if l == 0 else [pc1])
                S.op("dve", lambda e: e.tensor_copy(out=cb[:], in_=pc1[:, 0:1]), reads=[pc1], writes=[cb])
                for g in range(2):
                    S.op("dve", lambda e: e.memset(hid[:], 0.0), writes=[hid])
                    for n0 in range(0, NC, 512):
                        nn = min(512, NC - n0)
                        for l in range(32):
                            S.op("pe", lambda e, l=l, g=g, n0=n0, nn=nn: e.matmul(
                                ph[:, 0:nn], lhsT=w1[:, l, :], rhs=kvT[:, g, n0 * 16 + l: n0 * 16 + l + (nn - 1) * 16 + 1: 16], start=(l == 0), stop=(l == 31)),
                                reads=[w1, kvT], writes=[ph] if l == 0 else (), pwrites=() if l == 0 else [ph])
                        S.op("act", lambda e, n0=n0, nn=nn: e.activation(out=hid[:, n0:n0 + nn], in_=ph[:, 0:nn], func=AF.Silu, bias=cb[:, 0:1]),
                             reads=[ph, cb], pwrites=[hid])
                    if kv == 0:
                        for n0 in range(0, NC, 512):
                            nn = min(512, NC - n0)
                            S.op("pe", lambda e, n0=n0, nn=nn: e.matmul(pk[0:64, 0:nn], lhsT=w2[:], rhs=hid[:, n0:n0 + nn], start=True, stop=True),
                                 reads=[w2, hid], writes=[pk])
                            S.op("dve", lambda e, g=g, n0=n0, nn=nn: e.tensor_copy(out=kcmpT[:, g, n0:n0 + nn], in_=pk[0:64, 0:nn]),
                                 reads=[pk], pwrites=[kcmpT])
                    else:
                        for nt in range(NT):
                            rows = min(128, NC - nt * 128)
                            S.op("pe", lambda e, nt=nt: e.matmul(pk[:, 0:64], lhsT=hid[:, nt * 128:(nt + 1) * 128], rhs=w2[:], start=True, stop=True),
                                 reads=[w2, hid], writes=[pk])
                            S.op("dve", lambda e, g=g, nt=nt: e.tensor_copy(out=Rc[:, nt, g, 0:64], in_=pk[:, 0:64]),
                                 reads=[pk], pwrites=[Rc])
            _barrier(S)
            S.stack_pop()

        qt = Ring([S.sb("B_q%d" % i, [64, 8, 128], BF16) for i in range(2)])
        gt = Ring([S.sb("B_g%d" % i, [128, 24]) for i in range(2)])
        nzt = Ring([S.sb("B_nz%d" % i, [128, 512], BF16) for i in range(2)])
        Et = Ring([S.sb("B_E%d" % i, [128, 512], BF16) for i in range(4)])
        psT = Ring([S.ps("B_psT%d" % i, [128, 512]) for i in range(3)])
        pcA = S.ps("B_pcA", [128, 2, 193])
        pcB = S.ps("B_pcB", [128, 2, 193])
        pos = S.ps("B_pos", [128, 4, 65])
        pow_ = S.ps("B_pow", [128, 4, 65])
        pmisc = S.ps("B_pmisc", [128, 4, 128], BF16)
        oc = S.sb("B_oc", [128, 4, 193])
        rcs = S.sb("B_rcs", [128, 4])
        rss = S.sb("B_rss", [128, 4])
        rws = S.sb("B_rws", [128, 4])
        cc = S.sb("B_cc", [128, 3, 4])
        sc = S.sb("B_sc", [128, 128])
        sc2 = S.sb("B_sc2", [128, 128])
        m1 = S.sb("B_m1", [128, 8])
        m2 = S.sb("B_m2", [128, 8])
        negq = S.sb("B_negq", [128, 2, 128], BF16)
        S.op("pool", lambda e: e.memset(negq[:], 0.0), writes=[negq])
        qAr = {(g_, a_): Ring([S.sb("B_qA%d%d_%d" % (g_, a_, i), [128, 4, 128], BF16) for i in range(2)]) for g_ in range(2) for a_ in range(NA)}
        yg = S.sb("B_yg", [128, 4, 64])
        ytmp = S.sb("B_ytmp", [128, 4, 64])
        ynsa = S.sb("B_ynsa", [128, 512], BF16)
        stg = Ring([S.sb("B_stg%d" % i, [128, 4, 128], BF16) for i in range(2)])

        def qk_exp(kT_ap, kbuf, q_ap, qbuf, neg_lhsT=None):
            p = psT.next()
            if False:
                pass
            else:
                S.op("pe", lambda e, p=p: e.matmul(p[:], lhsT=kT_ap, rhs=q_ap, start=True, stop=True), reads=[kbuf, qbuf], writes=[p])
            E = Et.next()
            S.op("act", lambda e, p=p, E=E: e.activation(out=E[:], in_=p[:], func=AF.Exp), reads=[p], writes=[E])
            return E

        def pipeline(tiles, L=2):
            Es = {}
            n = len(tiles)
            for i in range(n + L):
                if i < n:
                    Es[i] = tiles[i][0]()
                if i - L >= 0:
                    tiles[i - L][1](Es.pop(i - L))

        def mask(E, base, cm, qstep):
            S.op("pool", lambda e, E=E: e.affine_select(out=E[:], in_=E[:], pattern=[[0, 4], [qstep, 128]], compare_op=ALU.is_ge,
                                                       fill=0.0, base=base, channel_multiplier=cm), reads=[E], writes=[E])

        for qb in range(SEQ // 128):
            q0 = qb * 128
            q = qt.next(); gg = gt.next(); nz = nzt.next()
            S.dma("sp", q[:], scr["qT"].t[:, q0:q0 + 128].rearrange("(h d) t -> d h t", h=8), reads=[scr["qT"]], writes=[q])
            qAs = {}
            for g_ in range(2):
                for a_ in range(min(NA, qb * 128 // HALF + 1)):
                    qa = qAr[(g_, a_)].next()
                    qAs[(g_, a_)] = qa
                    S.dma("sp", qa[0:64, :, :], scr["qT"].t[g_ * 256:(g_ + 1) * 256, q0:q0 + 128].rearrange("(h d) t -> d h t", h=4),
                          reads=[scr["qT"]], writes=[qa])
            S.dma("sp", gg[:], scr["gate"].t[q0:q0 + 128, :], reads=[scr["gate"]], writes=[gg])
            S.dma("sp", nz[:], scr["nzs"].t[q0:q0 + 128, :], reads=[scr["nzs"]], writes=[nz])
            for g in range(2):
                q_ap = q[:, 4 * g:4 * g + 4, :].rearrange("d h q -> d (h q)")
                n_max = min(8 * qb + 6, NC - 1)
                ntl = n_max // 128 + 1
                def c_qk(nt, g=g, q_ap=q_ap, q=q):
                    E = qk_exp(kcmpT[:, g, nt * 128:(nt + 1) * 128], kcmpT, q_ap, q)
                    if q0 - 16 * (128 * nt + 127) - 31 < 0:
                        mask(E, q0 - 16 * 128 * nt - 31, -16, 1)
                    return E

                def c_pv(nt, E, g=g, ntl=ntl):
                    for h in range(4):
                        pcx = pcA if h < 2 else pcB
                        first = (nt == 0 and h % 2 == 0)
                        S.op("pe", lambda e, E=E, h=h, pcx=pcx, nt=nt, first=first, g=g, ntl=ntl: e.matmul(
                            pcx[:, h % 2, :], lhsT=E[:, h * 128:(h + 1) * 128], rhs=Rc[:, nt, g, :], start=first,
                            stop=(nt == ntl - 1 and h % 2 == 1), skip_group_check=True),
                            reads=[E, Rc], writes=[pcx] if first else (), pwrites=() if first else [pcx])
                pipeline([(lambda nt=nt: c_qk(nt), lambda E, nt=nt: c_pv(nt, E)) for nt in range(ntl)])
                S.op("act", lambda e: e.copy(out=oc[:, 0:2, :], in_=pcA[:]), reads=[pcA], pwrites=[oc])
                S.op("act", lambda e: e.copy(out=oc[:, 2:4, :], in_=pcB[:]), reads=[pcB], pwrites=[oc])
                S.op("dve", lambda e: e.tensor_scalar(out=rcs[:], in0=oc[:, :, 64], scalar1=1e-30, scalar2=None, op0=ALU.max),
                     reads=[oc], writes=[rcs])
                S.op("dve", lambda e: e.reciprocal(out=rcs[:], in_=rcs[:]), reads=[rcs], writes=[rcs])
                S.op("dve", lambda e: e.tensor_scalar(out=sc[:], in0=oc[:, 0, 65:193], scalar1=rcs[:, 0:1], scalar2=None, op0=ALU.mult),
                     reads=[oc, rcs], writes=[sc])
                for h in range(1, 4):
                    S.op("dve", lambda e, h=h: e.scalar_tensor_tensor(out=sc[:], in0=oc[:, h, 65:193], scalar=rcs[:, h:h + 1], in1=sc[:],
                                                                      op0=ALU.mult, op1=ALU.add), reads=[oc, rcs, sc], writes=[sc])
                for half in range(2):
                    tb = 2 * qb + half
                    ps_ = slice(half * 64, (half + 1) * 64)
                    if tb + 1 < 128:
                        S.op("dve", lambda e, ps_=ps_, tb=tb: e.memset(sc[ps_, tb + 1:128], -1e4), reads=[sc], writes=[sc])
                    lo = max(tb - 1, 0)
                    S.op("dve", lambda e, ps_=ps_, tb=tb, lo=lo: e.memset(sc[ps_, lo:tb + 1], 1e4), reads=[sc], writes=[sc])
                S.op("dve", lambda e: e.memset(sc[:, 0:1], 1e4), reads=[sc], writes=[sc])
                S.op("dve", lambda e: e.max(out=m1[:], in_=sc[:]), reads=[sc], writes=[m1])
                S.op("dve", lambda e: e.match_replace(out=sc2[:], in_to_replace=m1[:], in_values=sc[:], imm_value=-3e4),
                     reads=[sc, m1], writes=[sc2])
                S.op("dve", lambda e: e.max(out=m2[:], in_=sc2[:]), reads=[sc2], writes=[m2])
                S.op("dve", lambda e: e.tensor_scalar(out=negq[:, 0, :], in0=sc[:], scalar1=m2[:, 7:8], scalar2=-1e4, op0=ALU.is_lt, op1=ALU.mult),
                     reads=[sc, m2], pwrites=[negq])
                S.op("dve", lambda e: e.tensor_scalar(out=negq[:, 1, 64:128], in0=sc[:, 0:64], scalar1=m2[:, 7:8], scalar2=-1e4, op0=ALU.is_lt, op1=ALU.mult),
                     reads=[sc, m2], pwrites=[negq])
                na_here = min(NA, qb * 128 // HALF + 1)
                S.op("pe", lambda e: e.transpose(out=pmisc[:, 1, :], in_=negq[:, 1, :], identity=ident[:]), reads=[negq, ident], writes=[pmisc])
                if na_here > 1:
                    S.op("pe", lambda e: e.transpose(out=pmisc[:, 0, :], in_=negq[:, 0, :], identity=ident[:]), reads=[negq, ident], pwrites=[pmisc])
                for a_ in range(na_here):
                    qa = qAs[(g, a_)]
                    S.op("dve", lambda e, qa=qa, a_=a_: e.tensor_copy(out=qa[64:128, :, :],
                                                                   in_=pmisc[64:128, (1 - a_):(2 - a_), :].to_broadcast([64, 4, 128])),
                         reads=[pmisc], pwrites=[qa])
                kts = list(range(max(0, qb - 4), qb + 1))

                def w_qk(i, kt, g=g, q_ap=q_ap, q=q, qb=qb):
                    E = qk_exp(kwT[:, g, kt * 128:(kt + 1) * 128], kwT, q_ap, q)
                    if kt == qb - 4:
                        mask(E, -1, 1, -1)
                    if kt == qb:
                        mask(E, 0, -1, 1)
                    return E

                def w_pv(i, kt, E, g=g, kts=kts):
                    for h in range(4):
                        first = (i == 0 and h == 0)
                        S.op("pe", lambda e, E=E, h=h, kt=kt, first=first, last=(i == len(kts) - 1 and h == 3), g=g: e.matmul(
                            pow_[:, h, :], lhsT=E[:, h * 128:(h + 1) * 128], rhs=vw[:, kt, g, :], start=first, stop=last,
                            skip_group_check=True),
                            reads=[E, vw], writes=[pow_] if first else (), pwrites=() if first else [pow_])
                def s_qk(kt, g=g, q_ap=q_ap, q=q, qb=qb, qAs=qAs):
                    qa = qAs[(g, kt * 128 // HALF)]
                    E = qk_exp(ksT[:, g, kt * 128:(kt + 1) * 128], ksT, qa[:].rearrange("p h q -> p (h q)"), qa)
                    if kt == qb:
                        mask(E, 0, -1, 1)
                    return E

                def s_pv(kt, E, g=g, qb=qb):
                    for h in range(4):
                        first = (kt == 0 and h == 0)
                        S.op("pe", lambda e, E=E, h=h, kt=kt, first=first, last=(kt == qb and h == 3), g=g: e.matmul(
                            pos[:, h, :], lhsT=E[:, h * 128:(h + 1) * 128], rhs=vs[:, kt, g, :], start=first, stop=last,
                            skip_group_check=True),
                            reads=[E, vs], writes=[pos] if first else (), pwrites=() if first else [pos])
                pipeline([(lambda i=i, kt=kt: w_qk(i, kt), lambda E, i=i, kt=kt: w_pv(i, kt, E)) for i, kt in enumerate(kts)] +
                         [(lambda kt=kt: s_qk(kt), lambda E, kt=kt: s_pv(kt, E)) for kt in range(qb + 1)])
                S.op("dve", lambda e: e.reciprocal(out=rss[:], in_=pos[:, :, 64]), reads=[pos], writes=[rss])
                S.op("dve", lambda e: e.reciprocal(out=rws[:], in_=pow_[:, :, 64]), reads=[pow_], writes=[rws])
                gv = gg[:, g * 12:(g + 1) * 12].rearrange("p (h b) -> p b h", b=3)
                for b, rr in enumerate((rcs, rss, rws)):
                    S.op("dve", lambda e, b=b, rr=rr, gv=gv: e.tensor_tensor(out=cc[:, b, :], in0=gv[:, b, :], in1=rr[:], op=ALU.mult),
                         reads=[gg, rr], pwrites=[cc])
                if qb == 2:
                    dbg_dump(S, "oc%d" % g, oc[:], oc, [128, 4, 193])
                    dbg_dump(S, "pos%d" % g, pos[:], pos, [128, 4, 65])
                    dbg_dump(S, "pow%d" % g, pow_[:], pow_, [128, 4, 65])
                    dbg_dump(S, "cc%d" % g, cc[:], cc, [128, 3, 4])
                    dbg_dump(S, "gg%d" % g, gg[:], gg, [128, 24])
                    dbg_dump(S, "sc%d" % g, sc[:], sc, [128, 128])
                    dbg_dump(S, "negq%d" % g, negq[:], negq, [128, 128])
                bc = lambda b: cc[:, b, :].unsqueeze(2).to_broadcast([128, 4, 64])
                S.op("dve", lambda e: e.tensor_tensor(out=yg[:], in0=oc[:, :, 0:64], in1=bc(0), op=ALU.mult), reads=[oc, cc], writes=[yg])
                S.op("dve", lambda e: e.tensor_tensor(out=ytmp[:], in0=pos[:, :, 0:64], in1=bc(1), op=ALU.mult), reads=[pos, cc], writes=[ytmp])
                S.op("pool", lambda e: e.tensor_tensor(out=yg[:], in0=yg[:], in1=ytmp[:], op=ALU.add), reads=[yg, ytmp], writes=[yg])
                S.op("dve", lambda e: e.tensor_tensor(out=ytmp[:], in0=pow_[:, :, 0:64], in1=bc(2), op=ALU.mult), reads=[pow_, cc], writes=[ytmp])
                S.op("pool", lambda e: e.tensor_tensor(out=yg[:], in0=yg[:], in1=ytmp[:], op=ALU.add), reads=[yg, ytmp], writes=[yg])
                if qb == 2:
                    dbg_dump(S, "yg%d" % g, yg[:], yg, [128, 4, 64])
                S.op("pool", lambda e, g=g, nz=nz: e.tensor_tensor(out=ynsa[:, g * 256:(g + 1) * 256], in0=yg[:].rearrange("p h d -> p (h d)"),
                                                                   in1=nz[:, g * 256:(g + 1) * 256], op=ALU.mult),
                     reads=[yg, nz], pwrites=[ynsa])
            for k in range(4):
                S.op("pe", lambda e, k=k: e.transpose(out=pmisc[:, k, :], in_=ynsa[:, k * 128:(k + 1) * 128], identity=ident[:]),
                     reads=[ynsa, ident], writes=[pmisc] if k == 0 else (), pwrites=() if k == 0 else [pmisc])
            sg = stg.next()
            S.op("act", lambda e, sg=sg: e.copy(out=sg[:], in_=pmisc[:]), reads=[pmisc], writes=[sg])
            S.dma("sp", scr["ysT"].t[0, :, q0:q0 + 128].rearrange("(k p) t -> p k t", p=128), sg[:], reads=[sg], pwrites=[scr["ysT"]], key=sg)
        _barrier(S)
        S.stack_pop()


def phase_D_seq(S, nc, SEQ, lyr, Wd, scr):
    TP = 128
    TB = 8
    GN_EPS = 64e-5
    xtok = scr["xtok"]
    with contextlib.ExitStack() as st:
        S.stack_push(st)
        identF = make_ident(S, "D_ident", F32)
        ones = S.sb("D_ones", [128, 128])
        S.op("pool", lambda e: e.memset(ones[:], 0.0), writes=[ones])
        S.op("pool", lambda e: e.memset(ones[0:64, 0:64], 1.0), reads=[ones], writes=[ones])
        S.op("pool", lambda e: e.memset(ones[64:128, 64:128], 1.0), reads=[ones], writes=[ones])

        def cvec(name, key, n):
            t = S.sb("D_" + name, [128, n])
            S.dma("sp", t[:], Wd[key].t[lyr].rearrange("(c p) -> p c", p=128), reads=[Wd[key]], writes=[t],
                  allow_slow_non_contiguous=True)
            return t

        def cvec2(name, key):
            t = S.sb("D_" + name, [128, 4])
            S.dma("sp", t[:], Wd[key].t[lyr].rearrange("(c a) j -> (a j) c", a=2), reads=[Wd[key]], writes=[t],
                  allow_slow_non_contiguous=True)
            return t
        mu = cvec("mu", "rk_mu", 13)
        w0 = cvec("w0", "rk_w0", 4)
        a0 = cvec("a0", "rk_a0", 4)
        lg = cvec("lg", "rk_lnx_g", 4)
        lb = cvec("lb", "rk_lnx_b", 4)
        kkc = cvec2("kkc", "rk_kk")
        ka = cvec2("ka", "rk_ka")
        rkc = cvec2("rkc", "rk_rk")
        omka = S.sb("D_omka", [128, 4])
        S.op("pool", lambda e: e.tensor_scalar(out=omka[:], in0=ka[:], scalar1=-1.0, scalar2=1.0, op0=ALU.mult, op1=ALU.add),
             reads=[ka], writes=[omka])
        w2 = S.sb("D_w2", [64, 512], BF16)
        a2 = S.sb("D_a2", [128, 512], BF16)
        S.dma("pool", w2[:], Wd["rk_w2"].t[lyr], reads=[Wd["rk_w2"]], writes=[w2])
        S.dma("pool", a2[64:128, :], Wd["rk_a2"].t[lyr], reads=[Wd["rk_a2"]], writes=[a2])
        St = S.sb("D_state", [128, 4, 64])
        S.op("dve", lambda e: e.memset(St[:], 0.0), writes=[St])

        rst = S.sb("D_rst", [128, 13, TP + 1])
        xs = S.sb("D_xs", [128, 13, TP])
        th = S.sb("D_th", [128, TP], BF16)
        dd = S.sb("D_dd", [128, 4, TP])
        aa = S.sb("D_aa", [128, 4, TP])
        kkf = S.sb("D_kkf", [128, 4, TP])
        sq = S.sb("D_sq", [128, 4, TP])
        rn = S.sb("D_rn", [128, 4, TP])
        kp = S.sb("D_kp", [128, 4, TP])
        am = S.sb("D_am", [128, 4, TP])
        bm = S.sb("D_bm", [128, 4, TP])
        t1 = S.sb("D_t1", [128, 4, TP])
        bonus = S.sb("D_bonus", [128, 4, TP])
        vv = S.sb("D_vv", [128, 4, TP])
        tk = S.sb("D_tk", [128, 5, 4, 128])
        bcr = Ring([S.sb("D_bc%d" % i, [128, TB, 5, 256]) for i in range(2)])
        tmp = S.sb("D_tmp", [128, 4, 64])
        tmp2 = S.sb("D_tmp2", [128, 4, 64])
        kv = Ring([S.sb("D_kv%d" % i, [128, 4, 64]) for i in range(2)])
        sa = S.sb("D_sa", [128, 4])
        ybuf = S.sb("D_y", [128, 4, TP])
        ysq = S.sb("D_ysq", [128, 4, TP])
        mean = S.sb("D_mean", [128, 4, TP])
        var = S.sb("D_var", [128, 4, TP])
        rzt = S.sb("D_rz", [128, 4, TP], BF16)
        yo = S.sb("D_yo", [128, 4, TP], BF16)
        pa = Ring([S.ps("D_pa%d" % i, [128, 4, 128]) for i in range(4)])

        bc4 = lambda t: t[:].unsqueeze(2).to_broadcast([128, 4, TP])
        for nb in range(SEQ // TP):
            t0 = nb * TP
            S.dma("sp", rst[:, :, 1:TP + 1], scr["rsT"].t[:, t0:t0 + TP].rearrange("(c p) t -> p c t", p=128), reads=[scr["rsT"]],
                  writes=[rst])
            if nb == 0:
                S.op("pool", lambda e: e.memset(rst[:, :, 0:1], 0.0), reads=[rst], pwrites=[rst])
            else:
                S.dma("sp", rst[:, :, 0:1], scr["rsT"].t[:, t0 - 1:t0].rearrange("(c p) t -> p c t", p=128), reads=[scr["rsT"]],
                      pwrites=[rst], key=rst, allow_slow_non_contiguous=True)
            S.op("pool", lambda e: e.tensor_tensor(out=xs[:], in0=rst[:, :, 0:TP], in1=rst[:, :, 1:TP + 1], op=ALU.subtract),
                 reads=[rst], writes=[xs])
            S.op("pool", lambda e: e.tensor_tensor(out=xs[:], in0=xs[:], in1=mu[:].unsqueeze(2).to_broadcast([128, 13, TP]), op=ALU.mult),
                 reads=[xs, mu], writes=[xs])
            S.op("pool", lambda e: e.tensor_tensor(out=xs[:], in0=xs[:], in1=rst[:, :, 1:TP + 1], op=ALU.add), reads=[xs, rst], writes=[xs])
            r = xs[:, 0:4, :]; k = xs[:, 4:8, :]; v = xs[:, 8:12, :]
            S.op("act", lambda e: e.activation(out=th[0:64, :], in_=xs[0:64, 12, :], func=AF.Tanh), reads=[xs], pwrites=[th])
            S.op("act", lambda e: e.copy(out=th[64:128, :], in_=xs[64:128, 12, :]), reads=[xs], pwrites=[th])
            pw = pa.next(); pp = pa.next()
            for p in range(4):
                S.op("pe", lambda e, p=p, pw=pw: e.matmul(pw[:, p, :], lhsT=w2[0:64, p * 128:(p + 1) * 128], rhs=th[0:64, :], start=True, stop=True),
                     reads=[w2, th], writes=[pw] if p == 0 else (), pwrites=() if p == 0 else [pw])
                S.op("pe", lambda e, p=p, pp=pp: e.matmul(pp[:, p, :], lhsT=a2[64:128, p * 128:(p + 1) * 128], rhs=th[64:128, :], start=True, stop=True),
                     reads=[a2, th], writes=[pp] if p == 0 else (), pwrites=() if p == 0 else [pp])
            for p in range(4):
                S.op("act", lambda e, p=p, pw=pw: e.activation(out=dd[:, p, :], in_=pw[:, p, :], func=AF.Sigmoid, bias=w0[:, p:p + 1]),
                     reads=[pw, w0], pwrites=[dd])
                S.op("act", lambda e, p=p, pp=pp: e.activation(out=aa[:, p, :], in_=pp[:, p, :], func=AF.Sigmoid, bias=a0[:, p:p + 1]),
                     reads=[pp, a0], pwrites=[aa])
            S.op("act", lambda e: e.activation(out=dd[:], in_=dd[:], func=AF.Exp, scale=-0.6065306597126334), reads=[dd], writes=[dd])
            S.op("pool", lambda e: e.tensor_tensor(out=kkf[:], in0=k, in1=bc4(kkc), op=ALU.mult), reads=[xs, kkc], writes=[kkf])
            S.op("pool", lambda e: e.tensor_tensor(out=sq[:], in0=kkf[:], in1=kkf[:], op=ALU.mult), reads=[kkf], writes=[sq])
            pn = pa.next()
            for p in range(4):
                S.op("pe", lambda e, p=p, pn=pn: e.matmul(pn[:, p, :], lhsT=ones[:], rhs=sq[:, p, :], start=True, stop=True),
                     reads=[ones, sq], writes=[pn] if p == 0 else (), pwrites=() if p == 0 else [pn])
            S.op("act", lambda e, pn=pn: e.activation(out=rn[:], in_=pn[:], func=AF.Sqrt), reads=[pn], writes=[rn])
            S.op("pool", lambda e: e.tensor_scalar(out=rn[:], in0=rn[:], scalar1=1e-12, scalar2=None, op0=ALU.max), reads=[rn], writes=[rn])
            S.op("dve", lambda e: e.reciprocal(out=rn[:], in_=rn[:]), reads=[rn], writes=[rn])
            S.op("pool", lambda e: e.tensor_tensor(out=kkf[:], in0=kkf[:], in1=rn[:], op=ALU.mult), reads=[kkf, rn], writes=[kkf])
            S.op("pool", lambda e: e.tensor_tensor(out=t1[:], in0=aa[:], in1=bc4(ka), op=ALU.mult), reads=[aa, ka], writes=[t1])
            S.op("pool", lambda e: e.tensor_tensor(out=t1[:], in0=t1[:], in1=bc4(omka), op=ALU.add), reads=[t1, omka], writes=[t1])
            S.op("pool", lambda e: e.tensor_tensor(out=kp[:], in0=k, in1=t1[:], op=ALU.mult), reads=[xs, t1], writes=[kp])
            S.op("pool", lambda e: e.tensor_scalar(out=am[:], in0=kkf[:], scalar1=-1.0, scalar2=None, op0=ALU.mult), reads=[kkf], writes=[am])
            S.op("pool", lambda e: e.tensor_tensor(out=bm[:], in0=kkf[:], in1=aa[:], op=ALU.mult), reads=[kkf, aa], writes=[bm])
            S.op("pool", lambda e: e.tensor_tensor(out=t1[:], in0=r, in1=kp[:], op=ALU.mult), reads=[xs, kp], writes=[t1])
            S.op("pool", lambda e: e.tensor_tensor(out=sq[:], in0=t1[:], in1=bc4(rkc), op=ALU.mult), reads=[t1, rkc], writes=[sq])
            pr = pa.next()
            for p in range(4):
                S.op("pe", lambda e, p=p, pr=pr: e.matmul(pr[:, p, :], lhsT=ones[:], rhs=sq[:, p, :], start=True, stop=True),
                     reads=[ones, sq], writes=[pr] if p == 0 else (), pwrites=() if p == 0 else [pr])
            S.op("act", lambda e, pr=pr: e.copy(out=bonus[:], in_=pr[:]), reads=[pr], writes=[bonus])
            S.op("pool", lambda e: e.tensor_tensor(out=bonus[:], in0=bonus[:], in1=v, op=ALU.mult), reads=[bonus, xs], writes=[bonus])
            S.op("pool", lambda e: e.tensor_copy(out=vv[:], in_=v), reads=[xs], writes=[vv])
            S.op("pool", lambda e: e.tensor_copy(out=t1[:], in_=r), reads=[xs], writes=[t1])
            for oi, src in enumerate((am, bm, dd, kp, t1)):
                pt = pa.next()
                for p in range(4):
                    S.op("pe", lambda e, p=p, src=src, pt=pt: e.transpose(out=pt[:, p, :], in_=src[:, p, :], identity=identF[:]),
                         reads=[src, identF], writes=[pt] if p == 0 else (), pwrites=() if p == 0 else [pt])
                S.op("act", lambda e, oi=oi, pt=pt: e.copy(out=tk[:, oi, :, :], in_=pt[:]), reads=[pt], pwrites=[tk])
            for oi in range(5):
                for h2 in range(2):
                    S.dma("sp", xtok.t[t0:t0 + TP, oi, h2, :].rearrange("t (p j) -> t p j", p=4), tk[:, oi, :, h2 * 64:(h2 + 1) * 64],
                          reads=[tk], pwrites=[xtok], key=tk)
            S.dma("sp", rzt[:], scr["rzT"].t[:, t0:t0 + TP].rearrange("(c p) t -> p c t", p=128), reads=[scr["rzT"]], writes=[rzt])
            xflat = xtok.t.rearrange("t o h c -> (t o) h c")
            for tb in range(0, TP, TB):
                bc = bcr.next()
                for h2 in range(2):
                    S.dma("sp", bc[h2 * 64:(h2 + 1) * 64, :, :, :].rearrange("p t o c -> p (t o) c"),
                          xflat[(t0 + tb) * 5:(t0 + tb + TB) * 5, h2, :].partition_broadcast(64),
                          reads=[xtok], writes=[bc] if h2 == 0 else (), pwrites=() if h2 == 0 else [bc], key=bc)
                for tt in range(TB):
                    t = tb + tt
                    A = bc[:, tt, 0, :].rearrange("p (a j) -> p a j", a=4)
                    B = bc[:, tt, 1, :].rearrange("p (a j) -> p a j", a=4)
                    Dd = bc[:, tt, 2, :].rearrange("p (a j) -> p a j", a=4)
                    Kk = bc[:, tt, 3, :].rearrange("p (a j) -> p a j", a=4)
                    R = bc[:, tt, 4, :].rearrange("p (a j) -> p a j", a=4)
                    kvb = kv.next()
                    S.op("pool", lambda e, Kk=Kk, t=t, kvb=kvb: e.tensor_tensor(out=kvb[:], in0=Kk, in1=vv[:, :, t:t + 1].to_broadcast([128, 4, 64]),
                                                                              op=ALU.mult), reads=[bc, vv], writes=[kvb])
                    S.op("dve", lambda e, A=A: e.tensor_tensor(out=tmp[:], in0=St[:], in1=A, op=ALU.mult), reads=[St, bc], writes=[tmp])
                    S.op("dve", lambda e: e.tensor_reduce(out=sa[:], in_=tmp[:], axis=AX.X, op=ALU.add), reads=[tmp], writes=[sa])
                    S.op("dve", lambda e, Dd=Dd: e.tensor_tensor(out=St[:], in0=St[:], in1=Dd, op=ALU.mult), reads=[St, bc, tmp], writes=[St])
                    S.op("dve", lambda e, B=B: e.tensor_tensor(out=tmp2[:], in0=B, in1=sa[:].unsqueeze(2).to_broadcast([128, 4, 64]), op=ALU.mult),
                         reads=[bc, sa], writes=[tmp2])
                    S.op("dve", lambda e: e.tensor_tensor(out=St[:], in0=St[:], in1=tmp2[:], op=ALU.add), reads=[St, tmp2], writes=[St])
                    S.op("dve", lambda e, kvb=kvb: e.tensor_tensor(out=St[:], in0=St[:], in1=kvb[:], op=ALU.add), reads=[St, kvb], writes=[St])
                    S.op("dve", lambda e, R=R: e.tensor_tensor(out=tmp[:], in0=St[:], in1=R, op=ALU.mult), reads=[St, bc], writes=[tmp])
                    S.op("dve", lambda e, t=t: e.tensor_reduce(out=ybuf[:, :, t], in_=tmp[:], axis=AX.X, op=ALU.add), reads=[tmp], pwrites=[ybuf])
            S.op("pool", lambda e: e.tensor_tensor(out=ysq[:], in0=ybuf[:], in1=ybuf[:], op=ALU.mult), reads=[ybuf], writes=[ysq])
            pm = pa.next(); pq = pa.next()
            for p in range(4):
                S.op("pe", lambda e, p=p, pm=pm: e.matmul(pm[:, p, :], lhsT=ones[:], rhs=ybuf[:, p, :], start=True, stop=True),
                     reads=[ones, ybuf], writes=[pm] if p == 0 else (), pwrites=() if p == 0 else [pm])
                S.op("pe", lambda e, p=p, pq=pq: e.matmul(pq[:, p, :], lhsT=ones[:], rhs=ysq[:, p, :], start=True, stop=True),
                     reads=[ones, ysq], writes=[pq] if p == 0 else (), pwrites=() if p == 0 else [pq])
            S.op("act", lambda e, pm=pm: e.activation(out=mean[:], in_=pm[:], func=AF.Copy, scale=1.0 / 64), reads=[pm], writes=[mean])
            S.op("act", lambda e, pq=pq: e.activation(out=var[:], in_=pq[:], func=AF.Copy, scale=1.0 / 64), reads=[pq], writes=[var])
            S.op("pool", lambda e: e.tensor_tensor(out=ysq[:], in0=mean[:], in1=mean[:], op=ALU.mult), reads=[mean, ysq], writes=[ysq])
            S.op("pool", lambda e: e.tensor_tensor(out=var[:], in0=var[:], in1=ysq[:], op=ALU.subtract), reads=[var, ysq], writes=[var])
            S.op("act", lambda e: e.activation(out=var[:], in_=var[:], func=AF.Sqrt, bias=GN_EPS, scale=1.0), reads=[var], writes=[var])
            S.op("dve", lambda e: e.reciprocal(out=var[:], in_=var[:]), reads=[var], writes=[var])
            S.op("pool", lambda e: e.tensor_tensor(out=mean[:], in0=ybuf[:], in1=mean[:], op=ALU.subtract), reads=[ybuf, mean], writes=[mean])
            S.op("pool", lambda e: e.tensor_tensor(out=mean[:], in0=mean[:], in1=var[:], op=ALU.mult), reads=[mean, var], writes=[mean])
            S.op("pool", lambda e: e.tensor_tensor(out=mean[:], in0=mean[:], in1=bc4(lg), op=ALU.mult), reads=[mean, lg], writes=[mean])
            S.op("pool", lambda e: e.tensor_tensor(out=mean[:], in0=mean[:], in1=bc4(lb), op=ALU.add), reads=[mean, lb], writes=[mean])
            S.op("pool", lambda e: e.tensor_tensor(out=mean[:], in0=mean[:], in1=bonus[:], op=ALU.add), reads=[mean, bonus], writes=[mean])
            S.op("pool", lambda e: e.tensor_tensor(out=yo[:], in0=mean[:], in1=rzt[:], op=ALU.mult), reads=[mean, rzt], writes=[yo])
            S.dma("sp", scr["ysT"].t[2, :, t0:t0 + TP].rearrange("(c p) t -> p c t", p=128), yo[:], reads=[yo], pwrites=[scr["ysT"]], key=yo)
        _barrier(S)
        S.stack_pop()


_NC_CACHE = {}


def kernel(**inputs):
    SEQ = 8192
    if "nc" not in _NC_CACHE:
        _NC_CACHE["nc"] = build(SEQ, nlayers=2, enable=(1, 1, 1), scr_kind="Internal")
    nc = _NC_CACHE["nc"]
    x = np.ascontiguousarray(np.asarray(inputs["x"], dtype=np.float32))
    p = np.asarray(inputs["p"], dtype=np.float32)
    base = {}
    for k in WSPEC:
        v = np.ascontiguousarray(np.asarray(inputs[k], dtype=np.float32))
        base[k] = v.reshape(WSPEC[k])
    in_maps = []
    for b in range(8):
        m = dict(base)
        m["x"] = np.ascontiguousarray(x[b])
        m["p"] = np.ascontiguousarray(p[:, b])
        in_maps.append(m)
    res = run_bass_kernel_spmd(nc, in_maps, core_ids=list(range(8)))
    return np.stack([np.asarray(r["out"], dtype=np.float32) for r in res.results], axis=0)
```

```python
import contextlib
import numpy as np
import concourse.bass as bass
import concourse.mybir as mybir

F32 = mybir.dt.float32
BF16 = mybir.dt.bfloat16
AF = mybir.ActivationFunctionType
ALU = mybir.AluOpType
AX = mybir.AxisListType

ENGS = ("pe", "act", "dve", "pool", "sp")


class Buf:
    __slots__ = ("name", "w", "wfull", "r", "t")

    def __init__(self, name, t=None):
        self.name = name
        self.t = t
        self.w = []
        self.wfull = []
        self.r = []

    def __getitem__(self, k):
        return self.t[k]


class Op:
    __slots__ = ("eng", "fn", "deps", "marked", "tick", "dma", "idx")

    def __init__(self, eng, fn, dma):
        self.eng = eng
        self.fn = fn
        self.deps = []
        self.marked = False
        self.tick = None
        self.dma = dma
        self.idx = None


class DmaSem:
    def __init__(self):
        self.sem = None
        self.count = 0


class Sched:
    def __init__(self, nc, stack):
        self.nc = nc
        self.stack = stack
        self.ops = {e: [] for e in ENGS}
        self.all_ops = []
        self.dsems = {}
        self.n_sems = 0
        self.fence = []
        self.stacks = [stack]
        self.phase_keys = []
        self.free_ds = []
        self.all_ds = []
        self.keep = []

    def stack_push(self, st):
        self.stacks.append(st)
        self.phase_keys.append([])

    def stack_pop(self):
        self.stacks.pop()
        for kid in self.phase_keys.pop():
            ds = self.dsems.pop(kid, None)
            if ds is not None:
                self.free_ds.append(ds)

    def sb(self, name, shape, dt=F32):
        self.n_sems += 1
        name = "%s_u%d" % (name, self.n_sems)
        t = self.stacks[-1].enter_context(self.nc.sbuf_tensor(name, list(shape), dt))
        return Buf(name, t)

    def ps(self, name, shape, dt=F32):
        self.n_sems += 1
        name = "%s_u%d" % (name, self.n_sems)
        t = self.stacks[-1].enter_context(self.nc.psum_tensor(name, list(shape), dt))
        return Buf(name, t)

    def dram(self, name, shape, dt, kind="Internal"):
        t = self.nc.dram_tensor(name, list(shape), dt, kind=kind)
        return Buf(name, t.ap())

    def _add(self, eng, fn, reads, writes, pwrites, dma):
        op = Op(eng, fn, dma)
        deps = list(self.fence)
        for b in reads:
            deps.extend(b.w)
        for b in writes:
            deps.extend(b.w)
            deps.extend(b.r)
        for b in pwrites:
            deps.extend(b.wfull)
            deps.extend(b.r)
        seen = set()
        for d in deps:
            if id(d) in seen or d is op:
                continue
            seen.add(id(d))
            if d.eng == "pe" and eng == "pe" and d.dma is None and dma is None:
                continue
            op.deps.append(d)
            d.marked = True
        for b in reads:
            b.r.append(op)
            if len(b.r) > 24:
                b.r = self._prune(b.r)
        for b in writes:
            b.w = [op]
            b.wfull = [op]
            b.r = []
        for b in pwrites:
            b.w.append(op)
            if len(b.w) > 24:
                b.w = self._prune(b.w)
        op.idx = len(self.all_ops)
        self.all_ops.append(op)
        self.ops[eng].append(op)
        return op

    @staticmethod
    def _prune(lst):
        last = {}
        for o in lst:
            key = (o.eng, None) if o.dma is None else ("dma", id(o.dma))
            last[key] = o
        return list(last.values())

    def op(self, eng, fn, reads=(), writes=(), pwrites=()):
        return self._add(eng, fn, reads, writes, pwrites, None)

    def dma(self, eng, out_ap, in_ap, reads=(), writes=(), pwrites=(), key=None, **kw):
        if key is None:
            key = (list(writes) + list(pwrites))[0]
        ds = self.dsems.get(id(key))
        if ds is None:
            if self.free_ds:
                ds = self.free_ds.pop()
            else:
                ds = DmaSem()
                self.all_ds.append(ds)
            self.dsems[id(key)] = ds
            self.keep.append(key)
            if self.phase_keys:
                self.phase_keys[-1].append(id(key))
        fn = lambda e, o=out_ap, i=in_ap, kw=kw: e.dma_start(out=o, in_=i, **kw)
        op = self._add(eng, fn, reads, writes, pwrites, ds)
        ds.count += 16
        op.tick = ds.count
        return op

    def barrier_bufs(self, bufs):
        pass

    def emit(self):
        nc = self.nc
        stack = self.stack
        esem = {}
        for e in ENGS:
            esem[e] = stack.enter_context(nc.semaphore("s_" + e))
        for ds in self.all_ds:
            ds.sem = stack.enter_context(nc.semaphore("d%d" % self.n_sems))
            self.n_sems += 1
        for e in ENGS:
            c = 0
            for o in self.ops[e]:
                if o.dma is None:
                    if o.marked:
                        c += 1
                        o.tick = c
        self.max_ticks = {e: max([o.tick or 0 for o in self.ops[e] if o.dma is None] + [0]) for e in ENGS}

        def evkey(d):
            if d.dma is not None:
                return ("d", id(d.dma)), d.dma.sem, d.tick
            return ("e", d.eng), esem[d.eng], d.tick

        def run(eng_name, eng):
            seen = {}
            for o in self.ops[eng_name]:
                waits = {}
                for d in o.deps:
                    k, sem, val = evkey(d)
                    if seen.get(k, 0) >= val:
                        continue
                    if k not in waits or waits[k][1] < val:
                        waits[k] = (sem, val)
                for k, (sem, val) in waits.items():
                    eng.wait_ge(sem, val)
                    seen[k] = val
                inst = o.fn(eng)
                if o.dma is not None:
                    inst.then_inc(o.dma.sem, 16)
                elif o.marked:
                    inst.then_inc(esem[eng_name], 1)
            if eng_name == "sp":
                for e2 in ENGS:
                    m = self.max_ticks[e2]
                    if m > 0:
                        eng.wait_ge(esem[e2], m)
                for ds in self.all_ds:
                    if ds.count:
                        eng.wait_ge(ds.sem, ds.count)

        block = stack.enter_context(nc.Block())

        @block.tensor
        def _(e):
            run("pe", e)

        @block.scalar
        def _(e):
            run("act", e)

        @block.vector
        def _(e):
            run("dve", e)

        @block.gpsimd
        def _(e):
            run("pool", e)

        @block.sync
        def _(e):
            run("sp", e)


from concourse.bass_utils import run_bass_kernel_spmd

D = 1024
NCOL = 8600
PLE = 256
EPS = 1e-6


DEBUG = {}
_dbg_n = [0]


def dbg_dump(S, name, ap, buf, shape, cond=True):
    if not DEBUG.get("on") or not cond:
        return
    _dbg_n[0] += 1
    t = S.stacks[-1].enter_context(S.nc.sbuf_tensor("dbgsb_%d" % _dbg_n[0], list(shape), F32))
    tb = Buf("dbgsb", t)
    d = S.dram("dbg_" + name, list(shape), F32, kind="ExternalOutput")
    S.op("act", lambda e: e.copy(out=t[:], in_=ap), reads=[buf], writes=[tb])
    S.dma("sp", d.t, t[:], reads=[tb], writes=[d], key=tb)


class Ring:
    def __init__(self, bufs):
        self.bufs = bufs
        self.i = 0

    def next(self):
        b = self.bufs[self.i % len(self.bufs)]
        self.i += 1
        return b


def _barrier(S):
    fence = []
    for e in ENGS:
        comp = [o for o in S.ops[e] if o.dma is None]
        if comp:
            fence.append(comp[-1])
    lastd = {}
    for o in S.all_ops:
        if o.dma is not None:
            lastd[id(o.dma)] = o
    fence.extend(lastd.values())
    S.fence = fence


def make_ident(S, name="ident", dt=BF16):
    ident = S.sb(name, [128, 128], dt)
    S.op("pool", lambda e: e.memset(ident[:], 0.0), writes=[ident])
    S.op("pool", lambda e: e.affine_select(out=ident[:], in_=ident[:], pattern=[[-1, 128]],
                                           compare_op=ALU.not_equal, fill=1.0, base=0,
                                           channel_multiplier=1), reads=[ident], writes=[ident])
    return ident


def load_w_bf16(S, dst, k, src_ap, srcbuf):
    S.dma("pool", dst, src_ap, reads=[srcbuf], pwrites=[k], key=k, max_dma_last_dim=4096)


def rmsnorm_tile(S, xt_ap, xt_buf, g_buf, h_ap, h_buf, sq, ss, rs, eps=EPS, extra_reads=()):
    S.op("act", lambda e: e.activation(out=sq[:], in_=xt_ap, func=AF.Square, accum_out=ss[:]),
         reads=[xt_buf] + list(extra_reads), writes=[sq, ss])
    S.op("act", lambda e: e.activation(out=rs[:], in_=ss[:], func=AF.Sqrt, scale=1.0 / D, bias=eps),
         reads=[ss], writes=[rs])
    S.op("dve", lambda e: e.reciprocal(out=rs[:], in_=rs[:]), reads=[rs], writes=[rs])
    S.op("dve", lambda e: e.scalar_tensor_tensor(out=h_ap, in0=xt_ap, scalar=rs[:, 0:1], in1=g_buf[:],
                                                 op0=ALU.mult, op1=ALU.mult),
         reads=[xt_buf, rs, g_buf], pwrites=[h_buf])


def phase_A(S, nc, SEQ, lyr, x_src, Wd, scr):
    TT = 512
    nsub = TT // 128
    with contextlib.ExitStack() as st:
        S.stack_push(st)
        wt = S.sb("A_w", [128, 8, NCOL], BF16)
        gt = S.sb("A_g", [128, D])
        ident = make_ident(S, "A_ident")
        xt = S.sb("A_x", [128, nsub, D])
        sq = S.sb("A_sq", [128, D], BF16)
        ss = S.sb("A_ss", [128, 1])
        rs = S.sb("A_rs", [128, 1])
        h = S.sb("A_h", [128, nsub, D], BF16)
        hT = S.sb("A_hT", [128, 8, TT], BF16)
        stg_b = Ring([S.sb("A_sb%d" % i, [128, 512], BF16) for i in range(4)])
        stg_f = Ring([S.sb("A_sf%d" % i, [128, 512], F32) for i in range(3)])
        pT = Ring([S.ps("A_pT%d" % i, [128, 8, 128], BF16) for i in range(2)])
        pacc = Ring([S.ps("A_pa%d" % i, [128, 512], F32) for i in range(6)])

        w_in = Wd["w_in"]
        for k in range(8):
            S.dma("pool", wt[:, k, :], w_in.t[lyr, k * 128:(k + 1) * 128, 0:NCOL], reads=[w_in], pwrites=[wt],
                  key=wt, max_dma_last_dim=4096)
        S.dma("sp", gt[:], Wd["norm_g"].t[lyr:lyr + 1, :].partition_broadcast(128), reads=[Wd["norm_g"]],
              writes=[gt])

        FM = []
        for c in range(4):
            FM.append((c * 128, scr["qT"], c * 128, AF.Copy, 0.125, BF16))
        FM.append((512, scr["kcT"], 0, None, 1.0, BF16))
        FM.append((640, scr["vcT"], 0, None, 1.0, BF16))
        FM.append((768, scr["ksT"], 0, None, 1.0, BF16))
        FM.append((1024, scr["kwT"], 0, None, 1.0, BF16))
        for c in range(13):
            FM.append((3352 + c * 128, scr["rsT"], c * 128, None, 1.0, F32))
        for c in range(4):
            FM.append((5016 + c * 128, scr["rzT"], c * 128, AF.Silu, 1.0, BF16))
        for c in range(24):
            FM.append((5528 + c * 128, scr["mgT"], c * 128, AF.Sigmoid, 1.0, BF16))
        TM = [
            (896, 128, scr["vsw"], 0, None, BF16),
            (1152, 128, scr["vsw"], 128, None, BF16),
            (1280, 24, scr["gate"], 0, AF.Sigmoid, F32),
            (1304, 512, scr["nzs"], 0, AF.Silu, BF16),
            (1816, 512, scr["su"], 0, None, F32),
            (2328, 512, scr["sv"], 0, None, F32),
            (2840, 512, scr["szs"], 0, AF.Silu, BF16),
        ]
        evac_i = [0]

        def evac(out_ap, out_buf, in_ap, in_buf, func, scale):
            if func is None and scale == 1.0:
                if evac_i[0] % 2 == 0:
                    S.op("dve", lambda e: e.tensor_copy(out=out_ap, in_=in_ap), reads=[in_buf], writes=[out_buf])
                else:
                    S.op("act", lambda e: e.copy(out=out_ap, in_=in_ap), reads=[in_buf], writes=[out_buf])
                evac_i[0] += 1
            else:
                S.op("act", lambda e: e.activation(out=out_ap, in_=in_ap, func=func, scale=scale),
                     reads=[in_buf], writes=[out_buf])

        for ti in range(SEQ // TT):
            t0 = ti * TT
            S.dma("sp", xt[:], x_src.t[t0:t0 + TT, :].rearrange("(s p) d -> p s d", p=128), reads=[x_src],
                  writes=[xt])
            for s in range(nsub):
                rmsnorm_tile(S, xt[:, s, :], xt, gt, h[:, s, :], h, sq, ss, rs)
                pt = pT.next()
                for k in range(8):
                    S.op("pe", lambda e, k=k, s=s, pt=pt: e.transpose(out=pt[:, k, :], in_=h[:, s, k * 128:(k + 1) * 128],
                                                                     identity=ident[:]),
                         reads=[h, ident], writes=[pt] if k == 0 else (), pwrites=() if k == 0 else [pt])
                S.op("dve", lambda e, s=s, pt=pt: e.tensor_copy(out=hT[:, :, s * 128:(s + 1) * 128], in_=pt[:]),
                     reads=[pt], pwrites=[hT])
            for (c0, dbuf, r0, func, scale, dt) in FM:
                pa = pacc.next()
                for k in range(8):
                    S.op("pe", lambda e, k=k, pa=pa, c0=c0: e.matmul(pa[:], lhsT=wt[:, k, c0:c0 + 128], rhs=hT[:, k, :],
                                                                    start=(k == 0), stop=(k == 7)),
                         reads=[wt, hT], writes=[pa] if k == 0 else (), pwrites=() if k == 0 else [pa])
                sg = stg_b.next() if dt == BF16 else stg_f.next()
                evac(sg[:], sg, pa[:], pa, func, scale)
                S.dma("sp", dbuf.t[r0:r0 + 128, t0:t0 + TT], sg[:], reads=[sg], pwrites=[dbuf], key=sg)
            for s in range(nsub):
                for (c0, ncol, dbuf, dc0, func, dt) in TM:
                    pa = pacc.next()
                    for k in range(8):
                        S.op("pe", lambda e, k=k, pa=pa, c0=c0, ncol=ncol, s=s: e.matmul(
                            pa[:, 0:ncol], lhsT=hT[:, k, s * 128:(s + 1) * 128], rhs=wt[:, k, c0:c0 + ncol],
                            start=(k == 0), stop=(k == 7)),
                            reads=[wt, hT], writes=[pa] if k == 0 else (), pwrites=() if k == 0 else [pa])
                    sg = stg_b.next() if dt == BF16 else stg_f.next()
                    evac(sg[:, 0:ncol], sg, pa[:, 0:ncol], pa, func, 1.0)
                    S.dma("sp", dbuf.t[t0 + s * 128:t0 + (s + 1) * 128, dc0:dc0 + ncol], sg[:, 0:ncol], reads=[sg],
                          pwrites=[dbuf], key=sg)
        _barrier(S)
        S.stack_pop()


def make_scratch(S, SEQ, kind="Internal"):
    scr = {}
    def mk(name, shape, dt):
        scr[name] = S.dram(name, shape, dt, kind=kind)
    mk("qT", [512, SEQ], BF16)
    mk("kcT", [128, SEQ], BF16)
    mk("vcT", [128, SEQ], BF16)
    mk("ksT", [128, SEQ], BF16)
    mk("kwT", [128, SEQ], BF16)
    mk("vsw", [SEQ, 256], BF16)
    mk("gate", [SEQ, 24], F32)
    mk("nzs", [SEQ, 512], BF16)
    mk("su", [SEQ, 512], F32)
    mk("sv", [SEQ, 512], F32)
    mk("szs", [SEQ, 512], BF16)
    mk("rsT", [1664, SEQ], F32)
    mk("rzT", [512, SEQ], BF16)
    mk("mgT", [3072, SEQ], BF16)
    mk("ysT", [3, 512, SEQ], BF16)
    mk("xtok", [SEQ, 5, 2, 256], F32)
    return scr


def phase_C(S, nc, SEQ, lyr, Wd, scr):
    LN_EPS = 1e-5
    with contextlib.ExitStack() as st:
        S.stack_push(st)
        ident = make_ident(S, "C_ident")
        wraw = S.sb("C_wraw", [128, 8, 128])
        wbf = S.sb("C_wbf", [128, 8, 128], BF16)
        WT = S.sb("C_WT", [128, 8, 128], BF16)
        bsT = S.sb("C_bsT", [128, 8])
        lng = S.sb("C_lng", [128, 512])
        lnb = S.sb("C_lnb", [128, 512])
        pw = S.ps("C_pw", [128, 8, 128], BF16)
        S.dma("sp", wraw[:], Wd["sg_w"].t[lyr].rearrange("g t s -> t g s"), reads=[Wd["sg_w"]], writes=[wraw])
        S.dma("sp", bsT[:], Wd["sg_b"].t[lyr].rearrange("g t -> t g"), reads=[Wd["sg_b"]], writes=[bsT],
              allow_slow_non_contiguous=True)
        S.dma("sp", lng[:], Wd["sg_ln_g"].t[lyr:lyr + 1, :].partition_broadcast(128), reads=[Wd["sg_ln_g"]], writes=[lng])
        S.dma("sp", lnb[:], Wd["sg_ln_b"].t[lyr:lyr + 1, :].partition_broadcast(128), reads=[Wd["sg_ln_b"]], writes=[lnb])
        S.op("pool", lambda e: e.affine_select(out=wraw[:], in_=wraw[:], pattern=[[0, 8], [-1, 128]],
                                               compare_op=ALU.is_ge, fill=0.0, base=0, channel_multiplier=1),
             reads=[wraw], writes=[wraw])
        S.op("dve", lambda e: e.tensor_copy(out=wbf[:], in_=wraw[:]), reads=[wraw], writes=[wbf])
        for g in range(8):
            S.op("pe", lambda e, g=g: e.transpose(out=pw[:, g, :], in_=wbf[:, g, :], identity=ident[:]),
                 reads=[wbf, ident], pwrites=[pw])
        S.op("dve", lambda e: e.tensor_copy(out=WT[:], in_=pw[:]), reads=[pw], writes=[WT])

        NB = 2
        svt = Ring([S.sb("C_sv%d" % i, [128, 512]) for i in range(NB)])
        sut = Ring([S.sb("C_su%d" % i, [128, 512]) for i in range(NB)])
        szt = Ring([S.sb("C_sz%d" % i, [128, 512], BF16) for i in range(NB)])
        stats = S.sb("C_stats", [128, 6])
        mv = S.sb("C_mv", [128, 2])
        rstd = S.sb("C_rstd", [128, 1])
        vn0 = S.sb("C_vnf", [128, 512])
        vn = Ring([S.sb("C_vn%d" % i, [128, 512], BF16) for i in range(2)])
        y0 = S.sb("C_y0", [128, 512])
        yb = Ring([S.sb("C_yb%d" % i, [128, 512], BF16) for i in range(2)])
        pm = Ring([S.ps("C_pm%d" % i, [128, 512]) for i in range(2)])
        pt = Ring([S.ps("C_pt%d" % i, [128, 4, 128], BF16) for i in range(2)])
        stg = Ring([S.sb("C_stg%d" % i, [128, 4, 512], BF16) for i in range(2)])
        ys = scr["ysT"]
        sgb = None
        for c in range(SEQ // 128):
            t0 = c * 128
            v = svt.next(); u = sut.next(); z = szt.next()
            S.dma("sp", v[:], scr["sv"].t[t0:t0 + 128, :], reads=[scr["sv"]], writes=[v])
            S.dma("sp", u[:], scr["su"].t[t0:t0 + 128, :], reads=[scr["su"]], writes=[u])
            S.dma("sp", z[:], scr["szs"].t[t0:t0 + 128, :], reads=[scr["szs"]], writes=[z])
            S.op("dve", lambda e, v=v: e.bn_stats(out=stats[:], in_=v[:]), reads=[v], writes=[stats])
            S.op("dve", lambda e: e.bn_aggr(out=mv[:], in_=stats[:]), reads=[stats], writes=[mv])
            S.op("act", lambda e: e.activation(out=rstd[:], in_=mv[:, 1:2], func=AF.Sqrt, bias=LN_EPS, scale=1.0),
                 reads=[mv], writes=[rstd])
            S.op("dve", lambda e: e.reciprocal(out=rstd[:], in_=rstd[:]), reads=[rstd], writes=[rstd])
            S.op("dve", lambda e, v=v: e.tensor_scalar(out=vn0[:], in0=v[:], scalar1=mv[:, 0:1], scalar2=rstd[:, 0:1],
                                                       op0=ALU.subtract, op1=ALU.mult),
                 reads=[v, mv, rstd], writes=[vn0])
            S.op("pool", lambda e: e.tensor_tensor(out=vn0[:], in0=vn0[:], in1=lng[:], op=ALU.mult),
                 reads=[vn0, lng], writes=[vn0])
            vb = vn.next()
            S.op("pool", lambda e, vb=vb: e.tensor_tensor(out=vb[:], in0=vn0[:], in1=lnb[:], op=ALU.add),
                 reads=[vn0, lnb], writes=[vb])
            pmm = pm.next()
            for g in range(8):
                S.op("pe", lambda e, g=g, vb=vb, pmm=pmm: e.matmul(pmm[:, g * 64:(g + 1) * 64], lhsT=WT[:, g, :],
                                                                   rhs=vb[:, g * 64:(g + 1) * 64], start=True, stop=True),
                     reads=[WT, vb], writes=[pmm] if g == 0 else (), pwrites=() if g == 0 else [pmm])
            S.op("dve", lambda e, pmm=pmm: e.tensor_tensor(
                out=y0[:].rearrange("p (g d) -> p g d", g=8), in0=pmm[:].rearrange("p (g d) -> p g d", g=8),
                in1=bsT[:].unsqueeze(2).to_broadcast([128, 8, 64]), op=ALU.add), reads=[pmm, bsT], writes=[y0])
            S.op("pool", lambda e, u=u: e.tensor_tensor(out=y0[:], in0=y0[:], in1=u[:], op=ALU.mult),
                 reads=[y0, u], writes=[y0])
            y = yb.next()
            S.op("dve", lambda e, y=y, z=z: e.tensor_tensor(out=y[:], in0=y0[:], in1=z[:], op=ALU.mult),
                 reads=[y0, z], writes=[y])
            ptt = pt.next()
            for k in range(4):
                S.op("pe", lambda e, k=k, y=y, ptt=ptt: e.transpose(out=ptt[:, k, :], in_=y[:, k * 128:(k + 1) * 128],
                                                                    identity=ident[:]),
                     reads=[y, ident], writes=[ptt] if k == 0 else (), pwrites=() if k == 0 else [ptt])
            if c % 4 == 0:
                sgb = stg.next()
            cc = c % 4
            S.op("act", lambda e, ptt=ptt, sgb=sgb, cc=cc: e.copy(out=sgb[:, :, cc * 128:(cc + 1) * 128], in_=ptt[:]),
                 reads=[ptt], writes=[sgb] if cc == 0 else (), pwrites=() if cc == 0 else [sgb])
            if cc == 3 or c == SEQ // 128 - 1:
                tb = (c // 4) * 512
                n = (cc + 1) * 128
                S.dma("sp", ys.t[1, :, tb:tb + n].rearrange("(k p) t -> p k t", p=128), sgb[:, :, 0:n], reads=[sgb],
                      pwrites=[ys], key=sgb)
        _barrier(S)
        S.stack_pop()


def phase_E(S, nc, SEQ, lyr, x_src, x_dst, Wd, scr, final):
    TT = 512
    nsub = 4
    with contextlib.ExitStack() as st:
        S.stack_push(st)
        ident = make_ident(S, "E_ident")
        wb = S.sb("E_wb", [128, 3, 4, D], BF16)
        wo = S.sb("E_wo", [128, 8, D], BF16)
        wpg = S.sb("E_wpg", [128, 8, D], BF16)
        wpp = S.sb("E_wpp", [128, 2, D], BF16)
        gpl = S.sb("E_gpl", [128, D])
        gfin = S.sb("E_gfin", [128, D])
        for n in range(3):
            S.dma("pool", wb[:, n, :, :], Wd["w_branch"].t[lyr, n].rearrange("(k p) d -> p k d", p=128),
                  reads=[Wd["w_branch"]], pwrites=[wb], key=wb, max_dma_last_dim=4096)
        for k0 in range(0, 8, 4):
            S.dma("pool", wo[:, k0:k0 + 4, :], Wd["w_o"].t[lyr, k0 * 128:(k0 + 4) * 128, :].rearrange("(k p) d -> p k d", p=128),
                  reads=[Wd["w_o"]], pwrites=[wo], key=wo, max_dma_last_dim=4096)
            S.dma("pool", wpg[:, k0:k0 + 4, :], Wd["w_ple_gate"].t[lyr, k0 * 128:(k0 + 4) * 128, :].rearrange("(k p) d -> p k d", p=128),
                  reads=[Wd["w_ple_gate"]], pwrites=[wpg], key=wpg, max_dma_last_dim=4096)
        S.dma("pool", wpp[:], Wd["w_ple_proj"].t[lyr].rearrange("(k p) d -> p k d", p=128),
              reads=[Wd["w_ple_proj"]], pwrites=[wpp], key=wpp, max_dma_last_dim=4096)
        S.dma("sp", gpl[:], Wd["ple_norm_g"].t[lyr:lyr + 1, :].partition_broadcast(128), reads=[Wd["ple_norm_g"]], writes=[gpl])
        if final:
            S.dma("sp", gfin[:], Wd["final_norm_g"].t[0:1, :].partition_broadcast(128), reads=[Wd["final_norm_g"]], writes=[gfin])

        yst = S.sb("E_ys", [128, 3, 4, TT], BF16)
        mgt = S.sb("E_mg", [128, 24, TT], BF16)
        mrg = S.sb("E_mrg", [128, 8, TT])
        mrb = S.sb("E_mrb", [128, 8, TT], BF16)
        tmp = Ring([S.sb("E_tmp%d" % i, [128, TT]) for i in range(2)])
        xt = S.sb("E_x", [128, nsub, D])
        pin = S.sb("E_p", [128, nsub, PLE])
        pbf = S.sb("E_pbf", [128, PLE], BF16)
        pTs = S.sb("E_pT", [128, 2, 128], BF16)
        sq = S.sb("E_sq", [128, D], BF16)
        ss = S.sb("E_ss", [128, 1])
        rs = S.sb("E_rs", [128, 1])
        hp = S.sb("E_hp", [128, D], BF16)
        hpT = S.sb("E_hpT", [128, 8, 128], BF16)
        gate = S.sb("E_gate", [128, D])
        xo = Ring([S.sb("E_xo%d" % i, [128, D]) for i in range(2)])
        pz = Ring([S.ps("E_pz%d" % i, [128, TT]) for i in range(3)])
        po = Ring([S.ps("E_po%d" % i, [128, 512]) for i in range(2)])
        pg = Ring([S.ps("E_pg%d" % i, [128, 512]) for i in range(2)])
        ptr = S.ps("E_ptr", [128, 8, 128], BF16)

        for ti in range(SEQ // TT):
            t0 = ti * TT
            for n in range(3):
                S.dma("sp", yst[:, n, :, :], scr["ysT"].t[n, :, t0:t0 + TT].rearrange("(k p) t -> p k t", p=128),
                      reads=[scr["ysT"]], writes=[yst] if n == 0 else (), pwrites=() if n == 0 else [yst], key=yst)
            for k0 in range(0, 24, 8):
                S.dma("sp", mgt[:, k0:k0 + 8, :], scr["mgT"].t[k0 * 128:(k0 + 8) * 128, t0:t0 + TT].rearrange("(k p) t -> p k t", p=128),
                      reads=[scr["mgT"]], writes=[mgt] if k0 == 0 else (), pwrites=() if k0 == 0 else [mgt], key=mgt)
            S.dma("sp", xt[:], x_src.t[t0:t0 + TT, :].rearrange("(s p) d -> p s d", p=128), reads=[x_src], writes=[xt])
            S.dma("sp", pin[:], Wd["p"].t[lyr, t0:t0 + TT, :].rearrange("(s p) d -> p s d", p=128), reads=[Wd["p"]], writes=[pin])
            for dc in range(8):
                pzs = []
                for n in range(3):
                    pzz = pz.next()
                    pzs.append(pzz)
                    for k in range(4):
                        S.op("pe", lambda e, n=n, k=k, dc=dc, pzz=pzz: e.matmul(
                            pzz[:], lhsT=wb[:, n, k, dc * 128:(dc + 1) * 128], rhs=yst[:, n, k, :], start=(k == 0), stop=(k == 3)),
                            reads=[wb, yst], writes=[pzz] if k == 0 else (), pwrites=() if k == 0 else [pzz])
                S.op("dve", lambda e, dc=dc, p0=pzs[0]: e.tensor_tensor(out=mrg[:, dc, :], in0=p0[:], in1=mgt[:, dc, :], op=ALU.mult),
                     reads=[pzs[0], mgt], pwrites=[mrg])
                t1 = tmp.next()
                S.op("dve", lambda e, dc=dc, p1=pzs[1], t1=t1: e.tensor_tensor(out=t1[:], in0=p1[:], in1=mgt[:, 8 + dc, :], op=ALU.mult),
                     reads=[pzs[1], mgt], writes=[t1])
                t2 = tmp.next()
                S.op("dve", lambda e, dc=dc, p2=pzs[2], t2=t2: e.tensor_tensor(out=t2[:], in0=p2[:], in1=mgt[:, 16 + dc, :], op=ALU.mult),
                     reads=[pzs[2], mgt], writes=[t2])
                S.op("pool", lambda e, dc=dc, t1=t1: e.tensor_tensor(out=mrg[:, dc, :], in0=mrg[:, dc, :], in1=t1[:], op=ALU.add),
                     reads=[mrg, t1], pwrites=[mrg])
                S.op("pool", lambda e, dc=dc, t2=t2: e.tensor_tensor(out=mrb[:, dc, :], in0=mrg[:, dc, :], in1=t2[:], op=ALU.add),
                     reads=[mrg, t2], pwrites=[mrb])
            for s in range(nsub):
                for blk in range(2):
                    pp = po.next()
                    for k in range(8):
                        S.op("pe", lambda e, k=k, s=s, blk=blk, pp=pp: e.matmul(
                            pp[:], lhsT=mrb[:, k, s * 128:(s + 1) * 128], rhs=wo[:, k, blk * 512:(blk + 1) * 512],
                            start=(k == 0), stop=(k == 7)),
                            reads=[mrb, wo], writes=[pp] if k == 0 else (), pwrites=() if k == 0 else [pp])
                    S.op("dve", lambda e, s=s, blk=blk, pp=pp: e.tensor_tensor(
                        out=xt[:, s, blk * 512:(blk + 1) * 512], in0=pp[:], in1=xt[:, s, blk * 512:(blk + 1) * 512], op=ALU.add),
                        reads=[pp, xt], pwrites=[xt])
                rmsnorm_tile(S, xt[:, s, :], xt, gpl, hp[:], hp, sq, ss, rs)
                for k in range(8):
                    S.op("pe", lambda e, k=k: e.transpose(out=ptr[:, k, :], in_=hp[:, k * 128:(k + 1) * 128], identity=ident[:]),
                         reads=[hp, ident], writes=[ptr] if k == 0 else (), pwrites=() if k == 0 else [ptr])
                S.op("act", lambda e: e.copy(out=hpT[:], in_=ptr[:]), reads=[ptr], writes=[hpT])
                S.op("pool", lambda e, s=s: e.tensor_copy(out=pbf[:], in_=pin[:, s, :]), reads=[pin], writes=[pbf])
                for k in range(2):
                    S.op("pe", lambda e, k=k: e.transpose(out=ptr[:, k, :], in_=pbf[:, k * 128:(k + 1) * 128], identity=ident[:]),
                         reads=[pbf, ident, hpT], writes=[ptr] if k == 0 else (), pwrites=() if k == 0 else [ptr])
                S.op("act", lambda e: e.copy(out=pTs[:], in_=ptr[:, 0:2, :]), reads=[ptr], writes=[pTs])
                xout = xo.next()
                for blk in range(2):
                    pgg = pg.next()
                    for k in range(8):
                        S.op("pe", lambda e, k=k, blk=blk, pgg=pgg: e.matmul(
                            pgg[:], lhsT=hpT[:, k, :], rhs=wpg[:, k, blk * 512:(blk + 1) * 512], start=(k == 0), stop=(k == 7)),
                            reads=[hpT, wpg], writes=[pgg] if k == 0 else (), pwrites=() if k == 0 else [pgg])
                    S.op("act", lambda e, blk=blk, pgg=pgg: e.activation(out=gate[:, blk * 512:(blk + 1) * 512], in_=pgg[:], func=AF.Sigmoid),
                         reads=[pgg], pwrites=[gate])
                    ppp = pg.next()
                    for k in range(2):
                        S.op("pe", lambda e, k=k, blk=blk, ppp=ppp: e.matmul(
                            ppp[:], lhsT=pTs[:, k, :], rhs=wpp[:, k, blk * 512:(blk + 1) * 512], start=(k == 0), stop=(k == 1)),
                            reads=[pTs, wpp], writes=[ppp] if k == 0 else (), pwrites=() if k == 0 else [ppp])
                    S.op("dve", lambda e, blk=blk, ppp=ppp: e.tensor_tensor(
                        out=gate[:, blk * 512:(blk + 1) * 512], in0=ppp[:], in1=gate[:, blk * 512:(blk + 1) * 512], op=ALU.mult),
                        reads=[ppp, gate], pwrites=[gate])
                    S.op("pool", lambda e, blk=blk, s=s, xout=xout: e.tensor_tensor(
                        out=xout[:, blk * 512:(blk + 1) * 512], in0=gate[:, blk * 512:(blk + 1) * 512],
                        in1=xt[:, s, blk * 512:(blk + 1) * 512], op=ALU.add),
                        reads=[gate, xt], writes=[xout] if blk == 0 else (), pwrites=() if blk == 0 else [xout])
                if final:
                    S.op("act", lambda e, xout=xout: e.activation(out=sq[:], in_=xout[:], func=AF.Square, accum_out=ss[:]),
                         reads=[xout], writes=[sq, ss])
                    S.op("act", lambda e: e.activation(out=rs[:], in_=ss[:], func=AF.Sqrt, scale=1.0 / D, bias=EPS),
                         reads=[ss], writes=[rs])
                    S.op("dve", lambda e: e.reciprocal(out=rs[:], in_=rs[:]), reads=[rs], writes=[rs])
                    S.op("dve", lambda e, xout=xout: e.scalar_tensor_tensor(out=xout[:], in0=xout[:], scalar=rs[:, 0:1], in1=gfin[:],
                                                                            op0=ALU.mult, op1=ALU.mult),
                         reads=[xout, rs, gfin], writes=[xout])
                S.dma("sp", x_dst.t[t0 + s * 128:t0 + (s + 1) * 128, :], xout[:], reads=[xout], pwrites=[x_dst], key=xout)
        _barrier(S)
        S.stack_pop()


def phase_D(S, nc, SEQ, lyr, Wd, scr):
    TP = 128
    C = 16
    NCH = TP // C
    GN_EPS = 64e-5
    LD = 0.6065306597126334
    with contextlib.ExitStack() as st:
        S.stack_push(st)
        identB = make_ident(S, "D_identB", BF16)
        ones = S.sb("D_ones", [128, 128])
        S.op("pool", lambda e: e.memset(ones[:], 0.0), writes=[ones])
        S.op("pool", lambda e: e.memset(ones[0:64, 0:64], 1.0), reads=[ones], writes=[ones])
        S.op("pool", lambda e: e.memset(ones[64:128, 64:128], 1.0), reads=[ones], writes=[ones])
        Ff = S.sb("D_F", [128, 64], BF16)
        S.op("pool", lambda e: e.tensor_tensor(out=Ff[:], in0=identB[:, 0:64], in1=identB[:, 64:128], op=ALU.add), reads=[identB], writes=[Ff])
        Sel = S.sb("D_Sel", [128, 16], BF16)
        S.op("pool", lambda e: e.tensor_tensor(out=Sel[:], in0=identB[:, 0:16], in1=identB[:, 16:32], op=ALU.add), reads=[identB], writes=[Sel])
        for hh in range(2, 8):
            S.op("pool", lambda e, hh=hh: e.tensor_tensor(out=Sel[:], in0=Sel[:], in1=identB[:, hh * 16:(hh + 1) * 16], op=ALU.add),
                 reads=[identB, Sel], writes=[Sel])
        maskF = S.sb("D_maskF", [128, 4, 8], BF16)
        S.op("pool", lambda e: e.memset(maskF[:], 0.0), writes=[maskF])
        for p in range(4):
            for h2 in range(2):
                S.op("pool", lambda e, p=p, h2=h2: e.memset(maskF[h2 * 64:(h2 + 1) * 64, p, 2 * p + h2:2 * p + h2 + 1], 1.0), reads=[maskF], writes=[maskF])
        maskZ = S.sb("D_maskZ", [128, 4, 2], BF16)
        S.op("pool", lambda e: e.memset(maskZ[:], 1.0), writes=[maskZ])
        S.op("pool", lambda e: e.affine_select(out=maskZ[:], in_=maskZ[:], pattern=[[-32, 4], [-16, 2]], compare_op=ALU.is_ge, fill=0.0,
                                               base=0, channel_multiplier=1), reads=[maskZ], writes=[maskZ])
        S.op("pool", lambda e: e.affine_select(out=maskZ[:], in_=maskZ[:], pattern=[[32, 4], [16, 2]], compare_op=ALU.is_ge, fill=0.0,
                                               base=15, channel_multiplier=-1), reads=[maskZ], writes=[maskZ])

        def trimask(name, pat, cm, op):
            m = S.sb(name, [128, 128], BF16)
            S.op("pool", lambda e: e.memset(m[:], 1.0), writes=[m])
            S.op("pool", lambda e: e.affine_select(out=m[:], in_=m[:], pattern=pat, compare_op=op, fill=0.0, base=0, channel_multiplier=cm),
                 reads=[m], writes=[m])
            return m
        mSL = trimask("D_mSL", [[-16, 8], [-1, 16]], 1, ALU.is_gt)
        mSU = trimask("D_mSU", [[16, 8], [1, 16]], -1, ALU.is_gt)
        mUI = trimask("D_mUI", [[16, 8], [1, 16]], -1, ALU.is_ge)
        rm = S.sb("D_rm", [128, 512])
        S.op("pool", lambda e: e.memset(rm[:], 1.0), writes=[rm])
        S.op("pool", lambda e: e.memset(rm[:, 0:512:16], 0.0), reads=[rm], writes=[rm])

        def cvec(name, key, n):
            t = S.sb("D_" + name, [128, n])
            S.dma("sp", t[:], Wd[key].t[lyr].rearrange("(c p) -> p c", p=128), reads=[Wd[key]], writes=[t],
                  allow_slow_non_contiguous=True)
            return t

        def cvec2(name, key):
            t = S.sb("D_" + name, [128, 4])
            S.dma("sp", t[:], Wd[key].t[lyr].rearrange("(c a) j -> (a j) c", a=2), reads=[Wd[key]], writes=[t],
                  allow_slow_non_contiguous=True)
            return t
        mu = cvec("mu", "rk_mu", 13)
        w0 = cvec("w0", "rk_w0", 4)
        a0 = cvec("a0", "rk_a0", 4)
        lg = cvec("lg", "rk_lnx_g", 4)
        lb = cvec("lb", "rk_lnx_b", 4)
        kkc = cvec2("kkc", "rk_kk")
        ka = cvec2("ka", "rk_ka")
        rkc = cvec2("rkc", "rk_rk")
        omka = S.sb("D_omka", [128, 4])
        S.op("pool", lambda e: e.tensor_scalar(out=omka[:], in0=ka[:], scalar1=-1.0, scalar2=1.0, op0=ALU.mult, op1=ALU.add),
             reads=[ka], writes=[omka])
        w2 = S.sb("D_w2", [64, 512], BF16)
        a2 = S.sb("D_a2", [128, 512], BF16)
        S.dma("pool", w2[:], Wd["rk_w2"].t[lyr], reads=[Wd["rk_w2"]], writes=[w2])
        S.dma("pool", a2[64:128, :], Wd["rk_a2"].t[lyr], reads=[Wd["rk_a2"]], writes=[a2])

        Hm = S.sb("D_H", [128, 4, 64])
        Hn = S.sb("D_Hn", [128, 4, 64])
        Hbf = S.sb("D_Hbf", [128, 4, 64], BF16)
        S.op("pool", lambda e: e.memset(Hm[:], 0.0), writes=[Hm])
        S.op("pool", lambda e: e.memset(Hbf[:], 0.0), writes=[Hbf])

        rst = S.sb("D_rst", [128, 13, TP + 1])
        xs = S.sb("D_xs", [128, 13, TP])
        th = S.sb("D_th", [128, TP], BF16)
        sg = S.sb("D_sg", [128, 4, TP])
        cum = S.sb("D_cum", [128, 4, TP])
        E1 = S.sb("D_E1", [128, 4, TP])
        E2 = S.sb("D_E2", [128, 4, TP])
        E3 = S.sb("D_E3", [128, 4, TP])
        aa = S.sb("D_aa", [128, 4, TP])
        kkf = S.sb("D_kkf", [128, 4, TP])
        sq = S.sb("D_sq", [128, 4, TP])
        rn = S.sb("D_rn", [128, 4, TP])
        kp = S.sb("D_kp", [128, 4, TP])
        t1 = S.sb("D_t1", [128, 4, TP])
        t2 = S.sb("D_t2", [128, 4, TP])
        comp = [S.sb("D_cmp%d" % i, [128, 4, TP], BF16) for i in range(5)]
        ZXr = Ring([[S.sb("D_Z%d_%d" % (b, i), [128, NCH, 4, 128], BF16) for i in range(5)] for b in range(2)])
        DcR = Ring([S.sb("D_Dc%d" % i, [128, NCH, 4]) for i in range(2)])
        bonR = Ring([S.sb("D_bon%d" % i, [128, 4, TP]) for i in range(2)])
        rzR = Ring([S.sb("D_rz%d" % i, [128, 4, TP], BF16) for i in range(2)])
        ybR = Ring([S.sb("D_yb%d" % i, [128, 4, TP]) for i in range(2)])
        yo = S.sb("D_yo", [128, 4, TP], BF16)
        ppre = S.ps("D_ppre", [128, 4, TP])
        R4 = lambda nm, shp, dt=BF16: Ring([S.sb("D_%s%d" % (nm, i), shp, dt) for i in range(4)])
        WyZr = R4("WyZ", [128, 4, 128]); WhTr = R4("WhT", [128, 4, 128]); BtZr = R4("BtZ", [128, 4, 128]); KtZr = R4("KtZ", [128, 4, 128])
        U0r = R4("U0", [128, 64]); Vtr = R4("Vt", [128, 64]); PTr = R4("PT", [128, 128]); QTr = R4("QT", [128, 128])
        ysbR = Ring([S.sb("D_ysb%d" % i, [128, 64]) for i in range(5)])

        class Reg:
            def __init__(self, bank, ap):
                self.bank = bank
                self.t = ap

        class Lane:
            pass
        lanes = []
        for li in range(2):
            L = Lane()
            L.Gr = Ring([S.sb("D_G%d_%d" % (li, i), [128, 128], BF16) for i in range(2)])
            L.Nr = Ring([S.sb("D_N%d_%d" % (li, i), [128, 128], BF16) for i in range(2)])
            L.NTr = Ring([S.sb("D_NT%d_%d" % (li, i), [128, 128], BF16) for i in range(2)])
            L.MTs = S.sb("D_MTs%d" % li, [128, 128], BF16)
            L.X1Z = S.sb("D_X1Z%d" % li, [128, 4, 128], BF16)
            L.X1s = S.sb("D_X1s%d" % li, [128, 64], BF16)
            L.tks = S.sb("D_tks%d" % li, [128, 2, 64], BF16)
            ba = S.ps("D_ba%d" % li, [128, 512])
            bb = ppre if li == 0 else S.ps("D_bb%d" % li, [128, 4, 128])
            bg = S.ps("D_bg%d" % li, [128, 3, 128])
            L.tokc = Reg(ba, ba.t[:, 0:256].rearrange("q (o j) -> q o j", o=4))
            L.QTp = Reg(ba, ba.t[:, 256:384])
            L.mvp = Reg(ba, ba.t[:, 384:448])
            L.sc = [Reg(bb, bb.t[:, i, :]) for i in range(4)]
            L.bb = bb
            L.bg = bg
            lanes.append(L)
        bs = S.ps("D_bs", [128, 512])
        bt_ = S.ps("D_bt", [128, 512])
        WHp = Reg(bs, bs.t[:, 0:256].rearrange("q (p i) -> q p i", p=4))
        Yp = Reg(bs, bs.t[:, 256:320])
        yfp = Reg(bt_, bt_.t[:, 0:64].rearrange("q (p t) -> q p t", p=4))
        yn = S.sb("D_yn", [128, 64])
        YZ = S.sb("D_YZ", [128, 4, 128], BF16)
        stats = S.sb("D_stats", [128, 6])
        mv = S.sb("D_mv", [128, 2])
        rstd = S.sb("D_rstd", [128, 1])

        bc4 = lambda t: t[:].unsqueeze(2).to_broadcast([128, 4, TP])

        def mm(out_ap, obuf, lhsT, lbuf, rhs, rbuf, start, stop=True, first_write=False):
            obuf = getattr(obuf, "bank", obuf)
            S.op("pe", lambda e: e.matmul(out_ap, lhsT=lhsT, rhs=rhs, start=start, stop=stop, skip_group_check=True),
                 reads=[lbuf, rbuf], writes=[obuf] if first_write else (), pwrites=() if first_write else [obuf])

        def prep(nb):
            t0 = nb * TP
            S.dma("sp", rst[:, :, 1:TP + 1], scr["rsT"].t[:, t0:t0 + TP].rearrange("(c p) t -> p c t", p=128), reads=[scr["rsT"]], writes=[rst])
            if nb == 0:
                S.op("pool", lambda e: e.memset(rst[:, :, 0:1], 0.0), reads=[rst], pwrites=[rst])
            else:
                S.dma("sp", rst[:, :, 0:1], scr["rsT"].t[:, t0 - 1:t0].rearrange("(c p) t -> p c t", p=128), reads=[scr["rsT"]],
                      pwrites=[rst], key=rst, allow_slow_non_contiguous=True)
            S.op("pool", lambda e: e.tensor_tensor(out=xs[:], in0=rst[:, :, 0:TP], in1=rst[:, :, 1:TP + 1], op=ALU.subtract), reads=[rst], writes=[xs])
            S.op("pool", lambda e: e.tensor_tensor(out=xs[:], in0=xs[:], in1=mu[:].unsqueeze(2).to_broadcast([128, 13, TP]), op=ALU.mult),
                 reads=[xs, mu], writes=[xs])
            S.op("pool", lambda e: e.tensor_tensor(out=xs[:], in0=xs[:], in1=rst[:, :, 1:TP + 1], op=ALU.add), reads=[xs, rst], writes=[xs])
            r = xs[:, 0:4, :]; k = xs[:, 4:8, :]; v = xs[:, 8:12, :]
            S.op("act", lambda e: e.activation(out=th[0:64, :], in_=xs[0:64, 12, :], func=AF.Tanh), reads=[xs], pwrites=[th])
            S.op("act", lambda e: e.copy(out=th[64:128, :], in_=xs[64:128, 12, :]), reads=[xs], pwrites=[th])
            for p in range(4):
                mm(ppre[:, p, :], ppre, w2[0:64, p * 128:(p + 1) * 128], w2, th[0:64, :], th, True, first_write=(p == 0))
            for p in range(4):
                S.op("act", lambda e, p=p: e.activation(out=sg[:, p, :], in_=ppre[:, p, :], func=AF.Sigmoid, bias=w0[:, p:p + 1]),
                     reads=[w0], writes=[ppre], pwrites=[sg])
            for p in range(4):
                mm(ppre[:, p, :], ppre, a2[64:128, p * 128:(p + 1) * 128], a2, th[64:128, :], th, True, first_write=(p == 0))
            for p in range(4):
                S.op("act", lambda e, p=p: e.activation(out=aa[:, p, :], in_=ppre[:, p, :], func=AF.Sigmoid, bias=a0[:, p:p + 1]),
                     reads=[a0], writes=[ppre], pwrites=[aa])
            S.op("dve", lambda e: e.tensor_tensor_scan(out=cum[:].rearrange("q p t -> q (p t)"), data0=rm[:],
                                                       data1=sg[:].rearrange("q p t -> q (p t)"), initial=0.0, op0=ALU.mult, op1=ALU.add),
                 reads=[rm, sg], writes=[cum])
            S.op("act", lambda e: e.activation(out=E1[:], in_=cum[:], func=AF.Exp, scale=-LD), reads=[cum], writes=[E1])
            S.op("act", lambda e: e.activation(out=E2[:], in_=cum[:], func=AF.Exp, scale=LD), reads=[cum], writes=[E2])
            S.op("pool", lambda e: e.tensor_tensor(out=t2[:], in0=cum[:], in1=sg[:], op=ALU.subtract), reads=[cum, sg], writes=[t2])
            S.op("act", lambda e: e.activation(out=E3[:], in_=t2[:], func=AF.Exp, scale=-LD), reads=[t2], writes=[E3])
            Dc = DcR.next()
            S.op("pool", lambda e, Dc=Dc: e.tensor_copy(out=Dc[:].rearrange("q c p -> q p c"), in_=E1[:, :, 15:TP:16]), reads=[E1], writes=[Dc])
            S.op("pool", lambda e: e.tensor_tensor(out=kkf[:], in0=k, in1=bc4(kkc), op=ALU.mult), reads=[xs, kkc], writes=[kkf])
            S.op("pool", lambda e: e.tensor_tensor(out=sq[:], in0=kkf[:], in1=kkf[:], op=ALU.mult), reads=[kkf], writes=[sq])
            for p in range(4):
                mm(ppre[:, p, :], ppre, ones[:], ones, sq[:, p, :], sq, True, first_write=(p == 0))
            S.op("act", lambda e: e.activation(out=rn[:], in_=ppre[:], func=AF.Sqrt), writes=[rn, ppre])
            S.op("dve", lambda e: e.tensor_scalar(out=rn[:], in0=rn[:], scalar1=1e-12, scalar2=None, op0=ALU.max), reads=[rn], writes=[rn])
            S.op("dve", lambda e: e.reciprocal(out=rn[:], in_=rn[:]), reads=[rn], writes=[rn])
            S.op("pool", lambda e: e.tensor_tensor(out=kkf[:], in0=kkf[:], in1=rn[:], op=ALU.mult), reads=[kkf, rn], writes=[kkf])
            S.op("pool", lambda e: e.tensor_tensor(out=t1[:], in0=aa[:], in1=bc4(ka), op=ALU.mult), reads=[aa, ka], writes=[t1])
            S.op("pool", lambda e: e.tensor_tensor(out=t1[:], in0=t1[:], in1=bc4(omka), op=ALU.add), reads=[t1, omka], writes=[t1])
            S.op("pool", lambda e: e.tensor_tensor(out=kp[:], in0=k, in1=t1[:], op=ALU.mult), reads=[xs, t1], writes=[kp])
            At, Bt, Kt, Rt, Vb = comp
            S.op("pool", lambda e: e.scalar_tensor_tensor(out=At[:], in0=kkf[:], scalar=-1.0, in1=E3[:], op0=ALU.mult, op1=ALU.mult)
                 if False else e.tensor_tensor(out=t2[:], in0=kkf[:], in1=E3[:], op=ALU.mult), reads=[kkf, E3], writes=[t2])
            S.op("dve", lambda e: e.tensor_scalar(out=At[:], in0=t2[:], scalar1=-1.0, scalar2=None, op0=ALU.mult), reads=[t2], writes=[At])
            S.op("pool", lambda e: e.tensor_tensor(out=t2[:], in0=kkf[:], in1=aa[:], op=ALU.mult), reads=[kkf, aa], writes=[t2])
            S.op("pool", lambda e: e.tensor_tensor(out=Bt[:], in0=t2[:], in1=E2[:], op=ALU.mult), reads=[t2, E2], writes=[Bt])
            S.op("pool", lambda e: e.tensor_tensor(out=Kt[:], in0=kp[:], in1=E2[:], op=ALU.mult), reads=[kp, E2], writes=[Kt])
            S.op("pool", lambda e: e.tensor_tensor(out=Rt[:], in0=r, in1=E1[:], op=ALU.mult), reads=[xs, E1], writes=[Rt])
            S.op("pool", lambda e: e.tensor_copy(out=Vb[:], in_=v), reads=[xs], writes=[Vb])
            S.op("pool", lambda e: e.tensor_tensor(out=t1[:], in0=r, in1=kp[:], op=ALU.mult), reads=[xs, kp], writes=[t1])
            S.op("pool", lambda e: e.tensor_tensor(out=sq[:], in0=t1[:], in1=bc4(rkc), op=ALU.mult), reads=[t1, rkc], writes=[sq])
            for p in range(4):
                mm(ppre[:, p, :], ppre, ones[:], ones, sq[:, p, :], sq, True, first_write=(p == 0))
            bon = bonR.next()
            S.op("act", lambda e, bon=bon: e.copy(out=bon[:], in_=ppre[:]), writes=[bon, ppre])
            S.op("pool", lambda e, bon=bon: e.tensor_tensor(out=bon[:], in0=bon[:], in1=v, op=ALU.mult), reads=[bon, xs], writes=[bon])
            rzt = rzR.next()
            S.dma("sp", rzt[:], scr["rzT"].t[:, t0:t0 + TP].rearrange("(c p) t -> p c t", p=128), reads=[scr["rzT"]], writes=[rzt])
            ZX = ZXr.next()
            for oi in range(5):
                for p in range(4):
                    S.op("dve" if (oi * 4 + p) % 2 == 0 else "pool", lambda e, oi=oi, p=p, ZX=ZX: e.tensor_tensor(
                        out=ZX[oi][:, :, p, :].rearrange("q c (h t) -> q c h t", t=16),
                        in0=comp[oi][:, p, :].rearrange("q (c t) -> q c t", t=16).unsqueeze(2).to_broadcast([128, NCH, 8, 16]),
                        in1=maskF[:, p, :].unsqueeze(1).unsqueeze(3).to_broadcast([128, NCH, 8, 16]), op=ALU.mult),
                        reads=[comp[oi], maskF], writes=[ZX[oi]] if p == 0 else (), pwrites=() if p == 0 else [ZX[oi]])
            return dict(ZX=ZX, Dc=Dc, bon=bon, rzt=rzt, yb=ybR.next(), t0=t0)


        def pre(bt, c, L, pc):
            ZA, ZB, ZK, ZR, ZV = bt["ZX"]
            BtZ = BtZr.next(); KtZ = KtZr.next(); U0 = U0r.next(); Vt = Vtr.next(); PTs = PTr.next(); QTs = QTr.next()
            WyZ = WyZr.next(); WhT = WhTr.next()
            pc.update(BtZ=BtZ, KtZ=KtZ, U0=U0, Vt=Vt, PTs=PTs, QTs=QTs, WyZ=WyZ, WhT=WhT, c=c, bt=bt)
            tokc = L.tokc
            first = True
            for oi, Z in enumerate((ZA, ZB, ZK, ZV)):
                for p in range(4):
                    mm(tokc.t[:, oi, :], tokc, Z[:, c, p, :], Z, Ff[:], Ff, first, first_write=first)
                    first = False
            N1 = L.Nr.next(); NT1 = L.NTr.next()
            specs = ((L.sc[0], ZA, ZB, mSL, N1), (L.sc[1], ZB, ZA, mSU, NT1), (L.sc[2], ZK, ZA, mSU, L.MTs), (L.sc[3], ZB, ZR, mUI, PTs))
            for gi, (pb, Lh, R_, msk, dst) in enumerate(specs):
                for p in range(4):
                    mm(pb.t, pb, Lh[:, c, p, :], Lh, R_[:, c, p, :], R_, p == 0, first_write=(gi == 0 and p == 0))
            for p in range(4):
                mm(L.QTp.t, L.QTp, ZK[:, c, p, :], ZK, ZR[:, c, p, :], ZR, False, first_write=False)
            yield
            G0 = L.Gr.next()
            tks = L.tks
            S.op("act", lambda e: e.copy(out=G0[:, 0:64], in_=tokc.t[:, 0, :]), writes=[G0, tokc.bank])
            mz = maskZ[:].unsqueeze(3).to_broadcast([128, 4, 2, 64])
            S.op("dve", lambda e: e.tensor_copy(out=tks[:], in_=tokc.t[:, 1:3, :]), writes=[tks, tokc.bank])
            S.op("act", lambda e: e.copy(out=Vt[:], in_=tokc.t[:, 3, :]), writes=[Vt, tokc.bank])
            S.op("pool", lambda e: e.tensor_tensor(out=BtZ[:].rearrange("q p (a j) -> q p a j", a=2),
                                                   in0=tks[:, 0, :].unsqueeze(1).unsqueeze(1).to_broadcast([128, 4, 2, 64]), in1=mz, op=ALU.mult),
                 reads=[tks, maskZ], writes=[BtZ])
            S.op("pool", lambda e: e.tensor_tensor(out=KtZ[:].rearrange("q p (a j) -> q p a j", a=2),
                                                   in0=tks[:, 1, :].unsqueeze(1).unsqueeze(1).to_broadcast([128, 4, 2, 64]), in1=mz, op=ALU.mult),
                 reads=[tks, maskZ], writes=[KtZ])
            for gi, (pb, Lh, R_, msk, dst) in enumerate(specs):
                S.op("dve", lambda e, pb=pb, msk=msk, dst=dst: e.tensor_tensor(out=dst[:], in0=pb.t, in1=msk[:], op=ALU.mult),
                     reads=[msk], writes=[dst, pb.bank])
            S.op("dve", lambda e: e.tensor_tensor(out=QTs[:], in0=L.QTp.t, in1=mUI[:], op=ALU.mult), reads=[mUI], writes=[QTs, L.QTp.bank])
            yield
            mm(L.mvp.t, L.mvp, L.MTs[:], L.MTs, Vt[:], Vt, True, first_write=True)
            yield
            S.op("act", lambda e: e.copy(out=G0[:, 64:128], in_=L.mvp.t), writes=[L.mvp.bank], pwrites=[G0])
            yield
            G = G0; Nk = N1; NTk = NT1
            gb = L.bg
            for lev in range(4):
                mm(gb[:, 0, :], gb, identB[:], identB, G[:], G, True, stop=False, first_write=True)
                mm(gb[:, 0, :], gb, NTk[:], NTk, G[:], G, False)
                if lev < 3:
                    mm(gb[:, 1, :], gb, NTk[:], NTk, Nk[:], Nk, True)
                    mm(gb[:, 2, :], gb, Nk[:], Nk, NTk[:], NTk, True)
                    yield
                    G2 = L.Gr.next(); N2 = L.Nr.next(); NT2 = L.NTr.next()
                    S.op("act", lambda e, G2=G2: e.copy(out=G2[:], in_=gb[:, 0, :]), writes=[G2, gb])
                    S.op("act", lambda e, N2=N2: e.copy(out=N2[:], in_=gb[:, 1, :]), writes=[N2, gb])
                    S.op("act", lambda e, NT2=NT2: e.copy(out=NT2[:], in_=gb[:, 2, :]), writes=[NT2, gb])
                    G = G2; Nk = N2; NTk = NT2
                    yield
                else:
                    yield
                    S.op("act", lambda e: e.copy(out=L.X1s[:], in_=gb[:, 0, 0:64]), writes=[L.X1s, gb])
                    S.op("act", lambda e: e.copy(out=U0[:], in_=gb[:, 0, 64:128]), writes=[U0, gb])
                    S.op("dve", lambda e: e.tensor_tensor(out=L.X1Z[:].rearrange("q p (a j) -> q p a j", a=2),
                                                           in0=L.X1s[:].unsqueeze(1).unsqueeze(1).to_broadcast([128, 4, 2, 64]), in1=mz, op=ALU.mult),
                         reads=[L.X1s, maskZ], writes=[L.X1Z])
                    yield
            bb = L.bb
            for p in range(4):
                mm(bb[:, p, :], bb, identB[:], identB, ZR[:, c, p, :], ZR, p == 0, stop=False, first_write=(p == 0))
                mm(bb[:, p, :], bb, L.X1Z[:, p, :], L.X1Z, PTs[:], PTs, False)
            ba = L.tokc.bank
            for p in range(4):
                mm(ba[:, p * 128:(p + 1) * 128], ba, L.X1Z[:, p, :], L.X1Z, BtZ[:, p, :], BtZ, p == 0, first_write=(p == 0))
            yield
            S.op("act", lambda e: e.copy(out=WyZ[:], in_=bb[:]), writes=[WyZ, bb])
            S.op("dve", lambda e: e.tensor_copy(out=WhT[:].rearrange("q p m -> q (p m)"), in_=ba[:]), writes=[WhT, ba])
            yield

        def state_stream(pc):
            c = pc["c"]; bt = pc["bt"]
            BtZ, KtZ, U0, Vt, PTs, QTs, WyZ, WhT = (pc[k] for k in ("BtZ", "KtZ", "U0", "Vt", "PTs", "QTs", "WyZ", "WhT"))
            for p in range(4):
                mm(WHp.t[:, p, :], WHp, BtZ[:, p, :], BtZ, U0[:], U0, p == 0, stop=False, first_write=(p == 0))
            for p in range(4):
                mm(WHp.t[:, p, :], WHp, KtZ[:, p, :], KtZ, Vt[:], Vt, False, stop=False)
            mm(Yp.t, Yp, PTs[:], PTs, U0[:], U0, False, stop=False)
            mm(Yp.t, Yp, QTs[:], QTs, Vt[:], Vt, False, stop=False)
            yield
            for p in range(4):
                mm(Yp.t, Yp, WyZ[:, p, :], WyZ, Hbf[:, p, :], Hbf, False, stop=(p == 3))
            for p in range(4):
                mm(WHp.t[:, p, :], WHp, WhT[:, p, :], WhT, Hbf[:, p, :], Hbf, False, stop=True)
            yield
            Dc = bt["Dc"]
            S.op("dve", lambda e: e.tensor_tensor(out=Hn[:], in0=WHp.t, in1=Hm[:], op=ALU.add), reads=[Hm], writes=[Hn, WHp.bank])
            S.op("dve", lambda e: e.tensor_tensor(out=Hm[:], in0=Hn[:], in1=Dc[:, c, :].unsqueeze(2).to_broadcast([128, 4, 64]), op=ALU.mult),
                 reads=[Hn, Dc], writes=[Hm])
            ysb = ysbR.next()
            pc["ysb"] = ysb
            S.op("act", lambda e: e.copy(out=ysb[:], in_=Yp.t), writes=[ysb, Yp.bank])
            S.op("act", lambda e: e.copy(out=Hbf[:], in_=Hm[:]), reads=[Hm], writes=[Hbf])
            yield

        def out_stream(pc):
            c = pc["c"]; bt = pc["bt"]; ysb = pc["ysb"]
            S.op("dve", lambda e: e.bn_stats(out=stats[:], in_=ysb[:]), reads=[ysb], writes=[stats])
            S.op("dve", lambda e: e.bn_aggr(out=mv[:], in_=stats[:]), reads=[stats], writes=[mv])
            yield
            S.op("act", lambda e: e.activation(out=rstd[:], in_=mv[:, 1:2], func=AF.Sqrt, bias=GN_EPS, scale=1.0), reads=[mv], writes=[rstd])
            yield
            S.op("dve", lambda e: e.reciprocal(out=rstd[:], in_=rstd[:]), reads=[rstd], writes=[rstd])
            S.op("dve", lambda e: e.tensor_scalar(out=yn[:], in0=ysb[:], scalar1=mv[:, 0:1], scalar2=rstd[:, 0:1], op0=ALU.subtract, op1=ALU.mult),
                 reads=[ysb, mv, rstd], writes=[yn])
            yield
            S.op("dve", lambda e: e.tensor_tensor(out=YZ[:].rearrange("q p (a j) -> q p a j", a=2),
                                                   in0=yn[:].unsqueeze(1).unsqueeze(1).to_broadcast([128, 4, 2, 64]),
                                                   in1=maskZ[:].unsqueeze(3).to_broadcast([128, 4, 2, 64]), op=ALU.mult),
                 reads=[yn, maskZ], writes=[YZ])
            yield
            for p in range(4):
                mm(yfp.t[:, p, :], yfp, YZ[:, p, :], YZ, Sel[:], Sel, True, first_write=(p == 0))
            yield
            yb = bt["yb"]
            S.op("act", lambda e: e.copy(out=yb[:, :, c * 16:(c + 1) * 16], in_=yfp.t), writes=([yb] if c == 0 else []) + [yfp.bank],
                 pwrites=() if c == 0 else [yb])
            if c == NCH - 1:
                post(bt)
            yield

        def post(bt):
            yb = bt["yb"]; bon = bt["bon"]; rzt = bt["rzt"]; t0 = bt["t0"]
            S.op("pool", lambda e: e.tensor_tensor(out=yb[:], in0=yb[:], in1=bc4(lg), op=ALU.mult), reads=[yb, lg], writes=[yb])
            S.op("pool", lambda e: e.tensor_tensor(out=yb[:], in0=yb[:], in1=bc4(lb), op=ALU.add), reads=[yb, lb], writes=[yb])
            S.op("pool", lambda e: e.tensor_tensor(out=yb[:], in0=yb[:], in1=bon[:], op=ALU.add), reads=[yb, bon], writes=[yb])
            S.op("pool", lambda e: e.tensor_tensor(out=yo[:], in0=yb[:], in1=rzt[:], op=ALU.mult), reads=[yb, rzt], writes=[yo])
            S.dma("sp", scr["ysT"].t[2, :, t0:t0 + TP].rearrange("(c p) t -> p c t", p=128), yo[:], reads=[yo], pwrites=[scr["ysT"]], key=yo)

        chunks = []
        for nb in range(SEQ // TP):
            for c in range(NCH):
                chunks.append((nb, c))
        bts = {}
        nxt = 0
        lane_gen = [None, None]
        lane_pc = [None, None]
        done_order = {}
        next_state = 0
        state_gen = None; state_pc = None
        out_q = []; out_gen = None
        n_total = len(chunks)
        finished_out = 0
        pcs = {}
        while finished_out < n_total:
            for li in range(2):
                if lane_gen[li] is None and nxt < n_total and nxt - next_state < 3:
                    nb, c = chunks[nxt]
                    if nb not in bts:
                        bts[nb] = prep(nb)
                    pc = {"idx": nxt}
                    pcs[nxt] = pc
                    lane_gen[li] = pre(bts[nb], c, lanes[li], pc)
                    lane_pc[li] = pc
                    nxt += 1
                if lane_gen[li] is not None:
                    try:
                        next(lane_gen[li])
                    except StopIteration:
                        done_order[lane_pc[li]["idx"]] = True
                        lane_gen[li] = None
            for _rep in range(DEBUG.get("state_rep", 2)):
                if state_gen is None and done_order.get(next_state) and next_state - finished_out < 3:
                    state_pc = pcs[next_state]
                    state_gen = state_stream(state_pc)
                if state_gen is not None:
                    try:
                        next(state_gen)
                    except StopIteration:
                        out_q.append(state_pc)
                        state_gen = None
                        next_state += 1
            for _rep in range(DEBUG.get("out_rep", 1)):
                if out_gen is None and out_q:
                    out_gen = out_stream(out_q.pop(0))
                if out_gen is not None:
                    try:
                        next(out_gen)
                    except StopIteration:
                        out_gen = None
                        finished_out += 1
        _barrier(S)
        S.stack_pop()


WSPEC = {
    "norm_g": [2, 1024], "w_in": [2, 1024, 9112], "cmp_w1": [2, 2, 32, 64, 128], "cmp_w2": [2, 2, 128, 64],
    "cmp_pe": [2, 2, 32, 64], "sg_ln_g": [2, 512], "sg_ln_b": [2, 512], "sg_w": [2, 8, 128, 128], "sg_b": [2, 8, 128],
    "rk_mu": [2, 1664], "rk_w0": [2, 512], "rk_w2": [2, 64, 512], "rk_a0": [2, 512], "rk_a2": [2, 64, 512],
    "rk_kk": [2, 8, 64], "rk_ka": [2, 8, 64], "rk_rk": [2, 8, 64], "rk_lnx_g": [2, 512], "rk_lnx_b": [2, 512],
    "w_branch": [2, 3, 512, 1024], "w_o": [2, 1024, 1024], "ple_norm_g": [2, 1024], "w_ple_gate": [2, 1024, 1024],
    "w_ple_proj": [2, 256, 1024], "final_norm_g": [1, 1024],
}


def build(SEQ, nlayers=2, enable=(1, 1, 1), scr_kind="Internal"):
    nc = bass.Bass("TRN2", target_bir_lowering=False)
    with contextlib.ExitStack() as stack:
        S = Sched(nc, stack)
        x = Buf("x", nc.dram_tensor("x", [SEQ, D], F32, kind="ExternalInput").ap())
        Wd = {"p": Buf("p", nc.dram_tensor("p", [2, SEQ, PLE], F32, kind="ExternalInput").ap())}
        for k, shp in WSPEC.items():
            Wd[k] = Buf(k, nc.dram_tensor(k, shp, F32, kind="ExternalInput").ap())
        out = Buf("out", nc.dram_tensor("out", [SEQ, D], F32, kind="ExternalOutput").ap())
        scr = make_scratch(S, SEQ, kind=scr_kind)
        xmid = S.dram("xmid", [SEQ, D], F32, kind=scr_kind)
        cur = x
        for lyr in range(nlayers):
            last = lyr == nlayers - 1
            dst = out if last else xmid
            phase_A(S, nc, SEQ, lyr, cur, Wd, scr)
            if enable[0]:
                phase_B(S, nc, SEQ, lyr, Wd, scr)
            if enable[1]:
                phase_C(S, nc, SEQ, lyr, Wd, scr)
            if enable[2]:
                phase_D(S, nc, SEQ, lyr, Wd, scr)
            phase_E(S, nc, SEQ, lyr, cur, dst, Wd, scr, final=(last and nlayers == 2))
            cur = dst
        S.emit()
    return nc


def phase_B(S, nc, SEQ, lyr, Wd, scr):
    NC = (SEQ - 32) // 16 + 1
    NT = (NC + 127) // 128
    NCp = NT * 128
    KT = SEQ // 128
    with contextlib.ExitStack() as st:
        S.stack_push(st)
        ident = make_ident(S, "B_ident")
        ksT = S.sb("B_ksT", [128, 2, SEQ], BF16)
        HALF = min(4096, SEQ)
        NA = SEQ // HALF
        kwT = S.sb("B_kwT", [64, 2, SEQ], BF16)
        vs = S.sb("B_vs", [128, KT, 2, 65], BF16)
        vw = S.sb("B_vw", [128, KT, 2, 65], BF16)
        kcmpT = S.sb("B_kcmpT", [64, 2, NCp], BF16)
        Rc = S.sb("B_Rc", [128, NT, 2, 193], BF16)
        S.op("pool", lambda e: e.memset(ksT[64:128, :, :], 1.0), writes=[ksT])
        for g_ in range(2):
            for a_ in range(NA):
                S.op("pool", lambda e, g_=g_, a_=a_: e.affine_select(
                    out=ksT[64:128, g_, a_ * HALF:(a_ + 1) * HALF], in_=ksT[64:128, g_, a_ * HALF:(a_ + 1) * HALF], pattern=[[1, HALF]],
                    compare_op=ALU.is_ge, fill=0.0, base=0, channel_multiplier=-64), reads=[ksT], pwrites=[ksT])
                S.op("pool", lambda e, g_=g_, a_=a_: e.affine_select(
                    out=ksT[64:128, g_, a_ * HALF:(a_ + 1) * HALF], in_=ksT[64:128, g_, a_ * HALF:(a_ + 1) * HALF], pattern=[[-1, HALF]],
                    compare_op=ALU.is_ge, fill=0.0, base=63, channel_multiplier=64), reads=[ksT], pwrites=[ksT])
        S.dma("sp", ksT[0:64, :, :], scr["ksT"].t.rearrange("(g d) t -> d g t", g=2), reads=[scr["ksT"]], pwrites=[ksT], key=ksT)
        S.dma("sp", kwT[:], scr["kwT"].t.rearrange("(g d) t -> d g t", g=2), reads=[scr["kwT"]], writes=[kwT])
        S.op("pool", lambda e: e.memset(vs[:], 1.0), writes=[vs])
        S.op("pool", lambda e: e.memset(vw[:], 1.0), writes=[vw])
        for k0 in range(0, KT, 8):
            k1 = min(KT, k0 + 8)
            for (dst, c0) in ((vs, 0), (vw, 128)):
                for g in range(2):
                    S.dma("sp", dst[:, k0:k1, g, 0:64],
                          scr["vsw"].t[k0 * 128:k1 * 128, c0 + g * 64:c0 + (g + 1) * 64].rearrange("(k p) d -> p k d", p=128),
                          reads=[scr["vsw"]], pwrites=[dst], key=dst)
        S.op("pool", lambda e: e.memset(Rc[:], 1.0), writes=[Rc])
        for nt in range(NT):
            for g in range(2):
                S.op("pool", lambda e, nt=nt, g=g: e.affine_select(
                    out=Rc[:, nt, g, 65:193], in_=Rc[:, nt, g, 65:193], pattern=[[-4, 128]], compare_op=ALU.is_ge, fill=0.0,
                    base=nt * 128 + 1, channel_multiplier=1), reads=[Rc], writes=[Rc])
                S.op("pool", lambda e, nt=nt, g=g: e.affine_select(
                    out=Rc[:, nt, g, 65:193], in_=Rc[:, nt, g, 65:193], pattern=[[4, 128]], compare_op=ALU.is_ge, fill=0.0,
                    base=3 - nt * 128, channel_multiplier=-1), reads=[Rc], writes=[Rc])
        npad = NCp - NC
        if npad:
            S.op("pool", lambda e: e.affine_select(
                out=Rc[:, NT - 1, :, :], in_=Rc[:, NT - 1, :, :], pattern=[[0, 2 * 193]], compare_op=ALU.is_ge, fill=0.0,
                base=(NC - 1) - (NT - 1) * 128, channel_multiplier=-1), reads=[Rc], writes=[Rc])
        S.op("pool", lambda e: e.memset(kcmpT[:], 0.0), writes=[kcmpT])

        with contextlib.ExitStack() as st2:
            S.stack_push(st2)
            kvT = S.sb("B_kvT", [64, 2, SEQ], BF16)
            w1 = S.sb("B_w1", [64, 32, 128], BF16)
            w2 = S.sb("B_w2", [128, 64], BF16)
            peT = S.sb("B_peT", [64, 32])
            peTb = S.sb("B_peTb", [64, 32], BF16)
            cb = S.sb("B_cb", [128, 1])
            hid = S.sb("B_hid", [128, NCp], BF16)
            ph = S.ps("B_ph", [128, 512])
            pc1 = S.ps("B_pc1", [128, 512])
            pk = S.ps("B_pk", [128, 512])
            for kv in range(2):
                src = scr["kcT"] if kv == 0 else scr["vcT"]
                S.dma("sp", kvT[:], src.t.rearrange("(g d) t -> d g t", g=2), reads=[src], writes=[kvT])
                S.dma("pool", w1[:], Wd["cmp_w1"].t[lyr, kv].rearrange("l d h -> d l h"), reads=[Wd["cmp_w1"]], writes=[w1])
                S.dma("pool", w2[:], Wd["cmp_w2"].t[lyr, kv], reads=[Wd["cmp_w2"]], writes=[w2])
                S.dma("sp", peT[:], Wd["cmp_pe"].t[lyr, kv].rearrange("l d -> d l"), reads=[Wd["cmp_pe"]], writes=[peT],
                      allow_slow_non_contiguous=True)
                S.op("dve", lambda e: e.tensor_copy(out=peTb[:], in_=peT[:]), reads=[peT], writes=[peTb])
                for l in range(32):
                    S.op("pe", lambda e, l=l: e.matmul(pc1[:, 0:1], lhsT=w1[:, l, :], rhs=peTb[:, l:l + 1], start=(l == 0), stop=(l == 31)),
                         reads=[w1, peTb], writes=[pc1] if l == 0 else (), pwrites=() if l == 0 else [pc1])
                S.op("dve", lambda e: e.tensor_copy(out=cb[:], in_=pc1[:, 0:1]), reads=[pc1], writes=[cb])
                for g in range(2):
                    S.op("dve", lambda e: e.memset(hid[:], 0.0), writes=[hid])
                    for n0 in range(0, NC, 512):
                        nn = min(512, NC - n0)
                        for l in range(32):
                            S.op("pe", lambda e, l=l, g=g, n0=n0, nn=nn: e.matmul(
                                ph[:, 0:nn], lhsT=w1[:, l, :], rhs=kvT[:, g, n0 * 16 + l: n0 * 16 + l + (nn - 1) * 16 + 1: 16], start=(l == 0), stop=(l == 31)),
                                reads=[w1, kvT], writes=[ph] if l == 0 else (), pwrites=() if l == 0 else [ph])
                        S.op("act", lambda e, n0=n0, nn=nn: e.activation(out=hid[:, n0:n0 + nn], in_=ph[:, 0:nn], func=AF.Silu, bias=cb[:, 0:1]),
                             reads=[ph, cb], pwrites=[hid])
                    if kv == 0:
                        for n0 in range(0, NC, 512):
                            nn = min(512, NC - n0)
                            S.op("pe", lambda e, n0=n0, nn=nn: e.matmul(pk[0:64, 0:nn], lhsT=w2[:], rhs=hid[:, n0:n0 + nn], start=True, stop=True),
                                 reads=[w2, hid], writes=[pk])
                            S.op("dve", lambda e, g=g, n0=n0, nn=nn: e.tensor_copy(out=kcmpT[:, g, n0:n0 + nn], in_=pk[0:64, 0:nn]),
                                 reads=[pk], pwrites=[kcmpT])
                    else:
                        for nt in range(NT):
                            rows = min(128, NC - nt * 128)
                            S.op("pe", lambda e, nt=nt: e.matmul(pk[:, 0:64], lhsT=hid[:, nt * 128:(nt + 1) * 128], rhs=w2[:], start=True, stop=True),
                                 reads=[w2, hid], writes=[pk])
                            S.op("dve", lambda e, g=g, nt=nt: e.tensor_copy(out=Rc[:, nt, g, 0:64], in_=pk[:, 0:64]),
                                 reads=[pk], pwrites=[Rc])
            _barrier(S)
            S.stack_pop()

        qt = Ring([S.sb("B_q%d" % i, [64, 8, 128], BF16) for i in range(2)])
        gt = Ring([S.sb("B_g%d" % i, [128, 24]) for i in range(2)])
        nzt = Ring([S.sb("B_nz%d" % i, [128, 512], BF16) for i in range(2)])
        Et = Ring([S.sb("B_E%d" % i, [128, 512], BF16) for i in range(4)])
        psT = Ring([S.ps("B_psT%d" % i, [128, 512]) for i in range(3)])
        pcA = S.ps("B_pcA", [128, 2, 193])
        pcB = S.ps("B_pcB", [128, 2, 193])
        pos = S.ps("B_pos", [128, 4, 65])
        pow_ = S.ps("B_pow", [128, 4, 65])
        pmisc = S.ps("B_pmisc", [128, 4, 128], BF16)
        P2 = lambda nm, shp, dt=F32: [S.sb("B_%s%d" % (nm, i), shp, dt) for i in range(2)]
        oc2 = P2("oc", [128, 4, 193]); rcs2 = P2("rcs", [128, 4]); rss2 = P2("rss", [128, 4]); rws2 = P2("rws", [128, 4])
        cc2 = P2("cc", [128, 3, 4]); sc_2 = P2("sc", [128, 128]); sc2_2 = P2("sc2", [128, 128]); m1_2 = P2("m1", [128, 8]); m2_2 = P2("m2", [128, 8])
        pws2 = P2("pws", [128, 4, 65]); pss2 = P2("pss", [128, 4, 65])
        negq2 = P2("negq", [128, 2, 128], BF16)
        for _nq in negq2:
            S.op("pool", lambda e, _nq=_nq: e.memset(_nq[:], 0.0), writes=[_nq])
        qAr = {(g_, a_): Ring([S.sb("B_qA%d%d_%d" % (g_, a_, i), [128, 4, 128], BF16) for i in range(2)]) for g_ in range(2) for a_ in range(NA)}
        yg = S.sb("B_yg", [128, 4, 64])
        ytmp = S.sb("B_ytmp", [128, 4, 64])
        ynsa = S.sb("B_ynsa", [128, 512], BF16)
        stg = Ring([S.sb("B_stg%d" % i, [128, 4, 128], BF16) for i in range(2)])

        def qk_exp(kT_ap, kbuf, q_ap, qbuf, neg_lhsT=None):
            p = psT.next()
            if False:
                pass
            else:
                S.op("pe", lambda e, p=p: e.matmul(p[:], lhsT=kT_ap, rhs=q_ap, start=True, stop=True), reads=[kbuf, qbuf], writes=[p])
            E = Et.next()
            S.op("act", lambda e, p=p, E=E: e.activation(out=E[:], in_=p[:], func=AF.Exp), reads=[p], writes=[E])
            return E

        def pipeline(tiles, L=2):
            Es = {}
            n = len(tiles)
            for i in range(n + L):
                if i < n:
                    Es[i] = tiles[i][0]()
                if i - L >= 0:
                    tiles[i - L][1](Es.pop(i - L))

        def mask(E, base, cm, qstep):
            S.op("pool", lambda e, E=E: e.affine_select(out=E[:], in_=E[:], pattern=[[0, 4], [qstep, 128]], compare_op=ALU.is_ge,
                                                       fill=0.0, base=base, channel_multiplier=cm), reads=[E], writes=[E])

        for qb in range(SEQ // 128):
            q0 = qb * 128
            q = qt.next(); gg = gt.next(); nz = nzt.next()
            S.dma("sp", q[:], scr["qT"].t[:, q0:q0 + 128].rearrange("(h d) t -> d h t", h=8), reads=[scr["qT"]], writes=[q])
            qAs = {}
            for g_ in range(2):
                for a_ in range(min(NA, qb * 128 // HALF + 1)):
                    qa = qAr[(g_, a_)].next()
                    qAs[(g_, a_)] = qa
                    S.dma("sp", qa[0:64, :, :], scr["qT"].t[g_ * 256:(g_ + 1) * 256, q0:q0 + 128].rearrange("(h d) t -> d h t", h=4),
                          reads=[scr["qT"]], writes=[qa])
            S.dma("sp", gg[:], scr["gate"].t[q0:q0 + 128, :], reads=[scr["gate"]], writes=[gg])
            S.dma("sp", nz[:], scr["nzs"].t[q0:q0 + 128, :], reads=[scr["nzs"]], writes=[nz])
            def gbody(g, q=q, gg=gg, nz=nz, qAs=qAs, qb=qb, q0=q0):
                oc = oc2[g]; rcs = rcs2[g]; rss = rss2[g]; rws = rws2[g]; cc = cc2[g]; sc = sc_2[g]; sc2 = sc2_2[g]
                m1 = m1_2[g]; m2 = m2_2[g]; negq = negq2[g]; pws = pws2[g]; pss = pss2[g]
                q_ap = q[:, 4 * g:4 * g + 4, :].rearrange("d h q -> d (h q)")
                n_max = min(8 * qb + 6, NC - 1)
                ntl = n_max // 128 + 1
                def c_qk(nt, g=g, q_ap=q_ap, q=q):
                    E = qk_exp(kcmpT[:, g, nt * 128:(nt + 1) * 128], kcmpT, q_ap, q)
                    if q0 - 16 * (128 * nt + 127) - 31 < 0:
                        mask(E, q0 - 16 * 128 * nt - 31, -16, 1)
                    return E

                def c_pv(nt, E, g=g, ntl=ntl):
                    for h in range(4):
                        pcx = pcA if h < 2 else pcB
                        first = (nt == 0 and h % 2 == 0)
                        S.op("pe", lambda e, E=E, h=h, pcx=pcx, nt=nt, first=first, g=g, ntl=ntl: e.matmul(
                            pcx[:, h % 2, :], lhsT=E[:, h * 128:(h + 1) * 128], rhs=Rc[:, nt, g, :], start=first,
                            stop=(nt == ntl - 1 and h % 2 == 1), skip_group_check=True),
                            reads=[E, Rc], writes=[pcx] if first else (), pwrites=() if first else [pcx])
                pipeline([(lambda nt=nt: c_qk(nt), lambda E, nt=nt: c_pv(nt, E)) for nt in range(ntl)])
                S.op("act", lambda e: e.copy(out=oc[:, 0:2, :], in_=pcA[:]), reads=[pcA], pwrites=[oc])
                S.op("act", lambda e: e.copy(out=oc[:, 2:4, :], in_=pcB[:]), reads=[pcB], pwrites=[oc])
                S.op("dve", lambda e: e.tensor_scalar(out=rcs[:], in0=oc[:, :, 64], scalar1=1e-30, scalar2=None, op0=ALU.max),
                     reads=[oc], writes=[rcs])
                S.op("dve", lambda e: e.reciprocal(out=rcs[:], in_=rcs[:]), reads=[rcs], writes=[rcs])
                S.op("dve", lambda e: e.tensor_scalar(out=sc[:], in0=oc[:, 0, 65:193], scalar1=rcs[:, 0:1], scalar2=None, op0=ALU.mult),
                     reads=[oc, rcs], writes=[sc])
                for h in range(1, 4):
                    S.op("dve", lambda e, h=h: e.scalar_tensor_tensor(out=sc[:], in0=oc[:, h, 65:193], scalar=rcs[:, h:h + 1], in1=sc[:],
                                                                      op0=ALU.mult, op1=ALU.add), reads=[oc, rcs, sc], writes=[sc])
                for half in range(2):
                    tb = 2 * qb + half
                    ps_ = slice(half * 64, (half + 1) * 64)
                    if tb + 1 < 128:
                        S.op("dve", lambda e, ps_=ps_, tb=tb: e.memset(sc[ps_, tb + 1:128], -1e4), reads=[sc], writes=[sc])
                    lo = max(tb - 1, 0)
                    S.op("dve", lambda e, ps_=ps_, tb=tb, lo=lo: e.memset(sc[ps_, lo:tb + 1], 1e4), reads=[sc], writes=[sc])
                S.op("dve", lambda e: e.memset(sc[:, 0:1], 1e4), reads=[sc], writes=[sc])
                S.op("dve", lambda e: e.max(out=m1[:], in_=sc[:]), reads=[sc], writes=[m1])
                S.op("dve", lambda e: e.match_replace(out=sc2[:], in_to_replace=m1[:], in_values=sc[:], imm_value=-3e4),
                     reads=[sc, m1], writes=[sc2])
                S.op("dve", lambda e: e.max(out=m2[:], in_=sc2[:]), reads=[sc2], writes=[m2])
                S.op("dve", lambda e: e.tensor_scalar(out=negq[:, 0, :], in0=sc[:], scalar1=m2[:, 7:8], scalar2=-1e4, op0=ALU.is_lt, op1=ALU.mult),
                     reads=[sc, m2], pwrites=[negq])
                S.op("dve", lambda e: e.tensor_scalar(out=negq[:, 1, 64:128], in0=sc[:, 0:64], scalar1=m2[:, 7:8], scalar2=-1e4, op0=ALU.is_lt, op1=ALU.mult),
                     reads=[sc, m2], pwrites=[negq])
                yield
                kts = list(range(max(0, qb - 4), qb + 1))

                def w_qk(i, kt, g=g, q_ap=q_ap, q=q, qb=qb):
                    E = qk_exp(kwT[:, g, kt * 128:(kt + 1) * 128], kwT, q_ap, q)
                    if kt == qb - 4:
                        mask(E, -1, 1, -1)
                    if kt == qb:
                        mask(E, 0, -1, 1)
                    return E

                def w_pv(i, kt, E, g=g, kts=kts):
                    for h in range(4):
                        first = (i == 0 and h == 0)
                        S.op("pe", lambda e, E=E, h=h, kt=kt, first=first, last=(i == len(kts) - 1 and h == 3), g=g: e.matmul(
                            pow_[:, h, :], lhsT=E[:, h * 128:(h + 1) * 128], rhs=vw[:, kt, g, :], start=first, stop=last,
                            skip_group_check=True),
                            reads=[E, vw], writes=[pow_] if first else (), pwrites=() if first else [pow_])
                pipeline([(lambda i=i, kt=kt: w_qk(i, kt), lambda E, i=i, kt=kt: w_pv(i, kt, E)) for i, kt in enumerate(kts)])
                S.op("act", lambda e: e.copy(out=pws[:], in_=pow_[:]), reads=[pow_], writes=[pws])
                yield
                na_here = min(NA, qb * 128 // HALF + 1)
                S.op("pe", lambda e: e.transpose(out=pmisc[:, 1, :], in_=negq[:, 1, :], identity=ident[:]), reads=[negq, ident], writes=[pmisc])
                if na_here > 1:
                    S.op("pe", lambda e: e.transpose(out=pmisc[:, 0, :], in_=negq[:, 0, :], identity=ident[:]), reads=[negq, ident], pwrites=[pmisc])
                for a_ in range(na_here):
                    qa = qAs[(g, a_)]
                    S.op("dve", lambda e, qa=qa, a_=a_: e.tensor_copy(out=qa[64:128, :, :],
                                                                   in_=pmisc[64:128, (1 - a_):(2 - a_), :].to_broadcast([64, 4, 128])),
                         reads=[pmisc], pwrites=[qa])
                yield
                def s_qk(kt, g=g, q_ap=q_ap, q=q, qb=qb, qAs=qAs):
                    qa = qAs[(g, kt * 128 // HALF)]
                    E = qk_exp(ksT[:, g, kt * 128:(kt + 1) * 128], ksT, qa[:].rearrange("p h q -> p (h q)"), qa)
                    if kt == qb:
                        mask(E, 0, -1, 1)
                    return E

                def s_pv(kt, E, g=g, qb=qb):
                    for h in range(4):
                        first = (kt == 0 and h == 0)
                        S.op("pe", lambda e, E=E, h=h, kt=kt, first=first, last=(kt == qb and h == 3), g=g: e.matmul(
                            pos[:, h, :], lhsT=E[:, h * 128:(h + 1) * 128], rhs=vs[:, kt, g, :], start=first, stop=last,
                            skip_group_check=True),
                            reads=[E, vs], writes=[pos] if first else (), pwrites=() if first else [pos])
                pipeline([(lambda kt=kt: s_qk(kt), lambda E, kt=kt: s_pv(kt, E)) for kt in range(qb + 1)])
                S.op("act", lambda e: e.copy(out=pss[:], in_=pos[:]), reads=[pos], writes=[pss])
                yield
                S.op("dve", lambda e: e.reciprocal(out=rss[:], in_=pss[:, :, 64]), reads=[pss], writes=[rss])
                S.op("dve", lambda e: e.reciprocal(out=rws[:], in_=pws[:, :, 64]), reads=[pws], writes=[rws])
                gv = gg[:, g * 12:(g + 1) * 12].rearrange("p (h b) -> p b h", b=3)
                for b, rr in enumerate((rcs, rss, rws)):
                    S.op("dve", lambda e, b=b, rr=rr, gv=gv: e.tensor_tensor(out=cc[:, b, :], in0=gv[:, b, :], in1=rr[:], op=ALU.mult),
                         reads=[gg, rr], pwrites=[cc])
                bc = lambda b: cc[:, b, :].unsqueeze(2).to_broadcast([128, 4, 64])
                S.op("dve", lambda e: e.tensor_tensor(out=yg[:], in0=oc[:, :, 0:64], in1=bc(0), op=ALU.mult), reads=[oc, cc], writes=[yg])
                S.op("dve", lambda e: e.tensor_tensor(out=ytmp[:], in0=pss[:, :, 0:64], in1=bc(1), op=ALU.mult), reads=[pss, cc], writes=[ytmp])
                S.op("pool", lambda e: e.tensor_tensor(out=yg[:], in0=yg[:], in1=ytmp[:], op=ALU.add), reads=[yg, ytmp], writes=[yg])
                S.op("dve", lambda e: e.tensor_tensor(out=ytmp[:], in0=pws[:, :, 0:64], in1=bc(2), op=ALU.mult), reads=[pws, cc], writes=[ytmp])
                S.op("pool", lambda e: e.tensor_tensor(out=yg[:], in0=yg[:], in1=ytmp[:], op=ALU.add), reads=[yg, ytmp], writes=[yg])
                S.op("pool", lambda e, g=g, nz=nz: e.tensor_tensor(out=ynsa[:, g * 256:(g + 1) * 256], in0=yg[:].rearrange("p h d -> p (h d)"),
                                                                   in1=nz[:, g * 256:(g + 1) * 256], op=ALU.mult),
                     reads=[yg, nz], pwrites=[ynsa])
                yield
            gens = [gbody(0), gbody(1)]
            for _st in range(5):
                for gen_ in gens:
                    next(gen_)
            for k in range(4):
                S.op("pe", lambda e, k=k: e.transpose(out=pmisc[:, k, :], in_=ynsa[:, k * 128:(k + 1) * 128], identity=ident[:]),
                     reads=[ynsa, ident], writes=[pmisc] if k == 0 else (), pwrites=() if k == 0 else [pmisc])
            sg = stg.next()
            S.op("act", lambda e, sg=sg: e.copy(out=sg[:], in_=pmisc[:]), reads=[pmisc], writes=[sg])
            S.dma("sp", scr["ysT"].t[0, :, q0:q0 + 128].rearrange("(k p) t -> p k t", p=128), sg[:], reads=[sg], pwrites=[scr["ysT"]], key=sg)
        _barrier(S)
        S.stack_pop()


def phase_D_seq(S, nc, SEQ, lyr, Wd, scr):
    TP = 128
    TB = 8
    GN_EPS = 64e-5
    xtok = scr["xtok"]
    with contextlib.ExitStack() as st:
        S.stack_push(st)
        identF = make_ident(S, "D_ident", F32)
        ones = S.sb("D_ones", [128, 128])
        S.op("pool", lambda e: e.memset(ones[:], 0.0), writes=[ones])
        S.op("pool", lambda e: e.memset(ones[0:64, 0:64], 1.0), reads=[ones], writes=[ones])
        S.op("pool", lambda e: e.memset(ones[64:128, 64:128], 1.0), reads=[ones], writes=[ones])

        def cvec(name, key, n):
            t = S.sb("D_" + name, [128, n])
            S.dma("sp", t[:], Wd[key].t[lyr].rearrange("(c p) -> p c", p=128), reads=[Wd[key]], writes=[t],
                  allow_slow_non_contiguous=True)
            return t

        def cvec2(name, key):
            t = S.sb("D_" + name, [128, 4])
            S.dma("sp", t[:], Wd[key].t[lyr].rearrange("(c a) j -> (a j) c", a=2), reads=[Wd[key]], writes=[t],
                  allow_slow_non_contiguous=True)
            return t
        mu = cvec("mu", "rk_mu", 13)
        w0 = cvec("w0", "rk_w0", 4)
        a0 = cvec("a0", "rk_a0", 4)
        lg = cvec("lg", "rk_lnx_g", 4)
        lb = cvec("lb", "rk_lnx_b", 4)
        kkc = cvec2("kkc", "rk_kk")
        ka = cvec2("ka", "rk_ka")
        rkc = cvec2("rkc", "rk_rk")
        omka = S.sb("D_omka", [128, 4])
        S.op("pool", lambda e: e.tensor_scalar(out=omka[:], in0=ka[:], scalar1=-1.0, scalar2=1.0, op0=ALU.mult, op1=ALU.add),
             reads=[ka], writes=[omka])
        w2 = S.sb("D_w2", [64, 512], BF16)
        a2 = S.sb("D_a2", [128, 512], BF16)
        S.dma("pool", w2[:], Wd["rk_w2"].t[lyr], reads=[Wd["rk_w2"]], writes=[w2])
        S.dma("pool", a2[64:128, :], Wd["rk_a2"].t[lyr], reads=[Wd["rk_a2"]], writes=[a2])
        St = S.sb("D_state", [128, 4, 64])
        S.op("dve", lambda e: e.memset(St[:], 0.0), writes=[St])

        rst = S.sb("D_rst", [128, 13, TP + 1])
        xs = S.sb("D_xs", [128, 13, TP])
        th = S.sb("D_th", [128, TP], BF16)
        dd = S.sb("D_dd", [128, 4, TP])
        aa = S.sb("D_aa", [128, 4, TP])
        kkf = S.sb("D_kkf", [128, 4, TP])
        sq = S.sb("D_sq", [128, 4, TP])
        rn = S.sb("D_rn", [128, 4, TP])
        kp = S.sb("D_kp", [128, 4, TP])
        am = S.sb("D_am", [128, 4, TP])
        bm = S.sb("D_bm", [128, 4, TP])
        t1 = S.sb("D_t1", [128, 4, TP])
        bonus = S.sb("D_bonus", [128, 4, TP])
        vv = S.sb("D_vv", [128, 4, TP])
        tk = S.sb("D_tk", [128, 5, 4, 128])
        bcr = Ring([S.sb("D_bc%d" % i, [128, TB, 5, 256]) for i in range(2)])
        tmp = S.sb("D_tmp", [128, 4, 64])
        tmp2 = S.sb("D_tmp2", [128, 4, 64])
        kv = Ring([S.sb("D_kv%d" % i, [128, 4, 64]) for i in range(2)])
        sa = S.sb("D_sa", [128, 4])
        ybuf = S.sb("D_y", [128, 4, TP])
        ysq = S.sb("D_ysq", [128, 4, TP])
        mean = S.sb("D_mean", [128, 4, TP])
        var = S.sb("D_var", [128, 4, TP])
        rzt = S.sb("D_rz", [128, 4, TP], BF16)
        yo = S.sb("D_yo", [128, 4, TP], BF16)
        pa = Ring([S.ps("D_pa%d" % i, [128, 4, 128]) for i in range(4)])

        bc4 = lambda t: t[:].unsqueeze(2).to_broadcast([128, 4, TP])
        for nb in range(SEQ // TP):
            t0 = nb * TP
            S.dma("sp", rst[:, :, 1:TP + 1], scr["rsT"].t[:, t0:t0 + TP].rearrange("(c p) t -> p c t", p=128), reads=[scr["rsT"]],
                  writes=[rst])
            if nb == 0:
                S.op("pool", lambda e: e.memset(rst[:, :, 0:1], 0.0), reads=[rst], pwrites=[rst])
            else:
                S.dma("sp", rst[:, :, 0:1], scr["rsT"].t[:, t0 - 1:t0].rearrange("(c p) t -> p c t", p=128), reads=[scr["rsT"]],
                      pwrites=[rst], key=rst, allow_slow_non_contiguous=True)
            S.op("pool", lambda e: e.tensor_tensor(out=xs[:], in0=rst[:, :, 0:TP], in1=rst[:, :, 1:TP + 1], op=ALU.subtract),
                 reads=[rst], writes=[xs])
            S.op("pool", lambda e: e.tensor_tensor(out=xs[:], in0=xs[:], in1=mu[:].unsqueeze(2).to_broadcast([128, 13, TP]), op=ALU.mult),
                 reads=[xs, mu], writes=[xs])
            S.op("pool", lambda e: e.tensor_tensor(out=xs[:], in0=xs[:], in1=rst[:, :, 1:TP + 1], op=ALU.add), reads=[xs, rst], writes=[xs])
            r = xs[:, 0:4, :]; k = xs[:, 4:8, :]; v = xs[:, 8:12, :]
            S.op("act", lambda e: e.activation(out=th[0:64, :], in_=xs[0:64, 12, :], func=AF.Tanh), reads=[xs], pwrites=[th])
            S.op("act", lambda e: e.copy(out=th[64:128, :], in_=xs[64:128, 12, :]), reads=[xs], pwrites=[th])
            pw = pa.next(); pp = pa.next()
            for p in range(4):
                S.op("pe", lambda e, p=p, pw=pw: e.matmul(pw[:, p, :], lhsT=w2[0:64, p * 128:(p + 1) * 128], rhs=th[0:64, :], start=True, stop=True),
                     reads=[w2, th], writes=[pw] if p == 0 else (), pwrites=() if p == 0 else [pw])
                S.op("pe", lambda e, p=p, pp=pp: e.matmul(pp[:, p, :], lhsT=a2[64:128, p * 128:(p + 1) * 128], rhs=th[64:128, :], start=True, stop=True),
                     reads=[a2, th], writes=[pp] if p == 0 else (), pwrites=() if p == 0 else [pp])
            for p in range(4):
                S.op("act", lambda e, p=p, pw=pw: e.activation(out=dd[:, p, :], in_=pw[:, p, :], func=AF.Sigmoid, bias=w0[:, p:p + 1]),
                     reads=[pw, w0], pwrites=[dd])
                S.op("act", lambda e, p=p, pp=pp: e.activation(out=aa[:, p, :], in_=pp[:, p, :], func=AF.Sigmoid, bias=a0[:, p:p + 1]),
                     reads=[pp, a0], pwrites=[aa])
            S.op("act", lambda e: e.activation(out=dd[:], in_=dd[:], func=AF.Exp, scale=-0.6065306597126334), reads=[dd], writes=[dd])
            S.op("pool", lambda e: e.tensor_tensor(out=kkf[:], in0=k, in1=bc4(kkc), op=ALU.mult), reads=[xs, kkc], writes=[kkf])
            S.op("pool", lambda e: e.tensor_tensor(out=sq[:], in0=kkf[:], in1=kkf[:], op=ALU.mult), reads=[kkf], writes=[sq])
            pn = pa.next()
            for p in range(4):
                S.op("pe", lambda e, p=p, pn=pn: e.matmul(pn[:, p, :], lhsT=ones[:], rhs=sq[:, p, :], start=True, stop=True),
                     reads=[ones, sq], writes=[pn] if p == 0 else (), pwrites=() if p == 0 else [pn])
            S.op("act", lambda e, pn=pn: e.activation(out=rn[:], in_=pn[:], func=AF.Sqrt), reads=[pn], writes=[rn])
            S.op("pool", lambda e: e.tensor_scalar(out=rn[:], in0=rn[:], scalar1=1e-12, scalar2=None, op0=ALU.max), reads=[rn], writes=[rn])
            S.op("dve", lambda e: e.reciprocal(out=rn[:], in_=rn[:]), reads=[rn], writes=[rn])
            S.op("pool", lambda e: e.tensor_tensor(out=kkf[:], in0=kkf[:], in1=rn[:], op=ALU.mult), reads=[kkf, rn], writes=[kkf])
            S.op("pool", lambda e: e.tensor_tensor(out=t1[:], in0=aa[:], in1=bc4(ka), op=ALU.mult), reads=[aa, ka], writes=[t1])
            S.op("pool", lambda e: e.tensor_tensor(out=t1[:], in0=t1[:], in1=bc4(omka), op=ALU.add), reads=[t1, omka], writes=[t1])
            S.op("pool", lambda e: e.tensor_tensor(out=kp[:], in0=k, in1=t1[:], op=ALU.mult), reads=[xs, t1], writes=[kp])
            S.op("pool", lambda e: e.tensor_scalar(out=am[:], in0=kkf[:], scalar1=-1.0, scalar2=None, op0=ALU.mult), reads=[kkf], writes=[am])
            S.op("pool", lambda e: e.tensor_tensor(out=bm[:], in0=kkf[:], in1=aa[:], op=ALU.mult), reads=[kkf, aa], writes=[bm])
            S.op("pool", lambda e: e.tensor_tensor(out=t1[:], in0=r, in1=kp[:], op=ALU.mult), reads=[xs, kp], writes=[t1])
            S.op("pool", lambda e: e.tensor_tensor(out=sq[:], in0=t1[:], in1=bc4(rkc), op=ALU.mult), reads=[t1, rkc], writes=[sq])
            pr = pa.next()
            for p in range(4):
                S.op("pe", lambda e, p=p, pr=pr: e.matmul(pr[:, p, :], lhsT=ones[:], rhs=sq[:, p, :], start=True, stop=True),
                     reads=[ones, sq], writes=[pr] if p == 0 else (), pwrites=() if p == 0 else [pr])
            S.op("act", lambda e, pr=pr: e.copy(out=bonus[:], in_=pr[:]), reads=[pr], writes=[bonus])
            S.op("pool", lambda e: e.tensor_tensor(out=bonus[:], in0=bonus[:], in1=v, op=ALU.mult), reads=[bonus, xs], writes=[bonus])
            S.op("pool", lambda e: e.tensor_copy(out=vv[:], in_=v), reads=[xs], writes=[vv])
            S.op("pool", lambda e: e.tensor_copy(out=t1[:], in_=r), reads=[xs], writes=[t1])
            for oi, src in enumerate((am, bm, dd, kp, t1)):
                pt = pa.next()
                for p in range(4):
                    S.op("pe", lambda e, p=p, src=src, pt=pt: e.transpose(out=pt[:, p, :], in_=src[:, p, :], identity=identF[:]),
                         reads=[src, identF], writes=[pt] if p == 0 else (), pwrites=() if p == 0 else [pt])
                S.op("act", lambda e, oi=oi, pt=pt: e.copy(out=tk[:, oi, :, :], in_=pt[:]), reads=[pt], pwrites=[tk])
            for oi in range(5):
                for h2 in range(2):
                    S.dma("sp", xtok.t[t0:t0 + TP, oi, h2, :].rearrange("t (p j) -> t p j", p=4), tk[:, oi, :, h2 * 64:(h2 + 1) * 64],
                          reads=[tk], pwrites=[xtok], key=tk)
            S.dma("sp", rzt[:], scr["rzT"].t[:, t0:t0 + TP].rearrange("(c p) t -> p c t", p=128), reads=[scr["rzT"]], writes=[rzt])
            xflat = xtok.t.rearrange("t o h c -> (t o) h c")
            for tb in range(0, TP, TB):
                bc = bcr.next()
                for h2 in range(2):
                    S.dma("sp", bc[h2 * 64:(h2 + 1) * 64, :, :, :].rearrange("p t o c -> p (t o) c"),
                          xflat[(t0 + tb) * 5:(t0 + tb + TB) * 5, h2, :].partition_broadcast(64),
                          reads=[xtok], writes=[bc] if h2 == 0 else (), pwrites=() if h2 == 0 else [bc], key=bc)
                for tt in range(TB):
                    t = tb + tt
                    A = bc[:, tt, 0, :].rearrange("p (a j) -> p a j", a=4)
                    B = bc[:, tt, 1, :].rearrange("p (a j) -> p a j", a=4)
                    Dd = bc[:, tt, 2, :].rearrange("p (a j) -> p a j", a=4)
                    Kk = bc[:, tt, 3, :].rearrange("p (a j) -> p a j", a=4)
                    R = bc[:, tt, 4, :].rearrange("p (a j) -> p a j", a=4)
                    kvb = kv.next()
                    S.op("pool", lambda e, Kk=Kk, t=t, kvb=kvb: e.tensor_tensor(out=kvb[:], in0=Kk, in1=vv[:, :, t:t + 1].to_broadcast([128, 4, 64]),
                                                                              op=ALU.mult), reads=[bc, vv], writes=[kvb])
                    S.op("dve", lambda e, A=A: e.tensor_tensor(out=tmp[:], in0=St[:], in1=A, op=ALU.mult), reads=[St, bc], writes=[tmp])
                    S.op("dve", lambda e: e.tensor_reduce(out=sa[:], in_=tmp[:], axis=AX.X, op=ALU.add), reads=[tmp], writes=[sa])
                    S.op("dve", lambda e, Dd=Dd: e.tensor_tensor(out=St[:], in0=St[:], in1=Dd, op=ALU.mult), reads=[St, bc, tmp], writes=[St])
                    S.op("dve", lambda e, B=B: e.tensor_tensor(out=tmp2[:], in0=B, in1=sa[:].unsqueeze(2).to_broadcast([128, 4, 64]), op=ALU.mult),
                         reads=[bc, sa], writes=[tmp2])
                    S.op("dve", lambda e: e.tensor_tensor(out=St[:], in0=St[:], in1=tmp2[:], op=ALU.add), reads=[St, tmp2], writes=[St])
                    S.op("dve", lambda e, kvb=kvb: e.tensor_tensor(out=St[:], in0=St[:], in1=kvb[:], op=ALU.add), reads=[St, kvb], writes=[St])
                    S.op("dve", lambda e, R=R: e.tensor_tensor(out=tmp[:], in0=St[:], in1=R, op=ALU.mult), reads=[St, bc], writes=[tmp])
                    S.op("dve", lambda e, t=t: e.tensor_reduce(out=ybuf[:, :, t], in_=tmp[:], axis=AX.X, op=ALU.add), reads=[tmp], pwrites=[ybuf])
            S.op("pool", lambda e: e.tensor_tensor(out=ysq[:], in0=ybuf[:], in1=ybuf[:], op=ALU.mult), reads=[ybuf], writes=[ysq])
            pm = pa.next(); pq = pa.next()
            for p in range(4):
                S.op("pe", lambda e, p=p, pm=pm: e.matmul(pm[:, p, :], lhsT=ones[:], rhs=ybuf[:, p, :], start=True, stop=True),
                     reads=[ones, ybuf], writes=[pm] if p == 0 else (), pwrites=() if p == 0 else [pm])
                S.op("pe", lambda e, p=p, pq=pq: e.matmul(pq[:, p, :], lhsT=ones[:], rhs=ysq[:, p, :], start=True, stop=True),
                     reads=[ones, ysq], writes=[pq] if p == 0 else (), pwrites=() if p == 0 else [pq])
            S.op("act", lambda e, pm=pm: e.activation(out=mean[:], in_=pm[:], func=AF.Copy, scale=1.0 / 64), reads=[pm], writes=[mean])
            S.op("act", lambda e, pq=pq: e.activation(out=var[:], in_=pq[:], func=AF.Copy, scale=1.0 / 64), reads=[pq], writes=[var])
            S.op("pool", lambda e: e.tensor_tensor(out=ysq[:], in0=mean[:], in1=mean[:], op=ALU.mult), reads=[mean, ysq], writes=[ysq])
            S.op("pool", lambda e: e.tensor_tensor(out=var[:], in0=var[:], in1=ysq[:], op=ALU.subtract), reads=[var, ysq], writes=[var])
            S.op("act", lambda e: e.activation(out=var[:], in_=var[:], func=AF.Sqrt, bias=GN_EPS, scale=1.0), reads=[var], writes=[var])
            S.op("dve", lambda e: e.reciprocal(out=var[:], in_=var[:]), reads=[var], writes=[var])
            S.op("pool", lambda e: e.tensor_tensor(out=mean[:], in0=ybuf[:], in1=mean[:], op=ALU.subtract), reads=[ybuf, mean], writes=[mean])
            S.op("pool", lambda e: e.tensor_tensor(out=mean[:], in0=mean[:], in1=var[:], op=ALU.mult), reads=[mean, var], writes=[mean])
            S.op("pool", lambda e: e.tensor_tensor(out=mean[:], in0=mean[:], in1=bc4(lg), op=ALU.mult), reads=[mean, lg], writes=[mean])
            S.op("pool", lambda e: e.tensor_tensor(out=mean[:], in0=mean[:], in1=bc4(lb), op=ALU.add), reads=[mean, lb], writes=[mean])
            S.op("pool", lambda e: e.tensor_tensor(out=mean[:], in0=mean[:], in1=bonus[:], op=ALU.add), reads=[mean, bonus], writes=[mean])
            S.op("pool", lambda e: e.tensor_tensor(out=yo[:], in0=mean[:], in1=rzt[:], op=ALU.mult), reads=[mean, rzt], writes=[yo])
            S.dma("sp", scr["ysT"].t[2, :, t0:t0 + TP].rearrange("(c p) t -> p c t", p=128), yo[:], reads=[yo], pwrites=[scr["ysT"]], key=yo)
        _barrier(S)
        S.stack_pop()


_NC_CACHE = {}


def kernel(**inputs):
    SEQ = 8192
    if "nc" not in _NC_CACHE:
        _NC_CACHE["nc"] = build(SEQ, nlayers=2, enable=(1, 1, 1), scr_kind="Internal")
    nc = _NC_CACHE["nc"]
    x = np.ascontiguousarray(np.asarray(inputs["x"], dtype=np.float32))
    p = np.asarray(inputs["p"], dtype=np.float32)
    base = {}
    for k in WSPEC:
        v = np.ascontiguousarray(np.asarray(inputs[k], dtype=np.float32))
        base[k] = v.reshape(WSPEC[k])
    in_maps = []
    for b in range(8):
        m = dict(base)
        m["x"] = np.ascontiguousarray(x[b])
        m["p"] = np.ascontiguousarray(p[:, b])
        in_maps.append(m)
    res = run_bass_kernel_spmd(nc, in_maps, core_ids=list(range(8)))
    return np.stack([np.asarray(r["out"], dtype=np.float32) for r in res.results], axis=0)
```

```python
import contextlib
import numpy as np
import concourse.bass as bass
import concourse.mybir as mybir

F32 = mybir.dt.float32
BF16 = mybir.dt.bfloat16
AF = mybir.ActivationFunctionType
ALU = mybir.AluOpType
AX = mybir.AxisListType

ENGS = ("pe", "act", "dve", "pool", "sp")


class Buf:
    __slots__ = ("name", "w", "wfull", "r", "t")

    def __init__(self, name, t=None):
        self.name = name
        self.t = t
        self.w = []
        self.wfull = []
        self.r = []

    def __getitem__(self, k):
        return self.t[k]


class Op:
    __slots__ = ("eng", "fn", "deps", "marked", "tick", "dma", "idx")

    def __init__(self, eng, fn, dma):
        self.eng = eng
        self.fn = fn
        self.deps = []
        self.marked = False
        self.tick = None
        self.dma = dma
        self.idx = None


class DmaSem:
    def __init__(self):
        self.sem = None
        self.count = 0


class Sched:
    def __init__(self, nc, stack):
        self.nc = nc
        self.stack = stack
        self.ops = {e: [] for e in ENGS}
        self.all_ops = []
        self.dsems = {}
        self.n_sems = 0
        self.fence = []
        self.stacks = [stack]
        self.phase_keys = []
        self.free_ds = []
        self.all_ds = []
        self.keep = []

    def stack_push(self, st):
        self.stacks.append(st)
        self.phase_keys.append([])

    def stack_pop(self):
        self.stacks.pop()
        for kid in self.phase_keys.pop():
            ds = self.dsems.pop(kid, None)
            if ds is not None:
                self.free_ds.append(ds)

    def sb(self, name, shape, dt=F32):
        self.n_sems += 1
        name = "%s_u%d" % (name, self.n_sems)
        t = self.stacks[-1].enter_context(self.nc.sbuf_tensor(name, list(shape), dt))
        return Buf(name, t)

    def ps(self, name, shape, dt=F32):
        self.n_sems += 1
        name = "%s_u%d" % (name, self.n_sems)
        t = self.stacks[-1].enter_context(self.nc.psum_tensor(name, list(shape), dt))
        return Buf(name, t)

    def dram(self, name, shape, dt, kind="Internal"):
        t = self.nc.dram_tensor(name, list(shape), dt, kind=kind)
        return Buf(name, t.ap())

    def _add(self, eng, fn, reads, writes, pwrites, dma):
        op = Op(eng, fn, dma)
        deps = list(self.fence)
        for b in reads:
            deps.extend(b.w)
        for b in writes:
            deps.extend(b.w)
            deps.extend(b.r)
        for b in pwrites:
            deps.extend(b.wfull)
            deps.extend(b.r)
        seen = set()
        for d in deps:
            if id(d) in seen or d is op:
                continue
            seen.add(id(d))
            if d.eng == "pe" and eng == "pe" and d.dma is None and dma is None:
                continue
            op.deps.append(d)
            d.marked = True
        for b in reads:
            b.r.append(op)
            if len(b.r) > 24:
                b.r = self._prune(b.r)
        for b in writes:
            b.w = [op]
            b.wfull = [op]
            b.r = []
        for b in pwrites:
            b.w.append(op)
            if len(b.w) > 24:
                b.w = self._prune(b.w)
        op.idx = len(self.all_ops)
        self.all_ops.append(op)
        self.ops[eng].append(op)
        return op

    @staticmethod
    def _prune(lst):
        last = {}
        for o in lst:
            key = (o.eng, None) if o.dma is None else ("dma", id(o.dma))
            last[key] = o
        return list(last.values())

    def op(self, eng, fn, reads=(), writes=(), pwrites=()):
        return self._add(eng, fn, reads, writes, pwrites, None)

    def dma(self, eng, out_ap, in_ap, reads=(), writes=(), pwrites=(), key=None, **kw):
        if key is None:
            key = (list(writes) + list(pwrites))[0]
        ds = self.dsems.get(id(key))
        if ds is None:
            if self.free_ds:
                ds = self.free_ds.pop()
            else:
                ds = DmaSem()
                self.all_ds.append(ds)
            self.dsems[id(key)] = ds
            self.keep.append(key)
            if self.phase_keys:
                self.phase_keys[-1].append(id(key))
        fn = lambda e, o=out_ap, i=in_ap, kw=kw: e.dma_start(out=o, in_=i, **kw)
        op = self._add(eng, fn, reads, writes, pwrites, ds)
        ds.count += 16
        op.tick = ds.count
        return op

    def barrier_bufs(self, bufs):
        pass

    def emit(self):
        nc = self.nc
        stack = self.stack
        esem = {}
        for e in ENGS:
            esem[e] = stack.enter_context(nc.semaphore("s_" + e))
        for ds in self.all_ds:
            ds.sem = stack.enter_context(nc.semaphore("d%d" % self.n_sems))
            self.n_sems += 1
        for e in ENGS:
            c = 0
            for o in self.ops[e]:
                if o.dma is None:
                    if o.marked:
                        c += 1
                        o.tick = c
        self.max_ticks = {e: max([o.tick or 0 for o in self.ops[e] if o.dma is None] + [0]) for e in ENGS}

        def evkey(d):
            if d.dma is not None:
                return ("d", id(d.dma)), d.dma.sem, d.tick
            return ("e", d.eng), esem[d.eng], d.tick

        def run(eng_name, eng):
            seen = {}
            for o in self.ops[eng_name]:
                waits = {}
                for d in o.deps:
                    k, sem, val = evkey(d)
                    if seen.get(k, 0) >= val:
                        continue
                    if k not in waits or waits[k][1] < val:
                        waits[k] = (sem, val)
                for k, (sem, val) in waits.items():
                    eng.wait_ge(sem, val)
                    seen[k] = val
                inst = o.fn(eng)
                if o.dma is not None:
                    inst.then_inc(o.dma.sem, 16)
                elif o.marked:
                    inst.then_inc(esem[eng_name], 1)
            if eng_name == "sp":
                for e2 in ENGS:
                    m = self.max_ticks[e2]
                    if m > 0:
                        eng.wait_ge(esem[e2], m)
                for ds in self.all_ds:
                    if ds.count:
                        eng.wait_ge(ds.sem, ds.count)

        block = stack.enter_context(nc.Block())

        @block.tensor
        def _(e):
            run("pe", e)

        @block.scalar
        def _(e):
            run("act", e)

        @block.vector
        def _(e):
            run("dve", e)

        @block.gpsimd
        def _(e):
            run("pool", e)

        @block.sync
        def _(e):
            run("sp", e)


from concourse.bass_utils import run_bass_kernel_spmd

D = 1024
NCOL = 8600
PLE = 256
EPS = 1e-6


DEBUG = {}
_dbg_n = [0]


def dbg_dump(S, name, ap, buf, shape, cond=True):
    if not DEBUG.get("on") or not cond:
        return
    _dbg_n[0] += 1
    t = S.stacks[-1].enter_context(S.nc.sbuf_tensor("dbgsb_%d" % _dbg_n[0], list(shape), F32))
    tb = Buf("dbgsb", t)
    d = S.dram("dbg_" + name, list(shape), F32, kind="ExternalOutput")
    S.op("act", lambda e: e.copy(out=t[:], in_=ap), reads=[buf], writes=[tb])
    S.dma("sp", d.t, t[:], reads=[tb], writes=[d], key=tb)


class Ring:
    def __init__(self, bufs):
        self.bufs = bufs
        self.i = 0

    def next(self):
        b = self.bufs[self.i % len(self.bufs)]
        self.i += 1
        return b


def _barrier(S):
    fence = []
    for e in ENGS:
        comp = [o for o in S.ops[e] if o.dma is None]
        if comp:
            fence.append(comp[-1])
    lastd = {}
    for o in S.all_ops:
        if o.dma is not None:
            lastd[id(o.dma)] = o
    fence.extend(lastd.values())
    S.fence = fence


def make_ident(S, name="ident", dt=BF16):
    ident = S.sb(name, [128, 128], dt)
    S.op("pool", lambda e: e.memset(ident[:], 0.0), writes=[ident])
    S.op("pool", lambda e: e.affine_select(out=ident[:], in_=ident[:], pattern=[[-1, 128]],
                                           compare_op=ALU.not_equal, fill=1.0, base=0,
                                           channel_multiplier=1), reads=[ident], writes=[ident])
    return ident


def load_w_bf16(S, dst, k, src_ap, srcbuf):
    S.dma("pool", dst, src_ap, reads=[srcbuf], pwrites=[k], key=k, max_dma_last_dim=4096)


def rmsnorm_tile(S, xt_ap, xt_buf, g_buf, h_ap, h_buf, sq, ss, rs, eps=EPS, extra_reads=()):
    S.op("act", lambda e: e.activation(out=sq[:], in_=xt_ap, func=AF.Square, accum_out=ss[:]),
         reads=[xt_buf] + list(extra_reads), writes=[sq, ss])
    S.op("act", lambda e: e.activation(out=rs[:], in_=ss[:], func=AF.Sqrt, scale=1.0 / D, bias=eps),
         reads=[ss], writes=[rs])
    S.op("dve", lambda e: e.reciprocal(out=rs[:], in_=rs[:]), reads=[rs], writes=[rs])
    S.op("dve", lambda e: e.scalar_tensor_tensor(out=h_ap, in0=xt_ap, scalar=rs[:, 0:1], in1=g_buf[:],
                                                 op0=ALU.mult, op1=ALU.mult),
         reads=[xt_buf, rs, g_buf], pwrites=[h_buf])


def phase_A(S, nc, SEQ, lyr, x_src, Wd, scr):
    TT = 512
    nsub = TT // 128
    with contextlib.ExitStack() as st:
        S.stack_push(st)
        wt = S.sb("A_w", [128, 8, NCOL], BF16)
        gt = S.sb("A_g", [128, D])
        ident = make_ident(S, "A_ident")
        xt = S.sb("A_x", [128, nsub, D])
        sq = S.sb("A_sq", [128, D], BF16)
        ss = S.sb("A_ss", [128, 1])
        rs = S.sb("A_rs", [128, 1])
        h = S.sb("A_h", [128, nsub, D], BF16)
        hT = S.sb("A_hT", [128, 8, TT], BF16)
        stg_b = Ring([S.sb("A_sb%d" % i, [128, 512], BF16) for i in range(4)])
        stg_f = Ring([S.sb("A_sf%d" % i, [128, 512], F32) for i in range(3)])
        pT = Ring([S.ps("A_pT%d" % i, [128, 8, 128], BF16) for i in range(2)])
        pacc = Ring([S.ps("A_pa%d" % i, [128, 512], F32) for i in range(6)])

        w_in = Wd["w_in"]
        for k in range(8):
            S.dma("pool", wt[:, k, :], w_in.t[lyr, k * 128:(k + 1) * 128, 0:NCOL], reads=[w_in], pwrites=[wt],
                  key=wt, max_dma_last_dim=4096)
        S.dma("sp", gt[:], Wd["norm_g"].t[lyr:lyr + 1, :].partition_broadcast(128), reads=[Wd["norm_g"]],
              writes=[gt])

        FM = []
        for c in range(4):
            FM.append((c * 128, scr["qT"], c * 128, AF.Copy, 0.125, BF16))
        FM.append((512, scr["kcT"], 0, None, 1.0, BF16))
        FM.append((640, scr["vcT"], 0, None, 1.0, BF16))
        FM.append((768, scr["ksT"], 0, None, 1.0, BF16))
        FM.append((1024, scr["kwT"], 0, None, 1.0, BF16))
        for c in range(13):
            FM.append((3352 + c * 128, scr["rsT"], c * 128, None, 1.0, F32))
        for c in range(4):
            FM.append((5016 + c * 128, scr["rzT"], c * 128, AF.Silu, 1.0, BF16))
        for c in range(24):
            FM.append((5528 + c * 128, scr["mgT"], c * 128, AF.Sigmoid, 1.0, BF16))
        TM = [
            (896, 128, scr["vsw"], 0, None, BF16),
            (1152, 128, scr["vsw"], 128, None, BF16),
            (1280, 24, scr["gate"], 0, AF.Sigmoid, F32),
            (1304, 512, scr["nzs"], 0, AF.Silu, BF16),
            (1816, 512, scr["su"], 0, None, F32),
            (2328, 512, scr["sv"], 0, None, F32),
            (2840, 512, scr["szs"], 0, AF.Silu, BF16),
        ]
        evac_i = [0]

        def evac(out_ap, out_buf, in_ap, in_buf, func, scale):
            if func is None and scale == 1.0:
                if evac_i[0] % 2 == 0:
                    S.op("dve", lambda e: e.tensor_copy(out=out_ap, in_=in_ap), reads=[in_buf], writes=[out_buf])
                else:
                    S.op("act", lambda e: e.copy(out=out_ap, in_=in_ap), reads=[in_buf], writes=[out_buf])
                evac_i[0] += 1
            else:
                S.op("act", lambda e: e.activation(out=out_ap, in_=in_ap, func=func, scale=scale),
                     reads=[in_buf], writes=[out_buf])

        for ti in range(SEQ // TT):
            t0 = ti * TT
            S.dma("sp", xt[:], x_src.t[t0:t0 + TT, :].rearrange("(s p) d -> p s d", p=128), reads=[x_src],
                  writes=[xt])
            for s in range(nsub):
                rmsnorm_tile(S, xt[:, s, :], xt, gt, h[:, s, :], h, sq, ss, rs)
                pt = pT.next()
                for k in range(8):
                    S.op("pe", lambda e, k=k, s=s, pt=pt: e.transpose(out=pt[:, k, :], in_=h[:, s, k * 128:(k + 1) * 128],
                                                                     identity=ident[:]),
                         reads=[h, ident], writes=[pt] if k == 0 else (), pwrites=() if k == 0 else [pt])
                S.op("dve", lambda e, s=s, pt=pt: e.tensor_copy(out=hT[:, :, s * 128:(s + 1) * 128], in_=pt[:]),
                     reads=[pt], pwrites=[hT])
            for (c0, dbuf, r0, func, scale, dt) in FM:
                pa = pacc.next()
                for k in range(8):
                    S.op("pe", lambda e, k=k, pa=pa, c0=c0: e.matmul(pa[:], lhsT=wt[:, k, c0:c0 + 128], rhs=hT[:, k, :],
                                                                    start=(k == 0), stop=(k == 7)),
                         reads=[wt, hT], writes=[pa] if k == 0 else (), pwrites=() if k == 0 else [pa])
                sg = stg_b.next() if dt == BF16 else stg_f.next()
                evac(sg[:], sg, pa[:], pa, func, scale)
                S.dma("sp", dbuf.t[r0:r0 + 128, t0:t0 + TT], sg[:], reads=[sg], pwrites=[dbuf], key=sg)
            for s in range(nsub):
                for (c0, ncol, dbuf, dc0, func, dt) in TM:
                    pa = pacc.next()
                    for k in range(8):
                        S.op("pe", lambda e, k=k, pa=pa, c0=c0, ncol=ncol, s=s: e.matmul(
                            pa[:, 0:ncol], lhsT=hT[:, k, s * 128:(s + 1) * 128], rhs=wt[:, k, c0:c0 + ncol],
                            start=(k == 0), stop=(k == 7)),
                            reads=[wt, hT], writes=[pa] if k == 0 else (), pwrites=() if k == 0 else [pa])
                    sg = stg_b.next() if dt == BF16 else stg_f.next()
                    evac(sg[:, 0:ncol], sg, pa[:, 0:ncol], pa, func, 1.0)
                    S.dma("sp", dbuf.t[t0 + s * 128:t0 + (s + 1) * 128, dc0:dc0 + ncol], sg[:, 0:ncol], reads=[sg],
                          pwrites=[dbuf], key=sg)
        _barrier(S)
        S.stack_pop()


def make_scratch(S, SEQ, kind="Internal"):
    scr = {}
    def mk(name, shape, dt):
        scr[name] = S.dram(name, shape, dt, kind=kind)
    mk("qT", [512, SEQ], BF16)
    mk("kcT", [128, SEQ], BF16)
    mk("vcT", [128, SEQ], BF16)
    mk("ksT", [128, SEQ], BF16)
    mk("kwT", [128, SEQ], BF16)
    mk("vsw", [SEQ, 256], BF16)
    mk("gate", [SEQ, 24], F32)
    mk("nzs", [SEQ, 512], BF16)
    mk("su", [SEQ, 512], F32)
    mk("sv", [SEQ, 512], F32)
    mk("szs", [SEQ, 512], BF16)
    mk("rsT", [1664, SEQ], F32)
    mk("rzT", [512, SEQ], BF16)
    mk("mgT", [3072, SEQ], BF16)
    mk("ysT", [3, 512, SEQ], BF16)
    mk("xtok", [SEQ, 5, 2, 256], F32)
    return scr


def phase_C(S, nc, SEQ, lyr, Wd, scr):
    LN_EPS = 1e-5
    with contextlib.ExitStack() as st:
        S.stack_push(st)
        ident = make_ident(S, "C_ident")
        wraw = S.sb("C_wraw", [128, 8, 128])
        wbf = S.sb("C_wbf", [128, 8, 128], BF16)
        WT = S.sb("C_WT", [128, 8, 128], BF16)
        bsT = S.sb("C_bsT", [128, 8])
        lng = S.sb("C_lng", [128, 512])
        lnb = S.sb("C_lnb", [128, 512])
        pw = S.ps("C_pw", [128, 8, 128], BF16)
        S.dma("sp", wraw[:], Wd["sg_w"].t[lyr].rearrange("g t s -> t g s"), reads=[Wd["sg_w"]], writes=[wraw])
        S.dma("sp", bsT[:], Wd["sg_b"].t[lyr].rearrange("g t -> t g"), reads=[Wd["sg_b"]], writes=[bsT],
              allow_slow_non_contiguous=True)
        S.dma("sp", lng[:], Wd["sg_ln_g"].t[lyr:lyr + 1, :].partition_broadcast(128), reads=[Wd["sg_ln_g"]], writes=[lng])
        S.dma("sp", lnb[:], Wd["sg_ln_b"].t[lyr:lyr + 1, :].partition_broadcast(128), reads=[Wd["sg_ln_b"]], writes=[lnb])
        S.op("pool", lambda e: e.affine_select(out=wraw[:], in_=wraw[:], pattern=[[0, 8], [-1, 128]],
                                               compare_op=ALU.is_ge, fill=0.0, base=0, channel_multiplier=1),
             reads=[wraw], writes=[wraw])
        S.op("dve", lambda e: e.tensor_copy(out=wbf[:], in_=wraw[:]), reads=[wraw], writes=[wbf])
        for g in range(8):
            S.op("pe", lambda e, g=g: e.transpose(out=pw[:, g, :], in_=wbf[:, g, :], identity=ident[:]),
                 reads=[wbf, ident], pwrites=[pw])
        S.op("dve", lambda e: e.tensor_copy(out=WT[:], in_=pw[:]), reads=[pw], writes=[WT])

        NB = 2
        svt = Ring([S.sb("C_sv%d" % i, [128, 512]) for i in range(NB)])
        sut = Ring([S.sb("C_su%d" % i, [128, 512]) for i in range(NB)])
        szt = Ring([S.sb("C_sz%d" % i, [128, 512], BF16) for i in range(NB)])
        stats = S.sb("C_stats", [128, 6])
        mv = S.sb("C_mv", [128, 2])
        rstd = S.sb("C_rstd", [128, 1])
        vn0 = S.sb("C_vnf", [128, 512])
        vn = Ring([S.sb("C_vn%d" % i, [128, 512], BF16) for i in range(2)])
        y0 = S.sb("C_y0", [128, 512])
        yb = Ring([S.sb("C_yb%d" % i, [128, 512], BF16) for i in range(2)])
        pm = Ring([S.ps("C_pm%d" % i, [128, 512]) for i in range(2)])
        pt = Ring([S.ps("C_pt%d" % i, [128, 4, 128], BF16) for i in range(2)])
        stg = Ring([S.sb("C_stg%d" % i, [128, 4, 512], BF16) for i in range(2)])
        ys = scr["ysT"]
        sgb = None
        for c in range(SEQ // 128):
            t0 = c * 128
            v = svt.next(); u = sut.next(); z = szt.next()
            S.dma("sp", v[:], scr["sv"].t[t0:t0 + 128, :], reads=[scr["sv"]], writes=[v])
            S.dma("sp", u[:], scr["su"].t[t0:t0 + 128, :], reads=[scr["su"]], writes=[u])
            S.dma("sp", z[:], scr["szs"].t[t0:t0 + 128, :], reads=[scr["szs"]], writes=[z])
            S.op("dve", lambda e, v=v: e.bn_stats(out=stats[:], in_=v[:]), reads=[v], writes=[stats])
            S.op("dve", lambda e: e.bn_aggr(out=mv[:], in_=stats[:]), reads=[stats], writes=[mv])
            S.op("act", lambda e: e.activation(out=rstd[:], in_=mv[:, 1:2], func=AF.Sqrt, bias=LN_EPS, scale=1.0),
                 reads=[mv], writes=[rstd])
            S.op("dve", lambda e: e.reciprocal(out=rstd[:], in_=rstd[:]), reads=[rstd], writes=[rstd])
            S.op("dve", lambda e, v=v: e.tensor_scalar(out=vn0[:], in0=v[:], scalar1=mv[:, 0:1], scalar2=rstd[:, 0:1],
                                                       op0=ALU.subtract, op1=ALU.mult),
                 reads=[v, mv, rstd], writes=[vn0])
            S.op("pool", lambda e: e.tensor_tensor(out=vn0[:], in0=vn0[:], in1=lng[:], op=ALU.mult),
                 reads=[vn0, lng], writes=[vn0])
            vb = vn.next()
            S.op("pool", lambda e, vb=vb: e.tensor_tensor(out=vb[:], in0=vn0[:], in1=lnb[:], op=ALU.add),
                 reads=[vn0, lnb], writes=[vb])
            pmm = pm.next()
            for g in range(8):
                S.op("pe", lambda e, g=g, vb=vb, pmm=pmm: e.matmul(pmm[:, g * 64:(g + 1) * 64], lhsT=WT[:, g, :],
                                                                   rhs=vb[:, g * 64:(g + 1) * 64], start=True, stop=True),
                     reads=[WT, vb], writes=[pmm] if g == 0 else (), pwrites=() if g == 0 else [pmm])
            S.op("dve", lambda e, pmm=pmm: e.tensor_tensor(
                out=y0[:].rearrange("p (g d) -> p g d", g=8), in0=pmm[:].rearrange("p (g d) -> p g d", g=8),
                in1=bsT[:].unsqueeze(2).to_broadcast([128, 8, 64]), op=ALU.add), reads=[pmm, bsT], writes=[y0])
            S.op("pool", lambda e, u=u: e.tensor_tensor(out=y0[:], in0=y0[:], in1=u[:], op=ALU.mult),
                 reads=[y0, u], writes=[y0])
            y = yb.next()
            S.op("dve", lambda e, y=y, z=z: e.tensor_tensor(out=y[:], in0=y0[:], in1=z[:], op=ALU.mult),
                 reads=[y0, z], writes=[y])
            ptt = pt.next()
            for k in range(4):
                S.op("pe", lambda e, k=k, y=y, ptt=ptt: e.transpose(out=ptt[:, k, :], in_=y[:, k * 128:(k + 1) * 128],
                                                                    identity=ident[:]),
                     reads=[y, ident], writes=[ptt] if k == 0 else (), pwrites=() if k == 0 else [ptt])
            if c % 4 == 0:
                sgb = stg.next()
            cc = c % 4
            S.op("act", lambda e, ptt=ptt, sgb=sgb, cc=cc: e.copy(out=sgb[:, :, cc * 128:(cc + 1) * 128], in_=ptt[:]),
                 reads=[ptt], writes=[sgb] if cc == 0 else (), pwrites=() if cc == 0 else [sgb])
            if cc == 3 or c == SEQ // 128 - 1:
                tb = (c // 4) * 512
                n = (cc + 1) * 128
                S.dma("sp", ys.t[1, :, tb:tb + n].rearrange("(k p) t -> p k t", p=128), sgb[:, :, 0:n], reads=[sgb],
                      pwrites=[ys], key=sgb)
        _barrier(S)
        S.stack_pop()


def phase_E(S, nc, SEQ, lyr, x_src, x_dst, Wd, scr, final):
    TT = 512
    nsub = 4
    with contextlib.ExitStack() as st:
        S.stack_push(st)
        ident = make_ident(S, "E_ident")
        wb = S.sb("E_wb", [128, 3, 4, D], BF16)
        wo = S.sb("E_wo", [128, 8, D], BF16)
        wpg = S.sb("E_wpg", [128, 8, D], BF16)
        wpp = S.sb("E_wpp", [128, 2, D], BF16)
        gpl = S.sb("E_gpl", [128, D])
        gfin = S.sb("E_gfin", [128, D])
        for n in range(3):
            S.dma("pool", wb[:, n, :, :], Wd["w_branch"].t[lyr, n].rearrange("(k p) d -> p k d", p=128),
                  reads=[Wd["w_branch"]], pwrites=[wb], key=wb, max_dma_last_dim=4096)
        for k0 in range(0, 8, 4):
            S.dma("pool", wo[:, k0:k0 + 4, :], Wd["w_o"].t[lyr, k0 * 128:(k0 + 4) * 128, :].rearrange("(k p) d -> p k d", p=128),
                  reads=[Wd["w_o"]], pwrites=[wo], key=wo, max_dma_last_dim=4096)
            S.dma("pool", wpg[:, k0:k0 + 4, :], Wd["w_ple_gate"].t[lyr, k0 * 128:(k0 + 4) * 128, :].rearrange("(k p) d -> p k d", p=128),
                  reads=[Wd["w_ple_gate"]], pwrites=[wpg], key=wpg, max_dma_last_dim=4096)
        S.dma("pool", wpp[:], Wd["w_ple_proj"].t[lyr].rearrange("(k p) d -> p k d", p=128),
              reads=[Wd["w_ple_proj"]], pwrites=[wpp], key=wpp, max_dma_last_dim=4096)
        S.dma("sp", gpl[:], Wd["ple_norm_g"].t[lyr:lyr + 1, :].partition_broadcast(128), reads=[Wd["ple_norm_g"]], writes=[gpl])
        if final:
            S.dma("sp", gfin[:], Wd["final_norm_g"].t[0:1, :].partition_broadcast(128), reads=[Wd["final_norm_g"]], writes=[gfin])

        yst = S.sb("E_ys", [128, 3, 4, TT], BF16)
        mgt = S.sb("E_mg", [128, 24, TT], BF16)
        mrg = S.sb("E_mrg", [128, 8, TT])
        mrb = S.sb("E_mrb", [128, 8, TT], BF16)
        tmp = Ring([S.sb("E_tmp%d" % i, [128, TT]) for i in range(2)])
        xt = S.sb("E_x", [128, nsub, D])
        pin = S.sb("E_p", [128, nsub, PLE])
        pbf = S.sb("E_pbf", [128, PLE], BF16)
        pTs = S.sb("E_pT", [128, 2, 128], BF16)
        sq = S.sb("E_sq", [128, D], BF16)
        ss = S.sb("E_ss", [128, 1])
        rs = S.sb("E_rs", [128, 1])
        hp = S.sb("E_hp", [128, D], BF16)
        hpT = S.sb("E_hpT", [128, 8, 128], BF16)
        gate = S.sb("E_gate", [128, D])
        xo = Ring([S.sb("E_xo%d" % i, [128, D]) for i in range(2)])
        pz = Ring([S.ps("E_pz%d" % i, [128, TT]) for i in range(3)])
        po = Ring([S.ps("E_po%d" % i, [128, 512]) for i in range(2)])
        pg = Ring([S.ps("E_pg%d" % i, [128, 512]) for i in range(2)])
        ptr = S.ps("E_ptr", [128, 8, 128], BF16)

        for ti in range(SEQ // TT):
            t0 = ti * TT
            for n in range(3):
                S.dma("sp", yst[:, n, :, :], scr["ysT"].t[n, :, t0:t0 + TT].rearrange("(k p) t -> p k t", p=128),
                      reads=[scr["ysT"]], writes=[yst] if n == 0 else (), pwrites=() if n == 0 else [yst], key=yst)
            for k0 in range(0, 24, 8):
                S.dma("sp", mgt[:, k0:k0 + 8, :], scr["mgT"].t[k0 * 128:(k0 + 8) * 128, t0:t0 + TT].rearrange("(k p) t -> p k t", p=128),
                      reads=[scr["mgT"]], writes=[mgt] if k0 == 0 else (), pwrites=() if k0 == 0 else [mgt], key=mgt)
            S.dma("sp", xt[:], x_src.t[t0:t0 + TT, :].rearrange("(s p) d -> p s d", p=128), reads=[x_src], writes=[xt])
            S.dma("sp", pin[:], Wd["p"].t[lyr, t0:t0 + TT, :].rearrange("(s p) d -> p s d", p=128), reads=[Wd["p"]], writes=[pin])
            for dc in range(8):
                pzs = []
                for n in range(3):
                    pzz = pz.next()
                    pzs.append(pzz)
                    for k in range(4):
                        S.op("pe", lambda e, n=n, k=k, dc=dc, pzz=pzz: e.matmul(
                            pzz[:], lhsT=wb[:, n, k, dc * 128:(dc + 1) * 128], rhs=yst[:, n, k, :], start=(k == 0), stop=(k == 3)),
                            reads=[wb, yst], writes=[pzz] if k == 0 else (), pwrites=() if k == 0 else [pzz])
                S.op("dve", lambda e, dc=dc, p0=pzs[0]: e.tensor_tensor(out=mrg[:, dc, :], in0=p0[:], in1=mgt[:, dc, :], op=ALU.mult),
                     reads=[pzs[0], mgt], pwrites=[mrg])
                t1 = tmp.next()
                S.op("dve", lambda e, dc=dc, p1=pzs[1], t1=t1: e.tensor_tensor(out=t1[:], in0=p1[:], in1=mgt[:, 8 + dc, :], op=ALU.mult),
                     reads=[pzs[1], mgt], writes=[t1])
                t2 = tmp.next()
                S.op("dve", lambda e, dc=dc, p2=pzs[2], t2=t2: e.tensor_tensor(out=t2[:], in0=p2[:], in1=mgt[:, 16 + dc, :], op=ALU.mult),
                     reads=[pzs[2], mgt], writes=[t2])
                S.op("pool", lambda e, dc=dc, t1=t1: e.tensor_tensor(out=mrg[:, dc, :], in0=mrg[:, dc, :], in1=t1[:], op=ALU.add),
                     reads=[mrg, t1], pwrites=[mrg])
                S.op("pool", lambda e, dc=dc, t2=t2: e.tensor_tensor(out=mrb[:, dc, :], in0=mrg[:, dc, :], in1=t2[:], op=ALU.add),
                     reads=[mrg, t2], pwrites=[mrb])
            for s in range(nsub):
                for blk in range(2):
                    pp = po.next()
                    for k in range(8):
                        S.op("pe", lambda e, k=k, s=s, blk=blk, pp=pp: e.matmul(
                            pp[:], lhsT=mrb[:, k, s * 128:(s + 1) * 128], rhs=wo[:, k, blk * 512:(blk + 1) * 512],
                            start=(k == 0), stop=(k == 7)),
                            reads=[mrb, wo], writes=[pp] if k == 0 else (), pwrites=() if k == 0 else [pp])
                    S.op("dve", lambda e, s=s, blk=blk, pp=pp: e.tensor_tensor(
                        out=xt[:, s, blk * 512:(blk + 1) * 512], in0=pp[:], in1=xt[:, s, blk * 512:(blk + 1) * 512], op=ALU.add),
                        reads=[pp, xt], pwrites=[xt])
                rmsnorm_tile(S, xt[:, s, :], xt, gpl, hp[:], hp, sq, ss, rs)
                for k in range(8):
                    S.op("pe", lambda e, k=k: e.transpose(out=ptr[:, k, :], in_=hp[:, k * 128:(k + 1) * 128], identity=ident[:]),
                         reads=[hp, ident], writes=[ptr] if k == 0 else (), pwrites=() if k == 0 else [ptr])
                S.op("act", lambda e: e.copy(out=hpT[:], in_=ptr[:]), reads=[ptr], writes=[hpT])
                S.op("pool", lambda e, s=s: e.tensor_copy(out=pbf[:], in_=pin[:, s, :]), reads=[pin], writes=[pbf])
                for k in range(2):
                    S.op("pe", lambda e, k=k: e.transpose(out=ptr[:, k, :], in_=pbf[:, k * 128:(k + 1) * 128], identity=ident[:]),
                         reads=[pbf, ident, hpT], writes=[ptr] if k == 0 else (), pwrites=() if k == 0 else [ptr])
                S.op("act", lambda e: e.copy(out=pTs[:], in_=ptr[:, 0:2, :]), reads=[ptr], writes=[pTs])
                xout = xo.next()
                for blk in range(2):
                    pgg = pg.next()
                    for k in range(8):
                        S.op("pe", lambda e, k=k, blk=blk, pgg=pgg: e.matmul(
                            pgg[:], lhsT=hpT[:, k, :], rhs=wpg[:, k, blk * 512:(blk + 1) * 512], start=(k == 0), stop=(k == 7)),
                            reads=[hpT, wpg], writes=[pgg] if k == 0 else (), pwrites=() if k == 0 else [pgg])
                    S.op("act", lambda e, blk=blk, pgg=pgg: e.activation(out=gate[:, blk * 512:(blk + 1) * 512], in_=pgg[:], func=AF.Sigmoid),
                         reads=[pgg], pwrites=[gate])
                    ppp = pg.next()
                    for k in range(2):
                        S.op("pe", lambda e, k=k, blk=blk, ppp=ppp: e.matmul(
                            ppp[:], lhsT=pTs[:, k, :], rhs=wpp[:, k, blk * 512:(blk + 1) * 512], start=(k == 0), stop=(k == 1)),
                            reads=[pTs, wpp], writes=[ppp] if k == 0 else (), pwrites=() if k == 0 else [ppp])
                    S.op("dve", lambda e, blk=blk, ppp=ppp: e.tensor_tensor(
                        out=gate[:, blk * 512:(blk + 1) * 512], in0=ppp[:], in1=gate[:, blk * 512:(blk + 1) * 512], op=ALU.mult),
                        reads=[ppp, gate], pwrites=[gate])
                    S.op("pool", lambda e, blk=blk, s=s, xout=xout: e.tensor_tensor(
                        out=xout[:, blk * 512:(blk + 1) * 512], in0=gate[:, blk * 512:(blk + 1) * 512],
                        in1=xt[:, s, blk * 512:(blk + 1) * 512], op=ALU.add),
                        reads=[gate, xt], writes=[xout] if blk == 0 else (), pwrites=() if blk == 0 else [xout])
                if final:
                    S.op("act", lambda e, xout=xout: e.activation(out=sq[:], in_=xout[:], func=AF.Square, accum_out=ss[:]),
                         reads=[xout], writes=[sq, ss])
                    S.op("act", lambda e: e.activation(out=rs[:], in_=ss[:], func=AF.Sqrt, scale=1.0 / D, bias=EPS),
                         reads=[ss], writes=[rs])
                    S.op("dve", lambda e: e.reciprocal(out=rs[:], in_=rs[:]), reads=[rs], writes=[rs])
                    S.op("dve", lambda e, xout=xout: e.scalar_tensor_tensor(out=xout[:], in0=xout[:], scalar=rs[:, 0:1], in1=gfin[:],
                                                                            op0=ALU.mult, op1=ALU.mult),
                         reads=[xout, rs, gfin], writes=[xout])
                S.dma("sp", x_dst.t[t0 + s * 128:t0 + (s + 1) * 128, :], xout[:], reads=[xout], pwrites=[x_dst], key=xout)
        _barrier(S)
        S.stack_pop()


def phase_D(S, nc, SEQ, lyr, Wd, scr):
    TP = 128
    C = 16
    NCH = TP // C
    GN_EPS = 64e-5
    LD = 0.6065306597126334
    with contextlib.ExitStack() as st:
        S.stack_push(st)
        identB = make_ident(S, "D_identB", BF16)
        ones = S.sb("D_ones", [128, 128])
        S.op("pool", lambda e: e.memset(ones[:], 0.0), writes=[ones])
        S.op("pool", lambda e: e.memset(ones[0:64, 0:64], 1.0), reads=[ones], writes=[ones])
        S.op("pool", lambda e: e.memset(ones[64:128, 64:128], 1.0), reads=[ones], writes=[ones])
        Ff = S.sb("D_F", [128, 64], BF16)
        S.op("pool", lambda e: e.tensor_tensor(out=Ff[:], in0=identB[:, 0:64], in1=identB[:, 64:128], op=ALU.add), reads=[identB], writes=[Ff])
        Sel = S.sb("D_Sel", [128, 16], BF16)
        S.op("pool", lambda e: e.tensor_tensor(out=Sel[:], in0=identB[:, 0:16], in1=identB[:, 16:32], op=ALU.add), reads=[identB], writes=[Sel])
        for hh in range(2, 8):
            S.op("pool", lambda e, hh=hh: e.tensor_tensor(out=Sel[:], in0=Sel[:], in1=identB[:, hh * 16:(hh + 1) * 16], op=ALU.add),
                 reads=[identB, Sel], writes=[Sel])
        maskF = S.sb("D_maskF", [128, 4, 8], BF16)
        S.op("pool", lambda e: e.memset(maskF[:], 0.0), writes=[maskF])
        for p in range(4):
            for h2 in range(2):
                S.op("pool", lambda e, p=p, h2=h2: e.memset(maskF[h2 * 64:(h2 + 1) * 64, p, 2 * p + h2:2 * p + h2 + 1], 1.0), reads=[maskF], writes=[maskF])
        maskZ = S.sb("D_maskZ", [128, 4, 2], BF16)
        S.op("pool", lambda e: e.memset(maskZ[:], 1.0), writes=[maskZ])
        S.op("pool", lambda e: e.affine_select(out=maskZ[:], in_=maskZ[:], pattern=[[-32, 4], [-16, 2]], compare_op=ALU.is_ge, fill=0.0,
                                               base=0, channel_multiplier=1), reads=[maskZ], writes=[maskZ])
        S.op("pool", lambda e: e.affine_select(out=maskZ[:], in_=maskZ[:], pattern=[[32, 4], [16, 2]], compare_op=ALU.is_ge, fill=0.0,
                                               base=15, channel_multiplier=-1), reads=[maskZ], writes=[maskZ])

        def trimask(name, pat, cm, op):
            m = S.sb(name, [128, 128], BF16)
            S.op("pool", lambda e: e.memset(m[:], 1.0), writes=[m])
            S.op("pool", lambda e: e.affine_select(out=m[:], in_=m[:], pattern=pat, compare_op=op, fill=0.0, base=0, channel_multiplier=cm),
                 reads=[m], writes=[m])
            return m
        mSL = trimask("D_mSL", [[-16, 8], [-1, 16]], 1, ALU.is_gt)
        mSU = trimask("D_mSU", [[16, 8], [1, 16]], -1, ALU.is_gt)
        mUI = trimask("D_mUI", [[16, 8], [1, 16]], -1, ALU.is_ge)
        rm = S.sb("D_rm", [128, 512])
        S.op("pool", lambda e: e.memset(rm[:], 1.0), writes=[rm])
        S.op("pool", lambda e: e.memset(rm[:, 0:512:16], 0.0), reads=[rm], writes=[rm])

        def cvec(name, key, n):
            t = S.sb("D_" + name, [128, n])
            S.dma("sp", t[:], Wd[key].t[lyr].rearrange("(c p) -> p c", p=128), reads=[Wd[key]], writes=[t],
                  allow_slow_non_contiguous=True)
            return t

        def cvec2(name, key):
            t = S.sb("D_" + name, [128, 4])
            S.dma("sp", t[:], Wd[key].t[lyr].rearrange("(c a) j -> (a j) c", a=2), reads=[Wd[key]], writes=[t],
                  allow_slow_non_contiguous=True)
            return t
        mu = cvec("mu", "rk_mu", 13)
        w0 = cvec("w0", "rk_w0", 4)
        a0 = cvec("a0", "rk_a0", 4)
        lg = cvec("lg", "rk_lnx_g", 4)
        lb = cvec("lb", "rk_lnx_b", 4)
        kkc = cvec2("kkc", "rk_kk")
        ka = cvec2("ka", "rk_ka")
        rkc = cvec2("rkc", "rk_rk")
        omka = S.sb("D_omka", [128, 4])
        S.op("pool", lambda e: e.tensor_scalar(out=omka[:], in0=ka[:], scalar1=-1.0, scalar2=1.0, op0=ALU.mult, op1=ALU.add),
             reads=[ka], writes=[omka])
        w2 = S.sb("D_w2", [64, 512], BF16)
        a2 = S.sb("D_a2", [128, 512], BF16)
        S.dma("pool", w2[:], Wd["rk_w2"].t[lyr], reads=[Wd["rk_w2"]], writes=[w2])
        S.dma("pool", a2[64:128, :], Wd["rk_a2"].t[lyr], reads=[Wd["rk_a2"]], writes=[a2])

        Hm = S.sb("D_H", [128, 4, 64])
        Hn = S.sb("D_Hn", [128, 4, 64])
        Hbf = S.sb("D_Hbf", [128, 4, 64], BF16)
        S.op("pool", lambda e: e.memset(Hm[:], 0.0), writes=[Hm])
        S.op("pool", lambda e: e.memset(Hbf[:], 0.0), writes=[Hbf])

        rst = S.sb("D_rst", [128, 13, TP + 1])
        xs = S.sb("D_xs", [128, 13, TP])
        th = S.sb("D_th", [128, TP], BF16)
        sg = S.sb("D_sg", [128, 4, TP])
        cum = S.sb("D_cum", [128, 4, TP])
        E1 = S.sb("D_E1", [128, 4, TP])
        E2 = S.sb("D_E2", [128, 4, TP])
        E3 = S.sb("D_E3", [128, 4, TP])
        aa = S.sb("D_aa", [128, 4, TP])
        kkf = S.sb("D_kkf", [128, 4, TP])
        sq = S.sb("D_sq", [128, 4, TP])
        rn = S.sb("D_rn", [128, 4, TP])
        kp = S.sb("D_kp", [128, 4, TP])
        t1 = S.sb("D_t1", [128, 4, TP])
        t2 = S.sb("D_t2", [128, 4, TP])
        comp = [S.sb("D_cmp%d" % i, [128, 4, TP], BF16) for i in range(5)]
        ZXr = Ring([[S.sb("D_Z%d_%d" % (b, i), [128, NCH, 4, 128], BF16) for i in range(5)] for b in range(2)])
        DcR = Ring([S.sb("D_Dc%d" % i, [128, NCH, 4]) for i in range(2)])
        bonR = Ring([S.sb("D_bon%d" % i, [128, 4, TP]) for i in range(2)])
        rzR = Ring([S.sb("D_rz%d" % i, [128, 4, TP], BF16) for i in range(2)])
        ybR = Ring([S.sb("D_yb%d" % i, [128, 4, TP]) for i in range(2)])
        yo = S.sb("D_yo", [128, 4, TP], BF16)
        ppre = S.ps("D_ppre", [128, 4, TP])
        R4 = lambda nm, shp, dt=BF16: Ring([S.sb("D_%s%d" % (nm, i), shp, dt) for i in range(4)])
        WyZr = R4("WyZ", [128, 4, 128]); WhTr = R4("WhT", [128, 4, 128]); BtZr = R4("BtZ", [128, 4, 128]); KtZr = R4("KtZ", [128, 4, 128])
        U0r = R4("U0", [128, 64]); Vtr = R4("Vt", [128, 64]); PTr = R4("PT", [128, 128]); QTr = R4("QT", [128, 128])
        ysbR = Ring([S.sb("D_ysb%d" % i, [128, 64]) for i in range(5)])

        class Reg:
            def __init__(self, bank, ap):
                self.bank = bank
                self.t = ap

        class Lane:
            pass
        lanes = []
        for li in range(2):
            L = Lane()
            L.Gr = Ring([S.sb("D_G%d_%d" % (li, i), [128, 128], BF16) for i in range(2)])
            L.Nr = Ring([S.sb("D_N%d_%d" % (li, i), [128, 128], BF16) for i in range(2)])
            L.NTr = Ring([S.sb("D_NT%d_%d" % (li, i), [128, 128], BF16) for i in range(2)])
            L.MTs = S.sb("D_MTs%d" % li, [128, 128], BF16)
            L.X1Z = S.sb("D_X1Z%d" % li, [128, 4, 128], BF16)
            L.X1s = S.sb("D_X1s%d" % li, [128, 64], BF16)
            L.tks = S.sb("D_tks%d" % li, [128, 2, 64], BF16)
            ba = S.ps("D_ba%d" % li, [128, 512])
            bb = ppre if li == 0 else S.ps("D_bb%d" % li, [128, 4, 128])
            bg = S.ps("D_bg%d" % li, [128, 3, 128])
            L.tokc = Reg(ba, ba.t[:, 0:256].rearrange("q (o j) -> q o j", o=4))
            L.QTp = Reg(ba, ba.t[:, 256:384])
            L.mvp = Reg(ba, ba.t[:, 384:448])
            L.sc = [Reg(bb, bb.t[:, i, :]) for i in range(4)]
            L.bb = bb
            L.bg = bg
            lanes.append(L)
        bs = S.ps("D_bs", [128, 512])
        bt_ = S.ps("D_bt", [128, 512])
        WHp = Reg(bs, bs.t[:, 0:256].rearrange("q (p i) -> q p i", p=4))
        Yp = Reg(bs, bs.t[:, 256:320])
        yfp = Reg(bt_, bt_.t[:, 0:64].rearrange("q (p t) -> q p t", p=4))
        yn = S.sb("D_yn", [128, 64])
        YZ = S.sb("D_YZ", [128, 4, 128], BF16)
        stats = S.sb("D_stats", [128, 6])
        mv = S.sb("D_mv", [128, 2])
        rstd = S.sb("D_rstd", [128, 1])

        bc4 = lambda t: t[:].unsqueeze(2).to_broadcast([128, 4, TP])

        def mm(out_ap, obuf, lhsT, lbuf, rhs, rbuf, start, stop=True, first_write=False):
            obuf = getattr(obuf, "bank", obuf)
            S.op("pe", lambda e: e.matmul(out_ap, lhsT=lhsT, rhs=rhs, start=start, stop=stop, skip_group_check=True),
                 reads=[lbuf, rbuf], writes=[obuf] if first_write else (), pwrites=() if first_write else [obuf])

        def prep(nb):
            t0 = nb * TP
            S.dma("sp", rst[:, :, 1:TP + 1], scr["rsT"].t[:, t0:t0 + TP].rearrange("(c p) t -> p c t", p=128), reads=[scr["rsT"]], writes=[rst])
            if nb == 0:
                S.op("pool", lambda e: e.memset(rst[:, :, 0:1], 0.0), reads=[rst], pwrites=[rst])
            else:
                S.dma("sp", rst[:, :, 0:1], scr["rsT"].t[:, t0 - 1:t0].rearrange("(c p) t -> p c t", p=128), reads=[scr["rsT"]],
                      pwrites=[rst], key=rst, allow_slow_non_contiguous=True)
            S.op("pool", lambda e: e.tensor_tensor(out=xs[:], in0=rst[:, :, 0:TP], in1=rst[:, :, 1:TP + 1], op=ALU.subtract), reads=[rst], writes=[xs])
            S.op("pool", lambda e: e.tensor_tensor(out=xs[:], in0=xs[:], in1=mu[:].unsqueeze(2).to_broadcast([128, 13, TP]), op=ALU.mult),
                 reads=[xs, mu], writes=[xs])
            S.op("pool", lambda e: e.tensor_tensor(out=xs[:], in0=xs[:], in1=rst[:, :, 1:TP + 1], op=ALU.add), reads=[xs, rst], writes=[xs])
            r = xs[:, 0:4, :]; k = xs[:, 4:8, :]; v = xs[:, 8:12, :]
            S.op("act", lambda e: e.activation(out=th[0:64, :], in_=xs[0:64, 12, :], func=AF.Tanh), reads=[xs], pwrites=[th])
            S.op("act", lambda e: e.copy(out=th[64:128, :], in_=xs[64:128, 12, :]), reads=[xs], pwrites=[th])
            for p in range(4):
                mm(ppre[:, p, :], ppre, w2[0:64, p * 128:(p + 1) * 128], w2, th[0:64, :], th, True, first_write=(p == 0))
            for p in range(4):
                S.op("act", lambda e, p=p: e.activation(out=sg[:, p, :], in_=ppre[:, p, :], func=AF.Sigmoid, bias=w0[:, p:p + 1]),
                     reads=[w0], writes=[ppre], pwrites=[sg])
            for p in range(4):
                mm(ppre[:, p, :], ppre, a2[64:128, p * 128:(p + 1) * 128], a2, th[64:128, :], th, True, first_write=(p == 0))
            for p in range(4):
                S.op("act", lambda e, p=p: e.activation(out=aa[:, p, :], in_=ppre[:, p, :], func=AF.Sigmoid, bias=a0[:, p:p + 1]),
                     reads=[a0], writes=[ppre], pwrites=[aa])
            S.op("dve", lambda e: e.tensor_tensor_scan(out=cum[:].rearrange("q p t -> q (p t)"), data0=rm[:],
                                                       data1=sg[:].rearrange("q p t -> q (p t)"), initial=0.0, op0=ALU.mult, op1=ALU.add),
                 reads=[rm, sg], writes=[cum])
            S.op("act", lambda e: e.activation(out=E1[:], in_=cum[:], func=AF.Exp, scale=-LD), reads=[cum], writes=[E1])
            S.op("act", lambda e: e.activation(out=E2[:], in_=cum[:], func=AF.Exp, scale=LD), reads=[cum], writes=[E2])
            S.op("pool", lambda e: e.tensor_tensor(out=t2[:], in0=cum[:], in1=sg[:], op=ALU.subtract), reads=[cum, sg], writes=[t2])
            S.op("act", lambda e: e.activation(out=E3[:], in_=t2[:], func=AF.Exp, scale=-LD), reads=[t2], writes=[E3])
            Dc = DcR.next()
            S.op("pool", lambda e, Dc=Dc: e.tensor_copy(out=Dc[:].rearrange("q c p -> q p c"), in_=E1[:, :, 15:TP:16]), reads=[E1], writes=[Dc])
            S.op("pool", lambda e: e.tensor_tensor(out=kkf[:], in0=k, in1=bc4(kkc), op=ALU.mult), reads=[xs, kkc], writes=[kkf])
            S.op("pool", lambda e: e.tensor_tensor(out=sq[:], in0=kkf[:], in1=kkf[:], op=ALU.mult), reads=[kkf], writes=[sq])
            for p in range(4):
                mm(ppre[:, p, :], ppre, ones[:], ones, sq[:, p, :], sq, True, first_write=(p == 0))
            S.op("act", lambda e: e.activation(out=rn[:], in_=ppre[:], func=AF.Sqrt), writes=[rn, ppre])
            S.op("dve", lambda e: e.tensor_scalar(out=rn[:], in0=rn[:], scalar1=1e-12, scalar2=None, op0=ALU.max), reads=[rn], writes=[rn])
            S.op("dve", lambda e: e.reciprocal(out=rn[:], in_=rn[:]), reads=[rn], writes=[rn])
            S.op("pool", lambda e: e.tensor_tensor(out=kkf[:], in0=kkf[:], in1=rn[:], op=ALU.mult), reads=[kkf, rn], writes=[kkf])
            S.op("pool", lambda e: e.tensor_tensor(out=t1[:], in0=aa[:], in1=bc4(ka), op=ALU.mult), reads=[aa, ka], writes=[t1])
            S.op("pool", lambda e: e.tensor_tensor(out=t1[:], in0=t1[:], in1=bc4(omka), op=ALU.add), reads=[t1, omka], writes=[t1])
            S.op("pool", lambda e: e.tensor_tensor(out=kp[:], in0=k, in1=t1[:], op=ALU.mult), reads=[xs, t1], writes=[kp])
            At, Bt, Kt, Rt, Vb = comp
            S.op("pool", lambda e: e.scalar_tensor_tensor(out=At[:], in0=kkf[:], scalar=-1.0, in1=E3[:], op0=ALU.mult, op1=ALU.mult)
                 if False else e.tensor_tensor(out=t2[:], in0=kkf[:], in1=E3[:], op=ALU.mult), reads=[kkf, E3], writes=[t2])
            S.op("dve", lambda e: e.tensor_scalar(out=At[:], in0=t2[:], scalar1=-1.0, scalar2=None, op0=ALU.mult), reads=[t2], writes=[At])
            S.op("pool", lambda e: e.tensor_tensor(out=t2[:], in0=kkf[:], in1=aa[:], op=ALU.mult), reads=[kkf, aa], writes=[t2])
            S.op("pool", lambda e: e.tensor_tensor(out=Bt[:], in0=t2[:], in1=E2[:], op=ALU.mult), reads=[t2, E2], writes=[Bt])
            S.op("pool", lambda e: e.tensor_tensor(out=Kt[:], in0=kp[:], in1=E2[:], op=ALU.mult), reads=[kp, E2], writes=[Kt])
            S.op("pool", lambda e: e.tensor_tensor(out=Rt[:], in0=r, in1=E1[:], op=ALU.mult), reads=[xs, E1], writes=[Rt])
            S.op("pool", lambda e: e.tensor_copy(out=Vb[:], in_=v), reads=[xs], writes=[Vb])
            S.op("pool", lambda e: e.tensor_tensor(out=t1[:], in0=r, in1=kp[:], op=ALU.mult), reads=[xs, kp], writes=[t1])
            S.op("pool", lambda e: e.tensor_tensor(out=sq[:], in0=t1[:], in1=bc4(rkc), op=ALU.mult), reads=[t1, rkc], writes=[sq])
            for p in range(4):
                mm(ppre[:, p, :], ppre, ones[:], ones, sq[:, p, :], sq, True, first_write=(p == 0))
            bon = bonR.next()
            S.op("act", lambda e, bon=bon: e.copy(out=bon[:], in_=ppre[:]), writes=[bon, ppre])
            S.op("pool", lambda e, bon=bon: e.tensor_tensor(out=bon[:], in0=bon[:], in1=v, op=ALU.mult), reads=[bon, xs], writes=[bon])
            rzt = rzR.next()
            S.dma("sp", rzt[:], scr["rzT"].t[:, t0:t0 + TP].rearrange("(c p) t -> p c t", p=128), reads=[scr["rzT"]], writes=[rzt])
            ZX = ZXr.next()
            for oi in range(5):
                for p in range(4):
                    S.op("dve" if (oi * 4 + p) % 2 == 0 else "pool", lambda e, oi=oi, p=p, ZX=ZX: e.tensor_tensor(
                        out=ZX[oi][:, :, p, :].rearrange("q c (h t) -> q c h t", t=16),
                        in0=comp[oi][:, p, :].rearrange("q (c t) -> q c t", t=16).unsqueeze(2).to_broadcast([128, NCH, 8, 16]),
                        in1=maskF[:, p, :].unsqueeze(1).unsqueeze(3).to_broadcast([128, NCH, 8, 16]), op=ALU.mult),
                        reads=[comp[oi], maskF], writes=[ZX[oi]] if p == 0 else (), pwrites=() if p == 0 else [ZX[oi]])
            return dict(ZX=ZX, Dc=Dc, bon=bon, rzt=rzt, yb=ybR.next(), t0=t0)


        def pre(bt, c, L, pc):
            ZA, ZB, ZK, ZR, ZV = bt["ZX"]
            BtZ = BtZr.next(); KtZ = KtZr.next(); U0 = U0r.next(); Vt = Vtr.next(); PTs = PTr.next(); QTs = QTr.next()
            WyZ = WyZr.next(); WhT = WhTr.next()
            pc.update(BtZ=BtZ, KtZ=KtZ, U0=U0, Vt=Vt, PTs=PTs, QTs=QTs, WyZ=WyZ, WhT=WhT, c=c, bt=bt)
            tokc = L.tokc
            first = True
            for oi, Z in enumerate((ZA, ZB, ZK, ZV)):
                for p in range(4):
                    mm(tokc.t[:, oi, :], tokc, Z[:, c, p, :], Z, Ff[:], Ff, first, first_write=first)
                    first = False
            N1 = L.Nr.next(); NT1 = L.NTr.next()
            specs = ((L.sc[0], ZA, ZB, mSL, N1), (L.sc[1], ZB, ZA, mSU, NT1), (L.sc[2], ZK, ZA, mSU, L.MTs), (L.sc[3], ZB, ZR, mUI, PTs))
            for gi, (pb, Lh, R_, msk, dst) in enumerate(specs):
                for p in range(4):
                    mm(pb.t, pb, Lh[:, c, p, :], Lh, R_[:, c, p, :], R_, p == 0, first_write=(gi == 0 and p == 0))
            for p in range(4):
                mm(L.QTp.t, L.QTp, ZK[:, c, p, :], ZK, ZR[:, c, p, :], ZR, False, first_write=False)
            yield
            G0 = L.Gr.next()
            tks = L.tks
            S.op("act", lambda e: e.copy(out=G0[:, 0:64], in_=tokc.t[:, 0, :]), writes=[G0, tokc.bank])
            mz = maskZ[:].unsqueeze(3).to_broadcast([128, 4, 2, 64])
            S.op("dve", lambda e: e.tensor_copy(out=tks[:], in_=tokc.t[:, 1:3, :]), writes=[tks, tokc.bank])
            S.op("act", lambda e: e.copy(out=Vt[:], in_=tokc.t[:, 3, :]), writes=[Vt, tokc.bank])
            S.op("pool", lambda e: e.tensor_tensor(out=BtZ[:].rearrange("q p (a j) -> q p a j", a=2),
                                                   in0=tks[:, 0, :].unsqueeze(1).unsqueeze(1).to_broadcast([128, 4, 2, 64]), in1=mz, op=ALU.mult),
                 reads=[tks, maskZ], writes=[BtZ])
            S.op("pool", lambda e: e.tensor_tensor(out=KtZ[:].rearrange("q p (a j) -> q p a j", a=2),
                                                   in0=tks[:, 1, :].unsqueeze(1).unsqueeze(1).to_broadcast([128, 4, 2, 64]), in1=mz, op=ALU.mult),
                 reads=[tks, maskZ], writes=[KtZ])
            for gi, (pb, Lh, R_, msk, dst) in enumerate(specs):
                S.op("dve", lambda e, pb=pb, msk=msk, dst=dst: e.tensor_tensor(out=dst[:], in0=pb.t, in1=msk[:], op=ALU.mult),
                     reads=[msk], writes=[dst, pb.bank])
            S.op("dve", lambda e: e.tensor_tensor(out=QTs[:], in0=L.QTp.t, in1=mUI[:], op=ALU.mult), reads=[mUI], writes=[QTs, L.QTp.bank])
            yield
            mm(L.mvp.t, L.mvp, L.MTs[:], L.MTs, Vt[:], Vt, True, first_write=True)
            yield
            S.op("act", lambda e: e.copy(out=G0[:, 64:128], in_=L.mvp.t), writes=[L.mvp.bank], pwrites=[G0])
            yield
            G = G0; Nk = N1; NTk = NT1
            gb = L.bg
            for lev in range(4):
                mm(gb[:, 0, :], gb, identB[:], identB, G[:], G, True, stop=False, first_write=True)
                mm(gb[:, 0, :], gb, NTk[:], NTk, G[:], G, False)
                if lev < 3:
                    mm(gb[:, 1, :], gb, NTk[:], NTk, Nk[:], Nk, True)
                    mm(gb[:, 2, :], gb, Nk[:], Nk, NTk[:], NTk, True)
                    yield
                    G2 = L.Gr.next(); N2 = L.Nr.next(); NT2 = L.NTr.next()
                    S.op("act", lambda e, G2=G2: e.copy(out=G2[:], in_=gb[:, 0, :]), writes=[G2, gb])
                    S.op("act", lambda e, N2=N2: e.copy(out=N2[:], in_=gb[:, 1, :]), writes=[N2, gb])
                    S.op("act", lambda e, NT2=NT2: e.copy(out=NT2[:], in_=gb[:, 2, :]), writes=[NT2, gb])
                    G = G2; Nk = N2; NTk = NT2
                    yield
                else:
                    yield
                    S.op("act", lambda e: e.copy(out=L.X1s[:], in_=gb[:, 0, 0:64]), writes=[L.X1s, gb])
                    S.op("act", lambda e: e.copy(out=U0[:], in_=gb[:, 0, 64:128]), writes=[U0, gb])
                    S.op("dve", lambda e: e.tensor_tensor(out=L.X1Z[:].rearrange("q p (a j) -> q p a j", a=2),
                                                           in0=L.X1s[:].unsqueeze(1).unsqueeze(1).to_broadcast([128, 4, 2, 64]), in1=mz, op=ALU.mult),
                         reads=[L.X1s, maskZ], writes=[L.X1Z])
                    yield
            bb = L.bb
            for p in range(4):
                mm(bb[:, p, :], bb, identB[:], identB, ZR[:, c, p, :], ZR, p == 0, stop=False, first_write=(p == 0))
                mm(bb[:, p, :], bb, L.X1Z[:, p, :], L.X1Z, PTs[:], PTs, False)
            ba = L.tokc.bank
            for p in range(4):
                mm(ba[:, p * 128:(p + 1) * 128], ba, L.X1Z[:, p, :], L.X1Z, BtZ[:, p, :], BtZ, p == 0, first_write=(p == 0))
            yield
            S.op("act", lambda e: e.copy(out=WyZ[:], in_=bb[:]), writes=[WyZ, bb])
            S.op("dve", lambda e: e.tensor_copy(out=WhT[:].rearrange("q p m -> q (p m)"), in_=ba[:]), writes=[WhT, ba])
            yield

        def state_stream(pc):
            c = pc["c"]; bt = pc["bt"]
            BtZ, KtZ, U0, Vt, PTs, QTs, WyZ, WhT = (pc[k] for k in ("BtZ", "KtZ", "U0", "Vt", "PTs", "QTs", "WyZ", "WhT"))
            for p in range(4):
                mm(WHp.t[:, p, :], WHp, BtZ[:, p, :], BtZ, U0[:], U0, p == 0, stop=False, first_write=(p == 0))
            for p in range(4):
                mm(WHp.t[:, p, :], WHp, KtZ[:, p, :], KtZ, Vt[:], Vt, False, stop=False)
            mm(Yp.t, Yp, PTs[:], PTs, U0[:], U0, False, stop=False)
            mm(Yp.t, Yp, QTs[:], QTs, Vt[:], Vt, False, stop=False)
            yield
            for p in range(4):
                mm(Yp.t, Yp, WyZ[:, p, :], WyZ, Hbf[:, p, :], Hbf, False, stop=(p == 3))
            for p in range(4):
                mm(WHp.t[:, p, :], WHp, WhT[:, p, :], WhT, Hbf[:, p, :], Hbf, False, stop=True)
            yield
            Dc = bt["Dc"]
            S.op("dve", lambda e: e.tensor_tensor(out=Hn[:], in0=WHp.t, in1=Hm[:], op=ALU.add), reads=[Hm], writes=[Hn, WHp.bank])
            S.op("dve", lambda e: e.tensor_tensor(out=Hm[:], in0=Hn[:], in1=Dc[:, c, :].unsqueeze(2).to_broadcast([128, 4, 64]), op=ALU.mult),
                 reads=[Hn, Dc], writes=[Hm])
            ysb = ysbR.next()
            pc["ysb"] = ysb
            S.op("act", lambda e: e.copy(out=ysb[:], in_=Yp.t), writes=[ysb, Yp.bank])
            S.op("act", lambda e: e.copy(out=Hbf[:], in_=Hm[:]), reads=[Hm], writes=[Hbf])
            yield

        def out_stream(pc):
            c = pc["c"]; bt = pc["bt"]; ysb = pc["ysb"]
            S.op("dve", lambda e: e.bn_stats(out=stats[:], in_=ysb[:]), reads=[ysb], writes=[stats])
            S.op("dve", lambda e: e.bn_aggr(out=mv[:], in_=stats[:]), reads=[stats], writes=[mv])
            yield
            S.op("act", lambda e: e.activation(out=rstd[:], in_=mv[:, 1:2], func=AF.Sqrt, bias=GN_EPS, scale=1.0), reads=[mv], writes=[rstd])
            yield
            S.op("dve", lambda e: e.reciprocal(out=rstd[:], in_=rstd[:]), reads=[rstd], writes=[rstd])
            S.op("dve", lambda e: e.tensor_scalar(out=yn[:], in0=ysb[:], scalar1=mv[:, 0:1], scalar2=rstd[:, 0:1], op0=ALU.subtract, op1=ALU.mult),
                 reads=[ysb, mv, rstd], writes=[yn])
            yield
            S.op("dve", lambda e: e.tensor_tensor(out=YZ[:].rearrange("q p (a j) -> q p a j", a=2),
                                                   in0=yn[:].unsqueeze(1).unsqueeze(1).to_broadcast([128, 4, 2, 64]),
                                                   in1=maskZ[:].unsqueeze(3).to_broadcast([128, 4, 2, 64]), op=ALU.mult),
                 reads=[yn, maskZ], writes=[YZ])
            yield
            for p in range(4):
                mm(yfp.t[:, p, :], yfp, YZ[:, p, :], YZ, Sel[:], Sel, True, first_write=(p == 0))
            yield
            yb = bt["yb"]
            S.op("act", lambda e: e.copy(out=yb[:, :, c * 16:(c + 1) * 16], in_=yfp.t), writes=([yb] if c == 0 else []) + [yfp.bank],
                 pwrites=() if c == 0 else [yb])
            if c == NCH - 1:
                post(bt)
            yield

        def post(bt):
            yb = bt["yb"]; bon = bt["bon"]; rzt = bt["rzt"]; t0 = bt["t0"]
            S.op("pool", lambda e: e.tensor_tensor(out=yb[:], in0=yb[:], in1=bc4(lg), op=ALU.mult), reads=[yb, lg], writes=[yb])
            S.op("pool", lambda e: e.tensor_tensor(out=yb[:], in0=yb[:], in1=bc4(lb), op=ALU.add), reads=[yb, lb], writes=[yb])
            S.op("pool", lambda e: e.tensor_tensor(out=yb[:], in0=yb[:], in1=bon[:], op=ALU.add), reads=[yb, bon], writes=[yb])
            S.op("pool", lambda e: e.tensor_tensor(out=yo[:], in0=yb[:], in1=rzt[:], op=ALU.mult), reads=[yb, rzt], writes=[yo])
            S.dma("sp", scr["ysT"].t[2, :, t0:t0 + TP].rearrange("(c p) t -> p c t", p=128), yo[:], reads=[yo], pwrites=[scr["ysT"]], key=yo)

        chunks = []
        for nb in range(SEQ // TP):
            for c in range(NCH):
                chunks.append((nb, c))
        bts = {}
        nxt = 0
        lane_gen = [None, None]
        lane_pc = [None, None]
        done_order = {}
        next_state = 0
        state_gen = None; state_pc = None
        out_q = []; out_gen = None
        n_total = len(chunks)
        finished_out = 0
        pcs = {}
        while finished_out < n_total:
            for li in range(2):
                if lane_gen[li] is None and nxt < n_total and nxt - next_state < 3:
                    nb, c = chunks[nxt]
                    if nb not in bts:
                        bts[nb] = prep(nb)
                    pc = {"idx": nxt}
                    pcs[nxt] = pc
                    lane_gen[li] = pre(bts[nb], c, lanes[li], pc)
                    lane_pc[li] = pc
                    nxt += 1
                if lane_gen[li] is not None:
                    try:
                        next(lane_gen[li])
                    except StopIteration:
                        done_order[lane_pc[li]["idx"]] = True
                        lane_gen[li] = None
            for _rep in range(DEBUG.get("state_rep", 2)):
                if state_gen is None and done_order.get(next_state) and next_state - finished_out < 3:
                    state_pc = pcs[next_state]
                    state_gen = state_stream(state_pc)
                if state_gen is not None:
                    try:
                        next(state_gen)
                    except StopIteration:
                        out_q.append(state_pc)
                        state_gen = None
                        next_state += 1
            for _rep in range(DEBUG.get("out_rep", 1)):
                if out_gen is None and out_q:
                    out_gen = out_stream(out_q.pop(0))
                if out_gen is not None:
                    try:
                        next(out_gen)
                    except StopIteration:
                        out_gen = None
                        finished_out += 1
        _barrier(S)
        S.stack_pop()


WSPEC = {
    "norm_g": [2, 1024], "w_in": [2, 1024, 9112], "cmp_w1": [2, 2, 32, 64, 128], "cmp_w2": [2, 2, 128, 64],
    "cmp_pe": [2, 2, 32, 64], "sg_ln_g": [2, 512], "sg_ln_b": [2, 512], "sg_w": [2, 8, 128, 128], "sg_b": [2, 8, 128],
    "rk_mu": [2, 1664], "rk_w0": [2, 512], "rk_w2": [2, 64, 512], "rk_a0": [2, 512], "rk_a2": [2, 64, 512],
    "rk_kk": [2, 8, 64], "rk_ka": [2, 8, 64], "rk_rk": [2, 8, 64], "rk_lnx_g": [2, 512], "rk_lnx_b": [2, 512],
    "w_branch": [2, 3, 512, 1024], "w_o": [2, 1024, 1024], "ple_norm_g": [2, 1024], "w_ple_gate": [2, 1024, 1024],
    "w_ple_proj": [2, 256, 1024], "final_norm_g": [1, 1024],
}


def build(SEQ, nlayers=2, enable=(1, 1, 1), scr_kind="Internal"):
    nc = bass.Bass("TRN2", target_bir_lowering=False)
    with contextlib.ExitStack() as stack:
        S = Sched(nc, stack)
        x = Buf("x", nc.dram_tensor("x", [SEQ, D], F32, kind="ExternalInput").ap())
        Wd = {"p": Buf("p", nc.dram_tensor("p", [2, SEQ, PLE], F32, kind="ExternalInput").ap())}
        for k, shp in WSPEC.items():
            Wd[k] = Buf(k, nc.dram_tensor(k, shp, F32, kind="ExternalInput").ap())
        out = Buf("out", nc.dram_tensor("out", [SEQ, D], F32, kind="ExternalOutput").ap())
        scr = make_scratch(S, SEQ, kind=scr_kind)
        xmid = S.dram("xmid", [SEQ, D], F32, kind=scr_kind)
        cur = x
        for lyr in range(nlayers):
            last = lyr == nlayers - 1
            dst = out if last else xmid
            phase_A(S, nc, SEQ, lyr, cur, Wd, scr)
            if enable[0]:
                phase_B(S, nc, SEQ, lyr, Wd, scr)
            if enable[1]:
                phase_C(S, nc, SEQ, lyr, Wd, scr)
            if enable[2]:
                phase_D(S, nc, SEQ, lyr, Wd, scr)
            phase_E(S, nc, SEQ, lyr, cur, dst, Wd, scr, final=(last and nlayers == 2))
            cur = dst
        S.emit()
    return nc


def phase_B(S, nc, SEQ, lyr, Wd, scr):
    NC = (SEQ - 32) // 16 + 1
    NT = (NC + 127) // 128
    NCp = NT * 128
    KT = SEQ // 128
    with contextlib.ExitStack() as st:
        S.stack_push(st)
        ident = make_ident(S, "B_ident")
        ksT = S.sb("B_ksT", [128, 2, SEQ], BF16)
        HALF = min(4096, SEQ)
        NA = SEQ // HALF
        kwT = S.sb("B_kwT", [64, 2, SEQ], BF16)
        vs = S.sb("B_vs", [128, KT, 2, 65], BF16)
        vw = S.sb("B_vw", [128, KT, 2, 65], BF16)
        kcmpT = S.sb("B_kcmpT", [64, 2, NCp], BF16)
        Rc = S.sb("B_Rc", [128, NT, 2, 193], BF16)
        S.op("pool", lambda e: e.memset(ksT[64:128, :, :], 1.0), writes=[ksT])
        for g_ in range(2):
            for a_ in range(NA):
                S.op("pool", lambda e, g_=g_, a_=a_: e.affine_select(
                    out=ksT[64:128, g_, a_ * HALF:(a_ + 1) * HALF], in_=ksT[64:128, g_, a_ * HALF:(a_ + 1) * HALF], pattern=[[1, HALF]],
                    compare_op=ALU.is_ge, fill=0.0, base=0, channel_multiplier=-64), reads=[ksT], pwrites=[ksT])
                S.op("pool", lambda e, g_=g_, a_=a_: e.affine_select(
                    out=ksT[64:128, g_, a_ * HALF:(a_ + 1) * HALF], in_=ksT[64:128, g_, a_ * HALF:(a_ + 1) * HALF], pattern=[[-1, HALF]],
                    compare_op=ALU.is_ge, fill=0.0, base=63, channel_multiplier=64), reads=[ksT], pwrites=[ksT])
        S.dma("sp", ksT[0:64, :, :], scr["ksT"].t.rearrange("(g d) t -> d g t", g=2), reads=[scr["ksT"]], pwrites=[ksT], key=ksT)
        S.dma("sp", kwT[:], scr["kwT"].t.rearrange("(g d) t -> d g t", g=2), reads=[scr["kwT"]], writes=[kwT])
        S.op("pool", lambda e: e.memset(vs[:], 1.0), writes=[vs])
        S.op("pool", lambda e: e.memset(vw[:], 1.0), writes=[vw])
        for k0 in range(0, KT, 8):
            k1 = min(KT, k0 + 8)
            for (dst, c0) in ((vs, 0), (vw, 128)):
                for g in range(2):
                    S.dma("sp", dst[:, k0:k1, g, 0:64],
                          scr["vsw"].t[k0 * 128:k1 * 128, c0 + g * 64:c0 + (g + 1) * 64].rearrange("(k p) d -> p k d", p=128),
                          reads=[scr["vsw"]], pwrites=[dst], key=dst)
        S.op("pool", lambda e: e.memset(Rc[:], 1.0), writes=[Rc])
        for nt in range(NT):
            for g in range(2):
                S.op("pool", lambda e, nt=nt, g=g: e.affine_select(
                    out=Rc[:, nt, g, 65:193], in_=Rc[:, nt, g, 65:193], pattern=[[-4, 128]], compare_op=ALU.is_ge, fill=0.0,
                    base=nt * 128 + 1, channel_multiplier=1), reads=[Rc], writes=[Rc])
                S.op("pool", lambda e, nt=nt, g=g: e.affine_select(
                    out=Rc[:, nt, g, 65:193], in_=Rc[:, nt, g, 65:193], pattern=[[4, 128]], compare_op=ALU.is_ge, fill=0.0,
                    base=3 - nt * 128, channel_multiplier=-1), reads=[Rc], writes=[Rc])
        npad = NCp - NC
        if npad:
            S.op("pool", lambda e: e.affine_select(
                out=Rc[:, NT - 1, :, :], in_=Rc[:, NT - 1, :, :], pattern=[[0, 2 * 193]], compare_op=ALU.is_ge, fill=0.0,
                base=(NC - 1) - (NT - 1) * 128, channel_multiplier=-1), reads=[Rc], writes=[Rc])
        S.op("pool", lambda e: e.memset(kcmpT[:], 0.0), writes=[kcmpT])

        with contextlib.ExitStack() as st2:
            S.stack_push(st2)
            kvT = S.sb("B_kvT", [64, 2, SEQ], BF16)
            w1 = S.sb("B_w1", [64, 32, 128], BF16)
            w2 = S.sb("B_w2", [128, 64], BF16)
            peT = S.sb("B_peT", [64, 32])
            peTb = S.sb("B_peTb", [64, 32], BF16)
            cb = S.sb("B_cb", [128, 1])
            hid = S.sb("B_hid", [128, NCp], BF16)
            ph = S.ps("B_ph", [128, 512])
            pc1 = S.ps("B_pc1", [128, 512])
            pk = S.ps("B_pk", [128, 512])
            for kv in range(2):
                src = scr["kcT"] if kv == 0 else scr["vcT"]
                S.dma("sp", kvT[:], src.t.rearrange("(g d) t -> d g t", g=2), reads=[src], writes=[kvT])
                S.dma("pool", w1[:], Wd["cmp_w1"].t[lyr, kv].rearrange("l d h -> d l h"), reads=[Wd["cmp_w1"]], writes=[w1])
                S.dma("pool", w2[:], Wd["cmp_w2"].t[lyr, kv], reads=[Wd["cmp_w2"]], writes=[w2])
                S.dma("sp", peT[:], Wd["cmp_pe"].t[lyr, kv].rearrange("l d -> d l"), reads=[Wd["cmp_pe"]], writes=[peT],
                      allow_slow_non_contiguous=True)
                S.op("dve", lambda e: e.tensor_copy(out=peTb[:], in_=peT[:]), reads=[peT], writes=[peTb])
                for l in range(32):
                    S.op("pe", lambda e, l=l: e.matmul(pc1[:, 0:1], lhsT=w1[:, l, :], rhs=peTb[:, l:l + 1], start=(l == 0), stop=(l == 31)),
                         reads=[w1, peTb], writes=[pc1] if l == 0 else (), pwrites=() if l == 0 else [pc1])
                S.op("dve", lambda e: e.tensor_copy(out=cb[:], in_=pc1[:, 0:1]), reads=[pc1], writes=[cb])
                for g in range(2):
                    S.op("dve", lambda e: e.memset(hid[:], 0.0), writes=[hid])
                    for n0 in range(0, NC, 512):
                        nn = min(512, NC - n0)
                        for l in range(32):
                            S.op("pe", lambda e, l=l, g=g, n0=n0, nn=nn: e.matmul(
                                ph[:, 0:nn], lhsT=w1[:, l, :], rhs=kvT[:, g, n0 * 16 + l: n0 * 16 + l + (nn - 1) * 16 + 1: 16], start=(l == 0), stop=(l == 31)),
                                reads=[w1, kvT], writes=[ph] if l == 0 else (), pwrites=() if l == 0 else [ph])
                        S.op("act", lambda e, n0=n0, nn=nn: e.activation(out=hid[:, n0:n0 + nn], in_=ph[:, 0:nn], func=AF.Silu, bias=cb[:, 0:1]),
                             reads=[ph, cb], pwrites=[hid])
                    if kv == 0:
                        for n0 in range(0, NC, 512):
                            nn = min(512, NC - n0)
                            S.op("pe", lambda e, n0=n0, nn=nn: e.matmul(pk[0:64, 0:nn], lhsT=w2[:], rhs=hid[:, n0:n0 + nn], start=True, stop=True),
                                 reads=[w2, hid], writes=[pk])
                            S.op("dve", lambda e, g=g, n0=n0, nn=nn: e.tensor_copy(out=kcmpT[:, g, n0:n0 + nn], in_=pk[0:64, 0:nn]),
                                 reads=[pk], pwrites=[kcmpT])
                    else:
                        for nt in range(NT):
                            rows = min(128, NC - nt * 128)
                            S.op("pe", lambda e, nt=nt: e.matmul(pk[:, 0:64], lhsT=hid[:, nt * 128:(nt + 1) * 128], rhs=w2[:], start=True, stop=True),
                                 reads=[w2, hid], writes=[pk])
                            S.op("dve", lambda e, g=g, nt=nt: e.tensor_copy(out=Rc[:, nt, g, 0:64], in_=pk[:, 0:64]),
                                 reads=[pk], pwrites=[Rc])
            _barrier(S)
            S.stack_pop()

        qt = Ring([S.sb("B_q%d" % i, [64, 8, 128], BF16) for i in range(2)])
        gt = Ring([S.sb("B_g%d" % i, [128, 24]) for i in range(2)])
        nzt = Ring([S.sb("B_nz%d" % i, [128, 512], BF16) for i in range(2)])
        Et = Ring([S.sb("B_E%d" % i, [128, 512], BF16) for i in range(4)])
        psT = Ring([S.ps("B_psT%d" % i, [128, 512]) for i in range(3)])
        pcA = S.ps("B_pcA", [128, 2, 193])
        pcB = S.ps("B_pcB", [128, 2, 193])
        pos = S.ps("B_pos", [128, 4, 65])
        pow_ = S.ps("B_pow", [128, 4, 65])
        pmisc = S.ps("B_pmisc", [128, 4, 128], BF16)
        P2 = lambda nm, shp, dt=F32: [S.sb("B_%s%d" % (nm, i), shp, dt) for i in range(2)]
        oc2 = P2("oc", [128, 4, 193]); rcs2 = P2("rcs", [128, 4]); rss2 = P2("rss", [128, 4]); rws2 = P2("rws", [128, 4])
        cc2 = P2("cc", [128, 3, 4]); sc_2 = P2("sc", [128, 128]); sc2_2 = P2("sc2", [128, 128]); m1_2 = P2("m1", [128, 8]); m2_2 = P2("m2", [128, 8])
        pws2 = P2("pws", [128, 4, 65]); pss2 = P2("pss", [128, 4, 65])
        negq2 = P2("negq", [128, 2, 128], BF16)
        for _nq in negq2:
            S.op("pool", lambda e, _nq=_nq: e.memset(_nq[:], 0.0), writes=[_nq])
        qAr = {(g_, a_): Ring([S.sb("B_qA%d%d_%d" % (g_, a_, i), [128, 4, 128], BF16) for i in range(2)]) for g_ in range(2) for a_ in range(NA)}
        yg = S.sb("B_yg", [128, 4, 64])
        ytmp = S.sb("B_ytmp", [128, 4, 64])
        ynsaR = Ring([S.sb("B_ynsa%d" % i, [128, 512], BF16) for i in range(2)])
        stg = Ring([S.sb("B_stg%d" % i, [128, 4, 128], BF16) for i in range(2)])

        def qk_exp(kT_ap, kbuf, q_ap, qbuf, neg_lhsT=None):
            p = psT.next()
            if False:
                pass
            else:
                S.op("pe", lambda e, p=p: e.matmul(p[:], lhsT=kT_ap, rhs=q_ap, start=True, stop=True), reads=[kbuf, qbuf], writes=[p])
            E = Et.next()
            S.op("act", lambda e, p=p, E=E: e.activation(out=E[:], in_=p[:], func=AF.Exp), reads=[p], writes=[E])
            return E

        def pipeline(tiles, L=2):
            Es = {}
            n = len(tiles)
            for i in range(n + L):
                if i < n:
                    Es[i] = tiles[i][0]()
                if i - L >= 0:
                    tiles[i - L][1](Es.pop(i - L))

        def mask(E, base, cm, qstep):
            S.op("pool", lambda e, E=E: e.affine_select(out=E[:], in_=E[:], pattern=[[0, 4], [qstep, 128]], compare_op=ALU.is_ge,
                                                       fill=0.0, base=base, channel_multiplier=cm), reads=[E], writes=[E])

        pending_tail = [None]
        for qb in range(SEQ // 128):
            q0 = qb * 128
            q = qt.next(); gg = gt.next(); nz = nzt.next(); ynsa = ynsaR.next()
            S.dma("sp", q[:], scr["qT"].t[:, q0:q0 + 128].rearrange("(h d) t -> d h t", h=8), reads=[scr["qT"]], writes=[q])
            qAs = {}
            for g_ in range(2):
                for a_ in range(min(NA, qb * 128 // HALF + 1)):
                    qa = qAr[(g_, a_)].next()
                    qAs[(g_, a_)] = qa
                    S.dma("sp", qa[0:64, :, :], scr["qT"].t[g_ * 256:(g_ + 1) * 256, q0:q0 + 128].rearrange("(h d) t -> d h t", h=4),
                          reads=[scr["qT"]], writes=[qa])
            S.dma("sp", gg[:], scr["gate"].t[q0:q0 + 128, :], reads=[scr["gate"]], writes=[gg])
            S.dma("sp", nz[:], scr["nzs"].t[q0:q0 + 128, :], reads=[scr["nzs"]], writes=[nz])
            def gbody(g, q=q, gg=gg, nz=nz, qAs=qAs, qb=qb, q0=q0, ynsa=ynsa):
                oc = oc2[g]; rcs = rcs2[g]; rss = rss2[g]; rws = rws2[g]; cc = cc2[g]; sc = sc_2[g]; sc2 = sc2_2[g]
                m1 = m1_2[g]; m2 = m2_2[g]; negq = negq2[g]; pws = pws2[g]; pss = pss2[g]
                q_ap = q[:, 4 * g:4 * g + 4, :].rearrange("d h q -> d (h q)")
                n_max = min(8 * qb + 6, NC - 1)
                ntl = n_max // 128 + 1
                def c_qk(nt, g=g, q_ap=q_ap, q=q):
                    E = qk_exp(kcmpT[:, g, nt * 128:(nt + 1) * 128], kcmpT, q_ap, q)
                    if q0 - 16 * (128 * nt + 127) - 31 < 0:
                        mask(E, q0 - 16 * 128 * nt - 31, -16, 1)
                    return E

                def c_pv(nt, E, g=g, ntl=ntl):
                    for h in range(4):
                        pcx = pcA if h < 2 else pcB
                        first = (nt == 0 and h % 2 == 0)
                        S.op("pe", lambda e, E=E, h=h, pcx=pcx, nt=nt, first=first, g=g, ntl=ntl: e.matmul(
                            pcx[:, h % 2, :], lhsT=E[:, h * 128:(h + 1) * 128], rhs=Rc[:, nt, g, :], start=first,
                            stop=(nt == ntl - 1 and h % 2 == 1), skip_group_check=True),
                            reads=[E, Rc], writes=[pcx] if first else (), pwrites=() if first else [pcx])
                pipeline([(lambda nt=nt: c_qk(nt), lambda E, nt=nt: c_pv(nt, E)) for nt in range(ntl)])
                S.op("act", lambda e: e.copy(out=oc[:, 0:2, :], in_=pcA[:]), reads=[pcA], pwrites=[oc])
                S.op("act", lambda e: e.copy(out=oc[:, 2:4, :], in_=pcB[:]), reads=[pcB], pwrites=[oc])
                S.op("dve", lambda e: e.tensor_scalar(out=rcs[:], in0=oc[:, :, 64], scalar1=1e-30, scalar2=None, op0=ALU.max),
                     reads=[oc], writes=[rcs])
                S.op("dve", lambda e: e.reciprocal(out=rcs[:], in_=rcs[:]), reads=[rcs], writes=[rcs])
                S.op("dve", lambda e: e.tensor_scalar(out=sc[:], in0=oc[:, 0, 65:193], scalar1=rcs[:, 0:1], scalar2=None, op0=ALU.mult),
                     reads=[oc, rcs], writes=[sc])
                for h in range(1, 4):
                    S.op("dve", lambda e, h=h: e.scalar_tensor_tensor(out=sc[:], in0=oc[:, h, 65:193], scalar=rcs[:, h:h + 1], in1=sc[:],
                                                                      op0=ALU.mult, op1=ALU.add), reads=[oc, rcs, sc], writes=[sc])
                for half in range(2):
                    tb = 2 * qb + half
                    ps_ = slice(half * 64, (half + 1) * 64)
                    if tb + 1 < 128:
                        S.op("dve", lambda e, ps_=ps_, tb=tb: e.memset(sc[ps_, tb + 1:128], -1e4), reads=[sc], writes=[sc])
                    lo = max(tb - 1, 0)
                    S.op("dve", lambda e, ps_=ps_, tb=tb, lo=lo: e.memset(sc[ps_, lo:tb + 1], 1e4), reads=[sc], writes=[sc])
                S.op("dve", lambda e: e.memset(sc[:, 0:1], 1e4), reads=[sc], writes=[sc])
                S.op("dve", lambda e: e.max(out=m1[:], in_=sc[:]), reads=[sc], writes=[m1])
                S.op("dve", lambda e: e.match_replace(out=sc2[:], in_to_replace=m1[:], in_values=sc[:], imm_value=-3e4),
                     reads=[sc, m1], writes=[sc2])
                S.op("dve", lambda e: e.max(out=m2[:], in_=sc2[:]), reads=[sc2], writes=[m2])
                S.op("dve", lambda e: e.tensor_scalar(out=negq[:, 0, :], in0=sc[:], scalar1=m2[:, 7:8], scalar2=-1e4, op0=ALU.is_lt, op1=ALU.mult),
                     reads=[sc, m2], pwrites=[negq])
                S.op("dve", lambda e: e.tensor_scalar(out=negq[:, 1, 64:128], in0=sc[:, 0:64], scalar1=m2[:, 7:8], scalar2=-1e4, op0=ALU.is_lt, op1=ALU.mult),
                     reads=[sc, m2], pwrites=[negq])
                yield
                kts = list(range(max(0, qb - 4), qb + 1))

                def w_qk(i, kt, g=g, q_ap=q_ap, q=q, qb=qb):
                    E = qk_exp(kwT[:, g, kt * 128:(kt + 1) * 128], kwT, q_ap, q)
                    if kt == qb - 4:
                        mask(E, -1, 1, -1)
                    if kt == qb:
                        mask(E, 0, -1, 1)
                    return E

                def w_pv(i, kt, E, g=g, kts=kts):
                    for h in range(4):
                        first = (i == 0 and h == 0)
                        S.op("pe", lambda e, E=E, h=h, kt=kt, first=first, last=(i == len(kts) - 1 and h == 3), g=g: e.matmul(
                            pow_[:, h, :], lhsT=E[:, h * 128:(h + 1) * 128], rhs=vw[:, kt, g, :], start=first, stop=last,
                            skip_group_check=True),
                            reads=[E, vw], writes=[pow_] if first else (), pwrites=() if first else [pow_])
                pipeline([(lambda i=i, kt=kt: w_qk(i, kt), lambda E, i=i, kt=kt: w_pv(i, kt, E)) for i, kt in enumerate(kts)])
                S.op("act", lambda e: e.copy(out=pws[:], in_=pow_[:]), reads=[pow_], writes=[pws])
                yield
                na_here = min(NA, qb * 128 // HALF + 1)
                S.op("pe", lambda e: e.transpose(out=pmisc[:, 1, :], in_=negq[:, 1, :], identity=ident[:]), reads=[negq, ident], writes=[pmisc])
                if na_here > 1:
                    S.op("pe", lambda e: e.transpose(out=pmisc[:, 0, :], in_=negq[:, 0, :], identity=ident[:]), reads=[negq, ident], pwrites=[pmisc])
                for a_ in range(na_here):
                    qa = qAs[(g, a_)]
                    S.op("dve", lambda e, qa=qa, a_=a_: e.tensor_copy(out=qa[64:128, :, :],
                                                                   in_=pmisc[64:128, (1 - a_):(2 - a_), :].to_broadcast([64, 4, 128])),
                         reads=[pmisc], pwrites=[qa])
                yield
                def s_qk(kt, g=g, q_ap=q_ap, q=q, qb=qb, qAs=qAs):
                    qa = qAs[(g, kt * 128 // HALF)]
                    E = qk_exp(ksT[:, g, kt * 128:(kt + 1) * 128], ksT, qa[:].rearrange("p h q -> p (h q)"), qa)
                    if kt == qb:
                        mask(E, 0, -1, 1)
                    return E

                def s_pv(kt, E, g=g, qb=qb):
                    for h in range(4):
                        first = (kt == 0 and h == 0)
                        S.op("pe", lambda e, E=E, h=h, kt=kt, first=first, last=(kt == qb and h == 3), g=g: e.matmul(
                            pos[:, h, :], lhsT=E[:, h * 128:(h + 1) * 128], rhs=vs[:, kt, g, :], start=first, stop=last,
                            skip_group_check=True),
                            reads=[E, vs], writes=[pos] if first else (), pwrites=() if first else [pos])
                pipeline([(lambda kt=kt: s_qk(kt), lambda E, kt=kt: s_pv(kt, E)) for kt in range(qb + 1)])
                S.op("act", lambda e: e.copy(out=pss[:], in_=pos[:]), reads=[pos], writes=[pss])
                yield
                S.op("dve", lambda e: e.reciprocal(out=rss[:], in_=pss[:, :, 64]), reads=[pss], writes=[rss])
                S.op("dve", lambda e: e.reciprocal(out=rws[:], in_=pws[:, :, 64]), reads=[pws], writes=[rws])
                gv = gg[:, g * 12:(g + 1) * 12].rearrange("p (h b) -> p b h", b=3)
                for b, rr in enumerate((rcs, rss, rws)):
                    S.op("dve", lambda e, b=b, rr=rr, gv=gv: e.tensor_tensor(out=cc[:, b, :], in0=gv[:, b, :], in1=rr[:], op=ALU.mult),
                         reads=[gg, rr], pwrites=[cc])
                bc = lambda b: cc[:, b, :].unsqueeze(2).to_broadcast([128, 4, 64])
                S.op("dve", lambda e: e.tensor_tensor(out=yg[:], in0=oc[:, :, 0:64], in1=bc(0), op=ALU.mult), reads=[oc, cc], writes=[yg])
                S.op("dve", lambda e: e.tensor_tensor(out=ytmp[:], in0=pss[:, :, 0:64], in1=bc(1), op=ALU.mult), reads=[pss, cc], writes=[ytmp])
                S.op("pool", lambda e: e.tensor_tensor(out=yg[:], in0=yg[:], in1=ytmp[:], op=ALU.add), reads=[yg, ytmp], writes=[yg])
                S.op("dve", lambda e: e.tensor_tensor(out=ytmp[:], in0=pws[:, :, 0:64], in1=bc(2), op=ALU.mult), reads=[pws, cc], writes=[ytmp])
                S.op("pool", lambda e: e.tensor_tensor(out=yg[:], in0=yg[:], in1=ytmp[:], op=ALU.add), reads=[yg, ytmp], writes=[yg])
                S.op("pool", lambda e, g=g, nz=nz: e.tensor_tensor(out=ynsa[:, g * 256:(g + 1) * 256], in0=yg[:].rearrange("p h d -> p (h d)"),
                                                                   in1=nz[:, g * 256:(g + 1) * 256], op=ALU.mult),
                     reads=[yg, nz], pwrites=[ynsa])
                yield
            gens = [gbody(0), gbody(1)]
            for _st in range(5):
                for gen_ in gens:
                    next(gen_)
                if _st == 1 and pending_tail[0] is not None:
                    pending_tail[0]()
                    pending_tail[0] = None

            def tail(ynsa=ynsa, q0=q0):
                for k in range(4):
                    S.op("pe", lambda e, k=k: e.transpose(out=pmisc[:, k, :], in_=ynsa[:, k * 128:(k + 1) * 128], identity=ident[:]),
                         reads=[ynsa, ident], writes=[pmisc] if k == 0 else (), pwrites=() if k == 0 else [pmisc])
                sg = stg.next()
                S.op("act", lambda e, sg=sg: e.copy(out=sg[:], in_=pmisc[:]), reads=[pmisc], writes=[sg])
                S.dma("sp", scr["ysT"].t[0, :, q0:q0 + 128].rearrange("(k p) t -> p k t", p=128), sg[:], reads=[sg], pwrites=[scr["ysT"]], key=sg)
            pending_tail[0] = tail
        if pending_tail[0] is not None:
            pending_tail[0]()
        _barrier(S)
        S.stack_pop()


def phase_D_seq(S, nc, SEQ, lyr, Wd, scr):
    TP = 128
    TB = 8
    GN_EPS = 64e-5
    xtok = scr["xtok"]
    with contextlib.ExitStack() as st:
        S.stack_push(st)
        identF = make_ident(S, "D_ident", F32)
        ones = S.sb("D_ones", [128, 128])
        S.op("pool", lambda e: e.memset(ones[:], 0.0), writes=[ones])
        S.op("pool", lambda e: e.memset(ones[0:64, 0:64], 1.0), reads=[ones], writes=[ones])
        S.op("pool", lambda e: e.memset(ones[64:128, 64:128], 1.0), reads=[ones], writes=[ones])

        def cvec(name, key, n):
            t = S.sb("D_" + name, [128, n])
            S.dma("sp", t[:], Wd[key].t[lyr].rearrange("(c p) -> p c", p=128), reads=[Wd[key]], writes=[t],
                  allow_slow_non_contiguous=True)
            return t

        def cvec2(name, key):
            t = S.sb("D_" + name, [128, 4])
            S.dma("sp", t[:], Wd[key].t[lyr].rearrange("(c a) j -> (a j) c", a=2), reads=[Wd[key]], writes=[t],
                  allow_slow_non_contiguous=True)
            return t
        mu = cvec("mu", "rk_mu", 13)
        w0 = cvec("w0", "rk_w0", 4)
        a0 = cvec("a0", "rk_a0", 4)
        lg = cvec("lg", "rk_lnx_g", 4)
        lb = cvec("lb", "rk_lnx_b", 4)
        kkc = cvec2("kkc", "rk_kk")
        ka = cvec2("ka", "rk_ka")
        rkc = cvec2("rkc", "rk_rk")
        omka = S.sb("D_omka", [128, 4])
        S.op("pool", lambda e: e.tensor_scalar(out=omka[:], in0=ka[:], scalar1=-1.0, scalar2=1.0, op0=ALU.mult, op1=ALU.add),
             reads=[ka], writes=[omka])
        w2 = S.sb("D_w2", [64, 512], BF16)
        a2 = S.sb("D_a2", [128, 512], BF16)
        S.dma("pool", w2[:], Wd["rk_w2"].t[lyr], reads=[Wd["rk_w2"]], writes=[w2])
        S.dma("pool", a2[64:128, :], Wd["rk_a2"].t[lyr], reads=[Wd["rk_a2"]], writes=[a2])
        St = S.sb("D_state", [128, 4, 64])
        S.op("dve", lambda e: e.memset(St[:], 0.0), writes=[St])

        rst = S.sb("D_rst", [128, 13, TP + 1])
        xs = S.sb("D_xs", [128, 13, TP])
        th = S.sb("D_th", [128, TP], BF16)
        dd = S.sb("D_dd", [128, 4, TP])
        aa = S.sb("D_aa", [128, 4, TP])
        kkf = S.sb("D_kkf", [128, 4, TP])
        sq = S.sb("D_sq", [128, 4, TP])
        rn = S.sb("D_rn", [128, 4, TP])
        kp = S.sb("D_kp", [128, 4, TP])
        am = S.sb("D_am", [128, 4, TP])
        bm = S.sb("D_bm", [128, 4, TP])
        t1 = S.sb("D_t1", [128, 4, TP])
        bonus = S.sb("D_bonus", [128, 4, TP])
        vv = S.sb("D_vv", [128, 4, TP])
        tk = S.sb("D_tk", [128, 5, 4, 128])
        bcr = Ring([S.sb("D_bc%d" % i, [128, TB, 5, 256]) for i in range(2)])
        tmp = S.sb("D_tmp", [128, 4, 64])
        tmp2 = S.sb("D_tmp2", [128, 4, 64])
        kv = Ring([S.sb("D_kv%d" % i, [128, 4, 64]) for i in range(2)])
        sa = S.sb("D_sa", [128, 4])
        ybuf = S.sb("D_y", [128, 4, TP])
        ysq = S.sb("D_ysq", [128, 4, TP])
        mean = S.sb("D_mean", [128, 4, TP])
        var = S.sb("D_var", [128, 4, TP])
        rzt = S.sb("D_rz", [128, 4, TP], BF16)
        yo = S.sb("D_yo", [128, 4, TP], BF16)
        pa = Ring([S.ps("D_pa%d" % i, [128, 4, 128]) for i in range(4)])

        bc4 = lambda t: t[:].unsqueeze(2).to_broadcast([128, 4, TP])
        for nb in range(SEQ // TP):
            t0 = nb * TP
            S.dma("sp", rst[:, :, 1:TP + 1], scr["rsT"].t[:, t0:t0 + TP].rearrange("(c p) t -> p c t", p=128), reads=[scr["rsT"]],
                  writes=[rst])
            if nb == 0:
                S.op("pool", lambda e: e.memset(rst[:, :, 0:1], 0.0), reads=[rst], pwrites=[rst])
            else:
                S.dma("sp", rst[:, :, 0:1], scr["rsT"].t[:, t0 - 1:t0].rearrange("(c p) t -> p c t", p=128), reads=[scr["rsT"]],
                      pwrites=[rst], key=rst, allow_slow_non_contiguous=True)
            S.op("pool", lambda e: e.tensor_tensor(out=xs[:], in0=rst[:, :, 0:TP], in1=rst[:, :, 1:TP + 1], op=ALU.subtract),
                 reads=[rst], writes=[xs])
            S.op("pool", lambda e: e.tensor_tensor(out=xs[:], in0=xs[:], in1=mu[:].unsqueeze(2).to_broadcast([128, 13, TP]), op=ALU.mult),
                 reads=[xs, mu], writes=[xs])
            S.op("pool", lambda e: e.tensor_tensor(out=xs[:], in0=xs[:], in1=rst[:, :, 1:TP + 1], op=ALU.add), reads=[xs, rst], writes=[xs])
            r = xs[:, 0:4, :]; k = xs[:, 4:8, :]; v = xs[:, 8:12, :]
            S.op("act", lambda e: e.activation(out=th[0:64, :], in_=xs[0:64, 12, :], func=AF.Tanh), reads=[xs], pwrites=[th])
            S.op("act", lambda e: e.copy(out=th[64:128, :], in_=xs[64:128, 12, :]), reads=[xs], pwrites=[th])
            pw = pa.next(); pp = pa.next()
            for p in range(4):
                S.op("pe", lambda e, p=p, pw=pw: e.matmul(pw[:, p, :], lhsT=w2[0:64, p * 128:(p + 1) * 128], rhs=th[0:64, :], start=True, stop=True),
                     reads=[w2, th], writes=[pw] if p == 0 else (), pwrites=() if p == 0 else [pw])
                S.op("pe", lambda e, p=p, pp=pp: e.matmul(pp[:, p, :], lhsT=a2[64:128, p * 128:(p + 1) * 128], rhs=th[64:128, :], start=True, stop=True),
                     reads=[a2, th], writes=[pp] if p == 0 else (), pwrites=() if p == 0 else [pp])
            for p in range(4):
                S.op("act", lambda e, p=p, pw=pw: e.activation(out=dd[:, p, :], in_=pw[:, p, :], func=AF.Sigmoid, bias=w0[:, p:p + 1]),
                     reads=[pw, w0], pwrites=[dd])
                S.op("act", lambda e, p=p, pp=pp: e.activation(out=aa[:, p, :], in_=pp[:, p, :], func=AF.Sigmoid, bias=a0[:, p:p + 1]),
                     reads=[pp, a0], pwrites=[aa])
            S.op("act", lambda e: e.activation(out=dd[:], in_=dd[:], func=AF.Exp, scale=-0.6065306597126334), reads=[dd], writes=[dd])
            S.op("pool", lambda e: e.tensor_tensor(out=kkf[:], in0=k, in1=bc4(kkc), op=ALU.mult), reads=[xs, kkc], writes=[kkf])
            S.op("pool", lambda e: e.tensor_tensor(out=sq[:], in0=kkf[:], in1=kkf[:], op=ALU.mult), reads=[kkf], writes=[sq])
            pn = pa.next()
            for p in range(4):
                S.op("pe", lambda e, p=p, pn=pn: e.matmul(pn[:, p, :], lhsT=ones[:], rhs=sq[:, p, :], start=True, stop=True),
                     reads=[ones, sq], writes=[pn] if p == 0 else (), pwrites=() if p == 0 else [pn])
            S.op("act", lambda e, pn=pn: e.activation(out=rn[:], in_=pn[:], func=AF.Sqrt), reads=[pn], writes=[rn])
            S.op("pool", lambda e: e.tensor_scalar(out=rn[:], in0=rn[:], scalar1=1e-12, scalar2=None, op0=ALU.max), reads=[rn], writes=[rn])
            S.op("dve", lambda e: e.reciprocal(out=rn[:], in_=rn[:]), reads=[rn], writes=[rn])
            S.op("pool", lambda e: e.tensor_tensor(out=kkf[:], in0=kkf[:], in1=rn[:], op=ALU.mult), reads=[kkf, rn], writes=[kkf])
            S.op("pool", lambda e: e.tensor_tensor(out=t1[:], in0=aa[:], in1=bc4(ka), op=ALU.mult), reads=[aa, ka], writes=[t1])
            S.op("pool", lambda e: e.tensor_tensor(out=t1[:], in0=t1[:], in1=bc4(omka), op=ALU.add), reads=[t1, omka], writes=[t1])
            S.op("pool", lambda e: e.tensor_tensor(out=kp[:], in0=k, in1=t1[:], op=ALU.mult), reads=[xs, t1], writes=[kp])
            S.op("pool", lambda e: e.tensor_scalar(out=am[:], in0=kkf[:], scalar1=-1.0, scalar2=None, op0=ALU.mult), reads=[kkf], writes=[am])
            S.op("pool", lambda e: e.tensor_tensor(out=bm[:], in0=kkf[:], in1=aa[:], op=ALU.mult), reads=[kkf, aa], writes=[bm])
            S.op("pool", lambda e: e.tensor_tensor(out=t1[:], in0=r, in1=kp[:], op=ALU.mult), reads=[xs, kp], writes=[t1])
            S.op("pool", lambda e: e.tensor_tensor(out=sq[:], in0=t1[:], in1=bc4(rkc), op=ALU.mult), reads=[t1, rkc], writes=[sq])
            pr = pa.next()
            for p in range(4):
                S.op("pe", lambda e, p=p, pr=pr: e.matmul(pr[:, p, :], lhsT=ones[:], rhs=sq[:, p, :], start=True, stop=True),
                     reads=[ones, sq], writes=[pr] if p == 0 else (), pwrites=() if p == 0 else [pr])
            S.op("act", lambda e, pr=pr: e.copy(out=bonus[:], in_=pr[:]), reads=[pr], writes=[bonus])
            S.op("pool", lambda e: e.tensor_tensor(out=bonus[:], in0=bonus[:], in1=v, op=ALU.mult), reads=[bonus, xs], writes=[bonus])
            S.op("pool", lambda e: e.tensor_copy(out=vv[:], in_=v), reads=[xs], writes=[vv])
            S.op("pool", lambda e: e.tensor_copy(out=t1[:], in_=r), reads=[xs], writes=[t1])
            for oi, src in enumerate((am, bm, dd, kp, t1)):
                pt = pa.next()
                for p in range(4):
                    S.op("pe", lambda e, p=p, src=src, pt=pt: e.transpose(out=pt[:, p, :], in_=src[:, p, :], identity=identF[:]),
                         reads=[src, identF], writes=[pt] if p == 0 else (), pwrites=() if p == 0 else [pt])
                S.op("act", lambda e, oi=oi, pt=pt: e.copy(out=tk[:, oi, :, :], in_=pt[:]), reads=[pt], pwrites=[tk])
            for oi in range(5):
                for h2 in range(2):
                    S.dma("sp", xtok.t[t0:t0 + TP, oi, h2, :].rearrange("t (p j) -> t p j", p=4), tk[:, oi, :, h2 * 64:(h2 + 1) * 64],
                          reads=[tk], pwrites=[xtok], key=tk)
            S.dma("sp", rzt[:], scr["rzT"].t[:, t0:t0 + TP].rearrange("(c p) t -> p c t", p=128), reads=[scr["rzT"]], writes=[rzt])
            xflat = xtok.t.rearrange("t o h c -> (t o) h c")
            for tb in range(0, TP, TB):
                bc = bcr.next()
                for h2 in range(2):
                    S.dma("sp", bc[h2 * 64:(h2 + 1) * 64, :, :, :].rearrange("p t o c -> p (t o) c"),
                          xflat[(t0 + tb) * 5:(t0 + tb + TB) * 5, h2, :].partition_broadcast(64),
                          reads=[xtok], writes=[bc] if h2 == 0 else (), pwrites=() if h2 == 0 else [bc], key=bc)
                for tt in range(TB):
                    t = tb + tt
                    A = bc[:, tt, 0, :].rearrange("p (a j) -> p a j", a=4)
                    B = bc[:, tt, 1, :].rearrange("p (a j) -> p a j", a=4)
                    Dd = bc[:, tt, 2, :].rearrange("p (a j) -> p a j", a=4)
                    Kk = bc[:, tt, 3, :].rearrange("p (a j) -> p a j", a=4)
                    R = bc[:, tt, 4, :].rearrange("p (a j) -> p a j", a=4)
                    kvb = kv.next()
                    S.op("pool", lambda e, Kk=Kk, t=t, kvb=kvb: e.tensor_tensor(out=kvb[:], in0=Kk, in1=vv[:, :, t:t + 1].to_broadcast([128, 4, 64]),
                                                                              op=ALU.mult), reads=[bc, vv], writes=[kvb])
                    S.op("dve", lambda e, A=A: e.tensor_tensor(out=tmp[:], in0=St[:], in1=A, op=ALU.mult), reads=[St, bc], writes=[tmp])
                    S.op("dve", lambda e: e.tensor_reduce(out=sa[:], in_=tmp[:], axis=AX.X, op=ALU.add), reads=[tmp], writes=[sa])
                    S.op("dve", lambda e, Dd=Dd: e.tensor_tensor(out=St[:], in0=St[:], in1=Dd, op=ALU.mult), reads=[St, bc, tmp], writes=[St])
                    S.op("dve", lambda e, B=B: e.tensor_tensor(out=tmp2[:], in0=B, in1=sa[:].unsqueeze(2).to_broadcast([128, 4, 64]), op=ALU.mult),
                         reads=[bc, sa], writes=[tmp2])
                    S.op("dve", lambda e: e.tensor_tensor(out=St[:], in0=St[:], in1=tmp2[:], op=ALU.add), reads=[St, tmp2], writes=[St])
                    S.op("dve", lambda e, kvb=kvb: e.tensor_tensor(out=St[:], in0=St[:], in1=kvb[:], op=ALU.add), reads=[St, kvb], writes=[St])
                    S.op("dve", lambda e, R=R: e.tensor_tensor(out=tmp[:], in0=St[:], in1=R, op=ALU.mult), reads=[St, bc], writes=[tmp])
                    S.op("dve", lambda e, t=t: e.tensor_reduce(out=ybuf[:, :, t], in_=tmp[:], axis=AX.X, op=ALU.add), reads=[tmp], pwrites=[ybuf])
            S.op("pool", lambda e: e.tensor_tensor(out=ysq[:], in0=ybuf[:], in1=ybuf[:], op=ALU.mult), reads=[ybuf], writes=[ysq])
            pm = pa.next(); pq = pa.next()
            for p in range(4):
                S.op("pe", lambda e, p=p, pm=pm: e.matmul(pm[:, p, :], lhsT=ones[:], rhs=ybuf[:, p, :], start=True, stop=True),
                     reads=[ones, ybuf], writes=[pm] if p == 0 else (), pwrites=() if p == 0 else [pm])
                S.op("pe", lambda e, p=p, pq=pq: e.matmul(pq[:, p, :], lhsT=ones[:], rhs=ysq[:, p, :], start=True, stop=True),
                     reads=[ones, ysq], writes=[pq] if p == 0 else (), pwrites=() if p == 0 else [pq])
            S.op("act", lambda e, pm=pm: e.activation(out=mean[:], in_=pm[:], func=AF.Copy, scale=1.0 / 64), reads=[pm], writes=[mean])
            S.op("act", lambda e, pq=pq: e.activation(out=var[:], in_=pq[:], func=AF.Copy, scale=1.0 / 64), reads=[pq], writes=[var])
            S.op("pool", lambda e: e.tensor_tensor(out=ysq[:], in0=mean[:], in1=mean[:], op=ALU.mult), reads=[mean, ysq], writes=[ysq])
            S.op("pool", lambda e: e.tensor_tensor(out=var[:], in0=var[:], in1=ysq[:], op=ALU.subtract), reads=[var, ysq], writes=[var])
            S.op("act", lambda e: e.activation(out=var[:], in_=var[:], func=AF.Sqrt, bias=GN_EPS, scale=1.0), reads=[var], writes=[var])
            S.op("dve", lambda e: e.reciprocal(out=var[:], in_=var[:]), reads=[var], writes=[var])
            S.op("pool", lambda e: e.tensor_tensor(out=mean[:], in0=ybuf[:], in1=mean[:], op=ALU.subtract), reads=[ybuf, mean], writes=[mean])
            S.op("pool", lambda e: e.tensor_tensor(out=mean[:], in0=mean[:], in1=var[:], op=ALU.mult), reads=[mean, var], writes=[mean])
            S.op("pool", lambda e: e.tensor_tensor(out=mean[:], in0=mean[:], in1=bc4(lg), op=ALU.mult), reads=[mean, lg], writes=[mean])
            S.op("pool", lambda e: e.tensor_tensor(out=mean[:], in0=mean[:], in1=bc4(lb), op=ALU.add), reads=[mean, lb], writes=[mean])
            S.op("pool", lambda e: e.tensor_tensor(out=mean[:], in0=mean[:], in1=bonus[:], op=ALU.add), reads=[mean, bonus], writes=[mean])
            S.op("pool", lambda e: e.tensor_tensor(out=yo[:], in0=mean[:], in1=rzt[:], op=ALU.mult), reads=[mean, rzt], writes=[yo])
            S.dma("sp", scr["ysT"].t[2, :, t0:t0 + TP].rearrange("(c p) t -> p c t", p=128), yo[:], reads=[yo], pwrites=[scr["ysT"]], key=yo)
        _barrier(S)
        S.stack_pop()


_NC_CACHE = {}


def kernel(**inputs):
    SEQ = 8192
    if "nc" not in _NC_CACHE:
        _NC_CACHE["nc"] = build(SEQ, nlayers=2, enable=(1, 1, 1), scr_kind="Internal")
    nc = _NC_CACHE["nc"]
    x = np.ascontiguousarray(np.asarray(inputs["x"], dtype=np.float32))
    p = np.asarray(inputs["p"], dtype=np.float32)
    base = {}
    for k in WSPEC:
        v = np.ascontiguousarray(np.asarray(inputs[k], dtype=np.float32))
        base[k] = v.reshape(WSPEC[k])
    in_maps = []
    for b in range(8):
        m = dict(base)
        m["x"] = np.ascontiguousarray(x[b])
        m["p"] = np.ascontiguousarray(p[:, b])
        in_maps.append(m)
    res = run_bass_kernel_spmd(nc, in_maps, core_ids=list(range(8)))
    return np.stack([np.asarray(r["out"], dtype=np.float32) for r in res.results], axis=0)
```

```python
import contextlib
import numpy as np
import concourse.bass as bass
import concourse.mybir as mybir

F32 = mybir.dt.float32
BF16 = mybir.dt.bfloat16
AF = mybir.ActivationFunctionType
ALU = mybir.AluOpType
AX = mybir.AxisListType

ENGS = ("pe", "act", "dve", "pool", "sp")


class Buf:
    __slots__ = ("name", "w", "wfull", "r", "t")

    def __init__(self, name, t=None):
        self.name = name
        self.t = t
        self.w = []
        self.wfull = []
        self.r = []

    def __getitem__(self, k):
        return self.t[k]


class Op:
    __slots__ = ("eng", "fn", "deps", "marked", "tick", "dma", "idx")

    def __init__(self, eng, fn, dma):
        self.eng = eng
        self.fn = fn
        self.deps = []
        self.marked = False
        self.tick = None
        self.dma = dma
        self.idx = None


class DmaSem:
    def __init__(self):
        self.sem = None
        self.count = 0


class Sched:
    def __init__(self, nc, stack):
        self.nc = nc
        self.stack = stack
        self.ops = {e: [] for e in ENGS}
        self.all_ops = []
        self.dsems = {}
        self.n_sems = 0
        self.fence = []
        self.stacks = [stack]
        self.phase_keys = []
        self.free_ds = []
        self.all_ds = []
        self.keep = []

    def stack_push(self, st):
        self.stacks.append(st)
        self.phase_keys.append([])

    def stack_pop(self):
        self.stacks.pop()
        for kid in self.phase_keys.pop():
            ds = self.dsems.pop(kid, None)
            if ds is not None:
                self.free_ds.append(ds)

    def sb(self, name, shape, dt=F32):
        self.n_sems += 1
        name = "%s_u%d" % (name, self.n_sems)
        t = self.stacks[-1].enter_context(self.nc.sbuf_tensor(name, list(shape), dt))
        return Buf(name, t)

    def ps(self, name, shape, dt=F32):
        self.n_sems += 1
        name = "%s_u%d" % (name, self.n_sems)
        t = self.stacks[-1].enter_context(self.nc.psum_tensor(name, list(shape), dt))
        return Buf(name, t)

    def dram(self, name, shape, dt, kind="Internal"):
        t = self.nc.dram_tensor(name, list(shape), dt, kind=kind)
        return Buf(name, t.ap())

    def _add(self, eng, fn, reads, writes, pwrites, dma):
        op = Op(eng, fn, dma)
        deps = list(self.fence)
        for b in reads:
            deps.extend(b.w)
        for b in writes:
            deps.extend(b.w)
            deps.extend(b.r)
        for b in pwrites:
            deps.extend(b.wfull)
            deps.extend(b.r)
        seen = set()
        for d in deps:
            if id(d) in seen or d is op:
                continue
            seen.add(id(d))
            if d.eng == "pe" and eng == "pe" and d.dma is None and dma is None:
                continue
            op.deps.append(d)
            d.marked = True
        for b in reads:
            b.r.append(op)
            if len(b.r) > 24:
                b.r = self._prune(b.r)
        for b in writes:
            b.w = [op]
            b.wfull = [op]
            b.r = []
        for b in pwrites:
            b.w.append(op)
            if len(b.w) > 24:
                b.w = self._prune(b.w)
        op.idx = len(self.all_ops)
        self.all_ops.append(op)
        self.ops[eng].append(op)
        return op

    @staticmethod
    def _prune(lst):
        last = {}
        for o in lst:
            key = (o.eng, None) if o.dma is None else ("dma", id(o.dma))
            last[key] = o
        return list(last.values())

    def op(self, eng, fn, reads=(), writes=(), pwrites=()):
        return self._add(eng, fn, reads, writes, pwrites, None)

    def dma(self, eng, out_ap, in_ap, reads=(), writes=(), pwrites=(), key=None, **kw):
        if key is None:
            key = (list(writes) + list(pwrites))[0]
        ds = self.dsems.get(id(key))
        if ds is None:
            if self.free_ds:
                ds = self.free_ds.pop()
            else:
                ds = DmaSem()
                self.all_ds.append(ds)
            self.dsems[id(key)] = ds
            self.keep.append(key)
            if self.phase_keys:
                self.phase_keys[-1].append(id(key))
        fn = lambda e, o=out_ap, i=in_ap, kw=kw: e.dma_start(out=o, in_=i, **kw)
        op = self._add(eng, fn, reads, writes, pwrites, ds)
        ds.count += 16
        op.tick = ds.count
        return op

    def barrier_bufs(self, bufs):
        pass

    def emit(self):
        nc = self.nc
        stack = self.stack
        esem = {}
        for e in ENGS:
            esem[e] = stack.enter_context(nc.semaphore("s_" + e))
        for ds in self.all_ds:
            ds.sem = stack.enter_context(nc.semaphore("d%d" % self.n_sems))
            self.n_sems += 1
        for e in ENGS:
            c = 0
            for o in self.ops[e]:
                if o.dma is None:
                    if o.marked:
                        c += 1
                        o.tick = c
        self.max_ticks = {e: max([o.tick or 0 for o in self.ops[e] if o.dma is None] + [0]) for e in ENGS}

        def evkey(d):
            if d.dma is not None:
                return ("d", id(d.dma)), d.dma.sem, d.tick
            return ("e", d.eng), esem[d.eng], d.tick

        def run(eng_name, eng):
            seen = {}
            for o in self.ops[eng_name]:
                waits = {}
                for d in o.deps:
                    k, sem, val = evkey(d)
                    if seen.get(k, 0) >= val:
                        continue
                    if k not in waits or waits[k][1] < val:
                        waits[k] = (sem, val)
                for k, (sem, val) in waits.items():
                    eng.wait_ge(sem, val)
                    seen[k] = val
                inst = o.fn(eng)
                if o.dma is not None:
                    inst.then_inc(o.dma.sem, 16)
                elif o.marked:
                    inst.then_inc(esem[eng_name], 1)
            if eng_name == "sp":
                for e2 in ENGS:
                    m = self.max_ticks[e2]
                    if m > 0:
                        eng.wait_ge(esem[e2], m)
                for ds in self.all_ds:
                    if ds.count:
                        eng.wait_ge(ds.sem, ds.count)

        block = stack.enter_context(nc.Block())

        @block.tensor
        def _(e):
            run("pe", e)

        @block.scalar
        def _(e):
            run("act", e)

        @block.vector
        def _(e):
            run("dve", e)

        @block.gpsimd
        def _(e):
            run("pool", e)

        @block.sync
        def _(e):
            run("sp", e)


from concourse.bass_utils import run_bass_kernel_spmd

D = 1024
NCOL = 8600
PLE = 256
EPS = 1e-6


DEBUG = {}
_dbg_n = [0]


def dbg_dump(S, name, ap, buf, shape, cond=True):
    if not DEBUG.get("on") or not cond:
        return
    _dbg_n[0] += 1
    t = S.stacks[-1].enter_context(S.nc.sbuf_tensor("dbgsb_%d" % _dbg_n[0], list(shape), F32))
    tb = Buf("dbgsb", t)
    d = S.dram("dbg_" + name, list(shape), F32, kind="ExternalOutput")
    S.op("act", lambda e: e.copy(out=t[:], in_=ap), reads=[buf], writes=[tb])
    S.dma("sp", d.t, t[:], reads=[tb], writes=[d], key=tb)


class Ring:
    def __init__(self, bufs):
        self.bufs = bufs
        self.i = 0

    def next(self):
        b = self.bufs[self.i % len(self.bufs)]
        self.i += 1
        return b


def _barrier(S):
    fence = []
    for e in ENGS:
        comp = [o for o in S.ops[e] if o.dma is None]
        if comp:
            fence.append(comp[-1])
    lastd = {}
    for o in S.all_ops:
        if o.dma is not None:
            lastd[id(o.dma)] = o
    fence.extend(lastd.values())
    S.fence = fence


def make_ident(S, name="ident", dt=BF16):
    ident = S.sb(name, [128, 128], dt)
    S.op("pool", lambda e: e.memset(ident[:], 0.0), writes=[ident])
    S.op("pool", lambda e: e.affine_select(out=ident[:], in_=ident[:], pattern=[[-1, 128]],
                                           compare_op=ALU.not_equal, fill=1.0, base=0,
                                           channel_multiplier=1), reads=[ident], writes=[ident])
    return ident


def load_w_bf16(S, dst, k, src_ap, srcbuf):
    S.dma("pool", dst, src_ap, reads=[srcbuf], pwrites=[k], key=k, max_dma_last_dim=4096)


def rmsnorm_tile(S, xt_ap, xt_buf, g_buf, h_ap, h_buf, sq, ss, rs, eps=EPS, extra_reads=()):
    S.op("act", lambda e: e.activation(out=sq[:], in_=xt_ap, func=AF.Square, accum_out=ss[:]),
         reads=[xt_buf] + list(extra_reads), writes=[sq, ss])
    S.op("act", lambda e: e.activation(out=rs[:], in_=ss[:], func=AF.Sqrt, scale=1.0 / D, bias=eps),
         reads=[ss], writes=[rs])
    S.op("dve", lambda e: e.reciprocal(out=rs[:], in_=rs[:]), reads=[rs], writes=[rs])
    S.op("dve", lambda e: e.scalar_tensor_tensor(out=h_ap, in0=xt_ap, scalar=rs[:, 0:1], in1=g_buf[:],
                                                 op0=ALU.mult, op1=ALU.mult),
         reads=[xt_buf, rs, g_buf], pwrites=[h_buf])


def phase_A(S, nc, SEQ, lyr, x_src, Wd, scr):
    TT = 512
    nsub = TT // 128
    with contextlib.ExitStack() as st:
        S.stack_push(st)
        wt = S.sb("A_w", [128, 8, NCOL], BF16)
        gt = S.sb("A_g", [128, D])
        ident = make_ident(S, "A_ident")
        xt = S.sb("A_x", [128, nsub, D])
        sq = S.sb("A_sq", [128, D], BF16)
        ss = S.sb("A_ss", [128, 1])
        rs = S.sb("A_rs", [128, 1])
        h = S.sb("A_h", [128, nsub, D], BF16)
        hT = S.sb("A_hT", [128, 8, TT], BF16)
        stg_b = Ring([S.sb("A_sb%d" % i, [128, 512], BF16) for i in range(4)])
        stg_f = Ring([S.sb("A_sf%d" % i, [128, 512], F32) for i in range(3)])
        pT = Ring([S.ps("A_pT%d" % i, [128, 8, 128], BF16) for i in range(2)])
        pacc = Ring([S.ps("A_pa%d" % i, [128, 512], F32) for i in range(6)])

        w_in = Wd["w_in"]
        WG = [(0, 1280), (3352, 5528), (5528, 7064), (7064, 8600), (1280, 3352)]
        wtg = [Buf("A_wg%d" % i, wt.t) for i in range(len(WG))]

        def wgrp(c0):
            for i, (lo, hi) in enumerate(WG):
                if lo <= c0 < hi:
                    return wtg[i]
            raise ValueError(c0)
        for gi, (lo, hi) in enumerate(WG):
            for k in range(8):
                S.dma("pool", wt[:, k, lo:hi], w_in.t[lyr, k * 128:(k + 1) * 128, lo:hi], reads=[w_in], pwrites=[wtg[gi]],
                      key=wtg[gi], max_dma_last_dim=4096)
        S.dma("sp", gt[:], Wd["norm_g"].t[lyr:lyr + 1, :].partition_broadcast(128), reads=[Wd["norm_g"]],
              writes=[gt])

        FM = []
        for c in range(4):
            FM.append((c * 128, scr["qT"], c * 128, AF.Copy, 0.125, BF16))
        FM.append((512, scr["kcT"], 0, None, 1.0, BF16))
        FM.append((640, scr["vcT"], 0, None, 1.0, BF16))
        FM.append((768, scr["ksT"], 0, None, 1.0, BF16))
        FM.append((1024, scr["kwT"], 0, None, 1.0, BF16))
        for c in range(13):
            FM.append((3352 + c * 128, scr["rsT"], c * 128, None, 1.0, F32))
        for c in range(4):
            FM.append((5016 + c * 128, scr["rzT"], c * 128, AF.Silu, 1.0, BF16))
        for c in range(24):
            FM.append((5528 + c * 128, scr["mgT"], c * 128, AF.Sigmoid, 1.0, BF16))
        TM = [
            (896, 128, scr["vsw"], 0, None, BF16),
            (1152, 128, scr["vsw"], 128, None, BF16),
            (1280, 24, scr["gate"], 0, AF.Sigmoid, F32),
            (1304, 512, scr["nzs"], 0, AF.Silu, BF16),
            (1816, 512, scr["su"], 0, None, F32),
            (2328, 512, scr["sv"], 0, None, F32),
            (2840, 512, scr["szs"], 0, AF.Silu, BF16),
        ]
        evac_i = [0]

        def evac(out_ap, out_buf, in_ap, in_buf, func, scale):
            if func is None and scale == 1.0:
                if evac_i[0] % 2 == 0:
                    S.op("dve", lambda e: e.tensor_copy(out=out_ap, in_=in_ap), reads=[in_buf], writes=[out_buf])
                else:
                    S.op("act", lambda e: e.copy(out=out_ap, in_=in_ap), reads=[in_buf], writes=[out_buf])
                evac_i[0] += 1
            else:
                S.op("act", lambda e: e.activation(out=out_ap, in_=in_ap, func=func, scale=scale),
                     reads=[in_buf], writes=[out_buf])

        for ti in range(SEQ // TT):
            t0 = ti * TT
            S.dma("sp", xt[:], x_src.t[t0:t0 + TT, :].rearrange("(s p) d -> p s d", p=128), reads=[x_src],
                  writes=[xt])
            for s in range(nsub):
                rmsnorm_tile(S, xt[:, s, :], xt, gt, h[:, s, :], h, sq, ss, rs)
                pt = pT.next()
                for k in range(8):
                    S.op("pe", lambda e, k=k, s=s, pt=pt: e.transpose(out=pt[:, k, :], in_=h[:, s, k * 128:(k + 1) * 128],
                                                                     identity=ident[:]),
                         reads=[h, ident], writes=[pt] if k == 0 else (), pwrites=() if k == 0 else [pt])
                S.op("dve", lambda e, s=s, pt=pt: e.tensor_copy(out=hT[:, :, s * 128:(s + 1) * 128], in_=pt[:]),
                     reads=[pt], pwrites=[hT])
            for (c0, dbuf, r0, func, scale, dt) in FM:
                pa = pacc.next()
                for k in range(8):
                    S.op("pe", lambda e, k=k, pa=pa, c0=c0: e.matmul(pa[:], lhsT=wt[:, k, c0:c0 + 128], rhs=hT[:, k, :],
                                                                    start=(k == 0), stop=(k == 7)),
                         reads=[wgrp(c0), hT], writes=[pa] if k == 0 else (), pwrites=() if k == 0 else [pa])
                sg = stg_b.next() if dt == BF16 else stg_f.next()
                evac(sg[:], sg, pa[:], pa, func, scale)
                S.dma("sp", dbuf.t[r0:r0 + 128, t0:t0 + TT], sg[:], reads=[sg], pwrites=[dbuf], key=sg)
            for s in range(nsub):
                for (c0, ncol, dbuf, dc0, func, dt) in TM:
                    pa = pacc.next()
                    for k in range(8):
                        S.op("pe", lambda e, k=k, pa=pa, c0=c0, ncol=ncol, s=s: e.matmul(
                            pa[:, 0:ncol], lhsT=hT[:, k, s * 128:(s + 1) * 128], rhs=wt[:, k, c0:c0 + ncol],
                            start=(k == 0), stop=(k == 7)),
                            reads=[wgrp(c0), hT], writes=[pa] if k == 0 else (), pwrites=() if k == 0 else [pa])
                    sg = stg_b.next() if dt == BF16 else stg_f.next()
                    evac(sg[:, 0:ncol], sg, pa[:, 0:ncol], pa, func, 1.0)
                    S.dma("sp", dbuf.t[t0 + s * 128:t0 + (s + 1) * 128, dc0:dc0 + ncol], sg[:, 0:ncol], reads=[sg],
                          pwrites=[dbuf], key=sg)
        _barrier(S)
        S.stack_pop()


def make_scratch(S, SEQ, kind="Internal"):
    scr = {}
    def mk(name, shape, dt):
        scr[name] = S.dram(name, shape, dt, kind=kind)
    mk("qT", [512, SEQ], BF16)
    mk("kcT", [128, SEQ], BF16)
    mk("vcT", [128, SEQ], BF16)
    mk("ksT", [128, SEQ], BF16)
    mk("kwT", [128, SEQ], BF16)
    mk("vsw", [SEQ, 256], BF16)
    mk("gate", [SEQ, 24], F32)
    mk("nzs", [SEQ, 512], BF16)
    mk("su", [SEQ, 512], F32)
    mk("sv", [SEQ, 512], F32)
    mk("szs", [SEQ, 512], BF16)
    mk("rsT", [1664, SEQ], F32)
    mk("rzT", [512, SEQ], BF16)
    mk("mgT", [3072, SEQ], BF16)
    mk("ysT", [3, 512, SEQ], BF16)
    mk("xtok", [SEQ, 5, 2, 256], F32)
    return scr


def phase_C(S, nc, SEQ, lyr, Wd, scr):
    LN_EPS = 1e-5
    with contextlib.ExitStack() as st:
        S.stack_push(st)
        ident = make_ident(S, "C_ident")
        wraw = S.sb("C_wraw", [128, 8, 128])
        wbf = S.sb("C_wbf", [128, 8, 128], BF16)
        WT = S.sb("C_WT", [128, 8, 128], BF16)
        bsT = S.sb("C_bsT", [128, 8])
        lng = S.sb("C_lng", [128, 512])
        lnb = S.sb("C_lnb", [128, 512])
        pw = S.ps("C_pw", [128, 8, 128], BF16)
        S.dma("sp", wraw[:], Wd["sg_w"].t[lyr].rearrange("g t s -> t g s"), reads=[Wd["sg_w"]], writes=[wraw])
        S.dma("sp", bsT[:], Wd["sg_b"].t[lyr].rearrange("g t -> t g"), reads=[Wd["sg_b"]], writes=[bsT],
              allow_slow_non_contiguous=True)
        S.dma("sp", lng[:], Wd["sg_ln_g"].t[lyr:lyr + 1, :].partition_broadcast(128), reads=[Wd["sg_ln_g"]], writes=[lng])
        S.dma("sp", lnb[:], Wd["sg_ln_b"].t[lyr:lyr + 1, :].partition_broadcast(128), reads=[Wd["sg_ln_b"]], writes=[lnb])
        S.op("pool", lambda e: e.affine_select(out=wraw[:], in_=wraw[:], pattern=[[0, 8], [-1, 128]],
                                               compare_op=ALU.is_ge, fill=0.0, base=0, channel_multiplier=1),
             reads=[wraw], writes=[wraw])
        S.op("dve", lambda e: e.tensor_copy(out=wbf[:], in_=wraw[:]), reads=[wraw], writes=[wbf])
        for g in range(8):
            S.op("pe", lambda e, g=g: e.transpose(out=pw[:, g, :], in_=wbf[:, g, :], identity=ident[:]),
                 reads=[wbf, ident], pwrites=[pw])
        S.op("dve", lambda e: e.tensor_copy(out=WT[:], in_=pw[:]), reads=[pw], writes=[WT])

        NB = 2
        svt = Ring([S.sb("C_sv%d" % i, [128, 512]) for i in range(NB)])
        sut = Ring([S.sb("C_su%d" % i, [128, 512]) for i in range(NB)])
        szt = Ring([S.sb("C_sz%d" % i, [128, 512], BF16) for i in range(NB)])
        stats = S.sb("C_stats", [128, 6])
        mv = S.sb("C_mv", [128, 2])
        rstd = S.sb("C_rstd", [128, 1])
        vn0 = S.sb("C_vnf", [128, 512])
        vn = Ring([S.sb("C_vn%d" % i, [128, 512], BF16) for i in range(2)])
        y0 = S.sb("C_y0", [128, 512])
        yb = Ring([S.sb("C_yb%d" % i, [128, 512], BF16) for i in range(2)])
        pm = Ring([S.ps("C_pm%d" % i, [128, 512]) for i in range(2)])
        pt = Ring([S.ps("C_pt%d" % i, [128, 4, 128], BF16) for i in range(2)])
        stg = Ring([S.sb("C_stg%d" % i, [128, 4, 512], BF16) for i in range(2)])
        ys = scr["ysT"]
        sgb = None
        for c in range(SEQ // 128):
            t0 = c * 128
            v = svt.next(); u = sut.next(); z = szt.next()
            S.dma("sp", v[:], scr["sv"].t[t0:t0 + 128, :], reads=[scr["sv"]], writes=[v])
            S.dma("sp", u[:], scr["su"].t[t0:t0 + 128, :], reads=[scr["su"]], writes=[u])
            S.dma("sp", z[:], scr["szs"].t[t0:t0 + 128, :], reads=[scr["szs"]], writes=[z])
            S.op("dve", lambda e, v=v: e.bn_stats(out=stats[:], in_=v[:]), reads=[v], writes=[stats])
            S.op("dve", lambda e: e.bn_aggr(out=mv[:], in_=stats[:]), reads=[stats], writes=[mv])
            S.op("act", lambda e: e.activation(out=rstd[:], in_=mv[:, 1:2], func=AF.Sqrt, bias=LN_EPS, scale=1.0),
                 reads=[mv], writes=[rstd])
            S.op("dve", lambda e: e.reciprocal(out=rstd[:], in_=rstd[:]), reads=[rstd], writes=[rstd])
            S.op("dve", lambda e, v=v: e.tensor_scalar(out=vn0[:], in0=v[:], scalar1=mv[:, 0:1], scalar2=rstd[:, 0:1],
                                                       op0=ALU.subtract, op1=ALU.mult),
                 reads=[v, mv, rstd], writes=[vn0])
            S.op("pool", lambda e: e.tensor_tensor(out=vn0[:], in0=vn0[:], in1=lng[:], op=ALU.mult),
                 reads=[vn0, lng], writes=[vn0])
            vb = vn.next()
            S.op("pool", lambda e, vb=vb: e.tensor_tensor(out=vb[:], in0=vn0[:], in1=lnb[:], op=ALU.add),
                 reads=[vn0, lnb], writes=[vb])
            pmm = pm.next()
            for g in range(8):
                S.op("pe", lambda e, g=g, vb=vb, pmm=pmm: e.matmul(pmm[:, g * 64:(g + 1) * 64], lhsT=WT[:, g, :],
                                                                   rhs=vb[:, g * 64:(g + 1) * 64], start=True, stop=True),
                     reads=[WT, vb], writes=[pmm] if g == 0 else (), pwrites=() if g == 0 else [pmm])
            S.op("dve", lambda e, pmm=pmm: e.tensor_tensor(
                out=y0[:].rearrange("p (g d) -> p g d", g=8), in0=pmm[:].rearrange("p (g d) -> p g d", g=8),
                in1=bsT[:].unsqueeze(2).to_broadcast([128, 8, 64]), op=ALU.add), reads=[pmm, bsT], writes=[y0])
            S.op("pool", lambda e, u=u: e.tensor_tensor(out=y0[:], in0=y0[:], in1=u[:], op=ALU.mult),
                 reads=[y0, u], writes=[y0])
            y = yb.next()
            S.op("dve", lambda e, y=y, z=z: e.tensor_tensor(out=y[:], in0=y0[:], in1=z[:], op=ALU.mult),
                 reads=[y0, z], writes=[y])
            ptt = pt.next()
            for k in range(4):
                S.op("pe", lambda e, k=k, y=y, ptt=ptt: e.transpose(out=ptt[:, k, :], in_=y[:, k * 128:(k + 1) * 128],
                                                                    identity=ident[:]),
                     reads=[y, ident], writes=[ptt] if k == 0 else (), pwrites=() if k == 0 else [ptt])
            if c % 4 == 0:
                sgb = stg.next()
            cc = c % 4
            S.op("act", lambda e, ptt=ptt, sgb=sgb, cc=cc: e.copy(out=sgb[:, :, cc * 128:(cc + 1) * 128], in_=ptt[:]),
                 reads=[ptt], writes=[sgb] if cc == 0 else (), pwrites=() if cc == 0 else [sgb])
            if cc == 3 or c == SEQ // 128 - 1:
                tb = (c // 4) * 512
                n = (cc + 1) * 128
                S.dma("sp", ys.t[1, :, tb:tb + n].rearrange("(k p) t -> p k t", p=128), sgb[:, :, 0:n], reads=[sgb],
                      pwrites=[ys], key=sgb)
        _barrier(S)
        S.stack_pop()


def phase_E(S, nc, SEQ, lyr, x_src, x_dst, Wd, scr, final):
    TT = 512
    nsub = 4
    with contextlib.ExitStack() as st:
        S.stack_push(st)
        ident = make_ident(S, "E_ident")
        wb = S.sb("E_wb", [128, 3, 4, D], BF16)
        wo = S.sb("E_wo", [128, 8, D], BF16)
        wpg = S.sb("E_wpg", [128, 8, D], BF16)
        wpp = S.sb("E_wpp", [128, 2, D], BF16)
        gpl = S.sb("E_gpl", [128, D])
        gfin = S.sb("E_gfin", [128, D])
        for n in range(3):
            S.dma("pool", wb[:, n, :, :], Wd["w_branch"].t[lyr, n].rearrange("(k p) d -> p k d", p=128),
                  reads=[Wd["w_branch"]], pwrites=[wb], key=wb, max_dma_last_dim=4096)
        for k0 in range(0, 8, 4):
            S.dma("pool", wo[:, k0:k0 + 4, :], Wd["w_o"].t[lyr, k0 * 128:(k0 + 4) * 128, :].rearrange("(k p) d -> p k d", p=128),
                  reads=[Wd["w_o"]], pwrites=[wo], key=wo, max_dma_last_dim=4096)
            S.dma("pool", wpg[:, k0:k0 + 4, :], Wd["w_ple_gate"].t[lyr, k0 * 128:(k0 + 4) * 128, :].rearrange("(k p) d -> p k d", p=128),
                  reads=[Wd["w_ple_gate"]], pwrites=[wpg], key=wpg, max_dma_last_dim=4096)
        S.dma("pool", wpp[:], Wd["w_ple_proj"].t[lyr].rearrange("(k p) d -> p k d", p=128),
              reads=[Wd["w_ple_proj"]], pwrites=[wpp], key=wpp, max_dma_last_dim=4096)
        S.dma("sp", gpl[:], Wd["ple_norm_g"].t[lyr:lyr + 1, :].partition_broadcast(128), reads=[Wd["ple_norm_g"]], writes=[gpl])
        if final:
            S.dma("sp", gfin[:], Wd["final_norm_g"].t[0:1, :].partition_broadcast(128), reads=[Wd["final_norm_g"]], writes=[gfin])

        yst = S.sb("E_ys", [128, 3, 4, TT], BF16)
        mgt = S.sb("E_mg", [128, 24, TT], BF16)
        mrg = S.sb("E_mrg", [128, 8, TT])
        mrb = S.sb("E_mrb", [128, 8, TT], BF16)
        tmp = Ring([S.sb("E_tmp%d" % i, [128, TT]) for i in range(2)])
        xt = S.sb("E_x", [128, nsub, D])
        pin = S.sb("E_p", [128, nsub, PLE])
        pbf = S.sb("E_pbf", [128, PLE], BF16)
        pTs = S.sb("E_pT", [128, 2, 128], BF16)
        sq = S.sb("E_sq", [128, D], BF16)
        ss = S.sb("E_ss", [128, 1])
        rs = S.sb("E_rs", [128, 1])
        hp = S.sb("E_hp", [128, D], BF16)
        hpT = S.sb("E_hpT", [128, 8, 128], BF16)
        gate = S.sb("E_gate", [128, D])
        xo = Ring([S.sb("E_xo%d" % i, [128, D]) for i in range(2)])
        pz = Ring([S.ps("E_pz%d" % i, [128, TT]) for i in range(3)])
        po = Ring([S.ps("E_po%d" % i, [128, 512]) for i in range(2)])
        pg = Ring([S.ps("E_pg%d" % i, [128, 512]) for i in range(2)])
        ptr = S.ps("E_ptr", [128, 8, 128], BF16)

        for ti in range(SEQ // TT):
            t0 = ti * TT
            for n in range(3):
                S.dma("sp", yst[:, n, :, :], scr["ysT"].t[n, :, t0:t0 + TT].rearrange("(k p) t -> p k t", p=128),
                      reads=[scr["ysT"]], writes=[yst] if n == 0 else (), pwrites=() if n == 0 else [yst], key=yst)
            for k0 in range(0, 24, 8):
                S.dma("sp", mgt[:, k0:k0 + 8, :], scr["mgT"].t[k0 * 128:(k0 + 8) * 128, t0:t0 + TT].rearrange("(k p) t -> p k t", p=128),
                      reads=[scr["mgT"]], writes=[mgt] if k0 == 0 else (), pwrites=() if k0 == 0 else [mgt], key=mgt)
            S.dma("sp", xt[:], x_src.t[t0:t0 + TT, :].rearrange("(s p) d -> p s d", p=128), reads=[x_src], writes=[xt])
            S.dma("sp", pin[:], Wd["p"].t[lyr, t0:t0 + TT, :].rearrange("(s p) d -> p s d", p=128), reads=[Wd["p"]], writes=[pin])
            for dc in range(8):
                pzs = []
                for n in range(3):
                    pzz = pz.next()
                    pzs.append(pzz)
                    for k in range(4):
                        S.op("pe", lambda e, n=n, k=k, dc=dc, pzz=pzz: e.matmul(
                            pzz[:], lhsT=wb[:, n, k, dc * 128:(dc + 1) * 128], rhs=yst[:, n, k, :], start=(k == 0), stop=(k == 3)),
                            reads=[wb, yst], writes=[pzz] if k == 0 else (), pwrites=() if k == 0 else [pzz])
                S.op("dve", lambda e, dc=dc, p0=pzs[0]: e.tensor_tensor(out=mrg[:, dc, :], in0=p0[:], in1=mgt[:, dc, :], op=ALU.mult),
                     reads=[pzs[0], mgt], pwrites=[mrg])
                t1 = tmp.next()
                S.op("dve", lambda e, dc=dc, p1=pzs[1], t1=t1: e.tensor_tensor(out=t1[:], in0=p1[:], in1=mgt[:, 8 + dc, :], op=ALU.mult),
                     reads=[pzs[1], mgt], writes=[t1])
                t2 = tmp.next()
                S.op("dve", lambda e, dc=dc, p2=pzs[2], t2=t2: e.tensor_tensor(out=t2[:], in0=p2[:], in1=mgt[:, 16 + dc, :], op=ALU.mult),
                     reads=[pzs[2], mgt], writes=[t2])
                S.op("pool", lambda e, dc=dc, t1=t1: e.tensor_tensor(out=mrg[:, dc, :], in0=mrg[:, dc, :], in1=t1[:], op=ALU.add),
                     reads=[mrg, t1], pwrites=[mrg])
                S.op("pool", lambda e, dc=dc, t2=t2: e.tensor_tensor(out=mrb[:, dc, :], in0=mrg[:, dc, :], in1=t2[:], op=ALU.add),
                     reads=[mrg, t2], pwrites=[mrb])
            for s in range(nsub):
                for blk in range(2):
                    pp = po.next()
                    for k in range(8):
                        S.op("pe", lambda e, k=k, s=s, blk=blk, pp=pp: e.matmul(
                            pp[:], lhsT=mrb[:, k, s * 128:(s + 1) * 128], rhs=wo[:, k, blk * 512:(blk + 1) * 512],
                            start=(k == 0), stop=(k == 7)),
                            reads=[mrb, wo], writes=[pp] if k == 0 else (), pwrites=() if k == 0 else [pp])
                    S.op("dve", lambda e, s=s, blk=blk, pp=pp: e.tensor_tensor(
                        out=xt[:, s, blk * 512:(blk + 1) * 512], in0=pp[:], in1=xt[:, s, blk * 512:(blk + 1) * 512], op=ALU.add),
                        reads=[pp, xt], pwrites=[xt])
                rmsnorm_tile(S, xt[:, s, :], xt, gpl, hp[:], hp, sq, ss, rs)
                for k in range(8):
                    S.op("pe", lambda e, k=k: e.transpose(out=ptr[:, k, :], in_=hp[:, k * 128:(k + 1) * 128], identity=ident[:]),
                         reads=[hp, ident], writes=[ptr] if k == 0 else (), pwrites=() if k == 0 else [ptr])
                S.op("act", lambda e: e.copy(out=hpT[:], in_=ptr[:]), reads=[ptr], writes=[hpT])
                S.op("pool", lambda e, s=s: e.tensor_copy(out=pbf[:], in_=pin[:, s, :]), reads=[pin], writes=[pbf])
                for k in range(2):
                    S.op("pe", lambda e, k=k: e.transpose(out=ptr[:, k, :], in_=pbf[:, k * 128:(k + 1) * 128], identity=ident[:]),
                         reads=[pbf, ident, hpT], writes=[ptr] if k == 0 else (), pwrites=() if k == 0 else [ptr])
                S.op("act", lambda e: e.copy(out=pTs[:], in_=ptr[:, 0:2, :]), reads=[ptr], writes=[pTs])
                xout = xo.next()
                for blk in range(2):
                    pgg = pg.next()
                    for k in range(8):
                        S.op("pe", lambda e, k=k, blk=blk, pgg=pgg: e.matmul(
                            pgg[:], lhsT=hpT[:, k, :], rhs=wpg[:, k, blk * 512:(blk + 1) * 512], start=(k == 0), stop=(k == 7)),
                            reads=[hpT, wpg], writes=[pgg] if k == 0 else (), pwrites=() if k == 0 else [pgg])
                    S.op("act", lambda e, blk=blk, pgg=pgg: e.activation(out=gate[:, blk * 512:(blk + 1) * 512], in_=pgg[:], func=AF.Sigmoid),
                         reads=[pgg], pwrites=[gate])
                    ppp = pg.next()
                    for k in range(2):
                        S.op("pe", lambda e, k=k, blk=blk, ppp=ppp: e.matmul(
                            ppp[:], lhsT=pTs[:, k, :], rhs=wpp[:, k, blk * 512:(blk + 1) * 512], start=(k == 0), stop=(k == 1)),
                            reads=[pTs, wpp], writes=[ppp] if k == 0 else (), pwrites=() if k == 0 else [ppp])
                    S.op("dve", lambda e, blk=blk, ppp=ppp: e.tensor_tensor(
                        out=gate[:, blk * 512:(blk + 1) * 512], in0=ppp[:], in1=gate[:, blk * 512:(blk + 1) * 512], op=ALU.mult),
                        reads=[ppp, gate], pwrites=[gate])
                    S.op("pool", lambda e, blk=blk, s=s, xout=xout: e.tensor_tensor(
                        out=xout[:, blk * 512:(blk + 1) * 512], in0=gate[:, blk * 512:(blk + 1) * 512],
                        in1=xt[:, s, blk * 512:(blk + 1) * 512], op=ALU.add),
                        reads=[gate, xt], writes=[xout] if blk == 0 else (), pwrites=() if blk == 0 else [xout])
                if final:
                    S.op("act", lambda e, xout=xout: e.activation(out=sq[:], in_=xout[:], func=AF.Square, accum_out=ss[:]),
                         reads=[xout], writes=[sq, ss])
                    S.op("act", lambda e: e.activation(out=rs[:], in_=ss[:], func=AF.Sqrt, scale=1.0 / D, bias=EPS),
                         reads=[ss], writes=[rs])
                    S.op("dve", lambda e: e.reciprocal(out=rs[:], in_=rs[:]), reads=[rs], writes=[rs])
                    S.op("dve", lambda e, xout=xout: e.scalar_tensor_tensor(out=xout[:], in0=xout[:], scalar=rs[:, 0:1], in1=gfin[:],
                                                                            op0=ALU.mult, op1=ALU.mult),
                         reads=[xout, rs, gfin], writes=[xout])
                S.dma("sp", x_dst.t[t0 + s * 128:t0 + (s + 1) * 128, :], xout[:], reads=[xout], pwrites=[x_dst], key=xout)
        _barrier(S)
        S.stack_pop()


def phase_D(S, nc, SEQ, lyr, Wd, scr):
    TP = 128
    C = 16
    NCH = TP // C
    GN_EPS = 64e-5
    LD = 0.6065306597126334
    with contextlib.ExitStack() as st:
        S.stack_push(st)
        identB = make_ident(S, "D_identB", BF16)
        ones = S.sb("D_ones", [128, 128])
        S.op("pool", lambda e: e.memset(ones[:], 0.0), writes=[ones])
        S.op("pool", lambda e: e.memset(ones[0:64, 0:64], 1.0), reads=[ones], writes=[ones])
        S.op("pool", lambda e: e.memset(ones[64:128, 64:128], 1.0), reads=[ones], writes=[ones])
        Ff = S.sb("D_F", [128, 64], BF16)
        S.op("pool", lambda e: e.tensor_tensor(out=Ff[:], in0=identB[:, 0:64], in1=identB[:, 64:128], op=ALU.add), reads=[identB], writes=[Ff])
        Sel = S.sb("D_Sel", [128, 16], BF16)
        S.op("pool", lambda e: e.tensor_tensor(out=Sel[:], in0=identB[:, 0:16], in1=identB[:, 16:32], op=ALU.add), reads=[identB], writes=[Sel])
        for hh in range(2, 8):
            S.op("pool", lambda e, hh=hh: e.tensor_tensor(out=Sel[:], in0=Sel[:], in1=identB[:, hh * 16:(hh + 1) * 16], op=ALU.add),
                 reads=[identB, Sel], writes=[Sel])
        maskF = S.sb("D_maskF", [128, 4, 8], BF16)
        S.op("pool", lambda e: e.memset(maskF[:], 0.0), writes=[maskF])
        for p in range(4):
            for h2 in range(2):
                S.op("pool", lambda e, p=p, h2=h2: e.memset(maskF[h2 * 64:(h2 + 1) * 64, p, 2 * p + h2:2 * p + h2 + 1], 1.0), reads=[maskF], writes=[maskF])
        maskZ = S.sb("D_maskZ", [128, 4, 2], BF16)
        S.op("pool", lambda e: e.memset(maskZ[:], 1.0), writes=[maskZ])
        S.op("pool", lambda e: e.affine_select(out=maskZ[:], in_=maskZ[:], pattern=[[-32, 4], [-16, 2]], compare_op=ALU.is_ge, fill=0.0,
                                               base=0, channel_multiplier=1), reads=[maskZ], writes=[maskZ])
        S.op("pool", lambda e: e.affine_select(out=maskZ[:], in_=maskZ[:], pattern=[[32, 4], [16, 2]], compare_op=ALU.is_ge, fill=0.0,
                                               base=15, channel_multiplier=-1), reads=[maskZ], writes=[maskZ])

        def trimask(name, pat, cm, op):
            m = S.sb(name, [128, 128], BF16)
            S.op("pool", lambda e: e.memset(m[:], 1.0), writes=[m])
            S.op("pool", lambda e: e.affine_select(out=m[:], in_=m[:], pattern=pat, compare_op=op, fill=0.0, base=0, channel_multiplier=cm),
                 reads=[m], writes=[m])
            return m
        mSL = trimask("D_mSL", [[-16, 8], [-1, 16]], 1, ALU.is_gt)
        mSU = trimask("D_mSU", [[16, 8], [1, 16]], -1, ALU.is_gt)
        mUI = trimask("D_mUI", [[16, 8], [1, 16]], -1, ALU.is_ge)
        rm = S.sb("D_rm", [128, 512])
        S.op("pool", lambda e: e.memset(rm[:], 1.0), writes=[rm])
        S.op("pool", lambda e: e.memset(rm[:, 0:512:16], 0.0), reads=[rm], writes=[rm])

        def cvec(name, key, n):
            t = S.sb("D_" + name, [128, n])
            S.dma("sp", t[:], Wd[key].t[lyr].rearrange("(c p) -> p c", p=128), reads=[Wd[key]], writes=[t],
                  allow_slow_non_contiguous=True)
            return t

        def cvec2(name, key):
            t = S.sb("D_" + name, [128, 4])
            S.dma("sp", t[:], Wd[key].t[lyr].rearrange("(c a) j -> (a j) c", a=2), reads=[Wd[key]], writes=[t],
                  allow_slow_non_contiguous=True)
            return t
        mu = cvec("mu", "rk_mu", 13)
        w0 = cvec("w0", "rk_w0", 4)
        a0 = cvec("a0", "rk_a0", 4)
        lg = cvec("lg", "rk_lnx_g", 4)
        lb = cvec("lb", "rk_lnx_b", 4)
        kkc = cvec2("kkc", "rk_kk")
        ka = cvec2("ka", "rk_ka")
        rkc = cvec2("rkc", "rk_rk")
        omka = S.sb("D_omka", [128, 4])
        S.op("pool", lambda e: e.tensor_scalar(out=omka[:], in0=ka[:], scalar1=-1.0, scalar2=1.0, op0=ALU.mult, op1=ALU.add),
             reads=[ka], writes=[omka])
        w2 = S.sb("D_w2", [64, 512], BF16)
        a2 = S.sb("D_a2", [128, 512], BF16)
        S.dma("pool", w2[:], Wd["rk_w2"].t[lyr], reads=[Wd["rk_w2"]], writes=[w2])
        S.dma("pool", a2[64:128, :], Wd["rk_a2"].t[lyr], reads=[Wd["rk_a2"]], writes=[a2])

        Hm = S.sb("D_H", [128, 4, 64])
        Hn = S.sb("D_Hn", [128, 4, 64])
        Hbf = S.sb("D_Hbf", [128, 4, 64], BF16)
        S.op("pool", lambda e: e.memset(Hm[:], 0.0), writes=[Hm])
        S.op("pool", lambda e: e.memset(Hbf[:], 0.0), writes=[Hbf])

        rst = S.sb("D_rst", [128, 13, TP + 1])
        xs = S.sb("D_xs", [128, 13, TP])
        th = S.sb("D_th", [128, TP], BF16)
        sg = S.sb("D_sg", [128, 4, TP])
        cum = S.sb("D_cum", [128, 4, TP])
        E1 = S.sb("D_E1", [128, 4, TP])
        E2 = S.sb("D_E2", [128, 4, TP])
        E3 = S.sb("D_E3", [128, 4, TP])
        aa = S.sb("D_aa", [128, 4, TP])
        kkf = S.sb("D_kkf", [128, 4, TP])
        sq = S.sb("D_sq", [128, 4, TP])
        rn = S.sb("D_rn", [128, 4, TP])
        kp = S.sb("D_kp", [128, 4, TP])
        t1 = S.sb("D_t1", [128, 4, TP])
        t2 = S.sb("D_t2", [128, 4, TP])
        comp = [S.sb("D_cmp%d" % i, [128, 4, TP], BF16) for i in range(5)]
        ZXr = Ring([[S.sb("D_Z%d_%d" % (b, i), [128, NCH, 4, 128], BF16) for i in range(5)] for b in range(2)])
        DcR = Ring([S.sb("D_Dc%d" % i, [128, NCH, 4]) for i in range(2)])
        bonR = Ring([S.sb("D_bon%d" % i, [128, 4, TP]) for i in range(2)])
        rzR = Ring([S.sb("D_rz%d" % i, [128, 4, TP], BF16) for i in range(2)])
        ybR = Ring([S.sb("D_yb%d" % i, [128, 4, TP]) for i in range(2)])
        yo = S.sb("D_yo", [128, 4, TP], BF16)
        ppre = S.ps("D_ppre", [128, 4, TP])
        R4 = lambda nm, shp, dt=BF16: Ring([S.sb("D_%s%d" % (nm, i), shp, dt) for i in range(4)])
        WyZr = R4("WyZ", [128, 4, 128]); WhTr = R4("WhT", [128, 4, 128]); BtZr = R4("BtZ", [128, 4, 128]); KtZr = R4("KtZ", [128, 4, 128])
        U0r = R4("U0", [128, 64]); Vtr = R4("Vt", [128, 64]); PTr = R4("PT", [128, 128]); QTr = R4("QT", [128, 128])
        ysbR = Ring([S.sb("D_ysb%d" % i, [128, 64]) for i in range(5)])

        class Reg:
            def __init__(self, bank, ap):
                self.bank = bank
                self.t = ap

        class Lane:
            pass
        lanes = []
        for li in range(2):
            L = Lane()
            L.Gr = Ring([S.sb("D_G%d_%d" % (li, i), [128, 128], BF16) for i in range(2)])
            L.Nr = Ring([S.sb("D_N%d_%d" % (li, i), [128, 128], BF16) for i in range(2)])
            L.NTr = Ring([S.sb("D_NT%d_%d" % (li, i), [128, 128], BF16) for i in range(2)])
            L.MTs = S.sb("D_MTs%d" % li, [128, 128], BF16)
            L.X1Z = S.sb("D_X1Z%d" % li, [128, 4, 128], BF16)
            L.X1s = S.sb("D_X1s%d" % li, [128, 64], BF16)
            L.tks = S.sb("D_tks%d" % li, [128, 2, 64], BF16)
            ba = S.ps("D_ba%d" % li, [128, 512])
            bb = ppre if li == 0 else S.ps("D_bb%d" % li, [128, 4, 128])
            bg = S.ps("D_bg%d" % li, [128, 3, 128])
            L.tokc = Reg(ba, ba.t[:, 0:256].rearrange("q (o j) -> q o j", o=4))
            L.QTp = Reg(ba, ba.t[:, 256:384])
            L.mvp = Reg(ba, ba.t[:, 384:448])
            L.sc = [Reg(bb, bb.t[:, i, :]) for i in range(4)]
            L.bb = bb
            L.bg = bg
            lanes.append(L)
        bs = S.ps("D_bs", [128, 512])
        bt_ = S.ps("D_bt", [128, 512])
        WHp = Reg(bs, bs.t[:, 0:256].rearrange("q (p i) -> q p i", p=4))
        Yp = Reg(bs, bs.t[:, 256:320])
        yfp = Reg(bt_, bt_.t[:, 0:64].rearrange("q (p t) -> q p t", p=4))
        yn = S.sb("D_yn", [128, 64])
        YZ = S.sb("D_YZ", [128, 4, 128], BF16)
        stats = S.sb("D_stats", [128, 6])
        mv = S.sb("D_mv", [128, 2])
        rstd = S.sb("D_rstd", [128, 1])

        bc4 = lambda t: t[:].unsqueeze(2).to_broadcast([128, 4, TP])

        def mm(out_ap, obuf, lhsT, lbuf, rhs, rbuf, start, stop=True, first_write=False):
            obuf = getattr(obuf, "bank", obuf)
            S.op("pe", lambda e: e.matmul(out_ap, lhsT=lhsT, rhs=rhs, start=start, stop=stop, skip_group_check=True),
                 reads=[lbuf, rbuf], writes=[obuf] if first_write else (), pwrites=() if first_write else [obuf])

        def prep(nb):
            t0 = nb * TP
            S.dma("sp", rst[:, :, 1:TP + 1], scr["rsT"].t[:, t0:t0 + TP].rearrange("(c p) t -> p c t", p=128), reads=[scr["rsT"]], writes=[rst])
            if nb == 0:
                S.op("pool", lambda e: e.memset(rst[:, :, 0:1], 0.0), reads=[rst], pwrites=[rst])
            else:
                S.dma("sp", rst[:, :, 0:1], scr["rsT"].t[:, t0 - 1:t0].rearrange("(c p) t -> p c t", p=128), reads=[scr["rsT"]],
                      pwrites=[rst], key=rst, allow_slow_non_contiguous=True)
            S.op("pool", lambda e: e.tensor_tensor(out=xs[:], in0=rst[:, :, 0:TP], in1=rst[:, :, 1:TP + 1], op=ALU.subtract), reads=[rst], writes=[xs])
            S.op("pool", lambda e: e.tensor_tensor(out=xs[:], in0=xs[:], in1=mu[:].unsqueeze(2).to_broadcast([128, 13, TP]), op=ALU.mult),
                 reads=[xs, mu], writes=[xs])
            S.op("pool", lambda e: e.tensor_tensor(out=xs[:], in0=xs[:], in1=rst[:, :, 1:TP + 1], op=ALU.add), reads=[xs, rst], writes=[xs])
            r = xs[:, 0:4, :]; k = xs[:, 4:8, :]; v = xs[:, 8:12, :]
            S.op("act", lambda e: e.activation(out=th[0:64, :], in_=xs[0:64, 12, :], func=AF.Tanh), reads=[xs], pwrites=[th])
            S.op("act", lambda e: e.copy(out=th[64:128, :], in_=xs[64:128, 12, :]), reads=[xs], pwrites=[th])
            for p in range(4):
                mm(ppre[:, p, :], ppre, w2[0:64, p * 128:(p + 1) * 128], w2, th[0:64, :], th, True, first_write=(p == 0))
            for p in range(4):
                S.op("act", lambda e, p=p: e.activation(out=sg[:, p, :], in_=ppre[:, p, :], func=AF.Sigmoid, bias=w0[:, p:p + 1]),
                     reads=[w0], writes=[ppre], pwrites=[sg])
            for p in range(4):
                mm(ppre[:, p, :], ppre, a2[64:128, p * 128:(p + 1) * 128], a2, th[64:128, :], th, True, first_write=(p == 0))
            for p in range(4):
                S.op("act", lambda e, p=p: e.activation(out=aa[:, p, :], in_=ppre[:, p, :], func=AF.Sigmoid, bias=a0[:, p:p + 1]),
                     reads=[a0], writes=[ppre], pwrites=[aa])
            S.op("dve", lambda e: e.tensor_tensor_scan(out=cum[:].rearrange("q p t -> q (p t)"), data0=rm[:],
                                                       data1=sg[:].rearrange("q p t -> q (p t)"), initial=0.0, op0=ALU.mult, op1=ALU.add),
                 reads=[rm, sg], writes=[cum])
            S.op("act", lambda e: e.activation(out=E1[:], in_=cum[:], func=AF.Exp, scale=-LD), reads=[cum], writes=[E1])
            S.op("act", lambda e: e.activation(out=E2[:], in_=cum[:], func=AF.Exp, scale=LD), reads=[cum], writes=[E2])
            S.op("pool", lambda e: e.tensor_tensor(out=t2[:], in0=cum[:], in1=sg[:], op=ALU.subtract), reads=[cum, sg], writes=[t2])
            S.op("act", lambda e: e.activation(out=E3[:], in_=t2[:], func=AF.Exp, scale=-LD), reads=[t2], writes=[E3])
            Dc = DcR.next()
            S.op("pool", lambda e, Dc=Dc: e.tensor_copy(out=Dc[:].rearrange("q c p -> q p c"), in_=E1[:, :, 15:TP:16]), reads=[E1], writes=[Dc])
            S.op("pool", lambda e: e.tensor_tensor(out=kkf[:], in0=k, in1=bc4(kkc), op=ALU.mult), reads=[xs, kkc], writes=[kkf])
            S.op("pool", lambda e: e.tensor_tensor(out=sq[:], in0=kkf[:], in1=kkf[:], op=ALU.mult), reads=[kkf], writes=[sq])
            for p in range(4):
                mm(ppre[:, p, :], ppre, ones[:], ones, sq[:, p, :], sq, True, first_write=(p == 0))
            S.op("act", lambda e: e.activation(out=rn[:], in_=ppre[:], func=AF.Sqrt), writes=[rn, ppre])
            S.op("dve", lambda e: e.tensor_scalar(out=rn[:], in0=rn[:], scalar1=1e-12, scalar2=None, op0=ALU.max), reads=[rn], writes=[rn])
            S.op("dve", lambda e: e.reciprocal(out=rn[:], in_=rn[:]), reads=[rn], writes=[rn])
            S.op("pool", lambda e: e.tensor_tensor(out=kkf[:], in0=kkf[:], in1=rn[:], op=ALU.mult), reads=[kkf, rn], writes=[kkf])
            S.op("pool", lambda e: e.tensor_tensor(out=t1[:], in0=aa[:], in1=bc4(ka), op=ALU.mult), reads=[aa, ka], writes=[t1])
            S.op("pool", lambda e: e.tensor_tensor(out=t1[:], in0=t1[:], in1=bc4(omka), op=ALU.add), reads=[t1, omka], writes=[t1])
            S.op("pool", lambda e: e.tensor_tensor(out=kp[:], in0=k, in1=t1[:], op=ALU.mult), reads=[xs, t1], writes=[kp])
            At, Bt, Kt, Rt, Vb = comp
            S.op("pool", lambda e: e.scalar_tensor_tensor(out=At[:], in0=kkf[:], scalar=-1.0, in1=E3[:], op0=ALU.mult, op1=ALU.mult)
                 if False else e.tensor_tensor(out=t2[:], in0=kkf[:], in1=E3[:], op=ALU.mult), reads=[kkf, E3], writes=[t2])
            S.op("dve", lambda e: e.tensor_scalar(out=At[:], in0=t2[:], scalar1=-1.0, scalar2=None, op0=ALU.mult), reads=[t2], writes=[At])
            S.op("pool", lambda e: e.tensor_tensor(out=t2[:], in0=kkf[:], in1=aa[:], op=ALU.mult), reads=[kkf, aa], writes=[t2])
            S.op("pool", lambda e: e.tensor_tensor(out=Bt[:], in0=t2[:], in1=E2[:], op=ALU.mult), reads=[t2, E2], writes=[Bt])
            S.op("pool", lambda e: e.tensor_tensor(out=Kt[:], in0=kp[:], in1=E2[:], op=ALU.mult), reads=[kp, E2], writes=[Kt])
            S.op("pool", lambda e: e.tensor_tensor(out=Rt[:], in0=r, in1=E1[:], op=ALU.mult), reads=[xs, E1], writes=[Rt])
            S.op("pool", lambda e: e.tensor_copy(out=Vb[:], in_=v), reads=[xs], writes=[Vb])
            S.op("pool", lambda e: e.tensor_tensor(out=t1[:], in0=r, in1=kp[:], op=ALU.mult), reads=[xs, kp], writes=[t1])
            S.op("pool", lambda e: e.tensor_tensor(out=sq[:], in0=t1[:], in1=bc4(rkc), op=ALU.mult), reads=[t1, rkc], writes=[sq])
            for p in range(4):
                mm(ppre[:, p, :], ppre, ones[:], ones, sq[:, p, :], sq, True, first_write=(p == 0))
            bon = bonR.next()
            S.op("act", lambda e, bon=bon: e.copy(out=bon[:], in_=ppre[:]), writes=[bon, ppre])
            S.op("pool", lambda e, bon=bon: e.tensor_tensor(out=bon[:], in0=bon[:], in1=v, op=ALU.mult), reads=[bon, xs], writes=[bon])
            rzt = rzR.next()
            S.dma("sp", rzt[:], scr["rzT"].t[:, t0:t0 + TP].rearrange("(c p) t -> p c t", p=128), reads=[scr["rzT"]], writes=[rzt])
            ZX = ZXr.next()
            for oi in range(5):
                for p in range(4):
                    S.op("dve" if (oi * 4 + p) % 2 == 0 else "pool", lambda e, oi=oi, p=p, ZX=ZX: e.tensor_tensor(
                        out=ZX[oi][:, :, p, :].rearrange("q c (h t) -> q c h t", t=16),
                        in0=comp[oi][:, p, :].rearrange("q (c t) -> q c t", t=16).unsqueeze(2).to_broadcast([128, NCH, 8, 16]),
                        in1=maskF[:, p, :].unsqueeze(1).unsqueeze(3).to_broadcast([128, NCH, 8, 16]), op=ALU.mult),
                        reads=[comp[oi], maskF], writes=[ZX[oi]] if p == 0 else (), pwrites=() if p == 0 else [ZX[oi]])
            return dict(ZX=ZX, Dc=Dc, bon=bon, rzt=rzt, yb=ybR.next(), t0=t0)


        def pre(bt, c, L, pc):
            ZA, ZB, ZK, ZR, ZV = bt["ZX"]
            BtZ = BtZr.next(); KtZ = KtZr.next(); U0 = U0r.next(); Vt = Vtr.next(); PTs = PTr.next(); QTs = QTr.next()
            WyZ = WyZr.next(); WhT = WhTr.next()
            pc.update(BtZ=BtZ, KtZ=KtZ, U0=U0, Vt=Vt, PTs=PTs, QTs=QTs, WyZ=WyZ, WhT=WhT, c=c, bt=bt)
            tokc = L.tokc
            first = True
            for oi, Z in enumerate((ZA, ZB, ZK, ZV)):
                for p in range(4):
                    mm(tokc.t[:, oi, :], tokc, Z[:, c, p, :], Z, Ff[:], Ff, first, first_write=first)
                    first = False
            N1 = L.Nr.next(); NT1 = L.NTr.next()
            specs = ((L.sc[0], ZA, ZB, mSL, N1), (L.sc[1], ZB, ZA, mSU, NT1), (L.sc[2], ZK, ZA, mSU, L.MTs), (L.sc[3], ZB, ZR, mUI, PTs))
            for gi, (pb, Lh, R_, msk, dst) in enumerate(specs):
                for p in range(4):
                    mm(pb.t, pb, Lh[:, c, p, :], Lh, R_[:, c, p, :], R_, p == 0, first_write=(gi == 0 and p == 0))
            for p in range(4):
                mm(L.QTp.t, L.QTp, ZK[:, c, p, :], ZK, ZR[:, c, p, :], ZR, False, first_write=False)
            yield
            G0 = L.Gr.next()
            tks = L.tks
            S.op("act", lambda e: e.copy(out=G0[:, 0:64], in_=tokc.t[:, 0, :]), writes=[G0, tokc.bank])
            mz = maskZ[:].unsqueeze(3).to_broadcast([128, 4, 2, 64])
            S.op("dve", lambda e: e.tensor_copy(out=tks[:], in_=tokc.t[:, 1:3, :]), writes=[tks, tokc.bank])
            S.op("act", lambda e: e.copy(out=Vt[:], in_=tokc.t[:, 3, :]), writes=[Vt, tokc.bank])
            S.op("pool", lambda e: e.tensor_tensor(out=BtZ[:].rearrange("q p (a j) -> q p a j", a=2),
                                                   in0=tks[:, 0, :].unsqueeze(1).unsqueeze(1).to_broadcast([128, 4, 2, 64]), in1=mz, op=ALU.mult),
                 reads=[tks, maskZ], writes=[BtZ])
            S.op("pool", lambda e: e.tensor_tensor(out=KtZ[:].rearrange("q p (a j) -> q p a j", a=2),
                                                   in0=tks[:, 1, :].unsqueeze(1).unsqueeze(1).to_broadcast([128, 4, 2, 64]), in1=mz, op=ALU.mult),
                 reads=[tks, maskZ], writes=[KtZ])
            for gi, (pb, Lh, R_, msk, dst) in enumerate(specs):
                S.op("dve", lambda e, pb=pb, msk=msk, dst=dst: e.tensor_tensor(out=dst[:], in0=pb.t, in1=msk[:], op=ALU.mult),
                     reads=[msk], writes=[dst, pb.bank])
            S.op("dve", lambda e: e.tensor_tensor(out=QTs[:], in0=L.QTp.t, in1=mUI[:], op=ALU.mult), reads=[mUI], writes=[QTs, L.QTp.bank])
            yield
            mm(L.mvp.t, L.mvp, L.MTs[:], L.MTs, Vt[:], Vt, True, first_write=True)
            yield
            S.op("act", lambda e: e.copy(out=G0[:, 64:128], in_=L.mvp.t), writes=[L.mvp.bank], pwrites=[G0])
            yield
            G = G0; Nk = N1; NTk = NT1
            gb = L.bg
            for lev in range(4):
                mm(gb[:, 0, :], gb, identB[:], identB, G[:], G, True, stop=False, first_write=True)
                mm(gb[:, 0, :], gb, NTk[:], NTk, G[:], G, False)
                if lev < 3:
                    mm(gb[:, 1, :], gb, NTk[:], NTk, Nk[:], Nk, True)
                    mm(gb[:, 2, :], gb, Nk[:], Nk, NTk[:], NTk, True)
                    yield
                    G2 = L.Gr.next(); N2 = L.Nr.next(); NT2 = L.NTr.next()
                    S.op("act", lambda e, G2=G2: e.copy(out=G2[:], in_=gb[:, 0, :]), writes=[G2, gb])
                    S.op("act", lambda e, N2=N2: e.copy(out=N2[:], in_=gb[:, 1, :]), writes=[N2, gb])
                    S.op("act", lambda e, NT2=NT2: e.copy(out=NT2[:], in_=gb[:, 2, :]), writes=[NT2, gb])
                    G = G2; Nk = N2; NTk = NT2
                    yield
                else:
                    yield
                    S.op("act", lambda e: e.copy(out=L.X1s[:], in_=gb[:, 0, 0:64]), writes=[L.X1s, gb])
                    S.op("act", lambda e: e.copy(out=U0[:], in_=gb[:, 0, 64:128]), writes=[U0, gb])
                    S.op("dve", lambda e: e.tensor_tensor(out=L.X1Z[:].rearrange("q p (a j) -> q p a j", a=2),
                                                           in0=L.X1s[:].unsqueeze(1).unsqueeze(1).to_broadcast([128, 4, 2, 64]), in1=mz, op=ALU.mult),
                         reads=[L.X1s, maskZ], writes=[L.X1Z])
                    yield
            bb = L.bb
            for p in range(4):
                mm(bb[:, p, :], bb, identB[:], identB, ZR[:, c, p, :], ZR, p == 0, stop=False, first_write=(p == 0))
                mm(bb[:, p, :], bb, L.X1Z[:, p, :], L.X1Z, PTs[:], PTs, False)
            ba = L.tokc.bank
            for p in range(4):
                mm(ba[:, p * 128:(p + 1) * 128], ba, L.X1Z[:, p, :], L.X1Z, BtZ[:, p, :], BtZ, p == 0, first_write=(p == 0))
            yield
            S.op("act", lambda e: e.copy(out=WyZ[:], in_=bb[:]), writes=[WyZ, bb])
            S.op("dve", lambda e: e.tensor_copy(out=WhT[:].rearrange("q p m -> q (p m)"), in_=ba[:]), writes=[WhT, ba])
            yield

        def state_stream(pc):
            c = pc["c"]; bt = pc["bt"]
            BtZ, KtZ, U0, Vt, PTs, QTs, WyZ, WhT = (pc[k] for k in ("BtZ", "KtZ", "U0", "Vt", "PTs", "QTs", "WyZ", "WhT"))
            for p in range(4):
                mm(WHp.t[:, p, :], WHp, BtZ[:, p, :], BtZ, U0[:], U0, p == 0, stop=False, first_write=(p == 0))
            for p in range(4):
                mm(WHp.t[:, p, :], WHp, KtZ[:, p, :], KtZ, Vt[:], Vt, False, stop=False)
            mm(Yp.t, Yp, PTs[:], PTs, U0[:], U0, False, stop=False)
            mm(Yp.t, Yp, QTs[:], QTs, Vt[:], Vt, False, stop=False)
            yield
            for p in range(4):
                mm(Yp.t, Yp, WyZ[:, p, :], WyZ, Hbf[:, p, :], Hbf, False, stop=(p == 3))
            for p in range(4):
                mm(WHp.t[:, p, :], WHp, WhT[:, p, :], WhT, Hbf[:, p, :], Hbf, False, stop=True)
            yield
            Dc = bt["Dc"]
            S.op("dve", lambda e: e.tensor_tensor(out=Hn[:], in0=WHp.t, in1=Hm[:], op=ALU.add), reads=[Hm], writes=[Hn, WHp.bank])
            S.op("dve", lambda e: e.tensor_tensor(out=Hm[:], in0=Hn[:], in1=Dc[:, c, :].unsqueeze(2).to_broadcast([128, 4, 64]), op=ALU.mult),
                 reads=[Hn, Dc], writes=[Hm])
            ysb = ysbR.next()
            pc["ysb"] = ysb
            S.op("act", lambda e: e.copy(out=ysb[:], in_=Yp.t), writes=[ysb, Yp.bank])
            S.op("act", lambda e: e.copy(out=Hbf[:], in_=Hm[:]), reads=[Hm], writes=[Hbf])
            yield

        def out_stream(pc):
            c = pc["c"]; bt = pc["bt"]; ysb = pc["ysb"]
            S.op("dve", lambda e: e.bn_stats(out=stats[:], in_=ysb[:]), reads=[ysb], writes=[stats])
            S.op("dve", lambda e: e.bn_aggr(out=mv[:], in_=stats[:]), reads=[stats], writes=[mv])
            yield
            S.op("act", lambda e: e.activation(out=rstd[:], in_=mv[:, 1:2], func=AF.Sqrt, bias=GN_EPS, scale=1.0), reads=[mv], writes=[rstd])
            yield
            S.op("dve", lambda e: e.reciprocal(out=rstd[:], in_=rstd[:]), reads=[rstd], writes=[rstd])
            S.op("dve", lambda e: e.tensor_scalar(out=yn[:], in0=ysb[:], scalar1=mv[:, 0:1], scalar2=rstd[:, 0:1], op0=ALU.subtract, op1=ALU.mult),
                 reads=[ysb, mv, rstd], writes=[yn])
            yield
            S.op("dve", lambda e: e.tensor_tensor(out=YZ[:].rearrange("q p (a j) -> q p a j", a=2),
                                                   in0=yn[:].unsqueeze(1).unsqueeze(1).to_broadcast([128, 4, 2, 64]),
                                                   in1=maskZ[:].unsqueeze(3).to_broadcast([128, 4, 2, 64]), op=ALU.mult),
                 reads=[yn, maskZ], writes=[YZ])
            yield
            for p in range(4):
                mm(yfp.t[:, p, :], yfp, YZ[:, p, :], YZ, Sel[:], Sel, True, first_write=(p == 0))
            yield
            yb = bt["yb"]
            S.op("act", lambda e: e.copy(out=yb[:, :, c * 16:(c + 1) * 16], in_=yfp.t), writes=([yb] if c == 0 else []) + [yfp.bank],
                 pwrites=() if c == 0 else [yb])
            if c == NCH - 1:
                post(bt)
            yield

        def post(bt):
            yb = bt["yb"]; bon = bt["bon"]; rzt = bt["rzt"]; t0 = bt["t0"]
            S.op("pool", lambda e: e.tensor_tensor(out=yb[:], in0=yb[:], in1=bc4(lg), op=ALU.mult), reads=[yb, lg], writes=[yb])
            S.op("pool", lambda e: e.tensor_tensor(out=yb[:], in0=yb[:], in1=bc4(lb), op=ALU.add), reads=[yb, lb], writes=[yb])
            S.op("pool", lambda e: e.tensor_tensor(out=yb[:], in0=yb[:], in1=bon[:], op=ALU.add), reads=[yb, bon], writes=[yb])
            S.op("pool", lambda e: e.tensor_tensor(out=yo[:], in0=yb[:], in1=rzt[:], op=ALU.mult), reads=[yb, rzt], writes=[yo])
            S.dma("sp", scr["ysT"].t[2, :, t0:t0 + TP].rearrange("(c p) t -> p c t", p=128), yo[:], reads=[yo], pwrites=[scr["ysT"]], key=yo)

        chunks = []
        for nb in range(SEQ // TP):
            for c in range(NCH):
                chunks.append((nb, c))
        bts = {}
        nxt = 0
        lane_gen = [None, None]
        lane_pc = [None, None]
        done_order = {}
        next_state = 0
        state_gen = None; state_pc = None
        out_q = []; out_gen = None
        n_total = len(chunks)
        finished_out = 0
        pcs = {}
        while finished_out < n_total:
            for li in range(2):
                if lane_gen[li] is None and nxt < n_total and nxt - next_state < 3:
                    nb, c = chunks[nxt]
                    if nb not in bts:
                        bts[nb] = prep(nb)
                    pc = {"idx": nxt}
                    pcs[nxt] = pc
                    lane_gen[li] = pre(bts[nb], c, lanes[li], pc)
                    lane_pc[li] = pc
                    nxt += 1
                if lane_gen[li] is not None:
                    try:
                        next(lane_gen[li])
                    except StopIteration:
                        done_order[lane_pc[li]["idx"]] = True
                        lane_gen[li] = None
            for _rep in range(DEBUG.get("state_rep", 2)):
                if state_gen is None and done_order.get(next_state) and next_state - finished_out < 3:
                    state_pc = pcs[next_state]
                    state_gen = state_stream(state_pc)
                if state_gen is not None:
                    try:
                        next(state_gen)
                    except StopIteration:
                        out_q.append(state_pc)
                        state_gen = None
                        next_state += 1
            for _rep in range(DEBUG.get("out_rep", 1)):
                if out_gen is None and out_q:
                    out_gen = out_stream(out_q.pop(0))
                if out_gen is not None:
                    try:
                        next(out_gen)
                    except StopIteration:
                        out_gen = None
                        finished_out += 1
        _barrier(S)
        S.stack_pop()


WSPEC = {
    "norm_g": [2, 1024], "w_in": [2, 1024, 9112], "cmp_w1": [2, 2, 32, 64, 128], "cmp_w2": [2, 2, 128, 64],
    "cmp_pe": [2, 2, 32, 64], "sg_ln_g": [2, 512], "sg_ln_b": [2, 512], "sg_w": [2, 8, 128, 128], "sg_b": [2, 8, 128],
    "rk_mu": [2, 1664], "rk_w0": [2, 512], "rk_w2": [2, 64, 512], "rk_a0": [2, 512], "rk_a2": [2, 64, 512],
    "rk_kk": [2, 8, 64], "rk_ka": [2, 8, 64], "rk_rk": [2, 8, 64], "rk_lnx_g": [2, 512], "rk_lnx_b": [2, 512],
    "w_branch": [2, 3, 512, 1024], "w_o": [2, 1024, 1024], "ple_norm_g": [2, 1024], "w_ple_gate": [2, 1024, 1024],
    "w_ple_proj": [2, 256, 1024], "final_norm_g": [1, 1024],
}


def build(SEQ, nlayers=2, enable=(1, 1, 1), scr_kind="Internal"):
    nc = bass.Bass("TRN2", target_bir_lowering=False)
    with contextlib.ExitStack() as stack:
        S = Sched(nc, stack)
        x = Buf("x", nc.dram_tensor("x", [SEQ, D], F32, kind="ExternalInput").ap())
        Wd = {"p": Buf("p", nc.dram_tensor("p", [2, SEQ, PLE], F32, kind="ExternalInput").ap())}
        for k, shp in WSPEC.items():
            Wd[k] = Buf(k, nc.dram_tensor(k, shp, F32, kind="ExternalInput").ap())
        out = Buf("out", nc.dram_tensor("out", [SEQ, D], F32, kind="ExternalOutput").ap())
        scr = make_scratch(S, SEQ, kind=scr_kind)
        xmid = S.dram("xmid", [SEQ, D], F32, kind=scr_kind)
        cur = x
        for lyr in range(nlayers):
            last = lyr == nlayers - 1
            dst = out if last else xmid
            phase_A(S, nc, SEQ, lyr, cur, Wd, scr)
            if enable[0]:
                phase_B(S, nc, SEQ, lyr, Wd, scr)
            if enable[1]:
                phase_C(S, nc, SEQ, lyr, Wd, scr)
            if enable[2]:
                phase_D(S, nc, SEQ, lyr, Wd, scr)
            phase_E(S, nc, SEQ, lyr, cur, dst, Wd, scr, final=(last and nlayers == 2))
            cur = dst
        S.emit()
    return nc


def phase_B(S, nc, SEQ, lyr, Wd, scr):
    NC = (SEQ - 32) // 16 + 1
    NT = (NC + 127) // 128
    NCp = NT * 128
    KT = SEQ // 128
    with contextlib.ExitStack() as st:
        S.stack_push(st)
        ident = make_ident(S, "B_ident")
        ksT = S.sb("B_ksT", [128, 2, SEQ], BF16)
        HALF = min(4096, SEQ)
        NA = SEQ // HALF
        kwT = S.sb("B_kwT", [64, 2, SEQ], BF16)
        vs = S.sb("B_vs", [128, KT, 2, 65], BF16)
        vw = S.sb("B_vw", [128, KT, 2, 65], BF16)
        kcmpT = S.sb("B_kcmpT", [64, 2, NCp], BF16)
        Rc = S.sb("B_Rc", [128, NT, 2, 193], BF16)
        S.op("pool", lambda e: e.memset(ksT[64:128, :, :], 1.0), writes=[ksT])
        for g_ in range(2):
            for a_ in range(NA):
                S.op("pool", lambda e, g_=g_, a_=a_: e.affine_select(
                    out=ksT[64:128, g_, a_ * HALF:(a_ + 1) * HALF], in_=ksT[64:128, g_, a_ * HALF:(a_ + 1) * HALF], pattern=[[1, HALF]],
                    compare_op=ALU.is_ge, fill=0.0, base=0, channel_multiplier=-64), reads=[ksT], pwrites=[ksT])
                S.op("pool", lambda e, g_=g_, a_=a_: e.affine_select(
                    out=ksT[64:128, g_, a_ * HALF:(a_ + 1) * HALF], in_=ksT[64:128, g_, a_ * HALF:(a_ + 1) * HALF], pattern=[[-1, HALF]],
                    compare_op=ALU.is_ge, fill=0.0, base=63, channel_multiplier=64), reads=[ksT], pwrites=[ksT])
        S.dma("sp", ksT[0:64, :, :], scr["ksT"].t.rearrange("(g d) t -> d g t", g=2), reads=[scr["ksT"]], pwrites=[ksT], key=ksT)
        S.dma("sp", kwT[:], scr["kwT"].t.rearrange("(g d) t -> d g t", g=2), reads=[scr["kwT"]], writes=[kwT])
        S.op("pool", lambda e: e.memset(vs[:], 1.0), writes=[vs])
        S.op("pool", lambda e: e.memset(vw[:], 1.0), writes=[vw])
        for k0 in range(0, KT, 8):
            k1 = min(KT, k0 + 8)
            for (dst, c0) in ((vs, 0), (vw, 128)):
                for g in range(2):
                    S.dma("sp", dst[:, k0:k1, g, 0:64],
                          scr["vsw"].t[k0 * 128:k1 * 128, c0 + g * 64:c0 + (g + 1) * 64].rearrange("(k p) d -> p k d", p=128),
                          reads=[scr["vsw"]], pwrites=[dst], key=dst)
        S.op("pool", lambda e: e.memset(Rc[:], 1.0), writes=[Rc])
        for nt in range(NT):
            for g in range(2):
                S.op("pool", lambda e, nt=nt, g=g: e.affine_select(
                    out=Rc[:, nt, g, 65:193], in_=Rc[:, nt, g, 65:193], pattern=[[-4, 128]], compare_op=ALU.is_ge, fill=0.0,
                    base=nt * 128 + 1, channel_multiplier=1), reads=[Rc], writes=[Rc])
                S.op("pool", lambda e, nt=nt, g=g: e.affine_select(
                    out=Rc[:, nt, g, 65:193], in_=Rc[:, nt, g, 65:193], pattern=[[4, 128]], compare_op=ALU.is_ge, fill=0.0,
                    base=3 - nt * 128, channel_multiplier=-1), reads=[Rc], writes=[Rc])
        npad = NCp - NC
        if npad:
            S.op("pool", lambda e: e.affine_select(
                out=Rc[:, NT - 1, :, :], in_=Rc[:, NT - 1, :, :], pattern=[[0, 2 * 193]], compare_op=ALU.is_ge, fill=0.0,
                base=(NC - 1) - (NT - 1) * 128, channel_multiplier=-1), reads=[Rc], writes=[Rc])
        S.op("pool", lambda e: e.memset(kcmpT[:], 0.0), writes=[kcmpT])

        with contextlib.ExitStack() as st2:
            S.stack_push(st2)
            kvT = S.sb("B_kvT", [64, 2, SEQ], BF16)
            w1 = S.sb("B_w1", [64, 32, 128], BF16)
            w2 = S.sb("B_w2", [128, 64], BF16)
            peT = S.sb("B_peT", [64, 32])
            peTb = S.sb("B_peTb", [64, 32], BF16)
            cb = S.sb("B_cb", [128, 1])
            hid = S.sb("B_hid", [128, NCp], BF16)
            ph = S.ps("B_ph", [128, 512])
            pc1 = S.ps("B_pc1", [128, 512])
            pk = S.ps("B_pk", [128, 512])
            for kv in range(2):
                src = scr["kcT"] if kv == 0 else scr["vcT"]
                S.dma("sp", kvT[:], src.t.rearrange("(g d) t -> d g t", g=2), reads=[src], writes=[kvT])
                S.dma("pool", w1[:], Wd["cmp_w1"].t[lyr, kv].rearrange("l d h -> d l h"), reads=[Wd["cmp_w1"]], writes=[w1])
                S.dma("pool", w2[:], Wd["cmp_w2"].t[lyr, kv], reads=[Wd["cmp_w2"]], writes=[w2])
                S.dma("sp", peT[:], Wd["cmp_pe"].t[lyr, kv].rearrange("l d -> d l"), reads=[Wd["cmp_pe"]], writes=[peT],
                      allow_slow_non_contiguous=True)
                S.op("dve", lambda e: e.tensor_copy(out=peTb[:], in_=peT[:]), reads=[peT], writes=[peTb])
                for l in range(32):
                    S.op("pe", lambda e, l=l: e.matmul(pc1[:, 0:1], lhsT=w1[:, l, :], rhs=peTb[:, l:l + 1], start=(l == 0), stop=(l == 31)),
                         reads=[w1, peTb], writes=[pc1] if l == 0 else (), pwrites=() if l == 0 else [pc1])
                S.op("dve", lambda e: e.tensor_copy(out=cb[:], in_=pc1[:, 0:1]), reads=[pc1], writes=[cb])
                for g in range(2):
                    S.op("dve", lambda e: e.memset(hid[:], 0.0), writes=[hid])
                    for n0 in range(0, NC, 512):
                        nn = min(512, NC - n0)
                        for l in range(32):
                            S.op("pe", lambda e, l=l, g=g, n0=n0, nn=nn: e.matmul(
                                ph[:, 0:nn], lhsT=w1[:, l, :], rhs=kvT[:, g, n0 * 16 + l: n0 * 16 + l + (nn - 1) * 16 + 1: 16], start=(l == 0), stop=(l == 31)),
                                reads=[w1, kvT], writes=[ph] if l == 0 else (), pwrites=() if l == 0 else [ph])
                        S.op("act", lambda e, n0=n0, nn=nn: e.activation(out=hid[:, n0:n0 + nn], in_=ph[:, 0:nn], func=AF.Silu, bias=cb[:, 0:1]),
                             reads=[ph, cb], pwrites=[hid])
                    if kv == 0:
                        for n0 in range(0, NC, 512):
                            nn = min(512, NC - n0)
                            S.op("pe", lambda e, n0=n0, nn=nn: e.matmul(pk[0:64, 0:nn], lhsT=w2[:], rhs=hid[:, n0:n0 + nn], start=True, stop=True),
                                 reads=[w2, hid], writes=[pk])
                            S.op("dve", lambda e, g=g, n0=n0, nn=nn: e.tensor_copy(out=kcmpT[:, g, n0:n0 + nn], in_=pk[0:64, 0:nn]),
                                 reads=[pk], pwrites=[kcmpT])
                    else:
                        for nt in range(NT):
                            rows = min(128, NC - nt * 128)
                            S.op("pe", lambda e, nt=nt: e.matmul(pk[:, 0:64], lhsT=hid[:, nt * 128:(nt + 1) * 128], rhs=w2[:], start=True, stop=True),
                                 reads=[w2, hid], writes=[pk])
                            S.op("dve", lambda e, g=g, nt=nt: e.tensor_copy(out=Rc[:, nt, g, 0:64], in_=pk[:, 0:64]),
                                 reads=[pk], pwrites=[Rc])
            _barrier(S)
            S.stack_pop()

        qt = Ring([S.sb("B_q%d" % i, [64, 8, 128], BF16) for i in range(2)])
        gt = Ring([S.sb("B_g%d" % i, [128, 24]) for i in range(2)])
        nzt = Ring([S.sb("B_nz%d" % i, [128, 512], BF16) for i in range(2)])
        Et = Ring([S.sb("B_E%d" % i, [128, 512], BF16) for i in range(4)])
        psT = Ring([S.ps("B_psT%d" % i, [128, 512]) for i in range(3)])
        pcA = S.ps("B_pcA", [128, 2, 193])
        pcB = S.ps("B_pcB", [128, 2, 193])
        pos = S.ps("B_pos", [128, 4, 65])
        pow_ = S.ps("B_pow", [128, 4, 65])
        pmisc = S.ps("B_pmisc", [128, 4, 128], BF16)
        P2 = lambda nm, shp, dt=F32: [S.sb("B_%s%d" % (nm, i), shp, dt) for i in range(2)]
        oc2 = P2("oc", [128, 4, 193]); rcs2 = P2("rcs", [128, 4]); rss2 = P2("rss", [128, 4]); rws2 = P2("rws", [128, 4])
        cc2 = P2("cc", [128, 3, 4]); sc_2 = P2("sc", [128, 128]); sc2_2 = P2("sc2", [128, 128]); m1_2 = P2("m1", [128, 8]); m2_2 = P2("m2", [128, 8])
        pws2 = P2("pws", [128, 4, 65]); pss2 = P2("pss", [128, 4, 65])
        negq2 = P2("negq", [128, 2, 128], BF16)
        for _nq in negq2:
            S.op("pool", lambda e, _nq=_nq: e.memset(_nq[:], 0.0), writes=[_nq])
        qAr = {(g_, a_): Ring([S.sb("B_qA%d%d_%d" % (g_, a_, i), [128, 4, 128], BF16) for i in range(2)]) for g_ in range(2) for a_ in range(NA)}
        yg = S.sb("B_yg", [128, 4, 64])
        ytmp = S.sb("B_ytmp", [128, 4, 64])
        ynsaR = Ring([S.sb("B_ynsa%d" % i, [128, 512], BF16) for i in range(2)])
        stg = Ring([S.sb("B_stg%d" % i, [128, 4, 128], BF16) for i in range(2)])

        def qk_exp(kT_ap, kbuf, q_ap, qbuf, neg_lhsT=None):
            p = psT.next()
            if False:
                pass
            else:
                S.op("pe", lambda e, p=p: e.matmul(p[:], lhsT=kT_ap, rhs=q_ap, start=True, stop=True), reads=[kbuf, qbuf], writes=[p])
            E = Et.next()
            S.op("act", lambda e, p=p, E=E: e.activation(out=E[:], in_=p[:], func=AF.Exp), reads=[p], writes=[E])
            return E

        def pipeline(tiles, L=2):
            Es = {}
            n = len(tiles)
            for i in range(n + L):
                if i < n:
                    Es[i] = tiles[i][0]()
                if i - L >= 0:
                    tiles[i - L][1](Es.pop(i - L))

        def mask(E, base, cm, qstep):
            S.op("pool", lambda e, E=E: e.affine_select(out=E[:], in_=E[:], pattern=[[0, 4], [qstep, 128]], compare_op=ALU.is_ge,
                                                       fill=0.0, base=base, channel_multiplier=cm), reads=[E], writes=[E])

        pending_tail = [None]
        for qb in range(SEQ // 128):
            q0 = qb * 128
            q = qt.next(); gg = gt.next(); nz = nzt.next(); ynsa = ynsaR.next()
            S.dma("sp", q[:], scr["qT"].t[:, q0:q0 + 128].rearrange("(h d) t -> d h t", h=8), reads=[scr["qT"]], writes=[q])
            qAs = {}
            for g_ in range(2):
                for a_ in range(min(NA, qb * 128 // HALF + 1)):
                    qa = qAr[(g_, a_)].next()
                    qAs[(g_, a_)] = qa
                    S.dma("sp", qa[0:64, :, :], scr["qT"].t[g_ * 256:(g_ + 1) * 256, q0:q0 + 128].rearrange("(h d) t -> d h t", h=4),
                          reads=[scr["qT"]], writes=[qa])
            S.dma("sp", gg[:], scr["gate"].t[q0:q0 + 128, :], reads=[scr["gate"]], writes=[gg])
            S.dma("sp", nz[:], scr["nzs"].t[q0:q0 + 128, :], reads=[scr["nzs"]], writes=[nz])
            def gbody(g, q=q, gg=gg, nz=nz, qAs=qAs, qb=qb, q0=q0, ynsa=ynsa):
                oc = oc2[g]; rcs = rcs2[g]; rss = rss2[g]; rws = rws2[g]; cc = cc2[g]; sc = sc_2[g]; sc2 = sc2_2[g]
                m1 = m1_2[g]; m2 = m2_2[g]; negq = negq2[g]; pws = pws2[g]; pss = pss2[g]
                q_ap = q[:, 4 * g:4 * g + 4, :].rearrange("d h q -> d (h q)")
                n_max = min(8 * qb + 6, NC - 1)
                ntl = n_max // 128 + 1
                def c_qk(nt, g=g, q_ap=q_ap, q=q):
                    E = qk_exp(kcmpT[:, g, nt * 128:(nt + 1) * 128], kcmpT, q_ap, q)
                    if q0 - 16 * (128 * nt + 127) - 31 < 0:
                        mask(E, q0 - 16 * 128 * nt - 31, -16, 1)
                    return E

                def c_pv(nt, E, g=g, ntl=ntl):
                    for h in range(4):
                        pcx = pcA if h < 2 else pcB
                        first = (nt == 0 and h % 2 == 0)
                        S.op("pe", lambda e, E=E, h=h, pcx=pcx, nt=nt, first=first, g=g, ntl=ntl: e.matmul(
                            pcx[:, h % 2, :], lhsT=E[:, h * 128:(h + 1) * 128], rhs=Rc[:, nt, g, :], start=first,
                            stop=(nt == ntl - 1 and h % 2 == 1), skip_group_check=True),
                            reads=[E, Rc], writes=[pcx] if first else (), pwrites=() if first else [pcx])
                pipeline([(lambda nt=nt: c_qk(nt), lambda E, nt=nt: c_pv(nt, E)) for nt in range(ntl)])
                S.op("act", lambda e: e.copy(out=oc[:, 0:2, :], in_=pcA[:]), reads=[pcA], pwrites=[oc])
                S.op("act", lambda e: e.copy(out=oc[:, 2:4, :], in_=pcB[:]), reads=[pcB], pwrites=[oc])
                S.op("dve", lambda e: e.tensor_scalar(out=rcs[:], in0=oc[:, :, 64], scalar1=1e-30, scalar2=None, op0=ALU.max),
                     reads=[oc], writes=[rcs])
                S.op("dve", lambda e: e.reciprocal(out=rcs[:], in_=rcs[:]), reads=[rcs], writes=[rcs])
                S.op("dve", lambda e: e.tensor_scalar(out=sc[:], in0=oc[:, 0, 65:193], scalar1=rcs[:, 0:1], scalar2=None, op0=ALU.mult),
                     reads=[oc, rcs], writes=[sc])
                for h in range(1, 4):
                    S.op("dve", lambda e, h=h: e.scalar_tensor_tensor(out=sc[:], in0=oc[:, h, 65:193], scalar=rcs[:, h:h + 1], in1=sc[:],
                                                                      op0=ALU.mult, op1=ALU.add), reads=[oc, rcs, sc], writes=[sc])
                for half in range(2):
                    tb = 2 * qb + half
                    ps_ = slice(half * 64, (half + 1) * 64)
                    if tb + 1 < 128:
                        S.op("dve", lambda e, ps_=ps_, tb=tb: e.memset(sc[ps_, tb + 1:128], -1e4), reads=[sc], writes=[sc])
                    lo = max(tb - 1, 0)
                    S.op("dve", lambda e, ps_=ps_, tb=tb, lo=lo: e.memset(sc[ps_, lo:tb + 1], 1e4), reads=[sc], writes=[sc])
                S.op("dve", lambda e: e.memset(sc[:, 0:1], 1e4), reads=[sc], writes=[sc])
                S.op("dve", lambda e: e.max(out=m1[:], in_=sc[:]), reads=[sc], writes=[m1])
                S.op("dve", lambda e: e.match_replace(out=sc2[:], in_to_replace=m1[:], in_values=sc[:], imm_value=-3e4),
                     reads=[sc, m1], writes=[sc2])
                S.op("dve", lambda e: e.max(out=m2[:], in_=sc2[:]), reads=[sc2], writes=[m2])
                S.op("dve", lambda e: e.tensor_scalar(out=negq[:, 0, :], in0=sc[:], scalar1=m2[:, 7:8], scalar2=-1e4, op0=ALU.is_lt, op1=ALU.mult),
                     reads=[sc, m2], pwrites=[negq])
                S.op("dve", lambda e: e.tensor_scalar(out=negq[:, 1, 64:128], in0=sc[:, 0:64], scalar1=m2[:, 7:8], scalar2=-1e4, op0=ALU.is_lt, op1=ALU.mult),
                     reads=[sc, m2], pwrites=[negq])
                yield
                kts = list(range(max(0, qb - 4), qb + 1))

                def w_qk(i, kt, g=g, q_ap=q_ap, q=q, qb=qb):
                    E = qk_exp(kwT[:, g, kt * 128:(kt + 1) * 128], kwT, q_ap, q)
                    if kt == qb - 4:
                        mask(E, -1, 1, -1)
                    if kt == qb:
                        mask(E, 0, -1, 1)
                    return E

                def w_pv(i, kt, E, g=g, kts=kts):
                    for h in range(4):
                        first = (i == 0 and h == 0)
                        S.op("pe", lambda e, E=E, h=h, kt=kt, first=first, last=(i == len(kts) - 1 and h == 3), g=g: e.matmul(
                            pow_[:, h, :], lhsT=E[:, h * 128:(h + 1) * 128], rhs=vw[:, kt, g, :], start=first, stop=last,
                            skip_group_check=True),
                            reads=[E, vw], writes=[pow_] if first else (), pwrites=() if first else [pow_])
                pipeline([(lambda i=i, kt=kt: w_qk(i, kt), lambda E, i=i, kt=kt: w_pv(i, kt, E)) for i, kt in enumerate(kts)])
                S.op("act", lambda e: e.copy(out=pws[:], in_=pow_[:]), reads=[pow_], writes=[pws])
                yield
                na_here = min(NA, qb * 128 // HALF + 1)
                S.op("pe", lambda e: e.transpose(out=pmisc[:, 1, :], in_=negq[:, 1, :], identity=ident[:]), reads=[negq, ident], writes=[pmisc])
                if na_here > 1:
                    S.op("pe", lambda e: e.transpose(out=pmisc[:, 0, :], in_=negq[:, 0, :], identity=ident[:]), reads=[negq, ident], pwrites=[pmisc])
                for a_ in range(na_here):
                    qa = qAs[(g, a_)]
                    S.op("dve", lambda e, qa=qa, a_=a_: e.tensor_copy(out=qa[64:128, :, :],
                                                                   in_=pmisc[64:128, (1 - a_):(2 - a_), :].to_broadcast([64, 4, 128])),
                         reads=[pmisc], pwrites=[qa])
                yield
                def s_qk(kt, g=g, q_ap=q_ap, q=q, qb=qb, qAs=qAs):
                    qa = qAs[(g, kt * 128 // HALF)]
                    E = qk_exp(ksT[:, g, kt * 128:(kt + 1) * 128], ksT, qa[:].rearrange("p h q -> p (h q)"), qa)
                    if kt == qb:
                        mask(E, 0, -1, 1)
                    return E

                def s_pv(kt, E, g=g, qb=qb):
                    for h in range(4):
                        first = (kt == 0 and h == 0)
                        S.op("pe", lambda e, E=E, h=h, kt=kt, first=first, last=(kt == qb and h == 3), g=g: e.matmul(
                            pos[:, h, :], lhsT=E[:, h * 128:(h + 1) * 128], rhs=vs[:, kt, g, :], start=first, stop=last,
                            skip_group_check=True),
                            reads=[E, vs], writes=[pos] if first else (), pwrites=() if first else [pos])
                pipeline([(lambda kt=kt: s_qk(kt), lambda E, kt=kt: s_pv(kt, E)) for kt in range(qb + 1)])
                S.op("act", lambda e: e.copy(out=pss[:], in_=pos[:]), reads=[pos], writes=[pss])
                yield
                S.op("dve", lambda e: e.reciprocal(out=rss[:], in_=pss[:, :, 64]), reads=[pss], writes=[rss])
                S.op("dve", lambda e: e.reciprocal(out=rws[:], in_=pws[:, :, 64]), reads=[pws], writes=[rws])
                gv = gg[:, g * 12:(g + 1) * 12].rearrange("p (h b) -> p b h", b=3)
                for b, rr in enumerate((rcs, rss, rws)):
                    S.op("dve", lambda e, b=b, rr=rr, gv=gv: e.tensor_tensor(out=cc[:, b, :], in0=gv[:, b, :], in1=rr[:], op=ALU.mult),
                         reads=[gg, rr], pwrites=[cc])
                bc = lambda b: cc[:, b, :].unsqueeze(2).to_broadcast([128, 4, 64])
                S.op("dve", lambda e: e.tensor_tensor(out=yg[:], in0=oc[:, :, 0:64], in1=bc(0), op=ALU.mult), reads=[oc, cc], writes=[yg])
                S.op("dve", lambda e: e.tensor_tensor(out=ytmp[:], in0=pss[:, :, 0:64], in1=bc(1), op=ALU.mult), reads=[pss, cc], writes=[ytmp])
                S.op("pool", lambda e: e.tensor_tensor(out=yg[:], in0=yg[:], in1=ytmp[:], op=ALU.add), reads=[yg, ytmp], writes=[yg])
                S.op("dve", lambda e: e.tensor_tensor(out=ytmp[:], in0=pws[:, :, 0:64], in1=bc(2), op=ALU.mult), reads=[pws, cc], writes=[ytmp])
                S.op("pool", lambda e: e.tensor_tensor(out=yg[:], in0=yg[:], in1=ytmp[:], op=ALU.add), reads=[yg, ytmp], writes=[yg])
                S.op("pool", lambda e, g=g, nz=nz: e.tensor_tensor(out=ynsa[:, g * 256:(g + 1) * 256], in0=yg[:].rearrange("p h d -> p (h d)"),
                                                                   in1=nz[:, g * 256:(g + 1) * 256], op=ALU.mult),
                     reads=[yg, nz], pwrites=[ynsa])
                yield
            gens = [gbody(0), gbody(1)]
            for _st in range(5):
                for gen_ in gens:
                    next(gen_)
                if _st == 1 and pending_tail[0] is not None:
                    pending_tail[0]()
                    pending_tail[0] = None

            def tail(ynsa=ynsa, q0=q0):
                for k in range(4):
                    S.op("pe", lambda e, k=k: e.transpose(out=pmisc[:, k, :], in_=ynsa[:, k * 128:(k + 1) * 128], identity=ident[:]),
                         reads=[ynsa, ident], writes=[pmisc] if k == 0 else (), pwrites=() if k == 0 else [pmisc])
                sg = stg.next()
                S.op("act", lambda e, sg=sg: e.copy(out=sg[:], in_=pmisc[:]), reads=[pmisc], writes=[sg])
                S.dma("sp", scr["ysT"].t[0, :, q0:q0 + 128].rearrange("(k p) t -> p k t", p=128), sg[:], reads=[sg], pwrites=[scr["ysT"]], key=sg)
            pending_tail[0] = tail
        if pending_tail[0] is not None:
            pending_tail[0]()
        _barrier(S)
        S.stack_pop()


def phase_D_seq(S, nc, SEQ, lyr, Wd, scr):
    TP = 128
    TB = 8
    GN_EPS = 64e-5
    xtok = scr["xtok"]
    with contextlib.ExitStack() as st:
        S.stack_push(st)
        identF = make_ident(S, "D_ident", F32)
        ones = S.sb("D_ones", [128, 128])
        S.op("pool", lambda e: e.memset(ones[:], 0.0), writes=[ones])
        S.op("pool", lambda e: e.memset(ones[0:64, 0:64], 1.0), reads=[ones], writes=[ones])
        S.op("pool", lambda e: e.memset(ones[64:128, 64:128], 1.0), reads=[ones], writes=[ones])

        def cvec(name, key, n):
            t = S.sb("D_" + name, [128, n])
            S.dma("sp", t[:], Wd[key].t[lyr].rearrange("(c p) -> p c", p=128), reads=[Wd[key]], writes=[t],
                  allow_slow_non_contiguous=True)
            return t

        def cvec2(name, key):
            t = S.sb("D_" + name, [128, 4])
            S.dma("sp", t[:], Wd[key].t[lyr].rearrange("(c a) j -> (a j) c", a=2), reads=[Wd[key]], writes=[t],
                  allow_slow_non_contiguous=True)
            return t
        mu = cvec("mu", "rk_mu", 13)
        w0 = cvec("w0", "rk_w0", 4)
        a0 = cvec("a0", "rk_a0", 4)
        lg = cvec("lg", "rk_lnx_g", 4)
        lb = cvec("lb", "rk_lnx_b", 4)
        kkc = cvec2("kkc", "rk_kk")
        ka = cvec2("ka", "rk_ka")
        rkc = cvec2("rkc", "rk_rk")
        omka = S.sb("D_omka", [128, 4])
        S.op("pool", lambda e: e.tensor_scalar(out=omka[:], in0=ka[:], scalar1=-1.0, scalar2=1.0, op0=ALU.mult, op1=ALU.add),
             reads=[ka], writes=[omka])
        w2 = S.sb("D_w2", [64, 512], BF16)
        a2 = S.sb("D_a2", [128, 512], BF16)
        S.dma("pool", w2[:], Wd["rk_w2"].t[lyr], reads=[Wd["rk_w2"]], writes=[w2])
        S.dma("pool", a2[64:128, :], Wd["rk_a2"].t[lyr], reads=[Wd["rk_a2"]], writes=[a2])
        St = S.sb("D_state", [128, 4, 64])
        S.op("dve", lambda e: e.memset(St[:], 0.0), writes=[St])

        rst = S.sb("D_rst", [128, 13, TP + 1])
        xs = S.sb("D_xs", [128, 13, TP])
        th = S.sb("D_th", [128, TP], BF16)
        dd = S.sb("D_dd", [128, 4, TP])
        aa = S.sb("D_aa", [128, 4, TP])
        kkf = S.sb("D_kkf", [128, 4, TP])
        sq = S.sb("D_sq", [128, 4, TP])
        rn = S.sb("D_rn", [128, 4, TP])
        kp = S.sb("D_kp", [128, 4, TP])
        am = S.sb("D_am", [128, 4, TP])
        bm = S.sb("D_bm", [128, 4, TP])
        t1 = S.sb("D_t1", [128, 4, TP])
        bonus = S.sb("D_bonus", [128, 4, TP])
        vv = S.sb("D_vv", [128, 4, TP])
        tk = S.sb("D_tk", [128, 5, 4, 128])
        bcr = Ring([S.sb("D_bc%d" % i, [128, TB, 5, 256]) for i in range(2)])
        tmp = S.sb("D_tmp", [128, 4, 64])
        tmp2 = S.sb("D_tmp2", [128, 4, 64])
        kv = Ring([S.sb("D_kv%d" % i, [128, 4, 64]) for i in range(2)])
        sa = S.sb("D_sa", [128, 4])
        ybuf = S.sb("D_y", [128, 4, TP])
        ysq = S.sb("D_ysq", [128, 4, TP])
        mean = S.sb("D_mean", [128, 4, TP])
        var = S.sb("D_var", [128, 4, TP])
        rzt = S.sb("D_rz", [128, 4, TP], BF16)
        yo = S.sb("D_yo", [128, 4, TP], BF16)
        pa = Ring([S.ps("D_pa%d" % i, [128, 4, 128]) for i in range(4)])

        bc4 = lambda t: t[:].unsqueeze(2).to_broadcast([128, 4, TP])
        for nb in range(SEQ // TP):
            t0 = nb * TP
            S.dma("sp", rst[:, :, 1:TP + 1], scr["rsT"].t[:, t0:t0 + TP].rearrange("(c p) t -> p c t", p=128), reads=[scr["rsT"]],
                  writes=[rst])
            if nb == 0:
                S.op("pool", lambda e: e.memset(rst[:, :, 0:1], 0.0), reads=[rst], pwrites=[rst])
            else:
                S.dma("sp", rst[:, :, 0:1], scr["rsT"].t[:, t0 - 1:t0].rearrange("(c p) t -> p c t", p=128), reads=[scr["rsT"]],
                      pwrites=[rst], key=rst, allow_slow_non_contiguous=True)
            S.op("pool", lambda e: e.tensor_tensor(out=xs[:], in0=rst[:, :, 0:TP], in1=rst[:, :, 1:TP + 1], op=ALU.subtract),
                 reads=[rst], writes=[xs])
            S.op("pool", lambda e: e.tensor_tensor(out=xs[:], in0=xs[:], in1=mu[:].unsqueeze(2).to_broadcast([128, 13, TP]), op=ALU.mult),
                 reads=[xs, mu], writes=[xs])
            S.op("pool", lambda e: e.tensor_tensor(out=xs[:], in0=xs[:], in1=rst[:, :, 1:TP + 1], op=ALU.add), reads=[xs, rst], writes=[xs])
            r = xs[:, 0:4, :]; k = xs[:, 4:8, :]; v = xs[:, 8:12, :]
            S.op("act", lambda e: e.activation(out=th[0:64, :], in_=xs[0:64, 12, :], func=AF.Tanh), reads=[xs], pwrites=[th])
            S.op("act", lambda e: e.copy(out=th[64:128, :], in_=xs[64:128, 12, :]), reads=[xs], pwrites=[th])
            pw = pa.next(); pp = pa.next()
            for p in range(4):
                S.op("pe", lambda e, p=p, pw=pw: e.matmul(pw[:, p, :], lhsT=w2[0:64, p * 128:(p + 1) * 128], rhs=th[0:64, :], start=True, stop=True),
                     reads=[w2, th], writes=[pw] if p == 0 else (), pwrites=() if p == 0 else [pw])
                S.op("pe", lambda e, p=p, pp=pp: e.matmul(pp[:, p, :], lhsT=a2[64:128, p * 128:(p + 1) * 128], rhs=th[64:128, :], start=True, stop=True),
                     reads=[a2, th], writes=[pp] if p == 0 else (), pwrites=() if p == 0 else [pp])
            for p in range(4):
                S.op("act", lambda e, p=p, pw=pw: e.activation(out=dd[:, p, :], in_=pw[:, p, :], func=AF.Sigmoid, bias=w0[:, p:p + 1]),
                     reads=[pw, w0], pwrites=[dd])
                S.op("act", lambda e, p=p, pp=pp: e.activation(out=aa[:, p, :], in_=pp[:, p, :], func=AF.Sigmoid, bias=a0[:, p:p + 1]),
                     reads=[pp, a0], pwrites=[aa])
            S.op("act", lambda e: e.activation(out=dd[:], in_=dd[:], func=AF.Exp, scale=-0.6065306597126334), reads=[dd], writes=[dd])
            S.op("pool", lambda e: e.tensor_tensor(out=kkf[:], in0=k, in1=bc4(kkc), op=ALU.mult), reads=[xs, kkc], writes=[kkf])
            S.op("pool", lambda e: e.tensor_tensor(out=sq[:], in0=kkf[:], in1=kkf[:], op=ALU.mult), reads=[kkf], writes=[sq])
            pn = pa.next()
            for p in range(4):
                S.op("pe", lambda e, p=p, pn=pn: e.matmul(pn[:, p, :], lhsT=ones[:], rhs=sq[:, p, :], start=True, stop=True),
                     reads=[ones, sq], writes=[pn] if p == 0 else (), pwrites=() if p == 0 else [pn])
            S.op("act", lambda e, pn=pn: e.activation(out=rn[:], in_=pn[:], func=AF.Sqrt), reads=[pn], writes=[rn])
            S.op("pool", lambda e: e.tensor_scalar(out=rn[:], in0=rn[:], scalar1=1e-12, scalar2=None, op0=ALU.max), reads=[rn], writes=[rn])
            S.op("dve", lambda e: e.reciprocal(out=rn[:], in_=rn[:]), reads=[rn], writes=[rn])
            S.op("pool", lambda e: e.tensor_tensor(out=kkf[:], in0=kkf[:], in1=rn[:], op=ALU.mult), reads=[kkf, rn], writes=[kkf])
            S.op("pool", lambda e: e.tensor_tensor(out=t1[:], in0=aa[:], in1=bc4(ka), op=ALU.mult), reads=[aa, ka], writes=[t1])
            S.op("pool", lambda e: e.tensor_tensor(out=t1[:], in0=t1[:], in1=bc4(omka), op=ALU.add), reads=[t1, omka], writes=[t1])
            S.op("pool", lambda e: e.tensor_tensor(out=kp[:], in0=k, in1=t1[:], op=ALU.mult), reads=[xs, t1], writes=[kp])
            S.op("pool", lambda e: e.tensor_scalar(out=am[:], in0=kkf[:], scalar1=-1.0, scalar2=None, op0=ALU.mult), reads=[kkf], writes=[am])
            S.op("pool", lambda e: e.tensor_tensor(out=bm[:], in0=kkf[:], in1=aa[:], op=ALU.mult), reads=[kkf, aa], writes=[bm])
            S.op("pool", lambda e: e.tensor_tensor(out=t1[:], in0=r, in1=kp[:], op=ALU.mult), reads=[xs, kp], writes=[t1])
            S.op("pool", lambda e: e.tensor_tensor(out=sq[:], in0=t1[:], in1=bc4(rkc), op=ALU.mult), reads=[t1, rkc], writes=[sq])
            pr = pa.next()
            for p in range(4):
                S.op("pe", lambda e, p=p, pr=pr: e.matmul(pr[:, p, :], lhsT=ones[:], rhs=sq[:, p, :], start=True, stop=True),
                     reads=[ones, sq], writes=[pr] if p == 0 else (), pwrites=() if p == 0 else [pr])
            S.op("act", lambda e, pr=pr: e.copy(out=bonus[:], in_=pr[:]), reads=[pr], writes=[bonus])
            S.op("pool", lambda e: e.tensor_tensor(out=bonus[:], in0=bonus[:], in1=v, op=ALU.mult), reads=[bonus, xs], writes=[bonus])
            S.op("pool", lambda e: e.tensor_copy(out=vv[:], in_=v), reads=[xs], writes=[vv])
            S.op("pool", lambda e: e.tensor_copy(out=t1[:], in_=r), reads=[xs], writes=[t1])
            for oi, src in enumerate((am, bm, dd, kp, t1)):
                pt = pa.next()
                for p in range(4):
                    S.op("pe", lambda e, p=p, src=src, pt=pt: e.transpose(out=pt[:, p, :], in_=src[:, p, :], identity=identF[:]),
                         reads=[src, identF], writes=[pt] if p == 0 else (), pwrites=() if p == 0 else [pt])
                S.op("act", lambda e, oi=oi, pt=pt: e.copy(out=tk[:, oi, :, :], in_=pt[:]), reads=[pt], pwrites=[tk])
            for oi in range(5):
                for h2 in range(2):
                    S.dma("sp", xtok.t[t0:t0 + TP, oi, h2, :].rearrange("t (p j) -> t p j", p=4), tk[:, oi, :, h2 * 64:(h2 + 1) * 64],
                          reads=[tk], pwrites=[xtok], key=tk)
            S.dma("sp", rzt[:], scr["rzT"].t[:, t0:t0 + TP].rearrange("(c p) t -> p c t", p=128), reads=[scr["rzT"]], writes=[rzt])
            xflat = xtok.t.rearrange("t o h c -> (t o) h c")
            for tb in range(0, TP, TB):
                bc = bcr.next()
                for h2 in range(2):
                    S.dma("sp", bc[h2 * 64:(h2 + 1) * 64, :, :, :].rearrange("p t o c -> p (t o) c"),
                          xflat[(t0 + tb) * 5:(t0 + tb + TB) * 5, h2, :].partition_broadcast(64),
                          reads=[xtok], writes=[bc] if h2 == 0 else (), pwrites=() if h2 == 0 else [bc], key=bc)
                for tt in range(TB):
                    t = tb + tt
                    A = bc[:, tt, 0, :].rearrange("p (a j) -> p a j", a=4)
                    B = bc[:, tt, 1, :].rearrange("p (a j) -> p a j", a=4)
                    Dd = bc[:, tt, 2, :].rearrange("p (a j) -> p a j", a=4)
                    Kk = bc[:, tt, 3, :].rearrange("p (a j) -> p a j", a=4)
                    R = bc[:, tt, 4, :].rearrange("p (a j) -> p a j", a=4)
                    kvb = kv.next()
                    S.op("pool", lambda e, Kk=Kk, t=t, kvb=kvb: e.tensor_tensor(out=kvb[:], in0=Kk, in1=vv[:, :, t:t + 1].to_broadcast([128, 4, 64]),
                                                                              op=ALU.mult), reads=[bc, vv], writes=[kvb])
                    S.op("dve", lambda e, A=A: e.tensor_tensor(out=tmp[:], in0=St[:], in1=A, op=ALU.mult), reads=[St, bc], writes=[tmp])
                    S.op("dve", lambda e: e.tensor_reduce(out=sa[:], in_=tmp[:], axis=AX.X, op=ALU.add), reads=[tmp], writes=[sa])
                    S.op("dve", lambda e, Dd=Dd: e.tensor_tensor(out=St[:], in0=St[:], in1=Dd, op=ALU.mult), reads=[St, bc, tmp], writes=[St])
                    S.op("dve", lambda e, B=B: e.tensor_tensor(out=tmp2[:], in0=B, in1=sa[:].unsqueeze(2).to_broadcast([128, 4, 64]), op=ALU.mult),
                         reads=[bc, sa], writes=[tmp2])
                    S.op("dve", lambda e: e.tensor_tensor(out=St[:], in0=St[:], in1=tmp2[:], op=ALU.add), reads=[St, tmp2], writes=[St])
                    S.op("dve", lambda e, kvb=kvb: e.tensor_tensor(out=St[:], in0=St[:], in1=kvb[:], op=ALU.add), reads=[St, kvb], writes=[St])
                    S.op("dve", lambda e, R=R: e.tensor_tensor(out=tmp[:], in0=St[:], in1=R, op=ALU.mult), reads=[St, bc], writes=[tmp])
                    S.op("dve", lambda e, t=t: e.tensor_reduce(out=ybuf[:, :, t], in_=tmp[:], axis=AX.X, op=ALU.add), reads=[tmp], pwrites=[ybuf])
            S.op("pool", lambda e: e.tensor_tensor(out=ysq[:], in0=ybuf[:], in1=ybuf[:], op=ALU.mult), reads=[ybuf], writes=[ysq])
            pm = pa.next(); pq = pa.next()
            for p in range(4):
                S.op("pe", lambda e, p=p, pm=pm: e.matmul(pm[:, p, :], lhsT=ones[:], rhs=ybuf[:, p, :], start=True, stop=True),
                     reads=[ones, ybuf], writes=[pm] if p == 0 else (), pwrites=() if p == 0 else [pm])
                S.op("pe", lambda e, p=p, pq=pq: e.matmul(pq[:, p, :], lhsT=ones[:], rhs=ysq[:, p, :], start=True, stop=True),
                     reads=[ones, ysq], writes=[pq] if p == 0 else (), pwrites=() if p == 0 else [pq])
            S.op("act", lambda e, pm=pm: e.activation(out=mean[:], in_=pm[:], func=AF.Copy, scale=1.0 / 64), reads=[pm], writes=[mean])
            S.op("act", lambda e, pq=pq: e.activation(out=var[:], in_=pq[:], func=AF.Copy, scale=1.0 / 64), reads=[pq], writes=[var])
            S.op("pool", lambda e: e.tensor_tensor(out=ysq[:], in0=mean[:], in1=mean[:], op=ALU.mult), reads=[mean, ysq], writes=[ysq])
            S.op("pool", lambda e: e.tensor_tensor(out=var[:], in0=var[:], in1=ysq[:], op=ALU.subtract), reads=[var, ysq], writes=[var])
            S.op("act", lambda e: e.activation(out=var[:], in_=var[:], func=AF.Sqrt, bias=GN_EPS, scale=1.0), reads=[var], writes=[var])
            S.op("dve", lambda e: e.reciprocal(out=var[:], in_=var[:]), reads=[var], writes=[var])
            S.op("pool", lambda e: e.tensor_tensor(out=mean[:], in0=ybuf[:], in1=mean[:], op=ALU.subtract), reads=[ybuf, mean], writes=[mean])
            S.op("pool", lambda e: e.tensor_tensor(out=mean[:], in0=mean[:], in1=var[:], op=ALU.mult), reads=[mean, var], writes=[mean])
            S.op("pool", lambda e: e.tensor_tensor(out=mean[:], in0=mean[:], in1=bc4(lg), op=ALU.mult), reads=[mean, lg], writes=[mean])
            S.op("pool", lambda e: e.tensor_tensor(out=mean[:], in0=mean[:], in1=bc4(lb), op=ALU.add), reads=[mean, lb], writes=[mean])
            S.op("pool", lambda e: e.tensor_tensor(out=mean[:], in0=mean[:], in1=bonus[:], op=ALU.add), reads=[mean, bonus], writes=[mean])
            S.op("pool", lambda e: e.tensor_tensor(out=yo[:], in0=mean[:], in1=rzt[:], op=ALU.mult), reads=[mean, rzt], writes=[yo])
            S.dma("sp", scr["ysT"].t[2, :, t0:t0 + TP].rearrange("(c p) t -> p c t", p=128), yo[:], reads=[yo], pwrites=[scr["ysT"]], key=yo)
        _barrier(S)
        S.stack_pop()


_NC_CACHE = {}


def kernel(**inputs):
    SEQ = 8192
    if "nc" not in _NC_CACHE:
        _NC_CACHE["nc"] = build(SEQ, nlayers=2, enable=(1, 1, 1), scr_kind="Internal")
    nc = _NC_CACHE["nc"]
    x = np.ascontiguousarray(np.asarray(inputs["x"], dtype=np.float32))
    p = np.asarray(inputs["p"], dtype=np.float32)
    base = {}
    for k in WSPEC:
        v = np.ascontiguousarray(np.asarray(inputs[k], dtype=np.float32))
        base[k] = v.reshape(WSPEC[k])
    in_maps = []
    for b in range(8):
        m = dict(base)
        m["x"] = np.ascontiguousarray(x[b])
        m["p"] = np.ascontiguousarray(p[:, b])
        in_maps.append(m)
    res = run_bass_kernel_spmd(nc, in_maps, core_ids=list(range(8)))
    return np.stack([np.asarray(r["out"], dtype=np.float32) for r in res.results], axis=0)
```

```python
import contextlib
import numpy as np
import concourse.bass as bass
import concourse.mybir as mybir

F32 = mybir.dt.float32
BF16 = mybir.dt.bfloat16
AF = mybir.ActivationFunctionType
ALU = mybir.AluOpType
AX = mybir.AxisListType

ENGS = ("pe", "act", "dve", "pool", "sp")


class Buf:
    __slots__ = ("name", "w", "wfull", "r", "t")

    def __init__(self, name, t=None):
        self.name = name
        self.t = t
        self.w = []
        self.wfull = []
        self.r = []

    def __getitem__(self, k):
        return self.t[k]


class Op:
    __slots__ = ("eng", "fn", "deps", "marked", "tick", "dma", "idx")

    def __init__(self, eng, fn, dma):
        self.eng = eng
        self.fn = fn
        self.deps = []
        self.marked = False
        self.tick = None
        self.dma = dma
        self.idx = None


class DmaSem:
    def __init__(self):
        self.sem = None
        self.count = 0


class Sched:
    def __init__(self, nc, stack):
        self.nc = nc
        self.stack = stack
        self.ops = {e: [] for e in ENGS}
        self.all_ops = []
        self.dsems = {}
        self.n_sems = 0
        self.fence = []
        self.stacks = [stack]
        self.phase_keys = []
        self.free_ds = []
        self.all_ds = []
        self.keep = []

    def stack_push(self, st):
        self.stacks.append(st)
        self.phase_keys.append([])

    def stack_pop(self):
        self.stacks.pop()
        for kid in self.phase_keys.pop():
            ds = self.dsems.pop(kid, None)
            if ds is not None:
                self.free_ds.append(ds)

    def sb(self, name, shape, dt=F32):
        self.n_sems += 1
        name = "%s_u%d" % (name, self.n_sems)
        t = self.stacks[-1].enter_context(self.nc.sbuf_tensor(name, list(shape), dt))
        return Buf(name, t)

    def ps(self, name, shape, dt=F32):
        self.n_sems += 1
        name = "%s_u%d" % (name, self.n_sems)
        t = self.stacks[-1].enter_context(self.nc.psum_tensor(name, list(shape), dt))
        return Buf(name, t)

    def dram(self, name, shape, dt, kind="Internal"):
        t = self.nc.dram_tensor(name, list(shape), dt, kind=kind)
        return Buf(name, t.ap())

    def _add(self, eng, fn, reads, writes, pwrites, dma):
        op = Op(eng, fn, dma)
        deps = list(self.fence)
        for b in reads:
            deps.extend(b.w)
        for b in writes:
            deps.extend(b.w)
            deps.extend(b.r)
        for b in pwrites:
            deps.extend(b.wfull)
            deps.extend(b.r)
        seen = set()
        for d in deps:
            if id(d) in seen or d is op:
                continue
            seen.add(id(d))
            if d.eng == "pe" and eng == "pe" and d.dma is None and dma is None:
                continue
            op.deps.append(d)
            d.marked = True
        for b in reads:
            b.r.append(op)
            if len(b.r) > 24:
                b.r = self._prune(b.r)
        for b in writes:
            b.w = [op]
            b.wfull = [op]
            b.r = []
        for b in pwrites:
            b.w.append(op)
            if len(b.w) > 24:
                b.w = self._prune(b.w)
        op.idx = len(self.all_ops)
        self.all_ops.append(op)
        self.ops[eng].append(op)
        return op

    @staticmethod
    def _prune(lst):
        last = {}
        for o in lst:
            key = (o.eng, None) if o.dma is None else ("dma", id(o.dma))
            last[key] = o
        return list(last.values())

    def op(self, eng, fn, reads=(), writes=(), pwrites=()):
        return self._add(eng, fn, reads, writes, pwrites, None)

    def dma(self, eng, out_ap, in_ap, reads=(), writes=(), pwrites=(), key=None, **kw):
        if key is None:
            key = (list(writes) + list(pwrites))[0]
        ds = self.dsems.get(id(key))
        if ds is None:
            if self.free_ds:
                ds = self.free_ds.pop()
            else:
                ds = DmaSem()
                self.all_ds.append(ds)
            self.dsems[id(key)] = ds
            self.keep.append(key)
            if self.phase_keys:
                self.phase_keys[-1].append(id(key))
        fn = lambda e, o=out_ap, i=in_ap, kw=kw: e.dma_start(out=o, in_=i, **kw)
        op = self._add(eng, fn, reads, writes, pwrites, ds)
        ds.count += 16
        op.tick = ds.count
        return op

    def barrier_bufs(self, bufs):
        pass

    def emit(self):
        nc = self.nc
        stack = self.stack
        esem = {}
        for e in ENGS:
            esem[e] = stack.enter_context(nc.semaphore("s_" + e))
        for ds in self.all_ds:
            ds.sem = stack.enter_context(nc.semaphore("d%d" % self.n_sems))
            self.n_sems += 1
        for e in ENGS:
            c = 0
            for o in self.ops[e]:
                if o.dma is None:
                    if o.marked:
                        c += 1
                        o.tick = c
        self.max_ticks = {e: max([o.tick or 0 for o in self.ops[e] if o.dma is None] + [0]) for e in ENGS}

        def evkey(d):
            if d.dma is not None:
                return ("d", id(d.dma)), d.dma.sem, d.tick
            return ("e", d.eng), esem[d.eng], d.tick

        def run(eng_name, eng):
            seen = {}
            for o in self.ops[eng_name]:
                waits = {}
                for d in o.deps:
                    k, sem, val = evkey(d)
                    if seen.get(k, 0) >= val:
                        continue
                    if k not in waits or waits[k][1] < val:
                        waits[k] = (sem, val)
                for k, (sem, val) in waits.items():
                    eng.wait_ge(sem, val)
                    seen[k] = val
                inst = o.fn(eng)
                if o.dma is not None:
                    inst.then_inc(o.dma.sem, 16)
                elif o.marked:
                    inst.then_inc(esem[eng_name], 1)
            if eng_name == "sp":
                for e2 in ENGS:
                    m = self.max_ticks[e2]
                    if m > 0:
                        eng.wait_ge(esem[e2], m)
                for ds in self.all_ds:
                    if ds.count:
                        eng.wait_ge(ds.sem, ds.count)

        block = stack.enter_context(nc.Block())

        @block.tensor
        def _(e):
            run("pe", e)

        @block.scalar
        def _(e):
            run("act", e)

        @block.vector
        def _(e):
            run("dve", e)

        @block.gpsimd
        def _(e):
            run("pool", e)

        @block.sync
        def _(e):
            run("sp", e)


from concourse.bass_utils import run_bass_kernel_spmd

D = 1024
NCOL = 8600
PLE = 256
EPS = 1e-6


DEBUG = {}
_dbg_n = [0]


def dbg_dump(S, name, ap, buf, shape, cond=True):
    if not DEBUG.get("on") or not cond:
        return
    _dbg_n[0] += 1
    t = S.stacks[-1].enter_context(S.nc.sbuf_tensor("dbgsb_%d" % _dbg_n[0], list(shape), F32))
    tb = Buf("dbgsb", t)
    d = S.dram("dbg_" + name, list(shape), F32, kind="ExternalOutput")
    S.op("act", lambda e: e.copy(out=t[:], in_=ap), reads=[buf], writes=[tb])
    S.dma("sp", d.t, t[:], reads=[tb], writes=[d], key=tb)


class Ring:
    def __init__(self, bufs):
        self.bufs = bufs
        self.i = 0

    def next(self):
        b = self.bufs[self.i % len(self.bufs)]
        self.i += 1
        return b


def _barrier(S):
    fence = []
    for e in ENGS:
        comp = [o for o in S.ops[e] if o.dma is None]
        if comp:
            fence.append(comp[-1])
    lastd = {}
    for o in S.all_ops:
        if o.dma is not None:
            lastd[id(o.dma)] = o
    fence.extend(lastd.values())
    S.fence = fence


def make_ident(S, name="ident", dt=BF16):
    ident = S.sb(name, [128, 128], dt)
    S.op("pool", lambda e: e.memset(ident[:], 0.0), writes=[ident])
    S.op("pool", lambda e: e.affine_select(out=ident[:], in_=ident[:], pattern=[[-1, 128]],
                                           compare_op=ALU.not_equal, fill=1.0, base=0,
                                           channel_multiplier=1), reads=[ident], writes=[ident])
    return ident


def load_w_bf16(S, dst, k, src_ap, srcbuf):
    S.dma("pool", dst, src_ap, reads=[srcbuf], pwrites=[k], key=k, max_dma_last_dim=4096)


def rmsnorm_tile(S, xt_ap, xt_buf, g_buf, h_ap, h_buf, sq, ss, rs, eps=EPS, extra_reads=()):
    S.op("act", lambda e: e.activation(out=sq[:], in_=xt_ap, func=AF.Square, accum_out=ss[:]),
         reads=[xt_buf] + list(extra_reads), writes=[sq, ss])
    S.op("act", lambda e: e.activation(out=rs[:], in_=ss[:], func=AF.Sqrt, scale=1.0 / D, bias=eps),
         reads=[ss], writes=[rs])
    S.op("dve", lambda e: e.reciprocal(out=rs[:], in_=rs[:]), reads=[rs], writes=[rs])
    S.op("dve", lambda e: e.scalar_tensor_tensor(out=h_ap, in0=xt_ap, scalar=rs[:, 0:1], in1=g_buf[:],
                                                 op0=ALU.mult, op1=ALU.mult),
         reads=[xt_buf, rs, g_buf], pwrites=[h_buf])


def phase_A(S, nc, SEQ, lyr, x_src, Wd, scr):
    TT = 512
    nsub = TT // 128
    with contextlib.ExitStack() as st:
        S.stack_push(st)
        wt = S.sb("A_w", [128, 8, NCOL], BF16)
        gt = S.sb("A_g", [128, D])
        ident = make_ident(S, "A_ident")
        xt = S.sb("A_x", [128, nsub, D])
        sq = S.sb("A_sq", [128, D], BF16)
        ss = S.sb("A_ss", [128, 1])
        rs = S.sb("A_rs", [128, 1])
        h = S.sb("A_h", [128, nsub, D], BF16)
        hT = S.sb("A_hT", [128, 8, TT], BF16)
        stg_b = Ring([S.sb("A_sb%d" % i, [128, 512], BF16) for i in range(4)])
        stg_f = Ring([S.sb("A_sf%d" % i, [128, 512], F32) for i in range(3)])
        pT = Ring([S.ps("A_pT%d" % i, [128, 8, 128], BF16) for i in range(2)])
        pacc = Ring([S.ps("A_pa%d" % i, [128, 512], F32) for i in range(6)])

        w_in = Wd["w_in"]
        WG = [(0, 1280), (3352, 5528), (5528, 7064), (7064, 8600), (1280, 3352)]
        wtg = [Buf("A_wg%d" % i, wt.t) for i in range(len(WG))]

        def wgrp(c0):
            for i, (lo, hi) in enumerate(WG):
                if lo <= c0 < hi:
                    return wtg[i]
            raise ValueError(c0)
        for gi, (lo, hi) in enumerate(WG):
            for k in range(8):
                S.dma("pool", wt[:, k, lo:hi], w_in.t[lyr, k * 128:(k + 1) * 128, lo:hi], reads=[w_in], pwrites=[wtg[gi]],
                      key=wtg[gi], max_dma_last_dim=4096)
        S.dma("sp", gt[:], Wd["norm_g"].t[lyr:lyr + 1, :].partition_broadcast(128), reads=[Wd["norm_g"]],
              writes=[gt])

        FM = []
        for c in range(4):
            FM.append((c * 128, scr["qT"], c * 128, AF.Copy, 0.125, BF16))
        FM.append((512, scr["kcT"], 0, None, 1.0, BF16))
        FM.append((640, scr["vcT"], 0, None, 1.0, BF16))
        FM.append((768, scr["ksT"], 0, None, 1.0, BF16))
        FM.append((1024, scr["kwT"], 0, None, 1.0, BF16))
        for c in range(13):
            FM.append((3352 + c * 128, scr["rsT"], c * 128, None, 1.0, F32))
        for c in range(4):
            FM.append((5016 + c * 128, scr["rzT"], c * 128, AF.Silu, 1.0, BF16))
        for c in range(24):
            FM.append((5528 + c * 128, scr["mgT"], c * 128, AF.Sigmoid, 1.0, BF16))
        TM = [
            (896, 128, scr["vsw"], 0, None, BF16),
            (1152, 128, scr["vsw"], 128, None, BF16),
            (1280, 24, scr["gate"], 0, AF.Sigmoid, F32),
            (1304, 512, scr["nzs"], 0, AF.Silu, BF16),
            (1816, 512, scr["su"], 0, None, F32),
            (2328, 512, scr["sv"], 0, None, F32),
            (2840, 512, scr["szs"], 0, AF.Silu, BF16),
        ]
        evac_i = [0]

        def evac(out_ap, out_buf, in_ap, in_buf, func, scale):
            if func is None and scale == 1.0:
                if evac_i[0] % 2 == 0:
                    S.op("dve", lambda e: e.tensor_copy(out=out_ap, in_=in_ap), reads=[in_buf], writes=[out_buf])
                else:
                    S.op("act", lambda e: e.copy(out=out_ap, in_=in_ap), reads=[in_buf], writes=[out_buf])
                evac_i[0] += 1
            else:
                S.op("act", lambda e: e.activation(out=out_ap, in_=in_ap, func=func, scale=scale),
                     reads=[in_buf], writes=[out_buf])

        for ti in range(SEQ // TT):
            t0 = ti * TT
            S.dma("sp", xt[:], x_src.t[t0:t0 + TT, :].rearrange("(s p) d -> p s d", p=128), reads=[x_src],
                  writes=[xt])
            for s in range(nsub):
                rmsnorm_tile(S, xt[:, s, :], xt, gt, h[:, s, :], h, sq, ss, rs)
                pt = pT.next()
                for k in range(8):
                    S.op("pe", lambda e, k=k, s=s, pt=pt: e.transpose(out=pt[:, k, :], in_=h[:, s, k * 128:(k + 1) * 128],
                                                                     identity=ident[:]),
                         reads=[h, ident], writes=[pt] if k == 0 else (), pwrites=() if k == 0 else [pt])
                S.op("dve", lambda e, s=s, pt=pt: e.tensor_copy(out=hT[:, :, s * 128:(s + 1) * 128], in_=pt[:]),
                     reads=[pt], pwrites=[hT])
            for (c0, dbuf, r0, func, scale, dt) in FM:
                pa = pacc.next()
                for k in range(8):
                    S.op("pe", lambda e, k=k, pa=pa, c0=c0: e.matmul(pa[:], lhsT=wt[:, k, c0:c0 + 128], rhs=hT[:, k, :],
                                                                    start=(k == 0), stop=(k == 7)),
                         reads=[wgrp(c0), hT], writes=[pa] if k == 0 else (), pwrites=() if k == 0 else [pa])
                sg = stg_b.next() if dt == BF16 else stg_f.next()
                evac(sg[:], sg, pa[:], pa, func, scale)
                S.dma("sp", dbuf.t[r0:r0 + 128, t0:t0 + TT], sg[:], reads=[sg], pwrites=[dbuf], key=sg)
            for s in range(nsub):
                for (c0, ncol, dbuf, dc0, func, dt) in TM:
                    pa = pacc.next()
                    for k in range(8):
                        S.op("pe", lambda e, k=k, pa=pa, c0=c0, ncol=ncol, s=s: e.matmul(
                            pa[:, 0:ncol], lhsT=hT[:, k, s * 128:(s + 1) * 128], rhs=wt[:, k, c0:c0 + ncol],
                            start=(k == 0), stop=(k == 7)),
                            reads=[wgrp(c0), hT], writes=[pa] if k == 0 else (), pwrites=() if k == 0 else [pa])
                    sg = stg_b.next() if dt == BF16 else stg_f.next()
                    evac(sg[:, 0:ncol], sg, pa[:, 0:ncol], pa, func, 1.0)
                    S.dma("sp", dbuf.t[t0 + s * 128:t0 + (s + 1) * 128, dc0:dc0 + ncol], sg[:, 0:ncol], reads=[sg],
                          pwrites=[dbuf], key=sg)
        _barrier(S)
        S.stack_pop()


def make_scratch(S, SEQ, kind="Internal"):
    scr = {}
    def mk(name, shape, dt):
        scr[name] = S.dram(name, shape, dt, kind=kind)
    mk("qT", [512, SEQ], BF16)
    mk("kcT", [128, SEQ], BF16)
    mk("vcT", [128, SEQ], BF16)
    mk("ksT", [128, SEQ], BF16)
    mk("kwT", [128, SEQ], BF16)
    mk("vsw", [SEQ, 256], BF16)
    mk("gate", [SEQ, 24], F32)
    mk("nzs", [SEQ, 512], BF16)
    mk("su", [SEQ, 512], F32)
    mk("sv", [SEQ, 512], F32)
    mk("szs", [SEQ, 512], BF16)
    mk("rsT", [1664, SEQ], F32)
    mk("rzT", [512, SEQ], BF16)
    mk("mgT", [3072, SEQ], BF16)
    mk("ysT", [3, 512, SEQ], BF16)
    mk("xtok", [SEQ, 5, 2, 256], F32)
    return scr


def phase_C(S, nc, SEQ, lyr, Wd, scr):
    LN_EPS = 1e-5
    with contextlib.ExitStack() as st:
        S.stack_push(st)
        ident = make_ident(S, "C_ident")
        wraw = S.sb("C_wraw", [128, 8, 128])
        wbf = S.sb("C_wbf", [128, 8, 128], BF16)
        WT = S.sb("C_WT", [128, 8, 128], BF16)
        bsT = S.sb("C_bsT", [128, 8])
        lng = S.sb("C_lng", [128, 512])
        lnb = S.sb("C_lnb", [128, 512])
        pw = S.ps("C_pw", [128, 8, 128], BF16)
        S.dma("sp", wraw[:], Wd["sg_w"].t[lyr].rearrange("g t s -> t g s"), reads=[Wd["sg_w"]], writes=[wraw])
        S.dma("sp", bsT[:], Wd["sg_b"].t[lyr].rearrange("g t -> t g"), reads=[Wd["sg_b"]], writes=[bsT],
              allow_slow_non_contiguous=True)
        S.dma("sp", lng[:], Wd["sg_ln_g"].t[lyr:lyr + 1, :].partition_broadcast(128), reads=[Wd["sg_ln_g"]], writes=[lng])
        S.dma("sp", lnb[:], Wd["sg_ln_b"].t[lyr:lyr + 1, :].partition_broadcast(128), reads=[Wd["sg_ln_b"]], writes=[lnb])
        S.op("pool", lambda e: e.affine_select(out=wraw[:], in_=wraw[:], pattern=[[0, 8], [-1, 128]],
                                               compare_op=ALU.is_ge, fill=0.0, base=0, channel_multiplier=1),
             reads=[wraw], writes=[wraw])
        S.op("dve", lambda e: e.tensor_copy(out=wbf[:], in_=wraw[:]), reads=[wraw], writes=[wbf])
        for g in range(8):
            S.op("pe", lambda e, g=g: e.transpose(out=pw[:, g, :], in_=wbf[:, g, :], identity=ident[:]),
                 reads=[wbf, ident], pwrites=[pw])
        S.op("dve", lambda e: e.tensor_copy(out=WT[:], in_=pw[:]), reads=[pw], writes=[WT])

        NB = 2
        svt = Ring([S.sb("C_sv%d" % i, [128, 512]) for i in range(NB)])
        sut = Ring([S.sb("C_su%d" % i, [128, 512]) for i in range(NB)])
        szt = Ring([S.sb("C_sz%d" % i, [128, 512], BF16) for i in range(NB)])
        stats = S.sb("C_stats", [128, 6])
        mv = S.sb("C_mv", [128, 2])
        rstd = S.sb("C_rstd", [128, 1])
        vn0 = S.sb("C_vnf", [128, 512])
        vn = Ring([S.sb("C_vn%d" % i, [128, 512], BF16) for i in range(2)])
        y0 = S.sb("C_y0", [128, 512])
        yb = Ring([S.sb("C_yb%d" % i, [128, 512], BF16) for i in range(2)])
        pm = Ring([S.ps("C_pm%d" % i, [128, 512]) for i in range(2)])
        pt = Ring([S.ps("C_pt%d" % i, [128, 4, 128], BF16) for i in range(2)])
        stg = Ring([S.sb("C_stg%d" % i, [128, 4, 512], BF16) for i in range(2)])
        ys = scr["ysT"]
        sgb = None
        for c in range(SEQ // 128):
            t0 = c * 128
            v = svt.next(); u = sut.next(); z = szt.next()
            S.dma("sp", v[:], scr["sv"].t[t0:t0 + 128, :], reads=[scr["sv"]], writes=[v])
            S.dma("sp", u[:], scr["su"].t[t0:t0 + 128, :], reads=[scr["su"]], writes=[u])
            S.dma("sp", z[:], scr["szs"].t[t0:t0 + 128, :], reads=[scr["szs"]], writes=[z])
            S.op("dve", lambda e, v=v: e.bn_stats(out=stats[:], in_=v[:]), reads=[v], writes=[stats])
            S.op("dve", lambda e: e.bn_aggr(out=mv[:], in_=stats[:]), reads=[stats], writes=[mv])
            S.op("act", lambda e: e.activation(out=rstd[:], in_=mv[:, 1:2], func=AF.Sqrt, bias=LN_EPS, scale=1.0),
                 reads=[mv], writes=[rstd])
            S.op("dve", lambda e: e.reciprocal(out=rstd[:], in_=rstd[:]), reads=[rstd], writes=[rstd])
            S.op("dve", lambda e, v=v: e.tensor_scalar(out=vn0[:], in0=v[:], scalar1=mv[:, 0:1], scalar2=rstd[:, 0:1],
                                                       op0=ALU.subtract, op1=ALU.mult),
                 reads=[v, mv, rstd], writes=[vn0])
            S.op("pool", lambda e: e.tensor_tensor(out=vn0[:], in0=vn0[:], in1=lng[:], op=ALU.mult),
                 reads=[vn0, lng], writes=[vn0])
            vb = vn.next()
            S.op("pool", lambda e, vb=vb: e.tensor_tensor(out=vb[:], in0=vn0[:], in1=lnb[:], op=ALU.add),
                 reads=[vn0, lnb], writes=[vb])
            pmm = pm.next()
            for g in range(8):
                S.op("pe", lambda e, g=g, vb=vb, pmm=pmm: e.matmul(pmm[:, g * 64:(g + 1) * 64], lhsT=WT[:, g, :],
                                                                   rhs=vb[:, g * 64:(g + 1) * 64], start=True, stop=True),
                     reads=[WT, vb], writes=[pmm] if g == 0 else (), pwrites=() if g == 0 else [pmm])
            S.op("dve", lambda e, pmm=pmm: e.tensor_tensor(
                out=y0[:].rearrange("p (g d) -> p g d", g=8), in0=pmm[:].rearrange("p (g d) -> p g d", g=8),
                in1=bsT[:].unsqueeze(2).to_broadcast([128, 8, 64]), op=ALU.add), reads=[pmm, bsT], writes=[y0])
            S.op("pool", lambda e, u=u: e.tensor_tensor(out=y0[:], in0=y0[:], in1=u[:], op=ALU.mult),
                 reads=[y0, u], writes=[y0])
            y = yb.next()
            S.op("dve", lambda e, y=y, z=z: e.tensor_tensor(out=y[:], in0=y0[:], in1=z[:], op=ALU.mult),
                 reads=[y0, z], writes=[y])
            ptt = pt.next()
            for k in range(4):
                S.op("pe", lambda e, k=k, y=y, ptt=ptt: e.transpose(out=ptt[:, k, :], in_=y[:, k * 128:(k + 1) * 128],
                                                                    identity=ident[:]),
                     reads=[y, ident], writes=[ptt] if k == 0 else (), pwrites=() if k == 0 else [ptt])
            if c % 4 == 0:
                sgb = stg.next()
            cc = c % 4
            S.op("act", lambda e, ptt=ptt, sgb=sgb, cc=cc: e.copy(out=sgb[:, :, cc * 128:(cc + 1) * 128], in_=ptt[:]),
                 reads=[ptt], writes=[sgb] if cc == 0 else (), pwrites=() if cc == 0 else [sgb])
            if cc == 3 or c == SEQ // 128 - 1:
                tb = (c // 4) * 512
                n = (cc + 1) * 128
                S.dma("sp", ys.t[1, :, tb:tb + n].rearrange("(k p) t -> p k t", p=128), sgb[:, :, 0:n], reads=[sgb],
                      pwrites=[ys], key=sgb)
        _barrier(S)
        S.stack_pop()


def phase_E(S, nc, SEQ, lyr, x_src, x_dst, Wd, scr, final):
    TT = 512
    nsub = 4
    with contextlib.ExitStack() as st:
        S.stack_push(st)
        ident = make_ident(S, "E_ident")
        wb = S.sb("E_wb", [128, 3, 4, D], BF16)
        wo = S.sb("E_wo", [128, 8, D], BF16)
        wpg = S.sb("E_wpg", [128, 8, D], BF16)
        wpp = S.sb("E_wpp", [128, 2, D], BF16)
        gpl = S.sb("E_gpl", [128, D])
        gfin = S.sb("E_gfin", [128, D])
        for n in range(3):
            S.dma("pool", wb[:, n, :, :], Wd["w_branch"].t[lyr, n].rearrange("(k p) d -> p k d", p=128),
                  reads=[Wd["w_branch"]], pwrites=[wb], key=wb, max_dma_last_dim=4096)
        for k0 in range(0, 8, 4):
            S.dma("pool", wo[:, k0:k0 + 4, :], Wd["w_o"].t[lyr, k0 * 128:(k0 + 4) * 128, :].rearrange("(k p) d -> p k d", p=128),
                  reads=[Wd["w_o"]], pwrites=[wo], key=wo, max_dma_last_dim=4096)
            S.dma("pool", wpg[:, k0:k0 + 4, :], Wd["w_ple_gate"].t[lyr, k0 * 128:(k0 + 4) * 128, :].rearrange("(k p) d -> p k d", p=128),
                  reads=[Wd["w_ple_gate"]], pwrites=[wpg], key=wpg, max_dma_last_dim=4096)
        S.dma("pool", wpp[:], Wd["w_ple_proj"].t[lyr].rearrange("(k p) d -> p k d", p=128),
              reads=[Wd["w_ple_proj"]], pwrites=[wpp], key=wpp, max_dma_last_dim=4096)
        S.dma("sp", gpl[:], Wd["ple_norm_g"].t[lyr:lyr + 1, :].partition_broadcast(128), reads=[Wd["ple_norm_g"]], writes=[gpl])
        if final:
            S.dma("sp", gfin[:], Wd["final_norm_g"].t[0:1, :].partition_broadcast(128), reads=[Wd["final_norm_g"]], writes=[gfin])

        yst = S.sb("E_ys", [128, 3, 4, TT], BF16)
        mgt = S.sb("E_mg", [128, 24, TT], BF16)
        mrg = S.sb("E_mrg", [128, 8, TT])
        mrb = S.sb("E_mrb", [128, 8, TT], BF16)
        tmp = Ring([S.sb("E_tmp%d" % i, [128, TT]) for i in range(2)])
        xt = S.sb("E_x", [128, nsub, D])
        pin = S.sb("E_p", [128, nsub, PLE])
        pbf = S.sb("E_pbf", [128, PLE], BF16)
        pTs = S.sb("E_pT", [128, 2, 128], BF16)
        sq = S.sb("E_sq", [128, D], BF16)
        ss = S.sb("E_ss", [128, 1])
        rs = S.sb("E_rs", [128, 1])
        hp = S.sb("E_hp", [128, D], BF16)
        hpT = S.sb("E_hpT", [128, 8, 128], BF16)
        gate = S.sb("E_gate", [128, D])
        xo = Ring([S.sb("E_xo%d" % i, [128, D]) for i in range(2)])
        pz = Ring([S.ps("E_pz%d" % i, [128, TT]) for i in range(3)])
        po = Ring([S.ps("E_po%d" % i, [128, 512]) for i in range(2)])
        pg = Ring([S.ps("E_pg%d" % i, [128, 512]) for i in range(2)])
        ptr = S.ps("E_ptr", [128, 8, 128], BF16)

        def load_ym(ti):
            t0 = ti * TT
            for n in range(3):
                S.dma("sp", yst[:, n, :, :], scr["ysT"].t[n, :, t0:t0 + TT].rearrange("(k p) t -> p k t", p=128),
                      reads=[scr["ysT"]], writes=[yst] if n == 0 else (), pwrites=() if n == 0 else [yst], key=yst)
            for k0 in range(0, 24, 8):
                S.dma("sp", mgt[:, k0:k0 + 8, :], scr["mgT"].t[k0 * 128:(k0 + 8) * 128, t0:t0 + TT].rearrange("(k p) t -> p k t", p=128),
                      reads=[scr["mgT"]], writes=[mgt] if k0 == 0 else (), pwrites=() if k0 == 0 else [mgt], key=mgt)
        load_ym(0)
        for ti in range(SEQ // TT):
            t0 = ti * TT
            S.dma("sp", xt[:], x_src.t[t0:t0 + TT, :].rearrange("(s p) d -> p s d", p=128), reads=[x_src], writes=[xt])
            S.dma("sp", pin[:], Wd["p"].t[lyr, t0:t0 + TT, :].rearrange("(s p) d -> p s d", p=128), reads=[Wd["p"]], writes=[pin])
            for dc in range(8):
                pzs = []
                for n in range(3):
                    pzz = pz.next()
                    pzs.append(pzz)
                    for k in range(4):
                        S.op("pe", lambda e, n=n, k=k, dc=dc, pzz=pzz: e.matmul(
                            pzz[:], lhsT=wb[:, n, k, dc * 128:(dc + 1) * 128], rhs=yst[:, n, k, :], start=(k == 0), stop=(k == 3)),
                            reads=[wb, yst], writes=[pzz] if k == 0 else (), pwrites=() if k == 0 else [pzz])
                S.op("dve", lambda e, dc=dc, p0=pzs[0]: e.tensor_tensor(out=mrg[:, dc, :], in0=p0[:], in1=mgt[:, dc, :], op=ALU.mult),
                     reads=[pzs[0], mgt], pwrites=[mrg])
                t1 = tmp.next()
                S.op("dve", lambda e, dc=dc, p1=pzs[1], t1=t1: e.tensor_tensor(out=t1[:], in0=p1[:], in1=mgt[:, 8 + dc, :], op=ALU.mult),
                     reads=[pzs[1], mgt], writes=[t1])
                t2 = tmp.next()
                S.op("dve", lambda e, dc=dc, p2=pzs[2], t2=t2: e.tensor_tensor(out=t2[:], in0=p2[:], in1=mgt[:, 16 + dc, :], op=ALU.mult),
                     reads=[pzs[2], mgt], writes=[t2])
                S.op("pool", lambda e, dc=dc, t1=t1: e.tensor_tensor(out=mrg[:, dc, :], in0=mrg[:, dc, :], in1=t1[:], op=ALU.add),
                     reads=[mrg, t1], pwrites=[mrg])
                S.op("pool", lambda e, dc=dc, t2=t2: e.tensor_tensor(out=mrb[:, dc, :], in0=mrg[:, dc, :], in1=t2[:], op=ALU.add),
                     reads=[mrg, t2], pwrites=[mrb])
            if ti + 1 < SEQ // TT:
                load_ym(ti + 1)
            for s in range(nsub):
                for blk in range(2):
                    pp = po.next()
                    for k in range(8):
                        S.op("pe", lambda e, k=k, s=s, blk=blk, pp=pp: e.matmul(
                            pp[:], lhsT=mrb[:, k, s * 128:(s + 1) * 128], rhs=wo[:, k, blk * 512:(blk + 1) * 512],
                            start=(k == 0), stop=(k == 7)),
                            reads=[mrb, wo], writes=[pp] if k == 0 else (), pwrites=() if k == 0 else [pp])
                    S.op("dve", lambda e, s=s, blk=blk, pp=pp: e.tensor_tensor(
                        out=xt[:, s, blk * 512:(blk + 1) * 512], in0=pp[:], in1=xt[:, s, blk * 512:(blk + 1) * 512], op=ALU.add),
                        reads=[pp, xt], pwrites=[xt])
                rmsnorm_tile(S, xt[:, s, :], xt, gpl, hp[:], hp, sq, ss, rs)
                for k in range(8):
                    S.op("pe", lambda e, k=k: e.transpose(out=ptr[:, k, :], in_=hp[:, k * 128:(k + 1) * 128], identity=ident[:]),
                         reads=[hp, ident], writes=[ptr] if k == 0 else (), pwrites=() if k == 0 else [ptr])
                S.op("act", lambda e: e.copy(out=hpT[:], in_=ptr[:]), reads=[ptr], writes=[hpT])
                S.op("pool", lambda e, s=s: e.tensor_copy(out=pbf[:], in_=pin[:, s, :]), reads=[pin], writes=[pbf])
                for k in range(2):
                    S.op("pe", lambda e, k=k: e.transpose(out=ptr[:, k, :], in_=pbf[:, k * 128:(k + 1) * 128], identity=ident[:]),
                         reads=[pbf, ident, hpT], writes=[ptr] if k == 0 else (), pwrites=() if k == 0 else [ptr])
                S.op("act", lambda e: e.copy(out=pTs[:], in_=ptr[:, 0:2, :]), reads=[ptr], writes=[pTs])
                xout = xo.next()
                for blk in range(2):
                    pgg = pg.next()
                    for k in range(8):
                        S.op("pe", lambda e, k=k, blk=blk, pgg=pgg: e.matmul(
                            pgg[:], lhsT=hpT[:, k, :], rhs=wpg[:, k, blk * 512:(blk + 1) * 512], start=(k == 0), stop=(k == 7)),
                            reads=[hpT, wpg], writes=[pgg] if k == 0 else (), pwrites=() if k == 0 else [pgg])
                    S.op("act", lambda e, blk=blk, pgg=pgg: e.activation(out=gate[:, blk * 512:(blk + 1) * 512], in_=pgg[:], func=AF.Sigmoid),
                         reads=[pgg], pwrites=[gate])
                    ppp = pg.next()
                    for k in range(2):
                        S.op("pe", lambda e, k=k, blk=blk, ppp=ppp: e.matmul(
                            ppp[:], lhsT=pTs[:, k, :], rhs=wpp[:, k, blk * 512:(blk + 1) * 512], start=(k == 0), stop=(k == 1)),
                            reads=[pTs, wpp], writes=[ppp] if k == 0 else (), pwrites=() if k == 0 else [ppp])
                    S.op("dve", lambda e, blk=blk, ppp=ppp: e.tensor_tensor(
                        out=gate[:, blk * 512:(blk + 1) * 512], in0=ppp[:], in1=gate[:, blk * 512:(blk + 1) * 512], op=ALU.mult),
                        reads=[ppp, gate], pwrites=[gate])
                    S.op("pool", lambda e, blk=blk, s=s, xout=xout: e.tensor_tensor(
                        out=xout[:, blk * 512:(blk + 1) * 512], in0=gate[:, blk * 512:(blk + 1) * 512],
                        in1=xt[:, s, blk * 512:(blk + 1) * 512], op=ALU.add),
                        reads=[gate, xt], writes=[xout] if blk == 0 else (), pwrites=() if blk == 0 else [xout])
                if final:
                    S.op("act", lambda e, xout=xout: e.activation(out=sq[:], in_=xout[:], func=AF.Square, accum_out=ss[:]),
                         reads=[xout], writes=[sq, ss])
                    S.op("act", lambda e: e.activation(out=rs[:], in_=ss[:], func=AF.Sqrt, scale=1.0 / D, bias=EPS),
                         reads=[ss], writes=[rs])
                    S.op("dve", lambda e: e.reciprocal(out=rs[:], in_=rs[:]), reads=[rs], writes=[rs])
                    S.op("dve", lambda e, xout=xout: e.scalar_tensor_tensor(out=xout[:], in0=xout[:], scalar=rs[:, 0:1], in1=gfin[:],
                                                                            op0=ALU.mult, op1=ALU.mult),
                         reads=[xout, rs, gfin], writes=[xout])
                S.dma("sp", x_dst.t[t0 + s * 128:t0 + (s + 1) * 128, :], xout[:], reads=[xout], pwrites=[x_dst], key=xout)
        _barrier(S)
        S.stack_pop()


def phase_D(S, nc, SEQ, lyr, Wd, scr):
    TP = 128
    C = 16
    NCH = TP // C
    GN_EPS = 64e-5
    LD = 0.6065306597126334
    with contextlib.ExitStack() as st:
        S.stack_push(st)
        identB = make_ident(S, "D_identB", BF16)
        ones = S.sb("D_ones", [128, 128])
        S.op("pool", lambda e: e.memset(ones[:], 0.0), writes=[ones])
        S.op("pool", lambda e: e.memset(ones[0:64, 0:64], 1.0), reads=[ones], writes=[ones])
        S.op("pool", lambda e: e.memset(ones[64:128, 64:128], 1.0), reads=[ones], writes=[ones])
        Ff = S.sb("D_F", [128, 64], BF16)
        S.op("pool", lambda e: e.tensor_tensor(out=Ff[:], in0=identB[:, 0:64], in1=identB[:, 64:128], op=ALU.add), reads=[identB], writes=[Ff])
        Sel = S.sb("D_Sel", [128, 16], BF16)
        S.op("pool", lambda e: e.tensor_tensor(out=Sel[:], in0=identB[:, 0:16], in1=identB[:, 16:32], op=ALU.add), reads=[identB], writes=[Sel])
        for hh in range(2, 8):
            S.op("pool", lambda e, hh=hh: e.tensor_tensor(out=Sel[:], in0=Sel[:], in1=identB[:, hh * 16:(hh + 1) * 16], op=ALU.add),
                 reads=[identB, Sel], writes=[Sel])
        maskF = S.sb("D_maskF", [128, 4, 8], BF16)
        S.op("pool", lambda e: e.memset(maskF[:], 0.0), writes=[maskF])
        for p in range(4):
            for h2 in range(2):
                S.op("pool", lambda e, p=p, h2=h2: e.memset(maskF[h2 * 64:(h2 + 1) * 64, p, 2 * p + h2:2 * p + h2 + 1], 1.0), reads=[maskF], writes=[maskF])
        maskZ = S.sb("D_maskZ", [128, 4, 2], BF16)
        S.op("pool", lambda e: e.memset(maskZ[:], 1.0), writes=[maskZ])
        S.op("pool", lambda e: e.affine_select(out=maskZ[:], in_=maskZ[:], pattern=[[-32, 4], [-16, 2]], compare_op=ALU.is_ge, fill=0.0,
                                               base=0, channel_multiplier=1), reads=[maskZ], writes=[maskZ])
        S.op("pool", lambda e: e.affine_select(out=maskZ[:], in_=maskZ[:], pattern=[[32, 4], [16, 2]], compare_op=ALU.is_ge, fill=0.0,
                                               base=15, channel_multiplier=-1), reads=[maskZ], writes=[maskZ])

        def trimask(name, pat, cm, op):
            m = S.sb(name, [128, 128], BF16)
            S.op("pool", lambda e: e.memset(m[:], 1.0), writes=[m])
            S.op("pool", lambda e: e.affine_select(out=m[:], in_=m[:], pattern=pat, compare_op=op, fill=0.0, base=0, channel_multiplier=cm),
                 reads=[m], writes=[m])
            return m
        mSL = trimask("D_mSL", [[-16, 8], [-1, 16]], 1, ALU.is_gt)
        mSU = trimask("D_mSU", [[16, 8], [1, 16]], -1, ALU.is_gt)
        mUI = trimask("D_mUI", [[16, 8], [1, 16]], -1, ALU.is_ge)
        rm = S.sb("D_rm", [128, 512])
        S.op("pool", lambda e: e.memset(rm[:], 1.0), writes=[rm])
        S.op("pool", lambda e: e.memset(rm[:, 0:512:16], 0.0), reads=[rm], writes=[rm])

        def cvec(name, key, n):
            t = S.sb("D_" + name, [128, n])
            S.dma("sp", t[:], Wd[key].t[lyr].rearrange("(c p) -> p c", p=128), reads=[Wd[key]], writes=[t],
                  allow_slow_non_contiguous=True)
            return t

        def cvec2(name, key):
            t = S.sb("D_" + name, [128, 4])
            S.dma("sp", t[:], Wd[key].t[lyr].rearrange("(c a) j -> (a j) c", a=2), reads=[Wd[key]], writes=[t],
                  allow_slow_non_contiguous=True)
            return t
        mu = cvec("mu", "rk_mu", 13)
        w0 = cvec("w0", "rk_w0", 4)
        a0 = cvec("a0", "rk_a0", 4)
        lg = cvec("lg", "rk_lnx_g", 4)
        lb = cvec("lb", "rk_lnx_b", 4)
        kkc = cvec2("kkc", "rk_kk")
        ka = cvec2("ka", "rk_ka")
        rkc = cvec2("rkc", "rk_rk")
        omka = S.sb("D_omka", [128, 4])
        S.op("pool", lambda e: e.tensor_scalar(out=omka[:], in0=ka[:], scalar1=-1.0, scalar2=1.0, op0=ALU.mult, op1=ALU.add),
             reads=[ka], writes=[omka])
        w2 = S.sb("D_w2", [64, 512], BF16)
        a2 = S.sb("D_a2", [128, 512], BF16)
        S.dma("pool", w2[:], Wd["rk_w2"].t[lyr], reads=[Wd["rk_w2"]], writes=[w2])
        S.dma("pool", a2[64:128, :], Wd["rk_a2"].t[lyr], reads=[Wd["rk_a2"]], writes=[a2])

        Hm = S.sb("D_H", [128, 4, 64])
        Hn = S.sb("D_Hn", [128, 4, 64])
        Hbf = S.sb("D_Hbf", [128, 4, 64], BF16)
        S.op("pool", lambda e: e.memset(Hm[:], 0.0), writes=[Hm])
        S.op("pool", lambda e: e.memset(Hbf[:], 0.0), writes=[Hbf])

        rst = S.sb("D_rst", [128, 13, TP + 1])
        xs = S.sb("D_xs", [128, 13, TP])
        th = S.sb("D_th", [128, TP], BF16)
        sg = S.sb("D_sg", [128, 4, TP])
        cum = S.sb("D_cum", [128, 4, TP])
        E1 = S.sb("D_E1", [128, 4, TP])
        E2 = S.sb("D_E2", [128, 4, TP])
        E3 = S.sb("D_E3", [128, 4, TP])
        aa = S.sb("D_aa", [128, 4, TP])
        kkf = S.sb("D_kkf", [128, 4, TP])
        sq = S.sb("D_sq", [128, 4, TP])
        rn = S.sb("D_rn", [128, 4, TP])
        kp = S.sb("D_kp", [128, 4, TP])
        t1 = S.sb("D_t1", [128, 4, TP])
        t2 = S.sb("D_t2", [128, 4, TP])
        comp = [S.sb("D_cmp%d" % i, [128, 4, TP], BF16) for i in range(5)]
        ZXr = Ring([[S.sb("D_Z%d_%d" % (b, i), [128, NCH, 4, 128], BF16) for i in range(5)] for b in range(2)])
        DcR = Ring([S.sb("D_Dc%d" % i, [128, NCH, 4]) for i in range(2)])
        bonR = Ring([S.sb("D_bon%d" % i, [128, 4, TP]) for i in range(2)])
        rzR = Ring([S.sb("D_rz%d" % i, [128, 4, TP], BF16) for i in range(2)])
        ybR = Ring([S.sb("D_yb%d" % i, [128, 4, TP]) for i in range(2)])
        yo = S.sb("D_yo", [128, 4, TP], BF16)
        ppre = S.ps("D_ppre", [128, 4, TP])
        R4 = lambda nm, shp, dt=BF16: Ring([S.sb("D_%s%d" % (nm, i), shp, dt) for i in range(4)])
        WyZr = R4("WyZ", [128, 4, 128]); WhTr = R4("WhT", [128, 4, 128]); BtZr = R4("BtZ", [128, 4, 128]); KtZr = R4("KtZ", [128, 4, 128])
        U0r = R4("U0", [128, 64]); Vtr = R4("Vt", [128, 64]); PTr = R4("PT", [128, 128]); QTr = R4("QT", [128, 128])
        ysbR = Ring([S.sb("D_ysb%d" % i, [128, 64]) for i in range(5)])

        class Reg:
            def __init__(self, bank, ap):
                self.bank = bank
                self.t = ap

        class Lane:
            pass
        lanes = []
        for li in range(2):
            L = Lane()
            L.Gr = Ring([S.sb("D_G%d_%d" % (li, i), [128, 128], BF16) for i in range(2)])
            L.Nr = Ring([S.sb("D_N%d_%d" % (li, i), [128, 128], BF16) for i in range(2)])
            L.NTr = Ring([S.sb("D_NT%d_%d" % (li, i), [128, 128], BF16) for i in range(2)])
            L.MTs = S.sb("D_MTs%d" % li, [128, 128], BF16)
            L.X1Z = S.sb("D_X1Z%d" % li, [128, 4, 128], BF16)
            L.X1s = S.sb("D_X1s%d" % li, [128, 64], BF16)
            L.tks = S.sb("D_tks%d" % li, [128, 2, 64], BF16)
            ba = S.ps("D_ba%d" % li, [128, 512])
            bb = ppre if li == 0 else S.ps("D_bb%d" % li, [128, 4, 128])
            bg = S.ps("D_bg%d" % li, [128, 3, 128])
            L.tokc = Reg(ba, ba.t[:, 0:256].rearrange("q (o j) -> q o j", o=4))
            L.QTp = Reg(ba, ba.t[:, 256:384])
            L.mvp = Reg(ba, ba.t[:, 384:448])
            L.sc = [Reg(bb, bb.t[:, i, :]) for i in range(4)]
            L.bb = bb
            L.bg = bg
            lanes.append(L)
        bs = S.ps("D_bs", [128, 512])
        bt_ = S.ps("D_bt", [128, 512])
        WHp = Reg(bs, bs.t[:, 0:256].rearrange("q (p i) -> q p i", p=4))
        Yp = Reg(bs, bs.t[:, 256:320])
        yfp = Reg(bt_, bt_.t[:, 0:64].rearrange("q (p t) -> q p t", p=4))
        yn = S.sb("D_yn", [128, 64])
        YZ = S.sb("D_YZ", [128, 4, 128], BF16)
        stats = S.sb("D_stats", [128, 6])
        mv = S.sb("D_mv", [128, 2])
        rstd = S.sb("D_rstd", [128, 1])

        bc4 = lambda t: t[:].unsqueeze(2).to_broadcast([128, 4, TP])

        def mm(out_ap, obuf, lhsT, lbuf, rhs, rbuf, start, stop=True, first_write=False):
            obuf = getattr(obuf, "bank", obuf)
            S.op("pe", lambda e: e.matmul(out_ap, lhsT=lhsT, rhs=rhs, start=start, stop=stop, skip_group_check=True),
                 reads=[lbuf, rbuf], writes=[obuf] if first_write else (), pwrites=() if first_write else [obuf])

        def prep(nb):
            t0 = nb * TP
            S.dma("sp", rst[:, :, 1:TP + 1], scr["rsT"].t[:, t0:t0 + TP].rearrange("(c p) t -> p c t", p=128), reads=[scr["rsT"]], writes=[rst])
            if nb == 0:
                S.op("pool", lambda e: e.memset(rst[:, :, 0:1], 0.0), reads=[rst], pwrites=[rst])
            else:
                S.dma("sp", rst[:, :, 0:1], scr["rsT"].t[:, t0 - 1:t0].rearrange("(c p) t -> p c t", p=128), reads=[scr["rsT"]],
                      pwrites=[rst], key=rst, allow_slow_non_contiguous=True)
            S.op("pool", lambda e: e.tensor_tensor(out=xs[:], in0=rst[:, :, 0:TP], in1=rst[:, :, 1:TP + 1], op=ALU.subtract), reads=[rst], writes=[xs])
            S.op("pool", lambda e: e.tensor_tensor(out=xs[:], in0=xs[:], in1=mu[:].unsqueeze(2).to_broadcast([128, 13, TP]), op=ALU.mult),
                 reads=[xs, mu], writes=[xs])
            S.op("pool", lambda e: e.tensor_tensor(out=xs[:], in0=xs[:], in1=rst[:, :, 1:TP + 1], op=ALU.add), reads=[xs, rst], writes=[xs])
            r = xs[:, 0:4, :]; k = xs[:, 4:8, :]; v = xs[:, 8:12, :]
            S.op("act", lambda e: e.activation(out=th[0:64, :], in_=xs[0:64, 12, :], func=AF.Tanh), reads=[xs], pwrites=[th])
            S.op("act", lambda e: e.copy(out=th[64:128, :], in_=xs[64:128, 12, :]), reads=[xs], pwrites=[th])
            for p in range(4):
                mm(ppre[:, p, :], ppre, w2[0:64, p * 128:(p + 1) * 128], w2, th[0:64, :], th, True, first_write=(p == 0))
            for p in range(4):
                S.op("act", lambda e, p=p: e.activation(out=sg[:, p, :], in_=ppre[:, p, :], func=AF.Sigmoid, bias=w0[:, p:p + 1]),
                     reads=[w0], writes=[ppre], pwrites=[sg])
            for p in range(4):
                mm(ppre[:, p, :], ppre, a2[64:128, p * 128:(p + 1) * 128], a2, th[64:128, :], th, True, first_write=(p == 0))
            for p in range(4):
                S.op("act", lambda e, p=p: e.activation(out=aa[:, p, :], in_=ppre[:, p, :], func=AF.Sigmoid, bias=a0[:, p:p + 1]),
                     reads=[a0], writes=[ppre], pwrites=[aa])
            S.op("dve", lambda e: e.tensor_tensor_scan(out=cum[:].rearrange("q p t -> q (p t)"), data0=rm[:],
                                                       data1=sg[:].rearrange("q p t -> q (p t)"), initial=0.0, op0=ALU.mult, op1=ALU.add),
                 reads=[rm, sg], writes=[cum])
            S.op("act", lambda e: e.activation(out=E1[:], in_=cum[:], func=AF.Exp, scale=-LD), reads=[cum], writes=[E1])
            S.op("act", lambda e: e.activation(out=E2[:], in_=cum[:], func=AF.Exp, scale=LD), reads=[cum], writes=[E2])
            S.op("pool", lambda e: e.tensor_tensor(out=t2[:], in0=cum[:], in1=sg[:], op=ALU.subtract), reads=[cum, sg], writes=[t2])
            S.op("act", lambda e: e.activation(out=E3[:], in_=t2[:], func=AF.Exp, scale=-LD), reads=[t2], writes=[E3])
            Dc = DcR.next()
            S.op("pool", lambda e, Dc=Dc: e.tensor_copy(out=Dc[:].rearrange("q c p -> q p c"), in_=E1[:, :, 15:TP:16]), reads=[E1], writes=[Dc])
            S.op("pool", lambda e: e.tensor_tensor(out=kkf[:], in0=k, in1=bc4(kkc), op=ALU.mult), reads=[xs, kkc], writes=[kkf])
            S.op("pool", lambda e: e.tensor_tensor(out=sq[:], in0=kkf[:], in1=kkf[:], op=ALU.mult), reads=[kkf], writes=[sq])
            for p in range(4):
                mm(ppre[:, p, :], ppre, ones[:], ones, sq[:, p, :], sq, True, first_write=(p == 0))
            S.op("act", lambda e: e.activation(out=rn[:], in_=ppre[:], func=AF.Sqrt), writes=[rn, ppre])
            S.op("dve", lambda e: e.tensor_scalar(out=rn[:], in0=rn[:], scalar1=1e-12, scalar2=None, op0=ALU.max), reads=[rn], writes=[rn])
            S.op("dve", lambda e: e.reciprocal(out=rn[:], in_=rn[:]), reads=[rn], writes=[rn])
            S.op("pool", lambda e: e.tensor_tensor(out=kkf[:], in0=kkf[:], in1=rn[:], op=ALU.mult), reads=[kkf, rn], writes=[kkf])
            S.op("pool", lambda e: e.tensor_tensor(out=t1[:], in0=aa[:], in1=bc4(ka), op=ALU.mult), reads=[aa, ka], writes=[t1])
            S.op("pool", lambda e: e.tensor_tensor(out=t1[:], in0=t1[:], in1=bc4(omka), op=ALU.add), reads=[t1, omka], writes=[t1])
            S.op("pool", lambda e: e.tensor_tensor(out=kp[:], in0=k, in1=t1[:], op=ALU.mult), reads=[xs, t1], writes=[kp])
            At, Bt, Kt, Rt, Vb = comp
            S.op("pool", lambda e: e.scalar_tensor_tensor(out=At[:], in0=kkf[:], scalar=-1.0, in1=E3[:], op0=ALU.mult, op1=ALU.mult)
                 if False else e.tensor_tensor(out=t2[:], in0=kkf[:], in1=E3[:], op=ALU.mult), reads=[kkf, E3], writes=[t2])
            S.op("dve", lambda e: e.tensor_scalar(out=At[:], in0=t2[:], scalar1=-1.0, scalar2=None, op0=ALU.mult), reads=[t2], writes=[At])
            S.op("pool", lambda e: e.tensor_tensor(out=t2[:], in0=kkf[:], in1=aa[:], op=ALU.mult), reads=[kkf, aa], writes=[t2])
            S.op("pool", lambda e: e.tensor_tensor(out=Bt[:], in0=t2[:], in1=E2[:], op=ALU.mult), reads=[t2, E2], writes=[Bt])
            S.op("pool", lambda e: e.tensor_tensor(out=Kt[:], in0=kp[:], in1=E2[:], op=ALU.mult), reads=[kp, E2], writes=[Kt])
            S.op("pool", lambda e: e.tensor_tensor(out=Rt[:], in0=r, in1=E1[:], op=ALU.mult), reads=[xs, E1], writes=[Rt])
            S.op("pool", lambda e: e.tensor_copy(out=Vb[:], in_=v), reads=[xs], writes=[Vb])
            S.op("pool", lambda e: e.tensor_tensor(out=t1[:], in0=r, in1=kp[:], op=ALU.mult), reads=[xs, kp], writes=[t1])
            S.op("pool", lambda e: e.tensor_tensor(out=sq[:], in0=t1[:], in1=bc4(rkc), op=ALU.mult), reads=[t1, rkc], writes=[sq])
            for p in range(4):
                mm(ppre[:, p, :], ppre, ones[:], ones, sq[:, p, :], sq, True, first_write=(p == 0))
            bon = bonR.next()
            S.op("act", lambda e, bon=bon: e.copy(out=bon[:], in_=ppre[:]), writes=[bon, ppre])
            S.op("pool", lambda e, bon=bon: e.tensor_tensor(out=bon[:], in0=bon[:], in1=v, op=ALU.mult), reads=[bon, xs], writes=[bon])
            rzt = rzR.next()
            S.dma("sp", rzt[:], scr["rzT"].t[:, t0:t0 + TP].rearrange("(c p) t -> p c t", p=128), reads=[scr["rzT"]], writes=[rzt])
            ZX = ZXr.next()
            for oi in range(5):
                for p in range(4):
                    S.op("dve" if (oi * 4 + p) % 2 == 0 else "pool", lambda e, oi=oi, p=p, ZX=ZX: e.tensor_tensor(
                        out=ZX[oi][:, :, p, :].rearrange("q c (h t) -> q c h t", t=16),
                        in0=comp[oi][:, p, :].rearrange("q (c t) -> q c t", t=16).unsqueeze(2).to_broadcast([128, NCH, 8, 16]),
                        in1=maskF[:, p, :].unsqueeze(1).unsqueeze(3).to_broadcast([128, NCH, 8, 16]), op=ALU.mult),
                        reads=[comp[oi], maskF], writes=[ZX[oi]] if p == 0 else (), pwrites=() if p == 0 else [ZX[oi]])
            return dict(ZX=ZX, Dc=Dc, bon=bon, rzt=rzt, yb=ybR.next(), t0=t0)


        def pre(bt, c, L, pc):
            ZA, ZB, ZK, ZR, ZV = bt["ZX"]
            BtZ = BtZr.next(); KtZ = KtZr.next(); U0 = U0r.next(); Vt = Vtr.next(); PTs = PTr.next(); QTs = QTr.next()
            WyZ = WyZr.next(); WhT = WhTr.next()
            pc.update(BtZ=BtZ, KtZ=KtZ, U0=U0, Vt=Vt, PTs=PTs, QTs=QTs, WyZ=WyZ, WhT=WhT, c=c, bt=bt)
            tokc = L.tokc
            first = True
            for oi, Z in enumerate((ZA, ZB, ZK, ZV)):
                for p in range(4):
                    mm(tokc.t[:, oi, :], tokc, Z[:, c, p, :], Z, Ff[:], Ff, first, first_write=first)
                    first = False
            N1 = L.Nr.next(); NT1 = L.NTr.next()
            specs = ((L.sc[0], ZA, ZB, mSL, N1), (L.sc[1], ZB, ZA, mSU, NT1), (L.sc[2], ZK, ZA, mSU, L.MTs), (L.sc[3], ZB, ZR, mUI, PTs))
            for gi, (pb, Lh, R_, msk, dst) in enumerate(specs):
                for p in range(4):
                    mm(pb.t, pb, Lh[:, c, p, :], Lh, R_[:, c, p, :], R_, p == 0, first_write=(gi == 0 and p == 0))
            for p in range(4):
                mm(L.QTp.t, L.QTp, ZK[:, c, p, :], ZK, ZR[:, c, p, :], ZR, False, first_write=False)
            yield
            G0 = L.Gr.next()
            tks = L.tks
            S.op("act", lambda e: e.copy(out=G0[:, 0:64], in_=tokc.t[:, 0, :]), writes=[G0, tokc.bank])
            mz = maskZ[:].unsqueeze(3).to_broadcast([128, 4, 2, 64])
            S.op("dve", lambda e: e.tensor_copy(out=tks[:], in_=tokc.t[:, 1:3, :]), writes=[tks, tokc.bank])
            S.op("act", lambda e: e.copy(out=Vt[:], in_=tokc.t[:, 3, :]), writes=[Vt, tokc.bank])
            S.op("pool", lambda e: e.tensor_tensor(out=BtZ[:].rearrange("q p (a j) -> q p a j", a=2),
                                                   in0=tks[:, 0, :].unsqueeze(1).unsqueeze(1).to_broadcast([128, 4, 2, 64]), in1=mz, op=ALU.mult),
                 reads=[tks, maskZ], writes=[BtZ])
            S.op("pool", lambda e: e.tensor_tensor(out=KtZ[:].rearrange("q p (a j) -> q p a j", a=2),
                                                   in0=tks[:, 1, :].unsqueeze(1).unsqueeze(1).to_broadcast([128, 4, 2, 64]), in1=mz, op=ALU.mult),
                 reads=[tks, maskZ], writes=[KtZ])
            for gi, (pb, Lh, R_, msk, dst) in enumerate(specs):
                S.op("dve", lambda e, pb=pb, msk=msk, dst=dst: e.tensor_tensor(out=dst[:], in0=pb.t, in1=msk[:], op=ALU.mult),
                     reads=[msk], writes=[dst, pb.bank])
            S.op("dve", lambda e: e.tensor_tensor(out=QTs[:], in0=L.QTp.t, in1=mUI[:], op=ALU.mult), reads=[mUI], writes=[QTs, L.QTp.bank])
            yield
            mm(L.mvp.t, L.mvp, L.MTs[:], L.MTs, Vt[:], Vt, True, first_write=True)
            yield
            S.op("act", lambda e: e.copy(out=G0[:, 64:128], in_=L.mvp.t), writes=[L.mvp.bank], pwrites=[G0])
            yield
            G = G0; Nk = N1; NTk = NT1
            gb = L.bg
            for lev in range(4):
                mm(gb[:, 0, :], gb, identB[:], identB, G[:], G, True, stop=False, first_write=True)
                mm(gb[:, 0, :], gb, NTk[:], NTk, G[:], G, False)
                if lev < 3:
                    mm(gb[:, 1, :], gb, NTk[:], NTk, Nk[:], Nk, True)
                    mm(gb[:, 2, :], gb, Nk[:], Nk, NTk[:], NTk, True)
                    yield
                    G2 = L.Gr.next(); N2 = L.Nr.next(); NT2 = L.NTr.next()
                    S.op("act", lambda e, G2=G2: e.copy(out=G2[:], in_=gb[:, 0, :]), writes=[G2, gb])
                    S.op("act", lambda e, N2=N2: e.copy(out=N2[:], in_=gb[:, 1, :]), writes=[N2, gb])
                    S.op("act", lambda e, NT2=NT2: e.copy(out=NT2[:], in_=gb[:, 2, :]), writes=[NT2, gb])
                    G = G2; Nk = N2; NTk = NT2
                    yield
                else:
                    yield
                    S.op("act", lambda e: e.copy(out=L.X1s[:], in_=gb[:, 0, 0:64]), writes=[L.X1s, gb])
                    S.op("act", lambda e: e.copy(out=U0[:], in_=gb[:, 0, 64:128]), writes=[U0, gb])
                    S.op("dve", lambda e: e.tensor_tensor(out=L.X1Z[:].rearrange("q p (a j) -> q p a j", a=2),
                                                           in0=L.X1s[:].unsqueeze(1).unsqueeze(1).to_broadcast([128, 4, 2, 64]), in1=mz, op=ALU.mult),
                         reads=[L.X1s, maskZ], writes=[L.X1Z])
                    yield
            bb = L.bb
            for p in range(4):
                mm(bb[:, p, :], bb, identB[:], identB, ZR[:, c, p, :], ZR, p == 0, stop=False, first_write=(p == 0))
                mm(bb[:, p, :], bb, L.X1Z[:, p, :], L.X1Z, PTs[:], PTs, False)
            ba = L.tokc.bank
            for p in range(4):
                mm(ba[:, p * 128:(p + 1) * 128], ba, L.X1Z[:, p, :], L.X1Z, BtZ[:, p, :], BtZ, p == 0, first_write=(p == 0))
            yield
            S.op("act", lambda e: e.copy(out=WyZ[:], in_=bb[:]), writes=[WyZ, bb])
            S.op("dve", lambda e: e.tensor_copy(out=WhT[:].rearrange("q p m -> q (p m)"), in_=ba[:]), writes=[WhT, ba])
            yield

        def state_stream(pc):
            c = pc["c"]; bt = pc["bt"]
            BtZ, KtZ, U0, Vt, PTs, QTs, WyZ, WhT = (pc[k] for k in ("BtZ", "KtZ", "U0", "Vt", "PTs", "QTs", "WyZ", "WhT"))
            for p in range(4):
                mm(WHp.t[:, p, :], WHp, BtZ[:, p, :], BtZ, U0[:], U0, p == 0, stop=False, first_write=(p == 0))
            for p in range(4):
                mm(WHp.t[:, p, :], WHp, KtZ[:, p, :], KtZ, Vt[:], Vt, False, stop=False)
            mm(Yp.t, Yp, PTs[:], PTs, U0[:], U0, False, stop=False)
            mm(Yp.t, Yp, QTs[:], QTs, Vt[:], Vt, False, stop=False)
            yield
            for p in range(4):
                mm(Yp.t, Yp, WyZ[:, p, :], WyZ, Hbf[:, p, :], Hbf, False, stop=(p == 3))
            for p in range(4):
                mm(WHp.t[:, p, :], WHp, WhT[:, p, :], WhT, Hbf[:, p, :], Hbf, False, stop=True)
            yield
            Dc = bt["Dc"]
            S.op("dve", lambda e: e.tensor_tensor(out=Hn[:], in0=WHp.t, in1=Hm[:], op=ALU.add), reads=[Hm], writes=[Hn, WHp.bank])
            S.op("dve", lambda e: e.tensor_tensor(out=Hm[:], in0=Hn[:], in1=Dc[:, c, :].unsqueeze(2).to_broadcast([128, 4, 64]), op=ALU.mult),
                 reads=[Hn, Dc], writes=[Hm])
            ysb = ysbR.next()
            pc["ysb"] = ysb
            S.op("act", lambda e: e.copy(out=ysb[:], in_=Yp.t), writes=[ysb, Yp.bank])
            S.op("act", lambda e: e.copy(out=Hbf[:], in_=Hm[:]), reads=[Hm], writes=[Hbf])
            yield

        def out_stream(pc):
            c = pc["c"]; bt = pc["bt"]; ysb = pc["ysb"]
            S.op("dve", lambda e: e.bn_stats(out=stats[:], in_=ysb[:]), reads=[ysb], writes=[stats])
            S.op("dve", lambda e: e.bn_aggr(out=mv[:], in_=stats[:]), reads=[stats], writes=[mv])
            yield
            S.op("act", lambda e: e.activation(out=rstd[:], in_=mv[:, 1:2], func=AF.Sqrt, bias=GN_EPS, scale=1.0), reads=[mv], writes=[rstd])
            yield
            S.op("dve", lambda e: e.reciprocal(out=rstd[:], in_=rstd[:]), reads=[rstd], writes=[rstd])
            S.op("dve", lambda e: e.tensor_scalar(out=yn[:], in0=ysb[:], scalar1=mv[:, 0:1], scalar2=rstd[:, 0:1], op0=ALU.subtract, op1=ALU.mult),
                 reads=[ysb, mv, rstd], writes=[yn])
            yield
            S.op("dve", lambda e: e.tensor_tensor(out=YZ[:].rearrange("q p (a j) -> q p a j", a=2),
                                                   in0=yn[:].unsqueeze(1).unsqueeze(1).to_broadcast([128, 4, 2, 64]),
                                                   in1=maskZ[:].unsqueeze(3).to_broadcast([128, 4, 2, 64]), op=ALU.mult),
                 reads=[yn, maskZ], writes=[YZ])
            yield
            for p in range(4):
                mm(yfp.t[:, p, :], yfp, YZ[:, p, :], YZ, Sel[:], Sel, True, first_write=(p == 0))
            yield
            yb = bt["yb"]
            S.op("act", lambda e: e.copy(out=yb[:, :, c * 16:(c + 1) * 16], in_=yfp.t), writes=([yb] if c == 0 else []) + [yfp.bank],
                 pwrites=() if c == 0 else [yb])
            if c == NCH - 1:
                post(bt)
            yield

        def post(bt):
            yb = bt["yb"]; bon = bt["bon"]; rzt = bt["rzt"]; t0 = bt["t0"]
            S.op("pool", lambda e: e.tensor_tensor(out=yb[:], in0=yb[:], in1=bc4(lg), op=ALU.mult), reads=[yb, lg], writes=[yb])
            S.op("pool", lambda e: e.tensor_tensor(out=yb[:], in0=yb[:], in1=bc4(lb), op=ALU.add), reads=[yb, lb], writes=[yb])
            S.op("pool", lambda e: e.tensor_tensor(out=yb[:], in0=yb[:], in1=bon[:], op=ALU.add), reads=[yb, bon], writes=[yb])
            S.op("pool", lambda e: e.tensor_tensor(out=yo[:], in0=yb[:], in1=rzt[:], op=ALU.mult), reads=[yb, rzt], writes=[yo])
            S.dma("sp", scr["ysT"].t[2, :, t0:t0 + TP].rearrange("(c p) t -> p c t", p=128), yo[:], reads=[yo], pwrites=[scr["ysT"]], key=yo)

        chunks = []
        for nb in range(SEQ // TP):
            for c in range(NCH):
                chunks.append((nb, c))
        bts = {}
        nxt = 0
        lane_gen = [None, None]
        lane_pc = [None, None]
        done_order = {}
        next_state = 0
        state_gen = None; state_pc = None
        out_q = []; out_gen = None
        n_total = len(chunks)
        finished_out = 0
        pcs = {}
        while finished_out < n_total:
            for li in range(2):
                if lane_gen[li] is None and nxt < n_total and nxt - next_state < 3:
                    nb, c = chunks[nxt]
                    if nb not in bts:
                        bts[nb] = prep(nb)
                    pc = {"idx": nxt}
                    pcs[nxt] = pc
                    lane_gen[li] = pre(bts[nb], c, lanes[li], pc)
                    lane_pc[li] = pc
                    nxt += 1
                if lane_gen[li] is not None:
                    try:
                        next(lane_gen[li])
                    except StopIteration:
                        done_order[lane_pc[li]["idx"]] = True
                        lane_gen[li] = None
            for _rep in range(DEBUG.get("state_rep", 2)):
                if state_gen is None and done_order.get(next_state) and next_state - finished_out < 3:
                    state_pc = pcs[next_state]
                    state_gen = state_stream(state_pc)
                if state_gen is not None:
                    try:
                        next(state_gen)
                    except StopIteration:
                        out_q.append(state_pc)
                        state_gen = None
                        next_state += 1
            for _rep in range(DEBUG.get("out_rep", 1)):
                if out_gen is None and out_q:
                    out_gen = out_stream(out_q.pop(0))
                if out_gen is not None:
                    try:
                        next(out_gen)
                    except StopIteration:
                        out_gen = None
                        finished_out += 1
        _barrier(S)
        S.stack_pop()


WSPEC = {
    "norm_g": [2, 1024], "w_in": [2, 1024, 9112], "cmp_w1": [2, 2, 32, 64, 128], "cmp_w2": [2, 2, 128, 64],
    "cmp_pe": [2, 2, 32, 64], "sg_ln_g": [2, 512], "sg_ln_b": [2, 512], "sg_w": [2, 8, 128, 128], "sg_b": [2, 8, 128],
    "rk_mu": [2, 1664], "rk_w0": [2, 512], "rk_w2": [2, 64, 512], "rk_a0": [2, 512], "rk_a2": [2, 64, 512],
    "rk_kk": [2, 8, 64], "rk_ka": [2, 8, 64], "rk_rk": [2, 8, 64], "rk_lnx_g": [2, 512], "rk_lnx_b": [2, 512],
    "w_branch": [2, 3, 512, 1024], "w_o": [2, 1024, 1024], "ple_norm_g": [2, 1024], "w_ple_gate": [2, 1024, 1024],
    "w_ple_proj": [2, 256, 1024], "final_norm_g": [1, 1024],
}


def build(SEQ, nlayers=2, enable=(1, 1, 1), scr_kind="Internal"):
    nc = bass.Bass("TRN2", target_bir_lowering=False)
    with contextlib.ExitStack() as stack:
        S = Sched(nc, stack)
        x = Buf("x", nc.dram_tensor("x", [SEQ, D], F32, kind="ExternalInput").ap())
        Wd = {"p": Buf("p", nc.dram_tensor("p", [2, SEQ, PLE], F32, kind="ExternalInput").ap())}
        for k, shp in WSPEC.items():
            Wd[k] = Buf(k, nc.dram_tensor(k, shp, F32, kind="ExternalInput").ap())
        out = Buf("out", nc.dram_tensor("out", [SEQ, D], F32, kind="ExternalOutput").ap())
        scr = make_scratch(S, SEQ, kind=scr_kind)
        xmid = S.dram("xmid", [SEQ, D], F32, kind=scr_kind)
        cur = x
        for lyr in range(nlayers):
            last = lyr == nlayers - 1
            dst = out if last else xmid
            phase_A(S, nc, SEQ, lyr, cur, Wd, scr)
            if enable[0]:
                phase_B(S, nc, SEQ, lyr, Wd, scr)
            if enable[1]:
                phase_C(S, nc, SEQ, lyr, Wd, scr)
            if enable[2]:
                phase_D(S, nc, SEQ, lyr, Wd, scr)
            phase_E(S, nc, SEQ, lyr, cur, dst, Wd, scr, final=(last and nlayers == 2))
            cur = dst
        S.emit()
    return nc


def phase_B(S, nc, SEQ, lyr, Wd, scr):
    NC = (SEQ - 32) // 16 + 1
    NT = (NC + 127) // 128
    NCp = NT * 128
    KT = SEQ // 128
    with contextlib.ExitStack() as st:
        S.stack_push(st)
        ident = make_ident(S, "B_ident")
        ksT = S.sb("B_ksT", [128, 2, SEQ], BF16)
        HALF = min(4096, SEQ)
        NA = SEQ // HALF
        kwT = S.sb("B_kwT", [64, 2, SEQ], BF16)
        vs = S.sb("B_vs", [128, KT, 2, 65], BF16)
        vw = S.sb("B_vw", [128, KT, 2, 65], BF16)
        kcmpT = S.sb("B_kcmpT", [64, 2, NCp], BF16)
        Rc = S.sb("B_Rc", [128, NT, 2, 193], BF16)
        S.op("pool", lambda e: e.memset(ksT[64:128, :, :], 1.0), writes=[ksT])
        for g_ in range(2):
            for a_ in range(NA):
                S.op("pool", lambda e, g_=g_, a_=a_: e.affine_select(
                    out=ksT[64:128, g_, a_ * HALF:(a_ + 1) * HALF], in_=ksT[64:128, g_, a_ * HALF:(a_ + 1) * HALF], pattern=[[1, HALF]],
                    compare_op=ALU.is_ge, fill=0.0, base=0, channel_multiplier=-64), reads=[ksT], pwrites=[ksT])
                S.op("pool", lambda e, g_=g_, a_=a_: e.affine_select(
                    out=ksT[64:128, g_, a_ * HALF:(a_ + 1) * HALF], in_=ksT[64:128, g_, a_ * HALF:(a_ + 1) * HALF], pattern=[[-1, HALF]],
                    compare_op=ALU.is_ge, fill=0.0, base=63, channel_multiplier=64), reads=[ksT], pwrites=[ksT])
        S.dma("sp", ksT[0:64, :, :], scr["ksT"].t.rearrange("(g d) t -> d g t", g=2), reads=[scr["ksT"]], pwrites=[ksT], key=ksT)
        S.dma("sp", kwT[:], scr["kwT"].t.rearrange("(g d) t -> d g t", g=2), reads=[scr["kwT"]], writes=[kwT])
        S.op("pool", lambda e: e.memset(vs[:], 1.0), writes=[vs])
        S.op("pool", lambda e: e.memset(vw[:], 1.0), writes=[vw])
        for k0 in range(0, KT, 8):
            k1 = min(KT, k0 + 8)
            for (dst, c0) in ((vs, 0), (vw, 128)):
                for g in range(2):
                    S.dma("sp", dst[:, k0:k1, g, 0:64],
                          scr["vsw"].t[k0 * 128:k1 * 128, c0 + g * 64:c0 + (g + 1) * 64].rearrange("(k p) d -> p k d", p=128),
                          reads=[scr["vsw"]], pwrites=[dst], key=dst)
        S.op("pool", lambda e: e.memset(Rc[:], 1.0), writes=[Rc])
        for nt in range(NT):
            for g in range(2):
                S.op("pool", lambda e, nt=nt, g=g: e.affine_select(
                    out=Rc[:, nt, g, 65:193], in_=Rc[:, nt, g, 65:193], pattern=[[-4, 128]], compare_op=ALU.is_ge, fill=0.0,
                    base=nt * 128 + 1, channel_multiplier=1), reads=[Rc], writes=[Rc])
                S.op("pool", lambda e, nt=nt, g=g: e.affine_select(
                    out=Rc[:, nt, g, 65:193], in_=Rc[:, nt, g, 65:193], pattern=[[4, 128]], compare_op=ALU.is_ge, fill=0.0,
                    base=3 - nt * 128, channel_multiplier=-1), reads=[Rc], writes=[Rc])
        npad = NCp - NC
        if npad:
            S.op("pool", lambda e: e.affine_select(
                out=Rc[:, NT - 1, :, :], in_=Rc[:, NT - 1, :, :], pattern=[[0, 2 * 193]], compare_op=ALU.is_ge, fill=0.0,
                base=(NC - 1) - (NT - 1) * 128, channel_multiplier=-1), reads=[Rc], writes=[Rc])
        S.op("pool", lambda e: e.memset(kcmpT[:], 0.0), writes=[kcmpT])

        with contextlib.ExitStack() as st2:
            S.stack_push(st2)
            kvT = S.sb("B_kvT", [64, 2, SEQ], BF16)
            w1 = S.sb("B_w1", [64, 32, 128], BF16)
            w2 = S.sb("B_w2", [128, 64], BF16)
            peT = S.sb("B_peT", [64, 32])
            peTb = S.sb("B_peTb", [64, 32], BF16)
            cb = S.sb("B_cb", [128, 1])
            hid = S.sb("B_hid", [128, NCp], BF16)
            ph = S.ps("B_ph", [128, 512])
            pc1 = S.ps("B_pc1", [128, 512])
            pk = S.ps("B_pk", [128, 512])
            for kv in range(2):
                src = scr["kcT"] if kv == 0 else scr["vcT"]
                S.dma("sp", kvT[:], src.t.rearrange("(g d) t -> d g t", g=2), reads=[src], writes=[kvT])
                S.dma("pool", w1[:], Wd["cmp_w1"].t[lyr, kv].rearrange("l d h -> d l h"), reads=[Wd["cmp_w1"]], writes=[w1])
                S.dma("pool", w2[:], Wd["cmp_w2"].t[lyr, kv], reads=[Wd["cmp_w2"]], writes=[w2])
                S.dma("sp", peT[:], Wd["cmp_pe"].t[lyr, kv].rearrange("l d -> d l"), reads=[Wd["cmp_pe"]], writes=[peT],
                      allow_slow_non_contiguous=True)
                S.op("dve", lambda e: e.tensor_copy(out=peTb[:], in_=peT[:]), reads=[peT], writes=[peTb])
                for l in range(32):
                    S.op("pe", lambda e, l=l: e.matmul(pc1[:, 0:1], lhsT=w1[:, l, :], rhs=peTb[:, l:l + 1], start=(l == 0), stop=(l == 31)),
                         reads=[w1, peTb], writes=[pc1] if l == 0 else (), pwrites=() if l == 0 else [pc1])
                S.op("dve", lambda e: e.tensor_copy(out=cb[:], in_=pc1[:, 0:1]), reads=[pc1], writes=[cb])
                for g in range(2):
                    S.op("dve", lambda e: e.memset(hid[:], 0.0), writes=[hid])
                    for n0 in range(0, NC, 512):
                        nn = min(512, NC - n0)
                        for l in range(32):
                            S.op("pe", lambda e, l=l, g=g, n0=n0, nn=nn: e.matmul(
                                ph[:, 0:nn], lhsT=w1[:, l, :], rhs=kvT[:, g, n0 * 16 + l: n0 * 16 + l + (nn - 1) * 16 + 1: 16], start=(l == 0), stop=(l == 31)),
                                reads=[w1, kvT], writes=[ph] if l == 0 else (), pwrites=() if l == 0 else [ph])
                        S.op("act", lambda e, n0=n0, nn=nn: e.activation(out=hid[:, n0:n0 + nn], in_=ph[:, 0:nn], func=AF.Silu, bias=cb[:, 0:1]),
                             reads=[ph, cb], pwrites=[hid])
                    if kv == 0:
                        for n0 in range(0, NC, 512):
                            nn = min(512, NC - n0)
                            S.op("pe", lambda e, n0=n0, nn=nn: e.matmul(pk[0:64, 0:nn], lhsT=w2[:], rhs=hid[:, n0:n0 + nn], start=True, stop=True),
                                 reads=[w2, hid], writes=[pk])
                            S.op("dve", lambda e, g=g, n0=n0, nn=nn: e.tensor_copy(out=kcmpT[:, g, n0:n0 + nn], in_=pk[0:64, 0:nn]),
                                 reads=[pk], pwrites=[kcmpT])
                    else:
                        for nt in range(NT):
                            rows = min(128, NC - nt * 128)
                            S.op("pe", lambda e, nt=nt: e.matmul(pk[:, 0:64], lhsT=hid[:, nt * 128:(nt + 1) * 128], rhs=w2[:], start=True, stop=True),
                                 reads=[w2, hid], writes=[pk])
                            S.op("dve", lambda e, g=g, nt=nt: e.tensor_copy(out=Rc[:, nt, g, 0:64], in_=pk[:, 0:64]),
                                 reads=[pk], pwrites=[Rc])
            _barrier(S)
            S.stack_pop()

        qt = Ring([S.sb("B_q%d" % i, [64, 8, 128], BF16) for i in range(2)])
        gt = Ring([S.sb("B_g%d" % i, [128, 24]) for i in range(2)])
        nzt = Ring([S.sb("B_nz%d" % i, [128, 512], BF16) for i in range(2)])
        Et = Ring([S.sb("B_E%d" % i, [128, 512], BF16) for i in range(4)])
        psT = Ring([S.ps("B_psT%d" % i, [128, 512]) for i in range(3)])
        pcA = S.ps("B_pcA", [128, 2, 193])
        pcB = S.ps("B_pcB", [128, 2, 193])
        pos = S.ps("B_pos", [128, 4, 65])
        pow_ = S.ps("B_pow", [128, 4, 65])
        pmisc = S.ps("B_pmisc", [128, 4, 128], BF16)
        P2 = lambda nm, shp, dt=F32: [S.sb("B_%s%d" % (nm, i), shp, dt) for i in range(2)]
        oc2 = P2("oc", [128, 4, 193]); rcs2 = P2("rcs", [128, 4]); rss2 = P2("rss", [128, 4]); rws2 = P2("rws", [128, 4])
        cc2 = P2("cc", [128, 3, 4]); sc_2 = P2("sc", [128, 128]); sc2_2 = P2("sc2", [128, 128]); m1_2 = P2("m1", [128, 8]); m2_2 = P2("m2", [128, 8])
        pws2 = P2("pws", [128, 4, 65]); pss2 = P2("pss", [128, 4, 65])
        negq2 = P2("negq", [128, 2, 128], BF16)
        for _nq in negq2:
            S.op("pool", lambda e, _nq=_nq: e.memset(_nq[:], 0.0), writes=[_nq])
        qAr = {(g_, a_): Ring([S.sb("B_qA%d%d_%d" % (g_, a_, i), [128, 4, 128], BF16) for i in range(2)]) for g_ in range(2) for a_ in range(NA)}
        yg = S.sb("B_yg", [128, 4, 64])
        ytmp = S.sb("B_ytmp", [128, 4, 64])
        ynsaR = Ring([S.sb("B_ynsa%d" % i, [128, 512], BF16) for i in range(2)])
        stg = Ring([S.sb("B_stg%d" % i, [128, 4, 128], BF16) for i in range(2)])

        def qk_exp(kT_ap, kbuf, q_ap, qbuf, neg_lhsT=None):
            p = psT.next()
            if False:
                pass
            else:
                S.op("pe", lambda e, p=p: e.matmul(p[:], lhsT=kT_ap, rhs=q_ap, start=True, stop=True), reads=[kbuf, qbuf], writes=[p])
            E = Et.next()
            S.op("act", lambda e, p=p, E=E: e.activation(out=E[:], in_=p[:], func=AF.Exp), reads=[p], writes=[E])
            return E

        def pipeline(tiles, L=2):
            Es = {}
            n = len(tiles)
            for i in range(n + L):
                if i < n:
                    Es[i] = tiles[i][0]()
                if i - L >= 0:
                    tiles[i - L][1](Es.pop(i - L))

        def mask(E, base, cm, qstep):
            S.op("pool", lambda e, E=E: e.affine_select(out=E[:], in_=E[:], pattern=[[0, 4], [qstep, 128]], compare_op=ALU.is_ge,
                                                       fill=0.0, base=base, channel_multiplier=cm), reads=[E], writes=[E])

        pending_tail = [None]
        for qb in range(SEQ // 128):
            q0 = qb * 128
            q = qt.next(); gg = gt.next(); nz = nzt.next(); ynsa = ynsaR.next()
            S.dma("sp", q[:], scr["qT"].t[:, q0:q0 + 128].rearrange("(h d) t -> d h t", h=8), reads=[scr["qT"]], writes=[q])
            qAs = {}
            for g_ in range(2):
                for a_ in range(min(NA, qb * 128 // HALF + 1)):
                    qa = qAr[(g_, a_)].next()
                    qAs[(g_, a_)] = qa
                    S.dma("sp", qa[0:64, :, :], scr["qT"].t[g_ * 256:(g_ + 1) * 256, q0:q0 + 128].rearrange("(h d) t -> d h t", h=4),
                          reads=[scr["qT"]], writes=[qa])
            S.dma("sp", gg[:], scr["gate"].t[q0:q0 + 128, :], reads=[scr["gate"]], writes=[gg])
            S.dma("sp", nz[:], scr["nzs"].t[q0:q0 + 128, :], reads=[scr["nzs"]], writes=[nz])
            def gbody(g, q=q, gg=gg, nz=nz, qAs=qAs, qb=qb, q0=q0, ynsa=ynsa):
                oc = oc2[g]; rcs = rcs2[g]; rss = rss2[g]; rws = rws2[g]; cc = cc2[g]; sc = sc_2[g]; sc2 = sc2_2[g]
                m1 = m1_2[g]; m2 = m2_2[g]; negq = negq2[g]; pws = pws2[g]; pss = pss2[g]
                q_ap = q[:, 4 * g:4 * g + 4, :].rearrange("d h q -> d (h q)")
                n_max = min(8 * qb + 6, NC - 1)
                ntl = n_max // 128 + 1
                def c_qk(nt, g=g, q_ap=q_ap, q=q):
                    E = qk_exp(kcmpT[:, g, nt * 128:(nt + 1) * 128], kcmpT, q_ap, q)
                    if q0 - 16 * (128 * nt + 127) - 31 < 0:
                        mask(E, q0 - 16 * 128 * nt - 31, -16, 1)
                    return E

                def c_pv(nt, E, g=g, ntl=ntl):
                    for h in range(4):
                        pcx = pcA if h < 2 else pcB
                        first = (nt == 0 and h % 2 == 0)
                        S.op("pe", lambda e, E=E, h=h, pcx=pcx, nt=nt, first=first, g=g, ntl=ntl: e.matmul(
                            pcx[:, h % 2, :], lhsT=E[:, h * 128:(h + 1) * 128], rhs=Rc[:, nt, g, :], start=first,
                            stop=(nt == ntl - 1 and h % 2 == 1), skip_group_check=True),
                            reads=[E, Rc], writes=[pcx] if first else (), pwrites=() if first else [pcx])
                pipeline([(lambda nt=nt: c_qk(nt), lambda E, nt=nt: c_pv(nt, E)) for nt in range(ntl)])
                S.op("act", lambda e: e.copy(out=oc[:, 0:2, :], in_=pcA[:]), reads=[pcA], pwrites=[oc])
                S.op("act", lambda e: e.copy(out=oc[:, 2:4, :], in_=pcB[:]), reads=[pcB], pwrites=[oc])
                S.op("dve", lambda e: e.tensor_scalar(out=rcs[:], in0=oc[:, :, 64], scalar1=1e-30, scalar2=None, op0=ALU.max),
                     reads=[oc], writes=[rcs])
                S.op("dve", lambda e: e.reciprocal(out=rcs[:], in_=rcs[:]), reads=[rcs], writes=[rcs])
                S.op("dve", lambda e: e.tensor_scalar(out=sc[:], in0=oc[:, 0, 65:193], scalar1=rcs[:, 0:1], scalar2=None, op0=ALU.mult),
                     reads=[oc, rcs], writes=[sc])
                for h in range(1, 4):
                    S.op("dve", lambda e, h=h: e.scalar_tensor_tensor(out=sc[:], in0=oc[:, h, 65:193], scalar=rcs[:, h:h + 1], in1=sc[:],
                                                                      op0=ALU.mult, op1=ALU.add), reads=[oc, rcs, sc], writes=[sc])
                for half in range(2):
                    tb = 2 * qb + half
                    ps_ = slice(half * 64, (half + 1) * 64)
                    if tb + 1 < 128:
                        S.op("dve", lambda e, ps_=ps_, tb=tb: e.memset(sc[ps_, tb + 1:128], -1e4), reads=[sc], writes=[sc])
                    lo = max(tb - 1, 0)
                    S.op("dve", lambda e, ps_=ps_, tb=tb, lo=lo: e.memset(sc[ps_, lo:tb + 1], 1e4), reads=[sc], writes=[sc])
                S.op("dve", lambda e: e.memset(sc[:, 0:1], 1e4), reads=[sc], writes=[sc])
                S.op("dve", lambda e: e.max(out=m1[:], in_=sc[:]), reads=[sc], writes=[m1])
                S.op("dve", lambda e: e.match_replace(out=sc2[:], in_to_replace=m1[:], in_values=sc[:], imm_value=-3e4),
                     reads=[sc, m1], writes=[sc2])
                S.op("dve", lambda e: e.max(out=m2[:], in_=sc2[:]), reads=[sc2], writes=[m2])
                S.op("dve", lambda e: e.tensor_scalar(out=negq[:, 0, :], in0=sc[:], scalar1=m2[:, 7:8], scalar2=-1e4, op0=ALU.is_lt, op1=ALU.mult),
                     reads=[sc, m2], pwrites=[negq])
                S.op("dve", lambda e: e.tensor_scalar(out=negq[:, 1, 64:128], in0=sc[:, 0:64], scalar1=m2[:, 7:8], scalar2=-1e4, op0=ALU.is_lt, op1=ALU.mult),
                     reads=[sc, m2], pwrites=[negq])
                yield
                kts = list(range(max(0, qb - 4), qb + 1))

                def w_qk(i, kt, g=g, q_ap=q_ap, q=q, qb=qb):
                    E = qk_exp(kwT[:, g, kt * 128:(kt + 1) * 128], kwT, q_ap, q)
                    if kt == qb - 4:
                        mask(E, -1, 1, -1)
                    if kt == qb:
                        mask(E, 0, -1, 1)
                    return E

                def w_pv(i, kt, E, g=g, kts=kts):
                    for h in range(4):
                        first = (i == 0 and h == 0)
                        S.op("pe", lambda e, E=E, h=h, kt=kt, first=first, last=(i == len(kts) - 1 and h == 3), g=g: e.matmul(
                            pow_[:, h, :], lhsT=E[:, h * 128:(h + 1) * 128], rhs=vw[:, kt, g, :], start=first, stop=last,
                            skip_group_check=True),
                            reads=[E, vw], writes=[pow_] if first else (), pwrites=() if first else [pow_])
                pipeline([(lambda i=i, kt=kt: w_qk(i, kt), lambda E, i=i, kt=kt: w_pv(i, kt, E)) for i, kt in enumerate(kts)])
                S.op("act", lambda e: e.copy(out=pws[:], in_=pow_[:]), reads=[pow_], writes=[pws])
                yield
                na_here = min(NA, qb * 128 // HALF + 1)
                S.op("pe", lambda e: e.transpose(out=pmisc[:, 1, :], in_=negq[:, 1, :], identity=ident[:]), reads=[negq, ident], writes=[pmisc])
                if na_here > 1:
                    S.op("pe", lambda e: e.transpose(out=pmisc[:, 0, :], in_=negq[:, 0, :], identity=ident[:]), reads=[negq, ident], pwrites=[pmisc])
                for a_ in range(na_here):
                    qa = qAs[(g, a_)]
                    S.op("dve", lambda e, qa=qa, a_=a_: e.tensor_copy(out=qa[64:128, :, :],
                                                                   in_=pmisc[64:128, (1 - a_):(2 - a_), :].to_broadcast([64, 4, 128])),
                         reads=[pmisc], pwrites=[qa])
                yield
                def s_qk(kt, g=g, q_ap=q_ap, q=q, qb=qb, qAs=qAs):
                    qa = qAs[(g, kt * 128 // HALF)]
                    E = qk_exp(ksT[:, g, kt * 128:(kt + 1) * 128], ksT, qa[:].rearrange("p h q -> p (h q)"), qa)
                    if kt == qb:
                        mask(E, 0, -1, 1)
                    return E

                def s_pv(kt, E, g=g, qb=qb):
                    for h in range(4):
                        first = (kt == 0 and h == 0)
                        S.op("pe", lambda e, E=E, h=h, kt=kt, first=first, last=(kt == qb and h == 3), g=g: e.matmul(
                            pos[:, h, :], lhsT=E[:, h * 128:(h + 1) * 128], rhs=vs[:, kt, g, :], start=first, stop=last,
                            skip_group_check=True),
                            reads=[E, vs], writes=[pos] if first else (), pwrites=() if first else [pos])
                pipeline([(lambda kt=kt: s_qk(kt), lambda E, kt=kt: s_pv(kt, E)) for kt in range(qb + 1)])
                S.op("act", lambda e: e.copy(out=pss[:], in_=pos[:]), reads=[pos], writes=[pss])
                yield
                S.op("dve", lambda e: e.reciprocal(out=rss[:], in_=pss[:, :, 64]), reads=[pss], writes=[rss])
                S.op("dve", lambda e: e.reciprocal(out=rws[:], in_=pws[:, :, 64]), reads=[pws], writes=[rws])
                gv = gg[:, g * 12:(g + 1) * 12].rearrange("p (h b) -> p b h", b=3)
                for b, rr in enumerate((rcs, rss, rws)):
                    S.op("dve", lambda e, b=b, rr=rr, gv=gv: e.tensor_tensor(out=cc[:, b, :], in0=gv[:, b, :], in1=rr[:], op=ALU.mult),
                         reads=[gg, rr], pwrites=[cc])
                bc = lambda b: cc[:, b, :].unsqueeze(2).to_broadcast([128, 4, 64])
                S.op("dve", lambda e: e.tensor_tensor(out=yg[:], in0=oc[:, :, 0:64], in1=bc(0), op=ALU.mult), reads=[oc, cc], writes=[yg])
                S.op("dve", lambda e: e.tensor_tensor(out=ytmp[:], in0=pss[:, :, 0:64], in1=bc(1), op=ALU.mult), reads=[pss, cc], writes=[ytmp])
                S.op("pool", lambda e: e.tensor_tensor(out=yg[:], in0=yg[:], in1=ytmp[:], op=ALU.add), reads=[yg, ytmp], writes=[yg])
                S.op("dve", lambda e: e.tensor_tensor(out=ytmp[:], in0=pws[:, :, 0:64], in1=bc(2), op=ALU.mult), reads=[pws, cc], writes=[ytmp])
                S.op("pool", lambda e: e.tensor_tensor(out=yg[:], in0=yg[:], in1=ytmp[:], op=ALU.add), reads=[yg, ytmp], writes=[yg])
                S.op("pool", lambda e, g=g, nz=nz: e.tensor_tensor(out=ynsa[:, g * 256:(g + 1) * 256], in0=yg[:].rearrange("p h d -> p (h d)"),
                                                                   in1=nz[:, g * 256:(g + 1) * 256], op=ALU.mult),
                     reads=[yg, nz], pwrites=[ynsa])
                yield
            gens = [gbody(0), gbody(1)]
            for _st in range(5):
                for gen_ in gens:
                    next(gen_)
                if _st == 1 and pending_tail[0] is not None:
                    pending_tail[0]()
                    pending_tail[0] = None

            def tail(ynsa=ynsa, q0=q0):
                for k in range(4):
                    S.op("pe", lambda e, k=k: e.transpose(out=pmisc[:, k, :], in_=ynsa[:, k * 128:(k + 1) * 128], identity=ident[:]),
                         reads=[ynsa, ident], writes=[pmisc] if k == 0 else (), pwrites=() if k == 0 else [pmisc])
                sg = stg.next()
                S.op("act", lambda e, sg=sg: e.copy(out=sg[:], in_=pmisc[:]), reads=[pmisc], writes=[sg])
                S.dma("sp", scr["ysT"].t[0, :, q0:q0 + 128].rearrange("(k p) t -> p k t", p=128), sg[:], reads=[sg], pwrites=[scr["ysT"]], key=sg)
            pending_tail[0] = tail
        if pending_tail[0] is not None:
            pending_tail[0]()
        _barrier(S)
        S.stack_pop()


def phase_D_seq(S, nc, SEQ, lyr, Wd, scr):
    TP = 128
    TB = 8
    GN_EPS = 64e-5
    xtok = scr["xtok"]
    with contextlib.ExitStack() as st:
        S.stack_push(st)
        identF = make_ident(S, "D_ident", F32)
        ones = S.sb("D_ones", [128, 128])
        S.op("pool", lambda e: e.memset(ones[:], 0.0), writes=[ones])
        S.op("pool", lambda e: e.memset(ones[0:64, 0:64], 1.0), reads=[ones], writes=[ones])
        S.op("pool", lambda e: e.memset(ones[64:128, 64:128], 1.0), reads=[ones], writes=[ones])

        def cvec(name, key, n):
            t = S.sb("D_" + name, [128, n])
            S.dma("sp", t[:], Wd[key].t[lyr].rearrange("(c p) -> p c", p=128), reads=[Wd[key]], writes=[t],
                  allow_slow_non_contiguous=True)
            return t

        def cvec2(name, key):
            t = S.sb("D_" + name, [128, 4])
            S.dma("sp", t[:], Wd[key].t[lyr].rearrange("(c a) j -> (a j) c", a=2), reads=[Wd[key]], writes=[t],
                  allow_slow_non_contiguous=True)
            return t
        mu = cvec("mu", "rk_mu", 13)
        w0 = cvec("w0", "rk_w0", 4)
        a0 = cvec("a0", "rk_a0", 4)
        lg = cvec("lg", "rk_lnx_g", 4)
        lb = cvec("lb", "rk_lnx_b", 4)
        kkc = cvec2("kkc", "rk_kk")
        ka = cvec2("ka", "rk_ka")
        rkc = cvec2("rkc", "rk_rk")
        omka = S.sb("D_omka", [128, 4])
        S.op("pool", lambda e: e.tensor_scalar(out=omka[:], in0=ka[:], scalar1=-1.0, scalar2=1.0, op0=ALU.mult, op1=ALU.add),
             reads=[ka], writes=[omka])
        w2 = S.sb("D_w2", [64, 512], BF16)
        a2 = S.sb("D_a2", [128, 512], BF16)
        S.dma("pool", w2[:], Wd["rk_w2"].t[lyr], reads=[Wd["rk_w2"]], writes=[w2])
        S.dma("pool", a2[64:128, :], Wd["rk_a2"].t[lyr], reads=[Wd["rk_a2"]], writes=[a2])
        St = S.sb("D_state", [128, 4, 64])
        S.op("dve", lambda e: e.memset(St[:], 0.0), writes=[St])

        rst = S.sb("D_rst", [128, 13, TP + 1])
        xs = S.sb("D_xs", [128, 13, TP])
        th = S.sb("D_th", [128, TP], BF16)
        dd = S.sb("D_dd", [128, 4, TP])
        aa = S.sb("D_aa", [128, 4, TP])
        kkf = S.sb("D_kkf", [128, 4, TP])
        sq = S.sb("D_sq", [128, 4, TP])
        rn = S.sb("D_rn", [128, 4, TP])
        kp = S.sb("D_kp", [128, 4, TP])
        am = S.sb("D_am", [128, 4, TP])
        bm = S.sb("D_bm", [128, 4, TP])
        t1 = S.sb("D_t1", [128, 4, TP])
        bonus = S.sb("D_bonus", [128, 4, TP])
        vv = S.sb("D_vv", [128, 4, TP])
        tk = S.sb("D_tk", [128, 5, 4, 128])
        bcr = Ring([S.sb("D_bc%d" % i, [128, TB, 5, 256]) for i in range(2)])
        tmp = S.sb("D_tmp", [128, 4, 64])
        tmp2 = S.sb("D_tmp2", [128, 4, 64])
        kv = Ring([S.sb("D_kv%d" % i, [128, 4, 64]) for i in range(2)])
        sa = S.sb("D_sa", [128, 4])
        ybuf = S.sb("D_y", [128, 4, TP])
        ysq = S.sb("D_ysq", [128, 4, TP])
        mean = S.sb("D_mean", [128, 4, TP])
        var = S.sb("D_var", [128, 4, TP])
        rzt = S.sb("D_rz", [128, 4, TP], BF16)
        yo = S.sb("D_yo", [128, 4, TP], BF16)
        pa = Ring([S.ps("D_pa%d" % i, [128, 4, 128]) for i in range(4)])

        bc4 = lambda t: t[:].unsqueeze(2).to_broadcast([128, 4, TP])
        for nb in range(SEQ // TP):
            t0 = nb * TP
            S.dma("sp", rst[:, :, 1:TP + 1], scr["rsT"].t[:, t0:t0 + TP].rearrange("(c p) t -> p c t", p=128), reads=[scr["rsT"]],
                  writes=[rst])
            if nb == 0:
                S.op("pool", lambda e: e.memset(rst[:, :, 0:1], 0.0), reads=[rst], pwrites=[rst])
            else:
                S.dma("sp", rst[:, :, 0:1], scr["rsT"].t[:, t0 - 1:t0].rearrange("(c p) t -> p c t", p=128), reads=[scr["rsT"]],
                      pwrites=[rst], key=rst, allow_slow_non_contiguous=True)
            S.op("pool", lambda e: e.tensor_tensor(out=xs[:], in0=rst[:, :, 0:TP], in1=rst[:, :, 1:TP + 1], op=ALU.subtract),
                 reads=[rst], writes=[xs])
            S.op("pool", lambda e: e.tensor_tensor(out=xs[:], in0=xs[:], in1=mu[:].unsqueeze(2).to_broadcast([128, 13, TP]), op=ALU.mult),
                 reads=[xs, mu], writes=[xs])
            S.op("pool", lambda e: e.tensor_tensor(out=xs[:], in0=xs[:], in1=rst[:, :, 1:TP + 1], op=ALU.add), reads=[xs, rst], writes=[xs])
            r = xs[:, 0:4, :]; k = xs[:, 4:8, :]; v = xs[:, 8:12, :]
            S.op("act", lambda e: e.activation(out=th[0:64, :], in_=xs[0:64, 12, :], func=AF.Tanh), reads=[xs], pwrites=[th])
            S.op("act", lambda e: e.copy(out=th[64:128, :], in_=xs[64:128, 12, :]), reads=[xs], pwrites=[th])
            pw = pa.next(); pp = pa.next()
            for p in range(4):
                S.op("pe", lambda e, p=p, pw=pw: e.matmul(pw[:, p, :], lhsT=w2[0:64, p * 128:(p + 1) * 128], rhs=th[0:64, :], start=True, stop=True),
                     reads=[w2, th], writes=[pw] if p == 0 else (), pwrites=() if p == 0 else [pw])
                S.op("pe", lambda e, p=p, pp=pp: e.matmul(pp[:, p, :], lhsT=a2[64:128, p * 128:(p + 1) * 128], rhs=th[64:128, :], start=True, stop=True),
                     reads=[a2, th], writes=[pp] if p == 0 else (), pwrites=() if p == 0 else [pp])
            for p in range(4):
                S.op("act", lambda e, p=p, pw=pw: e.activation(out=dd[:, p, :], in_=pw[:, p, :], func=AF.Sigmoid, bias=w0[:, p:p + 1]),
                     reads=[pw, w0], pwrites=[dd])
                S.op("act", lambda e, p=p, pp=pp: e.activation(out=aa[:, p, :], in_=pp[:, p, :], func=AF.Sigmoid, bias=a0[:, p:p + 1]),
                     reads=[pp, a0], pwrites=[aa])
            S.op("act", lambda e: e.activation(out=dd[:], in_=dd[:], func=AF.Exp, scale=-0.6065306597126334), reads=[dd], writes=[dd])
            S.op("pool", lambda e: e.tensor_tensor(out=kkf[:], in0=k, in1=bc4(kkc), op=ALU.mult), reads=[xs, kkc], writes=[kkf])
            S.op("pool", lambda e: e.tensor_tensor(out=sq[:], in0=kkf[:], in1=kkf[:], op=ALU.mult), reads=[kkf], writes=[sq])
            pn = pa.next()
            for p in range(4):
                S.op("pe", lambda e, p=p, pn=pn: e.matmul(pn[:, p, :], lhsT=ones[:], rhs=sq[:, p, :], start=True, stop=True),
                     reads=[ones, sq], writes=[pn] if p == 0 else (), pwrites=() if p == 0 else [pn])
            S.op("act", lambda e, pn=pn: e.activation(out=rn[:], in_=pn[:], func=AF.Sqrt), reads=[pn], writes=[rn])
            S.op("pool", lambda e: e.tensor_scalar(out=rn[:], in0=rn[:], scalar1=1e-12, scalar2=None, op0=ALU.max), reads=[rn], writes=[rn])
            S.op("dve", lambda e: e.reciprocal(out=rn[:], in_=rn[:]), reads=[rn], writes=[rn])
            S.op("pool", lambda e: e.tensor_tensor(out=kkf[:], in0=kkf[:], in1=rn[:], op=ALU.mult), reads=[kkf, rn], writes=[kkf])
            S.op("pool", lambda e: e.tensor_tensor(out=t1[:], in0=aa[:], in1=bc4(ka), op=ALU.mult), reads=[aa, ka], writes=[t1])
            S.op("pool", lambda e: e.tensor_tensor(out=t1[:], in0=t1[:], in1=bc4(omka), op=ALU.add), reads=[t1, omka], writes=[t1])
            S.op("pool", lambda e: e.tensor_tensor(out=kp[:], in0=k, in1=t1[:], op=ALU.mult), reads=[xs, t1], writes=[kp])
            S.op("pool", lambda e: e.tensor_scalar(out=am[:], in0=kkf[:], scalar1=-1.0, scalar2=None, op0=ALU.mult), reads=[kkf], writes=[am])
            S.op("pool", lambda e: e.tensor_tensor(out=bm[:], in0=kkf[:], in1=aa[:], op=ALU.mult), reads=[kkf, aa], writes=[bm])
            S.op("pool", lambda e: e.tensor_tensor(out=t1[:], in0=r, in1=kp[:], op=ALU.mult), reads=[xs, kp], writes=[t1])
            S.op("pool", lambda e: e.tensor_tensor(out=sq[:], in0=t1[:], in1=bc4(rkc), op=ALU.mult), reads=[t1, rkc], writes=[sq])
            pr = pa.next()
            for p in range(4):
                S.op("pe", lambda e, p=p, pr=pr: e.matmul(pr[:, p, :], lhsT=ones[:], rhs=sq[:, p, :], start=True, stop=True),
                     reads=[ones, sq], writes=[pr] if p == 0 else (), pwrites=() if p == 0 else [pr])
            S.op("act", lambda e, pr=pr: e.copy(out=bonus[:], in_=pr[:]), reads=[pr], writes=[bonus])
            S.op("pool", lambda e: e.tensor_tensor(out=bonus[:], in0=bonus[:], in1=v, op=ALU.mult), reads=[bonus, xs], writes=[bonus])
            S.op("pool", lambda e: e.tensor_copy(out=vv[:], in_=v), reads=[xs], writes=[vv])
            S.op("pool", lambda e: e.tensor_copy(out=t1[:], in_=r), reads=[xs], writes=[t1])
            for oi, src in enumerate((am, bm, dd, kp, t1)):
                pt = pa.next()
                for p in range(4):
                    S.op("pe", lambda e, p=p, src=src, pt=pt: e.transpose(out=pt[:, p, :], in_=src[:, p, :], identity=identF[:]),
                         reads=[src, identF], writes=[pt] if p == 0 else (), pwrites=() if p == 0 else [pt])
                S.op("act", lambda e, oi=oi, pt=pt: e.copy(out=tk[:, oi, :, :], in_=pt[:]), reads=[pt], pwrites=[tk])
            for oi in range(5):
                for h2 in range(2):
                    S.dma("sp", xtok.t[t0:t0 + TP, oi, h2, :].rearrange("t (p j) -> t p j", p=4), tk[:, oi, :, h2 * 64:(h2 + 1) * 64],
                          reads=[tk], pwrites=[xtok], key=tk)
            S.dma("sp", rzt[:], scr["rzT"].t[:, t0:t0 + TP].rearrange("(c p) t -> p c t", p=128), reads=[scr["rzT"]], writes=[rzt])
            xflat = xtok.t.rearrange("t o h c -> (t o) h c")
            for tb in range(0, TP, TB):
                bc = bcr.next()
                for h2 in range(2):
                    S.dma("sp", bc[h2 * 64:(h2 + 1) * 64, :, :, :].rearrange("p t o c -> p (t o) c"),
                          xflat[(t0 + tb) * 5:(t0 + tb + TB) * 5, h2, :].partition_broadcast(64),
                          reads=[xtok], writes=[bc] if h2 == 0 else (), pwrites=() if h2 == 0 else [bc], key=bc)
                for tt in range(TB):
                    t = tb + tt
                    A = bc[:, tt, 0, :].rearrange("p (a j) -> p a j", a=4)
                    B = bc[:, tt, 1, :].rearrange("p (a j) -> p a j", a=4)
                    Dd = bc[:, tt, 2, :].rearrange("p (a j) -> p a j", a=4)
                    Kk = bc[:, tt, 3, :].rearrange("p (a j) -> p a j", a=4)
                    R = bc[:, tt, 4, :].rearrange("p (a j) -> p a j", a=4)
                    kvb = kv.next()
                    S.op("pool", lambda e, Kk=Kk, t=t, kvb=kvb: e.tensor_tensor(out=kvb[:], in0=Kk, in1=vv[:, :, t:t + 1].to_broadcast([128, 4, 64]),
                                                                              op=ALU.mult), reads=[bc, vv], writes=[kvb])
                    S.op("dve", lambda e, A=A: e.tensor_tensor(out=tmp[:], in0=St[:], in1=A, op=ALU.mult), reads=[St, bc], writes=[tmp])
                    S.op("dve", lambda e: e.tensor_reduce(out=sa[:], in_=tmp[:], axis=AX.X, op=ALU.add), reads=[tmp], writes=[sa])
                    S.op("dve", lambda e, Dd=Dd: e.tensor_tensor(out=St[:], in0=St[:], in1=Dd, op=ALU.mult), reads=[St, bc, tmp], writes=[St])
                    S.op("dve", lambda e, B=B: e.tensor_tensor(out=tmp2[:], in0=B, in1=sa[:].unsqueeze(2).to_broadcast([128, 4, 64]), op=ALU.mult),
                         reads=[bc, sa], writes=[tmp2])
                    S.op("dve", lambda e: e.tensor_tensor(out=St[:], in0=St[:], in1=tmp2[:], op=ALU.add), reads=[St, tmp2], writes=[St])
                    S.op("dve", lambda e, kvb=kvb: e.tensor_tensor(out=St[:], in0=St[:], in1=kvb[:], op=ALU.add), reads=[St, kvb], writes=[St])
                    S.op("dve", lambda e, R=R: e.tensor_tensor(out=tmp[:], in0=St[:], in1=R, op=ALU.mult), reads=[St, bc], writes=[tmp])
                    S.op("dve", lambda e, t=t: e.tensor_reduce(out=ybuf[:, :, t], in_=tmp[:], axis=AX.X, op=ALU.add), reads=[tmp], pwrites=[ybuf])
            S.op("pool", lambda e: e.tensor_tensor(out=ysq[:], in0=ybuf[:], in1=ybuf[:], op=ALU.mult), reads=[ybuf], writes=[ysq])
            pm = pa.next(); pq = pa.next()
            for p in range(4):
                S.op("pe", lambda e, p=p, pm=pm: e.matmul(pm[:, p, :], lhsT=ones[:], rhs=ybuf[:, p, :], start=True, stop=True),
                     reads=[ones, ybuf], writes=[pm] if p == 0 else (), pwrites=() if p == 0 else [pm])
                S.op("pe", lambda e, p=p, pq=pq: e.matmul(pq[:, p, :], lhsT=ones[:], rhs=ysq[:, p, :], start=True, stop=True),
                     reads=[ones, ysq], writes=[pq] if p == 0 else (), pwrites=() if p == 0 else [pq])
            S.op("act", lambda e, pm=pm: e.activation(out=mean[:], in_=pm[:], func=AF.Copy, scale=1.0 / 64), reads=[pm], writes=[mean])
            S.op("act", lambda e, pq=pq: e.activation(out=var[:], in_=pq[:], func=AF.Copy, scale=1.0 / 64), reads=[pq], writes=[var])
            S.op("pool", lambda e: e.tensor_tensor(out=ysq[:], in0=mean[:], in1=mean[:], op=ALU.mult), reads=[mean, ysq], writes=[ysq])
            S.op("pool", lambda e: e.tensor_tensor(out=var[:], in0=var[:], in1=ysq[:], op=ALU.subtract), reads=[var, ysq], writes=[var])
            S.op("act", lambda e: e.activation(out=var[:], in_=var[:], func=AF.Sqrt, bias=GN_EPS, scale=1.0), reads=[var], writes=[var])
            S.op("dve", lambda e: e.reciprocal(out=var[:], in_=var[:]), reads=[var], writes=[var])
            S.op("pool", lambda e: e.tensor_tensor(out=mean[:], in0=ybuf[:], in1=mean[:], op=ALU.subtract), reads=[ybuf, mean], writes=[mean])
            S.op("pool", lambda e: e.tensor_tensor(out=mean[:], in0=mean[:], in1=var[:], op=ALU.mult), reads=[mean, var], writes=[mean])
            S.op("pool", lambda e: e.tensor_tensor(out=mean[:], in0=mean[:], in1=bc4(lg), op=ALU.mult), reads=[mean, lg], writes=[mean])
            S.op("pool", lambda e: e.tensor_tensor(out=mean[:], in0=mean[:], in1=bc4(lb), op=ALU.add), reads=[mean, lb], writes=[mean])
            S.op("pool", lambda e: e.tensor_tensor(out=mean[:], in0=mean[:], in1=bonus[:], op=ALU.add), reads=[mean, bonus], writes=[mean])
            S.op("pool", lambda e: e.tensor_tensor(out=yo[:], in0=mean[:], in1=rzt[:], op=ALU.mult), reads=[mean, rzt], writes=[yo])
            S.dma("sp", scr["ysT"].t[2, :, t0:t0 + TP].rearrange("(c p) t -> p c t", p=128), yo[:], reads=[yo], pwrites=[scr["ysT"]], key=yo)
        _barrier(S)
        S.stack_pop()


_NC_CACHE = {}


def kernel(**inputs):
    SEQ = 8192
    if "nc" not in _NC_CACHE:
        _NC_CACHE["nc"] = build(SEQ, nlayers=2, enable=(1, 1, 1), scr_kind="Internal")
    nc = _NC_CACHE["nc"]
    x = np.ascontiguousarray(np.asarray(inputs["x"], dtype=np.float32))
    p = np.asarray(inputs["p"], dtype=np.float32)
    base = {}
    for k in WSPEC:
        v = np.ascontiguousarray(np.asarray(inputs[k], dtype=np.float32))
        base[k] = v.reshape(WSPEC[k])
    in_maps = []
    for b in range(8):
        m = dict(base)
        m["x"] = np.ascontiguousarray(x[b])
        m["p"] = np.ascontiguousarray(p[:, b])
        in_maps.append(m)
    res = run_bass_kernel_spmd(nc, in_maps, core_ids=list(range(8)))
    return np.stack([np.asarray(r["out"], dtype=np.float32) for r in res.results], axis=0)
```

```python
import contextlib
import numpy as np
import concourse.bass as bass
import concourse.mybir as mybir

F32 = mybir.dt.float32
BF16 = mybir.dt.bfloat16
AF = mybir.ActivationFunctionType
ALU = mybir.AluOpType
AX = mybir.AxisListType

ENGS = ("pe", "act", "dve", "pool", "sp")


class Buf:
    __slots__ = ("name", "w", "wfull", "r", "t")

    def __init__(self, name, t=None):
        self.name = name
        self.t = t
        self.w = []
        self.wfull = []
        self.r = []

    def __getitem__(self, k):
        return self.t[k]


class Op:
    __slots__ = ("eng", "fn", "deps", "marked", "tick", "dma", "idx")

    def __init__(self, eng, fn, dma):
        self.eng = eng
        self.fn = fn
        self.deps = []
        self.marked = False
        self.tick = None
        self.dma = dma
        self.idx = None


class DmaSem:
    def __init__(self):
        self.sem = None
        self.count = 0


class Sched:
    def __init__(self, nc, stack):
        self.nc = nc
        self.stack = stack
        self.ops = {e: [] for e in ENGS}
        self.all_ops = []
        self.dsems = {}
        self.n_sems = 0
        self.fence = []
        self.stacks = [stack]
        self.phase_keys = []
        self.free_ds = []
        self.all_ds = []
        self.keep = []

    def stack_push(self, st):
        self.stacks.append(st)
        self.phase_keys.append([])

    def stack_pop(self):
        self.stacks.pop()
        for kid in self.phase_keys.pop():
            ds = self.dsems.pop(kid, None)
            if ds is not None:
                self.free_ds.append(ds)

    def sb(self, name, shape, dt=F32):
        self.n_sems += 1
        name = "%s_u%d" % (name, self.n_sems)
        t = self.stacks[-1].enter_context(self.nc.sbuf_tensor(name, list(shape), dt))
        return Buf(name, t)

    def ps(self, name, shape, dt=F32):
        self.n_sems += 1
        name = "%s_u%d" % (name, self.n_sems)
        t = self.stacks[-1].enter_context(self.nc.psum_tensor(name, list(shape), dt))
        return Buf(name, t)

    def dram(self, name, shape, dt, kind="Internal"):
        t = self.nc.dram_tensor(name, list(shape), dt, kind=kind)
        return Buf(name, t.ap())

    def _add(self, eng, fn, reads, writes, pwrites, dma):
        op = Op(eng, fn, dma)
        deps = list(self.fence)
        for b in reads:
            deps.extend(b.w)
        for b in writes:
            deps.extend(b.w)
            deps.extend(b.r)
        for b in pwrites:
            deps.extend(b.wfull)
            deps.extend(b.r)
        seen = set()
        for d in deps:
            if id(d) in seen or d is op:
                continue
            seen.add(id(d))
            if d.eng == "pe" and eng == "pe" and d.dma is None and dma is None:
                continue
            op.deps.append(d)
            d.marked = True
        for b in reads:
            b.r.append(op)
            if len(b.r) > 24:
                b.r = self._prune(b.r)
        for b in writes:
            b.w = [op]
            b.wfull = [op]
            b.r = []
        for b in pwrites:
            b.w.append(op)
            if len(b.w) > 24:
                b.w = self._prune(b.w)
        op.idx = len(self.all_ops)
        self.all_ops.append(op)
        self.ops[eng].append(op)
        return op

    @staticmethod
    def _prune(lst):
        last = {}
        for o in lst:
            key = (o.eng, None) if o.dma is None else ("dma", id(o.dma))
            last[key] = o
        return list(last.values())

    def op(self, eng, fn, reads=(), writes=(), pwrites=()):
        return self._add(eng, fn, reads, writes, pwrites, None)

    def dma(self, eng, out_ap, in_ap, reads=(), writes=(), pwrites=(), key=None, **kw):
        if key is None:
            key = (list(writes) + list(pwrites))[0]
        ds = self.dsems.get(id(key))
        if ds is None:
            if self.free_ds:
                ds = self.free_ds.pop()
            else:
                ds = DmaSem()
                self.all_ds.append(ds)
            self.dsems[id(key)] = ds
            self.keep.append(key)
            if self.phase_keys:
                self.phase_keys[-1].append(id(key))
        fn = lambda e, o=out_ap, i=in_ap, kw=kw: e.dma_start(out=o, in_=i, **kw)
        op = self._add(eng, fn, reads, writes, pwrites, ds)
        ds.count += 16
        op.tick = ds.count
        return op

    def barrier_bufs(self, bufs):
        pass

    def emit(self):
        nc = self.nc
        stack = self.stack
        esem = {}
        for e in ENGS:
            esem[e] = stack.enter_context(nc.semaphore("s_" + e))
        for ds in self.all_ds:
            ds.sem = stack.enter_context(nc.semaphore("d%d" % self.n_sems))
            self.n_sems += 1
        for e in ENGS:
            c = 0
            for o in self.ops[e]:
                if o.dma is None:
                    if o.marked:
                        c += 1
                        o.tick = c
        self.max_ticks = {e: max([o.tick or 0 for o in self.ops[e] if o.dma is None] + [0]) for e in ENGS}

        def evkey(d):
            if d.dma is not None:
                return ("d", id(d.dma)), d.dma.sem, d.tick
            return ("e", d.eng), esem[d.eng], d.tick

        def run(eng_name, eng):
            seen = {}
            for o in self.ops[eng_name]:
                waits = {}
                for d in o.deps:
                    k, sem, val = evkey(d)
                    if seen.get(k, 0) >= val:
                        continue
                    if k not in waits or waits[k][1] < val:
                        waits[k] = (sem, val)
                for k, (sem, val) in waits.items():
                    eng.wait_ge(sem, val)
                    seen[k] = val
                inst = o.fn(eng)
                if o.dma is not None:
                    inst.then_inc(o.dma.sem, 16)
                elif o.marked:
                    inst.then_inc(esem[eng_name], 1)
            if eng_name == "sp":
                for e2 in ENGS:
                    m = self.max_ticks[e2]
                    if m > 0:
                        eng.wait_ge(esem[e2], m)
                for ds in self.all_ds:
                    if ds.count:
                        eng.wait_ge(ds.sem, ds.count)

        block = stack.enter_context(nc.Block())

        @block.tensor
        def _(e):
            run("pe", e)

        @block.scalar
        def _(e):
            run("act", e)

        @block.vector
        def _(e):
            run("dve", e)

        @block.gpsimd
        def _(e):
            run("pool", e)

        @block.sync
        def _(e):
            run("sp", e)


from concourse.bass_utils import run_bass_kernel_spmd

D = 1024
NCOL = 8600
PLE = 256
EPS = 1e-6


DEBUG = {}
_dbg_n = [0]


def dbg_dump(S, name, ap, buf, shape, cond=True):
    if not DEBUG.get("on") or not cond:
        return
    _dbg_n[0] += 1
    t = S.stacks[-1].enter_context(S.nc.sbuf_tensor("dbgsb_%d" % _dbg_n[0], list(shape), F32))
    tb = Buf("dbgsb", t)
    d = S.dram("dbg_" + name, list(shape), F32, kind="ExternalOutput")
    S.op("act", lambda e: e.copy(out=t[:], in_=ap), reads=[buf], writes=[tb])
    S.dma("sp", d.t, t[:], reads=[tb], writes=[d], key=tb)


class Ring:
    def __init__(self, bufs):
        self.bufs = bufs
        self.i = 0

    def next(self):
        b = self.bufs[self.i % len(self.bufs)]
        self.i += 1
        return b


def _barrier(S):
    fence = []
    for e in ENGS:
        comp = [o for o in S.ops[e] if o.dma is None]
        if comp:
            fence.append(comp[-1])
    lastd = {}
    for o in S.all_ops:
        if o.dma is not None:
            lastd[id(o.dma)] = o
    fence.extend(lastd.values())
    S.fence = fence


def make_ident(S, name="ident", dt=BF16):
    ident = S.sb(name, [128, 128], dt)
    S.op("pool", lambda e: e.memset(ident[:], 0.0), writes=[ident])
    S.op("pool", lambda e: e.affine_select(out=ident[:], in_=ident[:], pattern=[[-1, 128]],
                                           compare_op=ALU.not_equal, fill=1.0, base=0,
                                           channel_multiplier=1), reads=[ident], writes=[ident])
    return ident


def load_w_bf16(S, dst, k, src_ap, srcbuf):
    S.dma("pool", dst, src_ap, reads=[srcbuf], pwrites=[k], key=k, max_dma_last_dim=4096)


def rmsnorm_tile(S, xt_ap, xt_buf, g_buf, h_ap, h_buf, sq, ss, rs, eps=EPS, extra_reads=()):
    S.op("act", lambda e: e.activation(out=sq[:], in_=xt_ap, func=AF.Square, accum_out=ss[:]),
         reads=[xt_buf] + list(extra_reads), writes=[sq, ss])
    S.op("act", lambda e: e.activation(out=rs[:], in_=ss[:], func=AF.Sqrt, scale=1.0 / D, bias=eps),
         reads=[ss], writes=[rs])
    S.op("dve", lambda e: e.reciprocal(out=rs[:], in_=rs[:]), reads=[rs], writes=[rs])
    S.op("dve", lambda e: e.scalar_tensor_tensor(out=h_ap, in0=xt_ap, scalar=rs[:, 0:1], in1=g_buf[:],
                                                 op0=ALU.mult, op1=ALU.mult),
         reads=[xt_buf, rs, g_buf], pwrites=[h_buf])


def phase_A(S, nc, SEQ, lyr, x_src, Wd, scr):
    TT = 512
    nsub = TT // 128
    with contextlib.ExitStack() as st:
        S.stack_push(st)
        wt = S.sb("A_w", [128, 8, NCOL], BF16)
        gt = S.sb("A_g", [128, D])
        ident = make_ident(S, "A_ident")
        xt = S.sb("A_x", [128, nsub, D])
        sq = S.sb("A_sq", [128, D], BF16)
        ss = S.sb("A_ss", [128, 1])
        rs = S.sb("A_rs", [128, 1])
        h = S.sb("A_h", [128, nsub, D], BF16)
        hT = S.sb("A_hT", [128, 8, TT], BF16)
        stg_b = Ring([S.sb("A_sb%d" % i, [128, 512], BF16) for i in range(4)])
        stg_f = Ring([S.sb("A_sf%d" % i, [128, 512], F32) for i in range(3)])
        pT = Ring([S.ps("A_pT%d" % i, [128, 8, 128], BF16) for i in range(2)])
        pacc = Ring([S.ps("A_pa%d" % i, [128, 512], F32) for i in range(6)])

        w_in = Wd["w_in"]
        WG = [(0, 1280), (3352, 5528), (5528, 7064), (7064, 8600), (1280, 3352)]
        wtg = [Buf("A_wg%d" % i, wt.t) for i in range(len(WG))]

        def wgrp(c0):
            for i, (lo, hi) in enumerate(WG):
                if lo <= c0 < hi:
                    return wtg[i]
            raise ValueError(c0)
        for gi, (lo, hi) in enumerate(WG):
            for k in range(8):
                S.dma("pool", wt[:, k, lo:hi], w_in.t[lyr, k * 128:(k + 1) * 128, lo:hi], reads=[w_in], pwrites=[wtg[gi]],
                      key=wtg[gi], max_dma_last_dim=4096)
        S.dma("sp", gt[:], Wd["norm_g"].t[lyr:lyr + 1, :].partition_broadcast(128), reads=[Wd["norm_g"]],
              writes=[gt])

        FM = []
        for c in range(4):
            FM.append((c * 128, scr["qT"], c * 128, AF.Copy, 0.125, BF16))
        FM.append((512, scr["kcT"], 0, None, 1.0, BF16))
        FM.append((640, scr["vcT"], 0, None, 1.0, BF16))
        FM.append((768, scr["ksT"], 0, None, 1.0, BF16))
        FM.append((1024, scr["kwT"], 0, None, 1.0, BF16))
        for c in range(13):
            FM.append((3352 + c * 128, scr["rsT"], c * 128, None, 1.0, F32))
        for c in range(4):
            FM.append((5016 + c * 128, scr["rzT"], c * 128, AF.Silu, 1.0, BF16))
        for c in range(24):
            FM.append((5528 + c * 128, scr["mgT"], c * 128, AF.Sigmoid, 1.0, BF16))
        TM = [
            (896, 128, scr["vsw"], 0, None, BF16),
            (1152, 128, scr["vsw"], 128, None, BF16),
            (1280, 24, scr["gate"], 0, AF.Sigmoid, F32),
            (1304, 512, scr["nzs"], 0, AF.Silu, BF16),
            (1816, 512, scr["su"], 0, None, F32),
            (2328, 512, scr["sv"], 0, None, F32),
            (2840, 512, scr["szs"], 0, AF.Silu, BF16),
        ]
        evac_i = [0]

        def evac(out_ap, out_buf, in_ap, in_buf, func, scale):
            if func is None and scale == 1.0:
                if evac_i[0] % 2 == 0:
                    S.op("dve", lambda e: e.tensor_copy(out=out_ap, in_=in_ap), reads=[in_buf], writes=[out_buf])
                else:
                    S.op("act", lambda e: e.copy(out=out_ap, in_=in_ap), reads=[in_buf], writes=[out_buf])
                evac_i[0] += 1
            else:
                S.op("act", lambda e: e.activation(out=out_ap, in_=in_ap, func=func, scale=scale),
                     reads=[in_buf], writes=[out_buf])

        def load_x(ti):
            S.dma("sp", xt[:], x_src.t[ti * TT:(ti + 1) * TT, :].rearrange("(s p) d -> p s d", p=128), reads=[x_src],
                  writes=[xt])
        load_x(0)
        for ti in range(SEQ // TT):
            t0 = ti * TT
            for s in range(nsub):
                rmsnorm_tile(S, xt[:, s, :], xt, gt, h[:, s, :], h, sq, ss, rs)
                pt = pT.next()
                for k in range(8):
                    S.op("pe", lambda e, k=k, s=s, pt=pt: e.transpose(out=pt[:, k, :], in_=h[:, s, k * 128:(k + 1) * 128],
                                                                     identity=ident[:]),
                         reads=[h, ident], writes=[pt] if k == 0 else (), pwrites=() if k == 0 else [pt])
                S.op("dve", lambda e, s=s, pt=pt: e.tensor_copy(out=hT[:, :, s * 128:(s + 1) * 128], in_=pt[:]),
                     reads=[pt], pwrites=[hT])
            if ti + 1 < SEQ // TT:
                load_x(ti + 1)
            for (c0, dbuf, r0, func, scale, dt) in FM:
                pa = pacc.next()
                for k in range(8):
                    S.op("pe", lambda e, k=k, pa=pa, c0=c0: e.matmul(pa[:], lhsT=wt[:, k, c0:c0 + 128], rhs=hT[:, k, :],
                                                                    start=(k == 0), stop=(k == 7)),
                         reads=[wgrp(c0), hT], writes=[pa] if k == 0 else (), pwrites=() if k == 0 else [pa])
                sg = stg_b.next() if dt == BF16 else stg_f.next()
                evac(sg[:], sg, pa[:], pa, func, scale)
                S.dma("sp", dbuf.t[r0:r0 + 128, t0:t0 + TT], sg[:], reads=[sg], pwrites=[dbuf], key=sg)
            for s in range(nsub):
                for (c0, ncol, dbuf, dc0, func, dt) in TM:
                    pa = pacc.next()
                    for k in range(8):
                        S.op("pe", lambda e, k=k, pa=pa, c0=c0, ncol=ncol, s=s: e.matmul(
                            pa[:, 0:ncol], lhsT=hT[:, k, s * 128:(s + 1) * 128], rhs=wt[:, k, c0:c0 + ncol],
                            start=(k == 0), stop=(k == 7)),
                            reads=[wgrp(c0), hT], writes=[pa] if k == 0 else (), pwrites=() if k == 0 else [pa])
                    sg = stg_b.next() if dt == BF16 else stg_f.next()
                    evac(sg[:, 0:ncol], sg, pa[:, 0:ncol], pa, func, 1.0)
                    S.dma("sp", dbuf.t[t0 + s * 128:t0 + (s + 1) * 128, dc0:dc0 + ncol], sg[:, 0:ncol], reads=[sg],
                          pwrites=[dbuf], key=sg)
        _barrier(S)
        S.stack_pop()


def make_scratch(S, SEQ, kind="Internal"):
    scr = {}
    def mk(name, shape, dt):
        scr[name] = S.dram(name, shape, dt, kind=kind)
    mk("qT", [512, SEQ], BF16)
    mk("kcT", [128, SEQ], BF16)
    mk("vcT", [128, SEQ], BF16)
    mk("ksT", [128, SEQ], BF16)
    mk("kwT", [128, SEQ], BF16)
    mk("vsw", [SEQ, 256], BF16)
    mk("gate", [SEQ, 24], F32)
    mk("nzs", [SEQ, 512], BF16)
    mk("su", [SEQ, 512], F32)
    mk("sv", [SEQ, 512], F32)
    mk("szs", [SEQ, 512], BF16)
    mk("rsT", [1664, SEQ], F32)
    mk("rzT", [512, SEQ], BF16)
    mk("mgT", [3072, SEQ], BF16)
    mk("ysT", [3, 512, SEQ], BF16)
    mk("xtok", [SEQ, 5, 2, 256], F32)
    return scr


def phase_C(S, nc, SEQ, lyr, Wd, scr):
    LN_EPS = 1e-5
    with contextlib.ExitStack() as st:
        S.stack_push(st)
        ident = make_ident(S, "C_ident")
        wraw = S.sb("C_wraw", [128, 8, 128])
        wbf = S.sb("C_wbf", [128, 8, 128], BF16)
        WT = S.sb("C_WT", [128, 8, 128], BF16)
        bsT = S.sb("C_bsT", [128, 8])
        lng = S.sb("C_lng", [128, 512])
        lnb = S.sb("C_lnb", [128, 512])
        pw = S.ps("C_pw", [128, 8, 128], BF16)
        S.dma("sp", wraw[:], Wd["sg_w"].t[lyr].rearrange("g t s -> t g s"), reads=[Wd["sg_w"]], writes=[wraw])
        S.dma("sp", bsT[:], Wd["sg_b"].t[lyr].rearrange("g t -> t g"), reads=[Wd["sg_b"]], writes=[bsT],
              allow_slow_non_contiguous=True)
        S.dma("sp", lng[:], Wd["sg_ln_g"].t[lyr:lyr + 1, :].partition_broadcast(128), reads=[Wd["sg_ln_g"]], writes=[lng])
        S.dma("sp", lnb[:], Wd["sg_ln_b"].t[lyr:lyr + 1, :].partition_broadcast(128), reads=[Wd["sg_ln_b"]], writes=[lnb])
        S.op("pool", lambda e: e.affine_select(out=wraw[:], in_=wraw[:], pattern=[[0, 8], [-1, 128]],
                                               compare_op=ALU.is_ge, fill=0.0, base=0, channel_multiplier=1),
             reads=[wraw], writes=[wraw])
        S.op("dve", lambda e: e.tensor_copy(out=wbf[:], in_=wraw[:]), reads=[wraw], writes=[wbf])
        for g in range(8):
            S.op("pe", lambda e, g=g: e.transpose(out=pw[:, g, :], in_=wbf[:, g, :], identity=ident[:]),
                 reads=[wbf, ident], pwrites=[pw])
        S.op("dve", lambda e: e.tensor_copy(out=WT[:], in_=pw[:]), reads=[pw], writes=[WT])

        NB = 2
        svt = Ring([S.sb("C_sv%d" % i, [128, 512]) for i in range(NB)])
        sut = Ring([S.sb("C_su%d" % i, [128, 512]) for i in range(NB)])
        szt = Ring([S.sb("C_sz%d" % i, [128, 512], BF16) for i in range(NB)])
        stats = S.sb("C_stats", [128, 6])
        mv = S.sb("C_mv", [128, 2])
        rstd = S.sb("C_rstd", [128, 1])
        vn0 = S.sb("C_vnf", [128, 512])
        vn = Ring([S.sb("C_vn%d" % i, [128, 512], BF16) for i in range(2)])
        y0 = S.sb("C_y0", [128, 512])
        yb = Ring([S.sb("C_yb%d" % i, [128, 512], BF16) for i in range(2)])
        pm = Ring([S.ps("C_pm%d" % i, [128, 512]) for i in range(2)])
        pt = Ring([S.ps("C_pt%d" % i, [128, 4, 128], BF16) for i in range(2)])
        stg = Ring([S.sb("C_stg%d" % i, [128, 4, 512], BF16) for i in range(2)])
        ys = scr["ysT"]
        sgb = None
        for c in range(SEQ // 128):
            t0 = c * 128
            v = svt.next(); u = sut.next(); z = szt.next()
            S.dma("sp", v[:], scr["sv"].t[t0:t0 + 128, :], reads=[scr["sv"]], writes=[v])
            S.dma("sp", u[:], scr["su"].t[t0:t0 + 128, :], reads=[scr["su"]], writes=[u])
            S.dma("sp", z[:], scr["szs"].t[t0:t0 + 128, :], reads=[scr["szs"]], writes=[z])
            S.op("dve", lambda e, v=v: e.bn_stats(out=stats[:], in_=v[:]), reads=[v], writes=[stats])
            S.op("dve", lambda e: e.bn_aggr(out=mv[:], in_=stats[:]), reads=[stats], writes=[mv])
            S.op("act", lambda e: e.activation(out=rstd[:], in_=mv[:, 1:2], func=AF.Sqrt, bias=LN_EPS, scale=1.0),
                 reads=[mv], writes=[rstd])
            S.op("dve", lambda e: e.reciprocal(out=rstd[:], in_=rstd[:]), reads=[rstd], writes=[rstd])
            S.op("dve", lambda e, v=v: e.tensor_scalar(out=vn0[:], in0=v[:], scalar1=mv[:, 0:1], scalar2=rstd[:, 0:1],
                                                       op0=ALU.subtract, op1=ALU.mult),
                 reads=[v, mv, rstd], writes=[vn0])
            S.op("pool", lambda e: e.tensor_tensor(out=vn0[:], in0=vn0[:], in1=lng[:], op=ALU.mult),
                 reads=[vn0, lng], writes=[vn0])
            vb = vn.next()
            S.op("pool", lambda e, vb=vb: e.tensor_tensor(out=vb[:], in0=vn0[:], in1=lnb[:], op=ALU.add),
                 reads=[vn0, lnb], writes=[vb])
            pmm = pm.next()
            for g in range(8):
                S.op("pe", lambda e, g=g, vb=vb, pmm=pmm: e.matmul(pmm[:, g * 64:(g + 1) * 64], lhsT=WT[:, g, :],
                                                                   rhs=vb[:, g * 64:(g + 1) * 64], start=True, stop=True),
                     reads=[WT, vb], writes=[pmm] if g == 0 else (), pwrites=() if g == 0 else [pmm])
            S.op("dve", lambda e, pmm=pmm: e.tensor_tensor(
                out=y0[:].rearrange("p (g d) -> p g d", g=8), in0=pmm[:].rearrange("p (g d) -> p g d", g=8),
                in1=bsT[:].unsqueeze(2).to_broadcast([128, 8, 64]), op=ALU.add), reads=[pmm, bsT], writes=[y0])
            S.op("pool", lambda e, u=u: e.tensor_tensor(out=y0[:], in0=y0[:], in1=u[:], op=ALU.mult),
                 reads=[y0, u], writes=[y0])
            y = yb.next()
            S.op("dve", lambda e, y=y, z=z: e.tensor_tensor(out=y[:], in0=y0[:], in1=z[:], op=ALU.mult),
                 reads=[y0, z], writes=[y])
            ptt = pt.next()
            for k in range(4):
                S.op("pe", lambda e, k=k, y=y, ptt=ptt: e.transpose(out=ptt[:, k, :], in_=y[:, k * 128:(k + 1) * 128],
                                                                    identity=ident[:]),
                     reads=[y, ident], writes=[ptt] if k == 0 else (), pwrites=() if k == 0 else [ptt])
            if c % 4 == 0:
                sgb = stg.next()
            cc = c % 4
            S.op("act", lambda e, ptt=ptt, sgb=sgb, cc=cc: e.copy(out=sgb[:, :, cc * 128:(cc + 1) * 128], in_=ptt[:]),
                 reads=[ptt], writes=[sgb] if cc == 0 else (), pwrites=() if cc == 0 else [sgb])
            if cc == 3 or c == SEQ // 128 - 1:
                tb = (c // 4) * 512
                n = (cc + 1) * 128
                S.dma("sp", ys.t[1, :, tb:tb + n].rearrange("(k p) t -> p k t", p=128), sgb[:, :, 0:n], reads=[sgb],
                      pwrites=[ys], key=sgb)
        _barrier(S)
        S.stack_pop()


def phase_E(S, nc, SEQ, lyr, x_src, x_dst, Wd, scr, final):
    TT = 512
    nsub = 4
    with contextlib.ExitStack() as st:
        S.stack_push(st)
        ident = make_ident(S, "E_ident")
        wb = S.sb("E_wb", [128, 3, 4, D], BF16)
        wo = S.sb("E_wo", [128, 8, D], BF16)
        wpg = S.sb("E_wpg", [128, 8, D], BF16)
        wpp = S.sb("E_wpp", [128, 2, D], BF16)
        gpl = S.sb("E_gpl", [128, D])
        gfin = S.sb("E_gfin", [128, D])
        for n in range(3):
            S.dma("pool", wb[:, n, :, :], Wd["w_branch"].t[lyr, n].rearrange("(k p) d -> p k d", p=128),
                  reads=[Wd["w_branch"]], pwrites=[wb], key=wb, max_dma_last_dim=4096)
        for k0 in range(0, 8, 4):
            S.dma("pool", wo[:, k0:k0 + 4, :], Wd["w_o"].t[lyr, k0 * 128:(k0 + 4) * 128, :].rearrange("(k p) d -> p k d", p=128),
                  reads=[Wd["w_o"]], pwrites=[wo], key=wo, max_dma_last_dim=4096)
            S.dma("pool", wpg[:, k0:k0 + 4, :], Wd["w_ple_gate"].t[lyr, k0 * 128:(k0 + 4) * 128, :].rearrange("(k p) d -> p k d", p=128),
                  reads=[Wd["w_ple_gate"]], pwrites=[wpg], key=wpg, max_dma_last_dim=4096)
        S.dma("pool", wpp[:], Wd["w_ple_proj"].t[lyr].rearrange("(k p) d -> p k d", p=128),
              reads=[Wd["w_ple_proj"]], pwrites=[wpp], key=wpp, max_dma_last_dim=4096)
        S.dma("sp", gpl[:], Wd["ple_norm_g"].t[lyr:lyr + 1, :].partition_broadcast(128), reads=[Wd["ple_norm_g"]], writes=[gpl])
        if final:
            S.dma("sp", gfin[:], Wd["final_norm_g"].t[0:1, :].partition_broadcast(128), reads=[Wd["final_norm_g"]], writes=[gfin])

        yst = S.sb("E_ys", [128, 3, 4, TT], BF16)
        mgt = S.sb("E_mg", [128, 24, TT], BF16)
        mrg = S.sb("E_mrg", [128, 8, TT])
        mrb = S.sb("E_mrb", [128, 8, TT], BF16)
        tmp = Ring([S.sb("E_tmp%d" % i, [128, TT]) for i in range(2)])
        xt = S.sb("E_x", [128, nsub, D])
        pin = S.sb("E_p", [128, nsub, PLE])
        pbf = S.sb("E_pbf", [128, PLE], BF16)
        pTs = S.sb("E_pT", [128, 2, 128], BF16)
        sq = S.sb("E_sq", [128, D], BF16)
        ss = S.sb("E_ss", [128, 1])
        rs = S.sb("E_rs", [128, 1])
        hp = S.sb("E_hp", [128, D], BF16)
        hpT = S.sb("E_hpT", [128, 8, 128], BF16)
        gate = S.sb("E_gate", [128, D])
        xo = Ring([S.sb("E_xo%d" % i, [128, D]) for i in range(2)])
        pz = Ring([S.ps("E_pz%d" % i, [128, TT]) for i in range(3)])
        po = Ring([S.ps("E_po%d" % i, [128, 512]) for i in range(2)])
        pg = Ring([S.ps("E_pg%d" % i, [128, 512]) for i in range(2)])
        ptr = S.ps("E_ptr", [128, 8, 128], BF16)

        def load_ym(ti):
            t0 = ti * TT
            for n in range(3):
                S.dma("sp", yst[:, n, :, :], scr["ysT"].t[n, :, t0:t0 + TT].rearrange("(k p) t -> p k t", p=128),
                      reads=[scr["ysT"]], writes=[yst] if n == 0 else (), pwrites=() if n == 0 else [yst], key=yst)
            for k0 in range(0, 24, 8):
                S.dma("sp", mgt[:, k0:k0 + 8, :], scr["mgT"].t[k0 * 128:(k0 + 8) * 128, t0:t0 + TT].rearrange("(k p) t -> p k t", p=128),
                      reads=[scr["mgT"]], writes=[mgt] if k0 == 0 else (), pwrites=() if k0 == 0 else [mgt], key=mgt)
        load_ym(0)
        for ti in range(SEQ // TT):
            t0 = ti * TT
            S.dma("sp", xt[:], x_src.t[t0:t0 + TT, :].rearrange("(s p) d -> p s d", p=128), reads=[x_src], writes=[xt])
            S.dma("sp", pin[:], Wd["p"].t[lyr, t0:t0 + TT, :].rearrange("(s p) d -> p s d", p=128), reads=[Wd["p"]], writes=[pin])
            for dc in range(8):
                pzs = []
                for n in range(3):
                    pzz = pz.next()
                    pzs.append(pzz)
                    for k in range(4):
                        S.op("pe", lambda e, n=n, k=k, dc=dc, pzz=pzz: e.matmul(
                            pzz[:], lhsT=wb[:, n, k, dc * 128:(dc + 1) * 128], rhs=yst[:, n, k, :], start=(k == 0), stop=(k == 3)),
                            reads=[wb, yst], writes=[pzz] if k == 0 else (), pwrites=() if k == 0 else [pzz])
                S.op("dve", lambda e, dc=dc, p0=pzs[0]: e.tensor_tensor(out=mrg[:, dc, :], in0=p0[:], in1=mgt[:, dc, :], op=ALU.mult),
                     reads=[pzs[0], mgt], pwrites=[mrg])
                t1 = tmp.next()
                S.op("dve", lambda e, dc=dc, p1=pzs[1], t1=t1: e.tensor_tensor(out=t1[:], in0=p1[:], in1=mgt[:, 8 + dc, :], op=ALU.mult),
                     reads=[pzs[1], mgt], writes=[t1])
                t2 = tmp.next()
                S.op("dve", lambda e, dc=dc, p2=pzs[2], t2=t2: e.tensor_tensor(out=t2[:], in0=p2[:], in1=mgt[:, 16 + dc, :], op=ALU.mult),
                     reads=[pzs[2], mgt], writes=[t2])
                S.op("pool", lambda e, dc=dc, t1=t1: e.tensor_tensor(out=mrg[:, dc, :], in0=mrg[:, dc, :], in1=t1[:], op=ALU.add),
                     reads=[mrg, t1], pwrites=[mrg])
                S.op("pool", lambda e, dc=dc, t2=t2: e.tensor_tensor(out=mrb[:, dc, :], in0=mrg[:, dc, :], in1=t2[:], op=ALU.add),
                     reads=[mrg, t2], pwrites=[mrb])
            if ti + 1 < SEQ // TT:
                load_ym(ti + 1)
            for s in range(nsub):
                for blk in range(2):
                    pp = po.next()
                    for k in range(8):
                        S.op("pe", lambda e, k=k, s=s, blk=blk, pp=pp: e.matmul(
                            pp[:], lhsT=mrb[:, k, s * 128:(s + 1) * 128], rhs=wo[:, k, blk * 512:(blk + 1) * 512],
                            start=(k == 0), stop=(k == 7)),
                            reads=[mrb, wo], writes=[pp] if k == 0 else (), pwrites=() if k == 0 else [pp])
                    S.op("dve", lambda e, s=s, blk=blk, pp=pp: e.tensor_tensor(
                        out=xt[:, s, blk * 512:(blk + 1) * 512], in0=pp[:], in1=xt[:, s, blk * 512:(blk + 1) * 512], op=ALU.add),
                        reads=[pp, xt], pwrites=[xt])
                rmsnorm_tile(S, xt[:, s, :], xt, gpl, hp[:], hp, sq, ss, rs)
                for k in range(8):
                    S.op("pe", lambda e, k=k: e.transpose(out=ptr[:, k, :], in_=hp[:, k * 128:(k + 1) * 128], identity=ident[:]),
                         reads=[hp, ident], writes=[ptr] if k == 0 else (), pwrites=() if k == 0 else [ptr])
                S.op("act", lambda e: e.copy(out=hpT[:], in_=ptr[:]), reads=[ptr], writes=[hpT])
                S.op("pool", lambda e, s=s: e.tensor_copy(out=pbf[:], in_=pin[:, s, :]), reads=[pin], writes=[pbf])
                for k in range(2):
                    S.op("pe", lambda e, k=k: e.transpose(out=ptr[:, k, :], in_=pbf[:, k * 128:(k + 1) * 128], identity=ident[:]),
                         reads=[pbf, ident, hpT], writes=[ptr] if k == 0 else (), pwrites=() if k == 0 else [ptr])
                S.op("act", lambda e: e.copy(out=pTs[:], in_=ptr[:, 0:2, :]), reads=[ptr], writes=[pTs])
                xout = xo.next()
                for blk in range(2):
                    pgg = pg.next()
                    for k in range(8):
                        S.op("pe", lambda e, k=k, blk=blk, pgg=pgg: e.matmul(
                            pgg[:], lhsT=hpT[:, k, :], rhs=wpg[:, k, blk * 512:(blk + 1) * 512], start=(k == 0), stop=(k == 7)),
                            reads=[hpT, wpg], writes=[pgg] if k == 0 else (), pwrites=() if k == 0 else [pgg])
                    S.op("act", lambda e, blk=blk, pgg=pgg: e.activation(out=gate[:, blk * 512:(blk + 1) * 512], in_=pgg[:], func=AF.Sigmoid),
                         reads=[pgg], pwrites=[gate])
                    ppp = pg.next()
                    for k in range(2):
                        S.op("pe", lambda e, k=k, blk=blk, ppp=ppp: e.matmul(
                            ppp[:], lhsT=pTs[:, k, :], rhs=wpp[:, k, blk * 512:(blk + 1) * 512], start=(k == 0), stop=(k == 1)),
                            reads=[pTs, wpp], writes=[ppp] if k == 0 else (), pwrites=() if k == 0 else [ppp])
                    S.op("dve", lambda e, blk=blk, ppp=ppp: e.tensor_tensor(
                        out=gate[:, blk * 512:(blk + 1) * 512], in0=ppp[:], in1=gate[:, blk * 512:(blk + 1) * 512], op=ALU.mult),
                        reads=[ppp, gate], pwrites=[gate])
                    S.op("pool", lambda e, blk=blk, s=s, xout=xout: e.tensor_tensor(
                        out=xout[:, blk * 512:(blk + 1) * 512], in0=gate[:, blk * 512:(blk + 1) * 512],
                        in1=xt[:, s, blk * 512:(blk + 1) * 512], op=ALU.add),
                        reads=[gate, xt], writes=[xout] if blk == 0 else (), pwrites=() if blk == 0 else [xout])
                if final:
                    S.op("act", lambda e, xout=xout: e.activation(out=sq[:], in_=xout[:], func=AF.Square, accum_out=ss[:]),
                         reads=[xout], writes=[sq, ss])
                    S.op("act", lambda e: e.activation(out=rs[:], in_=ss[:], func=AF.Sqrt, scale=1.0 / D, bias=EPS),
                         reads=[ss], writes=[rs])
                    S.op("dve", lambda e: e.reciprocal(out=rs[:], in_=rs[:]), reads=[rs], writes=[rs])
                    S.op("dve", lambda e, xout=xout: e.scalar_tensor_tensor(out=xout[:], in0=xout[:], scalar=rs[:, 0:1], in1=gfin[:],
                                                                            op0=ALU.mult, op1=ALU.mult),
                         reads=[xout, rs, gfin], writes=[xout])
                S.dma("sp", x_dst.t[t0 + s * 128:t0 + (s + 1) * 128, :], xout[:], reads=[xout], pwrites=[x_dst], key=xout)
        _barrier(S)
        S.stack_pop()


def phase_D(S, nc, SEQ, lyr, Wd, scr):
    TP = 128
    C = 16
    NCH = TP // C
    GN_EPS = 64e-5
    LD = 0.6065306597126334
    with contextlib.ExitStack() as st:
        S.stack_push(st)
        identB = make_ident(S, "D_identB", BF16)
        ones = S.sb("D_ones", [128, 128])
        S.op("pool", lambda e: e.memset(ones[:], 0.0), writes=[ones])
        S.op("pool", lambda e: e.memset(ones[0:64, 0:64], 1.0), reads=[ones], writes=[ones])
        S.op("pool", lambda e: e.memset(ones[64:128, 64:128], 1.0), reads=[ones], writes=[ones])
        Ff = S.sb("D_F", [128, 64], BF16)
        S.op("pool", lambda e: e.tensor_tensor(out=Ff[:], in0=identB[:, 0:64], in1=identB[:, 64:128], op=ALU.add), reads=[identB], writes=[Ff])
        Sel = S.sb("D_Sel", [128, 16], BF16)
        S.op("pool", lambda e: e.tensor_tensor(out=Sel[:], in0=identB[:, 0:16], in1=identB[:, 16:32], op=ALU.add), reads=[identB], writes=[Sel])
        for hh in range(2, 8):
            S.op("pool", lambda e, hh=hh: e.tensor_tensor(out=Sel[:], in0=Sel[:], in1=identB[:, hh * 16:(hh + 1) * 16], op=ALU.add),
                 reads=[identB, Sel], writes=[Sel])
        maskF = S.sb("D_maskF", [128, 4, 8], BF16)
        S.op("pool", lambda e: e.memset(maskF[:], 0.0), writes=[maskF])
        for p in range(4):
            for h2 in range(2):
                S.op("pool", lambda e, p=p, h2=h2: e.memset(maskF[h2 * 64:(h2 + 1) * 64, p, 2 * p + h2:2 * p + h2 + 1], 1.0), reads=[maskF], writes=[maskF])
        maskZ = S.sb("D_maskZ", [128, 4, 2], BF16)
        S.op("pool", lambda e: e.memset(maskZ[:], 1.0), writes=[maskZ])
        S.op("pool", lambda e: e.affine_select(out=maskZ[:], in_=maskZ[:], pattern=[[-32, 4], [-16, 2]], compare_op=ALU.is_ge, fill=0.0,
                                               base=0, channel_multiplier=1), reads=[maskZ], writes=[maskZ])
        S.op("pool", lambda e: e.affine_select(out=maskZ[:], in_=maskZ[:], pattern=[[32, 4], [16, 2]], compare_op=ALU.is_ge, fill=0.0,
                                               base=15, channel_multiplier=-1), reads=[maskZ], writes=[maskZ])

        def trimask(name, pat, cm, op):
            m = S.sb(name, [128, 128], BF16)
            S.op("pool", lambda e: e.memset(m[:], 1.0), writes=[m])
            S.op("pool", lambda e: e.affine_select(out=m[:], in_=m[:], pattern=pat, compare_op=op, fill=0.0, base=0, channel_multiplier=cm),
                 reads=[m], writes=[m])
            return m
        mSL = trimask("D_mSL", [[-16, 8], [-1, 16]], 1, ALU.is_gt)
        mSU = trimask("D_mSU", [[16, 8], [1, 16]], -1, ALU.is_gt)
        mUI = trimask("D_mUI", [[16, 8], [1, 16]], -1, ALU.is_ge)
        rm = S.sb("D_rm", [128, 512])
        S.op("pool", lambda e: e.memset(rm[:], 1.0), writes=[rm])
        S.op("pool", lambda e: e.memset(rm[:, 0:512:16], 0.0), reads=[rm], writes=[rm])

        def cvec(name, key, n):
            t = S.sb("D_" + name, [128, n])
            S.dma("sp", t[:], Wd[key].t[lyr].rearrange("(c p) -> p c", p=128), reads=[Wd[key]], writes=[t],
                  allow_slow_non_contiguous=True)
            return t

        def cvec2(name, key):
            t = S.sb("D_" + name, [128, 4])
            S.dma("sp", t[:], Wd[key].t[lyr].rearrange("(c a) j -> (a j) c", a=2), reads=[Wd[key]], writes=[t],
                  allow_slow_non_contiguous=True)
            return t
        mu = cvec("mu", "rk_mu", 13)
        w0 = cvec("w0", "rk_w0", 4)
        a0 = cvec("a0", "rk_a0", 4)
        lg = cvec("lg", "rk_lnx_g", 4)
        lb = cvec("lb", "rk_lnx_b", 4)
        kkc = cvec2("kkc", "rk_kk")
        ka = cvec2("ka", "rk_ka")
        rkc = cvec2("rkc", "rk_rk")
        omka = S.sb("D_omka", [128, 4])
        S.op("pool", lambda e: e.tensor_scalar(out=omka[:], in0=ka[:], scalar1=-1.0, scalar2=1.0, op0=ALU.mult, op1=ALU.add),
             reads=[ka], writes=[omka])
        w2 = S.sb("D_w2", [64, 512], BF16)
        a2 = S.sb("D_a2", [128, 512], BF16)
        S.dma("pool", w2[:], Wd["rk_w2"].t[lyr], reads=[Wd["rk_w2"]], writes=[w2])
        S.dma("pool", a2[64:128, :], Wd["rk_a2"].t[lyr], reads=[Wd["rk_a2"]], writes=[a2])

        Hm = S.sb("D_H", [128, 4, 64])
        Hn = S.sb("D_Hn", [128, 4, 64])
        Hbf = S.sb("D_Hbf", [128, 4, 64], BF16)
        S.op("pool", lambda e: e.memset(Hm[:], 0.0), writes=[Hm])
        S.op("pool", lambda e: e.memset(Hbf[:], 0.0), writes=[Hbf])

        rst = S.sb("D_rst", [128, 13, TP + 1])
        xs = S.sb("D_xs", [128, 13, TP])
        th = S.sb("D_th", [128, TP], BF16)
        sg = S.sb("D_sg", [128, 4, TP])
        cum = S.sb("D_cum", [128, 4, TP])
        E1 = S.sb("D_E1", [128, 4, TP])
        E2 = S.sb("D_E2", [128, 4, TP])
        E3 = S.sb("D_E3", [128, 4, TP])
        aa = S.sb("D_aa", [128, 4, TP])
        kkf = S.sb("D_kkf", [128, 4, TP])
        sq = S.sb("D_sq", [128, 4, TP])
        rn = S.sb("D_rn", [128, 4, TP])
        kp = S.sb("D_kp", [128, 4, TP])
        t1 = S.sb("D_t1", [128, 4, TP])
        t2 = S.sb("D_t2", [128, 4, TP])
        comp = [S.sb("D_cmp%d" % i, [128, 4, TP], BF16) for i in range(5)]
        ZXr = Ring([[S.sb("D_Z%d_%d" % (b, i), [128, NCH, 4, 128], BF16) for i in range(5)] for b in range(2)])
        DcR = Ring([S.sb("D_Dc%d" % i, [128, NCH, 4]) for i in range(2)])
        bonR = Ring([S.sb("D_bon%d" % i, [128, 4, TP]) for i in range(2)])
        rzR = Ring([S.sb("D_rz%d" % i, [128, 4, TP], BF16) for i in range(2)])
        ybR = Ring([S.sb("D_yb%d" % i, [128, 4, TP]) for i in range(2)])
        yo = S.sb("D_yo", [128, 4, TP], BF16)
        ppre = S.ps("D_ppre", [128, 4, TP])
        R4 = lambda nm, shp, dt=BF16: Ring([S.sb("D_%s%d" % (nm, i), shp, dt) for i in range(4)])
        WyZr = R4("WyZ", [128, 4, 128]); WhTr = R4("WhT", [128, 4, 128]); BtZr = R4("BtZ", [128, 4, 128]); KtZr = R4("KtZ", [128, 4, 128])
        U0r = R4("U0", [128, 64]); Vtr = R4("Vt", [128, 64]); PTr = R4("PT", [128, 128]); QTr = R4("QT", [128, 128])
        ysbR = Ring([S.sb("D_ysb%d" % i, [128, 64]) for i in range(5)])

        class Reg:
            def __init__(self, bank, ap):
                self.bank = bank
                self.t = ap

        class Lane:
            pass
        lanes = []
        for li in range(2):
            L = Lane()
            L.Gr = Ring([S.sb("D_G%d_%d" % (li, i), [128, 128], BF16) for i in range(2)])
            L.Nr = Ring([S.sb("D_N%d_%d" % (li, i), [128, 128], BF16) for i in range(2)])
            L.NTr = Ring([S.sb("D_NT%d_%d" % (li, i), [128, 128], BF16) for i in range(2)])
            L.MTs = S.sb("D_MTs%d" % li, [128, 128], BF16)
            L.X1Z = S.sb("D_X1Z%d" % li, [128, 4, 128], BF16)
            L.X1s = S.sb("D_X1s%d" % li, [128, 64], BF16)
            L.tks = S.sb("D_tks%d" % li, [128, 2, 64], BF16)
            ba = S.ps("D_ba%d" % li, [128, 512])
            bb = ppre if li == 0 else S.ps("D_bb%d" % li, [128, 4, 128])
            bg = S.ps("D_bg%d" % li, [128, 3, 128])
            L.tokc = Reg(ba, ba.t[:, 0:256].rearrange("q (o j) -> q o j", o=4))
            L.QTp = Reg(ba, ba.t[:, 256:384])
            L.mvp = Reg(ba, ba.t[:, 384:448])
            L.sc = [Reg(bb, bb.t[:, i, :]) for i in range(4)]
            L.bb = bb
            L.bg = bg
            lanes.append(L)
        bs = S.ps("D_bs", [128, 512])
        bt_ = S.ps("D_bt", [128, 512])
        WHp = Reg(bs, bs.t[:, 0:256].rearrange("q (p i) -> q p i", p=4))
        Yp = Reg(bs, bs.t[:, 256:320])
        yfp = Reg(bt_, bt_.t[:, 0:64].rearrange("q (p t) -> q p t", p=4))
        yn = S.sb("D_yn", [128, 64])
        YZ = S.sb("D_YZ", [128, 4, 128], BF16)
        stats = S.sb("D_stats", [128, 6])
        mv = S.sb("D_mv", [128, 2])
        rstd = S.sb("D_rstd", [128, 1])

        bc4 = lambda t: t[:].unsqueeze(2).to_broadcast([128, 4, TP])

        def mm(out_ap, obuf, lhsT, lbuf, rhs, rbuf, start, stop=True, first_write=False):
            obuf = getattr(obuf, "bank", obuf)
            S.op("pe", lambda e: e.matmul(out_ap, lhsT=lhsT, rhs=rhs, start=start, stop=stop, skip_group_check=True),
                 reads=[lbuf, rbuf], writes=[obuf] if first_write else (), pwrites=() if first_write else [obuf])

        def prep(nb):
            t0 = nb * TP
            S.dma("sp", rst[:, :, 1:TP + 1], scr["rsT"].t[:, t0:t0 + TP].rearrange("(c p) t -> p c t", p=128), reads=[scr["rsT"]], writes=[rst])
            if nb == 0:
                S.op("pool", lambda e: e.memset(rst[:, :, 0:1], 0.0), reads=[rst], pwrites=[rst])
            else:
                S.dma("sp", rst[:, :, 0:1], scr["rsT"].t[:, t0 - 1:t0].rearrange("(c p) t -> p c t", p=128), reads=[scr["rsT"]],
                      pwrites=[rst], key=rst, allow_slow_non_contiguous=True)
            S.op("pool", lambda e: e.tensor_tensor(out=xs[:], in0=rst[:, :, 0:TP], in1=rst[:, :, 1:TP + 1], op=ALU.subtract), reads=[rst], writes=[xs])
            S.op("pool", lambda e: e.tensor_tensor(out=xs[:], in0=xs[:], in1=mu[:].unsqueeze(2).to_broadcast([128, 13, TP]), op=ALU.mult),
                 reads=[xs, mu], writes=[xs])
            S.op("pool", lambda e: e.tensor_tensor(out=xs[:], in0=xs[:], in1=rst[:, :, 1:TP + 1], op=ALU.add), reads=[xs, rst], writes=[xs])
            r = xs[:, 0:4, :]; k = xs[:, 4:8, :]; v = xs[:, 8:12, :]
            S.op("act", lambda e: e.activation(out=th[0:64, :], in_=xs[0:64, 12, :], func=AF.Tanh), reads=[xs], pwrites=[th])
            S.op("act", lambda e: e.copy(out=th[64:128, :], in_=xs[64:128, 12, :]), reads=[xs], pwrites=[th])
            for p in range(4):
                mm(ppre[:, p, :], ppre, w2[0:64, p * 128:(p + 1) * 128], w2, th[0:64, :], th, True, first_write=(p == 0))
            for p in range(4):
                S.op("act", lambda e, p=p: e.activation(out=sg[:, p, :], in_=ppre[:, p, :], func=AF.Sigmoid, bias=w0[:, p:p + 1]),
                     reads=[w0], writes=[ppre], pwrites=[sg])
            for p in range(4):
                mm(ppre[:, p, :], ppre, a2[64:128, p * 128:(p + 1) * 128], a2, th[64:128, :], th, True, first_write=(p == 0))
            for p in range(4):
                S.op("act", lambda e, p=p: e.activation(out=aa[:, p, :], in_=ppre[:, p, :], func=AF.Sigmoid, bias=a0[:, p:p + 1]),
                     reads=[a0], writes=[ppre], pwrites=[aa])
            S.op("dve", lambda e: e.tensor_tensor_scan(out=cum[:].rearrange("q p t -> q (p t)"), data0=rm[:],
                                                       data1=sg[:].rearrange("q p t -> q (p t)"), initial=0.0, op0=ALU.mult, op1=ALU.add),
                 reads=[rm, sg], writes=[cum])
            S.op("act", lambda e: e.activation(out=E1[:], in_=cum[:], func=AF.Exp, scale=-LD), reads=[cum], writes=[E1])
            S.op("act", lambda e: e.activation(out=E2[:], in_=cum[:], func=AF.Exp, scale=LD), reads=[cum], writes=[E2])
            S.op("pool", lambda e: e.tensor_tensor(out=t2[:], in0=cum[:], in1=sg[:], op=ALU.subtract), reads=[cum, sg], writes=[t2])
            S.op("act", lambda e: e.activation(out=E3[:], in_=t2[:], func=AF.Exp, scale=-LD), reads=[t2], writes=[E3])
            Dc = DcR.next()
            S.op("pool", lambda e, Dc=Dc: e.tensor_copy(out=Dc[:].rearrange("q c p -> q p c"), in_=E1[:, :, 15:TP:16]), reads=[E1], writes=[Dc])
            S.op("pool", lambda e: e.tensor_tensor(out=kkf[:], in0=k, in1=bc4(kkc), op=ALU.mult), reads=[xs, kkc], writes=[kkf])
            S.op("pool", lambda e: e.tensor_tensor(out=sq[:], in0=kkf[:], in1=kkf[:], op=ALU.mult), reads=[kkf], writes=[sq])
            for p in range(4):
                mm(ppre[:, p, :], ppre, ones[:], ones, sq[:, p, :], sq, True, first_write=(p == 0))
            S.op("act", lambda e: e.activation(out=rn[:], in_=ppre[:], func=AF.Sqrt), writes=[rn, ppre])
            S.op("dve", lambda e: e.tensor_scalar(out=rn[:], in0=rn[:], scalar1=1e-12, scalar2=None, op0=ALU.max), reads=[rn], writes=[rn])
            S.op("dve", lambda e: e.reciprocal(out=rn[:], in_=rn[:]), reads=[rn], writes=[rn])
            S.op("pool", lambda e: e.tensor_tensor(out=kkf[:], in0=kkf[:], in1=rn[:], op=ALU.mult), reads=[kkf, rn], writes=[kkf])
            S.op("pool", lambda e: e.tensor_tensor(out=t1[:], in0=aa[:], in1=bc4(ka), op=ALU.mult), reads=[aa, ka], writes=[t1])
            S.op("pool", lambda e: e.tensor_tensor(out=t1[:], in0=t1[:], in1=bc4(omka), op=ALU.add), reads=[t1, omka], writes=[t1])
            S.op("pool", lambda e: e.tensor_tensor(out=kp[:], in0=k, in1=t1[:], op=ALU.mult), reads=[xs, t1], writes=[kp])
            At, Bt, Kt, Rt, Vb = comp
            S.op("pool", lambda e: e.scalar_tensor_tensor(out=At[:], in0=kkf[:], scalar=-1.0, in1=E3[:], op0=ALU.mult, op1=ALU.mult)
                 if False else e.tensor_tensor(out=t2[:], in0=kkf[:], in1=E3[:], op=ALU.mult), reads=[kkf, E3], writes=[t2])
            S.op("dve", lambda e: e.tensor_scalar(out=At[:], in0=t2[:], scalar1=-1.0, scalar2=None, op0=ALU.mult), reads=[t2], writes=[At])
            S.op("pool", lambda e: e.tensor_tensor(out=t2[:], in0=kkf[:], in1=aa[:], op=ALU.mult), reads=[kkf, aa], writes=[t2])
            S.op("pool", lambda e: e.tensor_tensor(out=Bt[:], in0=t2[:], in1=E2[:], op=ALU.mult), reads=[t2, E2], writes=[Bt])
            S.op("pool", lambda e: e.tensor_tensor(out=Kt[:], in0=kp[:], in1=E2[:], op=ALU.mult), reads=[kp, E2], writes=[Kt])
            S.op("pool", lambda e: e.tensor_tensor(out=Rt[:], in0=r, in1=E1[:], op=ALU.mult), reads=[xs, E1], writes=[Rt])
            S.op("pool", lambda e: e.tensor_copy(out=Vb[:], in_=v), reads=[xs], writes=[Vb])
            S.op("pool", lambda e: e.tensor_tensor(out=t1[:], in0=r, in1=kp[:], op=ALU.mult), reads=[xs, kp], writes=[t1])
            S.op("pool", lambda e: e.tensor_tensor(out=sq[:], in0=t1[:], in1=bc4(rkc), op=ALU.mult), reads=[t1, rkc], writes=[sq])
            for p in range(4):
                mm(ppre[:, p, :], ppre, ones[:], ones, sq[:, p, :], sq, True, first_write=(p == 0))
            bon = bonR.next()
            S.op("act", lambda e, bon=bon: e.copy(out=bon[:], in_=ppre[:]), writes=[bon, ppre])
            S.op("pool", lambda e, bon=bon: e.tensor_tensor(out=bon[:], in0=bon[:], in1=v, op=ALU.mult), reads=[bon, xs], writes=[bon])
            rzt = rzR.next()
            S.dma("sp", rzt[:], scr["rzT"].t[:, t0:t0 + TP].rearrange("(c p) t -> p c t", p=128), reads=[scr["rzT"]], writes=[rzt])
            ZX = ZXr.next()
            for oi in range(5):
                for p in range(4):
                    S.op("dve" if (oi * 4 + p) % 2 == 0 else "pool", lambda e, oi=oi, p=p, ZX=ZX: e.tensor_tensor(
                        out=ZX[oi][:, :, p, :].rearrange("q c (h t) -> q c h t", t=16),
                        in0=comp[oi][:, p, :].rearrange("q (c t) -> q c t", t=16).unsqueeze(2).to_broadcast([128, NCH, 8, 16]),
                        in1=maskF[:, p, :].unsqueeze(1).unsqueeze(3).to_broadcast([128, NCH, 8, 16]), op=ALU.mult),
                        reads=[comp[oi], maskF], writes=[ZX[oi]] if p == 0 else (), pwrites=() if p == 0 else [ZX[oi]])
            return dict(ZX=ZX, Dc=Dc, bon=bon, rzt=rzt, yb=ybR.next(), t0=t0)


        def pre(bt, c, L, pc):
            ZA, ZB, ZK, ZR, ZV = bt["ZX"]
            BtZ = BtZr.next(); KtZ = KtZr.next(); U0 = U0r.next(); Vt = Vtr.next(); PTs = PTr.next(); QTs = QTr.next()
            WyZ = WyZr.next(); WhT = WhTr.next()
            pc.update(BtZ=BtZ, KtZ=KtZ, U0=U0, Vt=Vt, PTs=PTs, QTs=QTs, WyZ=WyZ, WhT=WhT, c=c, bt=bt)
            tokc = L.tokc
            first = True
            for oi, Z in enumerate((ZA, ZB, ZK, ZV)):
                for p in range(4):
                    mm(tokc.t[:, oi, :], tokc, Z[:, c, p, :], Z, Ff[:], Ff, first, first_write=first)
                    first = False
            N1 = L.Nr.next(); NT1 = L.NTr.next()
            specs = ((L.sc[0], ZA, ZB, mSL, N1), (L.sc[1], ZB, ZA, mSU, NT1), (L.sc[2], ZK, ZA, mSU, L.MTs), (L.sc[3], ZB, ZR, mUI, PTs))
            for gi, (pb, Lh, R_, msk, dst) in enumerate(specs):
                for p in range(4):
                    mm(pb.t, pb, Lh[:, c, p, :], Lh, R_[:, c, p, :], R_, p == 0, first_write=(gi == 0 and p == 0))
            for p in range(4):
                mm(L.QTp.t, L.QTp, ZK[:, c, p, :], ZK, ZR[:, c, p, :], ZR, False, first_write=False)
            yield
            G0 = L.Gr.next()
            tks = L.tks
            S.op("act", lambda e: e.copy(out=G0[:, 0:64], in_=tokc.t[:, 0, :]), writes=[G0, tokc.bank])
            mz = maskZ[:].unsqueeze(3).to_broadcast([128, 4, 2, 64])
            S.op("dve", lambda e: e.tensor_copy(out=tks[:], in_=tokc.t[:, 1:3, :]), writes=[tks, tokc.bank])
            S.op("act", lambda e: e.copy(out=Vt[:], in_=tokc.t[:, 3, :]), writes=[Vt, tokc.bank])
            S.op("pool", lambda e: e.tensor_tensor(out=BtZ[:].rearrange("q p (a j) -> q p a j", a=2),
                                                   in0=tks[:, 0, :].unsqueeze(1).unsqueeze(1).to_broadcast([128, 4, 2, 64]), in1=mz, op=ALU.mult),
                 reads=[tks, maskZ], writes=[BtZ])
            S.op("pool", lambda e: e.tensor_tensor(out=KtZ[:].rearrange("q p (a j) -> q p a j", a=2),
                                                   in0=tks[:, 1, :].unsqueeze(1).unsqueeze(1).to_broadcast([128, 4, 2, 64]), in1=mz, op=ALU.mult),
                 reads=[tks, maskZ], writes=[KtZ])
            for gi, (pb, Lh, R_, msk, dst) in enumerate(specs):
                S.op("dve", lambda e, pb=pb, msk=msk, dst=dst: e.tensor_tensor(out=dst[:], in0=pb.t, in1=msk[:], op=ALU.mult),
                     reads=[msk], writes=[dst, pb.bank])
            S.op("dve", lambda e: e.tensor_tensor(out=QTs[:], in0=L.QTp.t, in1=mUI[:], op=ALU.mult), reads=[mUI], writes=[QTs, L.QTp.bank])
            yield
            mm(L.mvp.t, L.mvp, L.MTs[:], L.MTs, Vt[:], Vt, True, first_write=True)
            yield
            S.op("act", lambda e: e.copy(out=G0[:, 64:128], in_=L.mvp.t), writes=[L.mvp.bank], pwrites=[G0])
            yield
            G = G0; Nk = N1; NTk = NT1
            gb = L.bg
            for lev in range(4):
                mm(gb[:, 0, :], gb, identB[:], identB, G[:], G, True, stop=False, first_write=True)
                mm(gb[:, 0, :], gb, NTk[:], NTk, G[:], G, False)
                if lev < 3:
                    mm(gb[:, 1, :], gb, NTk[:], NTk, Nk[:], Nk, True)
                    mm(gb[:, 2, :], gb, Nk[:], Nk, NTk[:], NTk, True)
                    yield
                    G2 = L.Gr.next(); N2 = L.Nr.next(); NT2 = L.NTr.next()
                    S.op("act", lambda e, G2=G2: e.copy(out=G2[:], in_=gb[:, 0, :]), writes=[G2, gb])
                    S.op("act", lambda e, N2=N2: e.copy(out=N2[:], in_=gb[:, 1, :]), writes=[N2, gb])
                    S.op("act", lambda e, NT2=NT2: e.copy(out=NT2[:], in_=gb[:, 2, :]), writes=[NT2, gb])
                    G = G2; Nk = N2; NTk = NT2
                    yield
                else:
                    yield
                    S.op("act", lambda e: e.copy(out=L.X1s[:], in_=gb[:, 0, 0:64]), writes=[L.X1s, gb])
                    S.op("act", lambda e: e.copy(out=U0[:], in_=gb[:, 0, 64:128]), writes=[U0, gb])
                    S.op("dve", lambda e: e.tensor_tensor(out=L.X1Z[:].rearrange("q p (a j) -> q p a j", a=2),
                                                           in0=L.X1s[:].unsqueeze(1).unsqueeze(1).to_broadcast([128, 4, 2, 64]), in1=mz, op=ALU.mult),
                         reads=[L.X1s, maskZ], writes=[L.X1Z])
                    yield
            bb = L.bb
            for p in range(4):
                mm(bb[:, p, :], bb, identB[:], identB, ZR[:, c, p, :], ZR, p == 0, stop=False, first_write=(p == 0))
                mm(bb[:, p, :], bb, L.X1Z[:, p, :], L.X1Z, PTs[:], PTs, False)
            ba = L.tokc.bank
            for p in range(4):
                mm(ba[:, p * 128:(p + 1) * 128], ba, L.X1Z[:, p, :], L.X1Z, BtZ[:, p, :], BtZ, p == 0, first_write=(p == 0))
            yield
            S.op("act", lambda e: e.copy(out=WyZ[:], in_=bb[:]), writes=[WyZ, bb])
            S.op("dve", lambda e: e.tensor_copy(out=WhT[:].rearrange("q p m -> q (p m)"), in_=ba[:]), writes=[WhT, ba])
            yield

        def state_stream(pc):
            c = pc["c"]; bt = pc["bt"]
            BtZ, KtZ, U0, Vt, PTs, QTs, WyZ, WhT = (pc[k] for k in ("BtZ", "KtZ", "U0", "Vt", "PTs", "QTs", "WyZ", "WhT"))
            for p in range(4):
                mm(WHp.t[:, p, :], WHp, BtZ[:, p, :], BtZ, U0[:], U0, p == 0, stop=False, first_write=(p == 0))
            for p in range(4):
                mm(WHp.t[:, p, :], WHp, KtZ[:, p, :], KtZ, Vt[:], Vt, False, stop=False)
            mm(Yp.t, Yp, PTs[:], PTs, U0[:], U0, False, stop=False)
            mm(Yp.t, Yp, QTs[:], QTs, Vt[:], Vt, False, stop=False)
            yield
            for p in range(4):
                mm(Yp.t, Yp, WyZ[:, p, :], WyZ, Hbf[:, p, :], Hbf, False, stop=(p == 3))
            for p in range(4):
                mm(WHp.t[:, p, :], WHp, WhT[:, p, :], WhT, Hbf[:, p, :], Hbf, False, stop=True)
            yield
            Dc = bt["Dc"]
            S.op("dve", lambda e: e.tensor_tensor(out=Hn[:], in0=WHp.t, in1=Hm[:], op=ALU.add), reads=[Hm], writes=[Hn, WHp.bank])
            S.op("dve", lambda e: e.tensor_tensor(out=Hm[:], in0=Hn[:], in1=Dc[:, c, :].unsqueeze(2).to_broadcast([128, 4, 64]), op=ALU.mult),
                 reads=[Hn, Dc], writes=[Hm])
            ysb = ysbR.next()
            pc["ysb"] = ysb
            S.op("act", lambda e: e.copy(out=ysb[:], in_=Yp.t), writes=[ysb, Yp.bank])
            S.op("act", lambda e: e.copy(out=Hbf[:], in_=Hm[:]), reads=[Hm], writes=[Hbf])
            yield

        def out_stream(pc):
            c = pc["c"]; bt = pc["bt"]; ysb = pc["ysb"]
            S.op("dve", lambda e: e.bn_stats(out=stats[:], in_=ysb[:]), reads=[ysb], writes=[stats])
            S.op("dve", lambda e: e.bn_aggr(out=mv[:], in_=stats[:]), reads=[stats], writes=[mv])
            yield
            S.op("act", lambda e: e.activation(out=rstd[:], in_=mv[:, 1:2], func=AF.Sqrt, bias=GN_EPS, scale=1.0), reads=[mv], writes=[rstd])
            yield
            S.op("dve", lambda e: e.reciprocal(out=rstd[:], in_=rstd[:]), reads=[rstd], writes=[rstd])
            S.op("dve", lambda e: e.tensor_scalar(out=yn[:], in0=ysb[:], scalar1=mv[:, 0:1], scalar2=rstd[:, 0:1], op0=ALU.subtract, op1=ALU.mult),
                 reads=[ysb, mv, rstd], writes=[yn])
            yield
            S.op("dve", lambda e: e.tensor_tensor(out=YZ[:].rearrange("q p (a j) -> q p a j", a=2),
                                                   in0=yn[:].unsqueeze(1).unsqueeze(1).to_broadcast([128, 4, 2, 64]),
                                                   in1=maskZ[:].unsqueeze(3).to_broadcast([128, 4, 2, 64]), op=ALU.mult),
                 reads=[yn, maskZ], writes=[YZ])
            yield
            for p in range(4):
                mm(yfp.t[:, p, :], yfp, YZ[:, p, :], YZ, Sel[:], Sel, True, first_write=(p == 0))
            yield
            yb = bt["yb"]
            S.op("act", lambda e: e.copy(out=yb[:, :, c * 16:(c + 1) * 16], in_=yfp.t), writes=([yb] if c == 0 else []) + [yfp.bank],
                 pwrites=() if c == 0 else [yb])
            if c == NCH - 1:
                post(bt)
            yield

        def post(bt):
            yb = bt["yb"]; bon = bt["bon"]; rzt = bt["rzt"]; t0 = bt["t0"]
            S.op("pool", lambda e: e.tensor_tensor(out=yb[:], in0=yb[:], in1=bc4(lg), op=ALU.mult), reads=[yb, lg], writes=[yb])
            S.op("pool", lambda e: e.tensor_tensor(out=yb[:], in0=yb[:], in1=bc4(lb), op=ALU.add), reads=[yb, lb], writes=[yb])
            S.op("pool", lambda e: e.tensor_tensor(out=yb[:], in0=yb[:], in1=bon[:], op=ALU.add), reads=[yb, bon], writes=[yb])
            S.op("pool", lambda e: e.tensor_tensor(out=yo[:], in0=yb[:], in1=rzt[:], op=ALU.mult), reads=[yb, rzt], writes=[yo])
            S.dma("sp", scr["ysT"].t[2, :, t0:t0 + TP].rearrange("(c p) t -> p c t", p=128), yo[:], reads=[yo], pwrites=[scr["ysT"]], key=yo)

        chunks = []
        for nb in range(SEQ // TP):
            for c in range(NCH):
                chunks.append((nb, c))
        bts = {}
        nxt = 0
        lane_gen = [None, None]
        lane_pc = [None, None]
        done_order = {}
        next_state = 0
        state_gen = None; state_pc = None
        out_q = []; out_gen = None
        n_total = len(chunks)
        finished_out = 0
        pcs = {}
        while finished_out < n_total:
            for li in range(2):
                if lane_gen[li] is None and nxt < n_total and nxt - next_state < 3:
                    nb, c = chunks[nxt]
                    if nb not in bts:
                        bts[nb] = prep(nb)
                    pc = {"idx": nxt}
                    pcs[nxt] = pc
                    lane_gen[li] = pre(bts[nb], c, lanes[li], pc)
                    lane_pc[li] = pc
                    nxt += 1
                if lane_gen[li] is not None:
                    try:
                        next(lane_gen[li])
                    except StopIteration:
                        done_order[lane_pc[li]["idx"]] = True
                        lane_gen[li] = None
            for _rep in range(DEBUG.get("state_rep", 2)):
                if state_gen is None and done_order.get(next_state) and next_state - finished_out < 3:
                    state_pc = pcs[next_state]
                    state_gen = state_stream(state_pc)
                if state_gen is not None:
                    try:
                        next(state_gen)
                    except StopIteration:
                        out_q.append(state_pc)
                        state_gen = None
                        next_state += 1
            for _rep in range(DEBUG.get("out_rep", 1)):
                if out_gen is None and out_q:
                    out_gen = out_stream(out_q.pop(0))
                if out_gen is not None:
                    try:
                        next(out_gen)
                    except StopIteration:
                        out_gen = None
                        finished_out += 1
        _barrier(S)
        S.stack_pop()


WSPEC = {
    "norm_g": [2, 1024], "w_in": [2, 1024, 9112], "cmp_w1": [2, 2, 32, 64, 128], "cmp_w2": [2, 2, 128, 64],
    "cmp_pe": [2, 2, 32, 64], "sg_ln_g": [2, 512], "sg_ln_b": [2, 512], "sg_w": [2, 8, 128, 128], "sg_b": [2, 8, 128],
    "rk_mu": [2, 1664], "rk_w0": [2, 512], "rk_w2": [2, 64, 512], "rk_a0": [2, 512], "rk_a2": [2, 64, 512],
    "rk_kk": [2, 8, 64], "rk_ka": [2, 8, 64], "rk_rk": [2, 8, 64], "rk_lnx_g": [2, 512], "rk_lnx_b": [2, 512],
    "w_branch": [2, 3, 512, 1024], "w_o": [2, 1024, 1024], "ple_norm_g": [2, 1024], "w_ple_gate": [2, 1024, 1024],
    "w_ple_proj": [2, 256, 1024], "final_norm_g": [1, 1024],
}


def build(SEQ, nlayers=2, enable=(1, 1, 1), scr_kind="Internal"):
    nc = bass.Bass("TRN2", target_bir_lowering=False)
    with contextlib.ExitStack() as stack:
        S = Sched(nc, stack)
        x = Buf("x", nc.dram_tensor("x", [SEQ, D], F32, kind="ExternalInput").ap())
        Wd = {"p": Buf("p", nc.dram_tensor("p", [2, SEQ, PLE], F32, kind="ExternalInput").ap())}
        for k, shp in WSPEC.items():
            Wd[k] = Buf(k, nc.dram_tensor(k, shp, F32, kind="ExternalInput").ap())
        out = Buf("out", nc.dram_tensor("out", [SEQ, D], F32, kind="ExternalOutput").ap())
        scr = make_scratch(S, SEQ, kind=scr_kind)
        xmid = S.dram("xmid", [SEQ, D], F32, kind=scr_kind)
        cur = x
        for lyr in range(nlayers):
            last = lyr == nlayers - 1
            dst = out if last else xmid
            phase_A(S, nc, SEQ, lyr, cur, Wd, scr)
            if enable[0]:
                phase_B(S, nc, SEQ, lyr, Wd, scr)
            if enable[1]:
                phase_C(S, nc, SEQ, lyr, Wd, scr)
            if enable[2]:
                phase_D(S, nc, SEQ, lyr, Wd, scr)
            phase_E(S, nc, SEQ, lyr, cur, dst, Wd, scr, final=(last and nlayers == 2))
            cur = dst
        S.emit()
    return nc


def phase_B(S, nc, SEQ, lyr, Wd, scr):
    NC = (SEQ - 32) // 16 + 1
    NT = (NC + 127) // 128
    NCp = NT * 128
    KT = SEQ // 128
    with contextlib.ExitStack() as st:
        S.stack_push(st)
        ident = make_ident(S, "B_ident")
        ksT = S.sb("B_ksT", [128, 2, SEQ], BF16)
        HALF = min(4096, SEQ)
        NA = SEQ // HALF
        kwT = S.sb("B_kwT", [64, 2, SEQ], BF16)
        vs = S.sb("B_vs", [128, KT, 2, 65], BF16)
        vw = S.sb("B_vw", [128, KT, 2, 65], BF16)
        kcmpT = S.sb("B_kcmpT", [64, 2, NCp], BF16)
        Rc = S.sb("B_Rc", [128, NT, 2, 193], BF16)
        S.op("pool", lambda e: e.memset(ksT[64:128, :, :], 1.0), writes=[ksT])
        for g_ in range(2):
            for a_ in range(NA):
                S.op("pool", lambda e, g_=g_, a_=a_: e.affine_select(
                    out=ksT[64:128, g_, a_ * HALF:(a_ + 1) * HALF], in_=ksT[64:128, g_, a_ * HALF:(a_ + 1) * HALF], pattern=[[1, HALF]],
                    compare_op=ALU.is_ge, fill=0.0, base=0, channel_multiplier=-64), reads=[ksT], pwrites=[ksT])
                S.op("pool", lambda e, g_=g_, a_=a_: e.affine_select(
                    out=ksT[64:128, g_, a_ * HALF:(a_ + 1) * HALF], in_=ksT[64:128, g_, a_ * HALF:(a_ + 1) * HALF], pattern=[[-1, HALF]],
                    compare_op=ALU.is_ge, fill=0.0, base=63, channel_multiplier=64), reads=[ksT], pwrites=[ksT])
        S.dma("sp", ksT[0:64, :, :], scr["ksT"].t.rearrange("(g d) t -> d g t", g=2), reads=[scr["ksT"]], pwrites=[ksT], key=ksT)
        S.dma("sp", kwT[:], scr["kwT"].t.rearrange("(g d) t -> d g t", g=2), reads=[scr["kwT"]], writes=[kwT])
        S.op("pool", lambda e: e.memset(vs[:], 1.0), writes=[vs])
        S.op("pool", lambda e: e.memset(vw[:], 1.0), writes=[vw])
        for k0 in range(0, KT, 8):
            k1 = min(KT, k0 + 8)
            for (dst, c0) in ((vs, 0), (vw, 128)):
                for g in range(2):
                    S.dma("sp", dst[:, k0:k1, g, 0:64],
                          scr["vsw"].t[k0 * 128:k1 * 128, c0 + g * 64:c0 + (g + 1) * 64].rearrange("(k p) d -> p k d", p=128),
                          reads=[scr["vsw"]], pwrites=[dst], key=dst)
        S.op("pool", lambda e: e.memset(Rc[:], 1.0), writes=[Rc])
        for nt in range(NT):
            for g in range(2):
                S.op("pool", lambda e, nt=nt, g=g: e.affine_select(
                    out=Rc[:, nt, g, 65:193], in_=Rc[:, nt, g, 65:193], pattern=[[-4, 128]], compare_op=ALU.is_ge, fill=0.0,
                    base=nt * 128 + 1, channel_multiplier=1), reads=[Rc], writes=[Rc])
                S.op("pool", lambda e, nt=nt, g=g: e.affine_select(
                    out=Rc[:, nt, g, 65:193], in_=Rc[:, nt, g, 65:193], pattern=[[4, 128]], compare_op=ALU.is_ge, fill=0.0,
                    base=3 - nt * 128, channel_multiplier=-1), reads=[Rc], writes=[Rc])
        npad = NCp - NC
        if npad:
            S.op("pool", lambda e: e.affine_select(
                out=Rc[:, NT - 1, :, :], in_=Rc[:, NT - 1, :, :], pattern=[[0, 2 * 193]], compare_op=ALU.is_ge, fill=0.0,
                base=(NC - 1) - (NT - 1) * 128, channel_multiplier=-1), reads=[Rc], writes=[Rc])
        S.op("pool", lambda e: e.memset(kcmpT[:], 0.0), writes=[kcmpT])

        with contextlib.ExitStack() as st2:
            S.stack_push(st2)
            kvT = S.sb("B_kvT", [64, 2, SEQ], BF16)
            w1 = S.sb("B_w1", [64, 32, 128], BF16)
            w2 = S.sb("B_w2", [128, 64], BF16)
            peT = S.sb("B_peT", [64, 32])
            peTb = S.sb("B_peTb", [64, 32], BF16)
            cb = S.sb("B_cb", [128, 1])
            hid = S.sb("B_hid", [128, NCp], BF16)
            ph = S.ps("B_ph", [128, 512])
            pc1 = S.ps("B_pc1", [128, 512])
            pk = S.ps("B_pk", [128, 512])
            for kv in range(2):
                src = scr["kcT"] if kv == 0 else scr["vcT"]
                S.dma("sp", kvT[:], src.t.rearrange("(g d) t -> d g t", g=2), reads=[src], writes=[kvT])
                S.dma("pool", w1[:], Wd["cmp_w1"].t[lyr, kv].rearrange("l d h -> d l h"), reads=[Wd["cmp_w1"]], writes=[w1])
                S.dma("pool", w2[:], Wd["cmp_w2"].t[lyr, kv], reads=[Wd["cmp_w2"]], writes=[w2])
                S.dma("sp", peT[:], Wd["cmp_pe"].t[lyr, kv].rearrange("l d -> d l"), reads=[Wd["cmp_pe"]], writes=[peT],
                      allow_slow_non_contiguous=True)
                S.op("dve", lambda e: e.tensor_copy(out=peTb[:], in_=peT[:]), reads=[peT], writes=[peTb])
                for l in range(32):
                    S.op("pe", lambda e, l=l: e.matmul(pc1[:, 0:1], lhsT=w1[:, l, :], rhs=peTb[:, l:l + 1], start=(l == 0), stop=(l == 31)),
                         reads=[w1, peTb], writes=[pc1] if l == 0 else (), pwrites=() if l == 0 else [pc1])
                S.op("dve", lambda e: e.tensor_copy(out=cb[:], in_=pc1[:, 0:1]), reads=[pc1], writes=[cb])
                for g in range(2):
                    S.op("dve", lambda e: e.memset(hid[:], 0.0), writes=[hid])
                    for n0 in range(0, NC, 512):
                        nn = min(512, NC - n0)
                        for l in range(32):
                            S.op("pe", lambda e, l=l, g=g, n0=n0, nn=nn: e.matmul(
                                ph[:, 0:nn], lhsT=w1[:, l, :], rhs=kvT[:, g, n0 * 16 + l: n0 * 16 + l + (nn - 1) * 16 + 1: 16], start=(l == 0), stop=(l == 31)),
                                reads=[w1, kvT], writes=[ph] if l == 0 else (), pwrites=() if l == 0 else [ph])
                        S.op("act", lambda e, n0=n0, nn=nn: e.activation(out=hid[:, n0:n0 + nn], in_=ph[:, 0:nn], func=AF.Silu, bias=cb[:, 0:1]),
                             reads=[ph, cb], pwrites=[hid])
                    if kv == 0:
                        for n0 in range(0, NC, 512):
                            nn = min(512, NC - n0)
                            S.op("pe", lambda e, n0=n0, nn=nn: e.matmul(pk[0:64, 0:nn], lhsT=w2[:], rhs=hid[:, n0:n0 + nn], start=True, stop=True),
                                 reads=[w2, hid], writes=[pk])
                            S.op("dve", lambda e, g=g, n0=n0, nn=nn: e.tensor_copy(out=kcmpT[:, g, n0:n0 + nn], in_=pk[0:64, 0:nn]),
                                 reads=[pk], pwrites=[kcmpT])
                    else:
                        for nt in range(NT):
                            rows = min(128, NC - nt * 128)
                            S.op("pe", lambda e, nt=nt: e.matmul(pk[:, 0:64], lhsT=hid[:, nt * 128:(nt + 1) * 128], rhs=w2[:], start=True, stop=True),
                                 reads=[w2, hid], writes=[pk])
                            S.op("dve", lambda e, g=g, nt=nt: e.tensor_copy(out=Rc[:, nt, g, 0:64], in_=pk[:, 0:64]),
                                 reads=[pk], pwrites=[Rc])
            _barrier(S)
            S.stack_pop()

        qt = Ring([S.sb("B_q%d" % i, [64, 8, 128], BF16) for i in range(2)])
        gt = Ring([S.sb("B_g%d" % i, [128, 24]) for i in range(2)])
        nzt = Ring([S.sb("B_nz%d" % i, [128, 512], BF16) for i in range(2)])
        Et = Ring([S.sb("B_E%d" % i, [128, 512], BF16) for i in range(4)])
        psT = Ring([S.ps("B_psT%d" % i, [128, 512]) for i in range(3)])
        pcA = S.ps("B_pcA", [128, 2, 193])
        pcB = S.ps("B_pcB", [128, 2, 193])
        pos = S.ps("B_pos", [128, 4, 65])
        pow_ = S.ps("B_pow", [128, 4, 65])
        pmisc = S.ps("B_pmisc", [128, 4, 128], BF16)
        P2 = lambda nm, shp, dt=F32: [S.sb("B_%s%d" % (nm, i), shp, dt) for i in range(2)]
        oc2 = P2("oc", [128, 4, 193]); rcs2 = P2("rcs", [128, 4]); rss2 = P2("rss", [128, 4]); rws2 = P2("rws", [128, 4])
        cc2 = P2("cc", [128, 3, 4]); sc_2 = P2("sc", [128, 128]); sc2_2 = P2("sc2", [128, 128]); m1_2 = P2("m1", [128, 8]); m2_2 = P2("m2", [128, 8])
        pws2 = P2("pws", [128, 4, 65]); pss2 = P2("pss", [128, 4, 65])
        negq2 = P2("negq", [128, 2, 128], BF16)
        for _nq in negq2:
            S.op("pool", lambda e, _nq=_nq: e.memset(_nq[:], 0.0), writes=[_nq])
        qAr = {(g_, a_): Ring([S.sb("B_qA%d%d_%d" % (g_, a_, i), [128, 4, 128], BF16) for i in range(2)]) for g_ in range(2) for a_ in range(NA)}
        yg = S.sb("B_yg", [128, 4, 64])
        ytmp = S.sb("B_ytmp", [128, 4, 64])
        ynsaR = Ring([S.sb("B_ynsa%d" % i, [128, 512], BF16) for i in range(2)])
        stg = Ring([S.sb("B_stg%d" % i, [128, 4, 128], BF16) for i in range(2)])

        def qk_exp(kT_ap, kbuf, q_ap, qbuf, neg_lhsT=None):
            p = psT.next()
            if False:
                pass
            else:
                S.op("pe", lambda e, p=p: e.matmul(p[:], lhsT=kT_ap, rhs=q_ap, start=True, stop=True), reads=[kbuf, qbuf], writes=[p])
            E = Et.next()
            S.op("act", lambda e, p=p, E=E: e.activation(out=E[:], in_=p[:], func=AF.Exp), reads=[p], writes=[E])
            return E

        def pipeline(tiles, L=2):
            Es = {}
            n = len(tiles)
            for i in range(n + L):
                if i < n:
                    Es[i] = tiles[i][0]()
                if i - L >= 0:
                    tiles[i - L][1](Es.pop(i - L))

        def mask(E, base, cm, qstep):
            S.op("pool", lambda e, E=E: e.affine_select(out=E[:], in_=E[:], pattern=[[0, 4], [qstep, 128]], compare_op=ALU.is_ge,
                                                       fill=0.0, base=base, channel_multiplier=cm), reads=[E], writes=[E])

        pending_tail = [None]
        for qb in range(SEQ // 128):
            q0 = qb * 128
            q = qt.next(); gg = gt.next(); nz = nzt.next(); ynsa = ynsaR.next()
            S.dma("sp", q[:], scr["qT"].t[:, q0:q0 + 128].rearrange("(h d) t -> d h t", h=8), reads=[scr["qT"]], writes=[q])
            qAs = {}
            for g_ in range(2):
                for a_ in range(min(NA, qb * 128 // HALF + 1)):
                    qa = qAr[(g_, a_)].next()
                    qAs[(g_, a_)] = qa
                    S.dma("sp", qa[0:64, :, :], scr["qT"].t[g_ * 256:(g_ + 1) * 256, q0:q0 + 128].rearrange("(h d) t -> d h t", h=4),
                          reads=[scr["qT"]], writes=[qa])
            S.dma("sp", gg[:], scr["gate"].t[q0:q0 + 128, :], reads=[scr["gate"]], writes=[gg])
            S.dma("sp", nz[:], scr["nzs"].t[q0:q0 + 128, :], reads=[scr["nzs"]], writes=[nz])
            def gbody(g, q=q, gg=gg, nz=nz, qAs=qAs, qb=qb, q0=q0, ynsa=ynsa):
                oc = oc2[g]; rcs = rcs2[g]; rss = rss2[g]; rws = rws2[g]; cc = cc2[g]; sc = sc_2[g]; sc2 = sc2_2[g]
                m1 = m1_2[g]; m2 = m2_2[g]; negq = negq2[g]; pws = pws2[g]; pss = pss2[g]
                q_ap = q[:, 4 * g:4 * g + 4, :].rearrange("d h q -> d (h q)")
                n_max = min(8 * qb + 6, NC - 1)
                ntl = n_max // 128 + 1
                def c_qk(nt, g=g, q_ap=q_ap, q=q):
                    E = qk_exp(kcmpT[:, g, nt * 128:(nt + 1) * 128], kcmpT, q_ap, q)
                    if q0 - 16 * (128 * nt + 127) - 31 < 0:
                        mask(E, q0 - 16 * 128 * nt - 31, -16, 1)
                    return E

                def c_pv(nt, E, g=g, ntl=ntl):
                    for h in range(4):
                        pcx = pcA if h < 2 else pcB
                        first = (nt == 0 and h % 2 == 0)
                        S.op("pe", lambda e, E=E, h=h, pcx=pcx, nt=nt, first=first, g=g, ntl=ntl: e.matmul(
                            pcx[:, h % 2, :], lhsT=E[:, h * 128:(h + 1) * 128], rhs=Rc[:, nt, g, :], start=first,
                            stop=(nt == ntl - 1 and h % 2 == 1), skip_group_check=True),
                            reads=[E, Rc], writes=[pcx] if first else (), pwrites=() if first else [pcx])
                pipeline([(lambda nt=nt: c_qk(nt), lambda E, nt=nt: c_pv(nt, E)) for nt in range(ntl)])
                S.op("act", lambda e: e.copy(out=oc[:, 0:2, :], in_=pcA[:]), reads=[pcA], pwrites=[oc])
                S.op("act", lambda e: e.copy(out=oc[:, 2:4, :], in_=pcB[:]), reads=[pcB], pwrites=[oc])
                S.op("dve", lambda e: e.tensor_scalar(out=rcs[:], in0=oc[:, :, 64], scalar1=1e-30, scalar2=None, op0=ALU.max),
                     reads=[oc], writes=[rcs])
                S.op("dve", lambda e: e.reciprocal(out=rcs[:], in_=rcs[:]), reads=[rcs], writes=[rcs])
                S.op("dve", lambda e: e.tensor_scalar(out=sc[:], in0=oc[:, 0, 65:193], scalar1=rcs[:, 0:1], scalar2=None, op0=ALU.mult),
                     reads=[oc, rcs], writes=[sc])
                for h in range(1, 4):
                    S.op("dve", lambda e, h=h: e.scalar_tensor_tensor(out=sc[:], in0=oc[:, h, 65:193], scalar=rcs[:, h:h + 1], in1=sc[:],
                                                                      op0=ALU.mult, op1=ALU.add), reads=[oc, rcs, sc], writes=[sc])
                for half in range(2):
                    tb = 2 * qb + half
                    ps_ = slice(half * 64, (half + 1) * 64)
                    if tb + 1 < 128:
                        S.op("dve", lambda e, ps_=ps_, tb=tb: e.memset(sc[ps_, tb + 1:128], -1e4), reads=[sc], writes=[sc])
                    lo = max(tb - 1, 0)
                    S.op("dve", lambda e, ps_=ps_, tb=tb, lo=lo: e.memset(sc[ps_, lo:tb + 1], 1e4), reads=[sc], writes=[sc])
                S.op("dve", lambda e: e.memset(sc[:, 0:1], 1e4), reads=[sc], writes=[sc])
                S.op("dve", lambda e: e.max(out=m1[:], in_=sc[:]), reads=[sc], writes=[m1])
                S.op("dve", lambda e: e.match_replace(out=sc2[:], in_to_replace=m1[:], in_values=sc[:], imm_value=-3e4),
                     reads=[sc, m1], writes=[sc2])
                S.op("dve", lambda e: e.max(out=m2[:], in_=sc2[:]), reads=[sc2], writes=[m2])
                S.op("dve", lambda e: e.tensor_scalar(out=negq[:, 0, :], in0=sc[:], scalar1=m2[:, 7:8], scalar2=-1e4, op0=ALU.is_lt, op1=ALU.mult),
                     reads=[sc, m2], pwrites=[negq])
                S.op("dve", lambda e: e.tensor_scalar(out=negq[:, 1, 64:128], in0=sc[:, 0:64], scalar1=m2[:, 7:8], scalar2=-1e4, op0=ALU.is_lt, op1=ALU.mult),
                     reads=[sc, m2], pwrites=[negq])
                yield
                kts = list(range(max(0, qb - 4), qb + 1))

                def w_qk(i, kt, g=g, q_ap=q_ap, q=q, qb=qb):
                    E = qk_exp(kwT[:, g, kt * 128:(kt + 1) * 128], kwT, q_ap, q)
                    if kt == qb - 4:
                        mask(E, -1, 1, -1)
                    if kt == qb:
                        mask(E, 0, -1, 1)
                    return E

                def w_pv(i, kt, E, g=g, kts=kts):
                    for h in range(4):
                        first = (i == 0 and h == 0)
                        S.op("pe", lambda e, E=E, h=h, kt=kt, first=first, last=(i == len(kts) - 1 and h == 3), g=g: e.matmul(
                            pow_[:, h, :], lhsT=E[:, h * 128:(h + 1) * 128], rhs=vw[:, kt, g, :], start=first, stop=last,
                            skip_group_check=True),
                            reads=[E, vw], writes=[pow_] if first else (), pwrites=() if first else [pow_])
                pipeline([(lambda i=i, kt=kt: w_qk(i, kt), lambda E, i=i, kt=kt: w_pv(i, kt, E)) for i, kt in enumerate(kts)])
                S.op("act", lambda e: e.copy(out=pws[:], in_=pow_[:]), reads=[pow_], writes=[pws])
                yield
                na_here = min(NA, qb * 128 // HALF + 1)
                S.op("pe", lambda e: e.transpose(out=pmisc[:, 1, :], in_=negq[:, 1, :], identity=ident[:]), reads=[negq, ident], writes=[pmisc])
                if na_here > 1:
                    S.op("pe", lambda e: e.transpose(out=pmisc[:, 0, :], in_=negq[:, 0, :], identity=ident[:]), reads=[negq, ident], pwrites=[pmisc])
                for a_ in range(na_here):
                    qa = qAs[(g, a_)]
                    S.op("dve", lambda e, qa=qa, a_=a_: e.tensor_copy(out=qa[64:128, :, :],
                                                                   in_=pmisc[64:128, (1 - a_):(2 - a_), :].to_broadcast([64, 4, 128])),
                         reads=[pmisc], pwrites=[qa])
                yield
                def s_qk(kt, g=g, q_ap=q_ap, q=q, qb=qb, qAs=qAs):
                    qa = qAs[(g, kt * 128 // HALF)]
                    E = qk_exp(ksT[:, g, kt * 128:(kt + 1) * 128], ksT, qa[:].rearrange("p h q -> p (h q)"), qa)
                    if kt == qb:
                        mask(E, 0, -1, 1)
                    return E

                def s_pv(kt, E, g=g, qb=qb):
                    for h in range(4):
                        first = (kt == 0 and h == 0)
                        S.op("pe", lambda e, E=E, h=h, kt=kt, first=first, last=(kt == qb and h == 3), g=g: e.matmul(
                            pos[:, h, :], lhsT=E[:, h * 128:(h + 1) * 128], rhs=vs[:, kt, g, :], start=first, stop=last,
                            skip_group_check=True),
                            reads=[E, vs], writes=[pos] if first else (), pwrites=() if first else [pos])
                pipeline([(lambda kt=kt: s_qk(kt), lambda E, kt=kt: s_pv(kt, E)) for kt in range(qb + 1)])
                S.op("act", lambda e: e.copy(out=pss[:], in_=pos[:]), reads=[pos], writes=[pss])
                yield
                S.op("dve", lambda e: e.reciprocal(out=rss[:], in_=pss[:, :, 64]), reads=[pss], writes=[rss])
                S.op("dve", lambda e: e.reciprocal(out=rws[:], in_=pws[:, :, 64]), reads=[pws], writes=[rws])
                gv = gg[:, g * 12:(g + 1) * 12].rearrange("p (h b) -> p b h", b=3)
                for b, rr in enumerate((rcs, rss, rws)):
                    S.op("dve", lambda e, b=b, rr=rr, gv=gv: e.tensor_tensor(out=cc[:, b, :], in0=gv[:, b, :], in1=rr[:], op=ALU.mult),
                         reads=[gg, rr], pwrites=[cc])
                bc = lambda b: cc[:, b, :].unsqueeze(2).to_broadcast([128, 4, 64])
                S.op("dve", lambda e: e.tensor_tensor(out=yg[:], in0=oc[:, :, 0:64], in1=bc(0), op=ALU.mult), reads=[oc, cc], writes=[yg])
                S.op("dve", lambda e: e.tensor_tensor(out=ytmp[:], in0=pss[:, :, 0:64], in1=bc(1), op=ALU.mult), reads=[pss, cc], writes=[ytmp])
                S.op("pool", lambda e: e.tensor_tensor(out=yg[:], in0=yg[:], in1=ytmp[:], op=ALU.add), reads=[yg, ytmp], writes=[yg])
                S.op("dve", lambda e: e.tensor_tensor(out=ytmp[:], in0=pws[:, :, 0:64], in1=bc(2), op=ALU.mult), reads=[pws, cc], writes=[ytmp])
                S.op("pool", lambda e: e.tensor_tensor(out=yg[:], in0=yg[:], in1=ytmp[:], op=ALU.add), reads=[yg, ytmp], writes=[yg])
                S.op("pool", lambda e, g=g, nz=nz: e.tensor_tensor(out=ynsa[:, g * 256:(g + 1) * 256], in0=yg[:].rearrange("p h d -> p (h d)"),
                                                                   in1=nz[:, g * 256:(g + 1) * 256], op=ALU.mult),
                     reads=[yg, nz], pwrites=[ynsa])
                yield
            gens = [gbody(0), gbody(1)]
            for _st in range(5):
                for gen_ in gens:
                    next(gen_)
                if _st == 1 and pending_tail[0] is not None:
                    pending_tail[0]()
                    pending_tail[0] = None

            def tail(ynsa=ynsa, q0=q0):
                for k in range(4):
                    S.op("pe", lambda e, k=k: e.transpose(out=pmisc[:, k, :], in_=ynsa[:, k * 128:(k + 1) * 128], identity=ident[:]),
                         reads=[ynsa, ident], writes=[pmisc] if k == 0 else (), pwrites=() if k == 0 else [pmisc])
                sg = stg.next()
                S.op("act", lambda e, sg=sg: e.copy(out=sg[:], in_=pmisc[:]), reads=[pmisc], writes=[sg])
                S.dma("sp", scr["ysT"].t[0, :, q0:q0 + 128].rearrange("(k p) t -> p k t", p=128), sg[:], reads=[sg], pwrites=[scr["ysT"]], key=sg)
            pending_tail[0] = tail
        if pending_tail[0] is not None:
            pending_tail[0]()
        _barrier(S)
        S.stack_pop()


def phase_D_seq(S, nc, SEQ, lyr, Wd, scr):
    TP = 128
    TB = 8
    GN_EPS = 64e-5
    xtok = scr["xtok"]
    with contextlib.ExitStack() as st:
        S.stack_push(st)
        identF = make_ident(S, "D_ident", F32)
        ones = S.sb("D_ones", [128, 128])
        S.op("pool", lambda e: e.memset(ones[:], 0.0), writes=[ones])
        S.op("pool", lambda e: e.memset(ones[0:64, 0:64], 1.0), reads=[ones], writes=[ones])
        S.op("pool", lambda e: e.memset(ones[64:128, 64:128], 1.0), reads=[ones], writes=[ones])

        def cvec(name, key, n):
            t = S.sb("D_" + name, [128, n])
            S.dma("sp", t[:], Wd[key].t[lyr].rearrange("(c p) -> p c", p=128), reads=[Wd[key]], writes=[t],
                  allow_slow_non_contiguous=True)
            return t

        def cvec2(name, key):
            t = S.sb("D_" + name, [128, 4])
            S.dma("sp", t[:], Wd[key].t[lyr].rearrange("(c a) j -> (a j) c", a=2), reads=[Wd[key]], writes=[t],
                  allow_slow_non_contiguous=True)
            return t
        mu = cvec("mu", "rk_mu", 13)
        w0 = cvec("w0", "rk_w0", 4)
        a0 = cvec("a0", "rk_a0", 4)
        lg = cvec("lg", "rk_lnx_g", 4)
        lb = cvec("lb", "rk_lnx_b", 4)
        kkc = cvec2("kkc", "rk_kk")
        ka = cvec2("ka", "rk_ka")
        rkc = cvec2("rkc", "rk_rk")
        omka = S.sb("D_omka", [128, 4])
        S.op("pool", lambda e: e.tensor_scalar(out=omka[:], in0=ka[:], scalar1=-1.0, scalar2=1.0, op0=ALU.mult, op1=ALU.add),
             reads=[ka], writes=[omka])
        w2 = S.sb("D_w2", [64, 512], BF16)
        a2 = S.sb("D_a2", [128, 512], BF16)
        S.dma("pool", w2[:], Wd["rk_w2"].t[lyr], reads=[Wd["rk_w2"]], writes=[w2])
        S.dma("pool", a2[64:128, :], Wd["rk_a2"].t[lyr], reads=[Wd["rk_a2"]], writes=[a2])
        St = S.sb("D_state", [128, 4, 64])
        S.op("dve", lambda e: e.memset(St[:], 0.0), writes=[St])

        rst = S.sb("D_rst", [128, 13, TP + 1])
        xs = S.sb("D_xs", [128, 13, TP])
        th = S.sb("D_th", [128, TP], BF16)
        dd = S.sb("D_dd", [128, 4, TP])
        aa = S.sb("D_aa", [128, 4, TP])
        kkf = S.sb("D_kkf", [128, 4, TP])
        sq = S.sb("D_sq", [128, 4, TP])
        rn = S.sb("D_rn", [128, 4, TP])
        kp = S.sb("D_kp", [128, 4, TP])
        am = S.sb("D_am", [128, 4, TP])
        bm = S.sb("D_bm", [128, 4, TP])
        t1 = S.sb("D_t1", [128, 4, TP])
        bonus = S.sb("D_bonus", [128, 4, TP])
        vv = S.sb("D_vv", [128, 4, TP])
        tk = S.sb("D_tk", [128, 5, 4, 128])
        bcr = Ring([S.sb("D_bc%d" % i, [128, TB, 5, 256]) for i in range(2)])
        tmp = S.sb("D_tmp", [128, 4, 64])
        tmp2 = S.sb("D_tmp2", [128, 4, 64])
        kv = Ring([S.sb("D_kv%d" % i, [128, 4, 64]) for i in range(2)])
        sa = S.sb("D_sa", [128, 4])
        ybuf = S.sb("D_y", [128, 4, TP])
        ysq = S.sb("D_ysq", [128, 4, TP])
        mean = S.sb("D_mean", [128, 4, TP])
        var = S.sb("D_var", [128, 4, TP])
        rzt = S.sb("D_rz", [128, 4, TP], BF16)
        yo = S.sb("D_yo", [128, 4, TP], BF16)
        pa = Ring([S.ps("D_pa%d" % i, [128, 4, 128]) for i in range(4)])

        bc4 = lambda t: t[:].unsqueeze(2).to_broadcast([128, 4, TP])
        for nb in range(SEQ // TP):
            t0 = nb * TP
            S.dma("sp", rst[:, :, 1:TP + 1], scr["rsT"].t[:, t0:t0 + TP].rearrange("(c p) t -> p c t", p=128), reads=[scr["rsT"]],
                  writes=[rst])
            if nb == 0:
                S.op("pool", lambda e: e.memset(rst[:, :, 0:1], 0.0), reads=[rst], pwrites=[rst])
            else:
                S.dma("sp", rst[:, :, 0:1], scr["rsT"].t[:, t0 - 1:t0].rearrange("(c p) t -> p c t", p=128), reads=[scr["rsT"]],
                      pwrites=[rst], key=rst, allow_slow_non_contiguous=True)
            S.op("pool", lambda e: e.tensor_tensor(out=xs[:], in0=rst[:, :, 0:TP], in1=rst[:, :, 1:TP + 1], op=ALU.subtract),
                 reads=[rst], writes=[xs])
            S.op("pool", lambda e: e.tensor_tensor(out=xs[:], in0=xs[:], in1=mu[:].unsqueeze(2).to_broadcast([128, 13, TP]), op=ALU.mult),
                 reads=[xs, mu], writes=[xs])
            S.op("pool", lambda e: e.tensor_tensor(out=xs[:], in0=xs[:], in1=rst[:, :, 1:TP + 1], op=ALU.add), reads=[xs, rst], writes=[xs])
            r = xs[:, 0:4, :]; k = xs[:, 4:8, :]; v = xs[:, 8:12, :]
            S.op("act", lambda e: e.activation(out=th[0:64, :], in_=xs[0:64, 12, :], func=AF.Tanh), reads=[xs], pwrites=[th])
            S.op("act", lambda e: e.copy(out=th[64:128, :], in_=xs[64:128, 12, :]), reads=[xs], pwrites=[th])
            pw = pa.next(); pp = pa.next()
            for p in range(4):
                S.op("pe", lambda e, p=p, pw=pw: e.matmul(pw[:, p, :], lhsT=w2[0:64, p * 128:(p + 1) * 128], rhs=th[0:64, :], start=True, stop=True),
                     reads=[w2, th], writes=[pw] if p == 0 else (), pwrites=() if p == 0 else [pw])
                S.op("pe", lambda e, p=p, pp=pp: e.matmul(pp[:, p, :], lhsT=a2[64:128, p * 128:(p + 1) * 128], rhs=th[64:128, :], start=True, stop=True),
                     reads=[a2, th], writes=[pp] if p == 0 else (), pwrites=() if p == 0 else [pp])
            for p in range(4):
                S.op("act", lambda e, p=p, pw=pw: e.activation(out=dd[:, p, :], in_=pw[:, p, :], func=AF.Sigmoid, bias=w0[:, p:p + 1]),
                     reads=[pw, w0], pwrites=[dd])
                S.op("act", lambda e, p=p, pp=pp: e.activation(out=aa[:, p, :], in_=pp[:, p, :], func=AF.Sigmoid, bias=a0[:, p:p + 1]),
                     reads=[pp, a0], pwrites=[aa])
            S.op("act", lambda e: e.activation(out=dd[:], in_=dd[:], func=AF.Exp, scale=-0.6065306597126334), reads=[dd], writes=[dd])
            S.op("pool", lambda e: e.tensor_tensor(out=kkf[:], in0=k, in1=bc4(kkc), op=ALU.mult), reads=[xs, kkc], writes=[kkf])
            S.op("pool", lambda e: e.tensor_tensor(out=sq[:], in0=kkf[:], in1=kkf[:], op=ALU.mult), reads=[kkf], writes=[sq])
            pn = pa.next()
            for p in range(4):
                S.op("pe", lambda e, p=p, pn=pn: e.matmul(pn[:, p, :], lhsT=ones[:], rhs=sq[:, p, :], start=True, stop=True),
                     reads=[ones, sq], writes=[pn] if p == 0 else (), pwrites=() if p == 0 else [pn])
            S.op("act", lambda e, pn=pn: e.activation(out=rn[:], in_=pn[:], func=AF.Sqrt), reads=[pn], writes=[rn])
            S.op("pool", lambda e: e.tensor_scalar(out=rn[:], in0=rn[:], scalar1=1e-12, scalar2=None, op0=ALU.max), reads=[rn], writes=[rn])
            S.op("dve", lambda e: e.reciprocal(out=rn[:], in_=rn[:]), reads=[rn], writes=[rn])
            S.op("pool", lambda e: e.tensor_tensor(out=kkf[:], in0=kkf[:], in1=rn[:], op=ALU.mult), reads=[kkf, rn], writes=[kkf])
            S.op("pool", lambda e: e.tensor_tensor(out=t1[:], in0=aa[:], in1=bc4(ka), op=ALU.mult), reads=[aa, ka], writes=[t1])
            S.op("pool", lambda e: e.tensor_tensor(out=t1[:], in0=t1[:], in1=bc4(omka), op=ALU.add), reads=[t1, omka], writes=[t1])
            S.op("pool", lambda e: e.tensor_tensor(out=kp[:], in0=k, in1=t1[:], op=ALU.mult), reads=[xs, t1], writes=[kp])
            S.op("pool", lambda e: e.tensor_scalar(out=am[:], in0=kkf[:], scalar1=-1.0, scalar2=None, op0=ALU.mult), reads=[kkf], writes=[am])
            S.op("pool", lambda e: e.tensor_tensor(out=bm[:], in0=kkf[:], in1=aa[:], op=ALU.mult), reads=[kkf, aa], writes=[bm])
            S.op("pool", lambda e: e.tensor_tensor(out=t1[:], in0=r, in1=kp[:], op=ALU.mult), reads=[xs, kp], writes=[t1])
            S.op("pool", lambda e: e.tensor_tensor(out=sq[:], in0=t1[:], in1=bc4(rkc), op=ALU.mult), reads=[t1, rkc], writes=[sq])
            pr = pa.next()
            for p in range(4):
                S.op("pe", lambda e, p=p, pr=pr: e.matmul(pr[:, p, :], lhsT=ones[:], rhs=sq[:, p, :], start=True, stop=True),
                     reads=[ones, sq], writes=[pr] if p == 0 else (), pwrites=() if p == 0 else [pr])
            S.op("act", lambda e, pr=pr: e.copy(out=bonus[:], in_=pr[:]), reads=[pr], writes=[bonus])
            S.op("pool", lambda e: e.tensor_tensor(out=bonus[:], in0=bonus[:], in1=v, op=ALU.mult), reads=[bonus, xs], writes=[bonus])
            S.op("pool", lambda e: e.tensor_copy(out=vv[:], in_=v), reads=[xs], writes=[vv])
            S.op("pool", lambda e: e.tensor_copy(out=t1[:], in_=r), reads=[xs], writes=[t1])
            for oi, src in enumerate((am, bm, dd, kp, t1)):
                pt = pa.next()
                for p in range(4):
                    S.op("pe", lambda e, p=p, src=src, pt=pt: e.transpose(out=pt[:, p, :], in_=src[:, p, :], identity=identF[:]),
                         reads=[src, identF], writes=[pt] if p == 0 else (), pwrites=() if p == 0 else [pt])
                S.op("act", lambda e, oi=oi, pt=pt: e.copy(out=tk[:, oi, :, :], in_=pt[:]), reads=[pt], pwrites=[tk])
            for oi in range(5):
                for h2 in range(2):
                    S.dma("sp", xtok.t[t0:t0 + TP, oi, h2, :].rearrange("t (p j) -> t p j", p=4), tk[:, oi, :, h2 * 64:(h2 + 1) * 64],
                          reads=[tk], pwrites=[xtok], key=tk)
            S.dma("sp", rzt[:], scr["rzT"].t[:, t0:t0 + TP].rearrange("(c p) t -> p c t", p=128), reads=[scr["rzT"]], writes=[rzt])
            xflat = xtok.t.rearrange("t o h c -> (t o) h c")
            for tb in range(0, TP, TB):
                bc = bcr.next()
                for h2 in range(2):
                    S.dma("sp", bc[h2 * 64:(h2 + 1) * 64, :, :, :].rearrange("p t o c -> p (t o) c"),
                          xflat[(t0 + tb) * 5:(t0 + tb + TB) * 5, h2, :].partition_broadcast(64),
                          reads=[xtok], writes=[bc] if h2 == 0 else (), pwrites=() if h2 == 0 else [bc], key=bc)
                for tt in range(TB):
                    t = tb + tt
                    A = bc[:, tt, 0, :].rearrange("p (a j) -> p a j", a=4)
                    B = bc[:, tt, 1, :].rearrange("p (a j) -> p a j", a=4)
                    Dd = bc[:, tt, 2, :].rearrange("p (a j) -> p a j", a=4)
                    Kk = bc[:, tt, 3, :].rearrange("p (a j) -> p a j", a=4)
                    R = bc[:, tt, 4, :].rearrange("p (a j) -> p a j", a=4)
                    kvb = kv.next()
                    S.op("pool", lambda e, Kk=Kk, t=t, kvb=kvb: e.tensor_tensor(out=kvb[:], in0=Kk, in1=vv[:, :, t:t + 1].to_broadcast([128, 4, 64]),
                                                                              op=ALU.mult), reads=[bc, vv], writes=[kvb])
                    S.op("dve", lambda e, A=A: e.tensor_tensor(out=tmp[:], in0=St[:], in1=A, op=ALU.mult), reads=[St, bc], writes=[tmp])
                    S.op("dve", lambda e: e.tensor_reduce(out=sa[:], in_=tmp[:], axis=AX.X, op=ALU.add), reads=[tmp], writes=[sa])
                    S.op("dve", lambda e, Dd=Dd: e.tensor_tensor(out=St[:], in0=St[:], in1=Dd, op=ALU.mult), reads=[St, bc, tmp], writes=[St])
                    S.op("dve", lambda e, B=B: e.tensor_tensor(out=tmp2[:], in0=B, in1=sa[:].unsqueeze(2).to_broadcast([128, 4, 64]), op=ALU.mult),
                         reads=[bc, sa], writes=[tmp2])
                    S.op("dve", lambda e: e.tensor_tensor(out=St[:], in0=St[:], in1=tmp2[:], op=ALU.add), reads=[St, tmp2], writes=[St])
                    S.op("dve", lambda e, kvb=kvb: e.tensor_tensor(out=St[:], in0=St[:], in1=kvb[:], op=ALU.add), reads=[St, kvb], writes=[St])
                    S.op("dve", lambda e, R=R: e.tensor_tensor(out=tmp[:], in0=St[:], in1=R, op=ALU.mult), reads=[St, bc], writes=[tmp])
                    S.op("dve", lambda e, t=t: e.tensor_reduce(out=ybuf[:, :, t], in_=tmp[:], axis=AX.X, op=ALU.add), reads=[tmp], pwrites=[ybuf])
            S.op("pool", lambda e: e.tensor_tensor(out=ysq[:], in0=ybuf[:], in1=ybuf[:], op=ALU.mult), reads=[ybuf], writes=[ysq])
            pm = pa.next(); pq = pa.next()
            for p in range(4):
                S.op("pe", lambda e, p=p, pm=pm: e.matmul(pm[:, p, :], lhsT=ones[:], rhs=ybuf[:, p, :], start=True, stop=True),
                     reads=[ones, ybuf], writes=[pm] if p == 0 else (), pwrites=() if p == 0 else [pm])
                S.op("pe", lambda e, p=p, pq=pq: e.matmul(pq[:, p, :], lhsT=ones[:], rhs=ysq[:, p, :], start=True, stop=True),
                     reads=[ones, ysq], writes=[pq] if p == 0 else (), pwrites=() if p == 0 else [pq])
            S.op("act", lambda e, pm=pm: e.activation(out=mean[:], in_=pm[:], func=AF.Copy, scale=1.0 / 64), reads=[pm], writes=[mean])
            S.op("act", lambda e, pq=pq: e.activation(out=var[:], in_=pq[:], func=AF.Copy, scale=1.0 / 64), reads=[pq], writes=[var])
            S.op("pool", lambda e: e.tensor_tensor(out=ysq[:], in0=mean[:], in1=mean[:], op=ALU.mult), reads=[mean, ysq], writes=[ysq])
            S.op("pool", lambda e: e.tensor_tensor(out=var[:], in0=var[:], in1=ysq[:], op=ALU.subtract), reads=[var, ysq], writes=[var])
            S.op("act", lambda e: e.activation(out=var[:], in_=var[:], func=AF.Sqrt, bias=GN_EPS, scale=1.0), reads=[var], writes=[var])
            S.op("dve", lambda e: e.reciprocal(out=var[:], in_=var[:]), reads=[var], writes=[var])
            S.op("pool", lambda e: e.tensor_tensor(out=mean[:], in0=ybuf[:], in1=mean[:], op=ALU.subtract), reads=[ybuf, mean], writes=[mean])
            S.op("pool", lambda e: e.tensor_tensor(out=mean[:], in0=mean[:], in1=var[:], op=ALU.mult), reads=[mean, var], writes=[mean])
            S.op("pool", lambda e: e.tensor_tensor(out=mean[:], in0=mean[:], in1=bc4(lg), op=ALU.mult), reads=[mean, lg], writes=[mean])
            S.op("pool", lambda e: e.tensor_tensor(out=mean[:], in0=mean[:], in1=bc4(lb), op=ALU.add), reads=[mean, lb], writes=[mean])
            S.op("pool", lambda e: e.tensor_tensor(out=mean[:], in0=mean[:], in1=bonus[:], op=ALU.add), reads=[mean, bonus], writes=[mean])
            S.op("pool", lambda e: e.tensor_tensor(out=yo[:], in0=mean[:], in1=rzt[:], op=ALU.mult), reads=[mean, rzt], writes=[yo])
            S.dma("sp", scr["ysT"].t[2, :, t0:t0 + TP].rearrange("(c p) t -> p c t", p=128), yo[:], reads=[yo], pwrites=[scr["ysT"]], key=yo)
        _barrier(S)
        S.stack_pop()


_NC_CACHE = {}


def kernel(**inputs):
    SEQ = 8192
    if "nc" not in _NC_CACHE:
        _NC_CACHE["nc"] = build(SEQ, nlayers=2, enable=(1, 1, 1), scr_kind="Internal")
    nc = _NC_CACHE["nc"]
    x = np.ascontiguousarray(np.asarray(inputs["x"], dtype=np.float32))
    p = np.asarray(inputs["p"], dtype=np.float32)
    base = {}
    for k in WSPEC:
        v = np.ascontiguousarray(np.asarray(inputs[k], dtype=np.float32))
        base[k] = v.reshape(WSPEC[k])
    in_maps = []
    for b in range(8):
        m = dict(base)
        m["x"] = np.ascontiguousarray(x[b])
        m["p"] = np.ascontiguousarray(p[:, b])
        in_maps.append(m)
    res = run_bass_kernel_spmd(nc, in_maps, core_ids=list(range(8)))
    return np.stack([np.asarray(r["out"], dtype=np.float32) for r in res.results], axis=0)
```
